# Optimizing a Trainium2 kernel written in Bass

```python
import math, functools
import jax, jax.numpy as jnp
from jax import lax
import numpy as np

D_MODEL = 1024
BATCH = 8
SEQ = 2048
DEPTH = 2
DEC_BATCH = 128
DEC_SEQ = 1
PAST_LEN = 16384
PAGE_SIZE = 128

N_EVEN = (DEPTH + 1) // 2
N_ODD = DEPTH // 2
NORM_EPS = 1e-5
CHUNK = 128
CONV_W = 4
SSD_D_INNER = D_MODEL
SSD_HEAD_DIM = 64
SSD_HEADS = SSD_D_INNER // SSD_HEAD_DIM
SSD_GROUPS = 2
SSD_D_STATE = 128
MLSTM_D_INNER = D_MODEL
MLSTM_HEADS = 4
MLSTM_HEAD_DIM = MLSTM_D_INNER // MLSTM_HEADS
QKV_BLOCK = 4
CONV_DIM = SSD_D_INNER + 2 * SSD_GROUPS * SSD_D_STATE + MLSTM_D_INNER
IN0_DIM = SSD_D_INNER + CONV_DIM + SSD_HEADS + MLSTM_D_INNER + 2 * MLSTM_HEADS
MIX0_DIM = SSD_D_INNER + MLSTM_D_INNER
RWKV_HEAD_DIM = 64
RWKV_HEADS = D_MODEL // RWKV_HEAD_DIM
DECAY_LORA = 64
AAA_LORA = 64
GATE_LORA = 160
RWKV_LN_EPS = 64e-5
D_FF = -(-8 * D_MODEL // (3 * 256)) * 256

kernel_name = 'hybrid_ssd_mlstm_rwkv7_decode_step'


def _rms(x, g):
    xf = x.astype(jnp.float32)
    y = xf * lax.rsqrt(jnp.mean(xf * xf, axis=-1, keepdims=True) + NORM_EPS)
    return y.astype(x.dtype) * g


def _group_rms(x, groups, g):
    shp = x.shape
    xf = x.astype(jnp.float32).reshape(shp[:-1] + (groups, shp[-1] // groups))
    y = xf * lax.rsqrt(jnp.mean(xf * xf, axis=-1, keepdims=True) + NORM_EPS)
    return y.reshape(shp).astype(x.dtype) * g


def _chunk_len(t):
    return CHUNK if t % CHUNK == 0 else t


def _to_chunks(a, length):
    b, t = a.shape[:2]
    a = a.astype(jnp.float32).reshape((b, t // length, length) + a.shape[2:])
    return jnp.moveaxis(a, 1, 0)


def _from_chunks(a):
    a = jnp.moveaxis(a, 0, 1)
    return a.reshape((a.shape[0], a.shape[1] * a.shape[2]) + a.shape[3:])


def _causal_conv(u, buf, w, bias):
    t = u.shape[1]
    ext = jnp.concatenate([buf.astype(u.dtype), u], axis=1)
    out = bias
    for j in range(CONV_W):
        out = out + ext[:, j:j + t] * w[j]
    return out, ext[:, t:]


def _blockdiag(x, w):
    shp = x.shape
    xb = x.reshape(shp[:-1] + (shp[-1] // QKV_BLOCK, QKV_BLOCK))
    return jnp.einsum('btnc,ncd->btnd', xb, w).reshape(shp)


def _ssd_scan(x, dt, a_neg, bm, cm, h0):
    b, t, h, p = x.shape
    g, n = bm.shape[2], bm.shape[3]
    hg = h // g
    length = _chunk_len(t)
    mask = jnp.tril(jnp.ones((length, length), dtype=bool))[None, :, :, None, None]
    a_g = a_neg.astype(jnp.float32).reshape(g, hg)

    def step(state, inp):
        xc, dtc, bc, cc = inp
        acum = jnp.cumsum(dtc * a_g, axis=1)
        seg = jnp.where(mask, acum[:, :, None] - acum[:, None], -jnp.inf)
        cb = jnp.einsum('btgn,bsgn->btsg', cc, bc)
        wts = jnp.exp(seg) * cb[..., None] * dtc[:, None]
        y = jnp.einsum('btsgh,bsghp->btghp', wts, xc)
        y = y + jnp.exp(acum)[..., None] * jnp.einsum('btgn,bghpn->btghp', cc, state)
        w_end = jnp.exp(acum[:, -1:] - acum) * dtc
        state = (jnp.exp(acum[:, -1])[..., None, None] * state
                 + jnp.einsum('bsghp,bsgn->bghpn', w_end[..., None] * xc, bc))
        return state, y

    xs = _to_chunks(x.reshape(b, t, g, hg, p), length)
    dts = _to_chunks(dt.reshape(b, t, g, hg), length)
    bs = _to_chunks(bm, length)
    cs = _to_chunks(cm, length)
    state, ys = lax.scan(step, h0.astype(jnp.float32).reshape(b, g, hg, p, n), (xs, dts, bs, cs))
    return _from_chunks(ys).reshape(b, t, h, p), state.reshape(b, h, p, n)


def _mlstm_scan(q, k, v, logi, logf, c0, n0, m0):
    b, t = q.shape[:2]
    length = _chunk_len(t)
    mask = jnp.tril(jnp.ones((length, length), dtype=bool))[None, :, :, None]

    def step(carry, inp):
        c, n, m = carry
        qc, kc, vc, lic, lfc = inp
        bcum = jnp.cumsum(lfc, axis=1)
        dlog = jnp.where(mask, bcum[:, :, None] - bcum[:, None] + lic[:, None], -jnp.inf)
        inter = bcum + m[:, None]
        m_t = jnp.maximum(inter, jnp.max(dlog, axis=2))
        s = jnp.einsum('bthd,bshd->btsh', qc, kc) * jnp.exp(dlog - m_t[:, :, None])
        w_inter = jnp.exp(inter - m_t)
        num = (jnp.einsum('btsh,bshv->bthv', s, vc)
               + w_inter[..., None] * jnp.einsum('bthd,bhdv->bthv', qc, c))
        den = jnp.sum(s, axis=2) + w_inter * jnp.einsum('bthd,bhd->bth', qc, n)
        h = num / jnp.maximum(jnp.abs(den), jnp.exp(-m_t))[..., None]
        b_end = bcum[:, -1]
        wlog = b_end[:, None] - bcum + lic
        m_new = jnp.maximum(b_end + m, jnp.max(wlog, axis=1))
        ws = jnp.exp(wlog - m_new[:, None])
        dc = jnp.exp(b_end + m - m_new)
        c = dc[..., None, None] * c + jnp.einsum('bshd,bshv->bhdv', ws[..., None] * kc, vc)
        n = dc[..., None] * n + jnp.einsum('bsh,bshd->bhd', ws, kc)
        return (c, n, m_new), h

    seq = (_to_chunks(q, length), _to_chunks(k, length), _to_chunks(v, length),
           _to_chunks(logi, length), _to_chunks(logf, length))
    init = (c0.astype(jnp.float32), n0.astype(jnp.float32), m0.astype(jnp.float32))
    (c, n, m), hs = lax.scan(step, init, seq)
    return _from_chunks(hs), (c, n, m)


def _rwkv7_scan(r, decay, k, v, kk, a, s0):
    def step(s, inp):
        r_t, w_t, k_t, v_t, kk_t, a_t = inp
        sa = jnp.einsum('bhvk,bhk->bhv', s, -kk_t)
        s = (s * w_t[:, :, None, :] + sa[..., None] * (kk_t * a_t)[:, :, None, :]
             + v_t[..., None] * k_t[:, :, None, :])
        return s, jnp.einsum('bhvk,bhk->bhv', s, r_t)

    seq = tuple(jnp.moveaxis(z.astype(jnp.float32), 1, 0) for z in (r, decay, k, v, kk, a))
    s, ys = lax.scan(step, s0.astype(jnp.float32), seq)
    return jnp.moveaxis(ys, 0, 1), s


def _ssd_mlstm_mixer(xn, conv_buf, ssm_h, mc, mn, mm, p, i):
    b, t, _ = xn.shape
    dty = xn.dtype
    proj = xn @ p['w_in0'][i]
    s1 = SSD_D_INNER
    s2 = s1 + CONV_DIM
    s3 = s2 + SSD_HEADS
    s4 = s3 + MLSTM_D_INNER
    s5 = s4 + MLSTM_HEADS
    z_ssd, conv_in, dt_pre, o_pre, i_pre, f_pre = jnp.split(proj, [s1, s2, s3, s4, s5], axis=-1)
    conv_out, new_conv = _causal_conv(conv_in, conv_buf, p['conv_w'][i], p['conv_b'][i])
    conv_act = jax.nn.silu(conv_out)
    gn = SSD_GROUPS * SSD_D_STATE
    xs, bm, cm, xm_act = jnp.split(conv_act, [SSD_D_INNER, SSD_D_INNER + gn, SSD_D_INNER + 2 * gn], axis=-1)
    xm_raw = conv_in[..., SSD_D_INNER + 2 * gn:]
    dt = jax.nn.softplus((dt_pre + p['ssd_dt_bias'][i]).astype(jnp.float32))
    a_neg = -jnp.exp(p['ssd_a_log'][i].astype(jnp.float32))
    xh = xs.reshape(b, t, SSD_HEADS, SSD_HEAD_DIM)
    y, new_h = _ssd_scan(xh, dt, a_neg, bm.reshape(b, t, SSD_GROUPS, SSD_D_STATE),
                         cm.reshape(b, t, SSD_GROUPS, SSD_D_STATE), ssm_h)
    y = y.astype(dty) + p['ssd_d'][i][:, None] * xh
    y = _group_rms(y.reshape(b, t, SSD_D_INNER) * jax.nn.silu(z_ssd), SSD_GROUPS, p['ssd_norm'][i])
    hs = (b, t, MLSTM_HEADS, MLSTM_HEAD_DIM)
    q = _blockdiag(xm_act, p['ml_wq'][i]).reshape(hs)
    k = _blockdiag(xm_act, p['ml_wk'][i]).reshape(hs) * (MLSTM_HEAD_DIM ** -0.5)
    v = _blockdiag(xm_raw, p['ml_wv'][i]).reshape(hs)
    logi = (i_pre + p['ml_i_bias'][i]).astype(jnp.float32)
    logf = jax.nn.log_sigmoid((f_pre + p['ml_f_bias'][i]).astype(jnp.float32))
    hm, (new_c, new_n, new_m) = _mlstm_scan(q, k, v, logi, logf, mc, mn, mm)
    hm = _group_rms(hm.astype(dty).reshape(b, t, MLSTM_D_INNER), MLSTM_HEADS, p['ml_norm'][i])
    hm = (hm + p['ml_skip'][i] * xm_act) * jax.nn.sigmoid(o_pre)
    out = jnp.concatenate([y, hm], axis=-1) @ p['w_out0'][i]
    return out, (new_conv, new_h, new_c, new_n, new_m)


def _rwkv7_mixer(xn, shift, wkv, p, j):
    b, t, d = xn.shape
    hs = (b, t, RWKV_HEADS, RWKV_HEAD_DIM)
    x_prev = jnp.concatenate([shift[:, None].astype(xn.dtype), xn[:, :-1]], axis=1)
    xx = x_prev - xn
    mu = p['rw_mu'][j]
    xr, xw, xk, xv, xa, xg = (xn + xx * mu[c] for c in range(6))
    r = xr @ p['rw_wr'][j]
    w = -jax.nn.softplus(-(p['rw_w0'][j] + jnp.tanh(xw @ p['rw_w1'][j]) @ p['rw_w2'][j])) - 0.5
    k = xk @ p['rw_wk'][j]
    v = xv @ p['rw_wv'][j]
    a = jax.nn.sigmoid(p['rw_a0'][j] + (xa @ p['rw_a1'][j]) @ p['rw_a2'][j])
    g = jax.nn.sigmoid(xg @ p['rw_g1'][j]) @ p['rw_g2'][j]
    kk = (k * p['rw_k_k'][j]).reshape(hs).astype(jnp.float32)
    kk = kk / jnp.maximum(jnp.sqrt(jnp.sum(kk * kk, axis=-1, keepdims=True)), 1e-12)
    k = k * (1 + (a - 1) * p['rw_k_a'][j])
    decay = jnp.exp(-jnp.exp(w.astype(jnp.float32)))
    y, new_s = _rwkv7_scan(r.reshape(hs), decay.reshape(hs), k.reshape(hs), v.reshape(hs),
                           kk, a.reshape(hs), wkv)
    mean = jnp.mean(y, axis=-1, keepdims=True)
    var = jnp.mean(jnp.square(y - mean), axis=-1, keepdims=True)
    y = ((y - mean) * lax.rsqrt(var + RWKV_LN_EPS)).reshape(b, t, d).astype(xn.dtype)
    y = y * p['rw_ln_w'][j] + p['rw_ln_b'][j]
    bonus = jnp.sum(r.reshape(hs) * k.reshape(hs) * p['rw_r_k'][j].reshape(RWKV_HEADS, RWKV_HEAD_DIM),
                    axis=-1, keepdims=True) * v.reshape(hs)
    y = (y + bonus.reshape(b, t, d)) * g
    return y @ p['rw_wo'][j], (xn[:, -1], new_s)


def _swiglu(x, w_gu, w_down):
    gate, up = jnp.split(x @ w_gu, 2, axis=-1)
    return (jax.nn.silu(gate) * up) @ w_down


def _trunk(x, conv, ssm, mc, mn, mm, shift, wkv, p):
    even_names = ('conv', 'ssm', 'mc', 'mn', 'mm')
    odd_names = ('shift', 'wkv')
    outs = {name: [] for name in even_names + odd_names}
    for layer in range(DEPTH):
        xn = _rms(x, p['norm_mix'][layer])
        if layer % 2 == 0:
            i = layer // 2
            mix, st = _ssd_mlstm_mixer(xn, conv[i], ssm[i], mc[i], mn[i], mm[i], p, i)
            for name, s in zip(even_names, st):
                outs[name].append(s)
        else:
            j = layer // 2
            mix, st = _rwkv7_mixer(xn, shift[j], wkv[j], p, j)
            for name, s in zip(odd_names, st):
                outs[name].append(s)
        x = x + mix.astype(x.dtype)
        x = x + _swiglu(_rms(x, p['norm_ffn'][layer]), p['ffn_w_gate_up'][layer], p['ffn_w_down'][layer])
    y = _rms(x, p['norm_final'])
    return y, tuple(jnp.stack(outs[name]) for name in even_names + odd_names)


def setup_inputs(seed: int = 0) -> dict:
    key = jax.random.key(seed)
    ks = iter(jax.random.split(key, 64))

    def nrm(shape, scale):
        return scale * jax.random.normal(next(ks), shape, jnp.float32)

    d = D_MODEL
    x_prompt = nrm((BATCH, SEQ, d), 1.0)
    x_sample = nrm((DEC_BATCH, DEC_SEQ, d), 1.0)
    state_conv = nrm((N_EVEN, DEC_BATCH, CONV_W - 1, CONV_DIM), 1.0)
    state_ssm = nrm((N_EVEN, DEC_BATCH, SSD_HEADS, SSD_HEAD_DIM, SSD_D_STATE), 0.1)
    state_mlstm_c = nrm((N_EVEN, DEC_BATCH, MLSTM_HEADS, MLSTM_HEAD_DIM, MLSTM_HEAD_DIM), 0.1)
    state_mlstm_n = nrm((N_EVEN, DEC_BATCH, MLSTM_HEADS, MLSTM_HEAD_DIM), 0.1)
    state_mlstm_m = nrm((N_EVEN, DEC_BATCH, MLSTM_HEADS), 1.0)
    state_shift = nrm((N_ODD, DEC_BATCH, d), 1.0)
    state_wkv = nrm((N_ODD, DEC_BATCH, RWKV_HEADS, RWKV_HEAD_DIM, RWKV_HEAD_DIM), 0.1)

    norm_mix = 1.0 + nrm((DEPTH, d), 0.02)
    norm_ffn = 1.0 + nrm((DEPTH, d), 0.02)
    norm_final = 1.0 + nrm((d,), 0.02)
    w_in0 = nrm((N_EVEN, d, IN0_DIM), d ** -0.5)
    conv_w = nrm((N_EVEN, CONV_W, CONV_DIM), 0.5)
    conv_b = nrm((N_EVEN, CONV_DIM), 0.02)
    dt0 = jnp.exp(jax.random.uniform(next(ks), (N_EVEN, SSD_HEADS), jnp.float32,
                                     minval=math.log(1e-3), maxval=math.log(1e-1)))
    ssd_dt_bias = dt0 + jnp.log(-jnp.expm1(-dt0))
    ssd_a_log = jnp.log(jax.random.uniform(next(ks), (N_EVEN, SSD_HEADS), jnp.float32, minval=1.0, maxval=16.0))
    ssd_d = 1.0 + nrm((N_EVEN, SSD_HEADS), 0.1)
    ssd_norm = 1.0 + nrm((N_EVEN, SSD_D_INNER), 0.02)
    nb = MLSTM_D_INNER // QKV_BLOCK
    ml_wq = nrm((N_EVEN, nb, QKV_BLOCK, QKV_BLOCK), QKV_BLOCK ** -0.5)
    ml_wk = nrm((N_EVEN, nb, QKV_BLOCK, QKV_BLOCK), QKV_BLOCK ** -0.5)
    ml_wv = nrm((N_EVEN, nb, QKV_BLOCK, QKV_BLOCK), QKV_BLOCK ** -0.5)
    ml_i_bias = nrm((N_EVEN, MLSTM_HEADS), 0.1)
    ml_f_bias = jnp.linspace(3.0, 6.0, MLSTM_HEADS)[None, :] + nrm((N_EVEN, MLSTM_HEADS), 0.1)
    ml_norm = 1.0 + nrm((N_EVEN, MLSTM_D_INNER), 0.02)
    ml_skip = 1.0 + nrm((N_EVEN, MLSTM_D_INNER), 0.02)
    w_out0 = nrm((N_EVEN, MIX0_DIM, d), MIX0_DIM ** -0.5)

    rw_mu = jax.random.uniform(next(ks), (N_ODD, 6, d), jnp.float32)
    rw_wr = nrm((N_ODD, d, d), d ** -0.5)
    rw_wk = nrm((N_ODD, d, d), d ** -0.5)
    rw_wv = nrm((N_ODD, d, d), d ** -0.5)
    rw_wo = nrm((N_ODD, d, d), d ** -0.5)
    rw_w0 = jnp.linspace(-6.0, 1.0, d)[None, :] + nrm((N_ODD, d), 0.1)
    rw_w1 = nrm((N_ODD, d, DECAY_LORA), d ** -0.5)
    rw_w2 = nrm((N_ODD, DECAY_LORA, d), 0.1 * DECAY_LORA ** -0.5)
    rw_a0 = nrm((N_ODD, d), 0.1)
    rw_a1 = nrm((N_ODD, d, AAA_LORA), d ** -0.5)
    rw_a2 = nrm((N_ODD, AAA_LORA, d), 0.1 * AAA_LORA ** -0.5)
    rw_g1 = nrm((N_ODD, d, GATE_LORA), d ** -0.5)
    rw_g2 = nrm((N_ODD, GATE_LORA, d), GATE_LORA ** -0.5)
    rw_k_k = 0.85 + nrm((N_ODD, d), 0.02)
    rw_k_a = 1.0 + nrm((N_ODD, d), 0.02)
    rw_r_k = nrm((N_ODD, d), 0.1)
    rw_ln_w = 1.0 + nrm((N_ODD, d), 0.02)
    rw_ln_b = nrm((N_ODD, d), 0.02)
    ffn_w_gate_up = nrm((DEPTH, d, 2 * D_FF), d ** -0.5)
    ffn_w_down = nrm((DEPTH, D_FF, d), D_FF ** -0.5)
    return {'x_prompt': x_prompt, 'x_sample': x_sample,
            'state_conv': state_conv, 'state_ssm': state_ssm, 'state_mlstm_c': state_mlstm_c,
            'state_mlstm_n': state_mlstm_n, 'state_mlstm_m': state_mlstm_m,
            'state_shift': state_shift, 'state_wkv': state_wkv,
            'norm_mix': norm_mix, 'norm_ffn': norm_ffn, 'norm_final': norm_final,
            'w_in0': w_in0, 'conv_w': conv_w, 'conv_b': conv_b, 'ssd_dt_bias': ssd_dt_bias,
            'ssd_a_log': ssd_a_log, 'ssd_d': ssd_d, 'ssd_norm': ssd_norm,
            'ml_wq': ml_wq, 'ml_wk': ml_wk, 'ml_wv': ml_wv, 'ml_i_bias': ml_i_bias, 'ml_f_bias': ml_f_bias,
            'ml_norm': ml_norm, 'ml_skip': ml_skip, 'w_out0': w_out0,
            'rw_mu': rw_mu, 'rw_wr': rw_wr, 'rw_wk': rw_wk, 'rw_wv': rw_wv, 'rw_wo': rw_wo,
            'rw_w0': rw_w0, 'rw_w1': rw_w1, 'rw_w2': rw_w2, 'rw_a0': rw_a0, 'rw_a1': rw_a1, 'rw_a2': rw_a2,
            'rw_g1': rw_g1, 'rw_g2': rw_g2, 'rw_k_k': rw_k_k, 'rw_k_a': rw_k_a, 'rw_r_k': rw_r_k,
            'rw_ln_w': rw_ln_w, 'rw_ln_b': rw_ln_b,
            'ffn_w_gate_up': ffn_w_gate_up, 'ffn_w_down': ffn_w_down}


def reference(x_prompt, x_sample, state_conv, state_ssm, state_mlstm_c, state_mlstm_n, state_mlstm_m,
              state_shift, state_wkv, norm_mix, norm_ffn, norm_final, w_in0, conv_w, conv_b,
              ssd_dt_bias, ssd_a_log, ssd_d, ssd_norm, ml_wq, ml_wk, ml_wv, ml_i_bias, ml_f_bias,
              ml_norm, ml_skip, w_out0, rw_mu, rw_wr, rw_wk, rw_wv, rw_wo, rw_w0, rw_w1, rw_w2,
              rw_a0, rw_a1, rw_a2, rw_g1, rw_g2, rw_k_k, rw_k_a, rw_r_k, rw_ln_w, rw_ln_b,
              ffn_w_gate_up, ffn_w_down):
    p = dict(norm_mix=norm_mix, norm_ffn=norm_ffn, norm_final=norm_final, w_in0=w_in0,
             conv_w=conv_w, conv_b=conv_b, ssd_dt_bias=ssd_dt_bias, ssd_a_log=ssd_a_log,
             ssd_d=ssd_d, ssd_norm=ssd_norm, ml_wq=ml_wq, ml_wk=ml_wk, ml_wv=ml_wv,
             ml_i_bias=ml_i_bias, ml_f_bias=ml_f_bias, ml_norm=ml_norm, ml_skip=ml_skip,
             w_out0=w_out0, rw_mu=rw_mu, rw_wr=rw_wr, rw_wk=rw_wk, rw_wv=rw_wv, rw_wo=rw_wo,
             rw_w0=rw_w0, rw_w1=rw_w1, rw_w2=rw_w2, rw_a0=rw_a0, rw_a1=rw_a1, rw_a2=rw_a2,
             rw_g1=rw_g1, rw_g2=rw_g2, rw_k_k=rw_k_k, rw_k_a=rw_k_a, rw_r_k=rw_r_k,
             rw_ln_w=rw_ln_w, rw_ln_b=rw_ln_b, ffn_w_gate_up=ffn_w_gate_up, ffn_w_down=ffn_w_down)
    bp = x_prompt.shape[0]
    zeros = functools.partial(jnp.zeros, dtype=x_prompt.dtype)
    y_prompt, (conv_p, ssm_p, mlstm_c_p, mlstm_n_p, mlstm_m_p, shift_p, wkv_p) = _trunk(
        x_prompt,
        zeros((N_EVEN, bp) + state_conv.shape[2:]),
        zeros((N_EVEN, bp) + state_ssm.shape[2:]),
        zeros((N_EVEN, bp) + state_mlstm_c.shape[2:]),
        zeros((N_EVEN, bp) + state_mlstm_n.shape[2:]),
        zeros((N_EVEN, bp) + state_mlstm_m.shape[2:]),
        zeros((N_ODD, bp) + state_shift.shape[2:]),
        zeros((N_ODD, bp) + state_wkv.shape[2:]),
        p)
    y_sample, (conv_s, ssm_s, mlstm_c_s, mlstm_n_s, mlstm_m_s, shift_s, wkv_s) = _trunk(
        x_sample, state_conv, state_ssm, state_mlstm_c, state_mlstm_n, state_mlstm_m,
        state_shift, state_wkv, p)
    return (y_prompt, y_sample, conv_p, conv_s, ssm_p, ssm_s, mlstm_c_p, mlstm_c_s,
            mlstm_n_p, mlstm_n_s, mlstm_m_p, mlstm_m_s, shift_p, shift_s, wkv_p, wkv_s)
```

```python
import numpy as np
import concourse.bass as bass
import concourse.mybir as mybir
from concourse.bass_utils import run_bass_kernel_spmd

F32 = mybir.dt.float32
BF16 = mybir.dt.bfloat16
ALU = mybir.AluOpType
AF = mybir.ActivationFunctionType
AX = mybir.AxisListType

NCORE = 8
D = 1024
T = 2048
NB = 16
IN0 = 4632
DFF = 2816
EPS = 1e-5


class Tok:
    __slots__ = ("sem", "val", "eng")

    def __init__(self, sem, val, eng):
        self.sem, self.val, self.eng = sem, val, eng


class Ref:
    __slots__ = ("T", "ap")

    def __init__(self, T_, ap):
        self.T, self.ap = T_, ap

    def __getitem__(self, k):
        return Ref(self.T, self.ap[k])

    def rearrange(self, p, **kw):
        return Ref(self.T, self.ap.rearrange(p, **kw))

    def unsqueeze(self, a):
        return Ref(self.T, self.ap.unsqueeze(a))

    def to_broadcast(self, shp):
        return Ref(self.T, self.ap.to_broadcast(list(shp)))

    def bc(self, axis, n):
        ap = self.ap.unsqueeze(axis)
        shp = list(ap.shape)
        shp[axis] = n
        return Ref(self.T, ap.to_broadcast(shp))


class TT:
    __slots__ = ("t", "name", "lw", "rd", "psum")

    def __init__(self, t, name, psum=False):
        self.t, self.name, self.lw, self.rd, self.psum = t, name, None, [], psum

    def __getitem__(self, k):
        return Ref(self, self.t[k])


def _Ts(*xs):
    return [x.T for x in xs if isinstance(x, Ref)]


def _a(x):
    return x.ap if isinstance(x, Ref) else x


class Eng:
    def __init__(self, fw, name, h):
        self.fw, self.name, self.h = fw, name, h
        self.sems, self.n, self.waited = [], 0, {}


class Fw:
    EPOCH = 30000
    NDMA = 10

    def __init__(self, nc):
        self.nc = nc
        self._ctx = []
        self.E = {}
        for name, h in (("pe", nc.tensor), ("dve", nc.vector), ("act", nc.scalar),
                        ("pool", nc.gpsimd), ("sp", nc.sync)):
            self.E[name] = Eng(self, name, h)
        self.dma_sems, self.dma_i = {}, {}
        self.ntile = 0
        self.sb_bytes = 0

    def enter(self, cm):
        v = cm.__enter__()
        self._ctx.append(cm)
        return v

    def close(self):
        for cm in reversed(self._ctx):
            cm.__exit__(None, None, None)
        self._ctx = []

    def new_sem(self, name):
        return self.enter(self.nc.semaphore(name))

    def presem(self, queues=("sp", "pool", "act"), epochs=3):
        for e in self.E.values():
            while len(e.sems) < epochs:
                e.sems.append(self.new_sem(f"e_{e.name}_{len(e.sems)}"))
        for q in queues:
            self.dma_sems[q] = [[self.new_sem(f"d_{q}_{i}"), 0] for i in range(Fw.NDMA)]
            self.dma_i[q] = 0

    def mark(self):
        return len(self._ctx)

    def release(self, mark):
        self.barrier()
        while len(self._ctx) > mark:
            self._ctx.pop().__exit__(None, None, None)

    def barrier(self):
        for eng in self.E.values():
            for q, slots in self.dma_sems.items():
                for sem, cnt in slots:
                    if cnt > 0:
                        self._wait(eng, Tok(sem, cnt, "dma"))
            for name, e in self.E.items():
                if e is eng or e.n == 0:
                    continue
                ep = (e.n - 1) // Fw.EPOCH
                self._wait(eng, Tok(e.sems[ep], (e.n - 1) % Fw.EPOCH + 1, name))

    def sb(self, shape, dt=F32, name="t"):
        self.ntile += 1
        n = 1
        for s in shape[1:]:
            n *= s
        self.sb_bytes += n * (2 if dt == BF16 else 4)
        return TT(self.enter(self.nc.sbuf_tensor(f"{name}_{self.ntile}", list(shape), dt)), name)

    def ps(self, shape, dt=F32, name="p"):
        self.ntile += 1
        return TT(self.enter(self.nc.psum_tensor(f"{name}_{self.ntile}", list(shape), dt)), name, psum=True)

    def view(self, ref, name="v"):
        return TT(ref.ap, name, psum=ref.T.psum)

    def _wait(self, eng, tok):
        if tok is None:
            return
        key = id(tok.sem)
        if eng.waited.get(key, 0) >= tok.val:
            return
        eng.h.wait_ge(tok.sem, tok.val)
        eng.waited[key] = tok.val

    def _deps(self, eng, reads, writes):
        for t in reads:
            if t.lw is not None:
                self._wait(eng, t.lw)
            if t.psum:
                for r in t.rd:
                    if r.eng != eng.name:
                        self._wait(eng, r)
        for t in writes:
            if t.lw is not None and t.lw.eng != eng.name:
                self._wait(eng, t.lw)
            for r in t.rd:
                if r.eng != eng.name:
                    self._wait(eng, r)

    def _mark(self, tok, reads, writes):
        for t in reads:
            t.rd.append(tok)
        for t in writes:
            t.lw = tok
            t.rd = []

    def op(self, e, fn, reads=(), writes=()):
        eng = self.E[e]
        self._deps(eng, reads, writes)
        ep = eng.n // Fw.EPOCH
        while len(eng.sems) <= ep:
            eng.sems.append(self.new_sem(f"e_{eng.name}_{len(eng.sems)}"))
        sem = eng.sems[ep]
        inst = fn(eng.h)
        val = eng.n % Fw.EPOCH + 1
        eng.n += 1
        inst.then_inc(sem, 1)
        tok = Tok(sem, val, eng.name)
        self._mark(tok, reads, writes)
        return tok

    def dma(self, q, out, in_, **kw):
        eng = self.E[q]
        if q not in self.dma_sems:
            self.dma_sems[q] = [[self.new_sem(f"d_{q}_{i}"), 0] for i in range(Fw.NDMA)]
            self.dma_i[q] = 0
        slot = self.dma_sems[q][self.dma_i[q] % Fw.NDMA]
        self.dma_i[q] += 1
        sem, cnt = slot
        if cnt > 0:
            self._wait(eng, Tok(sem, cnt, "dma"))
        reads, writes = _Ts(in_), _Ts(out)
        self._deps(eng, reads, writes)
        inst = eng.h.dma_start(out=_a(out), in_=_a(in_), **kw)
        slot[1] = cnt + 16
        inst.then_inc(sem, 16)
        tok = Tok(sem, cnt + 16, "dma")
        self._mark(tok, reads, writes)
        return tok

    def finish(self):
        eng = self.E["sp"]
        for q, slots in self.dma_sems.items():
            for sem, cnt in slots:
                if cnt > 0:
                    self._wait(eng, Tok(sem, cnt, "dma"))
        for name, e in self.E.items():
            if name == "sp" or e.n == 0:
                continue
            self._wait(eng, Tok(e.sems[(e.n - 1) // Fw.EPOCH], (e.n - 1) % Fw.EPOCH + 1, name))

    def mm(self, out, lhsT, rhs, start=True, stop=True):
        return self.op("pe", lambda e: e.matmul(_a(out), _a(lhsT), _a(rhs), start=start, stop=stop),
                       _Ts(lhsT, rhs), _Ts(out))

    def tr(self, out, in_, ident):
        return self.op("pe", lambda e: e.transpose(_a(out), _a(in_), _a(ident)), _Ts(in_, ident), _Ts(out))

    def act(self, out, in_, func, bias=None, scale=None, accum_out=None):
        kw = {}
        if bias is not None:
            kw["bias"] = _a(bias)
        if scale is not None:
            kw["scale"] = _a(scale)
        if accum_out is not None:
            kw["accum_out"] = _a(accum_out)
        return self.op("act", lambda e: e.activation(out=_a(out), in_=_a(in_), func=func, **kw),
                       _Ts(in_, bias, scale), _Ts(out, accum_out))

    def tt(self, e, out, in0, in1, op):
        return self.op(e, lambda h: h.tensor_tensor(out=_a(out), in0=_a(in0), in1=_a(in1), op=op),
                       _Ts(in0, in1), _Ts(out))

    def ts(self, e, out, in0, s1, s2, op0, op1=None, accum_out=None):
        kw = {}
        if op1 is not None:
            kw["op1"] = op1
        if accum_out is not None:
            kw["accum_out"] = _a(accum_out)
        return self.op(e, lambda h: h.tensor_scalar(out=_a(out), in0=_a(in0), scalar1=_a(s1), scalar2=_a(s2),
                                                    op0=op0, **kw),
                       _Ts(in0, s1, s2), _Ts(out, accum_out))

    def stt(self, out, in0, scalar, in1, op0, op1, accum_out=None):
        kw = {}
        if accum_out is not None:
            kw["accum_out"] = _a(accum_out)
        return self.op("dve", lambda h: h.scalar_tensor_tensor(out=_a(out), in0=_a(in0), scalar=_a(scalar),
                                                               in1=_a(in1), op0=op0, op1=op1, **kw),
                       _Ts(in0, scalar, in1), _Ts(out, accum_out))

    def cp(self, e, out, in_):
        if e == "act":
            return self.act(out, in_, AF.Copy)
        return self.op(e, lambda h: h.tensor_copy(out=_a(out), in_=_a(in_)), _Ts(in_), _Ts(out))

    def red(self, out, in_, op, axis=AX.X):
        return self.op("dve", lambda h: h.tensor_reduce(out=_a(out), in_=_a(in_), axis=axis, op=op),
                       _Ts(in_), _Ts(out))

    def recip(self, out, in_):
        return self.op("dve", lambda h: h.reciprocal(out=_a(out), in_=_a(in_)), _Ts(in_), _Ts(out))

    def memset(self, e, out, val):
        return self.op(e, lambda h: h.memset(_a(out), val), [], _Ts(out))


def host_consts():
    j = np.arange(128)
    c = np.zeros((128, 10, 128), np.float32)
    c[:, 0, :] = (j[:, None] == j[None, :])
    c[:, 1, :] = (j[:, None] <= j[None, :])
    c[:, 2, :] = (j[:, None] > j[None, :])
    c[:, 3, :] = np.where(j[None, :] <= j[:, None], 0.0, -30000.0)
    c[:, 4, :] = 1.0
    c[:, 5, :] = (j[:, None] == 127)
    c[:, 6, :] = (j[:, None] < j[None, :])
    c[:, 7, :] = c[:, 1, :]
    c[:, 8, :] = c[:, 6, :]
    c[:, 9, :] = c[:, 1, :]
    return c


def blockdiag(w):
    out = np.zeros((8, 128, 128), np.float32)
    w = w.reshape(8, 32, 4, 4)
    for nl in range(32):
        out[:, nl * 4:(nl + 1) * 4, nl * 4:(nl + 1) * 4] = w[:, nl]
    return np.ascontiguousarray(out.transpose(1, 0, 2))


def host_consts16():
    h = np.arange(16)
    q = np.arange(128)
    j = np.arange(8)
    e = (h[:, None, None] == (2 * j[None, :, None] + q[None, None, :] // 64)).astype(np.float32)
    sel = np.broadcast_to((h[:, None, None] == h[None, :, None]), (16, 16, 128)).astype(np.float32)
    return np.ascontiguousarray(np.concatenate([e.reshape(16, -1), sel.reshape(16, -1)], axis=1))


class IO:
    pass


def build(cfg):
    nc = bass.Bass("TRN2", target_bir_lowering=False)
    fw = Fw(nc)
    io = IO()
    NCH = cfg.get("nch", 16)
    dbg = cfg.get("dbg", ())
    phases = cfg.get("phases", ("0a", "0b", "1", "2", "3"))

    def din(name, shape):
        return nc.dram_tensor(name, list(shape), F32, kind="ExternalInput").ap()

    def dout(name, shape):
        return nc.dram_tensor(name, list(shape), F32, kind="ExternalOutput").ap()

    def dscr(name, shape):
        if name in dbg:
            return dout(name, shape)
        return nc.dram_tensor(name, list(shape), F32).ap()

    io.xp = din("xp", [T, D])
    io.cst = din("cst", [128, 10, 128])
    io.w_in0 = din("w_in0", [D, IN0])
    io.w_out0 = din("w_out0", [2 * D, D])
    io.norm_mix = din("norm_mix", [2, D])
    io.norm_ffn = din("norm_ffn", [2, D])
    io.norm_final = din("norm_final", [D])
    io.ssd_norm = din("ssd_norm", [D])
    io.small0 = din("small0", [64])
    io.convp = din("convp", [128, 20, 5])
    io.mlcol = din("mlcol", [128, 8, 2])
    io.bdq = din("bdq", [128, 8, 128])
    io.bdk = din("bdk", [128, 8, 128])
    io.bdv = din("bdv", [128, 8, 128])
    io.w_gu = din("w_gu", [2, D, 2 * DFF])
    io.w_dn = din("w_dn", [2, DFF, D])
    for nm in ("rw_wr", "rw_wk", "rw_wv", "rw_wo"):
        setattr(io, nm, din(nm, [D, D]))
    io.rw_w1 = din("rw_w1", [D, 64]); io.rw_w2 = din("rw_w2", [64, D])
    io.rw_a1 = din("rw_a1", [D, 64]); io.rw_a2 = din("rw_a2", [64, D])
    io.rw_g1 = din("rw_g1", [D, 160]); io.rw_g2 = din("rw_g2", [160, D])
    io.rw_rows = din("rw_rows", [7, D])
    io.rw_mu = din("rw_mu", [128, 8, 6])
    io.wkv_p = dout("wkv_p", [128, 8, 64])
    io.shift_p = dout("shift_p", [1, D])
    io.xs = din("xs", [NB, D])
    io.c16 = din("c16", [16, 8 * 128 + 16 * 128])
    io.eye16 = din("eye16", [128, 16, 16])
    io.dtcol = din("dtcol", [16, 4])
    io.conv_s_in = din("conv_s_in", [128, 20, 3, NB])
    io.ssm_s_in = din("ssm_s_in", [NB, D, 128])
    io.mc_s_in = din("mc_s_in", [NB, 4, 256, 256])
    io.mn_s_in = din("mn_s_in", [128, 8, NB])
    io.mm_s_in = din("mm_s_in", [NB, 4])
    io.shift_s_in = din("shift_s_in", [128, 8, NB])
    io.wkv_s_in = din("wkv_s_in", [NB, 128, 8, 64])
    io.y_s = dout("y_s", [NB, D])
    io.conv_s = dout("conv_s", [128, 20, 3, NB])
    io.ssm_s = dout("ssm_s", [NB, D, 128])
    io.mc_s = dout("mc_s", [NB, 4, 256, 256])
    io.mn_s = dout("mn_s", [128, 8, NB])
    io.mm_s = dout("mm_s", [NB, 4])
    io.shift_s = dout("shift_s", [NB, D])
    io.wkv_s = dout("wkv_s", [NB, 128, 8, 64])
    io.s1s = dscr("s1s", [NB, D])
    io.s2s = dscr("s2s", [NB, D])
    io.s3s = dscr("s3s", [NB, D])
    if "dbg_a" in dbg:
        io.dbg_a = dout("dbg_a", [NB, D]); io.dbg_b = dout("dbg_b", [NB, 64])
    io.s1 = dscr("s1", [T, D])
    io.s2 = dscr("s2", [T, D])
    io.s3 = dscr("s3", [T, D])
    io.y_p = dout("y_p", [T, D])
    io.ssm_p = dout("ssm_p", [128, D])
    io.mc_p = dout("mc_p", [128, 2, 4, 264])
    io.mm_p = dout("mm_p", [1, 4])
    io.conv_p = dout("conv_p", [128, 20, 3])

    fw.presem(epochs=5)

    cst = fw.sb([128, 10, 128], F32, "cst")
    fw.dma("sp", cst[:], io.cst[:, :, :])
    ident, tri_le, mask_gt, negmask, ones = (cst[:, i, :] for i in range(5))
    sel127 = cst[:, 5, :]
    m4 = cst[:, 6:10, :].rearrange("p a t -> p (a t)")
    identb = fw.sb([128, 128], BF16, "identb")
    fw.cp("dve", identb[:], ident)
    onesb = fw.sb([128, 128], BF16, "onesb")
    fw.cp("dve", onesb[:], ones)
    nst = fw.sb([128, 8], F32, "nst")
    c16 = fw.sb([16, 8 * 128 + 16 * 128], F32, "c16")
    fw.dma("sp", c16[:], io.c16[:, :])
    exp16 = c16[:, 0:1024].rearrange("p (j q) -> p j q", j=8)
    sel16 = c16[:, 1024:3072].rearrange("p (b q) -> p b q", b=16)
    eye16 = fw.sb([128, 16, 16], F32, "eye16")
    fw.dma("sp", eye16[:], io.eye16[:, :, :])
    SAMPLE = cfg.get("sample", True)

    PA = fw.ps([128, 1024], F32, "PA")
    PB = fw.ps([128, 1024], F32, "PB")
    PC = fw.ps([128, 512], F32, "PC")
    PD = fw.ps([128, 512], F32, "PD")
    PE = fw.ps([128, 512], F32, "PE")
    PT = fw.ps([128, 1024], BF16, "PT")
    pcd = [PC, PD]
    PT3 = PT[:, :].rearrange("p (k m) -> p k m", k=8)
    v16 = lambda r: r.rearrange("p (h q) -> p h q", h=16)

    def load_w(dst, src, kt0, kt1, q="pool", step=2):
        for k in range(kt0, kt1, step):
            k1 = min(k + step, kt1)
            fw.dma(q, dst[:, k:k1, :], src[k * 128:k1 * 128, :].rearrange("(k p) n -> p k n", p=128))

    def rmsnorm(x, g, out, M, junk):
        fw.act(junk[0:M, :], x, AF.Square, accum_out=nst[0:M, 0:1])
        fw.ts("dve", nst[0:M, 1:2], nst[0:M, 0:1], 1.0 / D, EPS, ALU.mult, ALU.add)
        fw.act(nst[0:M, 2:3], nst[0:M, 1:2], AF.Ln)
        fw.act(nst[0:M, 3:4], nst[0:M, 2:3], AF.Exp, scale=-0.5)
        fw.stt(out, x, nst[0:M, 3:4], g[0:M, :], ALU.mult, ALU.mult)

    def to_feat(src, dst, M):
        for kt in range(8):
            fw.tr(PT3[:, kt, 0:M], src[0:M, kt * 128:(kt + 1) * 128], identb[0:M, 0:M])
        fw.cp("dve", dst[:, :, 0:M], PT3[:, :, 0:M])

    def grp_rstd(src, ncol, dst, junk, M=128):
        fw.act(junk[0:M, 0:ncol], src, AF.Square, accum_out=nst[0:M, 4:5])
        fw.ts("dve", nst[0:M, 5:6], nst[0:M, 4:5], 1.0 / ncol, EPS, ALU.mult, ALU.add)
        fw.act(nst[0:M, 6:7], nst[0:M, 5:6], AF.Ln)
        fw.act(dst, nst[0:M, 6:7], AF.Exp, scale=-0.5)

    def proj_feat(W, col0, ntile, xT, M, evac):
        for gi, g0 in enumerate(range(0, ntile, 4)):
            n = min(4, ntile - g0)
            ps3 = pcd[gi % 2][:, :].rearrange("p (a m) -> p a m", a=4)
            for i in range(n):
                col = col0 + (g0 + i) * 128
                for kt in range(8):
                    fw.mm(ps3[:, i, 0:M], W[:, kt, col:col + 128], xT[:, kt, 0:M], start=kt == 0, stop=kt == 7)
            evac(g0, n, ps3[:, 0:n, 0:M])

    def conv_tiles(convin, convp, acc, ct0, n):
        for i in range(n):
            ct = ct0 + i
            fw.act(acc[:, i, :], convin[:, i, 0:128], AF.Identity, scale=convp[:, ct, 0:1], bias=convp[:, ct, 4:5])
            for j in range(1, 4):
                fw.stt(acc[:, i, :], convin[:, i, j:j + 128], convp[:, ct, j:j + 1], acc[:, i, :], ALU.mult, ALU.add)

    base_mark = fw.mark()

    if "0a" in phases:
        Wz = fw.sb([128, 8, 1024], BF16, "Wz")
        load_w(Wz, io.w_in0[:, 0:1024], 0, 8)
        Wc = fw.sb([128, 8, 1536], BF16, "Wc")
        load_w(Wc, io.w_in0[:, 1024:2560], 0, 8)
        Wdt = fw.sb([128, 8, 16], BF16, "Wdt")
        load_w(Wdt, io.w_in0[:, 3584:3600], 0, 8, step=8)
        Wo = fw.sb([128, 8, D], BF16, "Wo")
        load_w(Wo, io.w_out0[0:1024, :], 0, 8)
        gmix = fw.sb([128, D], F32, "gmix")
        fw.dma("sp", gmix[:], io.norm_mix[0, :].partition_broadcast(128))
        gssd = fw.sb([128, D], F32, "gssd")
        fw.dma("sp", gssd[:], io.ssd_norm.partition_broadcast(128))
        sm0 = fw.sb([128, 64], F32, "sm0")
        fw.dma("sp", sm0[:], io.small0.partition_broadcast(128))
        dtb_bc, D_bc = sm0[:, 0:16], sm0[:, 32:48]
        A_t = fw.sb([128, 16], F32, "A_t")
        fw.act(A_t[:], sm0[:, 16:32], AF.Exp)
        fw.ts("dve", A_t[:], A_t[:], -1.0, None, ALU.mult)
        convp = fw.sb([128, 20, 5], F32, "convp")
        fw.dma("sp", convp[:], io.convp[:, :, :])
        convin = fw.sb([128, 12, 131], F32, "convin")
        fw.memset("pool", convin[:], 0.0)
        ST = fw.sb([128, D], F32, "ST")
        fw.memset("pool", ST[:], 0.0)
        STb = fw.sb([128, D], BF16, "STb")
        fw.memset("pool", STb[:], 0.0)
        xt = fw.sb([128, D], F32, "xt")
        junk = fw.sb([128, D], F32, "junk")
        xn = fw.sb([128, D], BF16, "xn")
        xnT = fw.sb([128, 8, 128], BF16, "xnT")
        acc = fw.sb([128, 12, 128], F32, "acc")
        cact = fw.sb([128, 12, 128], BF16, "cact")
        zs = fw.sb([128, D], F32, "zs")
        xtok = fw.sb([128, D], BF16, "xtok")
        Btok = fw.sb([128, 256], BF16, "Btok")
        sm = fw.sb([128, 128], F32, "sm")
        Lh = [fw.sb([128, 4, 128], F32, f"Lh{i}") for i in range(2)]
        Eh = fw.sb([128, 4, 128], F32, "Eh")
        CBm = fw.sb([128, 2, 128], F32, "CBm")
        Wt = fw.sb([128, 16, 128], BF16, "Wt")
        t1 = fw.sb([128, D], F32, "t1")
        yn = fw.sb([128, D], BF16, "yn")
        ynT = fw.sb([128, 8, 128], BF16, "ynT")
        xw = fw.sb([128, D], BF16, "xw")
        x1 = fw.sb([128, D], F32, "x1")

        for c in range(NCH):
            fw.dma("sp", xt[:], io.xp[c * 128:(c + 1) * 128, :])
            rmsnorm(xt[:], gmix, xn[:], 128, junk)
            to_feat(xn, xnT, 128)
            proj_feat(Wc, 0, 12, xnT, 128, lambda g0, n, ps: fw.cp("act", convin[:, g0:g0 + n, 3:131], ps))
            for half in range(2):
                for kt in range(8):
                    fw.mm(PA[:, half * 512:(half + 1) * 512], xnT[:, kt, :], Wz[:, kt, half * 512:(half + 1) * 512],
                          start=kt == 0, stop=kt == 7)
            fw.act(zs[:], PA[:, :], AF.Silu)
            for kt in range(8):
                fw.mm(PE[:, 0:16], xnT[:, kt, :], Wdt[:, kt, :], start=kt == 0, stop=kt == 7)
            fw.tt("dve", sm[:, 0:16], PE[:, 0:16], dtb_bc, ALU.add)
            conv_tiles(convin, convp, acc, 0, 12)
            fw.act(cact[:], acc[:], AF.Silu)
            fw.cp("pool", convin[:, :, 0:3], convin[:, :, 128:131])
            for kt in range(8):
                fw.tr(PT3[:, kt, :], cact[:, kt, :], identb[:, :])
            fw.cp("dve", xtok[:], PT[:, :])
            for g in range(2):
                fw.tr(PT[:, g * 128:(g + 1) * 128], cact[:, 8 + g, :], identb[:, :])
            fw.cp("dve", Btok[:], PT[:, 0:256])
            fw.act(sm[:, 0:16], sm[:, 0:16], AF.Exp)
            fw.act(sm[:, 0:16], sm[:, 0:16], AF.Ln, bias=1.0)
            fw.tt("dve", sm[:, 16:32], sm[:, 0:16], A_t[:], ALU.mult)
            fw.mm(PE[:, 32:48], tri_le, sm[:, 16:32])
            fw.mm(PE[:, 48:64], ones, sm[:, 16:32])
            fw.act(sm[:, 32:48], PE[:, 32:48], AF.Exp)
            fw.cp("dve", sm[:, 64:80], PE[:, 32:48])
            fw.tt("dve", sm[:, 48:64], PE[:, 48:64], sm[:, 64:80], ALU.subtract)
            fw.act(sm[:, 48:64], sm[:, 48:64], AF.Exp)
            fw.tt("dve", sm[:, 48:64], sm[:, 48:64], sm[:, 0:16], ALU.mult)
            fw.act(sm[:, 80:96], PE[:, 48:64], AF.Exp)
            for g in range(2):
                fw.mm(PE[:, 128 + g * 128:256 + g * 128], cact[:, 8 + g, :], cact[:, 10 + g, :])
                fw.tt("dve", CBm[:, g, :], PE[:, 128 + g * 128:256 + g * 128], tri_le, ALU.mult)
            for hq in range(4):
                L = Lh[hq % 2]
                ps3 = pcd[hq % 2][:, :].rearrange("p (a m) -> p a m", a=4)
                for i in range(4):
                    h = hq * 4 + i
                    fw.ts("pool", L[:, i, :], mask_gt, sm[:, 16 + h:17 + h], None, ALU.mult)
                    fw.mm(ps3[:, i, :], L[:, i, :], tri_le)
                fw.act(Eh[:], ps3, AF.Exp)
                for i in range(4):
                    h = hq * 4 + i
                    fw.stt(Wt[:, h, :], Eh[:, i, :], sm[:, h:h + 1], CBm[:, h // 8, :], ALU.mult, ALU.mult)
            for h in range(16):
                fw.mm(PA[:, h * 64:(h + 1) * 64], Wt[:, h, :], xtok[:, h * 64:(h + 1) * 64])
            for g in range(2):
                fw.mm(PB[:, g * 512:(g + 1) * 512], cact[:, 10 + g, :], STb[:, g * 512:(g + 1) * 512])
            fw.tt("dve", v16(t1[:, :]), v16(PB[:, :]), sm[:, 32:48].bc(2, 64), ALU.mult)
            fw.tt("dve", t1[:], t1[:], PA[:, :], ALU.add)
            fw.tt("pool", v16(junk[:, :]), v16(xtok[:, :]), D_bc.bc(2, 64), ALU.mult)
            fw.tt("dve", t1[:], t1[:], junk[:], ALU.add)
            fw.tt("dve", t1[:], t1[:], zs[:], ALU.mult)
            for g in range(2):
                grp_rstd(t1[:, g * 512:(g + 1) * 512], 512, nst[:, 7:8], junk)
                fw.stt(yn[:, g * 512:(g + 1) * 512], t1[:, g * 512:(g + 1) * 512], nst[:, 7:8],
                       gssd[:, g * 512:(g + 1) * 512], ALU.mult, ALU.mult)
            to_feat(yn, ynT, 128)
            fw.tt("pool", v16(xw[:, :]), v16(xtok[:, :]), sm[:, 48:64].bc(2, 64), ALU.mult)
            for g in range(2):
                fw.mm(PB[:, g * 512:(g + 1) * 512], Btok[:, g * 128:(g + 1) * 128], xw[:, g * 512:(g + 1) * 512])
            fw.tt("dve", v16(ST[:, :]), v16(ST[:, :]), sm[:, 80:96].bc(2, 64), ALU.mult)
            fw.tt("dve", ST[:], ST[:], PB[:, :], ALU.add)
            fw.cp("pool", STb[:], ST[:])
            for half in range(2):
                for kt in range(8):
                    fw.mm(PA[:, half * 512:(half + 1) * 512], ynT[:, kt, :], Wo[:, kt, half * 512:(half + 1) * 512],
                          start=kt == 0, stop=kt == 7)
            fw.tt("dve", x1[:], xt[:], PA[:, :], ALU.add)
            fw.dma("sp", io.s1[c * 128:(c + 1) * 128, :], x1[:])

        if SAMPLE:
            dtcol = fw.sb([16, 4], F32, "dtcol")
            fw.dma("sp", dtcol[:], io.dtcol[:, :])
            fw.act(dtcol[:, 3:4], dtcol[:, 1:2], AF.Exp)
            fw.ts("dve", dtcol[:, 3:4], dtcol[:, 3:4], -1.0, None, ALU.mult)
            cst_s = fw.sb([128, 12, 3, NB], F32, "cst_s")
            fw.dma("sp", cst_s[:], io.conv_s_in[:, 0:12, :, :])
            uS = fw.sb([128, 12, NB], F32, "uS")
            accs = fw.sb([128, 12, NB], F32, "accs")
            tmps = fw.sb([128, 12, NB], F32, "tmps")
            cs = fw.sb([128, 12, NB], F32, "cs")
            zsT = fw.sb([128, 8, NB], F32, "zsT")
            dd = fw.sb([16, 48], F32, "dd")
            dx = fw.sb([128, 8, 48], F32, "dx")
            dtx = fw.sb([128, 8, NB], F32, "dtx")
            BCtok = fw.sb([16, 512], F32, "BCtok")
            Sb = [fw.sb([128, 8, 128], F32, f"Sb{i}") for i in range(2)]
            T1s = fw.sb([128, 8, 128], F32, "T1s")
            ysT = fw.sb([128, 8, NB], F32, "ysT")
            fw.dma("sp", xt[0:NB, :], io.xs[:, :])
            rmsnorm(xt[0:NB, :], gmix, xn[0:NB, :], NB, junk)
            to_feat(xn, xnT, NB)
            proj_feat(Wc, 0, 12, xnT, NB, lambda g0, n, ps: fw.cp("act", uS[:, g0:g0 + n, :], ps))
            proj_feat(Wz, 0, 8, xnT, NB, lambda g0, n, ps: fw.act(zsT[:, g0:g0 + n, :], ps, AF.Silu))
            wv = lambda j: convp[:, 0:12, j].bc(2, NB)
            fw.tt("dve", accs[:], cst_s[:, :, 0, :], wv(0), ALU.mult)
            fw.tt("dve", accs[:], accs[:], wv(4), ALU.add)
            for j in (1, 2):
                fw.tt("dve", tmps[:], cst_s[:, :, j, :], wv(j), ALU.mult)
                fw.tt("dve", accs[:], accs[:], tmps[:], ALU.add)
            fw.tt("dve", tmps[:], uS[:], wv(3), ALU.mult)
            fw.tt("dve", accs[:], accs[:], tmps[:], ALU.add)
            fw.act(cs[:], accs[:], AF.Silu)
            fw.dma("sp", io.conv_s[:, 0:12, 0:2, :], cst_s[:, :, 1:3, :])
            fw.dma("sp", io.conv_s[:, 0:12, 2, :], uS[:])
            for kt in range(8):
                fw.mm(PE[0:16, 0:16], Wdt[:, kt, :], xnT[:, kt, 0:NB], start=kt == 0, stop=kt == 7)
            fw.ts("dve", dd[:, 0:16], PE[0:16, 0:16], dtcol[:, 0:1], None, ALU.add)
            fw.act(dd[:, 0:16], dd[:, 0:16], AF.Exp)
            fw.act(dd[:, 0:16], dd[:, 0:16], AF.Ln, bias=1.0)
            fw.ts("dve", dd[:, 16:32], dd[:, 0:16], dtcol[:, 3:4], None, ALU.mult)
            fw.act(dd[:, 16:32], dd[:, 16:32], AF.Exp)
            fw.ts("dve", dd[:, 32:48], ones[0:16, 0:16], dtcol[:, 2:3], None, ALU.mult)
            for j in range(8):
                fw.mm(PE[:, 128 + j * 48:128 + (j + 1) * 48], exp16[:, j, :], dd[:, :])
            fw.cp("dve", dx[:], PE[:, 128:512].rearrange("p (j c) -> p j c", j=8))
            fw.tt("dve", dtx[:], dx[:, :, 0:16], cs[:, 0:8, :], ALU.mult)
            for i in range(4):
                fw.tr(PD[0:16, i * 128:(i + 1) * 128], cs[:, 8 + i, :], ident)
            fw.cp("dve", BCtok[:], PD[0:16, :])
            for b in range(NB):
                S = Sb[b % 2]
                fw.dma("sp", S[:], io.ssm_s_in[b].rearrange("(j q) n -> q j n", q=128))
                fw.mm(PC[:, :], sel16[:, b, :], BCtok[:, :])
                for g in range(2):
                    fw.tt("dve", T1s[:, 4 * g:4 * g + 4, :], PC[:, g * 128:(g + 1) * 128].bc(1, 4),
                          dtx[:, 4 * g:4 * g + 4, b].bc(2, 128), ALU.mult)
                fw.tt("pool", S[:], S[:], dx[:, :, 16 + b].bc(2, 128), ALU.mult)
                fw.tt("dve", S[:], S[:], T1s[:], ALU.add)
                fw.dma("sp", io.ssm_s[b].rearrange("(j q) n -> q j n", q=128), S[:])
                for g in range(2):
                    fw.tt("dve", T1s[:, 4 * g:4 * g + 4, :], S[:, 4 * g:4 * g + 4, :],
                          PC[:, 256 + g * 128:256 + (g + 1) * 128].bc(1, 4), ALU.mult)
                fw.red(ysT[:, :, b], T1s[:], ALU.add)
            fw.tt("dve", dtx[:], dx[:, :, 32:48], cs[:, 0:8, :], ALU.mult)
            fw.tt("dve", ysT[:], ysT[:], dtx[:], ALU.add)
            fw.tt("dve", ysT[:], ysT[:], zsT[:], ALU.mult)
            for j in range(8):
                fw.tr(PA[0:16, j * 128:(j + 1) * 128], ysT[:, j, :], ident)
            fw.cp("dve", t1[0:NB, :], PA[0:NB, :])
            for g in range(2):
                grp_rstd(t1[0:NB, g * 512:(g + 1) * 512], 512, nst[0:NB, 7:8], junk, NB)
                fw.stt(yn[0:NB, g * 512:(g + 1) * 512], t1[0:NB, g * 512:(g + 1) * 512], nst[0:NB, 7:8],
                       gssd[0:NB, g * 512:(g + 1) * 512], ALU.mult, ALU.mult)
            to_feat(yn, ynT, NB)
            for half in range(2):
                for kt in range(8):
                    fw.mm(PA[0:NB, half * 512:(half + 1) * 512], ynT[:, kt, 0:NB], Wo[:, kt, half * 512:(half + 1) * 512],
                          start=kt == 0, stop=kt == 7)
            fw.tt("dve", x1[0:NB, :], xt[0:NB, :], PA[0:NB, :], ALU.add)
            fw.dma("sp", io.s1s[:, :], x1[0:NB, :])
        fw.dma("sp", io.ssm_p[:, :], ST[:])
        fw.dma("sp", io.conv_p[:, 0:12, :], convin[:, :, 0:3])
        fw.release(base_mark)

    if "0b" in phases:
        Wx = fw.sb([128, 8, 1024], BF16, "Wx")
        load_w(Wx, io.w_in0[:, 2560:3584], 0, 8)
        Wg = fw.sb([128, 8, 1024], BF16, "Wg")
        load_w(Wg, io.w_in0[:, 3600:4624], 0, 8)
        Wif = fw.sb([128, 8, 16], BF16, "Wif")
        load_w(Wif, io.w_in0[:, 4616:4632], 0, 8, step=8)
        Wo = fw.sb([128, 8, D], BF16, "Wo")
        load_w(Wo, io.w_out0[1024:2048, :], 0, 8)
        BDq = fw.sb([128, 8, 128], BF16, "BDq")
        BDk = fw.sb([128, 8, 128], BF16, "BDk")
        BDv = fw.sb([128, 8, 128], BF16, "BDv")
        fw.dma("pool", BDq[:], io.bdq[:, :, :])
        fw.dma("pool", BDk[:], io.bdk[:, :, :])
        fw.dma("pool", BDv[:], io.bdv[:, :, :])
        gmix = fw.sb([128, D], F32, "gmix")
        fw.dma("sp", gmix[:], io.norm_mix[0, :].partition_broadcast(128))
        sm0 = fw.sb([128, 64], F32, "sm0")
        fw.dma("sp", sm0[:], io.small0.partition_broadcast(128))
        ib_bc, fb_bc = sm0[:, 48:52], sm0[:, 52:56]
        convp = fw.sb([128, 20, 5], F32, "convp")
        fw.dma("sp", convp[:], io.convp[:, :, :])
        mlcol = fw.sb([128, 8, 2], F32, "mlcol")
        fw.dma("sp", mlcol[:], io.mlcol[:, :, :])
        convin = fw.sb([128, 8, 131], F32, "convin")
        fw.memset("pool", convin[:], 0.0)
        Cst = fw.sb([128, 2, 4, 264], F32, "Cst")
        fw.memset("pool", Cst[:], 0.0)
        Cb = fw.sb([128, 2, 4, 264], BF16, "Cb")
        fw.memset("pool", Cb[:], 0.0)
        mprev = fw.sb([128, 4], F32, "mprev")
        fw.memset("pool", mprev[:], 0.0)
        xt = fw.sb([128, D], F32, "xt")
        junk = fw.sb([128, D], F32, "junk")
        xn = fw.sb([128, D], BF16, "xn")
        xnT = fw.sb([128, 8, 128], BF16, "xnT")
        acc = fw.sb([128, 8, 128], F32, "acc")
        cact = fw.sb([128, 8, 128], BF16, "cact")
        xmraw = fw.sb([128, 8, 128], BF16, "xmraw")
        sigoT = fw.sb([128, 8, 128], BF16, "sigoT")
        sm2 = fw.sb([128, 64], F32, "sm2")
        qT = fw.sb([128, 8, 128], BF16, "qT")
        kT = fw.sb([128, 8, 128], BF16, "kT")
        vtok = fw.sb([128, 4, 264], BF16, "vtok")
        fw.memset("pool", vtok[:], 1.0)
        kw_ = fw.sb([128, 4, 256], BF16, "kw")
        Rh = fw.sb([128, 128], F32, "Rh")
        dlm = fw.sb([128, 128], F32, "dlm")
        Dm = fw.sb([128, 128], F32, "Dm")
        Sg = fw.sb([128, 128], BF16, "Sg")
        SgT = fw.sb([128, 128], BF16, "SgT")
        hs = fw.sb([128, 16], F32, "hs")
        mt = fw.sb([128, 16], F32, "mt")
        fw.memset("pool", mt[:], 0.0)
        fw.memset("pool", sm2[:], 0.0)
        comb = fw.sb([128, 258], F32, "comb")
        hh = fw.sb([128, 256], F32, "hh")
        hmn = fw.sb([128, D], BF16, "hmn")
        hmnT = fw.sb([128, 8, 128], BF16, "hmnT")
        hmfT = fw.sb([128, 8, 128], BF16, "hmfT")
        x1 = fw.sb([128, D], F32, "x1")

        lvl = cfg.get('lvl', 99)
        for c in range(NCH):
            fw.dma("sp", xt[:], io.xp[c * 128:(c + 1) * 128, :])
            fw.dma("sp", x1[:], io.s1[c * 128:(c + 1) * 128, :])
            rmsnorm(xt[:], gmix, xn[:], 128, junk)
            to_feat(xn, xnT, 128)
            proj_feat(Wx, 0, 8, xnT, 128, lambda g0, n, ps: fw.cp("act", convin[:, g0:g0 + n, 3:131], ps))
            proj_feat(Wg, 0, 8, xnT, 128, lambda g0, n, ps: fw.act(sigoT[:, g0:g0 + n, :], ps, AF.Sigmoid))
            for kt in range(8):
                fw.mm(PE[:, 16:32], xnT[:, kt, :], Wif[:, kt, :], start=kt == 0, stop=kt == 7)
            fw.tt("dve", sm2[:, 0:4], PE[:, 24:28], ib_bc, ALU.add)
            fw.tt("dve", sm2[:, 4:8], PE[:, 28:32], fb_bc, ALU.add)
            conv_tiles(convin, convp, acc, 12, 8)
            fw.act(cact[:], acc[:], AF.Silu)
            fw.cp("pool", xmraw[:], convin[:, :, 3:131])
            fw.cp("pool", convin[:, :, 0:3], convin[:, :, 128:131])
            if lvl < 2:
                continue
            for tile in range(8):
                ps = pcd[tile % 2]
                fw.mm(ps[:, 0:128], (Wx[:, tile, 0:128] if cfg.get('alt') else BDq[:, tile, :]), cact[:, tile, :])
                fw.mm(ps[:, 128:256], (Wx[:, tile, 0:128] if cfg.get('alt') else BDk[:, tile, :]), cact[:, tile, :])
                if cfg.get('alt') != 2:
                    fw.cp("dve", qT[:, tile, :], ps[:, 0:128])
                if cfg.get('alt') not in (2, 3):
                    fw.ts("dve", kT[:, tile, :], ps[:, 128:256], 0.0625, None, ALU.mult)
            if lvl < 2.1:
                continue
            for tile in range(8):
                fw.mm(PA[:, tile * 128:(tile + 1) * 128], xmraw[:, tile, :], BDv[:, tile, :])
                fw.mm(PB[:, tile * 128:(tile + 1) * 128], cact[:, tile, :], BDk[:, tile, :])
            if lvl < 2.2:
                continue
            fw.cp("act", vtok[:, :, 0:256], PA[:, :].rearrange("p (h v) -> p h v", h=4))
            if lvl < 2.3:
                continue
            fw.act(sm2[:, 4:8], sm2[:, 4:8], AF.Exp, scale=-1.0)
            fw.act(sm2[:, 4:8], sm2[:, 4:8], AF.Ln, bias=1.0)
            fw.ts("dve", sm2[:, 4:8], sm2[:, 4:8], -1.0, None, ALU.mult)
            fw.mm(PE[:, 64:80], tri_le, sm2[:, 0:16])
            fw.mm(PE[:, 96:112], ones, sm2[:, 0:16])
            fw.cp("dve", sm2[:, 8:12], PE[:, 68:72])
            fw.cp("dve", sm2[:, 12:16], PE[:, 100:104])
            fw.tt("dve", sm2[:, 16:20], sm2[:, 8:12], mprev[:], ALU.add)
            if lvl < 3:
                continue
            for h in range(4):
                fw.ts("pool", Rh[:], mask_gt, sm2[:, 4 + h:5 + h], None, ALU.mult)
                fw.stt(Rh[:], ident, sm2[:, h:h + 1], Rh[:], ALU.mult, ALU.add)
                fw.mm(PD[:, 0:128], tri_le, Rh[:])
                fw.tt("dve", dlm[:], PD[:, 0:128], negmask, ALU.add)
                fw.red(hs[:, 0:1], dlm[:], ALU.max)
                fw.tt("dve", mt[:, h:h + 1], hs[:, 0:1], sm2[:, 16 + h:17 + h], ALU.max)
                fw.ts("dve", hs[:, 1:2], mt[:, h:h + 1], -1.0, None, ALU.mult)
                fw.act(Dm[:], dlm[:], AF.Exp, bias=hs[:, 1:2])
                fw.mm(PD[:, 128:256], qT[:, 2 * h, :], kT[:, 2 * h, :], start=True, stop=False)
                fw.mm(PD[:, 128:256], qT[:, 2 * h + 1, :], kT[:, 2 * h + 1, :], start=False, stop=True)
                fw.tt("dve", Sg[:], PD[:, 128:256], Dm[:], ALU.mult)
                fw.tr(PT[:, 0:128], Sg[:], identb[:, :])
                fw.cp("dve", SgT[:], PT[:, 0:128])
                fw.mm(PA[:, 0:258], SgT[:], vtok[:, h, 0:258])
                fw.mm(PA[:, 512:770], qT[:, 2 * h, :], Cb[:, 0, h, 0:258], start=True, stop=False)
                fw.mm(PA[:, 512:770], qT[:, 2 * h + 1, :], Cb[:, 1, h, 0:258], start=False, stop=True)
                fw.act(hs[:, 2:3], sm2[:, 16 + h:17 + h], AF.Exp, bias=hs[:, 1:2])
                fw.act(comb[:], PA[:, 512:770], AF.Copy, scale=hs[:, 2:3])
                fw.tt("dve", comb[:], comb[:], PA[:, 0:258], ALU.add)
                fw.act(hs[:, 3:4], mt[:, h:h + 1], AF.Exp, scale=-1.0)
                fw.ts("dve", hs[:, 6:7], comb[:, 256:257], -1.0, None, ALU.mult)
                fw.tt("dve", hs[:, 6:7], hs[:, 6:7], comb[:, 256:257], ALU.max)
                fw.tt("dve", hs[:, 4:5], hs[:, 6:7], hs[:, 3:4], ALU.max)
                fw.recip(hs[:, 5:6], hs[:, 4:5])
                fw.ts("dve", hh[:], comb[:, 0:256], hs[:, 5:6], None, ALU.mult)
                grp_rstd(hh[:], 256, nst[:, 7:8], junk)
                fw.ts("dve", hmn[:, h * 256:(h + 1) * 256], hh[:], nst[:, 7:8], None, ALU.mult)
            if lvl < 4:
                continue
            to_feat(hmn, hmnT, 128)
            for tile in range(8):
                fw.ts("dve", hmfT[:, tile, :], hmnT[:, tile, :], mlcol[:, tile, 0:1], None, ALU.mult)
                fw.stt(hmfT[:, tile, :], cact[:, tile, :], mlcol[:, tile, 1:2], hmfT[:, tile, :], ALU.mult, ALU.add)
            fw.tt("dve", hmfT[:], hmfT[:], sigoT[:], ALU.mult)
            if lvl < 5:
                continue
            fw.mm(PE[:, 112:128], sel127, mt[:])
            fw.cp("dve", sm2[:, 20:24], PE[:, 112:116])
            fw.tt("dve", sm2[:, 24:28], sm2[:, 12:16], sm2[:, 8:12], ALU.subtract)
            fw.tt("dve", sm2[:, 24:28], sm2[:, 24:28], sm2[:, 0:4], ALU.add)
            fw.tt("dve", sm2[:, 24:28], sm2[:, 24:28], sm2[:, 20:24], ALU.subtract)
            fw.act(sm2[:, 28:32], sm2[:, 24:28], AF.Exp)
            fw.ts("dve", sm2[:, 28:32], sm2[:, 28:32], 0.0625, None, ALU.mult)
            fw.tt("dve", sm2[:, 32:36], sm2[:, 12:16], mprev[:], ALU.add)
            fw.tt("dve", sm2[:, 32:36], sm2[:, 32:36], sm2[:, 20:24], ALU.subtract)
            fw.act(sm2[:, 32:36], sm2[:, 32:36], AF.Exp)
            fw.tt("dve", kw_[:], PB[:, :].rearrange("p (h d) -> p h d", h=4), sm2[:, 28:32].bc(2, 256), ALU.mult)
            for kt in range(2):
                for h in range(4):
                    fw.mm(PB[:, h * 256:(h + 1) * 256], kw_[:, h, kt * 128:(kt + 1) * 128], vtok[:, h, 0:256])
                    fw.mm(PE[:, 80 + 2 * h:82 + 2 * h], kw_[:, h, kt * 128:(kt + 1) * 128], onesb[:, 0:2])
                for h in range(4):
                    fw.stt(Cst[:, kt, h, 0:256], Cst[:, kt, h, 0:256], sm2[:, 32 + h:33 + h],
                           PB[:, h * 256:(h + 1) * 256], ALU.mult, ALU.add)
                    fw.stt(Cst[:, kt, h, 256:257], Cst[:, kt, h, 256:257], sm2[:, 32 + h:33 + h],
                           PE[:, 80 + 2 * h:81 + 2 * h], ALU.mult, ALU.add)
            fw.cp("pool", Cb[:], Cst[:])
            fw.cp("dve", mprev[:], sm2[:, 20:24])
            if lvl < 6:
                continue
            for half in range(2):
                for kt in range(8):
                    fw.mm(PA[:, half * 512:(half + 1) * 512], hmfT[:, kt, :], Wo[:, kt, half * 512:(half + 1) * 512],
                          start=kt == 0, stop=kt == 7)
            fw.tt("dve", x1[:], x1[:], PA[:, :], ALU.add)
            fw.dma("sp", io.s1[c * 128:(c + 1) * 128, :], x1[:])

        if SAMPLE:
            cst_s = fw.sb([128, 8, 3, NB], F32, "cst_s")
            fw.dma("sp", cst_s[:], io.conv_s_in[:, 12:20, :, :])
            uS = fw.sb([128, 8, NB], F32, "uS")
            accs = fw.sb([128, 8, NB], F32, "accs")
            tmps = fw.sb([128, 8, NB], F32, "tmps")
            cs = fw.sb([128, 8, NB], F32, "cs")
            cs_bf = fw.sb([128, 8, NB], BF16, "cs_bf")
            us_bf = fw.sb([128, 8, NB], BF16, "us_bf")
            qTs = fw.sb([128, 8, NB], F32, "qTs")
            kTs = fw.sb([128, 8, NB], F32, "kTs")
            kws = fw.sb([128, 8, NB], F32, "kws")
            nS = fw.sb([128, 8, NB], F32, "nS")
            vtoks = fw.sb([16, D], F32, "vtoks")
            g16 = fw.sb([16, 64], F32, "g16")
            Zd = fw.sb([16, 128], F32, "Zd")
            wd = fw.sb([128, 2, 4, NB], F32, "wd")
            qmask = fw.sb([128, 8, NB, NB], F32, "qmask")
            Cs = [fw.sb([128, 8, 256], F32, f"Cs{i}") for i in range(2)]
            Tt = fw.sb([128, 8, 256], F32, "Tt")
            numt = fw.sb([16, D], F32, "numt")
            fw.dma("sp", xt[0:NB, :], io.xs[:, :])
            fw.dma("sp", x1[0:NB, :], io.s1s[:, :])
            fw.dma("sp", g16[:, 8:12], io.mm_s_in[:, :])
            fw.dma("sp", nS[:], io.mn_s_in[:, :, :])
            rmsnorm(xt[0:NB, :], gmix, xn[0:NB, :], NB, junk)
            to_feat(xn, xnT, NB)
            proj_feat(Wx, 0, 8, xnT, NB, lambda g0, n, ps: fw.cp("act", uS[:, g0:g0 + n, :], ps))
            proj_feat(Wg, 0, 8, xnT, NB, lambda g0, n, ps: fw.act(sigoT[:, g0:g0 + n, 0:NB], ps, AF.Sigmoid))
            for kt in range(8):
                fw.mm(PE[0:NB, 16:32], xnT[:, kt, 0:NB], Wif[:, kt, :], start=kt == 0, stop=kt == 7)
            fw.tt("dve", g16[:, 0:4], PE[0:NB, 24:28], ib_bc[0:NB, :], ALU.add)
            fw.tt("dve", g16[:, 4:8], PE[0:NB, 28:32], fb_bc[0:NB, :], ALU.add)
            fw.act(g16[:, 4:8], g16[:, 4:8], AF.Exp, scale=-1.0)
            fw.act(g16[:, 4:8], g16[:, 4:8], AF.Ln, bias=1.0)
            fw.ts("dve", g16[:, 4:8], g16[:, 4:8], -1.0, None, ALU.mult)
            wv = lambda j: convp[:, 12:20, j].bc(2, NB)
            fw.tt("dve", accs[:], cst_s[:, :, 0, :], wv(0), ALU.mult)
            fw.tt("dve", accs[:], accs[:], wv(4), ALU.add)
            for j in (1, 2):
                fw.tt("dve", tmps[:], cst_s[:, :, j, :], wv(j), ALU.mult)
                fw.tt("dve", accs[:], accs[:], tmps[:], ALU.add)
            fw.tt("dve", tmps[:], uS[:], wv(3), ALU.mult)
            fw.tt("dve", accs[:], accs[:], tmps[:], ALU.add)
            fw.act(cs[:], accs[:], AF.Silu)
            fw.dma("sp", io.conv_s[:, 12:20, 0:2, :], cst_s[:, :, 1:3, :])
            fw.dma("sp", io.conv_s[:, 12:20, 2, :], uS[:])
            fw.cp("dve", cs_bf[:], cs[:])
            fw.cp("dve", us_bf[:], uS[:])
            for tile in range(8):
                ps = pcd[tile % 2]
                fw.mm(ps[:, 0:NB], BDq[:, tile, :], cs_bf[:, tile, :])
                fw.mm(ps[:, 16:16 + NB], BDk[:, tile, :], cs_bf[:, tile, :])
                fw.cp("dve", qTs[:, tile, :], ps[:, 0:NB])
                fw.ts("dve", kTs[:, tile, :], ps[:, 16:16 + NB], 0.0625, None, ALU.mult)
            for tile in range(8):
                fw.mm(PA[0:NB, tile * 128:(tile + 1) * 128], us_bf[:, tile, :], BDv[:, tile, :])
            fw.cp("act", vtoks[:], PA[0:NB, :])
            fw.tt("dve", g16[:, 16:20], g16[:, 4:8], g16[:, 8:12], ALU.add)
            fw.tt("dve", g16[:, 12:16], g16[:, 16:20], g16[:, 0:4], ALU.max)
            fw.dma("sp", io.mm_s[:, :], g16[:, 12:16])
            fw.tt("dve", g16[:, 20:24], g16[:, 0:4], g16[:, 12:16], ALU.subtract)
            fw.act(g16[:, 20:24], g16[:, 20:24], AF.Exp)
            fw.tt("dve", g16[:, 24:28], g16[:, 16:20], g16[:, 12:16], ALU.subtract)
            fw.act(g16[:, 24:28], g16[:, 24:28], AF.Exp)
            fw.act(g16[:, 28:32], g16[:, 12:16], AF.Exp, scale=-1.0)
            z3 = lambda r: r.rearrange("p (h b) -> p h b", h=4)
            fw.tt("dve", z3(Zd[:, 0:64]), g16[:, 20:24].bc(2, NB), ident[0:NB, 0:NB].bc(1, 4), ALU.mult)
            fw.tt("dve", z3(Zd[:, 64:128]), g16[:, 24:28].bc(2, NB), ident[0:NB, 0:NB].bc(1, 4), ALU.mult)
            fw.mm(PE[:, 128:256], ones[0:NB, :], Zd[:, :])
            fw.cp("dve", wd[:], PE[:, 128:256].rearrange("p (w h b) -> p w h b", w=2, h=4))
            k4 = lambda r: r.rearrange("p (h k) b -> p h k b", h=4)
            fw.tt("dve", k4(kws[:, :, :]), k4(kTs[:, :, :]), wd[:, 0, :, :].bc(2, 2), ALU.mult)
            fw.tt("dve", k4(nS[:, :, :]), k4(nS[:, :, :]), wd[:, 1, :, :].bc(2, 2), ALU.mult)
            fw.tt("dve", nS[:], nS[:], kws[:], ALU.add)
            fw.dma("sp", io.mn_s[:, :, :], nS[:])
            fw.tt("dve", tmps[:], qTs[:], nS[:], ALU.mult)
            for h in range(4):
                for kt in range(2):
                    fw.mm(PE[0:NB, 256 + 2 * h:258 + 2 * h], tmps[:, 2 * h + kt, :], ones[:, 0:2], start=kt == 0, stop=kt == 1)
            fw.tt("dve", qmask[:], qTs[:, :, :].bc(2, NB), eye16[:, :, :].bc(1, 8), ALU.mult)
            for b in range(NB):
                Cc = Cs[b % 2]
                fw.dma("sp", Cc[:], io.mc_s_in[b].rearrange("h (k p) v -> p (h k) v", p=128))
                fw.mm(PA[:, 0:512], sel16[:, b, :], vtoks[:, 0:512])
                fw.mm(PA[:, 512:1024], sel16[:, b, :], vtoks[:, 512:1024])
                fw.tt("dve", Tt[:, :, :].rearrange("p (h k) v -> p h k v", h=4),
                      PA[:, :].rearrange("p (h v) -> p h v", h=4).bc(2, 2),
                      kws[:, :, b].rearrange("p (h k) -> p h k", h=4).bc(3, 256), ALU.mult)
                fw.tt("pool", Cc[:, :, :].rearrange("p (h k) v -> p h (k v)", h=4),
                      Cc[:, :, :].rearrange("p (h k) v -> p h (k v)", h=4), wd[:, 1, :, b].bc(2, 512), ALU.mult)
                fw.tt("dve", Cc[:], Cc[:], Tt[:], ALU.add)
                fw.dma("sp", io.mc_s[b].rearrange("h (k p) v -> p (h k) v", p=128), Cc[:])
                for tile in range(8):
                    h, kt = tile // 2, tile % 2
                    fw.mm(PB[0:NB, h * 256:(h + 1) * 256], qmask[:, tile, b, :], Cc[:, tile, :],
                          start=(b == 0 and tile in (0, 4)), stop=(b == NB - 1 and kt == 1))
            fw.cp("act", numt[:], PB[0:NB, :])
            if "dbg_a" in dbg:
                fw.dma("sp", io.dbg_a[:, :], numt[:])
                fw.cp("dve", g16[:, 40:44], PE[0:NB, 256:264].rearrange("p (h t) -> p h t", t=2)[:, :, 0])
                fw.dma("sp", io.dbg_b[:, :], g16[:])
            dn = PE[0:NB, 256:264].rearrange("p (h t) -> p h t", t=2)[:, :, 0]
            fw.ts("dve", g16[:, 32:36], dn, -1.0, None, ALU.mult)
            fw.tt("dve", g16[:, 32:36], g16[:, 32:36], dn, ALU.max)
            fw.tt("dve", g16[:, 32:36], g16[:, 32:36], g16[:, 28:32], ALU.max)
            fw.recip(g16[:, 36:40], g16[:, 32:36])
            fw.tt("dve", numt[:, :].rearrange("p (h v) -> p h v", h=4), numt[:, :].rearrange("p (h v) -> p h v", h=4),
                  g16[:, 36:40].bc(2, 256), ALU.mult)
            for h in range(4):
                grp_rstd(numt[:, h * 256:(h + 1) * 256], 256, nst[0:NB, 7:8], junk, NB)
                fw.ts("dve", hmn[0:NB, h * 256:(h + 1) * 256], numt[:, h * 256:(h + 1) * 256], nst[0:NB, 7:8], None, ALU.mult)
            to_feat(hmn, hmnT, NB)
            for tile in range(8):
                fw.ts("dve", hmfT[:, tile, 0:NB], hmnT[:, tile, 0:NB], mlcol[:, tile, 0:1], None, ALU.mult)
                fw.stt(hmfT[:, tile, 0:NB], cs_bf[:, tile, :], mlcol[:, tile, 1:2], hmfT[:, tile, 0:NB], ALU.mult, ALU.add)
            fw.tt("dve", hmfT[:, :, 0:NB], hmfT[:, :, 0:NB], sigoT[:, :, 0:NB], ALU.mult)
            for half in range(2):
                for kt in range(8):
                    fw.mm(PA[0:NB, half * 512:(half + 1) * 512], hmfT[:, kt, 0:NB], Wo[:, kt, half * 512:(half + 1) * 512],
                          start=kt == 0, stop=kt == 7)
            fw.tt("dve", x1[0:NB, :], x1[0:NB, :], PA[0:NB, :], ALU.add)
            fw.dma("sp", io.s1s[:, :], x1[0:NB, :])
        fw.dma("sp", io.mc_p[:, :, :, :], Cst[:])
        fw.dma("sp", io.mm_p[:, :], mprev[0:1, :])
        fw.dma("sp", io.conv_p[:, 12:20, :], convin[:, :, 0:3])
        fw.release(base_mark)

    def ffn_phase(layer, src, dst, ssrc, sdst, final):
        Wgu = fw.sb([128, 8, 2 * DFF], BF16, "Wgu")
        load_w(Wgu, io.w_gu[layer], 0, 8, step=1)
        Wd = fw.sb([128, 22, D], BF16, "Wd")
        load_w(Wd, io.w_dn[layer], 0, 22)
        gf = fw.sb([128, D], F32, "gf")
        fw.dma("sp", gf[:], io.norm_ffn[layer, :].partition_broadcast(128))
        if final:
            gfin = fw.sb([128, D], F32, "gfin")
            fw.dma("sp", gfin[:], io.norm_final.partition_broadcast(128))
        xt = fw.sb([128, D], F32, "xt")
        junk = fw.sb([128, D], F32, "junk")
        xn = fw.sb([128, D], BF16, "xn")
        xnT = fw.sb([128, 8, 128], BF16, "xnT")
        hT = fw.sb([128, 22, 128], BF16, "hT")
        sg = [fw.sb([128, 128], F32, f"sg{i}") for i in range(2)]
        x2 = fw.sb([128, D], F32, "x2")
        yo = fw.sb([128, D], F32, "yo")
        for c in range(NCH + (1 if SAMPLE else 0)):
            M = 128 if c < NCH else NB
            srcc = src[c * 128:(c + 1) * 128, :] if c < NCH else ssrc[:, :]
            dstc = dst[c * 128:(c + 1) * 128, :] if c < NCH else sdst[:, :]
            fw.dma("sp", xt[0:M, :], srcc)
            rmsnorm(xt[0:M, :], gf, xn[0:M, :], M, junk)
            to_feat(xn, xnT, M)
            for j in range(22):
                ps = pcd[j % 2]
                for kt in range(8):
                    fw.mm(ps[:, 0:M], Wgu[:, kt, j * 128:(j + 1) * 128], xnT[:, kt, 0:M], start=kt == 0, stop=kt == 7)
                for kt in range(8):
                    fw.mm(ps[:, 128:128 + M], Wgu[:, kt, DFF + j * 128:DFF + (j + 1) * 128], xnT[:, kt, 0:M],
                          start=kt == 0, stop=kt == 7)
                fw.act(sg[j % 2][:, 0:M], ps[:, 0:M], AF.Silu)
                fw.tt("dve", hT[:, j, 0:M], sg[j % 2][:, 0:M], ps[:, 128:128 + M], ALU.mult)
            for half in range(2):
                for j in range(22):
                    fw.mm(PA[0:M, half * 512:(half + 1) * 512], hT[:, j, 0:M], Wd[:, j, half * 512:(half + 1) * 512],
                          start=j == 0, stop=j == 21)
            fw.tt("dve", x2[0:M, :], xt[0:M, :], PA[0:M, :], ALU.add)
            if final:
                rmsnorm(x2[0:M, :], gfin, yo[0:M, :], M, junk)
                fw.dma("sp", dstc, yo[0:M, :])
            else:
                fw.dma("sp", dstc, x2[0:M, :])
        fw.release(base_mark)

    if "1" in phases:
        ffn_phase(0, io.s1, io.s2, io.s1s, io.s2s, False)

    if "2" in phases:
        Wr = fw.sb([128, 8, D], BF16, "Wr"); load_w(Wr, io.rw_wr, 0, 8)
        Wk = fw.sb([128, 8, D], BF16, "Wk"); load_w(Wk, io.rw_wk, 0, 8)
        Wv = fw.sb([128, 8, D], BF16, "Wv"); load_w(Wv, io.rw_wv, 0, 8)
        Wo = fw.sb([128, 8, D], BF16, "Wo"); load_w(Wo, io.rw_wo, 0, 8)
        W1 = fw.sb([128, 8, 64], BF16, "W1"); load_w(W1, io.rw_w1, 0, 8, step=8)
        A1 = fw.sb([128, 8, 64], BF16, "A1"); load_w(A1, io.rw_a1, 0, 8, step=8)
        G1 = fw.sb([128, 8, 160], BF16, "G1"); load_w(G1, io.rw_g1, 0, 8, step=8)
        W2 = fw.sb([128, D], BF16, "W2"); fw.dma("pool", W2[0:64, :], io.rw_w2[:, :])
        A2 = fw.sb([128, D], BF16, "A2"); fw.dma("pool", A2[0:64, :], io.rw_a2[:, :])
        G2a = fw.sb([128, D], BF16, "G2a"); fw.dma("pool", G2a[:], io.rw_g2[0:128, :])
        G2b = fw.sb([128, D], BF16, "G2b"); fw.dma("pool", G2b[0:32, :], io.rw_g2[128:160, :])
        gm1 = fw.sb([128, D], F32, "gm1")
        fw.dma("sp", gm1[:], io.norm_mix[1, :].partition_broadcast(128))
        rows = []
        for i in range(7):
            rt = fw.sb([128, D], F32, f"row{i}")
            fw.dma("sp", rt[:], io.rw_rows[i, :].partition_broadcast(128))
            rows.append(rt)
        w0b, a0b, kkb_, kab, rkb, lnw, lnb = rows
        mu = fw.sb([128, 8, 6], F32, "mu")
        fw.dma("sp", mu[:], io.rw_mu[:, :, :])
        PB0 = fw.view(PB[:, 0:512], "PB0")
        PB1 = fw.view(PB[:, 512:1024], "PB1")
        NPS = [PB0, PB1, PC, PD]
        PAh = [PA[:, 0:512], PA[:, 512:1024]]
        PBh = [PB0[:, :], PB1[:, :]]
        h1 = fw.sb([128, 2, 128], BF16, "h1")

        def proj_tok(xT, W, Ph, M=128):
            for half in range(2):
                for kt in range(8):
                    fw.mm(Ph[half][0:M, :], xT[:, kt, 0:M], W[:, kt, half * 512:(half + 1) * 512],
                          start=kt == 0, stop=kt == 7)

        def lora(xT, Wa, nh, Wb_list, func, P, M=128):
            widths = [min(128, nh), nh - 128] if nh > 128 else [nh]
            for wi, wd in enumerate(widths):
                for kt in range(8):
                    fw.mm(PE[0:wd, wi * 128:wi * 128 + M], Wa[:, kt, wi * 128:wi * 128 + wd], xT[:, kt, 0:M],
                          start=kt == 0, stop=kt == 7)
                fw.act(h1[0:wd, wi, 0:M], PE[0:wd, wi * 128:wi * 128 + M], func)
            for half in range(2):
                for wi, wd in enumerate(widths):
                    fw.mm(P[half][0:M, :], h1[0:wd, wi, 0:M], Wb_list[wi][0:wd, half * 512:(half + 1) * 512],
                          start=wi == 0, stop=wi == len(widths) - 1)

        def rstd16(src16, dst16, mult_, eps, floor=None):
            if floor is not None:
                fw.ts("dve", dst16, src16, floor, None, ALU.max)
            else:
                fw.ts("dve", dst16, src16, mult_, eps, ALU.mult, ALU.add)
            fw.act(dst16, dst16, AF.Ln)
            fw.act(dst16, dst16, AF.Exp, scale=-0.5)

        mark2 = fw.mark()
        xt = fw.sb([128, D], F32, "xt")
        junk = fw.sb([128, D], F32, "junk")
        tmpA = fw.sb([128, D], F32, "tmpA")
        tmpB = fw.sb([128, D], F32, "tmpB")
        Et = fw.sb([128, D], F32, "Et")
        SB = [fw.sb([128, D], BF16, f"S{i}") for i in range(13)]
        xn = SB[0]; r_bf = SB[1]; kkn = SB[2]; kf_bf = SB[3]; b_bf = SB[4]; v_bf = SB[5]; bv = SB[6]
        g_bf = SB[7]; abar = SB[8]; bbar = SB[9]; kbar = SB[10]; btil = SB[11]; ktil = SB[12]
        rbar = SB[0]; yo = SB[8]
        xnTe = fw.sb([128, 8, 130], BF16, "xnTe")
        fw.memset("pool", xnTe[:], 0.0)
        xx = fw.sb([128, 8, 128], BF16, "xx")
        mixb = [fw.sb([128, 8, 128], BF16, f"mix{i}") for i in range(2)]
        arT = fw.sb([128, 8, 2, 128], BF16, "arT")
        bT = fw.sb([128, 8, 128], BF16, "bT")
        kT = fw.sb([128, 8, 128], BF16, "kT")
        yoT = fw.sb([128, 8, 128], BF16, "yoT")
        Ms = [fw.sb([128, 512], BF16, f"Ms{i}") for i in range(4)]
        Q0 = [fw.sb([128, 128], BF16, f"Q0{i}") for i in range(4)]
        PQ = [[fw.sb([128, 256], BF16, f"PQ{i}{k}") for k in range(2)] for i in range(4)]
        Zt = [[fw.sb([128, 128], BF16, f"Z{i}{k}") for k in range(2)] for i in range(4)]
        RHSb = [fw.sb([128, 64], BF16, f"RHS{i}") for i in range(4)]
        Ubp = [fw.sb([128, 2, 64], BF16, f"Ubp{i}") for i in range(2)]
        Hst = fw.sb([128, 8, 64], F32, "Hst")
        fw.memset("pool", Hst[:], 0.0)
        Hb = fw.sb([128, 8, 64], BF16, "Hb")
        fw.memset("pool", Hb[:], 0.0)
        eLT = fw.sb([128, 8], F32, "eLT")
        s16 = fw.sb([128, 64], F32, "s16")
        x3 = tmpB
        mcount = [0]

        def mix(cidx):
            dst = mixb[mcount[0] % 2]
            mcount[0] += 1
            for kt in range(8):
                fw.stt(dst[:, kt, :], xx[:, kt, :], mu[:, kt, cidx:cidx + 1], xnTe[:, kt, 1:129], ALU.mult, ALU.add)
            return dst

        for c in range(NCH):
            fw.dma("sp", xt[:], io.s2[c * 128:(c + 1) * 128, :])
            if c == NCH - 1:
                fw.act(junk[:, :], xt[:], AF.Square, accum_out=nst[:, 0:1])
                fw.ts("dve", nst[:, 1:2], nst[:, 0:1], 1.0 / D, EPS, ALU.mult, ALU.add)
                fw.act(nst[:, 2:3], nst[:, 1:2], AF.Ln)
                fw.act(nst[:, 3:4], nst[:, 2:3], AF.Exp, scale=-0.5)
                fw.stt(tmpA[:], xt[:], nst[:, 3:4], gm1[:], ALU.mult, ALU.mult)
                fw.dma("sp", io.shift_p[:, :], tmpA[127:128, :])
                fw.cp("dve", xn[:], tmpA[:])
            else:
                rmsnorm(xt[:], gm1, xn[:], 128, junk)
            for kt in range(8):
                fw.tr(PT3[:, kt, :], xn[:, kt * 128:(kt + 1) * 128], identb[:, :])
            fw.cp("dve", xnTe[:, :, 1:129], PT3)
            fw.tt("pool", xx[:], xnTe[:, :, 0:128], xnTe[:, :, 1:129], ALU.subtract)
            proj_tok(mix(0), Wr, PAh)
            fw.cp("act", r_bf[:], PA[:, :])
            proj_tok(mix(2), Wk, PAh)
            fw.tt("dve", tmpA[:], PA[:, :], kkb_[:], ALU.mult)
            fw.tt("pool", junk[:], tmpA[:], tmpA[:], ALU.mult)
            fw.red(s16[:, 0:16], v16(junk[:, :]), ALU.add)
            rstd16(s16[:, 0:16], s16[:, 16:32], None, None, floor=1e-24)
            fw.tt("dve", v16(kkn[:, :]), v16(tmpA[:, :]), s16[:, 16:32].bc(2, 64), ALU.mult)
            lora(mix(4), A1, 64, [A2], AF.Copy, PBh)
            for i in range(2):
                fw.tt("dve", tmpB[:, i * 512:(i + 1) * 512], PBh[i], a0b[:, i * 512:(i + 1) * 512], ALU.add)
            fw.act(tmpB[:], tmpB[:], AF.Sigmoid)
            fw.stt(junk[:], tmpB[:], 1.0, kab[:], ALU.subtract, ALU.mult)
            fw.ts("dve", junk[:], junk[:], 1.0, None, ALU.add)
            fw.tt("dve", kf_bf[:], PA[:, :], junk[:], ALU.mult)
            fw.tt("pool", b_bf[:], kkn[:], tmpB[:], ALU.mult)
            fw.tt("pool", junk[:], r_bf[:], kf_bf[:], ALU.mult)
            fw.tt("pool", junk[:], junk[:], rkb[:], ALU.mult)
            fw.red(s16[:, 32:48], v16(junk[:, :]), ALU.add)
            proj_tok(mix(3), Wv, PAh)
            fw.cp("act", v_bf[:], PA[:, :])
            fw.tt("dve", v16(bv[:, :]), v16(PA[:, :]), s16[:, 32:48].bc(2, 64), ALU.mult)
            lora(mix(5), G1, 160, [G2a, G2b], AF.Sigmoid, PBh)
            for i in range(2):
                fw.cp("act", g_bf[:, i * 512:(i + 1) * 512], PBh[i])
            lora(mix(1), W1, 64, [W2], AF.Tanh, PAh)
            fw.tt("dve", tmpA[:], PA[:, :], w0b[:], ALU.add)
            fw.act(tmpA[:], tmpA[:], AF.Exp, scale=-1.0)
            fw.act(tmpA[:], tmpA[:], AF.Ln, bias=1.0)
            fw.ts("dve", tmpA[:], tmpA[:], -1.0, -0.5, ALU.mult, ALU.add)
            fw.act(Et[:], tmpA[:], AF.Exp)
            fw.mm(PB0[:, :], tri_le, Et[:, 0:512])
            fw.mm(PB1[:, :], tri_le, Et[:, 512:1024])
            fw.mm(PA[:, 0:512], ones, Et[:, 0:512])
            fw.mm(PA[:, 512:1024], ones, Et[:, 512:1024])
            for kt in range(8):
                fw.mm(PE[:, kt * 16:(kt + 1) * 16], Et[:, kt * 128:(kt + 1) * 128], ones[:, 0:16])
            fw.act(eLT[:], PE[:, 0:128].rearrange("p (k s) -> p k s", s=16)[:, :, 0], AF.Exp, scale=-1.0)
            hv = lambda r, i: r[:, i * 512:(i + 1) * 512]
            for i, PBi in enumerate((PB0, PB1)):
                fw.act(hv(tmpA, i), PBi[:, :], AF.Exp, scale=-1.0)
                fw.tt("pool", hv(rbar, i), hv(r_bf, i), hv(tmpA, i), ALU.mult)
                fw.tt("dve", hv(tmpB, i), hv(Et, i), PBi[:, :], ALU.subtract)
                fw.act(hv(tmpB, i), hv(tmpB, i), AF.Exp)
                fw.stt(hv(abar, i), hv(kkn, i), -1.0, hv(tmpB, i), ALU.mult, ALU.mult)
            fw.cp("act", junk[:], PA[:, :])
            for i, PBi in enumerate((PB0, PB1)):
                fw.act(hv(tmpA, i), PBi[:, :], AF.Exp)
                fw.tt("pool", hv(bbar, i), hv(b_bf, i), hv(tmpA, i), ALU.mult)
                fw.tt("pool", hv(kbar, i), hv(kf_bf, i), hv(tmpA, i), ALU.mult)
                fw.tt("dve", hv(tmpB, i), PBi[:, :], hv(junk, i), ALU.subtract)
                fw.act(hv(tmpB, i), hv(tmpB, i), AF.Exp)
                fw.tt("pool", hv(btil, i), hv(b_bf, i), hv(tmpB, i), ALU.mult)
                fw.tt("dve", hv(ktil, i), hv(kf_bf, i), hv(tmpB, i), ALU.mult)
            for src, dst in ((abar, arT[:, :, 0, :]), (rbar, arT[:, :, 1, :]), (bbar, bT[:, :, :]), (kbar, kT[:, :, :])):
                for kt in range(8):
                    fw.tr(PT3[:, kt, :], src[:, kt * 128:(kt + 1) * 128], identb[:, :])
                fw.cp("dve", dst, PT3)
            for h0 in range(0, 16, 4):
                hd = []
                for i in range(4):
                    h = h0 + i
                    j, e = h // 2, h % 2
                    p0 = 64 * e
                    hd.append(dict(h=h, j=j, e=e, p0=p0, NP=NPS[i],
                                   aT=arT[p0:p0 + 64, j, 0, :], rT=arT[p0:p0 + 64, j, 1, :],
                                   ar=arT[p0:p0 + 64, j, :, :].rearrange("p a t -> p (a t)"),
                                   bT=bT[p0:p0 + 64, j, :], kT=kT[p0:p0 + 64, j, :]))
                for i, d in enumerate(hd):
                    fw.mm(PE[:, 0:256], d["bT"], d["ar"])
                    fw.mm(PE[:, 256:512], d["kT"], d["ar"])
                    fw.tt("dve", Ms[i][:], PE[:, :], m4, ALU.mult)
                    fw.mm(d["NP"][:, 0:128], d["aT"], d["bT"])
                    fw.tt("dve", Q0[i][:], d["NP"][:, 0:128], mask_gt, ALU.mult)
                    fw.tt("pool", Zt[i][0][:], Ms[i][:, 0:128], identb[:, :], ALU.add)
                    d["P"], d["Q"], d["Z"] = Ms[i][:, 0:128], Q0[i][:], Zt[i][0][:]
                for k in range(1, 7):
                    for i, d in enumerate(hd):
                        NP = d["NP"]
                        if k < 6:
                            fw.mm(NP[:, 0:128], d["Q"], d["P"])
                        fw.mm(NP[:, 128:256], d["P"], d["Q"])
                        pq = PQ[i][k % 2]
                        lo = 0 if k < 6 else 128
                        if i % 2 == 0:
                            fw.cp("act", pq[:, lo:256], NP[:, lo:256])
                        else:
                            fw.cp("dve", pq[:, lo:256], NP[:, lo:256])
                        d["P"], d["Q"] = pq[:, 0:128], pq[:, 128:256]
                    for i, d in enumerate(hd):
                        NP = d["NP"]
                        fw.mm(NP[:, 256:384], d["Q"], d["Z"])
                        zn = Zt[i][k % 2]
                        fw.tt("dve", zn[:], d["Z"], NP[:, 256:384], ALU.add)
                        d["Z"] = zn[:]
                for i, d in enumerate(hd):
                    NP, h, j, p0 = d["NP"], d["h"], d["j"], d["p0"]
                    fw.mm(NP[:, 384:448], d["aT"], Hb[p0:p0 + 64, j, :], start=True, stop=False)
                    fw.mm(NP[:, 384:448], Ms[i][:, 256:384], v_bf[:, h * 64:(h + 1) * 64], start=False, stop=True)
                    fw.cp("act", RHSb[i][:], NP[:, 384:448])
                for i, d in enumerate(hd):
                    NP, h, j, e = d["NP"], d["h"], d["j"], d["e"]
                    fw.mm(NP[:, 448:512], d["Z"], RHSb[i][:])
                    fw.cp("dve", Ubp[j % 2][:, e, :], NP[:, 448:512])
                for i, d in enumerate(hd):
                    h, j, e, p0 = d["h"], d["j"], d["e"], d["p0"]
                    ysl = PA[:, h * 64:(h + 1) * 64]
                    fw.mm(ysl, d["rT"], Hb[p0:p0 + 64, j, :], start=True, stop=False)
                    fw.mm(ysl, Ms[i][:, 128:256], Ubp[j % 2][:, e, :], start=False, stop=False)
                    fw.mm(ysl, Ms[i][:, 384:512], v_bf[:, h * 64:(h + 1) * 64], start=False, stop=True)
                for jj in range(2):
                    j = h0 // 2 + jj
                    NP = hd[2 * jj]["NP"]
                    fw.mm(NP[:, 0:128], btil[:, j * 128:(j + 1) * 128], Ubp[j % 2][:, :, :].rearrange("p e v -> p (e v)"),
                          start=True, stop=False)
                    fw.mm(NP[:, 0:128], ktil[:, j * 128:(j + 1) * 128], v_bf[:, j * 128:(j + 1) * 128],
                          start=False, stop=True)
                    for e in range(2):
                        p0 = 64 * e
                        fw.stt(Hst[p0:p0 + 64, j, :], Hst[p0:p0 + 64, j, :], eLT[p0:p0 + 64, j:j + 1],
                               NP[p0:p0 + 64, p0:p0 + 64], ALU.mult, ALU.add)
                    fw.cp("pool", Hb[:, j, :], Hst[:, j, :])
            fw.cp("act", tmpA[:], PA[:, :])
            fw.red(s16[:, 0:16], v16(tmpA[:, :]), ALU.add)
            fw.ts("dve", s16[:, 0:16], s16[:, 0:16], 1.0 / 64, None, ALU.mult)
            fw.tt("dve", v16(tmpA[:, :]), v16(tmpA[:, :]), s16[:, 0:16].bc(2, 64), ALU.subtract)
            fw.tt("pool", junk[:], tmpA[:], tmpA[:], ALU.mult)
            fw.red(s16[:, 16:32], v16(junk[:, :]), ALU.add)
            rstd16(s16[:, 16:32], s16[:, 48:64], 1.0 / 64, 64e-5)
            fw.tt("dve", v16(tmpA[:, :]), v16(tmpA[:, :]), s16[:, 48:64].bc(2, 64), ALU.mult)
            fw.tt("dve", tmpA[:], tmpA[:], lnw[:], ALU.mult)
            fw.tt("dve", tmpA[:], tmpA[:], lnb[:], ALU.add)
            fw.tt("dve", tmpA[:], tmpA[:], bv[:], ALU.add)
            fw.tt("dve", yo[:], tmpA[:], g_bf[:], ALU.mult)
            to_feat(yo, yoT, 128)
            for half in range(2):
                for kt in range(8):
                    fw.mm(PA[:, half * 512:(half + 1) * 512], yoT[:, kt, :], Wo[:, kt, half * 512:(half + 1) * 512],
                          start=kt == 0, stop=kt == 7)
            fw.tt("dve", x3[:], xt[:], PA[:, :], ALU.add)
            fw.dma("sp", io.s3[c * 128:(c + 1) * 128, :], x3[:])
            fw.cp("pool", xnTe[:, :, 0:1], xnTe[:, :, 128:129])
        fw.dma("sp", io.wkv_p[:, :, :], Hst[:])
        fw.release(mark2)
        if SAMPLE:
            M = NB
            f16 = lambda nm, dt=F32: fw.sb([NB, D], dt, nm)
            xts = f16("xts"); jk = f16("jk"); tA = f16("tA"); tB = f16("tB"); Es = f16("Es")
            r_s = f16("r_s", BF16); kk_s = f16("kk_s", BF16); kf_s = f16("kf_s", BF16); b_s = f16("b_s", BF16)
            v_s = f16("v_s"); bv_s = f16("bv_s", BF16); g_s = f16("g_s", BF16); xn_s = f16("xn_s", BF16)
            sa_tok = f16("sa_tok"); yo_s = f16("yo_s", BF16)
            xprev = fw.sb([128, 8, NB], F32, "xprev")
            xsT = fw.sb([128, 8, NB], BF16, "xsT")
            xxs = fw.sb([128, 8, NB], BF16, "xxs")
            mixs = [fw.sb([128, 8, NB], BF16, f"mixs{i}") for i in range(2)]
            featT = {nm: fw.sb([128, 8, NB], F32, nm) for nm in ("aT", "wT", "bTs", "kTs", "rTs")}
            amask = fw.sb([128, 8, NB, NB], F32, "amask")
            rmask = fw.sb([128, 8, NB, NB], F32, "rmask")
            Hs = [fw.sb([128, 8, 64], F32, f"Hs{i}") for i in range(2)]
            Tt = fw.sb([128, 8, 64], F32, "Tt")
            yoTs = fw.sb([128, 8, NB], BF16, "yoTs")
            s16 = fw.sb([NB, 64], F32, "s16s")
            v16s = lambda r: r.rearrange("p (h q) -> p h q", h=16)
            mc2 = [0]

            def mix_s(cidx):
                dst = mixs[mc2[0] % 2]
                mc2[0] += 1
                for kt in range(8):
                    fw.stt(dst[:, kt, :], xxs[:, kt, :], mu[:, kt, cidx:cidx + 1], xsT[:, kt, :], ALU.mult, ALU.add)
                return dst

            fw.dma("sp", xts[:], io.s2s[:, :])
            fw.dma("sp", xprev[:], io.shift_s_in[:, :, :])
            fw.act(jk[:], xts[:], AF.Square, accum_out=nst[0:M, 0:1])
            fw.ts("dve", nst[0:M, 1:2], nst[0:M, 0:1], 1.0 / D, EPS, ALU.mult, ALU.add)
            fw.act(nst[0:M, 2:3], nst[0:M, 1:2], AF.Ln)
            fw.act(nst[0:M, 3:4], nst[0:M, 2:3], AF.Exp, scale=-0.5)
            fw.stt(tA[:], xts[:], nst[0:M, 3:4], gm1[0:M, :], ALU.mult, ALU.mult)
            fw.dma("sp", io.shift_s[:, :], tA[:])
            fw.cp("dve", xn_s[:], tA[:])
            for kt in range(8):
                fw.tr(PT3[:, kt, 0:M], xn_s[:, kt * 128:(kt + 1) * 128], identb[0:M, 0:M])
            fw.cp("dve", xsT[:], PT3[:, :, 0:M])
            fw.tt("dve", xxs[:], xprev[:], xsT[:], ALU.subtract)
            PAm = [PA[0:M, 0:512], PA[0:M, 512:1024]]
            PBm = [PB0[0:M, :], PB1[0:M, :]]
            PAf = PA[0:M, :]
            proj_tok(mix_s(0), Wr, PAh, M)
            fw.cp("act", r_s[:], PAf)
            proj_tok(mix_s(2), Wk, PAh, M)
            fw.tt("dve", tA[:], PAf, kkb_[0:M, :], ALU.mult)
            fw.tt("dve", jk[:], tA[:], tA[:], ALU.mult)
            fw.red(s16[:, 0:16], v16s(jk[:, :]), ALU.add)
            rstd16(s16[:, 0:16], s16[:, 16:32], None, None, floor=1e-24)
            fw.tt("dve", v16s(kk_s[:, :]), v16s(tA[:, :]), s16[:, 16:32].bc(2, 64), ALU.mult)
            lora(mix_s(4), A1, 64, [A2], AF.Copy, PBh, M)
            for i in range(2):
                fw.tt("dve", tB[:, i * 512:(i + 1) * 512], PBm[i], a0b[0:M, i * 512:(i + 1) * 512], ALU.add)
            fw.act(tB[:], tB[:], AF.Sigmoid)
            fw.stt(jk[:], tB[:], 1.0, kab[0:M, :], ALU.subtract, ALU.mult)
            fw.ts("dve", jk[:], jk[:], 1.0, None, ALU.add)
            fw.tt("dve", kf_s[:], PAf, jk[:], ALU.mult)
            fw.tt("dve", b_s[:], kk_s[:], tB[:], ALU.mult)
            fw.tt("dve", jk[:], r_s[:], kf_s[:], ALU.mult)
            fw.tt("dve", jk[:], jk[:], rkb[0:M, :], ALU.mult)
            fw.red(s16[:, 32:48], v16s(jk[:, :]), ALU.add)
            proj_tok(mix_s(3), Wv, PAh, M)
            fw.cp("act", v_s[:], PAf)
            fw.tt("dve", v16s(bv_s[:, :]), v16s(PAf), s16[:, 32:48].bc(2, 64), ALU.mult)
            lora(mix_s(5), G1, 160, [G2a, G2b], AF.Sigmoid, PBh, M)
            for i in range(2):
                fw.cp("act", g_s[:, i * 512:(i + 1) * 512], PBm[i])
            lora(mix_s(1), W1, 64, [W2], AF.Tanh, PAh, M)
            fw.tt("dve", tA[:], PAf, w0b[0:M, :], ALU.add)
            fw.act(tA[:], tA[:], AF.Exp, scale=-1.0)
            fw.act(tA[:], tA[:], AF.Ln, bias=1.0)
            fw.ts("dve", tA[:], tA[:], -1.0, -0.5, ALU.mult, ALU.add)
            fw.act(Es[:], tA[:], AF.Exp)
            fw.act(Es[:], Es[:], AF.Exp, scale=-1.0)
            fw.ts("dve", tB[:], kk_s[:], -1.0, None, ALU.mult)
            for nm, src in (("aT", tB), ("wT", Es)):
                for kt in range(8):
                    fw.tr(PE[:, kt * 16:(kt + 1) * 16], src[:, kt * 128:(kt + 1) * 128], ident[0:M, 0:M])
                fw.cp("dve", featT[nm][:], PE[:, 0:128].rearrange("p (k b) -> p k b", k=8))
            for nm, src in (("bTs", b_s), ("kTs", kf_s), ("rTs", r_s)):
                for kt in range(8):
                    fw.tr(PT3[:, kt, 0:M], src[:, kt * 128:(kt + 1) * 128], identb[0:M, 0:M])
                fw.cp("dve", featT[nm][:], PT3[:, :, 0:M])
            fw.tt("dve", amask[:], featT["aT"][:, :, :].bc(2, NB), eye16[:, :, :].bc(1, 8), ALU.mult)
            fw.tt("dve", rmask[:], featT["rTs"][:, :, :].bc(2, NB), eye16[:, :, :].bc(1, 8), ALU.mult)
            SY = [PC, PD]
            for b in range(NB):
                H = Hs[b % 2]
                fw.dma("sp", H[:], io.wkv_s_in[b])
                for h in range(16):
                    j, e = h // 2, h % 2
                    p0 = 64 * e
                    fw.mm(SY[e][0:M, j * 64:(j + 1) * 64], amask[p0:p0 + 64, j, b, :], H[p0:p0 + 64, j, :],
                          start=(b == 0 and j == 0), stop=(b == NB - 1))
            je = lambda r: r.rearrange("p (j e v) -> p j e v", j=8, e=2)
            fw.cp("dve", je(sa_tok[:, :])[:, :, 0, :], PC[0:M, :].rearrange("p (j v) -> p j v", j=8))
            fw.cp("dve", je(sa_tok[:, :])[:, :, 1, :], PD[0:M, :].rearrange("p (j v) -> p j v", j=8))
            e4 = lambda r, e: r.rearrange("p (j e v) -> p j e v", j=8, e=2)[64 * e:64 * e + 64, :, e, :]
            for b in range(NB):
                H = Hs[b % 2]
                fw.dma("sp", H[:], io.wkv_s_in[b])
                fw.mm(PA[:, 0:512], sel16[:, b, :], sa_tok[:, 0:512])
                fw.mm(PA[:, 512:1024], sel16[:, b, :], sa_tok[:, 512:1024])
                fw.mm(PB0[:, :], sel16[:, b, :], v_s[:, 0:512])
                fw.mm(PB1[:, :], sel16[:, b, :], v_s[:, 512:1024])
                fw.tt("pool", H[:], H[:], featT["wT"][:, :, b].bc(2, 64), ALU.mult)
                for e in range(2):
                    p0 = 64 * e
                    fw.tt("dve", Tt[p0:p0 + 64, :, :], e4(PA[:, :], e), featT["bTs"][p0:p0 + 64, :, b].bc(2, 64), ALU.mult)
                fw.tt("dve", H[:], H[:], Tt[:], ALU.add)
                for e in range(2):
                    p0 = 64 * e
                    for half, PBx in enumerate((PB0, PB1)):
                        src = PBx[:, :].rearrange("p (j e v) -> p j e v", j=4, e=2)[p0:p0 + 64, :, e, :]
                        fw.tt("dve", Tt[p0:p0 + 64, 4 * half:4 * half + 4, :], src,
                              featT["kTs"][p0:p0 + 64, 4 * half:4 * half + 4, b].bc(2, 64), ALU.mult)
                fw.tt("dve", H[:], H[:], Tt[:], ALU.add)
                fw.dma("sp", io.wkv_s[b], H[:])
                for h in range(16):
                    j, e = h // 2, h % 2
                    p0 = 64 * e
                    fw.mm(SY[e][0:M, j * 64:(j + 1) * 64], rmask[p0:p0 + 64, j, b, :], H[p0:p0 + 64, j, :],
                          start=(b == 0 and j == 0), stop=(b == NB - 1))
            fw.cp("dve", je(tA[:, :])[:, :, 0, :], PC[0:M, :].rearrange("p (j v) -> p j v", j=8))
            fw.cp("dve", je(tA[:, :])[:, :, 1, :], PD[0:M, :].rearrange("p (j v) -> p j v", j=8))
            fw.red(s16[:, 0:16], v16s(tA[:, :]), ALU.add)
            fw.ts("dve", s16[:, 0:16], s16[:, 0:16], 1.0 / 64, None, ALU.mult)
            fw.tt("dve", v16s(tA[:, :]), v16s(tA[:, :]), s16[:, 0:16].bc(2, 64), ALU.subtract)
            fw.tt("dve", jk[:], tA[:], tA[:], ALU.mult)
            fw.red(s16[:, 16:32], v16s(jk[:, :]), ALU.add)
            rstd16(s16[:, 16:32], s16[:, 48:64], 1.0 / 64, 64e-5)
            fw.tt("dve", v16s(tA[:, :]), v16s(tA[:, :]), s16[:, 48:64].bc(2, 64), ALU.mult)
            fw.tt("dve", tA[:], tA[:], lnw[0:M, :], ALU.mult)
            fw.tt("dve", tA[:], tA[:], lnb[0:M, :], ALU.add)
            fw.tt("dve", tA[:], tA[:], bv_s[:], ALU.add)
            fw.tt("dve", yo_s[:], tA[:], g_s[:], ALU.mult)
            for kt in range(8):
                fw.tr(PT3[:, kt, 0:M], yo_s[:, kt * 128:(kt + 1) * 128], identb[0:M, 0:M])
            fw.cp("dve", yoTs[:], PT3[:, :, 0:M])
            for half in range(2):
                for kt in range(8):
                    fw.mm(PA[0:M, half * 512:(half + 1) * 512], yoTs[:, kt, :], Wo[:, kt, half * 512:(half + 1) * 512],
                          start=kt == 0, stop=kt == 7)
            fw.tt("dve", tB[:], xts[:], PA[0:M, :], ALU.add)
            fw.dma("sp", io.s3s[:, :], tB[:])
        fw.release(base_mark)

    if "3" in phases:
        ffn_phase(1, io.s3, io.y_p, io.s3s, io.y_s, True)

    fw.finish()
    fw.close()
    return nc


def prep_common(inp):
    f = lambda k: np.ascontiguousarray(np.asarray(inp[k], np.float32))
    m = {}
    m["cst"] = host_consts()
    m["w_in0"] = f("w_in0")[0]
    m["w_out0"] = f("w_out0")[0]
    m["norm_mix"] = f("norm_mix")
    m["norm_ffn"] = f("norm_ffn")
    m["norm_final"] = f("norm_final")
    m["ssd_norm"] = f("ssd_norm")[0]
    small0 = np.zeros(64, np.float32)
    small0[0:16] = f("ssd_dt_bias")[0]
    small0[16:32] = f("ssd_a_log")[0]
    small0[32:48] = f("ssd_d")[0]
    small0[48:52] = f("ml_i_bias")[0]
    small0[52:56] = f("ml_f_bias")[0]
    m["small0"] = small0
    cw = f("conv_w")[0].reshape(4, 20, 128).transpose(2, 1, 0)
    cb = f("conv_b")[0].reshape(20, 128).T[:, :, None]
    m["convp"] = np.ascontiguousarray(np.concatenate([cw, cb], axis=2))
    m["mlcol"] = np.ascontiguousarray(np.stack([f("ml_norm")[0].reshape(8, 128).T, f("ml_skip")[0].reshape(8, 128).T], axis=2))
    m["bdq"] = blockdiag(f("ml_wq")[0])
    m["bdk"] = blockdiag(f("ml_wk")[0])
    m["bdv"] = blockdiag(f("ml_wv")[0])
    for nm in ("rw_wr", "rw_wk", "rw_wv", "rw_wo", "rw_w1", "rw_w2", "rw_a1", "rw_a2", "rw_g1", "rw_g2"):
        m[nm] = f(nm)[0]
    m["rw_rows"] = np.ascontiguousarray(np.stack([f(k)[0] for k in ("rw_w0", "rw_a0", "rw_k_k", "rw_k_a", "rw_r_k", "rw_ln_w", "rw_ln_b")]))
    m["rw_mu"] = np.ascontiguousarray(f("rw_mu")[0].reshape(6, 8, 128).transpose(2, 1, 0))
    m["w_gu"] = f("ffn_w_gate_up")
    m["w_dn"] = f("ffn_w_down")
    return m


def prep_core(inp, core):
    f = lambda k: np.asarray(inp[k], np.float32)
    b0 = core * NB
    m = {}
    m["xp"] = np.ascontiguousarray(f("x_prompt")[core])
    m["xs"] = np.ascontiguousarray(f("x_sample")[b0:b0 + NB, 0, :])
    m["conv_s_in"] = np.ascontiguousarray(f("state_conv")[0, b0:b0 + NB].reshape(NB, 3, 20, 128).transpose(3, 2, 1, 0))
    m["ssm_s_in"] = np.ascontiguousarray(f("state_ssm")[0, b0:b0 + NB].reshape(NB, D, 128))
    m["mc_s_in"] = np.ascontiguousarray(f("state_mlstm_c")[0, b0:b0 + NB])
    m["mn_s_in"] = np.ascontiguousarray(f("state_mlstm_n")[0, b0:b0 + NB].reshape(NB, 4, 2, 128).transpose(3, 1, 2, 0).reshape(128, 8, NB))
    m["mm_s_in"] = np.ascontiguousarray(f("state_mlstm_m")[0, b0:b0 + NB])
    m["shift_s_in"] = np.ascontiguousarray(f("state_shift")[0, b0:b0 + NB].reshape(NB, 8, 128).transpose(2, 1, 0))
    m["wkv_s_in"] = np.ascontiguousarray(f("state_wkv")[0, b0:b0 + NB].reshape(NB, 8, 2, 64, 64).transpose(0, 2, 4, 1, 3).reshape(NB, 128, 8, 64))
    return m


def prep_consts(inp):
    f = lambda k: np.asarray(inp[k], np.float32)
    m = prep_common(inp)
    m["c16"] = host_consts16()
    m["eye16"] = np.ascontiguousarray(np.broadcast_to(np.eye(16, dtype=np.float32), (128, 16, 16)))
    dtcol = np.zeros((16, 4), np.float32)
    dtcol[:, 0] = f("ssd_dt_bias")[0]
    dtcol[:, 1] = f("ssd_a_log")[0]
    dtcol[:, 2] = f("ssd_d")[0]
    m["dtcol"] = dtcol
    return m


_NC_CACHE = {}


def kernel(**inp):
    if "nc" not in _NC_CACHE:
        _NC_CACHE["nc"] = build({})
    nc = _NC_CACHE["nc"]
    cm = prep_consts(inp)
    in_maps = [dict(cm, **prep_core(inp, c)) for c in range(NCORE)]
    res = run_bass_kernel_spmd(nc, in_maps, core_ids=list(range(NCORE)))
    R = res.results
    BT = NCORE * NB
    y_p = np.zeros((NCORE, T, D), np.float32)
    y_s = np.zeros((BT, 1, D), np.float32)
    conv_p = np.zeros((1, NCORE, 3, 2560), np.float32)
    conv_s = np.zeros((1, BT, 3, 2560), np.float32)
    ssm_p = np.zeros((1, NCORE, 16, 64, 128), np.float32)
    ssm_s = np.zeros((1, BT, 16, 64, 128), np.float32)
    mc_p = np.zeros((1, NCORE, 4, 256, 256), np.float32)
    mc_s = np.zeros((1, BT, 4, 256, 256), np.float32)
    mn_p = np.zeros((1, NCORE, 4, 256), np.float32)
    mn_s = np.zeros((1, BT, 4, 256), np.float32)
    mm_p = np.zeros((1, NCORE, 4), np.float32)
    mm_s = np.zeros((1, BT, 4), np.float32)
    sh_p = np.zeros((1, NCORE, D), np.float32)
    sh_s = np.zeros((1, BT, D), np.float32)
    wkv_p = np.zeros((1, NCORE, 16, 64, 64), np.float32)
    wkv_s = np.zeros((1, BT, 16, 64, 64), np.float32)
    for c in range(NCORE):
        r = R[c]
        sl = slice(c * NB, (c + 1) * NB)
        y_p[c] = r["y_p"]
        y_s[sl, 0] = r["y_s"]
        conv_p[0, c] = r["conv_p"].transpose(2, 1, 0).reshape(3, 2560)
        conv_s[0, sl] = r["conv_s"].transpose(3, 2, 1, 0).reshape(NB, 3, 2560)
        ssm_p[0, c] = r["ssm_p"].reshape(128, 16, 64).transpose(1, 2, 0)
        ssm_s[0, sl] = r["ssm_s"].reshape(NB, 16, 64, 128)
        mc = r["mc_p"]
        mc_p[0, c] = mc[:, :, :, :256].transpose(2, 1, 0, 3).reshape(4, 256, 256)
        mn_p[0, c] = mc[:, :, :, 256].transpose(2, 1, 0).reshape(4, 256)
        mc_s[0, sl] = r["mc_s"]
        mn_s[0, sl] = r["mn_s"].reshape(128, 4, 2, NB).transpose(3, 1, 2, 0).reshape(NB, 4, 256)
        mm_p[0, c] = r["mm_p"][0]
        mm_s[0, sl] = r["mm_s"]
        sh_p[0, c] = r["shift_p"][0]
        sh_s[0, sl] = r["shift_s"]
        wkv_p[0, c] = r["wkv_p"].reshape(2, 64, 8, 64).transpose(2, 0, 3, 1).reshape(16, 64, 64)
        wkv_s[0, sl] = r["wkv_s"].reshape(NB, 2, 64, 8, 64).transpose(0, 3, 1, 4, 2).reshape(NB, 16, 64, 64)
    return (y_p, y_s, conv_p, conv_s, ssm_p, ssm_s, mc_p, mc_s, mn_p, mn_s, mm_p, mm_s, sh_p, sh_s, wkv_p, wkv_s)
```

```python
import numpy as np
import concourse.bass as bass
import concourse.mybir as mybir
from concourse.bass_utils import run_bass_kernel_spmd

F32 = mybir.dt.float32
BF16 = mybir.dt.bfloat16
ALU = mybir.AluOpType
AF = mybir.ActivationFunctionType
AX = mybir.AxisListType

NCORE = 8
D = 1024
T = 2048
NB = 16
IN0 = 4632
DFF = 2816
EPS = 1e-5


class Tok:
    __slots__ = ("sem", "val", "eng")

    def __init__(self, sem, val, eng):
        self.sem, self.val, self.eng = sem, val, eng


class Ref:
    __slots__ = ("T", "ap")

    def __init__(self, T_, ap):
        self.T, self.ap = T_, ap

    def __getitem__(self, k):
        return Ref(self.T, self.ap[k])

    def rearrange(self, p, **kw):
        return Ref(self.T, self.ap.rearrange(p, **kw))

    def unsqueeze(self, a):
        return Ref(self.T, self.ap.unsqueeze(a))

    def to_broadcast(self, shp):
        return Ref(self.T, self.ap.to_broadcast(list(shp)))

    def bc(self, axis, n):
        ap = self.ap.unsqueeze(axis)
        shp = list(ap.shape)
        shp[axis] = n
        return Ref(self.T, ap.to_broadcast(shp))


class TT:
    __slots__ = ("t", "name", "lw", "rd", "psum")

    def __init__(self, t, name, psum=False):
        self.t, self.name, self.lw, self.rd, self.psum = t, name, None, [], psum

    def __getitem__(self, k):
        return Ref(self, self.t[k])


def _Ts(*xs):
    return [x.T for x in xs if isinstance(x, Ref)]


def _a(x):
    return x.ap if isinstance(x, Ref) else x


class Eng:
    def __init__(self, fw, name, h):
        self.fw, self.name, self.h = fw, name, h
        self.sems, self.n, self.waited = [], 0, {}


class Fw:
    EPOCH = 30000
    NDMA = 10

    def __init__(self, nc):
        self.nc = nc
        self._ctx = []
        self.E = {}
        for name, h in (("pe", nc.tensor), ("dve", nc.vector), ("act", nc.scalar),
                        ("pool", nc.gpsimd), ("sp", nc.sync)):
            self.E[name] = Eng(self, name, h)
        self.dma_sems, self.dma_i = {}, {}
        self.ntile = 0
        self.sb_bytes = 0

    def enter(self, cm):
        v = cm.__enter__()
        self._ctx.append(cm)
        return v

    def close(self):
        for cm in reversed(self._ctx):
            cm.__exit__(None, None, None)
        self._ctx = []

    def new_sem(self, name):
        return self.enter(self.nc.semaphore(name))

    def presem(self, queues=("sp", "pool", "act"), epochs=3):
        for e in self.E.values():
            while len(e.sems) < epochs:
                e.sems.append(self.new_sem(f"e_{e.name}_{len(e.sems)}"))
        for q in queues:
            self.dma_sems[q] = [[self.new_sem(f"d_{q}_{i}"), 0] for i in range(Fw.NDMA)]
            self.dma_i[q] = 0

    def mark(self):
        return len(self._ctx)

    def release(self, mark):
        self.barrier()
        while len(self._ctx) > mark:
            self._ctx.pop().__exit__(None, None, None)

    def barrier(self):
        for eng in self.E.values():
            for q, slots in self.dma_sems.items():
                for sem, cnt in slots:
                    if cnt > 0:
                        self._wait(eng, Tok(sem, cnt, "dma"))
            for name, e in self.E.items():
                if e is eng or e.n == 0:
                    continue
                ep = (e.n - 1) // Fw.EPOCH
                self._wait(eng, Tok(e.sems[ep], (e.n - 1) % Fw.EPOCH + 1, name))

    def sb(self, shape, dt=F32, name="t"):
        self.ntile += 1
        n = 1
        for s in shape[1:]:
            n *= s
        self.sb_bytes += n * (2 if dt == BF16 else 4)
        return TT(self.enter(self.nc.sbuf_tensor(f"{name}_{self.ntile}", list(shape), dt)), name)

    def ps(self, shape, dt=F32, name="p"):
        self.ntile += 1
        return TT(self.enter(self.nc.psum_tensor(f"{name}_{self.ntile}", list(shape), dt)), name, psum=True)

    def view(self, ref, name="v"):
        return TT(ref.ap, name, psum=ref.T.psum)

    def _wait(self, eng, tok):
        if tok is None:
            return
        key = id(tok.sem)
        if eng.waited.get(key, 0) >= tok.val:
            return
        eng.h.wait_ge(tok.sem, tok.val)
        eng.waited[key] = tok.val

    def _deps(self, eng, reads, writes):
        for t in reads:
            if t.lw is not None:
                self._wait(eng, t.lw)
            if t.psum:
                for r in t.rd:
                    if r.eng != eng.name:
                        self._wait(eng, r)
        for t in writes:
            if t.lw is not None and t.lw.eng != eng.name:
                self._wait(eng, t.lw)
            for r in t.rd:
                if r.eng != eng.name:
                    self._wait(eng, r)

    def _mark(self, tok, reads, writes):
        for t in reads:
            t.rd.append(tok)
        for t in writes:
            t.lw = tok
            t.rd = []

    def op(self, e, fn, reads=(), writes=()):
        eng = self.E[e]
        self._deps(eng, reads, writes)
        ep = eng.n // Fw.EPOCH
        while len(eng.sems) <= ep:
            eng.sems.append(self.new_sem(f"e_{eng.name}_{len(eng.sems)}"))
        sem = eng.sems[ep]
        inst = fn(eng.h)
        val = eng.n % Fw.EPOCH + 1
        eng.n += 1
        inst.then_inc(sem, 1)
        tok = Tok(sem, val, eng.name)
        self._mark(tok, reads, writes)
        return tok

    def dma(self, q, out, in_, **kw):
        eng = self.E[q]
        if q not in self.dma_sems:
            self.dma_sems[q] = [[self.new_sem(f"d_{q}_{i}"), 0] for i in range(Fw.NDMA)]
            self.dma_i[q] = 0
        slot = self.dma_sems[q][self.dma_i[q] % Fw.NDMA]
        self.dma_i[q] += 1
        sem, cnt = slot
        if cnt > 0:
            self._wait(eng, Tok(sem, cnt, "dma"))
        reads, writes = _Ts(in_), _Ts(out)
        self._deps(eng, reads, writes)
        inst = eng.h.dma_start(out=_a(out), in_=_a(in_), **kw)
        slot[1] = cnt + 16
        inst.then_inc(sem, 16)
        tok = Tok(sem, cnt + 16, "dma")
        self._mark(tok, reads, writes)
        return tok

    def finish(self):
        eng = self.E["sp"]
        for q, slots in self.dma_sems.items():
            for sem, cnt in slots:
                if cnt > 0:
                    self._wait(eng, Tok(sem, cnt, "dma"))
        for name, e in self.E.items():
            if name == "sp" or e.n == 0:
                continue
            self._wait(eng, Tok(e.sems[(e.n - 1) // Fw.EPOCH], (e.n - 1) % Fw.EPOCH + 1, name))

    def mm(self, out, lhsT, rhs, start=True, stop=True):
        return self.op("pe", lambda e: e.matmul(_a(out), _a(lhsT), _a(rhs), start=start, stop=stop),
                       _Ts(lhsT, rhs), _Ts(out))

    def tr(self, out, in_, ident):
        return self.op("pe", lambda e: e.transpose(_a(out), _a(in_), _a(ident)), _Ts(in_, ident), _Ts(out))

    def act(self, out, in_, func, bias=None, scale=None, accum_out=None):
        kw = {}
        if bias is not None:
            kw["bias"] = _a(bias)
        if scale is not None:
            kw["scale"] = _a(scale)
        if accum_out is not None:
            kw["accum_out"] = _a(accum_out)
        return self.op("act", lambda e: e.activation(out=_a(out), in_=_a(in_), func=func, **kw),
                       _Ts(in_, bias, scale), _Ts(out, accum_out))

    def tt(self, e, out, in0, in1, op):
        return self.op(e, lambda h: h.tensor_tensor(out=_a(out), in0=_a(in0), in1=_a(in1), op=op),
                       _Ts(in0, in1), _Ts(out))

    def ts(self, e, out, in0, s1, s2, op0, op1=None, accum_out=None):
        kw = {}
        if op1 is not None:
            kw["op1"] = op1
        if accum_out is not None:
            kw["accum_out"] = _a(accum_out)
        return self.op(e, lambda h: h.tensor_scalar(out=_a(out), in0=_a(in0), scalar1=_a(s1), scalar2=_a(s2),
                                                    op0=op0, **kw),
                       _Ts(in0, s1, s2), _Ts(out, accum_out))

    def stt(self, out, in0, scalar, in1, op0, op1, accum_out=None):
        kw = {}
        if accum_out is not None:
            kw["accum_out"] = _a(accum_out)
        return self.op("dve", lambda h: h.scalar_tensor_tensor(out=_a(out), in0=_a(in0), scalar=_a(scalar),
                                                               in1=_a(in1), op0=op0, op1=op1, **kw),
                       _Ts(in0, scalar, in1), _Ts(out, accum_out))

    def cp(self, e, out, in_):
        if e == "act":
            return self.act(out, in_, AF.Copy)
        return self.op(e, lambda h: h.tensor_copy(out=_a(out), in_=_a(in_)), _Ts(in_), _Ts(out))

    def red(self, out, in_, op, axis=AX.X):
        return self.op("dve", lambda h: h.tensor_reduce(out=_a(out), in_=_a(in_), axis=axis, op=op),
                       _Ts(in_), _Ts(out))

    def recip(self, out, in_):
        return self.op("dve", lambda h: h.reciprocal(out=_a(out), in_=_a(in_)), _Ts(in_), _Ts(out))

    def memset(self, e, out, val):
        return self.op(e, lambda h: h.memset(_a(out), val), [], _Ts(out))


def host_consts():
    j = np.arange(128)
    c = np.zeros((128, 10, 128), np.float32)
    c[:, 0, :] = (j[:, None] == j[None, :])
    c[:, 1, :] = (j[:, None] <= j[None, :])
    c[:, 2, :] = (j[:, None] > j[None, :])
    c[:, 3, :] = np.where(j[None, :] <= j[:, None], 0.0, -30000.0)
    c[:, 4, :] = 1.0
    c[:, 5, :] = (j[:, None] == 127)
    c[:, 6, :] = (j[:, None] < j[None, :])
    c[:, 7, :] = c[:, 1, :]
    c[:, 8, :] = c[:, 6, :]
    c[:, 9, :] = c[:, 1, :]
    return c


def blockdiag(w):
    out = np.zeros((8, 128, 128), np.float32)
    w = w.reshape(8, 32, 4, 4)
    for nl in range(32):
        out[:, nl * 4:(nl + 1) * 4, nl * 4:(nl + 1) * 4] = w[:, nl]
    return np.ascontiguousarray(out.transpose(1, 0, 2))


def host_consts16():
    h = np.arange(16)
    q = np.arange(128)
    j = np.arange(8)
    e = (h[:, None, None] == (2 * j[None, :, None] + q[None, None, :] // 64)).astype(np.float32)
    sel = np.broadcast_to((h[:, None, None] == h[None, :, None]), (16, 16, 128)).astype(np.float32)
    return np.ascontiguousarray(np.concatenate([e.reshape(16, -1), sel.reshape(16, -1)], axis=1))


class IO:
    pass


def build(cfg):
    nc = bass.Bass("TRN2", target_bir_lowering=False)
    fw = Fw(nc)
    io = IO()
    NCH = cfg.get("nch", 16)
    dbg = cfg.get("dbg", ())
    phases = cfg.get("phases", ("0a", "0b", "1", "2", "3"))

    def din(name, shape):
        return nc.dram_tensor(name, list(shape), F32, kind="ExternalInput").ap()

    def dout(name, shape):
        return nc.dram_tensor(name, list(shape), F32, kind="ExternalOutput").ap()

    def dscr(name, shape):
        if name in dbg:
            return dout(name, shape)
        return nc.dram_tensor(name, list(shape), F32).ap()

    io.xp = din("xp", [T, D])
    io.cst = din("cst", [128, 10, 128])
    io.w_in0 = din("w_in0", [D, IN0])
    io.w_out0 = din("w_out0", [2 * D, D])
    io.norm_mix = din("norm_mix", [2, D])
    io.norm_ffn = din("norm_ffn", [2, D])
    io.norm_final = din("norm_final", [D])
    io.ssd_norm = din("ssd_norm", [D])
    io.small0 = din("small0", [64])
    io.convp = din("convp", [128, 20, 5])
    io.mlcol = din("mlcol", [128, 8, 2])
    io.bdq = din("bdq", [128, 8, 128])
    io.bdk = din("bdk", [128, 8, 128])
    io.bdv = din("bdv", [128, 8, 128])
    io.w_gu = din("w_gu", [2, D, 2 * DFF])
    io.w_dn = din("w_dn", [2, DFF, D])
    for nm in ("rw_wr", "rw_wk", "rw_wv", "rw_wo"):
        setattr(io, nm, din(nm, [D, D]))
    io.rw_w1 = din("rw_w1", [D, 64]); io.rw_w2 = din("rw_w2", [64, D])
    io.rw_a1 = din("rw_a1", [D, 64]); io.rw_a2 = din("rw_a2", [64, D])
    io.rw_g1 = din("rw_g1", [D, 160]); io.rw_g2 = din("rw_g2", [160, D])
    io.rw_rows = din("rw_rows", [7, D])
    io.rw_mu = din("rw_mu", [128, 8, 6])
    io.wkv_p = dout("wkv_p", [128, 8, 64])
    io.shift_p = dout("shift_p", [1, D])
    io.xs = din("xs", [NB, D])
    io.c16 = din("c16", [16, 8 * 128 + 16 * 128])
    io.eye16 = din("eye16", [128, 16, 16])
    io.dtcol = din("dtcol", [16, 4])
    io.conv_s_in = din("conv_s_in", [128, 20, 3, NB])
    io.ssm_s_in = din("ssm_s_in", [NB, D, 128])
    io.mc_s_in = din("mc_s_in", [NB, 4, 256, 256])
    io.mn_s_in = din("mn_s_in", [128, 8, NB])
    io.mm_s_in = din("mm_s_in", [NB, 4])
    io.shift_s_in = din("shift_s_in", [128, 8, NB])
    io.wkv_s_in = din("wkv_s_in", [NB, 128, 8, 64])
    io.y_s = dout("y_s", [NB, D])
    io.conv_s = dout("conv_s", [128, 20, 3, NB])
    io.ssm_s = dout("ssm_s", [NB, D, 128])
    io.mc_s = dout("mc_s", [NB, 4, 256, 256])
    io.mn_s = dout("mn_s", [128, 8, NB])
    io.mm_s = dout("mm_s", [NB, 4])
    io.shift_s = dout("shift_s", [NB, D])
    io.wkv_s = dout("wkv_s", [NB, 128, 8, 64])
    io.s1s = dscr("s1s", [NB, D])
    io.s2s = dscr("s2s", [NB, D])
    io.s3s = dscr("s3s", [NB, D])
    if "dbg_a" in dbg:
        io.dbg_a = dout("dbg_a", [NB, D]); io.dbg_b = dout("dbg_b", [NB, 64])
    io.s1 = dscr("s1", [T, D])
    io.s2 = dscr("s2", [T, D])
    io.s3 = dscr("s3", [T, D])
    io.y_p = dout("y_p", [T, D])
    io.ssm_p = dout("ssm_p", [128, D])
    io.mc_p = dout("mc_p", [128, 2, 4, 264])
    io.mm_p = dout("mm_p", [1, 4])
    io.conv_p = dout("conv_p", [128, 20, 3])

    fw.presem(epochs=5)

    cst = fw.sb([128, 10, 128], F32, "cst")
    fw.dma("sp", cst[:], io.cst[:, :, :])
    ident, tri_le, mask_gt, negmask, ones = (cst[:, i, :] for i in range(5))
    sel127 = cst[:, 5, :]
    m4 = cst[:, 6:10, :].rearrange("p a t -> p (a t)")
    identb = fw.sb([128, 128], BF16, "identb")
    fw.cp("dve", identb[:], ident)
    onesb = fw.sb([128, 128], BF16, "onesb")
    fw.cp("dve", onesb[:], ones)
    nst = fw.sb([128, 8], F32, "nst")
    c16 = fw.sb([16, 8 * 128 + 16 * 128], F32, "c16")
    fw.dma("sp", c16[:], io.c16[:, :])
    exp16 = c16[:, 0:1024].rearrange("p (j q) -> p j q", j=8)
    sel16 = c16[:, 1024:3072].rearrange("p (b q) -> p b q", b=16)
    eye16 = fw.sb([128, 16, 16], F32, "eye16")
    fw.dma("sp", eye16[:], io.eye16[:, :, :])
    SAMPLE = cfg.get("sample", True)

    PA = fw.ps([128, 1024], F32, "PA")
    PB = fw.ps([128, 1024], F32, "PB")
    PC = fw.ps([128, 512], F32, "PC")
    PD = fw.ps([128, 512], F32, "PD")
    PE = fw.ps([128, 512], F32, "PE")
    PT = fw.ps([128, 1024], BF16, "PT")
    pcd = [PC, PD]
    PB0f = fw.view(PB[:, 0:512], "PB0f")
    PB1f = fw.view(PB[:, 512:1024], "PB1f")
    PT3 = PT[:, :].rearrange("p (k m) -> p k m", k=8)
    v16 = lambda r: r.rearrange("p (h q) -> p h q", h=16)

    def load_w(dst, src, kt0, kt1, q="pool", step=2):
        for k in range(kt0, kt1, step):
            k1 = min(k + step, kt1)
            fw.dma(q, dst[:, k:k1, :], src[k * 128:k1 * 128, :].rearrange("(k p) n -> p k n", p=128))

    def rmsnorm(x, g, out, M, junk):
        fw.act(junk[0:M, :], x, AF.Square, accum_out=nst[0:M, 0:1])
        fw.ts("dve", nst[0:M, 1:2], nst[0:M, 0:1], 1.0 / D, EPS, ALU.mult, ALU.add)
        fw.act(nst[0:M, 2:3], nst[0:M, 1:2], AF.Ln)
        fw.act(nst[0:M, 3:4], nst[0:M, 2:3], AF.Exp, scale=-0.5)
        fw.stt(out, x, nst[0:M, 3:4], g[0:M, :], ALU.mult, ALU.mult)

    def to_feat(src, dst, M):
        for kt in range(8):
            fw.tr(PT3[:, kt, 0:M], src[0:M, kt * 128:(kt + 1) * 128], identb[0:M, 0:M])
        fw.cp("dve", dst[:, :, 0:M], PT3[:, :, 0:M])

    def grp_rstd(src, ncol, dst, junk, M=128):
        fw.act(junk[0:M, 0:ncol], src, AF.Square, accum_out=nst[0:M, 4:5])
        fw.ts("dve", nst[0:M, 5:6], nst[0:M, 4:5], 1.0 / ncol, EPS, ALU.mult, ALU.add)
        fw.act(nst[0:M, 6:7], nst[0:M, 5:6], AF.Ln)
        fw.act(dst, nst[0:M, 6:7], AF.Exp, scale=-0.5)

    def proj_feat(W, col0, ntile, xT, M, evac):
        for gi, g0 in enumerate(range(0, ntile, 4)):
            n = min(4, ntile - g0)
            ps3 = pcd[gi % 2][:, :].rearrange("p (a m) -> p a m", a=4)
            for i in range(n):
                col = col0 + (g0 + i) * 128
                for kt in range(8):
                    fw.mm(ps3[:, i, 0:M], W[:, kt, col:col + 128], xT[:, kt, 0:M], start=kt == 0, stop=kt == 7)
            evac(g0, n, ps3[:, 0:n, 0:M])

    def conv_tiles(convin, convp, acc, ct0, n):
        for i in range(n):
            ct = ct0 + i
            fw.act(acc[:, i, :], convin[:, i, 0:128], AF.Identity, scale=convp[:, ct, 0:1], bias=convp[:, ct, 4:5])
            for j in range(1, 4):
                fw.stt(acc[:, i, :], convin[:, i, j:j + 128], convp[:, ct, j:j + 1], acc[:, i, :], ALU.mult, ALU.add)

    base_mark = fw.mark()

    if "0a" in phases:
        Wz = fw.sb([128, 8, 1024], BF16, "Wz")
        load_w(Wz, io.w_in0[:, 0:1024], 0, 8)
        Wc = fw.sb([128, 8, 1536], BF16, "Wc")
        load_w(Wc, io.w_in0[:, 1024:2560], 0, 8)
        Wdt = fw.sb([128, 8, 16], BF16, "Wdt")
        load_w(Wdt, io.w_in0[:, 3584:3600], 0, 8, step=8)
        Wo = fw.sb([128, 8, D], BF16, "Wo")
        load_w(Wo, io.w_out0[0:1024, :], 0, 8)
        gmix = fw.sb([128, D], F32, "gmix")
        fw.dma("sp", gmix[:], io.norm_mix[0, :].partition_broadcast(128))
        gssd = fw.sb([128, D], F32, "gssd")
        fw.dma("sp", gssd[:], io.ssd_norm.partition_broadcast(128))
        sm0 = fw.sb([128, 64], F32, "sm0")
        fw.dma("sp", sm0[:], io.small0.partition_broadcast(128))
        dtb_bc, D_bc = sm0[:, 0:16], sm0[:, 32:48]
        A_t = fw.sb([128, 16], F32, "A_t")
        fw.act(A_t[:], sm0[:, 16:32], AF.Exp)
        fw.ts("dve", A_t[:], A_t[:], -1.0, None, ALU.mult)
        convp = fw.sb([128, 20, 5], F32, "convp")
        fw.dma("sp", convp[:], io.convp[:, :, :])
        convin = fw.sb([128, 12, 131], F32, "convin")
        fw.memset("pool", convin[:], 0.0)
        ST = fw.sb([128, D], F32, "ST")
        fw.memset("pool", ST[:], 0.0)
        STb = fw.sb([128, D], BF16, "STb")
        fw.memset("pool", STb[:], 0.0)
        xt = fw.sb([128, D], F32, "xt")
        junk = fw.sb([128, D], F32, "junk")
        xn = fw.sb([128, D], BF16, "xn")
        xnT = fw.sb([128, 8, 128], BF16, "xnT")
        acc = fw.sb([128, 12, 128], F32, "acc")
        cact = fw.sb([128, 12, 128], BF16, "cact")
        zs = fw.sb([128, D], F32, "zs")
        xtok = fw.sb([128, D], BF16, "xtok")
        Btok = fw.sb([128, 256], BF16, "Btok")
        sm = fw.sb([128, 128], F32, "sm")
        Lh = [fw.sb([128, 4, 128], F32, f"Lh{i}") for i in range(2)]
        Eh = fw.sb([128, 4, 128], F32, "Eh")
        CBm = fw.sb([128, 2, 128], F32, "CBm")
        Wt = fw.sb([128, 16, 128], BF16, "Wt")
        t1 = fw.sb([128, D], F32, "t1")
        yn = fw.sb([128, D], BF16, "yn")
        ynT = fw.sb([128, 8, 128], BF16, "ynT")
        xw = fw.sb([128, D], BF16, "xw")
        x1 = fw.sb([128, D], F32, "x1")

        for c in range(NCH):
            fw.dma("sp", xt[:], io.xp[c * 128:(c + 1) * 128, :])
            rmsnorm(xt[:], gmix, xn[:], 128, junk)
            to_feat(xn, xnT, 128)
            proj_feat(Wc, 0, 12, xnT, 128, lambda g0, n, ps: fw.cp("act", convin[:, g0:g0 + n, 3:131], ps))
            for half in range(2):
                for kt in range(8):
                    fw.mm(PA[:, half * 512:(half + 1) * 512], xnT[:, kt, :], Wz[:, kt, half * 512:(half + 1) * 512],
                          start=kt == 0, stop=kt == 7)
            fw.act(zs[:], PA[:, :], AF.Silu)
            for kt in range(8):
                fw.mm(PE[:, 0:16], xnT[:, kt, :], Wdt[:, kt, :], start=kt == 0, stop=kt == 7)
            fw.tt("dve", sm[:, 0:16], PE[:, 0:16], dtb_bc, ALU.add)
            conv_tiles(convin, convp, acc, 0, 12)
            fw.act(cact[:], acc[:], AF.Silu)
            fw.cp("pool", convin[:, :, 0:3], convin[:, :, 128:131])
            for kt in range(8):
                fw.tr(PT3[:, kt, :], cact[:, kt, :], identb[:, :])
            fw.cp("dve", xtok[:], PT[:, :])
            for g in range(2):
                fw.tr(PT[:, g * 128:(g + 1) * 128], cact[:, 8 + g, :], identb[:, :])
            fw.cp("dve", Btok[:], PT[:, 0:256])
            fw.act(sm[:, 0:16], sm[:, 0:16], AF.Exp)
            fw.act(sm[:, 0:16], sm[:, 0:16], AF.Ln, bias=1.0)
            fw.tt("dve", sm[:, 16:32], sm[:, 0:16], A_t[:], ALU.mult)
            fw.mm(PE[:, 32:48], tri_le, sm[:, 16:32])
            fw.mm(PE[:, 48:64], ones, sm[:, 16:32])
            fw.act(sm[:, 32:48], PE[:, 32:48], AF.Exp)
            fw.cp("dve", sm[:, 64:80], PE[:, 32:48])
            fw.tt("dve", sm[:, 48:64], PE[:, 48:64], sm[:, 64:80], ALU.subtract)
            fw.act(sm[:, 48:64], sm[:, 48:64], AF.Exp)
            fw.tt("dve", sm[:, 48:64], sm[:, 48:64], sm[:, 0:16], ALU.mult)
            fw.act(sm[:, 80:96], PE[:, 48:64], AF.Exp)
            for g in range(2):
                fw.mm(PE[:, 128 + g * 128:256 + g * 128], cact[:, 8 + g, :], cact[:, 10 + g, :])
                fw.tt("dve", CBm[:, g, :], PE[:, 128 + g * 128:256 + g * 128], tri_le, ALU.mult)
            for hq in range(4):
                L = Lh[hq % 2]
                ps3 = pcd[hq % 2][:, :].rearrange("p (a m) -> p a m", a=4)
                for i in range(4):
                    h = hq * 4 + i
                    fw.ts("dve", L[:, i, :], mask_gt, sm[:, 16 + h:17 + h], None, ALU.mult)
                    fw.mm(ps3[:, i, :], L[:, i, :], tri_le)
                fw.act(Eh[:], ps3, AF.Exp)
                for i in range(4):
                    h = hq * 4 + i
                    fw.stt(Wt[:, h, :], Eh[:, i, :], sm[:, h:h + 1], CBm[:, h // 8, :], ALU.mult, ALU.mult)
            for h in range(16):
                fw.mm(PA[:, h * 64:(h + 1) * 64], Wt[:, h, :], xtok[:, h * 64:(h + 1) * 64])
            for g in range(2):
                fw.mm(PB[:, g * 512:(g + 1) * 512], cact[:, 10 + g, :], STb[:, g * 512:(g + 1) * 512])
            fw.tt("dve", v16(t1[:, :]), v16(PB[:, :]), sm[:, 32:48].bc(2, 64), ALU.mult)
            fw.tt("dve", t1[:], t1[:], PA[:, :], ALU.add)
            fw.tt("pool", v16(junk[:, :]), v16(xtok[:, :]), D_bc.bc(2, 64), ALU.mult)
            fw.tt("dve", t1[:], t1[:], junk[:], ALU.add)
            fw.tt("dve", t1[:], t1[:], zs[:], ALU.mult)
            for g in range(2):
                grp_rstd(t1[:, g * 512:(g + 1) * 512], 512, nst[:, 7:8], junk)
                fw.stt(yn[:, g * 512:(g + 1) * 512], t1[:, g * 512:(g + 1) * 512], nst[:, 7:8],
                       gssd[:, g * 512:(g + 1) * 512], ALU.mult, ALU.mult)
            to_feat(yn, ynT, 128)
            fw.tt("pool", v16(xw[:, :]), v16(xtok[:, :]), sm[:, 48:64].bc(2, 64), ALU.mult)
            for g in range(2):
                fw.mm(PB[:, g * 512:(g + 1) * 512], Btok[:, g * 128:(g + 1) * 128], xw[:, g * 512:(g + 1) * 512])
            fw.tt("dve", v16(ST[:, :]), v16(ST[:, :]), sm[:, 80:96].bc(2, 64), ALU.mult)
            fw.tt("dve", ST[:], ST[:], PB[:, :], ALU.add)
            fw.cp("pool", STb[:], ST[:])
            for half in range(2):
                for kt in range(8):
                    fw.mm(PA[:, half * 512:(half + 1) * 512], ynT[:, kt, :], Wo[:, kt, half * 512:(half + 1) * 512],
                          start=kt == 0, stop=kt == 7)
            fw.tt("dve", x1[:], xt[:], PA[:, :], ALU.add)
            fw.dma("sp", io.s1[c * 128:(c + 1) * 128, :], x1[:])

        if SAMPLE:
            dtcol = fw.sb([16, 4], F32, "dtcol")
            fw.dma("sp", dtcol[:], io.dtcol[:, :])
            fw.act(dtcol[:, 3:4], dtcol[:, 1:2], AF.Exp)
            fw.ts("dve", dtcol[:, 3:4], dtcol[:, 3:4], -1.0, None, ALU.mult)
            cst_s = fw.sb([128, 12, 3, NB], F32, "cst_s")
            fw.dma("sp", cst_s[:], io.conv_s_in[:, 0:12, :, :])
            uS = fw.sb([128, 12, NB], F32, "uS")
            accs = fw.sb([128, 12, NB], F32, "accs")
            tmps = fw.sb([128, 12, NB], F32, "tmps")
            cs = fw.sb([128, 12, NB], F32, "cs")
            zsT = fw.sb([128, 8, NB], F32, "zsT")
            dd = fw.sb([16, 48], F32, "dd")
            dx = fw.sb([128, 8, 48], F32, "dx")
            dtx = fw.sb([128, 8, NB], F32, "dtx")
            BCtok = fw.sb([16, 512], F32, "BCtok")
            Sb = [fw.sb([128, 8, 128], F32, f"Sb{i}") for i in range(2)]
            T1s = fw.sb([128, 8, 128], F32, "T1s")
            ysT = fw.sb([128, 8, NB], F32, "ysT")
            fw.dma("sp", xt[0:NB, :], io.xs[:, :])
            rmsnorm(xt[0:NB, :], gmix, xn[0:NB, :], NB, junk)
            to_feat(xn, xnT, NB)
            proj_feat(Wc, 0, 12, xnT, NB, lambda g0, n, ps: fw.cp("act", uS[:, g0:g0 + n, :], ps))
            proj_feat(Wz, 0, 8, xnT, NB, lambda g0, n, ps: fw.act(zsT[:, g0:g0 + n, :], ps, AF.Silu))
            wv = lambda j: convp[:, 0:12, j].bc(2, NB)
            fw.tt("dve", accs[:], cst_s[:, :, 0, :], wv(0), ALU.mult)
            fw.tt("dve", accs[:], accs[:], wv(4), ALU.add)
            for j in (1, 2):
                fw.tt("dve", tmps[:], cst_s[:, :, j, :], wv(j), ALU.mult)
                fw.tt("dve", accs[:], accs[:], tmps[:], ALU.add)
            fw.tt("dve", tmps[:], uS[:], wv(3), ALU.mult)
            fw.tt("dve", accs[:], accs[:], tmps[:], ALU.add)
            fw.act(cs[:], accs[:], AF.Silu)
            fw.dma("sp", io.conv_s[:, 0:12, 0:2, :], cst_s[:, :, 1:3, :])
            fw.dma("sp", io.conv_s[:, 0:12, 2, :], uS[:])
            for kt in range(8):
                fw.mm(PE[0:16, 0:16], Wdt[:, kt, :], xnT[:, kt, 0:NB], start=kt == 0, stop=kt == 7)
            fw.ts("dve", dd[:, 0:16], PE[0:16, 0:16], dtcol[:, 0:1], None, ALU.add)
            fw.act(dd[:, 0:16], dd[:, 0:16], AF.Exp)
            fw.act(dd[:, 0:16], dd[:, 0:16], AF.Ln, bias=1.0)
            fw.ts("dve", dd[:, 16:32], dd[:, 0:16], dtcol[:, 3:4], None, ALU.mult)
            fw.act(dd[:, 16:32], dd[:, 16:32], AF.Exp)
            fw.ts("dve", dd[:, 32:48], ones[0:16, 0:16], dtcol[:, 2:3], None, ALU.mult)
            for j in range(8):
                fw.mm(PE[:, 128 + j * 48:128 + (j + 1) * 48], exp16[:, j, :], dd[:, :])
            fw.cp("dve", dx[:], PE[:, 128:512].rearrange("p (j c) -> p j c", j=8))
            fw.tt("dve", dtx[:], dx[:, :, 0:16], cs[:, 0:8, :], ALU.mult)
            for i in range(4):
                fw.tr(PD[0:16, i * 128:(i + 1) * 128], cs[:, 8 + i, :], ident)
            fw.cp("dve", BCtok[:], PD[0:16, :])
            for b in range(NB):
                S = Sb[b % 2]
                fw.dma("sp", S[:], io.ssm_s_in[b].rearrange("(j q) n -> q j n", q=128))
                fw.mm(PC[:, :], sel16[:, b, :], BCtok[:, :])
                for g in range(2):
                    fw.tt("dve", T1s[:, 4 * g:4 * g + 4, :], PC[:, g * 128:(g + 1) * 128].bc(1, 4),
                          dtx[:, 4 * g:4 * g + 4, b].bc(2, 128), ALU.mult)
                fw.tt("pool", S[:], S[:], dx[:, :, 16 + b].bc(2, 128), ALU.mult)
                fw.tt("dve", S[:], S[:], T1s[:], ALU.add)
                fw.dma("sp", io.ssm_s[b].rearrange("(j q) n -> q j n", q=128), S[:])
                for g in range(2):
                    fw.tt("dve", T1s[:, 4 * g:4 * g + 4, :], S[:, 4 * g:4 * g + 4, :],
                          PC[:, 256 + g * 128:256 + (g + 1) * 128].bc(1, 4), ALU.mult)
                fw.red(ysT[:, :, b], T1s[:], ALU.add)
            fw.tt("dve", dtx[:], dx[:, :, 32:48], cs[:, 0:8, :], ALU.mult)
            fw.tt("dve", ysT[:], ysT[:], dtx[:], ALU.add)
            fw.tt("dve", ysT[:], ysT[:], zsT[:], ALU.mult)
            for j in range(8):
                fw.tr(PA[0:16, j * 128:(j + 1) * 128], ysT[:, j, :], ident)
            fw.cp("dve", t1[0:NB, :], PA[0:NB, :])
            for g in range(2):
                grp_rstd(t1[0:NB, g * 512:(g + 1) * 512], 512, nst[0:NB, 7:8], junk, NB)
                fw.stt(yn[0:NB, g * 512:(g + 1) * 512], t1[0:NB, g * 512:(g + 1) * 512], nst[0:NB, 7:8],
                       gssd[0:NB, g * 512:(g + 1) * 512], ALU.mult, ALU.mult)
            to_feat(yn, ynT, NB)
            for half in range(2):
                for kt in range(8):
                    fw.mm(PA[0:NB, half * 512:(half + 1) * 512], ynT[:, kt, 0:NB], Wo[:, kt, half * 512:(half + 1) * 512],
                          start=kt == 0, stop=kt == 7)
            fw.tt("dve", x1[0:NB, :], xt[0:NB, :], PA[0:NB, :], ALU.add)
            fw.dma("sp", io.s1s[:, :], x1[0:NB, :])
        fw.dma("sp", io.ssm_p[:, :], ST[:])
        fw.dma("sp", io.conv_p[:, 0:12, :], convin[:, :, 0:3])
        fw.release(base_mark)

    if "0b" in phases:
        Wx = fw.sb([128, 8, 1024], BF16, "Wx")
        load_w(Wx, io.w_in0[:, 2560:3584], 0, 8)
        Wg = fw.sb([128, 8, 1024], BF16, "Wg")
        load_w(Wg, io.w_in0[:, 3600:4624], 0, 8)
        Wif = fw.sb([128, 8, 16], BF16, "Wif")
        load_w(Wif, io.w_in0[:, 4616:4632], 0, 8, step=8)
        Wo = fw.sb([128, 8, D], BF16, "Wo")
        load_w(Wo, io.w_out0[1024:2048, :], 0, 8)
        BDq = fw.sb([128, 8, 128], BF16, "BDq")
        BDk = fw.sb([128, 8, 128], BF16, "BDk")
        BDv = fw.sb([128, 8, 128], BF16, "BDv")
        fw.dma("pool", BDq[:], io.bdq[:, :, :])
        fw.dma("pool", BDk[:], io.bdk[:, :, :])
        fw.dma("pool", BDv[:], io.bdv[:, :, :])
        gmix = fw.sb([128, D], F32, "gmix")
        fw.dma("sp", gmix[:], io.norm_mix[0, :].partition_broadcast(128))
        sm0 = fw.sb([128, 64], F32, "sm0")
        fw.dma("sp", sm0[:], io.small0.partition_broadcast(128))
        ib_bc, fb_bc = sm0[:, 48:52], sm0[:, 52:56]
        convp = fw.sb([128, 20, 5], F32, "convp")
        fw.dma("sp", convp[:], io.convp[:, :, :])
        mlcol = fw.sb([128, 8, 2], F32, "mlcol")
        fw.dma("sp", mlcol[:], io.mlcol[:, :, :])
        convin = fw.sb([128, 8, 131], F32, "convin")
        fw.memset("pool", convin[:], 0.0)
        Cst = fw.sb([128, 2, 4, 264], F32, "Cst")
        fw.memset("pool", Cst[:], 0.0)
        Cb = fw.sb([128, 2, 4, 264], BF16, "Cb")
        fw.memset("pool", Cb[:], 0.0)
        mprev = fw.sb([128, 4], F32, "mprev")
        fw.memset("pool", mprev[:], 0.0)
        xt = fw.sb([128, D], F32, "xt")
        junk = fw.sb([128, D], F32, "junk")
        xn = fw.sb([128, D], BF16, "xn")
        xnT = fw.sb([128, 8, 128], BF16, "xnT")
        acc = fw.sb([128, 8, 128], F32, "acc")
        cact = fw.sb([128, 8, 128], BF16, "cact")
        xmraw = fw.sb([128, 8, 128], BF16, "xmraw")
        sigoT = fw.sb([128, 8, 128], BF16, "sigoT")
        sm2 = fw.sb([128, 64], F32, "sm2")
        qT = fw.sb([128, 8, 128], BF16, "qT")
        kT = fw.sb([128, 8, 128], BF16, "kT")
        vtok = fw.sb([128, 4, 264], BF16, "vtok")
        fw.memset("pool", vtok[:], 1.0)
        kw_ = fw.sb([128, 4, 256], BF16, "kw")
        Rh = fw.sb([128, 128], F32, "Rh")
        dlm = fw.sb([128, 128], F32, "dlm")
        Dm = fw.sb([128, 128], F32, "Dm")
        Sg = fw.sb([128, 128], BF16, "Sg")
        SgT = fw.sb([128, 128], BF16, "SgT")
        hs = fw.sb([128, 16], F32, "hs")
        mt = fw.sb([128, 16], F32, "mt")
        fw.memset("pool", mt[:], 0.0)
        fw.memset("pool", sm2[:], 0.0)
        comb = fw.sb([128, 258], F32, "comb")
        hh = fw.sb([128, 256], F32, "hh")
        hmn = fw.sb([128, D], BF16, "hmn")
        hmnT = fw.sb([128, 8, 128], BF16, "hmnT")
        hmfT = fw.sb([128, 8, 128], BF16, "hmfT")
        x1 = fw.sb([128, D], F32, "x1")

        lvl = cfg.get('lvl', 99)
        for c in range(NCH):
            fw.dma("sp", xt[:], io.xp[c * 128:(c + 1) * 128, :])
            fw.dma("sp", x1[:], io.s1[c * 128:(c + 1) * 128, :])
            rmsnorm(xt[:], gmix, xn[:], 128, junk)
            to_feat(xn, xnT, 128)
            proj_feat(Wx, 0, 8, xnT, 128, lambda g0, n, ps: fw.cp("act", convin[:, g0:g0 + n, 3:131], ps))
            proj_feat(Wg, 0, 8, xnT, 128, lambda g0, n, ps: fw.act(sigoT[:, g0:g0 + n, :], ps, AF.Sigmoid))
            for kt in range(8):
                fw.mm(PE[:, 16:32], xnT[:, kt, :], Wif[:, kt, :], start=kt == 0, stop=kt == 7)
            fw.tt("dve", sm2[:, 0:4], PE[:, 24:28], ib_bc, ALU.add)
            fw.tt("dve", sm2[:, 4:8], PE[:, 28:32], fb_bc, ALU.add)
            conv_tiles(convin, convp, acc, 12, 8)
            fw.act(cact[:], acc[:], AF.Silu)
            fw.cp("pool", xmraw[:], convin[:, :, 3:131])
            fw.cp("pool", convin[:, :, 0:3], convin[:, :, 128:131])
            if lvl < 2:
                continue
            for tile in range(8):
                ps = pcd[tile % 2]
                fw.mm(ps[:, 0:128], (Wx[:, tile, 0:128] if cfg.get('alt') else BDq[:, tile, :]), cact[:, tile, :])
                fw.mm(ps[:, 128:256], (Wx[:, tile, 0:128] if cfg.get('alt') else BDk[:, tile, :]), cact[:, tile, :])
                if cfg.get('alt') != 2:
                    fw.cp("dve", qT[:, tile, :], ps[:, 0:128])
                if cfg.get('alt') not in (2, 3):
                    fw.ts("dve", kT[:, tile, :], ps[:, 128:256], 0.0625, None, ALU.mult)
            if lvl < 2.1:
                continue
            for tile in range(8):
                fw.mm(PA[:, tile * 128:(tile + 1) * 128], xmraw[:, tile, :], BDv[:, tile, :])
                fw.mm(PB[:, tile * 128:(tile + 1) * 128], cact[:, tile, :], BDk[:, tile, :])
            if lvl < 2.2:
                continue
            fw.cp("act", vtok[:, :, 0:256], PA[:, :].rearrange("p (h v) -> p h v", h=4))
            if lvl < 2.3:
                continue
            fw.act(sm2[:, 4:8], sm2[:, 4:8], AF.Exp, scale=-1.0)
            fw.act(sm2[:, 4:8], sm2[:, 4:8], AF.Ln, bias=1.0)
            fw.ts("dve", sm2[:, 4:8], sm2[:, 4:8], -1.0, None, ALU.mult)
            fw.mm(PE[:, 64:80], tri_le, sm2[:, 0:16])
            fw.mm(PE[:, 96:112], ones, sm2[:, 0:16])
            fw.cp("dve", sm2[:, 8:12], PE[:, 68:72])
            fw.cp("dve", sm2[:, 12:16], PE[:, 100:104])
            fw.tt("dve", sm2[:, 16:20], sm2[:, 8:12], mprev[:], ALU.add)
            if lvl < 3:
                continue
            for h in range(4):
                fw.ts("dve", Rh[:], mask_gt, sm2[:, 4 + h:5 + h], None, ALU.mult)
                fw.stt(Rh[:], ident, sm2[:, h:h + 1], Rh[:], ALU.mult, ALU.add)
                fw.mm(PD[:, 0:128], tri_le, Rh[:])
                fw.tt("dve", dlm[:], PD[:, 0:128], negmask, ALU.add)
                fw.red(hs[:, 0:1], dlm[:], ALU.max)
                fw.tt("dve", mt[:, h:h + 1], hs[:, 0:1], sm2[:, 16 + h:17 + h], ALU.max)
                fw.ts("dve", hs[:, 1:2], mt[:, h:h + 1], -1.0, None, ALU.mult)
                fw.act(Dm[:], dlm[:], AF.Exp, bias=hs[:, 1:2])
                fw.mm(PD[:, 128:256], qT[:, 2 * h, :], kT[:, 2 * h, :], start=True, stop=False)
                fw.mm(PD[:, 128:256], qT[:, 2 * h + 1, :], kT[:, 2 * h + 1, :], start=False, stop=True)
                fw.tt("dve", Sg[:], PD[:, 128:256], Dm[:], ALU.mult)
                fw.tr(PT[:, 0:128], Sg[:], identb[:, :])
                fw.cp("dve", SgT[:], PT[:, 0:128])
                fw.mm(PA[:, 0:258], SgT[:], vtok[:, h, 0:258])
                fw.mm(PA[:, 512:770], qT[:, 2 * h, :], Cb[:, 0, h, 0:258], start=True, stop=False)
                fw.mm(PA[:, 512:770], qT[:, 2 * h + 1, :], Cb[:, 1, h, 0:258], start=False, stop=True)
                fw.act(hs[:, 2:3], sm2[:, 16 + h:17 + h], AF.Exp, bias=hs[:, 1:2])
                fw.act(comb[:], PA[:, 512:770], AF.Copy, scale=hs[:, 2:3])
                fw.tt("dve", comb[:], comb[:], PA[:, 0:258], ALU.add)
                fw.act(hs[:, 3:4], mt[:, h:h + 1], AF.Exp, scale=-1.0)
                fw.ts("dve", hs[:, 6:7], comb[:, 256:257], -1.0, None, ALU.mult)
                fw.tt("dve", hs[:, 6:7], hs[:, 6:7], comb[:, 256:257], ALU.max)
                fw.tt("dve", hs[:, 4:5], hs[:, 6:7], hs[:, 3:4], ALU.max)
                fw.recip(hs[:, 5:6], hs[:, 4:5])
                fw.ts("dve", hh[:], comb[:, 0:256], hs[:, 5:6], None, ALU.mult)
                grp_rstd(hh[:], 256, nst[:, 7:8], junk)
                fw.ts("dve", hmn[:, h * 256:(h + 1) * 256], hh[:], nst[:, 7:8], None, ALU.mult)
            if lvl < 4:
                continue
            to_feat(hmn, hmnT, 128)
            for tile in range(8):
                fw.ts("dve", hmfT[:, tile, :], hmnT[:, tile, :], mlcol[:, tile, 0:1], None, ALU.mult)
                fw.stt(hmfT[:, tile, :], cact[:, tile, :], mlcol[:, tile, 1:2], hmfT[:, tile, :], ALU.mult, ALU.add)
            fw.tt("dve", hmfT[:], hmfT[:], sigoT[:], ALU.mult)
            if lvl < 5:
                continue
            fw.mm(PE[:, 112:128], sel127, mt[:])
            fw.cp("dve", sm2[:, 20:24], PE[:, 112:116])
            fw.tt("dve", sm2[:, 24:28], sm2[:, 12:16], sm2[:, 8:12], ALU.subtract)
            fw.tt("dve", sm2[:, 24:28], sm2[:, 24:28], sm2[:, 0:4], ALU.add)
            fw.tt("dve", sm2[:, 24:28], sm2[:, 24:28], sm2[:, 20:24], ALU.subtract)
            fw.act(sm2[:, 28:32], sm2[:, 24:28], AF.Exp)
            fw.ts("dve", sm2[:, 28:32], sm2[:, 28:32], 0.0625, None, ALU.mult)
            fw.tt("dve", sm2[:, 32:36], sm2[:, 12:16], mprev[:], ALU.add)
            fw.tt("dve", sm2[:, 32:36], sm2[:, 32:36], sm2[:, 20:24], ALU.subtract)
            fw.act(sm2[:, 32:36], sm2[:, 32:36], AF.Exp)
            fw.tt("dve", kw_[:], PB[:, :].rearrange("p (h d) -> p h d", h=4), sm2[:, 28:32].bc(2, 256), ALU.mult)
            for kt in range(2):
                for h in range(4):
                    fw.mm(PB[:, h * 256:(h + 1) * 256], kw_[:, h, kt * 128:(kt + 1) * 128], vtok[:, h, 0:256])
                    fw.mm(PE[:, 80 + 2 * h:82 + 2 * h], kw_[:, h, kt * 128:(kt + 1) * 128], onesb[:, 0:2])
                for h in range(4):
                    fw.stt(Cst[:, kt, h, 0:256], Cst[:, kt, h, 0:256], sm2[:, 32 + h:33 + h],
                           PB[:, h * 256:(h + 1) * 256], ALU.mult, ALU.add)
                    fw.stt(Cst[:, kt, h, 256:257], Cst[:, kt, h, 256:257], sm2[:, 32 + h:33 + h],
                           PE[:, 80 + 2 * h:81 + 2 * h], ALU.mult, ALU.add)
            fw.cp("pool", Cb[:], Cst[:])
            fw.cp("dve", mprev[:], sm2[:, 20:24])
            if lvl < 6:
                continue
            for half in range(2):
                for kt in range(8):
                    fw.mm(PA[:, half * 512:(half + 1) * 512], hmfT[:, kt, :], Wo[:, kt, half * 512:(half + 1) * 512],
                          start=kt == 0, stop=kt == 7)
            fw.tt("dve", x1[:], x1[:], PA[:, :], ALU.add)
            fw.dma("sp", io.s1[c * 128:(c + 1) * 128, :], x1[:])

        if SAMPLE:
            cst_s = fw.sb([128, 8, 3, NB], F32, "cst_s")
            fw.dma("sp", cst_s[:], io.conv_s_in[:, 12:20, :, :])
            uS = fw.sb([128, 8, NB], F32, "uS")
            accs = fw.sb([128, 8, NB], F32, "accs")
            tmps = fw.sb([128, 8, NB], F32, "tmps")
            cs = fw.sb([128, 8, NB], F32, "cs")
            cs_bf = fw.sb([128, 8, NB], BF16, "cs_bf")
            us_bf = fw.sb([128, 8, NB], BF16, "us_bf")
            qTs = fw.sb([128, 8, NB], F32, "qTs")
            kTs = fw.sb([128, 8, NB], F32, "kTs")
            kws = fw.sb([128, 8, NB], F32, "kws")
            nS = fw.sb([128, 8, NB], F32, "nS")
            vtoks = fw.sb([16, D], F32, "vtoks")
            g16 = fw.sb([16, 64], F32, "g16")
            Zd = fw.sb([16, 128], F32, "Zd")
            wd = fw.sb([128, 2, 4, NB], F32, "wd")
            qmask = fw.sb([128, 8, NB, NB], F32, "qmask")
            Cs = [fw.sb([128, 8, 256], F32, f"Cs{i}") for i in range(2)]
            Tt = fw.sb([128, 8, 256], F32, "Tt")
            numt = fw.sb([16, D], F32, "numt")
            fw.dma("sp", xt[0:NB, :], io.xs[:, :])
            fw.dma("sp", x1[0:NB, :], io.s1s[:, :])
            fw.dma("sp", g16[:, 8:12], io.mm_s_in[:, :])
            fw.dma("sp", nS[:], io.mn_s_in[:, :, :])
            rmsnorm(xt[0:NB, :], gmix, xn[0:NB, :], NB, junk)
            to_feat(xn, xnT, NB)
            proj_feat(Wx, 0, 8, xnT, NB, lambda g0, n, ps: fw.cp("act", uS[:, g0:g0 + n, :], ps))
            proj_feat(Wg, 0, 8, xnT, NB, lambda g0, n, ps: fw.act(sigoT[:, g0:g0 + n, 0:NB], ps, AF.Sigmoid))
            for kt in range(8):
                fw.mm(PE[0:NB, 16:32], xnT[:, kt, 0:NB], Wif[:, kt, :], start=kt == 0, stop=kt == 7)
            fw.tt("dve", g16[:, 0:4], PE[0:NB, 24:28], ib_bc[0:NB, :], ALU.add)
            fw.tt("dve", g16[:, 4:8], PE[0:NB, 28:32], fb_bc[0:NB, :], ALU.add)
            fw.act(g16[:, 4:8], g16[:, 4:8], AF.Exp, scale=-1.0)
            fw.act(g16[:, 4:8], g16[:, 4:8], AF.Ln, bias=1.0)
            fw.ts("dve", g16[:, 4:8], g16[:, 4:8], -1.0, None, ALU.mult)
            wv = lambda j: convp[:, 12:20, j].bc(2, NB)
            fw.tt("dve", accs[:], cst_s[:, :, 0, :], wv(0), ALU.mult)
            fw.tt("dve", accs[:], accs[:], wv(4), ALU.add)
            for j in (1, 2):
                fw.tt("dve", tmps[:], cst_s[:, :, j, :], wv(j), ALU.mult)
                fw.tt("dve", accs[:], accs[:], tmps[:], ALU.add)
            fw.tt("dve", tmps[:], uS[:], wv(3), ALU.mult)
            fw.tt("dve", accs[:], accs[:], tmps[:], ALU.add)
            fw.act(cs[:], accs[:], AF.Silu)
            fw.dma("sp", io.conv_s[:, 12:20, 0:2, :], cst_s[:, :, 1:3, :])
            fw.dma("sp", io.conv_s[:, 12:20, 2, :], uS[:])
            fw.cp("dve", cs_bf[:], cs[:])
            fw.cp("dve", us_bf[:], uS[:])
            for tile in range(8):
                ps = pcd[tile % 2]
                fw.mm(ps[:, 0:NB], BDq[:, tile, :], cs_bf[:, tile, :])
                fw.mm(ps[:, 16:16 + NB], BDk[:, tile, :], cs_bf[:, tile, :])
                fw.cp("dve", qTs[:, tile, :], ps[:, 0:NB])
                fw.ts("dve", kTs[:, tile, :], ps[:, 16:16 + NB], 0.0625, None, ALU.mult)
            for tile in range(8):
                fw.mm(PA[0:NB, tile * 128:(tile + 1) * 128], us_bf[:, tile, :], BDv[:, tile, :])
            fw.cp("act", vtoks[:], PA[0:NB, :])
            fw.tt("dve", g16[:, 16:20], g16[:, 4:8], g16[:, 8:12], ALU.add)
            fw.tt("dve", g16[:, 12:16], g16[:, 16:20], g16[:, 0:4], ALU.max)
            fw.dma("sp", io.mm_s[:, :], g16[:, 12:16])
            fw.tt("dve", g16[:, 20:24], g16[:, 0:4], g16[:, 12:16], ALU.subtract)
            fw.act(g16[:, 20:24], g16[:, 20:24], AF.Exp)
            fw.tt("dve", g16[:, 24:28], g16[:, 16:20], g16[:, 12:16], ALU.subtract)
            fw.act(g16[:, 24:28], g16[:, 24:28], AF.Exp)
            fw.act(g16[:, 28:32], g16[:, 12:16], AF.Exp, scale=-1.0)
            z3 = lambda r: r.rearrange("p (h b) -> p h b", h=4)
            fw.tt("dve", z3(Zd[:, 0:64]), g16[:, 20:24].bc(2, NB), ident[0:NB, 0:NB].bc(1, 4), ALU.mult)
            fw.tt("dve", z3(Zd[:, 64:128]), g16[:, 24:28].bc(2, NB), ident[0:NB, 0:NB].bc(1, 4), ALU.mult)
            fw.mm(PE[:, 128:256], ones[0:NB, :], Zd[:, :])
            fw.cp("dve", wd[:], PE[:, 128:256].rearrange("p (w h b) -> p w h b", w=2, h=4))
            k4 = lambda r: r.rearrange("p (h k) b -> p h k b", h=4)
            fw.tt("dve", k4(kws[:, :, :]), k4(kTs[:, :, :]), wd[:, 0, :, :].bc(2, 2), ALU.mult)
            fw.tt("dve", k4(nS[:, :, :]), k4(nS[:, :, :]), wd[:, 1, :, :].bc(2, 2), ALU.mult)
            fw.tt("dve", nS[:], nS[:], kws[:], ALU.add)
            fw.dma("sp", io.mn_s[:, :, :], nS[:])
            fw.tt("dve", tmps[:], qTs[:], nS[:], ALU.mult)
            for h in range(4):
                for kt in range(2):
                    fw.mm(PE[0:NB, 256 + 2 * h:258 + 2 * h], tmps[:, 2 * h + kt, :], ones[:, 0:2], start=kt == 0, stop=kt == 1)
            fw.tt("dve", qmask[:], qTs[:, :, :].bc(2, NB), eye16[:, :, :].bc(1, 8), ALU.mult)
            for b in range(NB):
                Cc = Cs[b % 2]
                fw.dma("sp", Cc[:], io.mc_s_in[b].rearrange("h (k p) v -> p (h k) v", p=128))
                fw.mm(PA[:, 0:512], sel16[:, b, :], vtoks[:, 0:512])
                fw.mm(PA[:, 512:1024], sel16[:, b, :], vtoks[:, 512:1024])
                fw.tt("dve", Tt[:, :, :].rearrange("p (h k) v -> p h k v", h=4),
                      PA[:, :].rearrange("p (h v) -> p h v", h=4).bc(2, 2),
                      kws[:, :, b].rearrange("p (h k) -> p h k", h=4).bc(3, 256), ALU.mult)
                fw.tt("pool", Cc[:, :, :].rearrange("p (h k) v -> p h (k v)", h=4),
                      Cc[:, :, :].rearrange("p (h k) v -> p h (k v)", h=4), wd[:, 1, :, b].bc(2, 512), ALU.mult)
                fw.tt("dve", Cc[:], Cc[:], Tt[:], ALU.add)
                fw.dma("sp", io.mc_s[b].rearrange("h (k p) v -> p (h k) v", p=128), Cc[:])
                for tile in range(8):
                    h, kt = tile // 2, tile % 2
                    fw.mm(PB[0:NB, h * 256:(h + 1) * 256], qmask[:, tile, b, :], Cc[:, tile, :],
                          start=(b == 0 and tile in (0, 4)), stop=(b == NB - 1 and kt == 1))
            fw.cp("act", numt[:], PB[0:NB, :])
            if "dbg_a" in dbg:
                fw.dma("sp", io.dbg_a[:, :], numt[:])
                fw.cp("dve", g16[:, 40:44], PE[0:NB, 256:264].rearrange("p (h t) -> p h t", t=2)[:, :, 0])
                fw.dma("sp", io.dbg_b[:, :], g16[:])
            dn = PE[0:NB, 256:264].rearrange("p (h t) -> p h t", t=2)[:, :, 0]
            fw.ts("dve", g16[:, 32:36], dn, -1.0, None, ALU.mult)
            fw.tt("dve", g16[:, 32:36], g16[:, 32:36], dn, ALU.max)
            fw.tt("dve", g16[:, 32:36], g16[:, 32:36], g16[:, 28:32], ALU.max)
            fw.recip(g16[:, 36:40], g16[:, 32:36])
            fw.tt("dve", numt[:, :].rearrange("p (h v) -> p h v", h=4), numt[:, :].rearrange("p (h v) -> p h v", h=4),
                  g16[:, 36:40].bc(2, 256), ALU.mult)
            for h in range(4):
                grp_rstd(numt[:, h * 256:(h + 1) * 256], 256, nst[0:NB, 7:8], junk, NB)
                fw.ts("dve", hmn[0:NB, h * 256:(h + 1) * 256], numt[:, h * 256:(h + 1) * 256], nst[0:NB, 7:8], None, ALU.mult)
            to_feat(hmn, hmnT, NB)
            for tile in range(8):
                fw.ts("dve", hmfT[:, tile, 0:NB], hmnT[:, tile, 0:NB], mlcol[:, tile, 0:1], None, ALU.mult)
                fw.stt(hmfT[:, tile, 0:NB], cs_bf[:, tile, :], mlcol[:, tile, 1:2], hmfT[:, tile, 0:NB], ALU.mult, ALU.add)
            fw.tt("dve", hmfT[:, :, 0:NB], hmfT[:, :, 0:NB], sigoT[:, :, 0:NB], ALU.mult)
            for half in range(2):
                for kt in range(8):
                    fw.mm(PA[0:NB, half * 512:(half + 1) * 512], hmfT[:, kt, 0:NB], Wo[:, kt, half * 512:(half + 1) * 512],
                          start=kt == 0, stop=kt == 7)
            fw.tt("dve", x1[0:NB, :], x1[0:NB, :], PA[0:NB, :], ALU.add)
            fw.dma("sp", io.s1s[:, :], x1[0:NB, :])
        fw.dma("sp", io.mc_p[:, :, :, :], Cst[:])
        fw.dma("sp", io.mm_p[:, :], mprev[0:1, :])
        fw.dma("sp", io.conv_p[:, 12:20, :], convin[:, :, 0:3])
        fw.release(base_mark)

    def ffn_phase(layer, src, dst, ssrc, sdst, final):
        Wgu = fw.sb([128, 8, 2 * DFF], BF16, "Wgu")
        load_w(Wgu, io.w_gu[layer], 0, 8, step=1)
        Wd = fw.sb([128, 22, D], BF16, "Wd")
        load_w(Wd, io.w_dn[layer], 0, 22)
        gf = fw.sb([128, D], F32, "gf")
        fw.dma("sp", gf[:], io.norm_ffn[layer, :].partition_broadcast(128))
        if final:
            gfin = fw.sb([128, D], F32, "gfin")
            fw.dma("sp", gfin[:], io.norm_final.partition_broadcast(128))
        GB = 4
        xt = fw.sb([128, D], F32, "xt")
        junk = fw.sb([128, D], F32, "junk")
        xn = fw.sb([128, D], BF16, "xn")
        xnT = fw.sb([128, 8, GB * 128], BF16, "xnT")
        hT = fw.sb([128, 22, GB * 128], BF16, "hT")
        sg = [fw.sb([128, GB * 128], F32, f"sg{i}") for i in range(2)]
        x2 = fw.sb([128, D], F32, "x2")
        yo = junk
        groups = [list(range(g, min(g + GB, NCH))) for g in range(0, NCH, GB)]
        if SAMPLE:
            groups.append([NCH])
        for grp in groups:
            samp = grp[0] == NCH
            M = NB if samp else 128
            W = M * len(grp)
            rows = lambda ap, c: (ap[:, :] if samp else ap[c * 128:(c + 1) * 128, :])
            for gi, c in enumerate(grp):
                fw.dma("sp", xt[0:M, :], rows(ssrc if samp else src, c))
                rmsnorm(xt[0:M, :], gf, xn[0:M, :], M, junk)
                for kt in range(8):
                    fw.tr(PT3[:, kt, 0:M], xn[0:M, kt * 128:(kt + 1) * 128], identb[0:M, 0:M])
                fw.cp("dve", xnT[:, :, gi * M:(gi + 1) * M], PT3[:, :, 0:M])
            for j in range(22):
                psg = pcd[j % 2]
                for kt in range(8):
                    fw.mm(psg[:, 0:W], Wgu[:, kt, j * 128:(j + 1) * 128], xnT[:, kt, 0:W], start=kt == 0, stop=kt == 7)
                psu = PB0f if j % 2 == 0 else PB1f
                for kt in range(8):
                    fw.mm(psu[:, 0:W], Wgu[:, kt, DFF + j * 128:DFF + (j + 1) * 128], xnT[:, kt, 0:W],
                          start=kt == 0, stop=kt == 7)
                fw.act(sg[j % 2][:, 0:W], psg[:, 0:W], AF.Silu)
                fw.tt("dve", hT[:, j, 0:W], sg[j % 2][:, 0:W], psu[:, 0:W], ALU.mult)
            for gi, c in enumerate(grp):
                for half in range(2):
                    for j in range(22):
                        fw.mm(PA[0:M, half * 512:(half + 1) * 512], hT[:, j, gi * M:(gi + 1) * M],
                              Wd[:, j, half * 512:(half + 1) * 512], start=j == 0, stop=j == 21)
                fw.dma("sp", xt[0:M, :], rows(ssrc if samp else src, c))
                fw.tt("dve", x2[0:M, :], xt[0:M, :], PA[0:M, :], ALU.add)
                if final:
                    rmsnorm(x2[0:M, :], gfin, yo[0:M, :], M, junk)
                    fw.dma("sp", rows(sdst if samp else dst, c), yo[0:M, :])
                else:
                    fw.dma("sp", rows(sdst if samp else dst, c), x2[0:M, :])
        fw.release(base_mark)

    if "1" in phases:
        ffn_phase(0, io.s1, io.s2, io.s1s, io.s2s, False)

    if "2" in phases:
        Wr = fw.sb([128, 8, D], BF16, "Wr"); load_w(Wr, io.rw_wr, 0, 8)
        Wk = fw.sb([128, 8, D], BF16, "Wk"); load_w(Wk, io.rw_wk, 0, 8)
        Wv = fw.sb([128, 8, D], BF16, "Wv"); load_w(Wv, io.rw_wv, 0, 8)
        Wo = fw.sb([128, 8, D], BF16, "Wo"); load_w(Wo, io.rw_wo, 0, 8)
        W1 = fw.sb([128, 8, 64], BF16, "W1"); load_w(W1, io.rw_w1, 0, 8, step=8)
        A1 = fw.sb([128, 8, 64], BF16, "A1"); load_w(A1, io.rw_a1, 0, 8, step=8)
        G1 = fw.sb([128, 8, 160], BF16, "G1"); load_w(G1, io.rw_g1, 0, 8, step=8)
        W2 = fw.sb([128, D], BF16, "W2"); fw.dma("pool", W2[0:64, :], io.rw_w2[:, :])
        A2 = fw.sb([128, D], BF16, "A2"); fw.dma("pool", A2[0:64, :], io.rw_a2[:, :])
        G2a = fw.sb([128, D], BF16, "G2a"); fw.dma("pool", G2a[:], io.rw_g2[0:128, :])
        G2b = fw.sb([128, D], BF16, "G2b"); fw.dma("pool", G2b[0:32, :], io.rw_g2[128:160, :])
        gm1 = fw.sb([128, D], F32, "gm1")
        fw.dma("sp", gm1[:], io.norm_mix[1, :].partition_broadcast(128))
        rows = []
        for i in range(7):
            rt = fw.sb([128, D], F32, f"row{i}")
            fw.dma("sp", rt[:], io.rw_rows[i, :].partition_broadcast(128))
            rows.append(rt)
        w0b, a0b, kkb_, kab, rkb, lnw, lnb = rows
        mu = fw.sb([128, 8, 6], F32, "mu")
        fw.dma("sp", mu[:], io.rw_mu[:, :, :])
        PB0 = fw.view(PB[:, 0:512], "PB0")
        PB1 = fw.view(PB[:, 512:1024], "PB1")
        NPS = [PB0, PB1, PC, PD]
        PAh = [PA[:, 0:512], PA[:, 512:1024]]
        PBh = [PB0[:, :], PB1[:, :]]
        h1 = fw.sb([128, 2, 128], BF16, "h1")

        def proj_tok(xT, W, Ph, M=128):
            for half in range(2):
                for kt in range(8):
                    fw.mm(Ph[half][0:M, :], xT[:, kt, 0:M], W[:, kt, half * 512:(half + 1) * 512],
                          start=kt == 0, stop=kt == 7)

        def lora(xT, Wa, nh, Wb_list, func, P, M=128):
            widths = [min(128, nh), nh - 128] if nh > 128 else [nh]
            for wi, wd in enumerate(widths):
                for kt in range(8):
                    fw.mm(PE[0:wd, wi * 128:wi * 128 + M], Wa[:, kt, wi * 128:wi * 128 + wd], xT[:, kt, 0:M],
                          start=kt == 0, stop=kt == 7)
                fw.act(h1[0:wd, wi, 0:M], PE[0:wd, wi * 128:wi * 128 + M], func)
            for half in range(2):
                for wi, wd in enumerate(widths):
                    fw.mm(P[half][0:M, :], h1[0:wd, wi, 0:M], Wb_list[wi][0:wd, half * 512:(half + 1) * 512],
                          start=wi == 0, stop=wi == len(widths) - 1)

        def rstd16(src16, dst16, mult_, eps, floor=None):
            if floor is not None:
                fw.ts("dve", dst16, src16, floor, None, ALU.max)
            else:
                fw.ts("dve", dst16, src16, mult_, eps, ALU.mult, ALU.add)
            fw.act(dst16, dst16, AF.Ln)
            fw.act(dst16, dst16, AF.Exp, scale=-0.5)

        mark2 = fw.mark()
        xt = fw.sb([128, D], F32, "xt")
        junk = fw.sb([128, D], F32, "junk")
        tmpA = fw.sb([128, D], F32, "tmpA")
        tmpB = fw.sb([128, D], F32, "tmpB")
        Et = fw.sb([128, D], F32, "Et")
        SB = [fw.sb([128, D], BF16, f"S{i}") for i in range(13)]
        xn = SB[0]; r_bf = SB[1]; kkn = SB[2]; kf_bf = SB[3]; b_bf = SB[4]; v_bf = SB[5]; bv = SB[6]
        g_bf = SB[7]; abar = SB[8]; bbar = SB[9]; kbar = SB[10]; btil = SB[11]; ktil = SB[12]
        rbar = SB[0]; yo = SB[8]
        xnTe = fw.sb([128, 8, 130], BF16, "xnTe")
        fw.memset("pool", xnTe[:], 0.0)
        xx = fw.sb([128, 8, 128], BF16, "xx")
        mixb = [fw.sb([128, 8, 128], BF16, f"mix{i}") for i in range(2)]
        arT = fw.sb([128, 8, 2, 128], BF16, "arT")
        bT = fw.sb([128, 8, 128], BF16, "bT")
        kT = fw.sb([128, 8, 128], BF16, "kT")
        yoT = fw.sb([128, 8, 128], BF16, "yoT")
        Ms = [fw.sb([128, 512], BF16, f"Ms{i}") for i in range(4)]
        Q0 = [fw.sb([128, 128], BF16, f"Q0{i}") for i in range(4)]
        PQ = [[fw.sb([128, 256], BF16, f"PQ{i}{k}") for k in range(2)] for i in range(4)]
        Zt = [[fw.sb([128, 128], BF16, f"Z{i}{k}") for k in range(2)] for i in range(4)]
        RHSb = [fw.sb([128, 64], BF16, f"RHS{i}") for i in range(4)]
        Ubp = [fw.sb([128, 2, 64], BF16, f"Ubp{i}") for i in range(2)]
        Hst = fw.sb([128, 8, 64], F32, "Hst")
        fw.memset("pool", Hst[:], 0.0)
        Hb = fw.sb([128, 8, 64], BF16, "Hb")
        fw.memset("pool", Hb[:], 0.0)
        eLT = fw.sb([128, 8], F32, "eLT")
        s16 = fw.sb([128, 64], F32, "s16")
        x3 = tmpB
        mcount = [0]

        def mix(cidx):
            dst = mixb[mcount[0] % 2]
            mcount[0] += 1
            for kt in range(8):
                fw.stt(dst[:, kt, :], xx[:, kt, :], mu[:, kt, cidx:cidx + 1], xnTe[:, kt, 1:129], ALU.mult, ALU.add)
            return dst

        for c in range(NCH):
            fw.dma("sp", xt[:], io.s2[c * 128:(c + 1) * 128, :])
            if c == NCH - 1:
                fw.act(junk[:, :], xt[:], AF.Square, accum_out=nst[:, 0:1])
                fw.ts("dve", nst[:, 1:2], nst[:, 0:1], 1.0 / D, EPS, ALU.mult, ALU.add)
                fw.act(nst[:, 2:3], nst[:, 1:2], AF.Ln)
                fw.act(nst[:, 3:4], nst[:, 2:3], AF.Exp, scale=-0.5)
                fw.stt(tmpA[:], xt[:], nst[:, 3:4], gm1[:], ALU.mult, ALU.mult)
                fw.dma("sp", io.shift_p[:, :], tmpA[127:128, :])
                fw.cp("dve", xn[:], tmpA[:])
            else:
                rmsnorm(xt[:], gm1, xn[:], 128, junk)
            for kt in range(8):
                fw.tr(PT3[:, kt, :], xn[:, kt * 128:(kt + 1) * 128], identb[:, :])
            fw.cp("dve", xnTe[:, :, 1:129], PT3)
            fw.tt("pool", xx[:], xnTe[:, :, 0:128], xnTe[:, :, 1:129], ALU.subtract)
            proj_tok(mix(0), Wr, PAh)
            fw.cp("act", r_bf[:], PA[:, :])
            proj_tok(mix(2), Wk, PAh)
            fw.tt("dve", tmpA[:], PA[:, :], kkb_[:], ALU.mult)
            fw.tt("pool", junk[:], tmpA[:], tmpA[:], ALU.mult)
            fw.red(s16[:, 0:16], v16(junk[:, :]), ALU.add)
            rstd16(s16[:, 0:16], s16[:, 16:32], None, None, floor=1e-24)
            fw.tt("dve", v16(kkn[:, :]), v16(tmpA[:, :]), s16[:, 16:32].bc(2, 64), ALU.mult)
            lora(mix(4), A1, 64, [A2], AF.Copy, PBh)
            for i in range(2):
                fw.tt("dve", tmpB[:, i * 512:(i + 1) * 512], PBh[i], a0b[:, i * 512:(i + 1) * 512], ALU.add)
            fw.act(tmpB[:], tmpB[:], AF.Sigmoid)
            fw.stt(junk[:], tmpB[:], 1.0, kab[:], ALU.subtract, ALU.mult)
            fw.ts("dve", junk[:], junk[:], 1.0, None, ALU.add)
            fw.tt("dve", kf_bf[:], PA[:, :], junk[:], ALU.mult)
            fw.tt("pool", b_bf[:], kkn[:], tmpB[:], ALU.mult)
            fw.tt("pool", junk[:], r_bf[:], kf_bf[:], ALU.mult)
            fw.tt("pool", junk[:], junk[:], rkb[:], ALU.mult)
            fw.red(s16[:, 32:48], v16(junk[:, :]), ALU.add)
            proj_tok(mix(3), Wv, PAh)
            fw.cp("act", v_bf[:], PA[:, :])
            fw.tt("dve", v16(bv[:, :]), v16(PA[:, :]), s16[:, 32:48].bc(2, 64), ALU.mult)
            lora(mix(5), G1, 160, [G2a, G2b], AF.Sigmoid, PBh)
            for i in range(2):
                fw.cp("act", g_bf[:, i * 512:(i + 1) * 512], PBh[i])
            lora(mix(1), W1, 64, [W2], AF.Tanh, PAh)
            fw.tt("dve", tmpA[:], PA[:, :], w0b[:], ALU.add)
            fw.act(tmpA[:], tmpA[:], AF.Exp, scale=-1.0)
            fw.act(tmpA[:], tmpA[:], AF.Ln, bias=1.0)
            fw.ts("dve", tmpA[:], tmpA[:], -1.0, -0.5, ALU.mult, ALU.add)
            fw.act(Et[:], tmpA[:], AF.Exp)
            fw.mm(PB0[:, :], tri_le, Et[:, 0:512])
            fw.mm(PB1[:, :], tri_le, Et[:, 512:1024])
            fw.mm(PA[:, 0:512], ones, Et[:, 0:512])
            fw.mm(PA[:, 512:1024], ones, Et[:, 512:1024])
            for kt in range(8):
                fw.mm(PE[:, kt * 16:(kt + 1) * 16], Et[:, kt * 128:(kt + 1) * 128], ones[:, 0:16])
            fw.act(eLT[:], PE[:, 0:128].rearrange("p (k s) -> p k s", s=16)[:, :, 0], AF.Exp, scale=-1.0)
            hv = lambda r, i: r[:, i * 512:(i + 1) * 512]
            for i, PBi in enumerate((PB0, PB1)):
                fw.act(hv(tmpA, i), PBi[:, :], AF.Exp, scale=-1.0)
                fw.tt("pool", hv(rbar, i), hv(r_bf, i), hv(tmpA, i), ALU.mult)
                fw.tt("dve", hv(tmpB, i), hv(Et, i), PBi[:, :], ALU.subtract)
                fw.act(hv(tmpB, i), hv(tmpB, i), AF.Exp)
                fw.stt(hv(abar, i), hv(kkn, i), -1.0, hv(tmpB, i), ALU.mult, ALU.mult)
            fw.cp("act", junk[:], PA[:, :])
            for i, PBi in enumerate((PB0, PB1)):
                fw.act(hv(tmpA, i), PBi[:, :], AF.Exp)
                fw.tt("pool", hv(bbar, i), hv(b_bf, i), hv(tmpA, i), ALU.mult)
                fw.tt("pool", hv(kbar, i), hv(kf_bf, i), hv(tmpA, i), ALU.mult)
                fw.tt("dve", hv(tmpB, i), PBi[:, :], hv(junk, i), ALU.subtract)
                fw.act(hv(tmpB, i), hv(tmpB, i), AF.Exp)
                fw.tt("pool", hv(btil, i), hv(b_bf, i), hv(tmpB, i), ALU.mult)
                fw.tt("dve", hv(ktil, i), hv(kf_bf, i), hv(tmpB, i), ALU.mult)
            for src, dst in ((abar, arT[:, :, 0, :]), (rbar, arT[:, :, 1, :]), (bbar, bT[:, :, :]), (kbar, kT[:, :, :])):
                for kt in range(8):
                    fw.tr(PT3[:, kt, :], src[:, kt * 128:(kt + 1) * 128], identb[:, :])
                fw.cp("dve", dst, PT3)
            for h0 in range(0, 16, 4):
                hd = []
                for i in range(4):
                    h = h0 + i
                    j, e = h // 2, h % 2
                    p0 = 64 * e
                    hd.append(dict(h=h, j=j, e=e, p0=p0, NP=NPS[i],
                                   aT=arT[p0:p0 + 64, j, 0, :], rT=arT[p0:p0 + 64, j, 1, :],
                                   ar=arT[p0:p0 + 64, j, :, :].rearrange("p a t -> p (a t)"),
                                   bT=bT[p0:p0 + 64, j, :], kT=kT[p0:p0 + 64, j, :]))
                for i, d in enumerate(hd):
                    fw.mm(PE[:, 0:256], d["bT"], d["ar"])
                    fw.mm(PE[:, 256:512], d["kT"], d["ar"])
                    fw.tt("dve", Ms[i][:], PE[:, :], m4, ALU.mult)
                    fw.mm(d["NP"][:, 0:128], d["aT"], d["bT"])
                    fw.tt("dve", Q0[i][:], d["NP"][:, 0:128], mask_gt, ALU.mult)
                    fw.tt("pool", Zt[i][0][:], Ms[i][:, 0:128], identb[:, :], ALU.add)
                    d["P"], d["Q"], d["Z"] = Ms[i][:, 0:128], Q0[i][:], Zt[i][0][:]
                for k in range(1, 7):
                    for i, d in enumerate(hd):
                        NP = d["NP"]
                        if k < 6:
                            fw.mm(NP[:, 0:128], d["Q"], d["P"])
                        fw.mm(NP[:, 128:256], d["P"], d["Q"])
                        pq = PQ[i][k % 2]
                        lo = 0 if k < 6 else 128
                        fw.cp("act", pq[:, lo:256], NP[:, lo:256])
                        d["P"], d["Q"] = pq[:, 0:128], pq[:, 128:256]
                    for i, d in enumerate(hd):
                        NP = d["NP"]
                        fw.mm(NP[:, 256:384], d["Q"], d["Z"])
                        zn = Zt[i][k % 2]
                        fw.tt("dve", zn[:], d["Z"], NP[:, 256:384], ALU.add)
                        d["Z"] = zn[:]
                for i, d in enumerate(hd):
                    NP, h, j, p0 = d["NP"], d["h"], d["j"], d["p0"]
                    fw.mm(NP[:, 384:448], d["aT"], Hb[p0:p0 + 64, j, :], start=True, stop=False)
                    fw.mm(NP[:, 384:448], Ms[i][:, 256:384], v_bf[:, h * 64:(h + 1) * 64], start=False, stop=True)
                    fw.cp("act", RHSb[i][:], NP[:, 384:448])
                for i, d in enumerate(hd):
                    NP, h, j, e = d["NP"], d["h"], d["j"], d["e"]
                    fw.mm(NP[:, 448:512], d["Z"], RHSb[i][:])
                    fw.cp("dve", Ubp[j % 2][:, e, :], NP[:, 448:512])
                for i, d in enumerate(hd):
                    h, j, e, p0 = d["h"], d["j"], d["e"], d["p0"]
                    ysl = PA[:, h * 64:(h + 1) * 64]
                    fw.mm(ysl, d["rT"], Hb[p0:p0 + 64, j, :], start=True, stop=False)
                    fw.mm(ysl, Ms[i][:, 128:256], Ubp[j % 2][:, e, :], start=False, stop=False)
                    fw.mm(ysl, Ms[i][:, 384:512], v_bf[:, h * 64:(h + 1) * 64], start=False, stop=True)
                for jj in range(2):
                    j = h0 // 2 + jj
                    NP = hd[2 * jj]["NP"]
                    fw.mm(NP[:, 0:128], btil[:, j * 128:(j + 1) * 128], Ubp[j % 2][:, :, :].rearrange("p e v -> p (e v)"),
                          start=True, stop=False)
                    fw.mm(NP[:, 0:128], ktil[:, j * 128:(j + 1) * 128], v_bf[:, j * 128:(j + 1) * 128],
                          start=False, stop=True)
                    for e in range(2):
                        p0 = 64 * e
                        fw.stt(Hst[p0:p0 + 64, j, :], Hst[p0:p0 + 64, j, :], eLT[p0:p0 + 64, j:j + 1],
                               NP[p0:p0 + 64, p0:p0 + 64], ALU.mult, ALU.add)
                    fw.cp("pool", Hb[:, j, :], Hst[:, j, :])
            fw.cp("act", tmpA[:], PA[:, :])
            fw.red(s16[:, 0:16], v16(tmpA[:, :]), ALU.add)
            fw.ts("dve", s16[:, 0:16], s16[:, 0:16], 1.0 / 64, None, ALU.mult)
            fw.tt("dve", v16(tmpA[:, :]), v16(tmpA[:, :]), s16[:, 0:16].bc(2, 64), ALU.subtract)
            fw.tt("pool", junk[:], tmpA[:], tmpA[:], ALU.mult)
            fw.red(s16[:, 16:32], v16(junk[:, :]), ALU.add)
            rstd16(s16[:, 16:32], s16[:, 48:64], 1.0 / 64, 64e-5)
            fw.tt("dve", v16(tmpA[:, :]), v16(tmpA[:, :]), s16[:, 48:64].bc(2, 64), ALU.mult)
            fw.tt("dve", tmpA[:], tmpA[:], lnw[:], ALU.mult)
            fw.tt("dve", tmpA[:], tmpA[:], lnb[:], ALU.add)
            fw.tt("dve", tmpA[:], tmpA[:], bv[:], ALU.add)
            fw.tt("dve", yo[:], tmpA[:], g_bf[:], ALU.mult)
            to_feat(yo, yoT, 128)
            for half in range(2):
                for kt in range(8):
                    fw.mm(PA[:, half * 512:(half + 1) * 512], yoT[:, kt, :], Wo[:, kt, half * 512:(half + 1) * 512],
                          start=kt == 0, stop=kt == 7)
            fw.tt("dve", x3[:], xt[:], PA[:, :], ALU.add)
            fw.dma("sp", io.s3[c * 128:(c + 1) * 128, :], x3[:])
            fw.cp("pool", xnTe[:, :, 0:1], xnTe[:, :, 128:129])
        fw.dma("sp", io.wkv_p[:, :, :], Hst[:])
        fw.release(mark2)
        if SAMPLE:
            M = NB
            f16 = lambda nm, dt=F32: fw.sb([NB, D], dt, nm)
            xts = f16("xts"); jk = f16("jk"); tA = f16("tA"); tB = f16("tB"); Es = f16("Es")
            r_s = f16("r_s", BF16); kk_s = f16("kk_s", BF16); kf_s = f16("kf_s", BF16); b_s = f16("b_s", BF16)
            v_s = f16("v_s"); bv_s = f16("bv_s", BF16); g_s = f16("g_s", BF16); xn_s = f16("xn_s", BF16)
            sa_tok = f16("sa_tok"); yo_s = f16("yo_s", BF16)
            xprev = fw.sb([128, 8, NB], F32, "xprev")
            xsT = fw.sb([128, 8, NB], BF16, "xsT")
            xxs = fw.sb([128, 8, NB], BF16, "xxs")
            mixs = [fw.sb([128, 8, NB], BF16, f"mixs{i}") for i in range(2)]
            featT = {nm: fw.sb([128, 8, NB], F32, nm) for nm in ("aT", "wT", "bTs", "kTs", "rTs")}
            amask = fw.sb([128, 8, NB, NB], F32, "amask")
            rmask = fw.sb([128, 8, NB, NB], F32, "rmask")
            Hs = [fw.sb([128, 8, 64], F32, f"Hs{i}") for i in range(2)]
            Tt = fw.sb([128, 8, 64], F32, "Tt")
            yoTs = fw.sb([128, 8, NB], BF16, "yoTs")
            s16 = fw.sb([NB, 64], F32, "s16s")
            v16s = lambda r: r.rearrange("p (h q) -> p h q", h=16)
            mc2 = [0]

            def mix_s(cidx):
                dst = mixs[mc2[0] % 2]
                mc2[0] += 1
                for kt in range(8):
                    fw.stt(dst[:, kt, :], xxs[:, kt, :], mu[:, kt, cidx:cidx + 1], xsT[:, kt, :], ALU.mult, ALU.add)
                return dst

            fw.dma("sp", xts[:], io.s2s[:, :])
            fw.dma("sp", xprev[:], io.shift_s_in[:, :, :])
            fw.act(jk[:], xts[:], AF.Square, accum_out=nst[0:M, 0:1])
            fw.ts("dve", nst[0:M, 1:2], nst[0:M, 0:1], 1.0 / D, EPS, ALU.mult, ALU.add)
            fw.act(nst[0:M, 2:3], nst[0:M, 1:2], AF.Ln)
            fw.act(nst[0:M, 3:4], nst[0:M, 2:3], AF.Exp, scale=-0.5)
            fw.stt(tA[:], xts[:], nst[0:M, 3:4], gm1[0:M, :], ALU.mult, ALU.mult)
            fw.dma("sp", io.shift_s[:, :], tA[:])
            fw.cp("dve", xn_s[:], tA[:])
            for kt in range(8):
                fw.tr(PT3[:, kt, 0:M], xn_s[:, kt * 128:(kt + 1) * 128], identb[0:M, 0:M])
            fw.cp("dve", xsT[:], PT3[:, :, 0:M])
            fw.tt("dve", xxs[:], xprev[:], xsT[:], ALU.subtract)
            PAm = [PA[0:M, 0:512], PA[0:M, 512:1024]]
            PBm = [PB0[0:M, :], PB1[0:M, :]]
            PAf = PA[0:M, :]
            proj_tok(mix_s(0), Wr, PAh, M)
            fw.cp("act", r_s[:], PAf)
            proj_tok(mix_s(2), Wk, PAh, M)
            fw.tt("dve", tA[:], PAf, kkb_[0:M, :], ALU.mult)
            fw.tt("dve", jk[:], tA[:], tA[:], ALU.mult)
            fw.red(s16[:, 0:16], v16s(jk[:, :]), ALU.add)
            rstd16(s16[:, 0:16], s16[:, 16:32], None, None, floor=1e-24)
            fw.tt("dve", v16s(kk_s[:, :]), v16s(tA[:, :]), s16[:, 16:32].bc(2, 64), ALU.mult)
            lora(mix_s(4), A1, 64, [A2], AF.Copy, PBh, M)
            for i in range(2):
                fw.tt("dve", tB[:, i * 512:(i + 1) * 512], PBm[i], a0b[0:M, i * 512:(i + 1) * 512], ALU.add)
            fw.act(tB[:], tB[:], AF.Sigmoid)
            fw.stt(jk[:], tB[:], 1.0, kab[0:M, :], ALU.subtract, ALU.mult)
            fw.ts("dve", jk[:], jk[:], 1.0, None, ALU.add)
            fw.tt("dve", kf_s[:], PAf, jk[:], ALU.mult)
            fw.tt("dve", b_s[:], kk_s[:], tB[:], ALU.mult)
            fw.tt("dve", jk[:], r_s[:], kf_s[:], ALU.mult)
            fw.tt("dve", jk[:], jk[:], rkb[0:M, :], ALU.mult)
            fw.red(s16[:, 32:48], v16s(jk[:, :]), ALU.add)
            proj_tok(mix_s(3), Wv, PAh, M)
            fw.cp("act", v_s[:], PAf)
            fw.tt("dve", v16s(bv_s[:, :]), v16s(PAf), s16[:, 32:48].bc(2, 64), ALU.mult)
            lora(mix_s(5), G1, 160, [G2a, G2b], AF.Sigmoid, PBh, M)
            for i in range(2):
                fw.cp("act", g_s[:, i * 512:(i + 1) * 512], PBm[i])
            lora(mix_s(1), W1, 64, [W2], AF.Tanh, PAh, M)
            fw.tt("dve", tA[:], PAf, w0b[0:M, :], ALU.add)
            fw.act(tA[:], tA[:], AF.Exp, scale=-1.0)
            fw.act(tA[:], tA[:], AF.Ln, bias=1.0)
            fw.ts("dve", tA[:], tA[:], -1.0, -0.5, ALU.mult, ALU.add)
            fw.act(Es[:], tA[:], AF.Exp)
            fw.act(Es[:], Es[:], AF.Exp, scale=-1.0)
            fw.ts("dve", tB[:], kk_s[:], -1.0, None, ALU.mult)
            for nm, src in (("aT", tB), ("wT", Es)):
                for kt in range(8):
                    fw.tr(PE[:, kt * 16:(kt + 1) * 16], src[:, kt * 128:(kt + 1) * 128], ident[0:M, 0:M])
                fw.cp("dve", featT[nm][:], PE[:, 0:128].rearrange("p (k b) -> p k b", k=8))
            for nm, src in (("bTs", b_s), ("kTs", kf_s), ("rTs", r_s)):
                for kt in range(8):
                    fw.tr(PT3[:, kt, 0:M], src[:, kt * 128:(kt + 1) * 128], identb[0:M, 0:M])
                fw.cp("dve", featT[nm][:], PT3[:, :, 0:M])
            fw.tt("dve", amask[:], featT["aT"][:, :, :].bc(2, NB), eye16[:, :, :].bc(1, 8), ALU.mult)
            fw.tt("dve", rmask[:], featT["rTs"][:, :, :].bc(2, NB), eye16[:, :, :].bc(1, 8), ALU.mult)
            SY = [PC, PD]
            for b in range(NB):
                H = Hs[b % 2]
                fw.dma("sp", H[:], io.wkv_s_in[b])
                for h in range(16):
                    j, e = h // 2, h % 2
                    p0 = 64 * e
                    fw.mm(SY[e][0:M, j * 64:(j + 1) * 64], amask[p0:p0 + 64, j, b, :], H[p0:p0 + 64, j, :],
                          start=(b == 0 and j == 0), stop=(b == NB - 1))
            je = lambda r: r.rearrange("p (j e v) -> p j e v", j=8, e=2)
            fw.cp("dve", je(sa_tok[:, :])[:, :, 0, :], PC[0:M, :].rearrange("p (j v) -> p j v", j=8))
            fw.cp("dve", je(sa_tok[:, :])[:, :, 1, :], PD[0:M, :].rearrange("p (j v) -> p j v", j=8))
            e4 = lambda r, e: r.rearrange("p (j e v) -> p j e v", j=8, e=2)[64 * e:64 * e + 64, :, e, :]
            for b in range(NB):
                H = Hs[b % 2]
                fw.dma("sp", H[:], io.wkv_s_in[b])
                fw.mm(PA[:, 0:512], sel16[:, b, :], sa_tok[:, 0:512])
                fw.mm(PA[:, 512:1024], sel16[:, b, :], sa_tok[:, 512:1024])
                fw.mm(PB0[:, :], sel16[:, b, :], v_s[:, 0:512])
                fw.mm(PB1[:, :], sel16[:, b, :], v_s[:, 512:1024])
                fw.tt("pool", H[:], H[:], featT["wT"][:, :, b].bc(2, 64), ALU.mult)
                for e in range(2):
                    p0 = 64 * e
                    fw.tt("dve", Tt[p0:p0 + 64, :, :], e4(PA[:, :], e), featT["bTs"][p0:p0 + 64, :, b].bc(2, 64), ALU.mult)
                fw.tt("dve", H[:], H[:], Tt[:], ALU.add)
                for e in range(2):
                    p0 = 64 * e
                    for half, PBx in enumerate((PB0, PB1)):
                        src = PBx[:, :].rearrange("p (j e v) -> p j e v", j=4, e=2)[p0:p0 + 64, :, e, :]
                        fw.tt("dve", Tt[p0:p0 + 64, 4 * half:4 * half + 4, :], src,
                              featT["kTs"][p0:p0 + 64, 4 * half:4 * half + 4, b].bc(2, 64), ALU.mult)
                fw.tt("dve", H[:], H[:], Tt[:], ALU.add)
                fw.dma("sp", io.wkv_s[b], H[:])
                for h in range(16):
                    j, e = h // 2, h % 2
                    p0 = 64 * e
                    fw.mm(SY[e][0:M, j * 64:(j + 1) * 64], rmask[p0:p0 + 64, j, b, :], H[p0:p0 + 64, j, :],
                          start=(b == 0 and j == 0), stop=(b == NB - 1))
            fw.cp("dve", je(tA[:, :])[:, :, 0, :], PC[0:M, :].rearrange("p (j v) -> p j v", j=8))
            fw.cp("dve", je(tA[:, :])[:, :, 1, :], PD[0:M, :].rearrange("p (j v) -> p j v", j=8))
            fw.red(s16[:, 0:16], v16s(tA[:, :]), ALU.add)
            fw.ts("dve", s16[:, 0:16], s16[:, 0:16], 1.0 / 64, None, ALU.mult)
            fw.tt("dve", v16s(tA[:, :]), v16s(tA[:, :]), s16[:, 0:16].bc(2, 64), ALU.subtract)
            fw.tt("dve", jk[:], tA[:], tA[:], ALU.mult)
            fw.red(s16[:, 16:32], v16s(jk[:, :]), ALU.add)
            rstd16(s16[:, 16:32], s16[:, 48:64], 1.0 / 64, 64e-5)
            fw.tt("dve", v16s(tA[:, :]), v16s(tA[:, :]), s16[:, 48:64].bc(2, 64), ALU.mult)
            fw.tt("dve", tA[:], tA[:], lnw[0:M, :], ALU.mult)
            fw.tt("dve", tA[:], tA[:], lnb[0:M, :], ALU.add)
            fw.tt("dve", tA[:], tA[:], bv_s[:], ALU.add)
            fw.tt("dve", yo_s[:], tA[:], g_s[:], ALU.mult)
            for kt in range(8):
                fw.tr(PT3[:, kt, 0:M], yo_s[:, kt * 128:(kt + 1) * 128], identb[0:M, 0:M])
            fw.cp("dve", yoTs[:], PT3[:, :, 0:M])
            for half in range(2):
                for kt in range(8):
                    fw.mm(PA[0:M, half * 512:(half + 1) * 512], yoTs[:, kt, :], Wo[:, kt, half * 512:(half + 1) * 512],
                          start=kt == 0, stop=kt == 7)
            fw.tt("dve", tB[:], xts[:], PA[0:M, :], ALU.add)
            fw.dma("sp", io.s3s[:, :], tB[:])
        fw.release(base_mark)

    if "3" in phases:
        ffn_phase(1, io.s3, io.y_p, io.s3s, io.y_s, True)

    fw.finish()
    fw.close()
    return nc


def prep_common(inp):
    f = lambda k: np.ascontiguousarray(np.asarray(inp[k], np.float32))
    m = {}
    m["cst"] = host_consts()
    m["w_in0"] = f("w_in0")[0]
    m["w_out0"] = f("w_out0")[0]
    m["norm_mix"] = f("norm_mix")
    m["norm_ffn"] = f("norm_ffn")
    m["norm_final"] = f("norm_final")
    m["ssd_norm"] = f("ssd_norm")[0]
    small0 = np.zeros(64, np.float32)
    small0[0:16] = f("ssd_dt_bias")[0]
    small0[16:32] = f("ssd_a_log")[0]
    small0[32:48] = f("ssd_d")[0]
    small0[48:52] = f("ml_i_bias")[0]
    small0[52:56] = f("ml_f_bias")[0]
    m["small0"] = small0
    cw = f("conv_w")[0].reshape(4, 20, 128).transpose(2, 1, 0)
    cb = f("conv_b")[0].reshape(20, 128).T[:, :, None]
    m["convp"] = np.ascontiguousarray(np.concatenate([cw, cb], axis=2))
    m["mlcol"] = np.ascontiguousarray(np.stack([f("ml_norm")[0].reshape(8, 128).T, f("ml_skip")[0].reshape(8, 128).T], axis=2))
    m["bdq"] = blockdiag(f("ml_wq")[0])
    m["bdk"] = blockdiag(f("ml_wk")[0])
    m["bdv"] = blockdiag(f("ml_wv")[0])
    for nm in ("rw_wr", "rw_wk", "rw_wv", "rw_wo", "rw_w1", "rw_w2", "rw_a1", "rw_a2", "rw_g1", "rw_g2"):
        m[nm] = f(nm)[0]
    m["rw_rows"] = np.ascontiguousarray(np.stack([f(k)[0] for k in ("rw_w0", "rw_a0", "rw_k_k", "rw_k_a", "rw_r_k", "rw_ln_w", "rw_ln_b")]))
    m["rw_mu"] = np.ascontiguousarray(f("rw_mu")[0].reshape(6, 8, 128).transpose(2, 1, 0))
    m["w_gu"] = f("ffn_w_gate_up")
    m["w_dn"] = f("ffn_w_down")
    return m


def prep_core(inp, core):
    f = lambda k: np.asarray(inp[k], np.float32)
    b0 = core * NB
    m = {}
    m["xp"] = np.ascontiguousarray(f("x_prompt")[core])
    m["xs"] = np.ascontiguousarray(f("x_sample")[b0:b0 + NB, 0, :])
    m["conv_s_in"] = np.ascontiguousarray(f("state_conv")[0, b0:b0 + NB].reshape(NB, 3, 20, 128).transpose(3, 2, 1, 0))
    m["ssm_s_in"] = np.ascontiguousarray(f("state_ssm")[0, b0:b0 + NB].reshape(NB, D, 128))
    m["mc_s_in"] = np.ascontiguousarray(f("state_mlstm_c")[0, b0:b0 + NB])
    m["mn_s_in"] = np.ascontiguousarray(f("state_mlstm_n")[0, b0:b0 + NB].reshape(NB, 4, 2, 128).transpose(3, 1, 2, 0).reshape(128, 8, NB))
    m["mm_s_in"] = np.ascontiguousarray(f("state_mlstm_m")[0, b0:b0 + NB])
    m["shift_s_in"] = np.ascontiguousarray(f("state_shift")[0, b0:b0 + NB].reshape(NB, 8, 128).transpose(2, 1, 0))
    m["wkv_s_in"] = np.ascontiguousarray(f("state_wkv")[0, b0:b0 + NB].reshape(NB, 8, 2, 64, 64).transpose(0, 2, 4, 1, 3).reshape(NB, 128, 8, 64))
    return m


def prep_consts(inp):
    f = lambda k: np.asarray(inp[k], np.float32)
    m = prep_common(inp)
    m["c16"] = host_consts16()
    m["eye16"] = np.ascontiguousarray(np.broadcast_to(np.eye(16, dtype=np.float32), (128, 16, 16)))
    dtcol = np.zeros((16, 4), np.float32)
    dtcol[:, 0] = f("ssd_dt_bias")[0]
    dtcol[:, 1] = f("ssd_a_log")[0]
    dtcol[:, 2] = f("ssd_d")[0]
    m["dtcol"] = dtcol
    return m


_NC_CACHE = {}


def kernel(**inp):
    if "nc" not in _NC_CACHE:
        _NC_CACHE["nc"] = build({})
    nc = _NC_CACHE["nc"]
    cm = prep_consts(inp)
    in_maps = [dict(cm, **prep_core(inp, c)) for c in range(NCORE)]
    res = run_bass_kernel_spmd(nc, in_maps, core_ids=list(range(NCORE)))
    R = res.results
    BT = NCORE * NB
    y_p = np.zeros((NCORE, T, D), np.float32)
    y_s = np.zeros((BT, 1, D), np.float32)
    conv_p = np.zeros((1, NCORE, 3, 2560), np.float32)
    conv_s = np.zeros((1, BT, 3, 2560), np.float32)
    ssm_p = np.zeros((1, NCORE, 16, 64, 128), np.float32)
    ssm_s = np.zeros((1, BT, 16, 64, 128), np.float32)
    mc_p = np.zeros((1, NCORE, 4, 256, 256), np.float32)
    mc_s = np.zeros((1, BT, 4, 256, 256), np.float32)
    mn_p = np.zeros((1, NCORE, 4, 256), np.float32)
    mn_s = np.zeros((1, BT, 4, 256), np.float32)
    mm_p = np.zeros((1, NCORE, 4), np.float32)
    mm_s = np.zeros((1, BT, 4), np.float32)
    sh_p = np.zeros((1, NCORE, D), np.float32)
    sh_s = np.zeros((1, BT, D), np.float32)
    wkv_p = np.zeros((1, NCORE, 16, 64, 64), np.float32)
    wkv_s = np.zeros((1, BT, 16, 64, 64), np.float32)
    for c in range(NCORE):
        r = R[c]
        sl = slice(c * NB, (c + 1) * NB)
        y_p[c] = r["y_p"]
        y_s[sl, 0] = r["y_s"]
        conv_p[0, c] = r["conv_p"].transpose(2, 1, 0).reshape(3, 2560)
        conv_s[0, sl] = r["conv_s"].transpose(3, 2, 1, 0).reshape(NB, 3, 2560)
        ssm_p[0, c] = r["ssm_p"].reshape(128, 16, 64).transpose(1, 2, 0)
        ssm_s[0, sl] = r["ssm_s"].reshape(NB, 16, 64, 128)
        mc = r["mc_p"]
        mc_p[0, c] = mc[:, :, :, :256].transpose(2, 1, 0, 3).reshape(4, 256, 256)
        mn_p[0, c] = mc[:, :, :, 256].transpose(2, 1, 0).reshape(4, 256)
        mc_s[0, sl] = r["mc_s"]
        mn_s[0, sl] = r["mn_s"].reshape(128, 4, 2, NB).transpose(3, 1, 2, 0).reshape(NB, 4, 256)
        mm_p[0, c] = r["mm_p"][0]
        mm_s[0, sl] = r["mm_s"]
        sh_p[0, c] = r["shift_p"][0]
        sh_s[0, sl] = r["shift_s"]
        wkv_p[0, c] = r["wkv_p"].reshape(2, 64, 8, 64).transpose(2, 0, 3, 1).reshape(16, 64, 64)
        wkv_s[0, sl] = r["wkv_s"].reshape(NB, 2, 64, 8, 64).transpose(0, 3, 1, 4, 2).reshape(NB, 16, 64, 64)
    return (y_p, y_s, conv_p, conv_s, ssm_p, ssm_s, mc_p, mc_s, mn_p, mn_s, mm_p, mm_s, sh_p, sh_s, wkv_p, wkv_s)
```

```python
import numpy as np
import concourse.bass as bass
import concourse.mybir as mybir
from concourse.bass_utils import run_bass_kernel_spmd

F32 = mybir.dt.float32
BF16 = mybir.dt.bfloat16
ALU = mybir.AluOpType
AF = mybir.ActivationFunctionType
AX = mybir.AxisListType

NCORE = 8
D = 1024
T = 2048
NB = 16
IN0 = 4632
DFF = 2816
EPS = 1e-5


class Tok:
    __slots__ = ("sem", "val", "eng")

    def __init__(self, sem, val, eng):
        self.sem, self.val, self.eng = sem, val, eng


class Ref:
    __slots__ = ("T", "ap")

    def __init__(self, T_, ap):
        self.T, self.ap = T_, ap

    def __getitem__(self, k):
        return Ref(self.T, self.ap[k])

    def rearrange(self, p, **kw):
        return Ref(self.T, self.ap.rearrange(p, **kw))

    def unsqueeze(self, a):
        return Ref(self.T, self.ap.unsqueeze(a))

    def to_broadcast(self, shp):
        return Ref(self.T, self.ap.to_broadcast(list(shp)))

    def bc(self, axis, n):
        ap = self.ap.unsqueeze(axis)
        shp = list(ap.shape)
        shp[axis] = n
        return Ref(self.T, ap.to_broadcast(shp))


class TT:
    __slots__ = ("t", "name", "lw", "rd", "psum")

    def __init__(self, t, name, psum=False):
        self.t, self.name, self.lw, self.rd, self.psum = t, name, None, [], psum

    def __getitem__(self, k):
        return Ref(self, self.t[k])


def _Ts(*xs):
    return [x.T for x in xs if isinstance(x, Ref)]


def _a(x):
    return x.ap if isinstance(x, Ref) else x


class Eng:
    def __init__(self, fw, name, h):
        self.fw, self.name, self.h = fw, name, h
        self.sems, self.n, self.waited = [], 0, {}


class Fw:
    EPOCH = 30000
    NDMA = 10

    def __init__(self, nc):
        self.nc = nc
        self._ctx = []
        self.E = {}
        for name, h in (("pe", nc.tensor), ("dve", nc.vector), ("act", nc.scalar),
                        ("pool", nc.gpsimd), ("sp", nc.sync)):
            self.E[name] = Eng(self, name, h)
        self.dma_sems, self.dma_i = {}, {}
        self.ntile = 0
        self.sb_bytes = 0

    def enter(self, cm):
        v = cm.__enter__()
        self._ctx.append(cm)
        return v

    def close(self):
        for cm in reversed(self._ctx):
            cm.__exit__(None, None, None)
        self._ctx = []

    def new_sem(self, name):
        return self.enter(self.nc.semaphore(name))

    def presem(self, queues=("sp", "pool", "act"), epochs=3):
        for e in self.E.values():
            while len(e.sems) < epochs:
                e.sems.append(self.new_sem(f"e_{e.name}_{len(e.sems)}"))
        for q in queues:
            self.dma_sems[q] = [[self.new_sem(f"d_{q}_{i}"), 0] for i in range(Fw.NDMA)]
            self.dma_i[q] = 0

    def mark(self):
        return len(self._ctx)

    def release(self, mark):
        self.barrier()
        while len(self._ctx) > mark:
            self._ctx.pop().__exit__(None, None, None)

    def barrier(self):
        for eng in self.E.values():
            for q, slots in self.dma_sems.items():
                for sem, cnt in slots:
                    if cnt > 0:
                        self._wait(eng, Tok(sem, cnt, "dma"))
            for name, e in self.E.items():
                if e is eng or e.n == 0:
                    continue
                ep = (e.n - 1) // Fw.EPOCH
                self._wait(eng, Tok(e.sems[ep], (e.n - 1) % Fw.EPOCH + 1, name))

    def sb(self, shape, dt=F32, name="t"):
        self.ntile += 1
        n = 1
        for s in shape[1:]:
            n *= s
        self.sb_bytes += n * (2 if dt == BF16 else 4)
        return TT(self.enter(self.nc.sbuf_tensor(f"{name}_{self.ntile}", list(shape), dt)), name)

    def ps(self, shape, dt=F32, name="p"):
        self.ntile += 1
        return TT(self.enter(self.nc.psum_tensor(f"{name}_{self.ntile}", list(shape), dt)), name, psum=True)

    def view(self, ref, name="v"):
        return TT(ref.ap, name, psum=ref.T.psum)

    def _wait(self, eng, tok):
        if tok is None:
            return
        key = id(tok.sem)
        if eng.waited.get(key, 0) >= tok.val:
            return
        eng.h.wait_ge(tok.sem, tok.val)
        eng.waited[key] = tok.val

    def _deps(self, eng, reads, writes):
        for t in reads:
            if t.lw is not None:
                self._wait(eng, t.lw)
            if t.psum:
                for r in t.rd:
                    if r.eng != eng.name:
                        self._wait(eng, r)
        strict = eng.name != "pe"
        for t in writes:
            if t.lw is not None and (strict or t.lw.eng != eng.name):
                self._wait(eng, t.lw)
            for r in t.rd:
                if strict or r.eng != eng.name:
                    self._wait(eng, r)

    def _mark(self, tok, reads, writes):
        for t in reads:
            t.rd.append(tok)
        for t in writes:
            t.lw = tok
            t.rd = []

    def op(self, e, fn, reads=(), writes=()):
        eng = self.E[e]
        self._deps(eng, reads, writes)
        ep = eng.n // Fw.EPOCH
        while len(eng.sems) <= ep:
            eng.sems.append(self.new_sem(f"e_{eng.name}_{len(eng.sems)}"))
        sem = eng.sems[ep]
        inst = fn(eng.h)
        val = eng.n % Fw.EPOCH + 1
        eng.n += 1
        inst.then_inc(sem, 1)
        tok = Tok(sem, val, eng.name)
        self._mark(tok, reads, writes)
        return tok

    def dma(self, q, out, in_, **kw):
        eng = self.E[q]
        if q not in self.dma_sems:
            self.dma_sems[q] = [[self.new_sem(f"d_{q}_{i}"), 0] for i in range(Fw.NDMA)]
            self.dma_i[q] = 0
        slot = self.dma_sems[q][self.dma_i[q] % Fw.NDMA]
        self.dma_i[q] += 1
        sem, cnt = slot
        if cnt > 0:
            self._wait(eng, Tok(sem, cnt, "dma"))
        reads, writes = _Ts(in_), _Ts(out)
        self._deps(eng, reads, writes)
        inst = eng.h.dma_start(out=_a(out), in_=_a(in_), **kw)
        slot[1] = cnt + 16
        inst.then_inc(sem, 16)
        tok = Tok(sem, cnt + 16, "dma")
        self._mark(tok, reads, writes)
        return tok

    def finish(self):
        eng = self.E["sp"]
        for q, slots in self.dma_sems.items():
            for sem, cnt in slots:
                if cnt > 0:
                    self._wait(eng, Tok(sem, cnt, "dma"))
        for name, e in self.E.items():
            if name == "sp" or e.n == 0:
                continue
            self._wait(eng, Tok(e.sems[(e.n - 1) // Fw.EPOCH], (e.n - 1) % Fw.EPOCH + 1, name))

    def mm(self, out, lhsT, rhs, start=True, stop=True):
        return self.op("pe", lambda e: e.matmul(_a(out), _a(lhsT), _a(rhs), start=start, stop=stop),
                       _Ts(lhsT, rhs), _Ts(out))

    def tr(self, out, in_, ident):
        return self.op("pe", lambda e: e.transpose(_a(out), _a(in_), _a(ident)), _Ts(in_, ident), _Ts(out))

    def act(self, out, in_, func, bias=None, scale=None, accum_out=None):
        kw = {}
        if bias is not None:
            kw["bias"] = _a(bias)
        if scale is not None:
            kw["scale"] = _a(scale)
        if accum_out is not None:
            kw["accum_out"] = _a(accum_out)
        return self.op("act", lambda e: e.activation(out=_a(out), in_=_a(in_), func=func, **kw),
                       _Ts(in_, bias, scale), _Ts(out, accum_out))

    def tt(self, e, out, in0, in1, op):
        return self.op(e, lambda h: h.tensor_tensor(out=_a(out), in0=_a(in0), in1=_a(in1), op=op),
                       _Ts(in0, in1), _Ts(out))

    def ts(self, e, out, in0, s1, s2, op0, op1=None, accum_out=None):
        kw = {}
        if op1 is not None:
            kw["op1"] = op1
        if accum_out is not None:
            kw["accum_out"] = _a(accum_out)
        return self.op(e, lambda h: h.tensor_scalar(out=_a(out), in0=_a(in0), scalar1=_a(s1), scalar2=_a(s2),
                                                    op0=op0, **kw),
                       _Ts(in0, s1, s2), _Ts(out, accum_out))

    def stt(self, out, in0, scalar, in1, op0, op1, accum_out=None):
        kw = {}
        if accum_out is not None:
            kw["accum_out"] = _a(accum_out)
        return self.op("dve", lambda h: h.scalar_tensor_tensor(out=_a(out), in0=_a(in0), scalar=_a(scalar),
                                                               in1=_a(in1), op0=op0, op1=op1, **kw),
                       _Ts(in0, scalar, in1), _Ts(out, accum_out))

    def cp(self, e, out, in_):
        if e == "act":
            return self.act(out, in_, AF.Copy)
        return self.op(e, lambda h: h.tensor_copy(out=_a(out), in_=_a(in_)), _Ts(in_), _Ts(out))

    def red(self, out, in_, op, axis=AX.X):
        return self.op("dve", lambda h: h.tensor_reduce(out=_a(out), in_=_a(in_), axis=axis, op=op),
                       _Ts(in_), _Ts(out))

    def recip(self, out, in_):
        return self.op("dve", lambda h: h.reciprocal(out=_a(out), in_=_a(in_)), _Ts(in_), _Ts(out))

    def memset(self, e, out, val):
        return self.op(e, lambda h: h.memset(_a(out), val), [], _Ts(out))


def host_consts():
    j = np.arange(128)
    c = np.zeros((128, 10, 128), np.float32)
    c[:, 0, :] = (j[:, None] == j[None, :])
    c[:, 1, :] = (j[:, None] <= j[None, :])
    c[:, 2, :] = (j[:, None] > j[None, :])
    c[:, 3, :] = np.where(j[None, :] <= j[:, None], 0.0, -30000.0)
    c[:, 4, :] = 1.0
    c[:, 5, :] = (j[:, None] == 127)
    c[:, 6, :] = (j[:, None] < j[None, :])
    c[:, 7, :] = c[:, 1, :]
    c[:, 8, :] = c[:, 6, :]
    c[:, 9, :] = c[:, 1, :]
    return c


def blockdiag(w):
    out = np.zeros((8, 128, 128), np.float32)
    w = w.reshape(8, 32, 4, 4)
    for nl in range(32):
        out[:, nl * 4:(nl + 1) * 4, nl * 4:(nl + 1) * 4] = w[:, nl]
    return np.ascontiguousarray(out.transpose(1, 0, 2))


def host_consts16():
    h = np.arange(16)
    q = np.arange(128)
    j = np.arange(8)
    e = (h[:, None, None] == (2 * j[None, :, None] + q[None, None, :] // 64)).astype(np.float32)
    sel = np.broadcast_to((h[:, None, None] == h[None, :, None]), (16, 16, 128)).astype(np.float32)
    return np.ascontiguousarray(np.concatenate([e.reshape(16, -1), sel.reshape(16, -1)], axis=1))


class IO:
    pass


def build(cfg):
    nc = bass.Bass("TRN2", target_bir_lowering=False)
    fw = Fw(nc)
    io = IO()
    NCH = cfg.get("nch", 16)
    dbg = cfg.get("dbg", ())
    phases = cfg.get("phases", ("0a", "0b", "1", "2", "3"))

    def din(name, shape):
        return nc.dram_tensor(name, list(shape), F32, kind="ExternalInput").ap()

    def dout(name, shape):
        return nc.dram_tensor(name, list(shape), F32, kind="ExternalOutput").ap()

    def dscr(name, shape):
        if name in dbg:
            return dout(name, shape)
        return nc.dram_tensor(name, list(shape), F32).ap()

    io.xp = din("xp", [T, D])
    io.cst = din("cst", [128, 10, 128])
    io.w_in0 = din("w_in0", [D, IN0])
    io.w_out0 = din("w_out0", [2 * D, D])
    io.norm_mix = din("norm_mix", [2, D])
    io.norm_ffn = din("norm_ffn", [2, D])
    io.norm_final = din("norm_final", [D])
    io.ssd_norm = din("ssd_norm", [D])
    io.small0 = din("small0", [64])
    io.convp = din("convp", [128, 20, 5])
    io.mlcol = din("mlcol", [128, 8, 2])
    io.bdq = din("bdq", [128, 8, 128])
    io.bdk = din("bdk", [128, 8, 128])
    io.bdv = din("bdv", [128, 8, 128])
    io.w_gu = din("w_gu", [2, D, 2 * DFF])
    io.w_dn = din("w_dn", [2, DFF, D])
    for nm in ("rw_wr", "rw_wk", "rw_wv", "rw_wo"):
        setattr(io, nm, din(nm, [D, D]))
    io.rw_w1 = din("rw_w1", [D, 64]); io.rw_w2 = din("rw_w2", [64, D])
    io.rw_a1 = din("rw_a1", [D, 64]); io.rw_a2 = din("rw_a2", [64, D])
    io.rw_g1 = din("rw_g1", [D, 160]); io.rw_g2 = din("rw_g2", [160, D])
    io.rw_rows = din("rw_rows", [7, D])
    io.rw_mu = din("rw_mu", [128, 8, 6])
    io.wkv_p = dout("wkv_p", [128, 8, 64])
    io.shift_p = dout("shift_p", [1, D])
    io.xs = din("xs", [NB, D])
    io.c16 = din("c16", [16, 8 * 128 + 16 * 128])
    io.eye16 = din("eye16", [128, 16, 16])
    io.dtcol = din("dtcol", [16, 4])
    io.conv_s_in = din("conv_s_in", [128, 20, 3, NB])
    io.ssm_s_in = din("ssm_s_in", [NB, D, 128])
    io.mc_s_in = din("mc_s_in", [NB, 4, 256, 256])
    io.mn_s_in = din("mn_s_in", [128, 8, NB])
    io.mm_s_in = din("mm_s_in", [NB, 4])
    io.shift_s_in = din("shift_s_in", [128, 8, NB])
    io.wkv_s_in = din("wkv_s_in", [NB, 128, 8, 64])
    io.y_s = dout("y_s", [NB, D])
    io.conv_s = dout("conv_s", [128, 20, 3, NB])
    io.ssm_s = dout("ssm_s", [NB, D, 128])
    io.mc_s = dout("mc_s", [NB, 4, 256, 256])
    io.mn_s = dout("mn_s", [128, 8, NB])
    io.mm_s = dout("mm_s", [NB, 4])
    io.shift_s = dout("shift_s", [NB, D])
    io.wkv_s = dout("wkv_s", [NB, 128, 8, 64])
    io.s1s = dscr("s1s", [NB, D])
    io.s2s = dscr("s2s", [NB, D])
    io.s3s = dscr("s3s", [NB, D])
    if "dbg_a" in dbg:
        io.dbg_a = dout("dbg_a", [NB, D]); io.dbg_b = dout("dbg_b", [NB, 64])
    io.s1 = dscr("s1", [T, D])
    io.s2 = dscr("s2", [T, D])
    io.s3 = dscr("s3", [T, D])
    io.y_p = dout("y_p", [T, D])
    io.ssm_p = dout("ssm_p", [128, D])
    io.mc_p = dout("mc_p", [128, 2, 4, 264])
    io.mm_p = dout("mm_p", [1, 4])
    io.conv_p = dout("conv_p", [128, 20, 3])

    fw.presem(epochs=5)

    cst = fw.sb([128, 10, 128], F32, "cst")
    fw.dma("sp", cst[:], io.cst[:, :, :])
    ident, tri_le, mask_gt, negmask, ones = (cst[:, i, :] for i in range(5))
    sel127 = cst[:, 5, :]
    m4 = cst[:, 6:10, :].rearrange("p a t -> p (a t)")
    identb = fw.sb([128, 128], BF16, "identb")
    fw.cp("dve", identb[:], ident)
    onesb = fw.sb([128, 128], BF16, "onesb")
    fw.cp("dve", onesb[:], ones)
    nst = fw.sb([128, 8], F32, "nst")
    c16 = fw.sb([16, 8 * 128 + 16 * 128], F32, "c16")
    fw.dma("sp", c16[:], io.c16[:, :])
    exp16 = c16[:, 0:1024].rearrange("p (j q) -> p j q", j=8)
    sel16 = c16[:, 1024:3072].rearrange("p (b q) -> p b q", b=16)
    eye16 = fw.sb([128, 16, 16], F32, "eye16")
    fw.dma("sp", eye16[:], io.eye16[:, :, :])
    SAMPLE = cfg.get("sample", True)

    PA = fw.ps([128, 1024], F32, "PA")
    PB = fw.ps([128, 1024], F32, "PB")
    PC = fw.ps([128, 512], F32, "PC")
    PD = fw.ps([128, 512], F32, "PD")
    PE = fw.ps([128, 512], F32, "PE")
    PT = fw.ps([128, 1024], BF16, "PT")
    pcd = [PC, PD]
    PB0f = fw.view(PB[:, 0:512], "PB0f")
    PB1f = fw.view(PB[:, 512:1024], "PB1f")
    PT3 = PT[:, :].rearrange("p (k m) -> p k m", k=8)
    v16 = lambda r: r.rearrange("p (h q) -> p h q", h=16)

    def load_w(dst, src, kt0, kt1, q="pool", step=2):
        for k in range(kt0, kt1, step):
            k1 = min(k + step, kt1)
            fw.dma(q, dst[:, k:k1, :], src[k * 128:k1 * 128, :].rearrange("(k p) n -> p k n", p=128))

    def rmsnorm(x, g, out, M, junk):
        fw.act(junk[0:M, :], x, AF.Square, accum_out=nst[0:M, 0:1])
        fw.ts("dve", nst[0:M, 1:2], nst[0:M, 0:1], 1.0 / D, EPS, ALU.mult, ALU.add)
        fw.act(nst[0:M, 2:3], nst[0:M, 1:2], AF.Ln)
        fw.act(nst[0:M, 3:4], nst[0:M, 2:3], AF.Exp, scale=-0.5)
        fw.stt(out, x, nst[0:M, 3:4], g[0:M, :], ALU.mult, ALU.mult)

    def to_feat(src, dst, M):
        for kt in range(8):
            fw.tr(PT3[:, kt, 0:M], src[0:M, kt * 128:(kt + 1) * 128], identb[0:M, 0:M])
        fw.cp("dve", dst[:, :, 0:M], PT3[:, :, 0:M])

    def grp_rstd(src, ncol, dst, junk, M=128):
        fw.act(junk[0:M, 0:ncol], src, AF.Square, accum_out=nst[0:M, 4:5])
        fw.ts("dve", nst[0:M, 5:6], nst[0:M, 4:5], 1.0 / ncol, EPS, ALU.mult, ALU.add)
        fw.act(nst[0:M, 6:7], nst[0:M, 5:6], AF.Ln)
        fw.act(dst, nst[0:M, 6:7], AF.Exp, scale=-0.5)

    def proj_feat(W, col0, ntile, xT, M, evac):
        for gi, g0 in enumerate(range(0, ntile, 4)):
            n = min(4, ntile - g0)
            ps3 = pcd[gi % 2][:, :].rearrange("p (a m) -> p a m", a=4)
            for i in range(n):
                col = col0 + (g0 + i) * 128
                for kt in range(8):
                    fw.mm(ps3[:, i, 0:M], W[:, kt, col:col + 128], xT[:, kt, 0:M], start=kt == 0, stop=kt == 7)
            evac(g0, n, ps3[:, 0:n, 0:M])

    def conv_tiles(convin, convp, acc, ct0, n):
        for i in range(n):
            ct = ct0 + i
            fw.act(acc[:, i, :], convin[:, i, 0:128], AF.Identity, scale=convp[:, ct, 0:1], bias=convp[:, ct, 4:5])
            for j in range(1, 4):
                fw.stt(acc[:, i, :], convin[:, i, j:j + 128], convp[:, ct, j:j + 1], acc[:, i, :], ALU.mult, ALU.add)

    base_mark = fw.mark()

    if "0a" in phases:
        Wc = fw.sb([128, 8, 1536], BF16, "Wc")
        load_w(Wc, io.w_in0[:, 1024:2560], 0, 8)
        Wz = fw.sb([128, 8, 1024], BF16, "Wz")
        load_w(Wz, io.w_in0[:, 0:1024], 0, 8)
        Wdt = fw.sb([128, 8, 16], BF16, "Wdt")
        load_w(Wdt, io.w_in0[:, 3584:3600], 0, 8, step=8)
        Wo = fw.sb([128, 8, D], BF16, "Wo")
        load_w(Wo, io.w_out0[0:1024, :], 0, 8)
        gmix = fw.sb([128, D], F32, "gmix")
        fw.dma("sp", gmix[:], io.norm_mix[0, :].partition_broadcast(128))
        gssd = fw.sb([128, D], F32, "gssd")
        fw.dma("sp", gssd[:], io.ssd_norm.partition_broadcast(128))
        sm0 = fw.sb([128, 64], F32, "sm0")
        fw.dma("sp", sm0[:], io.small0.partition_broadcast(128))
        dtb_bc, D_bc = sm0[:, 0:16], sm0[:, 32:48]
        A_t = fw.sb([128, 16], F32, "A_t")
        fw.act(A_t[:], sm0[:, 16:32], AF.Exp)
        fw.ts("dve", A_t[:], A_t[:], -1.0, None, ALU.mult)
        convp = fw.sb([128, 20, 5], F32, "convp")
        fw.dma("sp", convp[:], io.convp[:, :, :])
        convin = fw.sb([128, 12, 131], F32, "convin")
        fw.memset("pool", convin[:], 0.0)
        ST = fw.sb([128, D], F32, "ST")
        fw.memset("pool", ST[:], 0.0)
        STb = fw.sb([128, D], BF16, "STb")
        fw.memset("pool", STb[:], 0.0)
        xt = fw.sb([128, D], F32, "xt")
        junk = fw.sb([128, D], F32, "junk")
        xn = fw.sb([128, D], BF16, "xn")
        xnT = fw.sb([128, 8, 128], BF16, "xnT")
        acc = fw.sb([128, 12, 128], F32, "acc")
        cact = fw.sb([128, 12, 128], BF16, "cact")
        zs = fw.sb([128, D], F32, "zs")
        xtok = fw.sb([128, D], BF16, "xtok")
        Btok = fw.sb([128, 256], BF16, "Btok")
        sm = fw.sb([128, 128], F32, "sm")
        Lh = [fw.sb([128, 4, 128], F32, f"Lh{i}") for i in range(2)]
        Eh = fw.sb([128, 4, 128], F32, "Eh")
        CBm = fw.sb([128, 2, 128], F32, "CBm")
        Wt = fw.sb([128, 16, 128], BF16, "Wt")
        t1 = fw.sb([128, D], F32, "t1")
        yn = fw.sb([128, D], BF16, "yn")
        ynT = fw.sb([128, 8, 128], BF16, "ynT")
        xw = fw.sb([128, D], BF16, "xw")
        x1 = fw.sb([128, D], F32, "x1")

        for c in range(NCH):
            fw.dma("sp", xt[:], io.xp[c * 128:(c + 1) * 128, :])
            rmsnorm(xt[:], gmix, xn[:], 128, junk)
            to_feat(xn, xnT, 128)
            proj_feat(Wc, 0, 12, xnT, 128, lambda g0, n, ps: fw.cp("act", convin[:, g0:g0 + n, 3:131], ps))
            for half in range(2):
                for kt in range(8):
                    fw.mm(PA[:, half * 512:(half + 1) * 512], xnT[:, kt, :], Wz[:, kt, half * 512:(half + 1) * 512],
                          start=kt == 0, stop=kt == 7)
            fw.act(zs[:], PA[:, :], AF.Silu)
            for kt in range(8):
                fw.mm(PE[:, 0:16], xnT[:, kt, :], Wdt[:, kt, :], start=kt == 0, stop=kt == 7)
            fw.tt("dve", sm[:, 0:16], PE[:, 0:16], dtb_bc, ALU.add)
            conv_tiles(convin, convp, acc, 0, 12)
            fw.act(cact[:], acc[:], AF.Silu)
            fw.cp("pool", convin[:, :, 0:3], convin[:, :, 128:131])
            for kt in range(8):
                fw.tr(PT3[:, kt, :], cact[:, kt, :], identb[:, :])
            fw.cp("dve", xtok[:], PT[:, :])
            for g in range(2):
                fw.tr(PT[:, g * 128:(g + 1) * 128], cact[:, 8 + g, :], identb[:, :])
            fw.cp("dve", Btok[:], PT[:, 0:256])
            fw.act(sm[:, 0:16], sm[:, 0:16], AF.Exp)
            fw.act(sm[:, 0:16], sm[:, 0:16], AF.Ln, bias=1.0)
            fw.tt("dve", sm[:, 16:32], sm[:, 0:16], A_t[:], ALU.mult)
            fw.mm(PE[:, 32:48], tri_le, sm[:, 16:32])
            fw.mm(PE[:, 48:64], ones, sm[:, 16:32])
            fw.act(sm[:, 32:48], PE[:, 32:48], AF.Exp)
            fw.cp("dve", sm[:, 64:80], PE[:, 32:48])
            fw.tt("dve", sm[:, 48:64], PE[:, 48:64], sm[:, 64:80], ALU.subtract)
            fw.act(sm[:, 48:64], sm[:, 48:64], AF.Exp)
            fw.tt("dve", sm[:, 48:64], sm[:, 48:64], sm[:, 0:16], ALU.mult)
            fw.act(sm[:, 80:96], PE[:, 48:64], AF.Exp)
            for g in range(2):
                fw.mm(PE[:, 128 + g * 128:256 + g * 128], cact[:, 8 + g, :], cact[:, 10 + g, :])
                fw.tt("dve", CBm[:, g, :], PE[:, 128 + g * 128:256 + g * 128], tri_le, ALU.mult)
            for hq in range(4):
                L = Lh[hq % 2]
                ps3 = pcd[hq % 2][:, :].rearrange("p (a m) -> p a m", a=4)
                for i in range(4):
                    h = hq * 4 + i
                    fw.ts("dve", L[:, i, :], mask_gt, sm[:, 16 + h:17 + h], None, ALU.mult)
                    fw.mm(ps3[:, i, :], L[:, i, :], tri_le)
                fw.act(Eh[:], ps3, AF.Exp)
                for i in range(4):
                    h = hq * 4 + i
                    fw.stt(Wt[:, h, :], Eh[:, i, :], sm[:, h:h + 1], CBm[:, h // 8, :], ALU.mult, ALU.mult)
            for h in range(16):
                fw.mm(PA[:, h * 64:(h + 1) * 64], Wt[:, h, :], xtok[:, h * 64:(h + 1) * 64])
            for g in range(2):
                fw.mm(PB[:, g * 512:(g + 1) * 512], cact[:, 10 + g, :], STb[:, g * 512:(g + 1) * 512])
            fw.tt("dve", v16(t1[:, :]), v16(PB[:, :]), sm[:, 32:48].bc(2, 64), ALU.mult)
            fw.tt("dve", t1[:], t1[:], PA[:, :], ALU.add)
            fw.tt("pool", v16(junk[:, :]), v16(xtok[:, :]), D_bc.bc(2, 64), ALU.mult)
            fw.tt("dve", t1[:], t1[:], junk[:], ALU.add)
            fw.tt("dve", t1[:], t1[:], zs[:], ALU.mult)
            for g in range(2):
                grp_rstd(t1[:, g * 512:(g + 1) * 512], 512, nst[:, 7:8], junk)
                fw.stt(yn[:, g * 512:(g + 1) * 512], t1[:, g * 512:(g + 1) * 512], nst[:, 7:8],
                       gssd[:, g * 512:(g + 1) * 512], ALU.mult, ALU.mult)
            to_feat(yn, ynT, 128)
            fw.tt("pool", v16(xw[:, :]), v16(xtok[:, :]), sm[:, 48:64].bc(2, 64), ALU.mult)
            for g in range(2):
                fw.mm(PB[:, g * 512:(g + 1) * 512], Btok[:, g * 128:(g + 1) * 128], xw[:, g * 512:(g + 1) * 512])
            fw.tt("dve", v16(ST[:, :]), v16(ST[:, :]), sm[:, 80:96].bc(2, 64), ALU.mult)
            fw.tt("dve", ST[:], ST[:], PB[:, :], ALU.add)
            fw.cp("pool", STb[:], ST[:])
            for half in range(2):
                for kt in range(8):
                    fw.mm(PA[:, half * 512:(half + 1) * 512], ynT[:, kt, :], Wo[:, kt, half * 512:(half + 1) * 512],
                          start=kt == 0, stop=kt == 7)
            fw.tt("dve", x1[:], xt[:], PA[:, :], ALU.add)
            fw.dma("sp", io.s1[c * 128:(c + 1) * 128, :], x1[:])

        if SAMPLE:
            dtcol = fw.sb([16, 4], F32, "dtcol")
            fw.dma("sp", dtcol[:], io.dtcol[:, :])
            fw.act(dtcol[:, 3:4], dtcol[:, 1:2], AF.Exp)
            fw.ts("dve", dtcol[:, 3:4], dtcol[:, 3:4], -1.0, None, ALU.mult)
            cst_s = fw.sb([128, 12, 3, NB], F32, "cst_s")
            fw.dma("sp", cst_s[:], io.conv_s_in[:, 0:12, :, :])
            uS = fw.sb([128, 12, NB], F32, "uS")
            accs = fw.sb([128, 12, NB], F32, "accs")
            tmps = fw.sb([128, 12, NB], F32, "tmps")
            cs = fw.sb([128, 12, NB], F32, "cs")
            zsT = fw.sb([128, 8, NB], F32, "zsT")
            dd = fw.sb([16, 48], F32, "dd")
            dx = fw.sb([128, 8, 48], F32, "dx")
            dtx = fw.sb([128, 8, NB], F32, "dtx")
            BCtok = fw.sb([16, 512], F32, "BCtok")
            Sb = [fw.sb([128, 8, 128], F32, f"Sb{i}") for i in range(2)]
            T1s = fw.sb([128, 8, 128], F32, "T1s")
            ysT = fw.sb([128, 8, NB], F32, "ysT")
            fw.dma("sp", xt[0:NB, :], io.xs[:, :])
            rmsnorm(xt[0:NB, :], gmix, xn[0:NB, :], NB, junk)
            to_feat(xn, xnT, NB)
            proj_feat(Wc, 0, 12, xnT, NB, lambda g0, n, ps: fw.cp("act", uS[:, g0:g0 + n, :], ps))
            proj_feat(Wz, 0, 8, xnT, NB, lambda g0, n, ps: fw.act(zsT[:, g0:g0 + n, :], ps, AF.Silu))
            wv = lambda j: convp[:, 0:12, j].bc(2, NB)
            fw.tt("dve", accs[:], cst_s[:, :, 0, :], wv(0), ALU.mult)
            fw.tt("dve", accs[:], accs[:], wv(4), ALU.add)
            for j in (1, 2):
                fw.tt("dve", tmps[:], cst_s[:, :, j, :], wv(j), ALU.mult)
                fw.tt("dve", accs[:], accs[:], tmps[:], ALU.add)
            fw.tt("dve", tmps[:], uS[:], wv(3), ALU.mult)
            fw.tt("dve", accs[:], accs[:], tmps[:], ALU.add)
            fw.act(cs[:], accs[:], AF.Silu)
            fw.dma("sp", io.conv_s[:, 0:12, 0:2, :], cst_s[:, :, 1:3, :])
            fw.dma("sp", io.conv_s[:, 0:12, 2, :], uS[:])
            for kt in range(8):
                fw.mm(PE[0:16, 0:16], Wdt[:, kt, :], xnT[:, kt, 0:NB], start=kt == 0, stop=kt == 7)
            fw.ts("dve", dd[:, 0:16], PE[0:16, 0:16], dtcol[:, 0:1], None, ALU.add)
            fw.act(dd[:, 0:16], dd[:, 0:16], AF.Exp)
            fw.act(dd[:, 0:16], dd[:, 0:16], AF.Ln, bias=1.0)
            fw.ts("dve", dd[:, 16:32], dd[:, 0:16], dtcol[:, 3:4], None, ALU.mult)
            fw.act(dd[:, 16:32], dd[:, 16:32], AF.Exp)
            fw.ts("dve", dd[:, 32:48], ones[0:16, 0:16], dtcol[:, 2:3], None, ALU.mult)
            for j in range(8):
                fw.mm(PE[:, 128 + j * 48:128 + (j + 1) * 48], exp16[:, j, :], dd[:, :])
            fw.cp("dve", dx[:], PE[:, 128:512].rearrange("p (j c) -> p j c", j=8))
            fw.tt("dve", dtx[:], dx[:, :, 0:16], cs[:, 0:8, :], ALU.mult)
            for i in range(4):
                fw.tr(PD[0:16, i * 128:(i + 1) * 128], cs[:, 8 + i, :], ident)
            fw.cp("dve", BCtok[:], PD[0:16, :])
            for b in range(NB):
                S = Sb[b % 2]
                fw.dma("sp", S[:], io.ssm_s_in[b].rearrange("(j q) n -> q j n", q=128))
                fw.mm(PC[:, :], sel16[:, b, :], BCtok[:, :])
                for g in range(2):
                    fw.tt("dve", T1s[:, 4 * g:4 * g + 4, :], PC[:, g * 128:(g + 1) * 128].bc(1, 4),
                          dtx[:, 4 * g:4 * g + 4, b].bc(2, 128), ALU.mult)
                fw.tt("pool", S[:], S[:], dx[:, :, 16 + b].bc(2, 128), ALU.mult)
                fw.tt("dve", S[:], S[:], T1s[:], ALU.add)
                fw.dma("sp", io.ssm_s[b].rearrange("(j q) n -> q j n", q=128), S[:])
                for g in range(2):
                    fw.tt("dve", T1s[:, 4 * g:4 * g + 4, :], S[:, 4 * g:4 * g + 4, :],
                          PC[:, 256 + g * 128:256 + (g + 1) * 128].bc(1, 4), ALU.mult)
                fw.red(ysT[:, :, b], T1s[:], ALU.add)
            fw.tt("dve", dtx[:], dx[:, :, 32:48], cs[:, 0:8, :], ALU.mult)
            fw.tt("dve", ysT[:], ysT[:], dtx[:], ALU.add)
            fw.tt("dve", ysT[:], ysT[:], zsT[:], ALU.mult)
            for j in range(8):
                fw.tr(PA[0:16, j * 128:(j + 1) * 128], ysT[:, j, :], ident)
            fw.cp("dve", t1[0:NB, :], PA[0:NB, :])
            for g in range(2):
                grp_rstd(t1[0:NB, g * 512:(g + 1) * 512], 512, nst[0:NB, 7:8], junk, NB)
                fw.stt(yn[0:NB, g * 512:(g + 1) * 512], t1[0:NB, g * 512:(g + 1) * 512], nst[0:NB, 7:8],
                       gssd[0:NB, g * 512:(g + 1) * 512], ALU.mult, ALU.mult)
            to_feat(yn, ynT, NB)
            for half in range(2):
                for kt in range(8):
                    fw.mm(PA[0:NB, half * 512:(half + 1) * 512], ynT[:, kt, 0:NB], Wo[:, kt, half * 512:(half + 1) * 512],
                          start=kt == 0, stop=kt == 7)
            fw.tt("dve", x1[0:NB, :], xt[0:NB, :], PA[0:NB, :], ALU.add)
            fw.dma("sp", io.s1s[:, :], x1[0:NB, :])
        fw.dma("sp", io.ssm_p[:, :], ST[:])
        fw.dma("sp", io.conv_p[:, 0:12, :], convin[:, :, 0:3])
        fw.release(base_mark)

    if "0b" in phases:
        Wx = fw.sb([128, 8, 1024], BF16, "Wx")
        load_w(Wx, io.w_in0[:, 2560:3584], 0, 8)
        Wg = fw.sb([128, 8, 1024], BF16, "Wg")
        load_w(Wg, io.w_in0[:, 3600:4624], 0, 8)
        Wif = fw.sb([128, 8, 16], BF16, "Wif")
        load_w(Wif, io.w_in0[:, 4616:4632], 0, 8, step=8)
        Wo = fw.sb([128, 8, D], BF16, "Wo")
        load_w(Wo, io.w_out0[1024:2048, :], 0, 8)
        BDq = fw.sb([128, 8, 128], BF16, "BDq")
        BDk = fw.sb([128, 8, 128], BF16, "BDk")
        BDv = fw.sb([128, 8, 128], BF16, "BDv")
        fw.dma("pool", BDq[:], io.bdq[:, :, :])
        fw.dma("pool", BDk[:], io.bdk[:, :, :])
        fw.dma("pool", BDv[:], io.bdv[:, :, :])
        gmix = fw.sb([128, D], F32, "gmix")
        fw.dma("sp", gmix[:], io.norm_mix[0, :].partition_broadcast(128))
        sm0 = fw.sb([128, 64], F32, "sm0")
        fw.dma("sp", sm0[:], io.small0.partition_broadcast(128))
        ib_bc, fb_bc = sm0[:, 48:52], sm0[:, 52:56]
        convp = fw.sb([128, 20, 5], F32, "convp")
        fw.dma("sp", convp[:], io.convp[:, :, :])
        mlcol = fw.sb([128, 8, 2], F32, "mlcol")
        fw.dma("sp", mlcol[:], io.mlcol[:, :, :])
        convin = fw.sb([128, 8, 131], F32, "convin")
        fw.memset("pool", convin[:], 0.0)
        Cst = fw.sb([128, 2, 4, 264], F32, "Cst")
        fw.memset("pool", Cst[:], 0.0)
        Cb = fw.sb([128, 2, 4, 264], BF16, "Cb")
        fw.memset("pool", Cb[:], 0.0)
        mprev = fw.sb([128, 4], F32, "mprev")
        fw.memset("pool", mprev[:], 0.0)
        xt = fw.sb([128, D], F32, "xt")
        junk = fw.sb([128, D], F32, "junk")
        xn = fw.sb([128, D], BF16, "xn")
        xnT = fw.sb([128, 8, 128], BF16, "xnT")
        acc = fw.sb([128, 8, 128], F32, "acc")
        cact = fw.sb([128, 8, 128], BF16, "cact")
        xmraw = fw.sb([128, 8, 128], BF16, "xmraw")
        sigoT = fw.sb([128, 8, 128], BF16, "sigoT")
        sm2 = fw.sb([128, 64], F32, "sm2")
        qT = fw.sb([128, 8, 128], BF16, "qT")
        kT = fw.sb([128, 8, 128], BF16, "kT")
        vtok = fw.sb([128, 4, 264], BF16, "vtok")
        fw.memset("pool", vtok[:], 1.0)
        kw_ = fw.sb([128, 4, 256], BF16, "kw")
        Rh = fw.sb([128, 128], F32, "Rh")
        dlm = fw.sb([128, 128], F32, "dlm")
        Dm = fw.sb([128, 128], F32, "Dm")
        Sg = fw.sb([128, 128], BF16, "Sg")
        SgT = fw.sb([128, 128], BF16, "SgT")
        hs = fw.sb([128, 16], F32, "hs")
        mt = fw.sb([128, 16], F32, "mt")
        fw.memset("pool", mt[:], 0.0)
        fw.memset("pool", sm2[:], 0.0)
        comb = fw.sb([128, 258], F32, "comb")
        hh = fw.sb([128, 256], F32, "hh")
        hmn = fw.sb([128, D], BF16, "hmn")
        hmnT = fw.sb([128, 8, 128], BF16, "hmnT")
        hmfT = fw.sb([128, 8, 128], BF16, "hmfT")
        x1 = fw.sb([128, D], F32, "x1")

        lvl = cfg.get('lvl', 99)
        for c in range(NCH):
            fw.dma("sp", xt[:], io.xp[c * 128:(c + 1) * 128, :])
            fw.dma("sp", x1[:], io.s1[c * 128:(c + 1) * 128, :])
            rmsnorm(xt[:], gmix, xn[:], 128, junk)
            to_feat(xn, xnT, 128)
            proj_feat(Wx, 0, 8, xnT, 128, lambda g0, n, ps: fw.cp("act", convin[:, g0:g0 + n, 3:131], ps))
            proj_feat(Wg, 0, 8, xnT, 128, lambda g0, n, ps: fw.act(sigoT[:, g0:g0 + n, :], ps, AF.Sigmoid))
            for kt in range(8):
                fw.mm(PE[:, 16:32], xnT[:, kt, :], Wif[:, kt, :], start=kt == 0, stop=kt == 7)
            fw.tt("dve", sm2[:, 0:4], PE[:, 24:28], ib_bc, ALU.add)
            fw.tt("dve", sm2[:, 4:8], PE[:, 28:32], fb_bc, ALU.add)
            conv_tiles(convin, convp, acc, 12, 8)
            fw.act(cact[:], acc[:], AF.Silu)
            fw.cp("pool", xmraw[:], convin[:, :, 3:131])
            fw.cp("pool", convin[:, :, 0:3], convin[:, :, 128:131])
            if lvl < 2:
                continue
            for tile in range(8):
                ps = pcd[tile % 2]
                fw.mm(ps[:, 0:128], (Wx[:, tile, 0:128] if cfg.get('alt') else BDq[:, tile, :]), cact[:, tile, :])
                fw.mm(ps[:, 128:256], (Wx[:, tile, 0:128] if cfg.get('alt') else BDk[:, tile, :]), cact[:, tile, :])
                if cfg.get('alt') != 2:
                    fw.cp("dve", qT[:, tile, :], ps[:, 0:128])
                if cfg.get('alt') not in (2, 3):
                    fw.ts("dve", kT[:, tile, :], ps[:, 128:256], 0.0625, None, ALU.mult)
            if lvl < 2.1:
                continue
            for tile in range(8):
                fw.mm(PA[:, tile * 128:(tile + 1) * 128], xmraw[:, tile, :], BDv[:, tile, :])
                fw.mm(PB[:, tile * 128:(tile + 1) * 128], cact[:, tile, :], BDk[:, tile, :])
            if lvl < 2.2:
                continue
            fw.cp("act", vtok[:, :, 0:256], PA[:, :].rearrange("p (h v) -> p h v", h=4))
            if lvl < 2.3:
                continue
            fw.act(sm2[:, 4:8], sm2[:, 4:8], AF.Exp, scale=-1.0)
            fw.act(sm2[:, 4:8], sm2[:, 4:8], AF.Ln, bias=1.0)
            fw.ts("dve", sm2[:, 4:8], sm2[:, 4:8], -1.0, None, ALU.mult)
            fw.mm(PE[:, 64:80], tri_le, sm2[:, 0:16])
            fw.mm(PE[:, 96:112], ones, sm2[:, 0:16])
            fw.cp("dve", sm2[:, 8:12], PE[:, 68:72])
            fw.cp("dve", sm2[:, 12:16], PE[:, 100:104])
            fw.tt("dve", sm2[:, 16:20], sm2[:, 8:12], mprev[:], ALU.add)
            if lvl < 3:
                continue
            for h in range(4):
                fw.ts("dve", Rh[:], mask_gt, sm2[:, 4 + h:5 + h], None, ALU.mult)
                fw.stt(Rh[:], ident, sm2[:, h:h + 1], Rh[:], ALU.mult, ALU.add)
                fw.mm(PD[:, 0:128], tri_le, Rh[:])
                fw.tt("dve", dlm[:], PD[:, 0:128], negmask, ALU.add)
                fw.red(hs[:, 0:1], dlm[:], ALU.max)
                fw.tt("dve", mt[:, h:h + 1], hs[:, 0:1], sm2[:, 16 + h:17 + h], ALU.max)
                fw.ts("dve", hs[:, 1:2], mt[:, h:h + 1], -1.0, None, ALU.mult)
                fw.act(Dm[:], dlm[:], AF.Exp, bias=hs[:, 1:2])
                fw.mm(PD[:, 128:256], qT[:, 2 * h, :], kT[:, 2 * h, :], start=True, stop=False)
                fw.mm(PD[:, 128:256], qT[:, 2 * h + 1, :], kT[:, 2 * h + 1, :], start=False, stop=True)
                fw.tt("dve", Sg[:], PD[:, 128:256], Dm[:], ALU.mult)
                fw.tr(PT[:, 0:128], Sg[:], identb[:, :])
                fw.cp("dve", SgT[:], PT[:, 0:128])
                fw.mm(PA[:, 0:258], SgT[:], vtok[:, h, 0:258])
                fw.mm(PA[:, 512:770], qT[:, 2 * h, :], Cb[:, 0, h, 0:258], start=True, stop=False)
                fw.mm(PA[:, 512:770], qT[:, 2 * h + 1, :], Cb[:, 1, h, 0:258], start=False, stop=True)
                fw.act(hs[:, 2:3], sm2[:, 16 + h:17 + h], AF.Exp, bias=hs[:, 1:2])
                fw.act(comb[:], PA[:, 512:770], AF.Copy, scale=hs[:, 2:3])
                fw.tt("dve", comb[:], comb[:], PA[:, 0:258], ALU.add)
                fw.act(hs[:, 3:4], mt[:, h:h + 1], AF.Exp, scale=-1.0)
                fw.ts("dve", hs[:, 6:7], comb[:, 256:257], -1.0, None, ALU.mult)
                fw.tt("dve", hs[:, 6:7], hs[:, 6:7], comb[:, 256:257], ALU.max)
                fw.tt("dve", hs[:, 4:5], hs[:, 6:7], hs[:, 3:4], ALU.max)
                fw.recip(hs[:, 5:6], hs[:, 4:5])
                fw.ts("dve", hh[:], comb[:, 0:256], hs[:, 5:6], None, ALU.mult)
                grp_rstd(hh[:], 256, nst[:, 7:8], junk)
                fw.ts("dve", hmn[:, h * 256:(h + 1) * 256], hh[:], nst[:, 7:8], None, ALU.mult)
            if lvl < 4:
                continue
            to_feat(hmn, hmnT, 128)
            for tile in range(8):
                fw.ts("dve", hmfT[:, tile, :], hmnT[:, tile, :], mlcol[:, tile, 0:1], None, ALU.mult)
                fw.stt(hmfT[:, tile, :], cact[:, tile, :], mlcol[:, tile, 1:2], hmfT[:, tile, :], ALU.mult, ALU.add)
            fw.tt("dve", hmfT[:], hmfT[:], sigoT[:], ALU.mult)
            if lvl < 5:
                continue
            fw.mm(PE[:, 112:128], sel127, mt[:])
            fw.cp("dve", sm2[:, 20:24], PE[:, 112:116])
            fw.tt("dve", sm2[:, 24:28], sm2[:, 12:16], sm2[:, 8:12], ALU.subtract)
            fw.tt("dve", sm2[:, 24:28], sm2[:, 24:28], sm2[:, 0:4], ALU.add)
            fw.tt("dve", sm2[:, 24:28], sm2[:, 24:28], sm2[:, 20:24], ALU.subtract)
            fw.act(sm2[:, 28:32], sm2[:, 24:28], AF.Exp)
            fw.ts("dve", sm2[:, 28:32], sm2[:, 28:32], 0.0625, None, ALU.mult)
            fw.tt("dve", sm2[:, 32:36], sm2[:, 12:16], mprev[:], ALU.add)
            fw.tt("dve", sm2[:, 32:36], sm2[:, 32:36], sm2[:, 20:24], ALU.subtract)
            fw.act(sm2[:, 32:36], sm2[:, 32:36], AF.Exp)
            fw.tt("dve", kw_[:], PB[:, :].rearrange("p (h d) -> p h d", h=4), sm2[:, 28:32].bc(2, 256), ALU.mult)
            for kt in range(2):
                for h in range(4):
                    fw.mm(PB[:, h * 256:(h + 1) * 256], kw_[:, h, kt * 128:(kt + 1) * 128], vtok[:, h, 0:256])
                    fw.mm(PE[:, 80 + 2 * h:82 + 2 * h], kw_[:, h, kt * 128:(kt + 1) * 128], onesb[:, 0:2])
                for h in range(4):
                    fw.stt(Cst[:, kt, h, 0:256], Cst[:, kt, h, 0:256], sm2[:, 32 + h:33 + h],
                           PB[:, h * 256:(h + 1) * 256], ALU.mult, ALU.add)
                    fw.stt(Cst[:, kt, h, 256:257], Cst[:, kt, h, 256:257], sm2[:, 32 + h:33 + h],
                           PE[:, 80 + 2 * h:81 + 2 * h], ALU.mult, ALU.add)
            fw.cp("pool", Cb[:], Cst[:])
            fw.cp("dve", mprev[:], sm2[:, 20:24])
            if lvl < 6:
                continue
            for half in range(2):
                for kt in range(8):
                    fw.mm(PA[:, half * 512:(half + 1) * 512], hmfT[:, kt, :], Wo[:, kt, half * 512:(half + 1) * 512],
                          start=kt == 0, stop=kt == 7)
            fw.tt("dve", x1[:], x1[:], PA[:, :], ALU.add)
            fw.dma("sp", io.s1[c * 128:(c + 1) * 128, :], x1[:])

        if SAMPLE:
            cst_s = fw.sb([128, 8, 3, NB], F32, "cst_s")
            fw.dma("sp", cst_s[:], io.conv_s_in[:, 12:20, :, :])
            uS = fw.sb([128, 8, NB], F32, "uS")
            accs = fw.sb([128, 8, NB], F32, "accs")
            tmps = fw.sb([128, 8, NB], F32, "tmps")
            cs = fw.sb([128, 8, NB], F32, "cs")
            cs_bf = fw.sb([128, 8, NB], BF16, "cs_bf")
            us_bf = fw.sb([128, 8, NB], BF16, "us_bf")
            qTs = fw.sb([128, 8, NB], F32, "qTs")
            kTs = fw.sb([128, 8, NB], F32, "kTs")
            kws = fw.sb([128, 8, NB], F32, "kws")
            nS = fw.sb([128, 8, NB], F32, "nS")
            vtoks = fw.sb([16, D], F32, "vtoks")
            g16 = fw.sb([16, 64], F32, "g16")
            Zd = fw.sb([16, 128], F32, "Zd")
            wd = fw.sb([128, 2, 4, NB], F32, "wd")
            qmask = fw.sb([128, 8, NB, NB], F32, "qmask")
            Cs = [fw.sb([128, 8, 256], F32, f"Cs{i}") for i in range(2)]
            Tt = fw.sb([128, 8, 256], F32, "Tt")
            numt = fw.sb([16, D], F32, "numt")
            fw.dma("sp", xt[0:NB, :], io.xs[:, :])
            fw.dma("sp", x1[0:NB, :], io.s1s[:, :])
            fw.dma("sp", g16[:, 8:12], io.mm_s_in[:, :])
            fw.dma("sp", nS[:], io.mn_s_in[:, :, :])
            rmsnorm(xt[0:NB, :], gmix, xn[0:NB, :], NB, junk)
            to_feat(xn, xnT, NB)
            proj_feat(Wx, 0, 8, xnT, NB, lambda g0, n, ps: fw.cp("act", uS[:, g0:g0 + n, :], ps))
            proj_feat(Wg, 0, 8, xnT, NB, lambda g0, n, ps: fw.act(sigoT[:, g0:g0 + n, 0:NB], ps, AF.Sigmoid))
            for kt in range(8):
                fw.mm(PE[0:NB, 16:32], xnT[:, kt, 0:NB], Wif[:, kt, :], start=kt == 0, stop=kt == 7)
            fw.tt("dve", g16[:, 0:4], PE[0:NB, 24:28], ib_bc[0:NB, :], ALU.add)
            fw.tt("dve", g16[:, 4:8], PE[0:NB, 28:32], fb_bc[0:NB, :], ALU.add)
            fw.act(g16[:, 4:8], g16[:, 4:8], AF.Exp, scale=-1.0)
            fw.act(g16[:, 4:8], g16[:, 4:8], AF.Ln, bias=1.0)
            fw.ts("dve", g16[:, 4:8], g16[:, 4:8], -1.0, None, ALU.mult)
            wv = lambda j: convp[:, 12:20, j].bc(2, NB)
            fw.tt("dve", accs[:], cst_s[:, :, 0, :], wv(0), ALU.mult)
            fw.tt("dve", accs[:], accs[:], wv(4), ALU.add)
            for j in (1, 2):
                fw.tt("dve", tmps[:], cst_s[:, :, j, :], wv(j), ALU.mult)
                fw.tt("dve", accs[:], accs[:], tmps[:], ALU.add)
            fw.tt("dve", tmps[:], uS[:], wv(3), ALU.mult)
            fw.tt("dve", accs[:], accs[:], tmps[:], ALU.add)
            fw.act(cs[:], accs[:], AF.Silu)
            fw.dma("sp", io.conv_s[:, 12:20, 0:2, :], cst_s[:, :, 1:3, :])
            fw.dma("sp", io.conv_s[:, 12:20, 2, :], uS[:])
            fw.cp("dve", cs_bf[:], cs[:])
            fw.cp("dve", us_bf[:], uS[:])
            for tile in range(8):
                ps = pcd[tile % 2]
                fw.mm(ps[:, 0:NB], BDq[:, tile, :], cs_bf[:, tile, :])
                fw.mm(ps[:, 16:16 + NB], BDk[:, tile, :], cs_bf[:, tile, :])
                fw.cp("dve", qTs[:, tile, :], ps[:, 0:NB])
                fw.ts("dve", kTs[:, tile, :], ps[:, 16:16 + NB], 0.0625, None, ALU.mult)
            for tile in range(8):
                fw.mm(PA[0:NB, tile * 128:(tile + 1) * 128], us_bf[:, tile, :], BDv[:, tile, :])
            fw.cp("act", vtoks[:], PA[0:NB, :])
            fw.tt("dve", g16[:, 16:20], g16[:, 4:8], g16[:, 8:12], ALU.add)
            fw.tt("dve", g16[:, 12:16], g16[:, 16:20], g16[:, 0:4], ALU.max)
            fw.dma("sp", io.mm_s[:, :], g16[:, 12:16])
            fw.tt("dve", g16[:, 20:24], g16[:, 0:4], g16[:, 12:16], ALU.subtract)
            fw.act(g16[:, 20:24], g16[:, 20:24], AF.Exp)
            fw.tt("dve", g16[:, 24:28], g16[:, 16:20], g16[:, 12:16], ALU.subtract)
            fw.act(g16[:, 24:28], g16[:, 24:28], AF.Exp)
            fw.act(g16[:, 28:32], g16[:, 12:16], AF.Exp, scale=-1.0)
            z3 = lambda r: r.rearrange("p (h b) -> p h b", h=4)
            fw.tt("dve", z3(Zd[:, 0:64]), g16[:, 20:24].bc(2, NB), ident[0:NB, 0:NB].bc(1, 4), ALU.mult)
            fw.tt("dve", z3(Zd[:, 64:128]), g16[:, 24:28].bc(2, NB), ident[0:NB, 0:NB].bc(1, 4), ALU.mult)
            fw.mm(PE[:, 128:256], ones[0:NB, :], Zd[:, :])
            fw.cp("dve", wd[:], PE[:, 128:256].rearrange("p (w h b) -> p w h b", w=2, h=4))
            k4 = lambda r: r.rearrange("p (h k) b -> p h k b", h=4)
            fw.tt("dve", k4(kws[:, :, :]), k4(kTs[:, :, :]), wd[:, 0, :, :].bc(2, 2), ALU.mult)
            fw.tt("dve", k4(nS[:, :, :]), k4(nS[:, :, :]), wd[:, 1, :, :].bc(2, 2), ALU.mult)
            fw.tt("dve", nS[:], nS[:], kws[:], ALU.add)
            fw.dma("sp", io.mn_s[:, :, :], nS[:])
            fw.tt("dve", tmps[:], qTs[:], nS[:], ALU.mult)
            for h in range(4):
                for kt in range(2):
                    fw.mm(PE[0:NB, 256 + 2 * h:258 + 2 * h], tmps[:, 2 * h + kt, :], ones[:, 0:2], start=kt == 0, stop=kt == 1)
            fw.tt("dve", qmask[:], qTs[:, :, :].bc(2, NB), eye16[:, :, :].bc(1, 8), ALU.mult)
            for b in range(NB):
                Cc = Cs[b % 2]
                fw.dma("sp", Cc[:], io.mc_s_in[b].rearrange("h (k p) v -> p (h k) v", p=128))
                fw.mm(PA[:, 0:512], sel16[:, b, :], vtoks[:, 0:512])
                fw.mm(PA[:, 512:1024], sel16[:, b, :], vtoks[:, 512:1024])
                fw.tt("dve", Tt[:, :, :].rearrange("p (h k) v -> p h k v", h=4),
                      PA[:, :].rearrange("p (h v) -> p h v", h=4).bc(2, 2),
                      kws[:, :, b].rearrange("p (h k) -> p h k", h=4).bc(3, 256), ALU.mult)
                fw.tt("pool", Cc[:, :, :].rearrange("p (h k) v -> p h (k v)", h=4),
                      Cc[:, :, :].rearrange("p (h k) v -> p h (k v)", h=4), wd[:, 1, :, b].bc(2, 512), ALU.mult)
                fw.tt("dve", Cc[:], Cc[:], Tt[:], ALU.add)
                fw.dma("sp", io.mc_s[b].rearrange("h (k p) v -> p (h k) v", p=128), Cc[:])
                for tile in range(8):
                    h, kt = tile // 2, tile % 2
                    fw.mm(PB[0:NB, h * 256:(h + 1) * 256], qmask[:, tile, b, :], Cc[:, tile, :],
                          start=(b == 0 and tile in (0, 4)), stop=(b == NB - 1 and kt == 1))
            fw.cp("act", numt[:], PB[0:NB, :])
            if "dbg_a" in dbg:
                fw.dma("sp", io.dbg_a[:, :], numt[:])
                fw.cp("dve", g16[:, 40:44], PE[0:NB, 256:264].rearrange("p (h t) -> p h t", t=2)[:, :, 0])
                fw.dma("sp", io.dbg_b[:, :], g16[:])
            dn = PE[0:NB, 256:264].rearrange("p (h t) -> p h t", t=2)[:, :, 0]
            fw.ts("dve", g16[:, 32:36], dn, -1.0, None, ALU.mult)
            fw.tt("dve", g16[:, 32:36], g16[:, 32:36], dn, ALU.max)
            fw.tt("dve", g16[:, 32:36], g16[:, 32:36], g16[:, 28:32], ALU.max)
            fw.recip(g16[:, 36:40], g16[:, 32:36])
            fw.tt("dve", numt[:, :].rearrange("p (h v) -> p h v", h=4), numt[:, :].rearrange("p (h v) -> p h v", h=4),
                  g16[:, 36:40].bc(2, 256), ALU.mult)
            for h in range(4):
                grp_rstd(numt[:, h * 256:(h + 1) * 256], 256, nst[0:NB, 7:8], junk, NB)
                fw.ts("dve", hmn[0:NB, h * 256:(h + 1) * 256], numt[:, h * 256:(h + 1) * 256], nst[0:NB, 7:8], None, ALU.mult)
            to_feat(hmn, hmnT, NB)
            for tile in range(8):
                fw.ts("dve", hmfT[:, tile, 0:NB], hmnT[:, tile, 0:NB], mlcol[:, tile, 0:1], None, ALU.mult)
                fw.stt(hmfT[:, tile, 0:NB], cs_bf[:, tile, :], mlcol[:, tile, 1:2], hmfT[:, tile, 0:NB], ALU.mult, ALU.add)
            fw.tt("dve", hmfT[:, :, 0:NB], hmfT[:, :, 0:NB], sigoT[:, :, 0:NB], ALU.mult)
            for half in range(2):
                for kt in range(8):
                    fw.mm(PA[0:NB, half * 512:(half + 1) * 512], hmfT[:, kt, 0:NB], Wo[:, kt, half * 512:(half + 1) * 512],
                          start=kt == 0, stop=kt == 7)
            fw.tt("dve", x1[0:NB, :], x1[0:NB, :], PA[0:NB, :], ALU.add)
            fw.dma("sp", io.s1s[:, :], x1[0:NB, :])
        fw.dma("sp", io.mc_p[:, :, :, :], Cst[:])
        fw.dma("sp", io.mm_p[:, :], mprev[0:1, :])
        fw.dma("sp", io.conv_p[:, 12:20, :], convin[:, :, 0:3])
        fw.release(base_mark)

    def ffn_phase(layer, src, dst, ssrc, sdst, final):
        Wgu = fw.sb([128, 8, 2 * DFF], BF16, "Wgu")
        load_w(Wgu, io.w_gu[layer], 0, 8, step=1)
        Wd = fw.sb([128, 22, D], BF16, "Wd")
        load_w(Wd, io.w_dn[layer], 0, 22)
        gf = fw.sb([128, D], F32, "gf")
        fw.dma("sp", gf[:], io.norm_ffn[layer, :].partition_broadcast(128))
        if final:
            gfin = fw.sb([128, D], F32, "gfin")
            fw.dma("sp", gfin[:], io.norm_final.partition_broadcast(128))
        GB = 4
        xt = fw.sb([128, D], F32, "xt")
        junk = fw.sb([128, D], F32, "junk")
        xn = fw.sb([128, D], BF16, "xn")
        xnT = fw.sb([128, 8, GB * 128], BF16, "xnT")
        hT = fw.sb([128, 22, GB * 128], BF16, "hT")
        sg = [fw.sb([128, GB * 128], F32, f"sg{i}") for i in range(2)]
        x2 = fw.sb([128, D], F32, "x2")
        yo = junk
        groups = [list(range(g, min(g + GB, NCH))) for g in range(0, NCH, GB)]
        if SAMPLE:
            groups.append([NCH])
        for grp in groups:
            samp = grp[0] == NCH
            M = NB if samp else 128
            W = M * len(grp)
            rows = lambda ap, c: (ap[:, :] if samp else ap[c * 128:(c + 1) * 128, :])
            for gi, c in enumerate(grp):
                fw.dma("sp", xt[0:M, :], rows(ssrc if samp else src, c))
                rmsnorm(xt[0:M, :], gf, xn[0:M, :], M, junk)
                for kt in range(8):
                    fw.tr(PT3[:, kt, 0:M], xn[0:M, kt * 128:(kt + 1) * 128], identb[0:M, 0:M])
                fw.cp("dve", xnT[:, :, gi * M:(gi + 1) * M], PT3[:, :, 0:M])
            for j in range(22):
                psg = pcd[j % 2]
                for kt in range(8):
                    fw.mm(psg[:, 0:W], Wgu[:, kt, j * 128:(j + 1) * 128], xnT[:, kt, 0:W], start=kt == 0, stop=kt == 7)
                psu = PB0f if j % 2 == 0 else PB1f
                for kt in range(8):
                    fw.mm(psu[:, 0:W], Wgu[:, kt, DFF + j * 128:DFF + (j + 1) * 128], xnT[:, kt, 0:W],
                          start=kt == 0, stop=kt == 7)
                fw.act(sg[j % 2][:, 0:W], psg[:, 0:W], AF.Silu)
                fw.tt("dve", hT[:, j, 0:W], sg[j % 2][:, 0:W], psu[:, 0:W], ALU.mult)
            for gi, c in enumerate(grp):
                for half in range(2):
                    for j in range(22):
                        fw.mm(PA[0:M, half * 512:(half + 1) * 512], hT[:, j, gi * M:(gi + 1) * M],
                              Wd[:, j, half * 512:(half + 1) * 512], start=j == 0, stop=j == 21)
                fw.dma("sp", xt[0:M, :], rows(ssrc if samp else src, c))
                fw.tt("dve", x2[0:M, :], xt[0:M, :], PA[0:M, :], ALU.add)
                if final:
                    rmsnorm(x2[0:M, :], gfin, yo[0:M, :], M, junk)
                    fw.dma("sp", rows(sdst if samp else dst, c), yo[0:M, :])
                else:
                    fw.dma("sp", rows(sdst if samp else dst, c), x2[0:M, :])
        fw.release(base_mark)

    if "1" in phases:
        ffn_phase(0, io.s1, io.s2, io.s1s, io.s2s, False)

    if "2" in phases:
        Wr = fw.sb([128, 8, D], BF16, "Wr"); load_w(Wr, io.rw_wr, 0, 8)
        Wk = fw.sb([128, 8, D], BF16, "Wk"); load_w(Wk, io.rw_wk, 0, 8)
        A1 = fw.sb([128, 8, 64], BF16, "A1"); load_w(A1, io.rw_a1, 0, 8, step=8)
        A2 = fw.sb([128, D], BF16, "A2"); fw.dma("pool", A2[0:64, :], io.rw_a2[:, :])
        Wv = fw.sb([128, 8, D], BF16, "Wv"); load_w(Wv, io.rw_wv, 0, 8)
        G1 = fw.sb([128, 8, 160], BF16, "G1"); load_w(G1, io.rw_g1, 0, 8, step=8)
        G2a = fw.sb([128, D], BF16, "G2a"); fw.dma("pool", G2a[:], io.rw_g2[0:128, :])
        G2b = fw.sb([128, D], BF16, "G2b"); fw.dma("pool", G2b[0:32, :], io.rw_g2[128:160, :])
        W1 = fw.sb([128, 8, 64], BF16, "W1"); load_w(W1, io.rw_w1, 0, 8, step=8)
        W2 = fw.sb([128, D], BF16, "W2"); fw.dma("pool", W2[0:64, :], io.rw_w2[:, :])
        Wo = fw.sb([128, 8, D], BF16, "Wo"); load_w(Wo, io.rw_wo, 0, 8)
        gm1 = fw.sb([128, D], F32, "gm1")
        fw.dma("sp", gm1[:], io.norm_mix[1, :].partition_broadcast(128))
        rows = []
        for i in range(7):
            rt = fw.sb([128, D], F32, f"row{i}")
            fw.dma("sp", rt[:], io.rw_rows[i, :].partition_broadcast(128))
            rows.append(rt)
        w0b, a0b, kkb_, kab, rkb, lnw, lnb = rows
        mu = fw.sb([128, 8, 6], F32, "mu")
        fw.dma("sp", mu[:], io.rw_mu[:, :, :])
        PB0 = fw.view(PB[:, 0:512], "PB0")
        PB1 = fw.view(PB[:, 512:1024], "PB1")
        NPS = [PB0, PB1, PC, PD]
        PAh = [PA[:, 0:512], PA[:, 512:1024]]
        PBh = [PB0[:, :], PB1[:, :]]
        h1 = fw.sb([128, 2, 128], BF16, "h1")

        def proj_tok(xT, W, Ph, M=128):
            for half in range(2):
                for kt in range(8):
                    fw.mm(Ph[half][0:M, :], xT[:, kt, 0:M], W[:, kt, half * 512:(half + 1) * 512],
                          start=kt == 0, stop=kt == 7)

        def lora(xT, Wa, nh, Wb_list, func, P, M=128):
            widths = [min(128, nh), nh - 128] if nh > 128 else [nh]
            for wi, wd in enumerate(widths):
                for kt in range(8):
                    fw.mm(PE[0:wd, wi * 128:wi * 128 + M], Wa[:, kt, wi * 128:wi * 128 + wd], xT[:, kt, 0:M],
                          start=kt == 0, stop=kt == 7)
                fw.act(h1[0:wd, wi, 0:M], PE[0:wd, wi * 128:wi * 128 + M], func)
            for half in range(2):
                for wi, wd in enumerate(widths):
                    fw.mm(P[half][0:M, :], h1[0:wd, wi, 0:M], Wb_list[wi][0:wd, half * 512:(half + 1) * 512],
                          start=wi == 0, stop=wi == len(widths) - 1)

        def rstd16(src16, dst16, mult_, eps, floor=None):
            if floor is not None:
                fw.ts("dve", dst16, src16, floor, None, ALU.max)
            else:
                fw.ts("dve", dst16, src16, mult_, eps, ALU.mult, ALU.add)
            fw.act(dst16, dst16, AF.Ln)
            fw.act(dst16, dst16, AF.Exp, scale=-0.5)

        mark2 = fw.mark()
        xt = fw.sb([128, D], F32, "xt")
        junk = fw.sb([128, D], F32, "junk")
        tmpA = fw.sb([128, D], F32, "tmpA")
        tmpB = fw.sb([128, D], F32, "tmpB")
        Et = fw.sb([128, D], F32, "Et")
        SB = [fw.sb([128, D], BF16, f"S{i}") for i in range(13)]
        xn = SB[0]; r_bf = SB[1]; kkn = SB[2]; kf_bf = SB[3]; b_bf = SB[4]; v_bf = SB[5]; bv = SB[6]
        g_bf = SB[7]; abar = SB[8]; bbar = SB[9]; kbar = SB[10]; btil = SB[11]; ktil = SB[12]
        rbar = SB[0]; yo = SB[8]
        xnTe = fw.sb([128, 8, 130], BF16, "xnTe")
        fw.memset("pool", xnTe[:], 0.0)
        xx = fw.sb([128, 8, 128], BF16, "xx")
        mixb = [fw.sb([128, 8, 128], BF16, f"mix{i}") for i in range(2)]
        arT = fw.sb([128, 8, 2, 128], BF16, "arT")
        bT = fw.sb([128, 8, 128], BF16, "bT")
        kT = fw.sb([128, 8, 128], BF16, "kT")
        yoT = fw.sb([128, 8, 128], BF16, "yoT")
        Ms = [fw.sb([128, 512], BF16, f"Ms{i}") for i in range(4)]
        Q0 = [fw.sb([128, 128], BF16, f"Q0{i}") for i in range(4)]
        PQ = [[fw.sb([128, 384], BF16, f"PQ{i}{k}") for k in range(2)] for i in range(4)]
        RHSb = [fw.sb([128, 64], BF16, f"RHS{i}") for i in range(4)]
        Ubp = [fw.sb([128, 2, 64], BF16, f"Ubp{i}") for i in range(2)]
        Hst = fw.sb([128, 8, 64], F32, "Hst")
        fw.memset("pool", Hst[:], 0.0)
        Hb = fw.sb([128, 8, 64], BF16, "Hb")
        fw.memset("pool", Hb[:], 0.0)
        eLT = fw.sb([128, 8], F32, "eLT")
        s16 = fw.sb([128, 64], F32, "s16")
        x3 = tmpB
        mcount = [0]

        def mix(cidx):
            dst = mixb[mcount[0] % 2]
            mcount[0] += 1
            for kt in range(8):
                fw.stt(dst[:, kt, :], xx[:, kt, :], mu[:, kt, cidx:cidx + 1], xnTe[:, kt, 1:129], ALU.mult, ALU.add)
            return dst

        for c in range(NCH):
            fw.dma("sp", xt[:], io.s2[c * 128:(c + 1) * 128, :])
            if c == NCH - 1:
                fw.act(junk[:, :], xt[:], AF.Square, accum_out=nst[:, 0:1])
                fw.ts("dve", nst[:, 1:2], nst[:, 0:1], 1.0 / D, EPS, ALU.mult, ALU.add)
                fw.act(nst[:, 2:3], nst[:, 1:2], AF.Ln)
                fw.act(nst[:, 3:4], nst[:, 2:3], AF.Exp, scale=-0.5)
                fw.stt(tmpA[:], xt[:], nst[:, 3:4], gm1[:], ALU.mult, ALU.mult)
                fw.dma("sp", io.shift_p[:, :], tmpA[127:128, :])
                fw.cp("dve", xn[:], tmpA[:])
            else:
                rmsnorm(xt[:], gm1, xn[:], 128, junk)
            for kt in range(8):
                fw.tr(PT3[:, kt, :], xn[:, kt * 128:(kt + 1) * 128], identb[:, :])
            fw.cp("dve", xnTe[:, :, 1:129], PT3)
            fw.tt("pool", xx[:], xnTe[:, :, 0:128], xnTe[:, :, 1:129], ALU.subtract)
            proj_tok(mix(0), Wr, PAh)
            fw.cp("act", r_bf[:], PA[:, :])
            proj_tok(mix(2), Wk, PAh)
            fw.tt("dve", tmpA[:], PA[:, :], kkb_[:], ALU.mult)
            fw.tt("pool", junk[:], tmpA[:], tmpA[:], ALU.mult)
            fw.red(s16[:, 0:16], v16(junk[:, :]), ALU.add)
            rstd16(s16[:, 0:16], s16[:, 16:32], None, None, floor=1e-24)
            fw.tt("dve", v16(kkn[:, :]), v16(tmpA[:, :]), s16[:, 16:32].bc(2, 64), ALU.mult)
            lora(mix(4), A1, 64, [A2], AF.Copy, PBh)
            for i in range(2):
                fw.tt("dve", tmpB[:, i * 512:(i + 1) * 512], PBh[i], a0b[:, i * 512:(i + 1) * 512], ALU.add)
            fw.act(tmpB[:], tmpB[:], AF.Sigmoid)
            fw.stt(junk[:], tmpB[:], 1.0, kab[:], ALU.subtract, ALU.mult)
            fw.ts("dve", junk[:], junk[:], 1.0, None, ALU.add)
            fw.tt("dve", kf_bf[:], PA[:, :], junk[:], ALU.mult)
            fw.tt("pool", b_bf[:], kkn[:], tmpB[:], ALU.mult)
            fw.tt("pool", junk[:], r_bf[:], kf_bf[:], ALU.mult)
            fw.tt("pool", junk[:], junk[:], rkb[:], ALU.mult)
            fw.red(s16[:, 32:48], v16(junk[:, :]), ALU.add)
            proj_tok(mix(3), Wv, PAh)
            fw.cp("act", v_bf[:], PA[:, :])
            fw.tt("dve", v16(bv[:, :]), v16(PA[:, :]), s16[:, 32:48].bc(2, 64), ALU.mult)
            lora(mix(5), G1, 160, [G2a, G2b], AF.Sigmoid, PBh)
            for i in range(2):
                fw.cp("act", g_bf[:, i * 512:(i + 1) * 512], PBh[i])
            lora(mix(1), W1, 64, [W2], AF.Tanh, PAh)
            fw.tt("dve", tmpA[:], PA[:, :], w0b[:], ALU.add)
            fw.act(tmpA[:], tmpA[:], AF.Exp, scale=-1.0)
            fw.act(tmpA[:], tmpA[:], AF.Ln, bias=1.0)
            fw.ts("dve", tmpA[:], tmpA[:], -1.0, -0.5, ALU.mult, ALU.add)
            fw.act(Et[:], tmpA[:], AF.Exp)
            fw.mm(PB0[:, :], tri_le, Et[:, 0:512])
            fw.mm(PB1[:, :], tri_le, Et[:, 512:1024])
            fw.mm(PA[:, 0:512], ones, Et[:, 0:512])
            fw.mm(PA[:, 512:1024], ones, Et[:, 512:1024])
            for kt in range(8):
                fw.mm(PE[:, kt * 16:(kt + 1) * 16], Et[:, kt * 128:(kt + 1) * 128], ones[:, 0:16])
            fw.act(eLT[:], PE[:, 0:128].rearrange("p (k s) -> p k s", s=16)[:, :, 0], AF.Exp, scale=-1.0)
            hv = lambda r, i: r[:, i * 512:(i + 1) * 512]
            for i, PBi in enumerate((PB0, PB1)):
                fw.act(hv(tmpA, i), PBi[:, :], AF.Exp, scale=-1.0)
                fw.tt("pool", hv(rbar, i), hv(r_bf, i), hv(tmpA, i), ALU.mult)
                fw.tt("dve", hv(tmpB, i), hv(Et, i), PBi[:, :], ALU.subtract)
                fw.act(hv(tmpB, i), hv(tmpB, i), AF.Exp)
                fw.stt(hv(abar, i), hv(kkn, i), -1.0, hv(tmpB, i), ALU.mult, ALU.mult)
            fw.cp("act", junk[:], PA[:, :])
            for i, PBi in enumerate((PB0, PB1)):
                fw.act(hv(tmpA, i), PBi[:, :], AF.Exp)
                fw.tt("pool", hv(bbar, i), hv(b_bf, i), hv(tmpA, i), ALU.mult)
                fw.tt("pool", hv(kbar, i), hv(kf_bf, i), hv(tmpA, i), ALU.mult)
                fw.tt("dve", hv(tmpB, i), PBi[:, :], hv(junk, i), ALU.subtract)
                fw.act(hv(tmpB, i), hv(tmpB, i), AF.Exp)
                fw.tt("pool", hv(btil, i), hv(b_bf, i), hv(tmpB, i), ALU.mult)
                fw.tt("dve", hv(ktil, i), hv(kf_bf, i), hv(tmpB, i), ALU.mult)
            for src, dst in ((abar, arT[:, :, 0, :]), (rbar, arT[:, :, 1, :]), (bbar, bT[:, :, :]), (kbar, kT[:, :, :])):
                for kt in range(8):
                    fw.tr(PT3[:, kt, :], src[:, kt * 128:(kt + 1) * 128], identb[:, :])
                fw.cp("dve", dst, PT3)
            for h0 in range(0, 16, 4):
                hd = []
                for i in range(4):
                    h = h0 + i
                    j, e = h // 2, h % 2
                    p0 = 64 * e
                    hd.append(dict(h=h, j=j, e=e, p0=p0, NP=NPS[i],
                                   aT=arT[p0:p0 + 64, j, 0, :], rT=arT[p0:p0 + 64, j, 1, :],
                                   ar=arT[p0:p0 + 64, j, :, :].rearrange("p a t -> p (a t)"),
                                   bT=bT[p0:p0 + 64, j, :], kT=kT[p0:p0 + 64, j, :]))
                for i, d in enumerate(hd):
                    fw.mm(PE[:, 0:256], d["bT"], d["ar"])
                    fw.mm(PE[:, 256:512], d["kT"], d["ar"])
                    fw.tt("dve", Ms[i][:], PE[:, :], m4, ALU.mult)
                    fw.mm(d["NP"][:, 0:128], d["aT"], d["bT"])
                    fw.tt("dve", Q0[i][:], d["NP"][:, 0:128], mask_gt, ALU.mult)
                    d["P"], d["Q"], d["Z"] = Ms[i][:, 0:128], Q0[i][:], identb[:, :]
                for k in range(7):
                    for i, d in enumerate(hd):
                        NP = d["NP"]
                        if k < 6:
                            fw.mm(NP[:, 0:128], d["Q"], d["P"])
                            fw.mm(NP[:, 128:256], d["P"], d["Q"])
                        fw.mm(NP[:, 256:384], identb[:, :], d["Z"], start=True, stop=False)
                        fw.mm(NP[:, 256:384], d["Q"], d["Z"], start=False, stop=True)
                    for i, d in enumerate(hd):
                        NP = d["NP"]
                        pq = PQ[i][k % 2]
                        lo = 0 if k < 6 else 256
                        fw.cp("act" if i % 2 == 0 else "dve", pq[:, lo:384], NP[:, lo:384])
                        d["P"], d["Q"], d["Z"] = pq[:, 0:128], pq[:, 128:256], pq[:, 256:384]
                for i, d in enumerate(hd):
                    NP, h, j, p0 = d["NP"], d["h"], d["j"], d["p0"]
                    fw.mm(NP[:, 384:448], d["aT"], Hb[p0:p0 + 64, j, :], start=True, stop=False)
                    fw.mm(NP[:, 384:448], Ms[i][:, 256:384], v_bf[:, h * 64:(h + 1) * 64], start=False, stop=True)
                    fw.cp("act", RHSb[i][:], NP[:, 384:448])
                for i, d in enumerate(hd):
                    NP, h, j, e = d["NP"], d["h"], d["j"], d["e"]
                    fw.mm(NP[:, 448:512], d["Z"], RHSb[i][:])
                    fw.cp("dve", Ubp[j % 2][:, e, :], NP[:, 448:512])
                for i, d in enumerate(hd):
                    h, j, e, p0 = d["h"], d["j"], d["e"], d["p0"]
                    ysl = PA[:, h * 64:(h + 1) * 64]
                    fw.mm(ysl, d["rT"], Hb[p0:p0 + 64, j, :], start=True, stop=False)
                    fw.mm(ysl, Ms[i][:, 128:256], Ubp[j % 2][:, e, :], start=False, stop=False)
                    fw.mm(ysl, Ms[i][:, 384:512], v_bf[:, h * 64:(h + 1) * 64], start=False, stop=True)
                for jj in range(2):
                    j = h0 // 2 + jj
                    NP = hd[2 * jj]["NP"]
                    fw.mm(NP[:, 0:128], btil[:, j * 128:(j + 1) * 128], Ubp[j % 2][:, :, :].rearrange("p e v -> p (e v)"),
                          start=True, stop=False)
                    fw.mm(NP[:, 0:128], ktil[:, j * 128:(j + 1) * 128], v_bf[:, j * 128:(j + 1) * 128],
                          start=False, stop=True)
                    for e in range(2):
                        p0 = 64 * e
                        fw.stt(Hst[p0:p0 + 64, j, :], Hst[p0:p0 + 64, j, :], eLT[p0:p0 + 64, j:j + 1],
                               NP[p0:p0 + 64, p0:p0 + 64], ALU.mult, ALU.add)
                    fw.cp("pool", Hb[:, j, :], Hst[:, j, :])
            fw.cp("act", tmpA[:], PA[:, :])
            fw.red(s16[:, 0:16], v16(tmpA[:, :]), ALU.add)
            fw.ts("dve", s16[:, 0:16], s16[:, 0:16], 1.0 / 64, None, ALU.mult)
            fw.tt("dve", v16(tmpA[:, :]), v16(tmpA[:, :]), s16[:, 0:16].bc(2, 64), ALU.subtract)
            fw.tt("pool", junk[:], tmpA[:], tmpA[:], ALU.mult)
            fw.red(s16[:, 16:32], v16(junk[:, :]), ALU.add)
            rstd16(s16[:, 16:32], s16[:, 48:64], 1.0 / 64, 64e-5)
            fw.tt("dve", v16(tmpA[:, :]), v16(tmpA[:, :]), s16[:, 48:64].bc(2, 64), ALU.mult)
            fw.tt("dve", tmpA[:], tmpA[:], lnw[:], ALU.mult)
            fw.tt("dve", tmpA[:], tmpA[:], lnb[:], ALU.add)
            fw.tt("dve", tmpA[:], tmpA[:], bv[:], ALU.add)
            fw.tt("dve", yo[:], tmpA[:], g_bf[:], ALU.mult)
            to_feat(yo, yoT, 128)
            for half in range(2):
                for kt in range(8):
                    fw.mm(PA[:, half * 512:(half + 1) * 512], yoT[:, kt, :], Wo[:, kt, half * 512:(half + 1) * 512],
                          start=kt == 0, stop=kt == 7)
            fw.tt("dve", x3[:], xt[:], PA[:, :], ALU.add)
            fw.dma("sp", io.s3[c * 128:(c + 1) * 128, :], x3[:])
            fw.cp("pool", xnTe[:, :, 0:1], xnTe[:, :, 128:129])
        fw.dma("sp", io.wkv_p[:, :, :], Hst[:])
        fw.release(mark2)
        if SAMPLE:
            M = NB
            f16 = lambda nm, dt=F32: fw.sb([NB, D], dt, nm)
            xts = f16("xts"); jk = f16("jk"); tA = f16("tA"); tB = f16("tB"); Es = f16("Es")
            r_s = f16("r_s", BF16); kk_s = f16("kk_s", BF16); kf_s = f16("kf_s", BF16); b_s = f16("b_s", BF16)
            v_s = f16("v_s"); bv_s = f16("bv_s", BF16); g_s = f16("g_s", BF16); xn_s = f16("xn_s", BF16)
            sa_tok = f16("sa_tok"); yo_s = f16("yo_s", BF16)
            xprev = fw.sb([128, 8, NB], F32, "xprev")
            xsT = fw.sb([128, 8, NB], BF16, "xsT")
            xxs = fw.sb([128, 8, NB], BF16, "xxs")
            mixs = [fw.sb([128, 8, NB], BF16, f"mixs{i}") for i in range(2)]
            featT = {nm: fw.sb([128, 8, NB], F32, nm) for nm in ("aT", "wT", "bTs", "kTs", "rTs")}
            amask = fw.sb([128, 8, NB, NB], F32, "amask")
            rmask = fw.sb([128, 8, NB, NB], F32, "rmask")
            Hs = [fw.sb([128, 8, 64], F32, f"Hs{i}") for i in range(2)]
            Tt = fw.sb([128, 8, 64], F32, "Tt")
            yoTs = fw.sb([128, 8, NB], BF16, "yoTs")
            s16 = fw.sb([NB, 64], F32, "s16s")
            v16s = lambda r: r.rearrange("p (h q) -> p h q", h=16)
            mc2 = [0]

            def mix_s(cidx):
                dst = mixs[mc2[0] % 2]
                mc2[0] += 1
                for kt in range(8):
                    fw.stt(dst[:, kt, :], xxs[:, kt, :], mu[:, kt, cidx:cidx + 1], xsT[:, kt, :], ALU.mult, ALU.add)
                return dst

            fw.dma("sp", xts[:], io.s2s[:, :])
            fw.dma("sp", xprev[:], io.shift_s_in[:, :, :])
            fw.act(jk[:], xts[:], AF.Square, accum_out=nst[0:M, 0:1])
            fw.ts("dve", nst[0:M, 1:2], nst[0:M, 0:1], 1.0 / D, EPS, ALU.mult, ALU.add)
            fw.act(nst[0:M, 2:3], nst[0:M, 1:2], AF.Ln)
            fw.act(nst[0:M, 3:4], nst[0:M, 2:3], AF.Exp, scale=-0.5)
            fw.stt(tA[:], xts[:], nst[0:M, 3:4], gm1[0:M, :], ALU.mult, ALU.mult)
            fw.dma("sp", io.shift_s[:, :], tA[:])
            fw.cp("dve", xn_s[:], tA[:])
            for kt in range(8):
                fw.tr(PT3[:, kt, 0:M], xn_s[:, kt * 128:(kt + 1) * 128], identb[0:M, 0:M])
            fw.cp("dve", xsT[:], PT3[:, :, 0:M])
            fw.tt("dve", xxs[:], xprev[:], xsT[:], ALU.subtract)
            PAm = [PA[0:M, 0:512], PA[0:M, 512:1024]]
            PBm = [PB0[0:M, :], PB1[0:M, :]]
            PAf = PA[0:M, :]
            proj_tok(mix_s(0), Wr, PAh, M)
            fw.cp("act", r_s[:], PAf)
            proj_tok(mix_s(2), Wk, PAh, M)
            fw.tt("dve", tA[:], PAf, kkb_[0:M, :], ALU.mult)
            fw.tt("dve", jk[:], tA[:], tA[:], ALU.mult)
            fw.red(s16[:, 0:16], v16s(jk[:, :]), ALU.add)
            rstd16(s16[:, 0:16], s16[:, 16:32], None, None, floor=1e-24)
            fw.tt("dve", v16s(kk_s[:, :]), v16s(tA[:, :]), s16[:, 16:32].bc(2, 64), ALU.mult)
            lora(mix_s(4), A1, 64, [A2], AF.Copy, PBh, M)
            for i in range(2):
                fw.tt("dve", tB[:, i * 512:(i + 1) * 512], PBm[i], a0b[0:M, i * 512:(i + 1) * 512], ALU.add)
            fw.act(tB[:], tB[:], AF.Sigmoid)
            fw.stt(jk[:], tB[:], 1.0, kab[0:M, :], ALU.subtract, ALU.mult)
            fw.ts("dve", jk[:], jk[:], 1.0, None, ALU.add)
            fw.tt("dve", kf_s[:], PAf, jk[:], ALU.mult)
            fw.tt("dve", b_s[:], kk_s[:], tB[:], ALU.mult)
            fw.tt("dve", jk[:], r_s[:], kf_s[:], ALU.mult)
            fw.tt("dve", jk[:], jk[:], rkb[0:M, :], ALU.mult)
            fw.red(s16[:, 32:48], v16s(jk[:, :]), ALU.add)
            proj_tok(mix_s(3), Wv, PAh, M)
            fw.cp("act", v_s[:], PAf)
            fw.tt("dve", v16s(bv_s[:, :]), v16s(PAf), s16[:, 32:48].bc(2, 64), ALU.mult)
            lora(mix_s(5), G1, 160, [G2a, G2b], AF.Sigmoid, PBh, M)
            for i in range(2):
                fw.cp("act", g_s[:, i * 512:(i + 1) * 512], PBm[i])
            lora(mix_s(1), W1, 64, [W2], AF.Tanh, PAh, M)
            fw.tt("dve", tA[:], PAf, w0b[0:M, :], ALU.add)
            fw.act(tA[:], tA[:], AF.Exp, scale=-1.0)
            fw.act(tA[:], tA[:], AF.Ln, bias=1.0)
            fw.ts("dve", tA[:], tA[:], -1.0, -0.5, ALU.mult, ALU.add)
            fw.act(Es[:], tA[:], AF.Exp)
            fw.act(Es[:], Es[:], AF.Exp, scale=-1.0)
            fw.ts("dve", tB[:], kk_s[:], -1.0, None, ALU.mult)
            for nm, src in (("aT", tB), ("wT", Es)):
                for kt in range(8):
                    fw.tr(PE[:, kt * 16:(kt + 1) * 16], src[:, kt * 128:(kt + 1) * 128], ident[0:M, 0:M])
                fw.cp("dve", featT[nm][:], PE[:, 0:128].rearrange("p (k b) -> p k b", k=8))
            for nm, src in (("bTs", b_s), ("kTs", kf_s), ("rTs", r_s)):
                for kt in range(8):
                    fw.tr(PT3[:, kt, 0:M], src[:, kt * 128:(kt + 1) * 128], identb[0:M, 0:M])
                fw.cp("dve", featT[nm][:], PT3[:, :, 0:M])
            fw.tt("dve", amask[:], featT["aT"][:, :, :].bc(2, NB), eye16[:, :, :].bc(1, 8), ALU.mult)
            fw.tt("dve", rmask[:], featT["rTs"][:, :, :].bc(2, NB), eye16[:, :, :].bc(1, 8), ALU.mult)
            SY = [PC, PD]
            for b in range(NB):
                H = Hs[b % 2]
                fw.dma("sp", H[:], io.wkv_s_in[b])
                for h in range(16):
                    j, e = h // 2, h % 2
                    p0 = 64 * e
                    fw.mm(SY[e][0:M, j * 64:(j + 1) * 64], amask[p0:p0 + 64, j, b, :], H[p0:p0 + 64, j, :],
                          start=(b == 0 and j == 0), stop=(b == NB - 1))
            je = lambda r: r.rearrange("p (j e v) -> p j e v", j=8, e=2)
            fw.cp("dve", je(sa_tok[:, :])[:, :, 0, :], PC[0:M, :].rearrange("p (j v) -> p j v", j=8))
            fw.cp("dve", je(sa_tok[:, :])[:, :, 1, :], PD[0:M, :].rearrange("p (j v) -> p j v", j=8))
            e4 = lambda r, e: r.rearrange("p (j e v) -> p j e v", j=8, e=2)[64 * e:64 * e + 64, :, e, :]
            for b in range(NB):
                H = Hs[b % 2]
                fw.dma("sp", H[:], io.wkv_s_in[b])
                fw.mm(PA[:, 0:512], sel16[:, b, :], sa_tok[:, 0:512])
                fw.mm(PA[:, 512:1024], sel16[:, b, :], sa_tok[:, 512:1024])
                fw.mm(PB0[:, :], sel16[:, b, :], v_s[:, 0:512])
                fw.mm(PB1[:, :], sel16[:, b, :], v_s[:, 512:1024])
                fw.tt("pool", H[:], H[:], featT["wT"][:, :, b].bc(2, 64), ALU.mult)
                for e in range(2):
                    p0 = 64 * e
                    fw.tt("dve", Tt[p0:p0 + 64, :, :], e4(PA[:, :], e), featT["bTs"][p0:p0 + 64, :, b].bc(2, 64), ALU.mult)
                fw.tt("dve", H[:], H[:], Tt[:], ALU.add)
                for e in range(2):
                    p0 = 64 * e
                    for half, PBx in enumerate((PB0, PB1)):
                        src = PBx[:, :].rearrange("p (j e v) -> p j e v", j=4, e=2)[p0:p0 + 64, :, e, :]
                        fw.tt("dve", Tt[p0:p0 + 64, 4 * half:4 * half + 4, :], src,
                              featT["kTs"][p0:p0 + 64, 4 * half:4 * half + 4, b].bc(2, 64), ALU.mult)
                fw.tt("dve", H[:], H[:], Tt[:], ALU.add)
                fw.dma("sp", io.wkv_s[b], H[:])
                for h in range(16):
                    j, e = h // 2, h % 2
                    p0 = 64 * e
                    fw.mm(SY[e][0:M, j * 64:(j + 1) * 64], rmask[p0:p0 + 64, j, b, :], H[p0:p0 + 64, j, :],
                          start=(b == 0 and j == 0), stop=(b == NB - 1))
            fw.cp("dve", je(tA[:, :])[:, :, 0, :], PC[0:M, :].rearrange("p (j v) -> p j v", j=8))
            fw.cp("dve", je(tA[:, :])[:, :, 1, :], PD[0:M, :].rearrange("p (j v) -> p j v", j=8))
            fw.red(s16[:, 0:16], v16s(tA[:, :]), ALU.add)
            fw.ts("dve", s16[:, 0:16], s16[:, 0:16], 1.0 / 64, None, ALU.mult)
            fw.tt("dve", v16s(tA[:, :]), v16s(tA[:, :]), s16[:, 0:16].bc(2, 64), ALU.subtract)
            fw.tt("dve", jk[:], tA[:], tA[:], ALU.mult)
            fw.red(s16[:, 16:32], v16s(jk[:, :]), ALU.add)
            rstd16(s16[:, 16:32], s16[:, 48:64], 1.0 / 64, 64e-5)
            fw.tt("dve", v16s(tA[:, :]), v16s(tA[:, :]), s16[:, 48:64].bc(2, 64), ALU.mult)
            fw.tt("dve", tA[:], tA[:], lnw[0:M, :], ALU.mult)
            fw.tt("dve", tA[:], tA[:], lnb[0:M, :], ALU.add)
            fw.tt("dve", tA[:], tA[:], bv_s[:], ALU.add)
            fw.tt("dve", yo_s[:], tA[:], g_s[:], ALU.mult)
            for kt in range(8):
                fw.tr(PT3[:, kt, 0:M], yo_s[:, kt * 128:(kt + 1) * 128], identb[0:M, 0:M])
            fw.cp("dve", yoTs[:], PT3[:, :, 0:M])
            for half in range(2):
                for kt in range(8):
                    fw.mm(PA[0:M, half * 512:(half + 1) * 512], yoTs[:, kt, :], Wo[:, kt, half * 512:(half + 1) * 512],
                          start=kt == 0, stop=kt == 7)
            fw.tt("dve", tB[:], xts[:], PA[0:M, :], ALU.add)
            fw.dma("sp", io.s3s[:, :], tB[:])
        fw.release(base_mark)

    if "3" in phases:
        ffn_phase(1, io.s3, io.y_p, io.s3s, io.y_s, True)

    fw.finish()
    fw.close()
    return nc


def prep_common(inp):
    f = lambda k: np.ascontiguousarray(np.asarray(inp[k], np.float32))
    m = {}
    m["cst"] = host_consts()
    m["w_in0"] = f("w_in0")[0]
    m["w_out0"] = f("w_out0")[0]
    m["norm_mix"] = f("norm_mix")
    m["norm_ffn"] = f("norm_ffn")
    m["norm_final"] = f("norm_final")
    m["ssd_norm"] = f("ssd_norm")[0]
    small0 = np.zeros(64, np.float32)
    small0[0:16] = f("ssd_dt_bias")[0]
    small0[16:32] = f("ssd_a_log")[0]
    small0[32:48] = f("ssd_d")[0]
    small0[48:52] = f("ml_i_bias")[0]
    small0[52:56] = f("ml_f_bias")[0]
    m["small0"] = small0
    cw = f("conv_w")[0].reshape(4, 20, 128).transpose(2, 1, 0)
    cb = f("conv_b")[0].reshape(20, 128).T[:, :, None]
    m["convp"] = np.ascontiguousarray(np.concatenate([cw, cb], axis=2))
    m["mlcol"] = np.ascontiguousarray(np.stack([f("ml_norm")[0].reshape(8, 128).T, f("ml_skip")[0].reshape(8, 128).T], axis=2))
    m["bdq"] = blockdiag(f("ml_wq")[0])
    m["bdk"] = blockdiag(f("ml_wk")[0])
    m["bdv"] = blockdiag(f("ml_wv")[0])
    for nm in ("rw_wr", "rw_wk", "rw_wv", "rw_wo", "rw_w1", "rw_w2", "rw_a1", "rw_a2", "rw_g1", "rw_g2"):
        m[nm] = f(nm)[0]
    m["rw_rows"] = np.ascontiguousarray(np.stack([f(k)[0] for k in ("rw_w0", "rw_a0", "rw_k_k", "rw_k_a", "rw_r_k", "rw_ln_w", "rw_ln_b")]))
    m["rw_mu"] = np.ascontiguousarray(f("rw_mu")[0].reshape(6, 8, 128).transpose(2, 1, 0))
    m["w_gu"] = f("ffn_w_gate_up")
    m["w_dn"] = f("ffn_w_down")
    return m


def prep_core(inp, core):
    f = lambda k: np.asarray(inp[k], np.float32)
    b0 = core * NB
    m = {}
    m["xp"] = np.ascontiguousarray(f("x_prompt")[core])
    m["xs"] = np.ascontiguousarray(f("x_sample")[b0:b0 + NB, 0, :])
    m["conv_s_in"] = np.ascontiguousarray(f("state_conv")[0, b0:b0 + NB].reshape(NB, 3, 20, 128).transpose(3, 2, 1, 0))
    m["ssm_s_in"] = np.ascontiguousarray(f("state_ssm")[0, b0:b0 + NB].reshape(NB, D, 128))
    m["mc_s_in"] = np.ascontiguousarray(f("state_mlstm_c")[0, b0:b0 + NB])
    m["mn_s_in"] = np.ascontiguousarray(f("state_mlstm_n")[0, b0:b0 + NB].reshape(NB, 4, 2, 128).transpose(3, 1, 2, 0).reshape(128, 8, NB))
    m["mm_s_in"] = np.ascontiguousarray(f("state_mlstm_m")[0, b0:b0 + NB])
    m["shift_s_in"] = np.ascontiguousarray(f("state_shift")[0, b0:b0 + NB].reshape(NB, 8, 128).transpose(2, 1, 0))
    m["wkv_s_in"] = np.ascontiguousarray(f("state_wkv")[0, b0:b0 + NB].reshape(NB, 8, 2, 64, 64).transpose(0, 2, 4, 1, 3).reshape(NB, 128, 8, 64))
    return m


def prep_consts(inp):
    f = lambda k: np.asarray(inp[k], np.float32)
    m = prep_common(inp)
    m["c16"] = host_consts16()
    m["eye16"] = np.ascontiguousarray(np.broadcast_to(np.eye(16, dtype=np.float32), (128, 16, 16)))
    dtcol = np.zeros((16, 4), np.float32)
    dtcol[:, 0] = f("ssd_dt_bias")[0]
    dtcol[:, 1] = f("ssd_a_log")[0]
    dtcol[:, 2] = f("ssd_d")[0]
    m["dtcol"] = dtcol
    return m


_NC_CACHE = {}


def kernel(**inp):
    if "nc" not in _NC_CACHE:
        _NC_CACHE["nc"] = build({})
    nc = _NC_CACHE["nc"]
    cm = prep_consts(inp)
    in_maps = [dict(cm, **prep_core(inp, c)) for c in range(NCORE)]
    res = run_bass_kernel_spmd(nc, in_maps, core_ids=list(range(NCORE)))
    R = res.results
    BT = NCORE * NB
    y_p = np.zeros((NCORE, T, D), np.float32)
    y_s = np.zeros((BT, 1, D), np.float32)
    conv_p = np.zeros((1, NCORE, 3, 2560), np.float32)
    conv_s = np.zeros((1, BT, 3, 2560), np.float32)
    ssm_p = np.zeros((1, NCORE, 16, 64, 128), np.float32)
    ssm_s = np.zeros((1, BT, 16, 64, 128), np.float32)
    mc_p = np.zeros((1, NCORE, 4, 256, 256), np.float32)
    mc_s = np.zeros((1, BT, 4, 256, 256), np.float32)
    mn_p = np.zeros((1, NCORE, 4, 256), np.float32)
    mn_s = np.zeros((1, BT, 4, 256), np.float32)
    mm_p = np.zeros((1, NCORE, 4), np.float32)
    mm_s = np.zeros((1, BT, 4), np.float32)
    sh_p = np.zeros((1, NCORE, D), np.float32)
    sh_s = np.zeros((1, BT, D), np.float32)
    wkv_p = np.zeros((1, NCORE, 16, 64, 64), np.float32)
    wkv_s = np.zeros((1, BT, 16, 64, 64), np.float32)
    for c in range(NCORE):
        r = R[c]
        sl = slice(c * NB, (c + 1) * NB)
        y_p[c] = r["y_p"]
        y_s[sl, 0] = r["y_s"]
        conv_p[0, c] = r["conv_p"].transpose(2, 1, 0).reshape(3, 2560)
        conv_s[0, sl] = r["conv_s"].transpose(3, 2, 1, 0).reshape(NB, 3, 2560)
        ssm_p[0, c] = r["ssm_p"].reshape(128, 16, 64).transpose(1, 2, 0)
        ssm_s[0, sl] = r["ssm_s"].reshape(NB, 16, 64, 128)
        mc = r["mc_p"]
        mc_p[0, c] = mc[:, :, :, :256].transpose(2, 1, 0, 3).reshape(4, 256, 256)
        mn_p[0, c] = mc[:, :, :, 256].transpose(2, 1, 0).reshape(4, 256)
        mc_s[0, sl] = r["mc_s"]
        mn_s[0, sl] = r["mn_s"].reshape(128, 4, 2, NB).transpose(3, 1, 2, 0).reshape(NB, 4, 256)
        mm_p[0, c] = r["mm_p"][0]
        mm_s[0, sl] = r["mm_s"]
        sh_p[0, c] = r["shift_p"][0]
        sh_s[0, sl] = r["shift_s"]
        wkv_p[0, c] = r["wkv_p"].reshape(2, 64, 8, 64).transpose(2, 0, 3, 1).reshape(16, 64, 64)
        wkv_s[0, sl] = r["wkv_s"].reshape(NB, 2, 64, 8, 64).transpose(0, 3, 1, 4, 2).reshape(NB, 16, 64, 64)
    return (y_p, y_s, conv_p, conv_s, ssm_p, ssm_s, mc_p, mc_s, mn_p, mn_s, mm_p, mm_s, sh_p, sh_s, wkv_p, wkv_s)
```

```python
import numpy as np
import concourse.bass as bass
import concourse.mybir as mybir
from concourse.bass_utils import run_bass_kernel_spmd

F32 = mybir.dt.float32
BF16 = mybir.dt.bfloat16
ALU = mybir.AluOpType
AF = mybir.ActivationFunctionType
AX = mybir.AxisListType

NCORE = 8
D = 1024
T = 2048
NB = 16
IN0 = 4632
DFF = 2816
EPS = 1e-5


class Tok:
    __slots__ = ("sem", "val", "eng", "seq")

    def __init__(self, sem, val, eng, seq=None):
        self.sem, self.val, self.eng, self.seq = sem, val, eng, seq


class Ref:
    __slots__ = ("T", "ap")

    def __init__(self, T_, ap):
        self.T, self.ap = T_, ap

    def __getitem__(self, k):
        return Ref(self.T, self.ap[k])

    def rearrange(self, p, **kw):
        return Ref(self.T, self.ap.rearrange(p, **kw))

    def unsqueeze(self, a):
        return Ref(self.T, self.ap.unsqueeze(a))

    def to_broadcast(self, shp):
        return Ref(self.T, self.ap.to_broadcast(list(shp)))

    def bc(self, axis, n):
        ap = self.ap.unsqueeze(axis)
        shp = list(ap.shape)
        shp[axis] = n
        return Ref(self.T, ap.to_broadcast(shp))


class TT:
    __slots__ = ("t", "name", "lw", "rd", "psum")

    def __init__(self, t, name, psum=False):
        self.t, self.name, self.lw, self.rd, self.psum = t, name, None, [], psum

    def __getitem__(self, k):
        return Ref(self, self.t[k])


def _Ts(*xs):
    return [x.T for x in xs if isinstance(x, Ref)]


def _a(x):
    return x.ap if isinstance(x, Ref) else x


class Eng:
    def __init__(self, fw, name, h):
        self.fw, self.name, self.h = fw, name, h
        self.sems, self.n, self.waited, self.nsig = [], 0, {}, 0


class Fw:
    EPOCH = 30000
    NDMA = 10

    def __init__(self, nc, need=None):
        self.nc = nc
        self.need = need
        self.waited_on = set()
        self._ctx = []
        self.E = {}
        for name, h in (("pe", nc.tensor), ("dve", nc.vector), ("act", nc.scalar),
                        ("pool", nc.gpsimd), ("sp", nc.sync)):
            self.E[name] = Eng(self, name, h)
        self.dma_sems, self.dma_i = {}, {}
        self.ntile = 0
        self.sb_bytes = 0

    def enter(self, cm):
        v = cm.__enter__()
        self._ctx.append(cm)
        return v

    def close(self):
        for cm in reversed(self._ctx):
            cm.__exit__(None, None, None)
        self._ctx = []

    def new_sem(self, name):
        return self.enter(self.nc.semaphore(name))

    def presem(self, queues=("sp", "pool", "act"), epochs=3):
        for e in self.E.values():
            while len(e.sems) < epochs:
                e.sems.append(self.new_sem(f"e_{e.name}_{len(e.sems)}"))
        for q in queues:
            self.dma_sems[q] = [[self.new_sem(f"d_{q}_{i}"), 0] for i in range(Fw.NDMA)]
            self.dma_i[q] = 0

    def mark(self):
        return len(self._ctx)

    def release(self, mark):
        self.barrier()
        while len(self._ctx) > mark:
            self._ctx.pop().__exit__(None, None, None)

    def _last_tok(self, e):
        return e.last

    def barrier(self):
        for eng in self.E.values():
            for q, slots in self.dma_sems.items():
                for sem, cnt in slots:
                    if cnt > 0:
                        self._wait(eng, Tok(sem, cnt, "dma"))
            for name, e in self.E.items():
                if e is eng or e.n == 0:
                    continue
                self._wait(eng, self._last_tok(e))

    def sb(self, shape, dt=F32, name="t"):
        self.ntile += 1
        n = 1
        for s in shape[1:]:
            n *= s
        self.sb_bytes += n * (2 if dt == BF16 else 4)
        return TT(self.enter(self.nc.sbuf_tensor(f"{name}_{self.ntile}", list(shape), dt)), name)

    def ps(self, shape, dt=F32, name="p"):
        self.ntile += 1
        return TT(self.enter(self.nc.psum_tensor(f"{name}_{self.ntile}", list(shape), dt)), name, psum=True)

    def view(self, ref, name="v"):
        return TT(ref.ap, name, psum=ref.T.psum)

    def _wait(self, eng, tok):
        if tok is None:
            return
        key = id(tok.sem)
        if eng.waited.get(key, 0) >= tok.val:
            return
        if tok.seq is not None:
            self.waited_on.add((tok.eng, tok.seq))
            assert tok.val == int(tok.val), "wait on a non-signalling instruction (two-pass mismatch)"
        eng.h.wait_ge(tok.sem, int(tok.val))
        eng.waited[key] = tok.val

    def _deps(self, eng, reads, writes):
        for t in reads:
            if t.lw is not None:
                self._wait(eng, t.lw)
            if t.psum:
                for r in t.rd:
                    if r.eng != eng.name:
                        self._wait(eng, r)
        strict = eng.name != "pe"
        for t in writes:
            if t.lw is not None and (strict or t.lw.eng != eng.name):
                self._wait(eng, t.lw)
            for r in t.rd:
                if strict or r.eng != eng.name:
                    self._wait(eng, r)

    def _mark(self, tok, reads, writes):
        for t in reads:
            t.rd.append(tok)
        for t in writes:
            t.lw = tok
            t.rd = []

    def op(self, e, fn, reads=(), writes=()):
        eng = self.E[e]
        self._deps(eng, reads, writes)
        seq = eng.n
        eng.n += 1
        signal = self.need is None or (eng.name, seq) in self.need
        ep = eng.nsig // Fw.EPOCH
        while len(eng.sems) <= ep:
            eng.sems.append(self.new_sem(f"e_{eng.name}_{len(eng.sems)}"))
        sem = eng.sems[ep]
        inst = fn(eng.h)
        if signal:
            val = eng.nsig % Fw.EPOCH + 1
            eng.nsig += 1
            inst.then_inc(sem, 1)
        else:
            val = eng.nsig % Fw.EPOCH + 0.5
        tok = Tok(sem, val, eng.name, seq)
        eng.last = tok
        self._mark(tok, reads, writes)
        return tok

    def dma(self, q, out, in_, **kw):
        eng = self.E[q]
        if q not in self.dma_sems:
            self.dma_sems[q] = [[self.new_sem(f"d_{q}_{i}"), 0] for i in range(Fw.NDMA)]
            self.dma_i[q] = 0
        slot = self.dma_sems[q][self.dma_i[q] % Fw.NDMA]
        self.dma_i[q] += 1
        sem, cnt = slot
        if cnt > 0:
            self._wait(eng, Tok(sem, cnt, "dma"))
        reads, writes = _Ts(in_), _Ts(out)
        self._deps(eng, reads, writes)
        inst = eng.h.dma_start(out=_a(out), in_=_a(in_), **kw)
        slot[1] = cnt + 16
        inst.then_inc(sem, 16)
        tok = Tok(sem, cnt + 16, "dma")
        self._mark(tok, reads, writes)
        return tok

    def finish(self):
        eng = self.E["sp"]
        for q, slots in self.dma_sems.items():
            for sem, cnt in slots:
                if cnt > 0:
                    self._wait(eng, Tok(sem, cnt, "dma"))
        for name, e in self.E.items():
            if name == "sp" or e.n == 0:
                continue
            self._wait(eng, self._last_tok(e))

    def mm(self, out, lhsT, rhs, start=True, stop=True, skip=False):
        kw = {"skip_group_check": True} if skip else {}
        return self.op("pe", lambda e: e.matmul(_a(out), _a(lhsT), _a(rhs), start=start, stop=stop, **kw),
                       _Ts(lhsT, rhs), _Ts(out))

    def tr(self, out, in_, ident):
        return self.op("pe", lambda e: e.transpose(_a(out), _a(in_), _a(ident)), _Ts(in_, ident), _Ts(out))

    def act(self, out, in_, func, bias=None, scale=None, accum_out=None):
        kw = {}
        if bias is not None:
            kw["bias"] = _a(bias)
        if scale is not None:
            kw["scale"] = _a(scale)
        if accum_out is not None:
            kw["accum_out"] = _a(accum_out)
        return self.op("act", lambda e: e.activation(out=_a(out), in_=_a(in_), func=func, **kw),
                       _Ts(in_, bias, scale), _Ts(out, accum_out))

    def tt(self, e, out, in0, in1, op):
        return self.op(e, lambda h: h.tensor_tensor(out=_a(out), in0=_a(in0), in1=_a(in1), op=op),
                       _Ts(in0, in1), _Ts(out))

    def ts(self, e, out, in0, s1, s2, op0, op1=None, accum_out=None):
        kw = {}
        if op1 is not None:
            kw["op1"] = op1
        if accum_out is not None:
            kw["accum_out"] = _a(accum_out)
        return self.op(e, lambda h: h.tensor_scalar(out=_a(out), in0=_a(in0), scalar1=_a(s1), scalar2=_a(s2),
                                                    op0=op0, **kw),
                       _Ts(in0, s1, s2), _Ts(out, accum_out))

    def stt(self, out, in0, scalar, in1, op0, op1, accum_out=None):
        kw = {}
        if accum_out is not None:
            kw["accum_out"] = _a(accum_out)
        return self.op("dve", lambda h: h.scalar_tensor_tensor(out=_a(out), in0=_a(in0), scalar=_a(scalar),
                                                               in1=_a(in1), op0=op0, op1=op1, **kw),
                       _Ts(in0, scalar, in1), _Ts(out, accum_out))

    def cp(self, e, out, in_):
        if e == "act":
            return self.act(out, in_, AF.Copy)
        return self.op(e, lambda h: h.tensor_copy(out=_a(out), in_=_a(in_)), _Ts(in_), _Ts(out))

    def red(self, out, in_, op, axis=AX.X):
        return self.op("dve", lambda h: h.tensor_reduce(out=_a(out), in_=_a(in_), axis=axis, op=op),
                       _Ts(in_), _Ts(out))

    def recip(self, out, in_):
        return self.op("dve", lambda h: h.reciprocal(out=_a(out), in_=_a(in_)), _Ts(in_), _Ts(out))

    def memset(self, e, out, val):
        return self.op(e, lambda h: h.memset(_a(out), val), [], _Ts(out))


def host_consts():
    j = np.arange(128)
    c = np.zeros((128, 10, 128), np.float32)
    c[:, 0, :] = (j[:, None] == j[None, :])
    c[:, 1, :] = (j[:, None] <= j[None, :])
    c[:, 2, :] = (j[:, None] > j[None, :])
    c[:, 3, :] = np.where(j[None, :] <= j[:, None], 0.0, -30000.0)
    c[:, 4, :] = 1.0
    c[:, 5, :] = (j[:, None] == 127)
    c[:, 6, :] = (j[:, None] < j[None, :])
    c[:, 7, :] = c[:, 1, :]
    c[:, 8, :] = c[:, 6, :]
    c[:, 9, :] = c[:, 1, :]
    return c


def blockdiag(w):
    out = np.zeros((8, 128, 128), np.float32)
    w = w.reshape(8, 32, 4, 4)
    for nl in range(32):
        out[:, nl * 4:(nl + 1) * 4, nl * 4:(nl + 1) * 4] = w[:, nl]
    return np.ascontiguousarray(out.transpose(1, 0, 2))


def host_consts16():
    h = np.arange(16)
    q = np.arange(128)
    j = np.arange(8)
    e = (h[:, None, None] == (2 * j[None, :, None] + q[None, None, :] // 64)).astype(np.float32)
    sel = np.broadcast_to((h[:, None, None] == h[None, :, None]), (16, 16, 128)).astype(np.float32)
    return np.ascontiguousarray(np.concatenate([e.reshape(16, -1), sel.reshape(16, -1)], axis=1))


class IO:
    pass


def build(cfg):
    _, fw1 = _build(cfg, None)
    nc, fw2 = _build(cfg, fw1.waited_on)
    return nc


def _build(cfg, need):
    nc = bass.Bass("TRN2", target_bir_lowering=False)
    fw = Fw(nc, need)
    io = IO()
    NCH = cfg.get("nch", 16)
    dbg = cfg.get("dbg", ())
    phases = cfg.get("phases", ("0a", "0b", "1", "2", "3"))

    def din(name, shape):
        return nc.dram_tensor(name, list(shape), F32, kind="ExternalInput").ap()

    def dout(name, shape):
        return nc.dram_tensor(name, list(shape), F32, kind="ExternalOutput").ap()

    def dscr(name, shape):
        if name in dbg:
            return dout(name, shape)
        return nc.dram_tensor(name, list(shape), F32).ap()

    io.xp = din("xp", [T, D])
    io.cst = din("cst", [128, 10, 128])
    io.w_in0 = din("w_in0", [D, IN0])
    io.w_out0 = din("w_out0", [2 * D, D])
    io.norm_mix = din("norm_mix", [2, D])
    io.norm_ffn = din("norm_ffn", [2, D])
    io.norm_final = din("norm_final", [D])
    io.ssd_norm = din("ssd_norm", [D])
    io.small0 = din("small0", [64])
    io.convp = din("convp", [128, 20, 5])
    io.mlcol = din("mlcol", [128, 8, 2])
    io.bdq = din("bdq", [128, 8, 128])
    io.bdk = din("bdk", [128, 8, 128])
    io.bdv = din("bdv", [128, 8, 128])
    io.w_gu = din("w_gu", [2, D, 2 * DFF])
    io.w_dn = din("w_dn", [2, DFF, D])
    for nm in ("rw_wr", "rw_wk", "rw_wv", "rw_wo"):
        setattr(io, nm, din(nm, [D, D]))
    io.rw_w1 = din("rw_w1", [D, 64]); io.rw_w2 = din("rw_w2", [64, D])
    io.rw_a1 = din("rw_a1", [D, 64]); io.rw_a2 = din("rw_a2", [64, D])
    io.rw_g1 = din("rw_g1", [D, 160]); io.rw_g2 = din("rw_g2", [160, D])
    io.rw_rows = din("rw_rows", [7, D])
    io.rw_mu = din("rw_mu", [128, 8, 6])
    io.wkv_p = dout("wkv_p", [128, 8, 64])
    io.shift_p = dout("shift_p", [1, D])
    io.xs = din("xs", [NB, D])
    io.c16 = din("c16", [16, 8 * 128 + 16 * 128])
    io.eye16 = din("eye16", [128, 16, 16])
    io.dtcol = din("dtcol", [16, 4])
    io.conv_s_in = din("conv_s_in", [128, 20, 3, NB])
    io.ssm_s_in = din("ssm_s_in", [NB, D, 128])
    io.mc_s_in = din("mc_s_in", [NB, 4, 256, 256])
    io.mn_s_in = din("mn_s_in", [128, 8, NB])
    io.mm_s_in = din("mm_s_in", [NB, 4])
    io.shift_s_in = din("shift_s_in", [128, 8, NB])
    io.wkv_s_in = din("wkv_s_in", [NB, 128, 8, 64])
    io.y_s = dout("y_s", [NB, D])
    io.conv_s = dout("conv_s", [128, 20, 3, NB])
    io.ssm_s = dout("ssm_s", [NB, D, 128])
    io.mc_s = dout("mc_s", [NB, 4, 256, 256])
    io.mn_s = dout("mn_s", [128, 8, NB])
    io.mm_s = dout("mm_s", [NB, 4])
    io.shift_s = dout("shift_s", [NB, D])
    io.wkv_s = dout("wkv_s", [NB, 128, 8, 64])
    io.s1s = dscr("s1s", [NB, D])
    io.s2s = dscr("s2s", [NB, D])
    io.s3s = dscr("s3s", [NB, D])
    if "dbg_a" in dbg:
        io.dbg_a = dout("dbg_a", [NB, D]); io.dbg_b = dout("dbg_b", [NB, 64])
    io.s1 = dscr("s1", [T, D])
    io.s2 = dscr("s2", [T, D])
    io.s3 = dscr("s3", [T, D])
    io.y_p = dout("y_p", [T, D])
    io.ssm_p = dout("ssm_p", [128, D])
    io.mc_p = dout("mc_p", [128, 2, 4, 264])
    io.mm_p = dout("mm_p", [1, 4])
    io.conv_p = dout("conv_p", [128, 20, 3])

    fw.presem(epochs=5)

    cst = fw.sb([128, 10, 128], F32, "cst")
    fw.dma("sp", cst[:], io.cst[:, :, :])
    ident, tri_le, mask_gt, negmask, ones = (cst[:, i, :] for i in range(5))
    sel127 = cst[:, 5, :]
    m4 = cst[:, 6:10, :].rearrange("p a t -> p (a t)")
    identb = fw.sb([128, 128], BF16, "identb")
    fw.cp("dve", identb[:], ident)
    onesb = fw.sb([128, 128], BF16, "onesb")
    fw.cp("dve", onesb[:], ones)
    nst = fw.sb([128, 8], F32, "nst")
    c16 = fw.sb([16, 8 * 128 + 16 * 128], F32, "c16")
    fw.dma("sp", c16[:], io.c16[:, :])
    exp16 = c16[:, 0:1024].rearrange("p (j q) -> p j q", j=8)
    sel16 = c16[:, 1024:3072].rearrange("p (b q) -> p b q", b=16)
    eye16 = fw.sb([128, 16, 16], F32, "eye16")
    fw.dma("sp", eye16[:], io.eye16[:, :, :])
    SAMPLE = cfg.get("sample", True)

    PA = fw.ps([128, 1024], F32, "PA")
    PB = fw.ps([128, 1024], F32, "PB")
    PC = fw.ps([128, 512], F32, "PC")
    PD = fw.ps([128, 512], F32, "PD")
    PE = fw.ps([128, 512], F32, "PE")
    PT = fw.ps([128, 1024], BF16, "PT")
    pcd = [PC, PD]
    PB0f = fw.view(PB[:, 0:512], "PB0f")
    PB1f = fw.view(PB[:, 512:1024], "PB1f")
    PT3 = PT[:, :].rearrange("p (k m) -> p k m", k=8)
    v16 = lambda r: r.rearrange("p (h q) -> p h q", h=16)

    def load_w(dst, src, kt0, kt1, q="pool", step=2):
        for k in range(kt0, kt1, step):
            k1 = min(k + step, kt1)
            fw.dma(q, dst[:, k:k1, :], src[k * 128:k1 * 128, :].rearrange("(k p) n -> p k n", p=128))

    def rmsnorm(x, g, out, M, junk):
        fw.act(junk[0:M, :], x, AF.Square, accum_out=nst[0:M, 0:1])
        fw.ts("dve", nst[0:M, 1:2], nst[0:M, 0:1], 1.0 / D, EPS, ALU.mult, ALU.add)
        fw.act(nst[0:M, 2:3], nst[0:M, 1:2], AF.Ln)
        fw.act(nst[0:M, 3:4], nst[0:M, 2:3], AF.Exp, scale=-0.5)
        fw.stt(out, x, nst[0:M, 3:4], g[0:M, :], ALU.mult, ALU.mult)

    def to_feat(src, dst, M):
        for kt in range(8):
            fw.tr(PT3[:, kt, 0:M], src[0:M, kt * 128:(kt + 1) * 128], identb[0:M, 0:M])
        fw.cp("dve", dst[:, :, 0:M], PT3[:, :, 0:M])

    def grp_rstd(src, ncol, dst, junk, M=128):
        fw.act(junk[0:M, 0:ncol], src, AF.Square, accum_out=nst[0:M, 4:5])
        fw.ts("dve", nst[0:M, 5:6], nst[0:M, 4:5], 1.0 / ncol, EPS, ALU.mult, ALU.add)
        fw.act(nst[0:M, 6:7], nst[0:M, 5:6], AF.Ln)
        fw.act(dst, nst[0:M, 6:7], AF.Exp, scale=-0.5)

    def proj_feat(W, col0, ntile, xT, M, evac):
        for gi, g0 in enumerate(range(0, ntile, 4)):
            n = min(4, ntile - g0)
            ps3 = pcd[gi % 2][:, :].rearrange("p (a m) -> p a m", a=4)
            for i in range(n):
                col = col0 + (g0 + i) * 128
                for kt in range(8):
                    fw.mm(ps3[:, i, 0:M], W[:, kt, col:col + 128], xT[:, kt, 0:M], start=kt == 0, stop=kt == 7)
            evac(g0, n, ps3[:, 0:n, 0:M])

    def conv_tiles(convin, convp, acc, ct0, n):
        for i in range(n):
            ct = ct0 + i
            fw.act(acc[:, i, :], convin[:, i, 0:128], AF.Identity, scale=convp[:, ct, 0:1], bias=convp[:, ct, 4:5])
            for j in range(1, 4):
                fw.stt(acc[:, i, :], convin[:, i, j:j + 128], convp[:, ct, j:j + 1], acc[:, i, :], ALU.mult, ALU.add)

    base_mark = fw.mark()

    if "0a" in phases:
        Wc = fw.sb([128, 8, 1536], BF16, "Wc")
        load_w(Wc, io.w_in0[:, 1024:2560], 0, 8)
        Wz = fw.sb([128, 8, 1024], BF16, "Wz")
        load_w(Wz, io.w_in0[:, 0:1024], 0, 8)
        Wdt = fw.sb([128, 8, 16], BF16, "Wdt")
        load_w(Wdt, io.w_in0[:, 3584:3600], 0, 8, step=8)
        Wo = fw.sb([128, 8, D], BF16, "Wo")
        load_w(Wo, io.w_out0[0:1024, :], 0, 8)
        gmix = fw.sb([128, D], F32, "gmix")
        fw.dma("sp", gmix[:], io.norm_mix[0, :].partition_broadcast(128))
        gssd = fw.sb([128, D], F32, "gssd")
        fw.dma("sp", gssd[:], io.ssd_norm.partition_broadcast(128))
        sm0 = fw.sb([128, 64], F32, "sm0")
        fw.dma("sp", sm0[:], io.small0.partition_broadcast(128))
        dtb_bc, D_bc = sm0[:, 0:16], sm0[:, 32:48]
        A_t = fw.sb([128, 16], F32, "A_t")
        fw.act(A_t[:], sm0[:, 16:32], AF.Exp)
        fw.ts("dve", A_t[:], A_t[:], -1.0, None, ALU.mult)
        convp = fw.sb([128, 20, 5], F32, "convp")
        fw.dma("sp", convp[:], io.convp[:, :, :])
        convin = fw.sb([128, 12, 131], F32, "convin")
        fw.memset("pool", convin[:], 0.0)
        ST = fw.sb([128, D], F32, "ST")
        fw.memset("pool", ST[:], 0.0)
        STb = fw.sb([128, D], BF16, "STb")
        fw.memset("pool", STb[:], 0.0)
        xt = fw.sb([128, D], F32, "xt")
        junk = fw.sb([128, D], F32, "junk")
        xn = fw.sb([128, D], BF16, "xn")
        xnT = fw.sb([128, 8, 128], BF16, "xnT")
        acc = fw.sb([128, 12, 128], F32, "acc")
        cact = fw.sb([128, 12, 128], BF16, "cact")
        zs = fw.sb([128, D], F32, "zs")
        xtok = fw.sb([128, D], BF16, "xtok")
        Btok = fw.sb([128, 256], BF16, "Btok")
        sm = fw.sb([128, 128], F32, "sm")
        Lh = [fw.sb([128, 4, 128], F32, f"Lh{i}") for i in range(2)]
        Eh = fw.sb([128, 4, 128], F32, "Eh")
        CBm = fw.sb([128, 2, 128], F32, "CBm")
        Wt = fw.sb([128, 16, 128], BF16, "Wt")
        t1 = fw.sb([128, D], F32, "t1")
        yn = fw.sb([128, D], BF16, "yn")
        ynT = fw.sb([128, 8, 128], BF16, "ynT")
        xw = fw.sb([128, D], BF16, "xw")
        x1 = fw.sb([128, D], F32, "x1")

        for c in range(NCH):
            fw.dma("sp", xt[:], io.xp[c * 128:(c + 1) * 128, :])
            rmsnorm(xt[:], gmix, xn[:], 128, junk)
            to_feat(xn, xnT, 128)
            proj_feat(Wc, 0, 12, xnT, 128, lambda g0, n, ps: fw.cp("act", convin[:, g0:g0 + n, 3:131], ps))
            for half in range(2):
                for kt in range(8):
                    fw.mm(PA[:, half * 512:(half + 1) * 512], xnT[:, kt, :], Wz[:, kt, half * 512:(half + 1) * 512],
                          start=kt == 0, stop=kt == 7)
            fw.act(zs[:], PA[:, :], AF.Silu)
            for kt in range(8):
                fw.mm(PE[:, 0:16], xnT[:, kt, :], Wdt[:, kt, :], start=kt == 0, stop=kt == 7)
            fw.tt("dve", sm[:, 0:16], PE[:, 0:16], dtb_bc, ALU.add)
            conv_tiles(convin, convp, acc, 0, 12)
            fw.act(cact[:], acc[:], AF.Silu)
            fw.cp("pool", convin[:, :, 0:3], convin[:, :, 128:131])
            for kt in range(8):
                fw.tr(PT3[:, kt, :], cact[:, kt, :], identb[:, :])
            fw.cp("dve", xtok[:], PT[:, :])
            for g in range(2):
                fw.tr(PT[:, g * 128:(g + 1) * 128], cact[:, 8 + g, :], identb[:, :])
            fw.cp("dve", Btok[:], PT[:, 0:256])
            fw.act(sm[:, 0:16], sm[:, 0:16], AF.Exp)
            fw.act(sm[:, 0:16], sm[:, 0:16], AF.Ln, bias=1.0)
            fw.tt("dve", sm[:, 16:32], sm[:, 0:16], A_t[:], ALU.mult)
            fw.mm(PE[:, 32:48], tri_le, sm[:, 16:32])
            fw.mm(PE[:, 48:64], ones, sm[:, 16:32])
            fw.act(sm[:, 32:48], PE[:, 32:48], AF.Exp)
            fw.cp("dve", sm[:, 64:80], PE[:, 32:48])
            fw.tt("dve", sm[:, 48:64], PE[:, 48:64], sm[:, 64:80], ALU.subtract)
            fw.act(sm[:, 48:64], sm[:, 48:64], AF.Exp)
            fw.tt("dve", sm[:, 48:64], sm[:, 48:64], sm[:, 0:16], ALU.mult)
            fw.act(sm[:, 80:96], PE[:, 48:64], AF.Exp)
            for g in range(2):
                fw.mm(PE[:, 128 + g * 128:256 + g * 128], cact[:, 8 + g, :], cact[:, 10 + g, :])
                fw.tt("dve", CBm[:, g, :], PE[:, 128 + g * 128:256 + g * 128], tri_le, ALU.mult)
            for hq in range(4):
                L = Lh[hq % 2]
                ps3 = pcd[hq % 2][:, :].rearrange("p (a m) -> p a m", a=4)
                for i in range(4):
                    h = hq * 4 + i
                    fw.ts("dve", L[:, i, :], mask_gt, sm[:, 16 + h:17 + h], None, ALU.mult)
                    fw.mm(ps3[:, i, :], L[:, i, :], tri_le)
                fw.act(Eh[:], ps3, AF.Exp)
                for i in range(4):
                    h = hq * 4 + i
                    fw.stt(Wt[:, h, :], Eh[:, i, :], sm[:, h:h + 1], CBm[:, h // 8, :], ALU.mult, ALU.mult)
            for h in range(16):
                fw.mm(PA[:, h * 64:(h + 1) * 64], Wt[:, h, :], xtok[:, h * 64:(h + 1) * 64])
            for g in range(2):
                fw.mm(PB[:, g * 512:(g + 1) * 512], cact[:, 10 + g, :], STb[:, g * 512:(g + 1) * 512])
            fw.tt("dve", v16(t1[:, :]), v16(PB[:, :]), sm[:, 32:48].bc(2, 64), ALU.mult)
            fw.tt("dve", t1[:], t1[:], PA[:, :], ALU.add)
            fw.tt("pool", v16(junk[:, :]), v16(xtok[:, :]), D_bc.bc(2, 64), ALU.mult)
            fw.tt("dve", t1[:], t1[:], junk[:], ALU.add)
            fw.tt("dve", t1[:], t1[:], zs[:], ALU.mult)
            for g in range(2):
                grp_rstd(t1[:, g * 512:(g + 1) * 512], 512, nst[:, 7:8], junk)
                fw.stt(yn[:, g * 512:(g + 1) * 512], t1[:, g * 512:(g + 1) * 512], nst[:, 7:8],
                       gssd[:, g * 512:(g + 1) * 512], ALU.mult, ALU.mult)
            to_feat(yn, ynT, 128)
            fw.tt("pool", v16(xw[:, :]), v16(xtok[:, :]), sm[:, 48:64].bc(2, 64), ALU.mult)
            for g in range(2):
                fw.mm(PB[:, g * 512:(g + 1) * 512], Btok[:, g * 128:(g + 1) * 128], xw[:, g * 512:(g + 1) * 512])
            fw.tt("dve", v16(ST[:, :]), v16(ST[:, :]), sm[:, 80:96].bc(2, 64), ALU.mult)
            fw.tt("dve", ST[:], ST[:], PB[:, :], ALU.add)
            fw.cp("pool", STb[:], ST[:])
            for half in range(2):
                for kt in range(8):
                    fw.mm(PA[:, half * 512:(half + 1) * 512], ynT[:, kt, :], Wo[:, kt, half * 512:(half + 1) * 512],
                          start=kt == 0, stop=kt == 7)
            fw.tt("dve", x1[:], xt[:], PA[:, :], ALU.add)
            fw.dma("sp", io.s1[c * 128:(c + 1) * 128, :], x1[:])

        if SAMPLE:
            dtcol = fw.sb([16, 4], F32, "dtcol")
            fw.dma("sp", dtcol[:], io.dtcol[:, :])
            fw.act(dtcol[:, 3:4], dtcol[:, 1:2], AF.Exp)
            fw.ts("dve", dtcol[:, 3:4], dtcol[:, 3:4], -1.0, None, ALU.mult)
            cst_s = fw.sb([128, 12, 3, NB], F32, "cst_s")
            fw.dma("sp", cst_s[:], io.conv_s_in[:, 0:12, :, :])
            uS = fw.sb([128, 12, NB], F32, "uS")
            accs = fw.sb([128, 12, NB], F32, "accs")
            tmps = fw.sb([128, 12, NB], F32, "tmps")
            cs = fw.sb([128, 12, NB], F32, "cs")
            zsT = fw.sb([128, 8, NB], F32, "zsT")
            dd = fw.sb([16, 48], F32, "dd")
            dx = fw.sb([128, 8, 48], F32, "dx")
            dtx = fw.sb([128, 8, NB], F32, "dtx")
            BCtok = fw.sb([16, 512], F32, "BCtok")
            Sb = [fw.sb([128, 8, 128], F32, f"Sb{i}") for i in range(2)]
            T1s = fw.sb([128, 8, 128], F32, "T1s")
            ysT = fw.sb([128, 8, NB], F32, "ysT")
            fw.dma("sp", xt[0:NB, :], io.xs[:, :])
            rmsnorm(xt[0:NB, :], gmix, xn[0:NB, :], NB, junk)
            to_feat(xn, xnT, NB)
            proj_feat(Wc, 0, 12, xnT, NB, lambda g0, n, ps: fw.cp("act", uS[:, g0:g0 + n, :], ps))
            proj_feat(Wz, 0, 8, xnT, NB, lambda g0, n, ps: fw.act(zsT[:, g0:g0 + n, :], ps, AF.Silu))
            wv = lambda j: convp[:, 0:12, j].bc(2, NB)
            fw.tt("dve", accs[:], cst_s[:, :, 0, :], wv(0), ALU.mult)
            fw.tt("dve", accs[:], accs[:], wv(4), ALU.add)
            for j in (1, 2):
                fw.tt("dve", tmps[:], cst_s[:, :, j, :], wv(j), ALU.mult)
                fw.tt("dve", accs[:], accs[:], tmps[:], ALU.add)
            fw.tt("dve", tmps[:], uS[:], wv(3), ALU.mult)
            fw.tt("dve", accs[:], accs[:], tmps[:], ALU.add)
            fw.act(cs[:], accs[:], AF.Silu)
            fw.dma("sp", io.conv_s[:, 0:12, 0:2, :], cst_s[:, :, 1:3, :])
            fw.dma("sp", io.conv_s[:, 0:12, 2, :], uS[:])
            for kt in range(8):
                fw.mm(PE[0:16, 0:16], Wdt[:, kt, :], xnT[:, kt, 0:NB], start=kt == 0, stop=kt == 7)
            fw.ts("dve", dd[:, 0:16], PE[0:16, 0:16], dtcol[:, 0:1], None, ALU.add)
            fw.act(dd[:, 0:16], dd[:, 0:16], AF.Exp)
            fw.act(dd[:, 0:16], dd[:, 0:16], AF.Ln, bias=1.0)
            fw.ts("dve", dd[:, 16:32], dd[:, 0:16], dtcol[:, 3:4], None, ALU.mult)
            fw.act(dd[:, 16:32], dd[:, 16:32], AF.Exp)
            fw.ts("dve", dd[:, 32:48], ones[0:16, 0:16], dtcol[:, 2:3], None, ALU.mult)
            for j in range(8):
                fw.mm(PE[:, 128 + j * 48:128 + (j + 1) * 48], exp16[:, j, :], dd[:, :])
            fw.cp("dve", dx[:], PE[:, 128:512].rearrange("p (j c) -> p j c", j=8))
            fw.tt("dve", dtx[:], dx[:, :, 0:16], cs[:, 0:8, :], ALU.mult)
            for i in range(4):
                fw.tr(PD[0:16, i * 128:(i + 1) * 128], cs[:, 8 + i, :], ident)
            fw.cp("dve", BCtok[:], PD[0:16, :])
            for b in range(NB):
                S = Sb[b % 2]
                fw.dma("sp", S[:], io.ssm_s_in[b].rearrange("(j q) n -> q j n", q=128))
                fw.mm(PC[:, :], sel16[:, b, :], BCtok[:, :])
                for g in range(2):
                    fw.tt("dve", T1s[:, 4 * g:4 * g + 4, :], PC[:, g * 128:(g + 1) * 128].bc(1, 4),
                          dtx[:, 4 * g:4 * g + 4, b].bc(2, 128), ALU.mult)
                fw.tt("pool", S[:], S[:], dx[:, :, 16 + b].bc(2, 128), ALU.mult)
                fw.tt("dve", S[:], S[:], T1s[:], ALU.add)
                fw.dma("sp", io.ssm_s[b].rearrange("(j q) n -> q j n", q=128), S[:])
                for g in range(2):
                    fw.tt("dve", T1s[:, 4 * g:4 * g + 4, :], S[:, 4 * g:4 * g + 4, :],
                          PC[:, 256 + g * 128:256 + (g + 1) * 128].bc(1, 4), ALU.mult)
                fw.red(ysT[:, :, b], T1s[:], ALU.add)
            fw.tt("dve", dtx[:], dx[:, :, 32:48], cs[:, 0:8, :], ALU.mult)
            fw.tt("dve", ysT[:], ysT[:], dtx[:], ALU.add)
            fw.tt("dve", ysT[:], ysT[:], zsT[:], ALU.mult)
            for j in range(8):
                fw.tr(PA[0:16, j * 128:(j + 1) * 128], ysT[:, j, :], ident)
            fw.cp("dve", t1[0:NB, :], PA[0:NB, :])
            for g in range(2):
                grp_rstd(t1[0:NB, g * 512:(g + 1) * 512], 512, nst[0:NB, 7:8], junk, NB)
                fw.stt(yn[0:NB, g * 512:(g + 1) * 512], t1[0:NB, g * 512:(g + 1) * 512], nst[0:NB, 7:8],
                       gssd[0:NB, g * 512:(g + 1) * 512], ALU.mult, ALU.mult)
            to_feat(yn, ynT, NB)
            for half in range(2):
                for kt in range(8):
                    fw.mm(PA[0:NB, half * 512:(half + 1) * 512], ynT[:, kt, 0:NB], Wo[:, kt, half * 512:(half + 1) * 512],
                          start=kt == 0, stop=kt == 7)
            fw.tt("dve", x1[0:NB, :], xt[0:NB, :], PA[0:NB, :], ALU.add)
            fw.dma("sp", io.s1s[:, :], x1[0:NB, :])
        fw.dma("sp", io.ssm_p[:, :], ST[:])
        fw.dma("sp", io.conv_p[:, 0:12, :], convin[:, :, 0:3])
        fw.release(base_mark)

    if "0b" in phases:
        Wx = fw.sb([128, 8, 1024], BF16, "Wx")
        load_w(Wx, io.w_in0[:, 2560:3584], 0, 8)
        Wg = fw.sb([128, 8, 1024], BF16, "Wg")
        load_w(Wg, io.w_in0[:, 3600:4624], 0, 8)
        Wif = fw.sb([128, 8, 16], BF16, "Wif")
        load_w(Wif, io.w_in0[:, 4616:4632], 0, 8, step=8)
        Wo = fw.sb([128, 8, D], BF16, "Wo")
        load_w(Wo, io.w_out0[1024:2048, :], 0, 8)
        BDq = fw.sb([128, 8, 128], BF16, "BDq")
        BDk = fw.sb([128, 8, 128], BF16, "BDk")
        BDv = fw.sb([128, 8, 128], BF16, "BDv")
        fw.dma("pool", BDq[:], io.bdq[:, :, :])
        fw.dma("pool", BDk[:], io.bdk[:, :, :])
        fw.dma("pool", BDv[:], io.bdv[:, :, :])
        gmix = fw.sb([128, D], F32, "gmix")
        fw.dma("sp", gmix[:], io.norm_mix[0, :].partition_broadcast(128))
        sm0 = fw.sb([128, 64], F32, "sm0")
        fw.dma("sp", sm0[:], io.small0.partition_broadcast(128))
        ib_bc, fb_bc = sm0[:, 48:52], sm0[:, 52:56]
        convp = fw.sb([128, 20, 5], F32, "convp")
        fw.dma("sp", convp[:], io.convp[:, :, :])
        mlcol = fw.sb([128, 8, 2], F32, "mlcol")
        fw.dma("sp", mlcol[:], io.mlcol[:, :, :])
        convin = fw.sb([128, 8, 131], F32, "convin")
        fw.memset("pool", convin[:], 0.0)
        Cst = fw.sb([128, 2, 4, 264], F32, "Cst")
        fw.memset("pool", Cst[:], 0.0)
        Cb = fw.sb([128, 2, 4, 264], BF16, "Cb")
        fw.memset("pool", Cb[:], 0.0)
        mprev = fw.sb([128, 4], F32, "mprev")
        fw.memset("pool", mprev[:], 0.0)
        xt = fw.sb([128, D], F32, "xt")
        junk = fw.sb([128, D], F32, "junk")
        xn = fw.sb([128, D], BF16, "xn")
        xnT = fw.sb([128, 8, 128], BF16, "xnT")
        acc = fw.sb([128, 8, 128], F32, "acc")
        cact = fw.sb([128, 8, 128], BF16, "cact")
        xmraw = fw.sb([128, 8, 128], BF16, "xmraw")
        sigoT = fw.sb([128, 8, 128], BF16, "sigoT")
        sm2 = fw.sb([128, 64], F32, "sm2")
        qT = fw.sb([128, 8, 128], BF16, "qT")
        kT = fw.sb([128, 8, 128], BF16, "kT")
        vtok = fw.sb([128, 4, 264], BF16, "vtok")
        fw.memset("pool", vtok[:], 1.0)
        kw_ = fw.sb([128, 4, 256], BF16, "kw")
        Rh = fw.sb([128, 128], F32, "Rh")
        dlm = fw.sb([128, 128], F32, "dlm")
        Dm = fw.sb([128, 128], F32, "Dm")
        Sg = fw.sb([128, 128], BF16, "Sg")
        SgT = fw.sb([128, 128], BF16, "SgT")
        hs = fw.sb([128, 16], F32, "hs")
        mt = fw.sb([128, 16], F32, "mt")
        fw.memset("pool", mt[:], 0.0)
        fw.memset("pool", sm2[:], 0.0)
        comb = fw.sb([128, 258], F32, "comb")
        hh = fw.sb([128, 256], F32, "hh")
        hmn = fw.sb([128, D], BF16, "hmn")
        hmnT = fw.sb([128, 8, 128], BF16, "hmnT")
        hmfT = fw.sb([128, 8, 128], BF16, "hmfT")
        x1 = fw.sb([128, D], F32, "x1")

        lvl = cfg.get('lvl', 99)
        for c in range(NCH):
            fw.dma("sp", xt[:], io.xp[c * 128:(c + 1) * 128, :])
            fw.dma("sp", x1[:], io.s1[c * 128:(c + 1) * 128, :])
            rmsnorm(xt[:], gmix, xn[:], 128, junk)
            to_feat(xn, xnT, 128)
            proj_feat(Wx, 0, 8, xnT, 128, lambda g0, n, ps: fw.cp("act", convin[:, g0:g0 + n, 3:131], ps))
            proj_feat(Wg, 0, 8, xnT, 128, lambda g0, n, ps: fw.act(sigoT[:, g0:g0 + n, :], ps, AF.Sigmoid))
            for kt in range(8):
                fw.mm(PE[:, 16:32], xnT[:, kt, :], Wif[:, kt, :], start=kt == 0, stop=kt == 7)
            fw.tt("dve", sm2[:, 0:4], PE[:, 24:28], ib_bc, ALU.add)
            fw.tt("dve", sm2[:, 4:8], PE[:, 28:32], fb_bc, ALU.add)
            conv_tiles(convin, convp, acc, 12, 8)
            fw.act(cact[:], acc[:], AF.Silu)
            fw.cp("pool", xmraw[:], convin[:, :, 3:131])
            fw.cp("pool", convin[:, :, 0:3], convin[:, :, 128:131])
            if lvl < 2:
                continue
            for tile in range(8):
                ps = pcd[tile % 2]
                fw.mm(ps[:, 0:128], (Wx[:, tile, 0:128] if cfg.get('alt') else BDq[:, tile, :]), cact[:, tile, :])
                fw.mm(ps[:, 128:256], (Wx[:, tile, 0:128] if cfg.get('alt') else BDk[:, tile, :]), cact[:, tile, :])
                if cfg.get('alt') != 2:
                    fw.cp("dve", qT[:, tile, :], ps[:, 0:128])
                if cfg.get('alt') not in (2, 3):
                    fw.ts("dve", kT[:, tile, :], ps[:, 128:256], 0.0625, None, ALU.mult)
            if lvl < 2.1:
                continue
            for tile in range(8):
                fw.mm(PA[:, tile * 128:(tile + 1) * 128], xmraw[:, tile, :], BDv[:, tile, :])
                fw.mm(PB[:, tile * 128:(tile + 1) * 128], cact[:, tile, :], BDk[:, tile, :])
            if lvl < 2.2:
                continue
            fw.cp("act", vtok[:, :, 0:256], PA[:, :].rearrange("p (h v) -> p h v", h=4))
            if lvl < 2.3:
                continue
            fw.act(sm2[:, 4:8], sm2[:, 4:8], AF.Exp, scale=-1.0)
            fw.act(sm2[:, 4:8], sm2[:, 4:8], AF.Ln, bias=1.0)
            fw.ts("dve", sm2[:, 4:8], sm2[:, 4:8], -1.0, None, ALU.mult)
            fw.mm(PE[:, 64:80], tri_le, sm2[:, 0:16])
            fw.mm(PE[:, 96:112], ones, sm2[:, 0:16])
            fw.cp("dve", sm2[:, 8:12], PE[:, 68:72])
            fw.cp("dve", sm2[:, 12:16], PE[:, 100:104])
            fw.tt("dve", sm2[:, 16:20], sm2[:, 8:12], mprev[:], ALU.add)
            if lvl < 3:
                continue
            for h in range(4):
                fw.ts("dve", Rh[:], mask_gt, sm2[:, 4 + h:5 + h], None, ALU.mult)
                fw.stt(Rh[:], ident, sm2[:, h:h + 1], Rh[:], ALU.mult, ALU.add)
                fw.mm(PD[:, 0:128], tri_le, Rh[:])
                fw.tt("dve", dlm[:], PD[:, 0:128], negmask, ALU.add)
                fw.red(hs[:, 0:1], dlm[:], ALU.max)
                fw.tt("dve", mt[:, h:h + 1], hs[:, 0:1], sm2[:, 16 + h:17 + h], ALU.max)
                fw.ts("dve", hs[:, 1:2], mt[:, h:h + 1], -1.0, None, ALU.mult)
                fw.act(Dm[:], dlm[:], AF.Exp, bias=hs[:, 1:2])
                fw.mm(PD[:, 128:256], qT[:, 2 * h, :], kT[:, 2 * h, :], start=True, stop=False)
                fw.mm(PD[:, 128:256], qT[:, 2 * h + 1, :], kT[:, 2 * h + 1, :], start=False, stop=True)
                fw.tt("dve", Sg[:], PD[:, 128:256], Dm[:], ALU.mult)
                fw.tr(PT[:, 0:128], Sg[:], identb[:, :])
                fw.cp("dve", SgT[:], PT[:, 0:128])
                fw.mm(PA[:, 0:258], SgT[:], vtok[:, h, 0:258])
                fw.mm(PA[:, 512:770], qT[:, 2 * h, :], Cb[:, 0, h, 0:258], start=True, stop=False)
                fw.mm(PA[:, 512:770], qT[:, 2 * h + 1, :], Cb[:, 1, h, 0:258], start=False, stop=True)
                fw.act(hs[:, 2:3], sm2[:, 16 + h:17 + h], AF.Exp, bias=hs[:, 1:2])
                fw.act(comb[:], PA[:, 512:770], AF.Copy, scale=hs[:, 2:3])
                fw.tt("dve", comb[:], comb[:], PA[:, 0:258], ALU.add)
                fw.act(hs[:, 3:4], mt[:, h:h + 1], AF.Exp, scale=-1.0)
                fw.ts("dve", hs[:, 6:7], comb[:, 256:257], -1.0, None, ALU.mult)
                fw.tt("dve", hs[:, 6:7], hs[:, 6:7], comb[:, 256:257], ALU.max)
                fw.tt("dve", hs[:, 4:5], hs[:, 6:7], hs[:, 3:4], ALU.max)
                fw.recip(hs[:, 5:6], hs[:, 4:5])
                fw.ts("dve", hh[:], comb[:, 0:256], hs[:, 5:6], None, ALU.mult)
                grp_rstd(hh[:], 256, nst[:, 7:8], junk)
                fw.ts("dve", hmn[:, h * 256:(h + 1) * 256], hh[:], nst[:, 7:8], None, ALU.mult)
            if lvl < 4:
                continue
            to_feat(hmn, hmnT, 128)
            for tile in range(8):
                fw.ts("dve", hmfT[:, tile, :], hmnT[:, tile, :], mlcol[:, tile, 0:1], None, ALU.mult)
                fw.stt(hmfT[:, tile, :], cact[:, tile, :], mlcol[:, tile, 1:2], hmfT[:, tile, :], ALU.mult, ALU.add)
            fw.tt("dve", hmfT[:], hmfT[:], sigoT[:], ALU.mult)
            if lvl < 5:
                continue
            fw.mm(PE[:, 112:128], sel127, mt[:])
            fw.cp("dve", sm2[:, 20:24], PE[:, 112:116])
            fw.tt("dve", sm2[:, 24:28], sm2[:, 12:16], sm2[:, 8:12], ALU.subtract)
            fw.tt("dve", sm2[:, 24:28], sm2[:, 24:28], sm2[:, 0:4], ALU.add)
            fw.tt("dve", sm2[:, 24:28], sm2[:, 24:28], sm2[:, 20:24], ALU.subtract)
            fw.act(sm2[:, 28:32], sm2[:, 24:28], AF.Exp)
            fw.ts("dve", sm2[:, 28:32], sm2[:, 28:32], 0.0625, None, ALU.mult)
            fw.tt("dve", sm2[:, 32:36], sm2[:, 12:16], mprev[:], ALU.add)
            fw.tt("dve", sm2[:, 32:36], sm2[:, 32:36], sm2[:, 20:24], ALU.subtract)
            fw.act(sm2[:, 32:36], sm2[:, 32:36], AF.Exp)
            fw.tt("dve", kw_[:], PB[:, :].rearrange("p (h d) -> p h d", h=4), sm2[:, 28:32].bc(2, 256), ALU.mult)
            for kt in range(2):
                for h in range(4):
                    fw.mm(PB[:, h * 256:(h + 1) * 256], kw_[:, h, kt * 128:(kt + 1) * 128], vtok[:, h, 0:256])
                    fw.mm(PE[:, 80 + 2 * h:82 + 2 * h], kw_[:, h, kt * 128:(kt + 1) * 128], onesb[:, 0:2])
                for h in range(4):
                    fw.stt(Cst[:, kt, h, 0:256], Cst[:, kt, h, 0:256], sm2[:, 32 + h:33 + h],
                           PB[:, h * 256:(h + 1) * 256], ALU.mult, ALU.add)
                    fw.stt(Cst[:, kt, h, 256:257], Cst[:, kt, h, 256:257], sm2[:, 32 + h:33 + h],
                           PE[:, 80 + 2 * h:81 + 2 * h], ALU.mult, ALU.add)
            fw.cp("pool", Cb[:], Cst[:])
            fw.cp("dve", mprev[:], sm2[:, 20:24])
            if lvl < 6:
                continue
            for half in range(2):
                for kt in range(8):
                    fw.mm(PA[:, half * 512:(half + 1) * 512], hmfT[:, kt, :], Wo[:, kt, half * 512:(half + 1) * 512],
                          start=kt == 0, stop=kt == 7)
            fw.tt("dve", x1[:], x1[:], PA[:, :], ALU.add)
            fw.dma("sp", io.s1[c * 128:(c + 1) * 128, :], x1[:])

        if SAMPLE:
            cst_s = fw.sb([128, 8, 3, NB], F32, "cst_s")
            fw.dma("sp", cst_s[:], io.conv_s_in[:, 12:20, :, :])
            uS = fw.sb([128, 8, NB], F32, "uS")
            accs = fw.sb([128, 8, NB], F32, "accs")
            tmps = fw.sb([128, 8, NB], F32, "tmps")
            cs = fw.sb([128, 8, NB], F32, "cs")
            cs_bf = fw.sb([128, 8, NB], BF16, "cs_bf")
            us_bf = fw.sb([128, 8, NB], BF16, "us_bf")
            qTs = fw.sb([128, 8, NB], F32, "qTs")
            kTs = fw.sb([128, 8, NB], F32, "kTs")
            kws = fw.sb([128, 8, NB], F32, "kws")
            nS = fw.sb([128, 8, NB], F32, "nS")
            vtoks = fw.sb([16, D], F32, "vtoks")
            g16 = fw.sb([16, 64], F32, "g16")
            Zd = fw.sb([16, 128], F32, "Zd")
            wd = fw.sb([128, 2, 4, NB], F32, "wd")
            qmask = fw.sb([128, 8, NB, NB], F32, "qmask")
            Cs = [fw.sb([128, 8, 256], F32, f"Cs{i}") for i in range(2)]
            Tt = fw.sb([128, 8, 256], F32, "Tt")
            numt = fw.sb([16, D], F32, "numt")
            fw.dma("sp", xt[0:NB, :], io.xs[:, :])
            fw.dma("sp", x1[0:NB, :], io.s1s[:, :])
            fw.dma("sp", g16[:, 8:12], io.mm_s_in[:, :])
            fw.dma("sp", nS[:], io.mn_s_in[:, :, :])
            rmsnorm(xt[0:NB, :], gmix, xn[0:NB, :], NB, junk)
            to_feat(xn, xnT, NB)
            proj_feat(Wx, 0, 8, xnT, NB, lambda g0, n, ps: fw.cp("act", uS[:, g0:g0 + n, :], ps))
            proj_feat(Wg, 0, 8, xnT, NB, lambda g0, n, ps: fw.act(sigoT[:, g0:g0 + n, 0:NB], ps, AF.Sigmoid))
            for kt in range(8):
                fw.mm(PE[0:NB, 16:32], xnT[:, kt, 0:NB], Wif[:, kt, :], start=kt == 0, stop=kt == 7)
            fw.tt("dve", g16[:, 0:4], PE[0:NB, 24:28], ib_bc[0:NB, :], ALU.add)
            fw.tt("dve", g16[:, 4:8], PE[0:NB, 28:32], fb_bc[0:NB, :], ALU.add)
            fw.act(g16[:, 4:8], g16[:, 4:8], AF.Exp, scale=-1.0)
            fw.act(g16[:, 4:8], g16[:, 4:8], AF.Ln, bias=1.0)
            fw.ts("dve", g16[:, 4:8], g16[:, 4:8], -1.0, None, ALU.mult)
            wv = lambda j: convp[:, 12:20, j].bc(2, NB)
            fw.tt("dve", accs[:], cst_s[:, :, 0, :], wv(0), ALU.mult)
            fw.tt("dve", accs[:], accs[:], wv(4), ALU.add)
            for j in (1, 2):
                fw.tt("dve", tmps[:], cst_s[:, :, j, :], wv(j), ALU.mult)
                fw.tt("dve", accs[:], accs[:], tmps[:], ALU.add)
            fw.tt("dve", tmps[:], uS[:], wv(3), ALU.mult)
            fw.tt("dve", accs[:], accs[:], tmps[:], ALU.add)
            fw.act(cs[:], accs[:], AF.Silu)
            fw.dma("sp", io.conv_s[:, 12:20, 0:2, :], cst_s[:, :, 1:3, :])
            fw.dma("sp", io.conv_s[:, 12:20, 2, :], uS[:])
            fw.cp("dve", cs_bf[:], cs[:])
            fw.cp("dve", us_bf[:], uS[:])
            for tile in range(8):
                ps = pcd[tile % 2]
                fw.mm(ps[:, 0:NB], BDq[:, tile, :], cs_bf[:, tile, :])
                fw.mm(ps[:, 16:16 + NB], BDk[:, tile, :], cs_bf[:, tile, :])
                fw.cp("dve", qTs[:, tile, :], ps[:, 0:NB])
                fw.ts("dve", kTs[:, tile, :], ps[:, 16:16 + NB], 0.0625, None, ALU.mult)
            for tile in range(8):
                fw.mm(PA[0:NB, tile * 128:(tile + 1) * 128], us_bf[:, tile, :], BDv[:, tile, :])
            fw.cp("act", vtoks[:], PA[0:NB, :])
            fw.tt("dve", g16[:, 16:20], g16[:, 4:8], g16[:, 8:12], ALU.add)
            fw.tt("dve", g16[:, 12:16], g16[:, 16:20], g16[:, 0:4], ALU.max)
            fw.dma("sp", io.mm_s[:, :], g16[:, 12:16])
            fw.tt("dve", g16[:, 20:24], g16[:, 0:4], g16[:, 12:16], ALU.subtract)
            fw.act(g16[:, 20:24], g16[:, 20:24], AF.Exp)
            fw.tt("dve", g16[:, 24:28], g16[:, 16:20], g16[:, 12:16], ALU.subtract)
            fw.act(g16[:, 24:28], g16[:, 24:28], AF.Exp)
            fw.act(g16[:, 28:32], g16[:, 12:16], AF.Exp, scale=-1.0)
            z3 = lambda r: r.rearrange("p (h b) -> p h b", h=4)
            fw.tt("dve", z3(Zd[:, 0:64]), g16[:, 20:24].bc(2, NB), ident[0:NB, 0:NB].bc(1, 4), ALU.mult)
            fw.tt("dve", z3(Zd[:, 64:128]), g16[:, 24:28].bc(2, NB), ident[0:NB, 0:NB].bc(1, 4), ALU.mult)
            fw.mm(PE[:, 128:256], ones[0:NB, :], Zd[:, :])
            fw.cp("dve", wd[:], PE[:, 128:256].rearrange("p (w h b) -> p w h b", w=2, h=4))
            k4 = lambda r: r.rearrange("p (h k) b -> p h k b", h=4)
            fw.tt("dve", k4(kws[:, :, :]), k4(kTs[:, :, :]), wd[:, 0, :, :].bc(2, 2), ALU.mult)
            fw.tt("dve", k4(nS[:, :, :]), k4(nS[:, :, :]), wd[:, 1, :, :].bc(2, 2), ALU.mult)
            fw.tt("dve", nS[:], nS[:], kws[:], ALU.add)
            fw.dma("sp", io.mn_s[:, :, :], nS[:])
            fw.tt("dve", tmps[:], qTs[:], nS[:], ALU.mult)
            for h in range(4):
                for kt in range(2):
                    fw.mm(PE[0:NB, 256 + 2 * h:258 + 2 * h], tmps[:, 2 * h + kt, :], ones[:, 0:2], start=kt == 0, stop=kt == 1)
            fw.tt("dve", qmask[:], qTs[:, :, :].bc(2, NB), eye16[:, :, :].bc(1, 8), ALU.mult)
            for b in range(NB):
                Cc = Cs[b % 2]
                fw.dma("sp", Cc[:], io.mc_s_in[b].rearrange("h (k p) v -> p (h k) v", p=128))
                fw.mm(PA[:, 0:512], sel16[:, b, :], vtoks[:, 0:512])
                fw.mm(PA[:, 512:1024], sel16[:, b, :], vtoks[:, 512:1024])
                fw.tt("dve", Tt[:, :, :].rearrange("p (h k) v -> p h k v", h=4),
                      PA[:, :].rearrange("p (h v) -> p h v", h=4).bc(2, 2),
                      kws[:, :, b].rearrange("p (h k) -> p h k", h=4).bc(3, 256), ALU.mult)
                fw.tt("pool", Cc[:, :, :].rearrange("p (h k) v -> p h (k v)", h=4),
                      Cc[:, :, :].rearrange("p (h k) v -> p h (k v)", h=4), wd[:, 1, :, b].bc(2, 512), ALU.mult)
                fw.tt("dve", Cc[:], Cc[:], Tt[:], ALU.add)
                fw.dma("sp", io.mc_s[b].rearrange("h (k p) v -> p (h k) v", p=128), Cc[:])
                for tile in range(8):
                    h, kt = tile // 2, tile % 2
                    fw.mm(PB[0:NB, h * 256:(h + 1) * 256], qmask[:, tile, b, :], Cc[:, tile, :],
                          start=(b == 0 and tile in (0, 4)), stop=(b == NB - 1 and kt == 1), skip=True)
            fw.cp("act", numt[:], PB[0:NB, :])
            if "dbg_a" in dbg:
                fw.dma("sp", io.dbg_a[:, :], numt[:])
                fw.cp("dve", g16[:, 40:44], PE[0:NB, 256:264].rearrange("p (h t) -> p h t", t=2)[:, :, 0])
                fw.dma("sp", io.dbg_b[:, :], g16[:])
            dn = PE[0:NB, 256:264].rearrange("p (h t) -> p h t", t=2)[:, :, 0]
            fw.ts("dve", g16[:, 32:36], dn, -1.0, None, ALU.mult)
            fw.tt("dve", g16[:, 32:36], g16[:, 32:36], dn, ALU.max)
            fw.tt("dve", g16[:, 32:36], g16[:, 32:36], g16[:, 28:32], ALU.max)
            fw.recip(g16[:, 36:40], g16[:, 32:36])
            fw.tt("dve", numt[:, :].rearrange("p (h v) -> p h v", h=4), numt[:, :].rearrange("p (h v) -> p h v", h=4),
                  g16[:, 36:40].bc(2, 256), ALU.mult)
            for h in range(4):
                grp_rstd(numt[:, h * 256:(h + 1) * 256], 256, nst[0:NB, 7:8], junk, NB)
                fw.ts("dve", hmn[0:NB, h * 256:(h + 1) * 256], numt[:, h * 256:(h + 1) * 256], nst[0:NB, 7:8], None, ALU.mult)
            to_feat(hmn, hmnT, NB)
            for tile in range(8):
                fw.ts("dve", hmfT[:, tile, 0:NB], hmnT[:, tile, 0:NB], mlcol[:, tile, 0:1], None, ALU.mult)
                fw.stt(hmfT[:, tile, 0:NB], cs_bf[:, tile, :], mlcol[:, tile, 1:2], hmfT[:, tile, 0:NB], ALU.mult, ALU.add)
            fw.tt("dve", hmfT[:, :, 0:NB], hmfT[:, :, 0:NB], sigoT[:, :, 0:NB], ALU.mult)
            for half in range(2):
                for kt in range(8):
                    fw.mm(PA[0:NB, half * 512:(half + 1) * 512], hmfT[:, kt, 0:NB], Wo[:, kt, half * 512:(half + 1) * 512],
                          start=kt == 0, stop=kt == 7)
            fw.tt("dve", x1[0:NB, :], x1[0:NB, :], PA[0:NB, :], ALU.add)
            fw.dma("sp", io.s1s[:, :], x1[0:NB, :])
        fw.dma("sp", io.mc_p[:, :, :, :], Cst[:])
        fw.dma("sp", io.mm_p[:, :], mprev[0:1, :])
        fw.dma("sp", io.conv_p[:, 12:20, :], convin[:, :, 0:3])
        fw.release(base_mark)

    def ffn_phase(layer, src, dst, ssrc, sdst, final):
        Wgu = fw.sb([128, 8, 2 * DFF], BF16, "Wgu")
        load_w(Wgu, io.w_gu[layer], 0, 8, step=1)
        Wd = fw.sb([128, 22, D], BF16, "Wd")
        load_w(Wd, io.w_dn[layer], 0, 22)
        gf = fw.sb([128, D], F32, "gf")
        fw.dma("sp", gf[:], io.norm_ffn[layer, :].partition_broadcast(128))
        if final:
            gfin = fw.sb([128, D], F32, "gfin")
            fw.dma("sp", gfin[:], io.norm_final.partition_broadcast(128))
        GB = 4
        xt = fw.sb([128, D], F32, "xt")
        junk = fw.sb([128, D], F32, "junk")
        xn = fw.sb([128, D], BF16, "xn")
        xnT = fw.sb([128, 8, GB * 128], BF16, "xnT")
        hT = fw.sb([128, 22, GB * 128], BF16, "hT")
        sg = [fw.sb([128, GB * 128], F32, f"sg{i}") for i in range(2)]
        x2 = fw.sb([128, D], F32, "x2")
        yo = junk
        groups = [list(range(g, min(g + GB, NCH))) for g in range(0, NCH, GB)]
        if SAMPLE:
            groups.append([NCH])
        for grp in groups:
            samp = grp[0] == NCH
            M = NB if samp else 128
            W = M * len(grp)
            rows = lambda ap, c: (ap[:, :] if samp else ap[c * 128:(c + 1) * 128, :])
            for gi, c in enumerate(grp):
                fw.dma("sp", xt[0:M, :], rows(ssrc if samp else src, c))
                rmsnorm(xt[0:M, :], gf, xn[0:M, :], M, junk)
                for kt in range(8):
                    fw.tr(PT3[:, kt, 0:M], xn[0:M, kt * 128:(kt + 1) * 128], identb[0:M, 0:M])
                fw.cp("dve", xnT[:, :, gi * M:(gi + 1) * M], PT3[:, :, 0:M])
            for j in range(22):
                psg = pcd[j % 2]
                for kt in range(8):
                    fw.mm(psg[:, 0:W], Wgu[:, kt, j * 128:(j + 1) * 128], xnT[:, kt, 0:W], start=kt == 0, stop=kt == 7)
                psu = PB0f if j % 2 == 0 else PB1f
                for kt in range(8):
                    fw.mm(psu[:, 0:W], Wgu[:, kt, DFF + j * 128:DFF + (j + 1) * 128], xnT[:, kt, 0:W],
                          start=kt == 0, stop=kt == 7)
                fw.act(sg[j % 2][:, 0:W], psg[:, 0:W], AF.Silu)
                fw.tt("dve", hT[:, j, 0:W], sg[j % 2][:, 0:W], psu[:, 0:W], ALU.mult)
            for gi, c in enumerate(grp):
                for half in range(2):
                    for j in range(22):
                        fw.mm(PA[0:M, half * 512:(half + 1) * 512], hT[:, j, gi * M:(gi + 1) * M],
                              Wd[:, j, half * 512:(half + 1) * 512], start=j == 0, stop=j == 21)
                fw.dma("sp", xt[0:M, :], rows(ssrc if samp else src, c))
                fw.tt("dve", x2[0:M, :], xt[0:M, :], PA[0:M, :], ALU.add)
                if final:
                    rmsnorm(x2[0:M, :], gfin, yo[0:M, :], M, junk)
                    fw.dma("sp", rows(sdst if samp else dst, c), yo[0:M, :])
                else:
                    fw.dma("sp", rows(sdst if samp else dst, c), x2[0:M, :])
        fw.release(base_mark)

    if "1" in phases:
        ffn_phase(0, io.s1, io.s2, io.s1s, io.s2s, False)

    if "2" in phases:
        Wr = fw.sb([128, 8, D], BF16, "Wr"); load_w(Wr, io.rw_wr, 0, 8)
        Wk = fw.sb([128, 8, D], BF16, "Wk"); load_w(Wk, io.rw_wk, 0, 8)
        A1 = fw.sb([128, 8, 64], BF16, "A1"); load_w(A1, io.rw_a1, 0, 8, step=8)
        A2 = fw.sb([128, D], BF16, "A2"); fw.dma("pool", A2[0:64, :], io.rw_a2[:, :])
        Wv = fw.sb([128, 8, D], BF16, "Wv"); load_w(Wv, io.rw_wv, 0, 8)
        G1 = fw.sb([128, 8, 160], BF16, "G1"); load_w(G1, io.rw_g1, 0, 8, step=8)
        G2a = fw.sb([128, D], BF16, "G2a"); fw.dma("pool", G2a[:], io.rw_g2[0:128, :])
        G2b = fw.sb([128, D], BF16, "G2b"); fw.dma("pool", G2b[0:32, :], io.rw_g2[128:160, :])
        W1 = fw.sb([128, 8, 64], BF16, "W1"); load_w(W1, io.rw_w1, 0, 8, step=8)
        W2 = fw.sb([128, D], BF16, "W2"); fw.dma("pool", W2[0:64, :], io.rw_w2[:, :])
        Wo = fw.sb([128, 8, D], BF16, "Wo"); load_w(Wo, io.rw_wo, 0, 8)
        gm1 = fw.sb([128, D], F32, "gm1")
        fw.dma("sp", gm1[:], io.norm_mix[1, :].partition_broadcast(128))
        rows = []
        for i in range(7):
            rt = fw.sb([128, D], F32, f"row{i}")
            fw.dma("sp", rt[:], io.rw_rows[i, :].partition_broadcast(128))
            rows.append(rt)
        w0b, a0b, kkb_, kab, rkb, lnw, lnb = rows
        mu = fw.sb([128, 8, 6], F32, "mu")
        fw.dma("sp", mu[:], io.rw_mu[:, :, :])
        PB0 = fw.view(PB[:, 0:512], "PB0")
        PB1 = fw.view(PB[:, 512:1024], "PB1")
        NPS = [PB0, PB1, PC, PD]
        PAh = [PA[:, 0:512], PA[:, 512:1024]]
        PBh = [PB0[:, :], PB1[:, :]]
        h1 = fw.sb([128, 2, 128], BF16, "h1")

        def proj_tok(xT, W, Ph, M=128):
            for half in range(2):
                for kt in range(8):
                    fw.mm(Ph[half][0:M, :], xT[:, kt, 0:M], W[:, kt, half * 512:(half + 1) * 512],
                          start=kt == 0, stop=kt == 7)

        def lora(xT, Wa, nh, Wb_list, func, P, M=128):
            widths = [min(128, nh), nh - 128] if nh > 128 else [nh]
            for wi, wd in enumerate(widths):
                for kt in range(8):
                    fw.mm(PE[0:wd, wi * 128:wi * 128 + M], Wa[:, kt, wi * 128:wi * 128 + wd], xT[:, kt, 0:M],
                          start=kt == 0, stop=kt == 7)
                fw.act(h1[0:wd, wi, 0:M], PE[0:wd, wi * 128:wi * 128 + M], func)
            for half in range(2):
                for wi, wd in enumerate(widths):
                    fw.mm(P[half][0:M, :], h1[0:wd, wi, 0:M], Wb_list[wi][0:wd, half * 512:(half + 1) * 512],
                          start=wi == 0, stop=wi == len(widths) - 1)

        def rstd16(src16, dst16, mult_, eps, floor=None):
            if floor is not None:
                fw.ts("dve", dst16, src16, floor, None, ALU.max)
            else:
                fw.ts("dve", dst16, src16, mult_, eps, ALU.mult, ALU.add)
            fw.act(dst16, dst16, AF.Ln)
            fw.act(dst16, dst16, AF.Exp, scale=-0.5)

        mark2 = fw.mark()
        xt = fw.sb([128, D], F32, "xt")
        junk = fw.sb([128, D], F32, "junk")
        tmpA = fw.sb([128, D], F32, "tmpA")
        tmpB = fw.sb([128, D], F32, "tmpB")
        Et = fw.sb([128, D], F32, "Et")
        SB = [fw.sb([128, D], BF16, f"S{i}") for i in range(13)]
        xn = SB[0]; r_bf = SB[1]; kkn = SB[2]; kf_bf = SB[3]; b_bf = SB[4]; v_bf = SB[5]; bv = SB[6]
        g_bf = SB[7]; abar = SB[8]; bbar = SB[9]; kbar = SB[10]; btil = SB[11]; ktil = SB[12]
        rbar = SB[0]; yo = SB[8]
        xnTe = fw.sb([128, 8, 130], BF16, "xnTe")
        fw.memset("pool", xnTe[:], 0.0)
        xx = fw.sb([128, 8, 128], BF16, "xx")
        mixb = [fw.sb([128, 8, 128], BF16, f"mix{i}") for i in range(2)]
        arT = fw.sb([128, 8, 2, 128], BF16, "arT")
        bT = fw.sb([128, 8, 128], BF16, "bT")
        kT = fw.sb([128, 8, 128], BF16, "kT")
        yoT = fw.sb([128, 8, 128], BF16, "yoT")
        Ms = [fw.sb([128, 512], BF16, f"Ms{i}") for i in range(4)]
        Q0 = [fw.sb([128, 128], BF16, f"Q0{i}") for i in range(4)]
        PQ = [[fw.sb([128, 384], BF16, f"PQ{i}{k}") for k in range(2)] for i in range(4)]
        RHSb = [fw.sb([128, 64], BF16, f"RHS{i}") for i in range(4)]
        Ubp = [fw.sb([128, 2, 64], BF16, f"Ubp{i}") for i in range(2)]
        Hst = fw.sb([128, 8, 64], F32, "Hst")
        fw.memset("pool", Hst[:], 0.0)
        Hb = fw.sb([128, 8, 64], BF16, "Hb")
        fw.memset("pool", Hb[:], 0.0)
        eLT = fw.sb([128, 8], F32, "eLT")
        s16 = fw.sb([128, 64], F32, "s16")
        x3 = tmpB
        mcount = [0]

        def mix(cidx):
            dst = mixb[mcount[0] % 2]
            mcount[0] += 1
            for kt in range(8):
                fw.stt(dst[:, kt, :], xx[:, kt, :], mu[:, kt, cidx:cidx + 1], xnTe[:, kt, 1:129], ALU.mult, ALU.add)
            return dst

        for c in range(NCH):
            fw.dma("sp", xt[:], io.s2[c * 128:(c + 1) * 128, :])
            if c == NCH - 1:
                fw.act(junk[:, :], xt[:], AF.Square, accum_out=nst[:, 0:1])
                fw.ts("dve", nst[:, 1:2], nst[:, 0:1], 1.0 / D, EPS, ALU.mult, ALU.add)
                fw.act(nst[:, 2:3], nst[:, 1:2], AF.Ln)
                fw.act(nst[:, 3:4], nst[:, 2:3], AF.Exp, scale=-0.5)
                fw.stt(tmpA[:], xt[:], nst[:, 3:4], gm1[:], ALU.mult, ALU.mult)
                fw.dma("sp", io.shift_p[:, :], tmpA[127:128, :])
                fw.cp("dve", xn[:], tmpA[:])
            else:
                rmsnorm(xt[:], gm1, xn[:], 128, junk)
            for kt in range(8):
                fw.tr(PT3[:, kt, :], xn[:, kt * 128:(kt + 1) * 128], identb[:, :])
            fw.cp("dve", xnTe[:, :, 1:129], PT3)
            fw.tt("pool", xx[:], xnTe[:, :, 0:128], xnTe[:, :, 1:129], ALU.subtract)
            proj_tok(mix(0), Wr, PAh)
            fw.cp("act", r_bf[:], PA[:, :])
            proj_tok(mix(2), Wk, PAh)
            fw.tt("dve", tmpA[:], PA[:, :], kkb_[:], ALU.mult)
            fw.tt("pool", junk[:], tmpA[:], tmpA[:], ALU.mult)
            fw.red(s16[:, 0:16], v16(junk[:, :]), ALU.add)
            rstd16(s16[:, 0:16], s16[:, 16:32], None, None, floor=1e-24)
            fw.tt("dve", v16(kkn[:, :]), v16(tmpA[:, :]), s16[:, 16:32].bc(2, 64), ALU.mult)
            lora(mix(4), A1, 64, [A2], AF.Copy, PBh)
            for i in range(2):
                fw.tt("dve", tmpB[:, i * 512:(i + 1) * 512], PBh[i], a0b[:, i * 512:(i + 1) * 512], ALU.add)
            fw.act(tmpB[:], tmpB[:], AF.Sigmoid)
            fw.stt(junk[:], tmpB[:], 1.0, kab[:], ALU.subtract, ALU.mult)
            fw.ts("dve", junk[:], junk[:], 1.0, None, ALU.add)
            fw.tt("dve", kf_bf[:], PA[:, :], junk[:], ALU.mult)
            fw.tt("pool", b_bf[:], kkn[:], tmpB[:], ALU.mult)
            fw.tt("pool", junk[:], r_bf[:], kf_bf[:], ALU.mult)
            fw.tt("pool", junk[:], junk[:], rkb[:], ALU.mult)
            fw.red(s16[:, 32:48], v16(junk[:, :]), ALU.add)
            proj_tok(mix(3), Wv, PAh)
            fw.cp("act", v_bf[:], PA[:, :])
            fw.tt("dve", v16(bv[:, :]), v16(PA[:, :]), s16[:, 32:48].bc(2, 64), ALU.mult)
            lora(mix(5), G1, 160, [G2a, G2b], AF.Sigmoid, PBh)
            for i in range(2):
                fw.cp("act", g_bf[:, i * 512:(i + 1) * 512], PBh[i])
            lora(mix(1), W1, 64, [W2], AF.Tanh, PAh)
            fw.tt("dve", tmpA[:], PA[:, :], w0b[:], ALU.add)
            fw.act(tmpA[:], tmpA[:], AF.Exp, scale=-1.0)
            fw.act(tmpA[:], tmpA[:], AF.Ln, bias=1.0)
            fw.ts("dve", tmpA[:], tmpA[:], -1.0, -0.5, ALU.mult, ALU.add)
            fw.act(Et[:], tmpA[:], AF.Exp)
            fw.mm(PB0[:, :], tri_le, Et[:, 0:512])
            fw.mm(PB1[:, :], tri_le, Et[:, 512:1024])
            fw.mm(PA[:, 0:512], ones, Et[:, 0:512])
            fw.mm(PA[:, 512:1024], ones, Et[:, 512:1024])
            for kt in range(8):
                fw.mm(PE[:, kt * 16:(kt + 1) * 16], Et[:, kt * 128:(kt + 1) * 128], ones[:, 0:16])
            fw.act(eLT[:], PE[:, 0:128].rearrange("p (k s) -> p k s", s=16)[:, :, 0], AF.Exp, scale=-1.0)
            hv = lambda r, i: r[:, i * 512:(i + 1) * 512]
            for i, PBi in enumerate((PB0, PB1)):
                fw.act(hv(tmpA, i), PBi[:, :], AF.Exp, scale=-1.0)
                fw.tt("pool", hv(rbar, i), hv(r_bf, i), hv(tmpA, i), ALU.mult)
                fw.tt("dve", hv(tmpB, i), hv(Et, i), PBi[:, :], ALU.subtract)
                fw.act(hv(tmpB, i), hv(tmpB, i), AF.Exp)
                fw.stt(hv(abar, i), hv(kkn, i), -1.0, hv(tmpB, i), ALU.mult, ALU.mult)
            fw.cp("act", junk[:], PA[:, :])
            for i, PBi in enumerate((PB0, PB1)):
                fw.act(hv(tmpA, i), PBi[:, :], AF.Exp)
                fw.tt("pool", hv(bbar, i), hv(b_bf, i), hv(tmpA, i), ALU.mult)
                fw.tt("pool", hv(kbar, i), hv(kf_bf, i), hv(tmpA, i), ALU.mult)
                fw.tt("dve", hv(tmpB, i), PBi[:, :], hv(junk, i), ALU.subtract)
                fw.act(hv(tmpB, i), hv(tmpB, i), AF.Exp)
                fw.tt("pool", hv(btil, i), hv(b_bf, i), hv(tmpB, i), ALU.mult)
                fw.tt("dve", hv(ktil, i), hv(kf_bf, i), hv(tmpB, i), ALU.mult)
            for src, dst in ((abar, arT[:, :, 0, :]), (rbar, arT[:, :, 1, :]), (bbar, bT[:, :, :]), (kbar, kT[:, :, :])):
                for kt in range(8):
                    fw.tr(PT3[:, kt, :], src[:, kt * 128:(kt + 1) * 128], identb[:, :])
                fw.cp("dve", dst, PT3)
            for h0 in range(0, 16, 4):
                hd = []
                for i in range(4):
                    h = h0 + i
                    j, e = h // 2, h % 2
                    p0 = 64 * e
                    hd.append(dict(h=h, j=j, e=e, p0=p0, NP=NPS[i],
                                   aT=arT[p0:p0 + 64, j, 0, :], rT=arT[p0:p0 + 64, j, 1, :],
                                   ar=arT[p0:p0 + 64, j, :, :].rearrange("p a t -> p (a t)"),
                                   bT=bT[p0:p0 + 64, j, :], kT=kT[p0:p0 + 64, j, :]))
                for i, d in enumerate(hd):
                    fw.mm(PE[:, 0:256], d["bT"], d["ar"])
                    fw.mm(PE[:, 256:512], d["kT"], d["ar"])
                    fw.tt("dve", Ms[i][:], PE[:, :], m4, ALU.mult)
                    fw.mm(d["NP"][:, 0:128], d["aT"], d["bT"])
                    fw.tt("dve", Q0[i][:], d["NP"][:, 0:128], mask_gt, ALU.mult)
                    d["P"], d["Q"], d["Z"] = Ms[i][:, 0:128], Q0[i][:], identb[:, :]
                for k in range(7):
                    for i, d in enumerate(hd):
                        NP = d["NP"]
                        if k < 6:
                            fw.mm(NP[:, 0:128], d["Q"], d["P"])
                            fw.mm(NP[:, 128:256], d["P"], d["Q"])
                        fw.mm(NP[:, 256:384], identb[:, :], d["Z"], start=True, stop=False)
                        fw.mm(NP[:, 256:384], d["Q"], d["Z"], start=False, stop=True)
                    for i, d in enumerate(hd):
                        NP = d["NP"]
                        pq = PQ[i][k % 2]
                        lo = 0 if k < 6 else 256
                        fw.cp("act" if i % 2 == 0 else "dve", pq[:, lo:384], NP[:, lo:384])
                        d["P"], d["Q"], d["Z"] = pq[:, 0:128], pq[:, 128:256], pq[:, 256:384]
                for i, d in enumerate(hd):
                    NP, h, j, p0 = d["NP"], d["h"], d["j"], d["p0"]
                    fw.mm(NP[:, 384:448], d["aT"], Hb[p0:p0 + 64, j, :], start=True, stop=False)
                    fw.mm(NP[:, 384:448], Ms[i][:, 256:384], v_bf[:, h * 64:(h + 1) * 64], start=False, stop=True)
                    fw.cp("act", RHSb[i][:], NP[:, 384:448])
                for i, d in enumerate(hd):
                    NP, h, j, e = d["NP"], d["h"], d["j"], d["e"]
                    fw.mm(NP[:, 448:512], d["Z"], RHSb[i][:])
                    fw.cp("dve", Ubp[j % 2][:, e, :], NP[:, 448:512])
                for i, d in enumerate(hd):
                    h, j, e, p0 = d["h"], d["j"], d["e"], d["p0"]
                    ysl = PA[:, h * 64:(h + 1) * 64]
                    fw.mm(ysl, d["rT"], Hb[p0:p0 + 64, j, :], start=True, stop=False)
                    fw.mm(ysl, Ms[i][:, 128:256], Ubp[j % 2][:, e, :], start=False, stop=False)
                    fw.mm(ysl, Ms[i][:, 384:512], v_bf[:, h * 64:(h + 1) * 64], start=False, stop=True)
                for jj in range(2):
                    j = h0 // 2 + jj
                    NP = hd[2 * jj]["NP"]
                    fw.mm(NP[:, 0:128], btil[:, j * 128:(j + 1) * 128], Ubp[j % 2][:, :, :].rearrange("p e v -> p (e v)"),
                          start=True, stop=False)
                    fw.mm(NP[:, 0:128], ktil[:, j * 128:(j + 1) * 128], v_bf[:, j * 128:(j + 1) * 128],
                          start=False, stop=True)
                    for e in range(2):
                        p0 = 64 * e
                        fw.stt(Hst[p0:p0 + 64, j, :], Hst[p0:p0 + 64, j, :], eLT[p0:p0 + 64, j:j + 1],
                               NP[p0:p0 + 64, p0:p0 + 64], ALU.mult, ALU.add)
                    fw.cp("pool", Hb[:, j, :], Hst[:, j, :])
            fw.cp("act", tmpA[:], PA[:, :])
            fw.red(s16[:, 0:16], v16(tmpA[:, :]), ALU.add)
            fw.ts("dve", s16[:, 0:16], s16[:, 0:16], 1.0 / 64, None, ALU.mult)
            fw.tt("dve", v16(tmpA[:, :]), v16(tmpA[:, :]), s16[:, 0:16].bc(2, 64), ALU.subtract)
            fw.tt("pool", junk[:], tmpA[:], tmpA[:], ALU.mult)
            fw.red(s16[:, 16:32], v16(junk[:, :]), ALU.add)
            rstd16(s16[:, 16:32], s16[:, 48:64], 1.0 / 64, 64e-5)
            fw.tt("dve", v16(tmpA[:, :]), v16(tmpA[:, :]), s16[:, 48:64].bc(2, 64), ALU.mult)
            fw.tt("dve", tmpA[:], tmpA[:], lnw[:], ALU.mult)
            fw.tt("dve", tmpA[:], tmpA[:], lnb[:], ALU.add)
            fw.tt("dve", tmpA[:], tmpA[:], bv[:], ALU.add)
            fw.tt("dve", yo[:], tmpA[:], g_bf[:], ALU.mult)
            to_feat(yo, yoT, 128)
            for half in range(2):
                for kt in range(8):
                    fw.mm(PA[:, half * 512:(half + 1) * 512], yoT[:, kt, :], Wo[:, kt, half * 512:(half + 1) * 512],
                          start=kt == 0, stop=kt == 7)
            fw.tt("dve", x3[:], xt[:], PA[:, :], ALU.add)
            fw.dma("sp", io.s3[c * 128:(c + 1) * 128, :], x3[:])
            fw.cp("pool", xnTe[:, :, 0:1], xnTe[:, :, 128:129])
        fw.dma("sp", io.wkv_p[:, :, :], Hst[:])
        fw.release(mark2)
        if SAMPLE:
            M = NB
            f16 = lambda nm, dt=F32: fw.sb([NB, D], dt, nm)
            xts = f16("xts"); jk = f16("jk"); tA = f16("tA"); tB = f16("tB"); Es = f16("Es")
            r_s = f16("r_s", BF16); kk_s = f16("kk_s", BF16); kf_s = f16("kf_s", BF16); b_s = f16("b_s", BF16)
            v_s = f16("v_s"); bv_s = f16("bv_s", BF16); g_s = f16("g_s", BF16); xn_s = f16("xn_s", BF16)
            sa_tok = f16("sa_tok"); yo_s = f16("yo_s", BF16)
            xprev = fw.sb([128, 8, NB], F32, "xprev")
            xsT = fw.sb([128, 8, NB], BF16, "xsT")
            xxs = fw.sb([128, 8, NB], BF16, "xxs")
            mixs = [fw.sb([128, 8, NB], BF16, f"mixs{i}") for i in range(2)]
            featT = {nm: fw.sb([128, 8, NB], F32, nm) for nm in ("aT", "wT", "bTs", "kTs", "rTs")}
            amask = fw.sb([128, 8, NB, NB], F32, "amask")
            rmask = fw.sb([128, 8, NB, NB], F32, "rmask")
            Hs = [fw.sb([128, 8, 64], F32, f"Hs{i}") for i in range(2)]
            Tt = fw.sb([128, 8, 64], F32, "Tt")
            yoTs = fw.sb([128, 8, NB], BF16, "yoTs")
            s16 = fw.sb([NB, 64], F32, "s16s")
            v16s = lambda r: r.rearrange("p (h q) -> p h q", h=16)
            mc2 = [0]

            def mix_s(cidx):
                dst = mixs[mc2[0] % 2]
                mc2[0] += 1
                for kt in range(8):
                    fw.stt(dst[:, kt, :], xxs[:, kt, :], mu[:, kt, cidx:cidx + 1], xsT[:, kt, :], ALU.mult, ALU.add)
                return dst

            fw.dma("sp", xts[:], io.s2s[:, :])
            fw.dma("sp", xprev[:], io.shift_s_in[:, :, :])
            fw.act(jk[:], xts[:], AF.Square, accum_out=nst[0:M, 0:1])
            fw.ts("dve", nst[0:M, 1:2], nst[0:M, 0:1], 1.0 / D, EPS, ALU.mult, ALU.add)
            fw.act(nst[0:M, 2:3], nst[0:M, 1:2], AF.Ln)
            fw.act(nst[0:M, 3:4], nst[0:M, 2:3], AF.Exp, scale=-0.5)
            fw.stt(tA[:], xts[:], nst[0:M, 3:4], gm1[0:M, :], ALU.mult, ALU.mult)
            fw.dma("sp", io.shift_s[:, :], tA[:])
            fw.cp("dve", xn_s[:], tA[:])
            for kt in range(8):
                fw.tr(PT3[:, kt, 0:M], xn_s[:, kt * 128:(kt + 1) * 128], identb[0:M, 0:M])
            fw.cp("dve", xsT[:], PT3[:, :, 0:M])
            fw.tt("dve", xxs[:], xprev[:], xsT[:], ALU.subtract)
            PAm = [PA[0:M, 0:512], PA[0:M, 512:1024]]
            PBm = [PB0[0:M, :], PB1[0:M, :]]
            PAf = PA[0:M, :]
            proj_tok(mix_s(0), Wr, PAh, M)
            fw.cp("act", r_s[:], PAf)
            proj_tok(mix_s(2), Wk, PAh, M)
            fw.tt("dve", tA[:], PAf, kkb_[0:M, :], ALU.mult)
            fw.tt("dve", jk[:], tA[:], tA[:], ALU.mult)
            fw.red(s16[:, 0:16], v16s(jk[:, :]), ALU.add)
            rstd16(s16[:, 0:16], s16[:, 16:32], None, None, floor=1e-24)
            fw.tt("dve", v16s(kk_s[:, :]), v16s(tA[:, :]), s16[:, 16:32].bc(2, 64), ALU.mult)
            lora(mix_s(4), A1, 64, [A2], AF.Copy, PBh, M)
            for i in range(2):
                fw.tt("dve", tB[:, i * 512:(i + 1) * 512], PBm[i], a0b[0:M, i * 512:(i + 1) * 512], ALU.add)
            fw.act(tB[:], tB[:], AF.Sigmoid)
            fw.stt(jk[:], tB[:], 1.0, kab[0:M, :], ALU.subtract, ALU.mult)
            fw.ts("dve", jk[:], jk[:], 1.0, None, ALU.add)
            fw.tt("dve", kf_s[:], PAf, jk[:], ALU.mult)
            fw.tt("dve", b_s[:], kk_s[:], tB[:], ALU.mult)
            fw.tt("dve", jk[:], r_s[:], kf_s[:], ALU.mult)
            fw.tt("dve", jk[:], jk[:], rkb[0:M, :], ALU.mult)
            fw.red(s16[:, 32:48], v16s(jk[:, :]), ALU.add)
            proj_tok(mix_s(3), Wv, PAh, M)
            fw.cp("act", v_s[:], PAf)
            fw.tt("dve", v16s(bv_s[:, :]), v16s(PAf), s16[:, 32:48].bc(2, 64), ALU.mult)
            lora(mix_s(5), G1, 160, [G2a, G2b], AF.Sigmoid, PBh, M)
            for i in range(2):
                fw.cp("act", g_s[:, i * 512:(i + 1) * 512], PBm[i])
            lora(mix_s(1), W1, 64, [W2], AF.Tanh, PAh, M)
            fw.tt("dve", tA[:], PAf, w0b[0:M, :], ALU.add)
            fw.act(tA[:], tA[:], AF.Exp, scale=-1.0)
            fw.act(tA[:], tA[:], AF.Ln, bias=1.0)
            fw.ts("dve", tA[:], tA[:], -1.0, -0.5, ALU.mult, ALU.add)
            fw.act(Es[:], tA[:], AF.Exp)
            fw.act(Es[:], Es[:], AF.Exp, scale=-1.0)
            fw.ts("dve", tB[:], kk_s[:], -1.0, None, ALU.mult)
            for nm, src in (("aT", tB), ("wT", Es)):
                for kt in range(8):
                    fw.tr(PE[:, kt * 16:(kt + 1) * 16], src[:, kt * 128:(kt + 1) * 128], ident[0:M, 0:M])
                fw.cp("dve", featT[nm][:], PE[:, 0:128].rearrange("p (k b) -> p k b", k=8))
            for nm, src in (("bTs", b_s), ("kTs", kf_s), ("rTs", r_s)):
                for kt in range(8):
                    fw.tr(PT3[:, kt, 0:M], src[:, kt * 128:(kt + 1) * 128], identb[0:M, 0:M])
                fw.cp("dve", featT[nm][:], PT3[:, :, 0:M])
            fw.tt("dve", amask[:], featT["aT"][:, :, :].bc(2, NB), eye16[:, :, :].bc(1, 8), ALU.mult)
            fw.tt("dve", rmask[:], featT["rTs"][:, :, :].bc(2, NB), eye16[:, :, :].bc(1, 8), ALU.mult)
            SY = [PC, PD]
            for b in range(NB):
                H = Hs[b % 2]
                fw.dma("sp", H[:], io.wkv_s_in[b])
                for h in range(16):
                    j, e = h // 2, h % 2
                    p0 = 64 * e
                    fw.mm(SY[e][0:M, j * 64:(j + 1) * 64], amask[p0:p0 + 64, j, b, :], H[p0:p0 + 64, j, :],
                          start=(b == 0 and j == 0), stop=(b == NB - 1), skip=True)
            je = lambda r: r.rearrange("p (j e v) -> p j e v", j=8, e=2)
            fw.cp("dve", je(sa_tok[:, :])[:, :, 0, :], PC[0:M, :].rearrange("p (j v) -> p j v", j=8))
            fw.cp("dve", je(sa_tok[:, :])[:, :, 1, :], PD[0:M, :].rearrange("p (j v) -> p j v", j=8))
            e4 = lambda r, e: r.rearrange("p (j e v) -> p j e v", j=8, e=2)[64 * e:64 * e + 64, :, e, :]
            for b in range(NB):
                H = Hs[b % 2]
                fw.dma("sp", H[:], io.wkv_s_in[b])
                fw.mm(PA[:, 0:512], sel16[:, b, :], sa_tok[:, 0:512])
                fw.mm(PA[:, 512:1024], sel16[:, b, :], sa_tok[:, 512:1024])
                fw.mm(PB0[:, :], sel16[:, b, :], v_s[:, 0:512])
                fw.mm(PB1[:, :], sel16[:, b, :], v_s[:, 512:1024])
                fw.tt("pool", H[:], H[:], featT["wT"][:, :, b].bc(2, 64), ALU.mult)
                for e in range(2):
                    p0 = 64 * e
                    fw.tt("dve", Tt[p0:p0 + 64, :, :], e4(PA[:, :], e), featT["bTs"][p0:p0 + 64, :, b].bc(2, 64), ALU.mult)
                fw.tt("dve", H[:], H[:], Tt[:], ALU.add)
                for e in range(2):
                    p0 = 64 * e
                    for half, PBx in enumerate((PB0, PB1)):
                        src = PBx[:, :].rearrange("p (j e v) -> p j e v", j=4, e=2)[p0:p0 + 64, :, e, :]
                        fw.tt("dve", Tt[p0:p0 + 64, 4 * half:4 * half + 4, :], src,
                              featT["kTs"][p0:p0 + 64, 4 * half:4 * half + 4, b].bc(2, 64), ALU.mult)
                fw.tt("dve", H[:], H[:], Tt[:], ALU.add)
                fw.dma("sp", io.wkv_s[b], H[:])
                for h in range(16):
                    j, e = h // 2, h % 2
                    p0 = 64 * e
                    fw.mm(SY[e][0:M, j * 64:(j + 1) * 64], rmask[p0:p0 + 64, j, b, :], H[p0:p0 + 64, j, :],
                          start=(b == 0 and j == 0), stop=(b == NB - 1), skip=True)
            fw.cp("dve", je(tA[:, :])[:, :, 0, :], PC[0:M, :].rearrange("p (j v) -> p j v", j=8))
            fw.cp("dve", je(tA[:, :])[:, :, 1, :], PD[0:M, :].rearrange("p (j v) -> p j v", j=8))
            fw.red(s16[:, 0:16], v16s(tA[:, :]), ALU.add)
            fw.ts("dve", s16[:, 0:16], s16[:, 0:16], 1.0 / 64, None, ALU.mult)
            fw.tt("dve", v16s(tA[:, :]), v16s(tA[:, :]), s16[:, 0:16].bc(2, 64), ALU.subtract)
            fw.tt("dve", jk[:], tA[:], tA[:], ALU.mult)
            fw.red(s16[:, 16:32], v16s(jk[:, :]), ALU.add)
            rstd16(s16[:, 16:32], s16[:, 48:64], 1.0 / 64, 64e-5)
            fw.tt("dve", v16s(tA[:, :]), v16s(tA[:, :]), s16[:, 48:64].bc(2, 64), ALU.mult)
            fw.tt("dve", tA[:], tA[:], lnw[0:M, :], ALU.mult)
            fw.tt("dve", tA[:], tA[:], lnb[0:M, :], ALU.add)
            fw.tt("dve", tA[:], tA[:], bv_s[:], ALU.add)
            fw.tt("dve", yo_s[:], tA[:], g_s[:], ALU.mult)
            for kt in range(8):
                fw.tr(PT3[:, kt, 0:M], yo_s[:, kt * 128:(kt + 1) * 128], identb[0:M, 0:M])
            fw.cp("dve", yoTs[:], PT3[:, :, 0:M])
            for half in range(2):
                for kt in range(8):
                    fw.mm(PA[0:M, half * 512:(half + 1) * 512], yoTs[:, kt, :], Wo[:, kt, half * 512:(half + 1) * 512],
                          start=kt == 0, stop=kt == 7)
            fw.tt("dve", tB[:], xts[:], PA[0:M, :], ALU.add)
            fw.dma("sp", io.s3s[:, :], tB[:])
        fw.release(base_mark)

    if "3" in phases:
        ffn_phase(1, io.s3, io.y_p, io.s3s, io.y_s, True)

    fw.finish()
    fw.close()
    return nc, fw


def prep_common(inp):
    f = lambda k: np.ascontiguousarray(np.asarray(inp[k], np.float32))
    m = {}
    m["cst"] = host_consts()
    m["w_in0"] = f("w_in0")[0]
    m["w_out0"] = f("w_out0")[0]
    m["norm_mix"] = f("norm_mix")
    m["norm_ffn"] = f("norm_ffn")
    m["norm_final"] = f("norm_final")
    m["ssd_norm"] = f("ssd_norm")[0]
    small0 = np.zeros(64, np.float32)
    small0[0:16] = f("ssd_dt_bias")[0]
    small0[16:32] = f("ssd_a_log")[0]
    small0[32:48] = f("ssd_d")[0]
    small0[48:52] = f("ml_i_bias")[0]
    small0[52:56] = f("ml_f_bias")[0]
    m["small0"] = small0
    cw = f("conv_w")[0].reshape(4, 20, 128).transpose(2, 1, 0)
    cb = f("conv_b")[0].reshape(20, 128).T[:, :, None]
    m["convp"] = np.ascontiguousarray(np.concatenate([cw, cb], axis=2))
    m["mlcol"] = np.ascontiguousarray(np.stack([f("ml_norm")[0].reshape(8, 128).T, f("ml_skip")[0].reshape(8, 128).T], axis=2))
    m["bdq"] = blockdiag(f("ml_wq")[0])
    m["bdk"] = blockdiag(f("ml_wk")[0])
    m["bdv"] = blockdiag(f("ml_wv")[0])
    for nm in ("rw_wr", "rw_wk", "rw_wv", "rw_wo", "rw_w1", "rw_w2", "rw_a1", "rw_a2", "rw_g1", "rw_g2"):
        m[nm] = f(nm)[0]
    m["rw_rows"] = np.ascontiguousarray(np.stack([f(k)[0] for k in ("rw_w0", "rw_a0", "rw_k_k", "rw_k_a", "rw_r_k", "rw_ln_w", "rw_ln_b")]))
    m["rw_mu"] = np.ascontiguousarray(f("rw_mu")[0].reshape(6, 8, 128).transpose(2, 1, 0))
    m["w_gu"] = f("ffn_w_gate_up")
    m["w_dn"] = f("ffn_w_down")
    return m


def prep_core(inp, core):
    f = lambda k: np.asarray(inp[k], np.float32)
    b0 = core * NB
    m = {}
    m["xp"] = np.ascontiguousarray(f("x_prompt")[core])
    m["xs"] = np.ascontiguousarray(f("x_sample")[b0:b0 + NB, 0, :])
    m["conv_s_in"] = np.ascontiguousarray(f("state_conv")[0, b0:b0 + NB].reshape(NB, 3, 20, 128).transpose(3, 2, 1, 0))
    m["ssm_s_in"] = np.ascontiguousarray(f("state_ssm")[0, b0:b0 + NB].reshape(NB, D, 128))
    m["mc_s_in"] = np.ascontiguousarray(f("state_mlstm_c")[0, b0:b0 + NB])
    m["mn_s_in"] = np.ascontiguousarray(f("state_mlstm_n")[0, b0:b0 + NB].reshape(NB, 4, 2, 128).transpose(3, 1, 2, 0).reshape(128, 8, NB))
    m["mm_s_in"] = np.ascontiguousarray(f("state_mlstm_m")[0, b0:b0 + NB])
    m["shift_s_in"] = np.ascontiguousarray(f("state_shift")[0, b0:b0 + NB].reshape(NB, 8, 128).transpose(2, 1, 0))
    m["wkv_s_in"] = np.ascontiguousarray(f("state_wkv")[0, b0:b0 + NB].reshape(NB, 8, 2, 64, 64).transpose(0, 2, 4, 1, 3).reshape(NB, 128, 8, 64))
    return m


def prep_consts(inp):
    f = lambda k: np.asarray(inp[k], np.float32)
    m = prep_common(inp)
    m["c16"] = host_consts16()
    m["eye16"] = np.ascontiguousarray(np.broadcast_to(np.eye(16, dtype=np.float32), (128, 16, 16)))
    dtcol = np.zeros((16, 4), np.float32)
    dtcol[:, 0] = f("ssd_dt_bias")[0]
    dtcol[:, 1] = f("ssd_a_log")[0]
    dtcol[:, 2] = f("ssd_d")[0]
    m["dtcol"] = dtcol
    return m


_NC_CACHE = {}


def kernel(**inp):
    if "nc" not in _NC_CACHE:
        _NC_CACHE["nc"] = build({})
    nc = _NC_CACHE["nc"]
    cm = prep_consts(inp)
    in_maps = [dict(cm, **prep_core(inp, c)) for c in range(NCORE)]
    res = run_bass_kernel_spmd(nc, in_maps, core_ids=list(range(NCORE)))
    R = res.results
    BT = NCORE * NB
    y_p = np.zeros((NCORE, T, D), np.float32)
    y_s = np.zeros((BT, 1, D), np.float32)
    conv_p = np.zeros((1, NCORE, 3, 2560), np.float32)
    conv_s = np.zeros((1, BT, 3, 2560), np.float32)
    ssm_p = np.zeros((1, NCORE, 16, 64, 128), np.float32)
    ssm_s = np.zeros((1, BT, 16, 64, 128), np.float32)
    mc_p = np.zeros((1, NCORE, 4, 256, 256), np.float32)
    mc_s = np.zeros((1, BT, 4, 256, 256), np.float32)
    mn_p = np.zeros((1, NCORE, 4, 256), np.float32)
    mn_s = np.zeros((1, BT, 4, 256), np.float32)
    mm_p = np.zeros((1, NCORE, 4), np.float32)
    mm_s = np.zeros((1, BT, 4), np.float32)
    sh_p = np.zeros((1, NCORE, D), np.float32)
    sh_s = np.zeros((1, BT, D), np.float32)
    wkv_p = np.zeros((1, NCORE, 16, 64, 64), np.float32)
    wkv_s = np.zeros((1, BT, 16, 64, 64), np.float32)
    for c in range(NCORE):
        r = R[c]
        sl = slice(c * NB, (c + 1) * NB)
        y_p[c] = r["y_p"]
        y_s[sl, 0] = r["y_s"]
        conv_p[0, c] = r["conv_p"].transpose(2, 1, 0).reshape(3, 2560)
        conv_s[0, sl] = r["conv_s"].transpose(3, 2, 1, 0).reshape(NB, 3, 2560)
        ssm_p[0, c] = r["ssm_p"].reshape(128, 16, 64).transpose(1, 2, 0)
        ssm_s[0, sl] = r["ssm_s"].reshape(NB, 16, 64, 128)
        mc = r["mc_p"]
        mc_p[0, c] = mc[:, :, :, :256].transpose(2, 1, 0, 3).reshape(4, 256, 256)
        mn_p[0, c] = mc[:, :, :, 256].transpose(2, 1, 0).reshape(4, 256)
        mc_s[0, sl] = r["mc_s"]
        mn_s[0, sl] = r["mn_s"].reshape(128, 4, 2, NB).transpose(3, 1, 2, 0).reshape(NB, 4, 256)
        mm_p[0, c] = r["mm_p"][0]
        mm_s[0, sl] = r["mm_s"]
        sh_p[0, c] = r["shift_p"][0]
        sh_s[0, sl] = r["shift_s"]
        wkv_p[0, c] = r["wkv_p"].reshape(2, 64, 8, 64).transpose(2, 0, 3, 1).reshape(16, 64, 64)
        wkv_s[0, sl] = r["wkv_s"].reshape(NB, 2, 64, 8, 64).transpose(0, 3, 1, 4, 2).reshape(NB, 16, 64, 64)
    return (y_p, y_s, conv_p, conv_s, ssm_p, ssm_s, mc_p, mc_s, mn_p, mn_s, mm_p, mm_s, sh_p, sh_s, wkv_p, wkv_s)
```

```python
import numpy as np
import concourse.bass as bass
import concourse.mybir as mybir
from concourse.bass_utils import run_bass_kernel_spmd

F32 = mybir.dt.float32
BF16 = mybir.dt.bfloat16
ALU = mybir.AluOpType
AF = mybir.ActivationFunctionType
AX = mybir.AxisListType

NCORE = 8
D = 1024
T = 2048
NB = 16
IN0 = 4632
DFF = 2816
EPS = 1e-5


class Tok:
    __slots__ = ("sem", "val", "eng", "seq")

    def __init__(self, sem, val, eng, seq=None):
        self.sem, self.val, self.eng, self.seq = sem, val, eng, seq


class Ref:
    __slots__ = ("T", "ap")

    def __init__(self, T_, ap):
        self.T, self.ap = T_, ap

    def __getitem__(self, k):
        return Ref(self.T, self.ap[k])

    def rearrange(self, p, **kw):
        return Ref(self.T, self.ap.rearrange(p, **kw))

    def unsqueeze(self, a):
        return Ref(self.T, self.ap.unsqueeze(a))

    def to_broadcast(self, shp):
        return Ref(self.T, self.ap.to_broadcast(list(shp)))

    def bc(self, axis, n):
        ap = self.ap.unsqueeze(axis)
        shp = list(ap.shape)
        shp[axis] = n
        return Ref(self.T, ap.to_broadcast(shp))


class TT:
    __slots__ = ("t", "name", "lw", "rd", "psum")

    def __init__(self, t, name, psum=False):
        self.t, self.name, self.lw, self.rd, self.psum = t, name, None, [], psum

    def __getitem__(self, k):
        return Ref(self, self.t[k])


def _Ts(*xs):
    return [x.T for x in xs if isinstance(x, Ref)]


def _a(x):
    return x.ap if isinstance(x, Ref) else x


class Eng:
    def __init__(self, fw, name, h):
        self.fw, self.name, self.h = fw, name, h
        self.sems, self.n, self.waited, self.nsig = [], 0, {}, 0


class Fw:
    EPOCH = 30000
    NDMA = 10

    def __init__(self, nc, need=None):
        self.nc = nc
        self.need = need
        self.waited_on = set()
        self._ctx = []
        self.E = {}
        for name, h in (("pe", nc.tensor), ("dve", nc.vector), ("act", nc.scalar),
                        ("pool", nc.gpsimd), ("sp", nc.sync)):
            self.E[name] = Eng(self, name, h)
        self.dma_sems, self.dma_i = {}, {}
        self.ntile = 0
        self.sb_bytes = 0

    def enter(self, cm):
        v = cm.__enter__()
        self._ctx.append(cm)
        return v

    def close(self):
        for cm in reversed(self._ctx):
            cm.__exit__(None, None, None)
        self._ctx = []

    def new_sem(self, name):
        return self.enter(self.nc.semaphore(name))

    def presem(self, queues=("sp", "pool", "act"), epochs=3):
        for e in self.E.values():
            while len(e.sems) < epochs:
                e.sems.append(self.new_sem(f"e_{e.name}_{len(e.sems)}"))
        for q in queues:
            self.dma_sems[q] = [[self.new_sem(f"d_{q}_{i}"), 0] for i in range(Fw.NDMA)]
            self.dma_i[q] = 0

    def mark(self):
        return len(self._ctx)

    def release(self, mark):
        self.barrier()
        while len(self._ctx) > mark:
            self._ctx.pop().__exit__(None, None, None)

    def _last_tok(self, e):
        return e.last

    def barrier(self):
        for eng in self.E.values():
            for q, slots in self.dma_sems.items():
                for sem, cnt in slots:
                    if cnt > 0:
                        self._wait(eng, Tok(sem, cnt, "dma"))
            for name, e in self.E.items():
                if e is eng or e.n == 0:
                    continue
                self._wait(eng, self._last_tok(e))

    def sb(self, shape, dt=F32, name="t"):
        self.ntile += 1
        n = 1
        for s in shape[1:]:
            n *= s
        self.sb_bytes += n * (2 if dt == BF16 else 4)
        return TT(self.enter(self.nc.sbuf_tensor(f"{name}_{self.ntile}", list(shape), dt)), name)

    def ps(self, shape, dt=F32, name="p"):
        self.ntile += 1
        return TT(self.enter(self.nc.psum_tensor(f"{name}_{self.ntile}", list(shape), dt)), name, psum=True)

    def view(self, ref, name="v"):
        return TT(ref.ap, name, psum=ref.T.psum)

    def _wait(self, eng, tok):
        if tok is None:
            return
        key = id(tok.sem)
        if eng.waited.get(key, 0) >= tok.val:
            return
        if tok.seq is not None:
            self.waited_on.add((tok.eng, tok.seq))
            assert tok.val == int(tok.val), "wait on a non-signalling instruction (two-pass mismatch)"
        eng.h.wait_ge(tok.sem, int(tok.val))
        eng.waited[key] = tok.val

    def _deps(self, eng, reads, writes):
        for t in reads:
            if t.lw is not None:
                self._wait(eng, t.lw)
            if t.psum:
                for r in t.rd:
                    if r.eng != eng.name:
                        self._wait(eng, r)
        strict = eng.name != "pe"
        for t in writes:
            if t.lw is not None and (strict or t.lw.eng != eng.name):
                self._wait(eng, t.lw)
            for r in t.rd:
                if strict or r.eng != eng.name:
                    self._wait(eng, r)

    def _mark(self, tok, reads, writes):
        for t in reads:
            t.rd.append(tok)
        for t in writes:
            t.lw = tok
            t.rd = []

    def op(self, e, fn, reads=(), writes=()):
        eng = self.E[e]
        self._deps(eng, reads, writes)
        seq = eng.n
        eng.n += 1
        signal = self.need is None or (eng.name, seq) in self.need
        ep = eng.nsig // Fw.EPOCH
        while len(eng.sems) <= ep:
            eng.sems.append(self.new_sem(f"e_{eng.name}_{len(eng.sems)}"))
        sem = eng.sems[ep]
        inst = fn(eng.h)
        if signal:
            val = eng.nsig % Fw.EPOCH + 1
            eng.nsig += 1
            inst.then_inc(sem, 1)
        else:
            val = eng.nsig % Fw.EPOCH + 0.5
        tok = Tok(sem, val, eng.name, seq)
        eng.last = tok
        self._mark(tok, reads, writes)
        return tok

    def dma(self, q, out, in_, **kw):
        eng = self.E[q]
        if q not in self.dma_sems:
            self.dma_sems[q] = [[self.new_sem(f"d_{q}_{i}"), 0] for i in range(Fw.NDMA)]
            self.dma_i[q] = 0
        slot = self.dma_sems[q][self.dma_i[q] % Fw.NDMA]
        self.dma_i[q] += 1
        sem, cnt = slot
        if cnt > 0:
            self._wait(eng, Tok(sem, cnt, "dma"))
        reads, writes = _Ts(in_), _Ts(out)
        self._deps(eng, reads, writes)
        inst = eng.h.dma_start(out=_a(out), in_=_a(in_), **kw)
        slot[1] = cnt + 16
        inst.then_inc(sem, 16)
        tok = Tok(sem, cnt + 16, "dma")
        self._mark(tok, reads, writes)
        return tok

    def finish(self):
        eng = self.E["sp"]
        for q, slots in self.dma_sems.items():
            for sem, cnt in slots:
                if cnt > 0:
                    self._wait(eng, Tok(sem, cnt, "dma"))
        for name, e in self.E.items():
            if name == "sp" or e.n == 0:
                continue
            self._wait(eng, self._last_tok(e))

    def mm(self, out, lhsT, rhs, start=True, stop=True, skip=False):
        kw = {"skip_group_check": True} if skip else {}
        return self.op("pe", lambda e: e.matmul(_a(out), _a(lhsT), _a(rhs), start=start, stop=stop, **kw),
                       _Ts(lhsT, rhs), _Ts(out))

    def tr(self, out, in_, ident):
        return self.op("pe", lambda e: e.transpose(_a(out), _a(in_), _a(ident)), _Ts(in_, ident), _Ts(out))

    def act(self, out, in_, func, bias=None, scale=None, accum_out=None):
        kw = {}
        if bias is not None:
            kw["bias"] = _a(bias)
        if scale is not None:
            kw["scale"] = _a(scale)
        if accum_out is not None:
            kw["accum_out"] = _a(accum_out)
        return self.op("act", lambda e: e.activation(out=_a(out), in_=_a(in_), func=func, **kw),
                       _Ts(in_, bias, scale), _Ts(out, accum_out))

    def tt(self, e, out, in0, in1, op):
        return self.op(e, lambda h: h.tensor_tensor(out=_a(out), in0=_a(in0), in1=_a(in1), op=op),
                       _Ts(in0, in1), _Ts(out))

    def ts(self, e, out, in0, s1, s2, op0, op1=None, accum_out=None):
        kw = {}
        if op1 is not None:
            kw["op1"] = op1
        if accum_out is not None:
            kw["accum_out"] = _a(accum_out)
        return self.op(e, lambda h: h.tensor_scalar(out=_a(out), in0=_a(in0), scalar1=_a(s1), scalar2=_a(s2),
                                                    op0=op0, **kw),
                       _Ts(in0, s1, s2), _Ts(out, accum_out))

    def stt(self, out, in0, scalar, in1, op0, op1, accum_out=None):
        kw = {}
        if accum_out is not None:
            kw["accum_out"] = _a(accum_out)
        return self.op("dve", lambda h: h.scalar_tensor_tensor(out=_a(out), in0=_a(in0), scalar=_a(scalar),
                                                               in1=_a(in1), op0=op0, op1=op1, **kw),
                       _Ts(in0, scalar, in1), _Ts(out, accum_out))

    def cp(self, e, out, in_):
        if e == "act":
            return self.act(out, in_, AF.Copy)
        return self.op(e, lambda h: h.tensor_copy(out=_a(out), in_=_a(in_)), _Ts(in_), _Ts(out))

    def red(self, out, in_, op, axis=AX.X):
        return self.op("dve", lambda h: h.tensor_reduce(out=_a(out), in_=_a(in_), axis=axis, op=op),
                       _Ts(in_), _Ts(out))

    def recip(self, out, in_):
        return self.op("dve", lambda h: h.reciprocal(out=_a(out), in_=_a(in_)), _Ts(in_), _Ts(out))

    def memset(self, e, out, val):
        return self.op(e, lambda h: h.memset(_a(out), val), [], _Ts(out))


def host_consts():
    j = np.arange(128)
    c = np.zeros((128, 10, 128), np.float32)
    c[:, 0, :] = (j[:, None] == j[None, :])
    c[:, 1, :] = (j[:, None] <= j[None, :])
    c[:, 2, :] = (j[:, None] > j[None, :])
    c[:, 3, :] = np.where(j[None, :] <= j[:, None], 0.0, -30000.0)
    c[:, 4, :] = 1.0
    c[:, 5, :] = (j[:, None] == 127)
    c[:, 6, :] = (j[:, None] < j[None, :])
    c[:, 7, :] = c[:, 1, :]
    c[:, 8, :] = c[:, 6, :]
    c[:, 9, :] = c[:, 1, :]
    return c


def blockdiag(w):
    out = np.zeros((8, 128, 128), np.float32)
    w = w.reshape(8, 32, 4, 4)
    for nl in range(32):
        out[:, nl * 4:(nl + 1) * 4, nl * 4:(nl + 1) * 4] = w[:, nl]
    return np.ascontiguousarray(out.transpose(1, 0, 2))


def host_consts16():
    h = np.arange(16)
    q = np.arange(128)
    j = np.arange(8)
    e = (h[:, None, None] == (2 * j[None, :, None] + q[None, None, :] // 64)).astype(np.float32)
    sel = np.broadcast_to((h[:, None, None] == h[None, :, None]), (16, 16, 128)).astype(np.float32)
    return np.ascontiguousarray(np.concatenate([e.reshape(16, -1), sel.reshape(16, -1)], axis=1))


class IO:
    pass


def build(cfg):
    _, fw1 = _build(cfg, None)
    nc, fw2 = _build(cfg, fw1.waited_on)
    return nc


def _build(cfg, need):
    nc = bass.Bass("TRN2", target_bir_lowering=False)
    fw = Fw(nc, need)
    io = IO()
    NCH = cfg.get("nch", 16)
    dbg = cfg.get("dbg", ())
    phases = cfg.get("phases", ("0a", "0b", "1", "2", "3"))

    def din(name, shape):
        return nc.dram_tensor(name, list(shape), F32, kind="ExternalInput").ap()

    def dout(name, shape):
        return nc.dram_tensor(name, list(shape), F32, kind="ExternalOutput").ap()

    def dscr(name, shape):
        if name in dbg:
            return dout(name, shape)
        return nc.dram_tensor(name, list(shape), F32).ap()

    io.xp = din("xp", [T, D])
    io.cst = din("cst", [128, 10, 128])
    io.w_in0 = din("w_in0", [D, IN0])
    io.w_out0 = din("w_out0", [2 * D, D])
    io.norm_mix = din("norm_mix", [2, D])
    io.norm_ffn = din("norm_ffn", [2, D])
    io.norm_final = din("norm_final", [D])
    io.ssd_norm = din("ssd_norm", [D])
    io.small0 = din("small0", [64])
    io.convp = din("convp", [128, 20, 5])
    io.mlcol = din("mlcol", [128, 8, 2])
    io.bdq = din("bdq", [128, 8, 128])
    io.bdk = din("bdk", [128, 8, 128])
    io.bdv = din("bdv", [128, 8, 128])
    io.w_gu = din("w_gu", [2, D, 2 * DFF])
    io.w_dn = din("w_dn", [2, DFF, D])
    for nm in ("rw_wr", "rw_wk", "rw_wv", "rw_wo"):
        setattr(io, nm, din(nm, [D, D]))
    io.rw_w1 = din("rw_w1", [D, 64]); io.rw_w2 = din("rw_w2", [64, D])
    io.rw_a1 = din("rw_a1", [D, 64]); io.rw_a2 = din("rw_a2", [64, D])
    io.rw_g1 = din("rw_g1", [D, 160]); io.rw_g2 = din("rw_g2", [160, D])
    io.rw_rows = din("rw_rows", [7, D])
    io.rw_mu = din("rw_mu", [128, 8, 6])
    io.wkv_p = dout("wkv_p", [128, 8, 64])
    io.shift_p = dout("shift_p", [1, D])
    io.xs = din("xs", [NB, D])
    io.c16 = din("c16", [16, 8 * 128 + 16 * 128])
    io.eye16 = din("eye16", [128, 16, 16])
    io.dtcol = din("dtcol", [16, 4])
    io.conv_s_in = din("conv_s_in", [128, 20, 3, NB])
    io.ssm_s_in = din("ssm_s_in", [NB, D, 128])
    io.mc_s_in = din("mc_s_in", [NB, 4, 256, 256])
    io.mn_s_in = din("mn_s_in", [128, 8, NB])
    io.mm_s_in = din("mm_s_in", [NB, 4])
    io.shift_s_in = din("shift_s_in", [128, 8, NB])
    io.wkv_s_in = din("wkv_s_in", [NB, 128, 8, 64])
    io.y_s = dout("y_s", [NB, D])
    io.conv_s = dout("conv_s", [128, 20, 3, NB])
    io.ssm_s = dout("ssm_s", [NB, D, 128])
    io.mc_s = dout("mc_s", [NB, 4, 256, 256])
    io.mn_s = dout("mn_s", [128, 8, NB])
    io.mm_s = dout("mm_s", [NB, 4])
    io.shift_s = dout("shift_s", [NB, D])
    io.wkv_s = dout("wkv_s", [NB, 128, 8, 64])
    io.s1s = dscr("s1s", [NB, D])
    io.s2s = dscr("s2s", [NB, D])
    io.s3s = dscr("s3s", [NB, D])
    if "dbg_a" in dbg:
        io.dbg_a = dout("dbg_a", [NB, D]); io.dbg_b = dout("dbg_b", [NB, 64])
    io.s1 = dscr("s1", [T, D])
    io.s2 = dscr("s2", [T, D])
    io.s3 = dscr("s3", [T, D])
    io.y_p = dout("y_p", [T, D])
    io.ssm_p = dout("ssm_p", [128, D])
    io.mc_p = dout("mc_p", [128, 2, 4, 264])
    io.mm_p = dout("mm_p", [1, 4])
    io.conv_p = dout("conv_p", [128, 20, 3])

    fw.presem(epochs=5)

    cst = fw.sb([128, 10, 128], F32, "cst")
    fw.dma("sp", cst[:], io.cst[:, :, :])
    ident, tri_le, mask_gt, negmask, ones = (cst[:, i, :] for i in range(5))
    sel127 = cst[:, 5, :]
    m4 = cst[:, 6:10, :].rearrange("p a t -> p (a t)")
    identb = fw.sb([128, 128], BF16, "identb")
    fw.cp("dve", identb[:], ident)
    onesb = fw.sb([128, 128], BF16, "onesb")
    fw.cp("dve", onesb[:], ones)
    nst = fw.sb([128, 8], F32, "nst")
    c16 = fw.sb([16, 8 * 128 + 16 * 128], F32, "c16")
    fw.dma("sp", c16[:], io.c16[:, :])
    exp16 = c16[:, 0:1024].rearrange("p (j q) -> p j q", j=8)
    sel16 = c16[:, 1024:3072].rearrange("p (b q) -> p b q", b=16)
    eye16 = fw.sb([128, 16, 16], F32, "eye16")
    fw.dma("sp", eye16[:], io.eye16[:, :, :])
    SAMPLE = cfg.get("sample", True)

    PA = fw.ps([128, 1024], F32, "PA")
    PB = fw.ps([128, 1024], F32, "PB")
    PC = fw.ps([128, 512], F32, "PC")
    PD = fw.ps([128, 512], F32, "PD")
    PE = fw.ps([128, 512], F32, "PE")
    PT = fw.ps([128, 1024], BF16, "PT")
    pcd = [PC, PD]
    PB0f = fw.view(PB[:, 0:512], "PB0f")
    PB1f = fw.view(PB[:, 512:1024], "PB1f")
    PT3 = PT[:, :].rearrange("p (k m) -> p k m", k=8)
    v16 = lambda r: r.rearrange("p (h q) -> p h q", h=16)

    def load_w(dst, src, kt0, kt1, q="pool", step=2):
        N = src.shape[1]
        cw = 1024 if N > 1024 else N
        if N <= 1024:
            kstep = max(1, min(step, 2048 // max(N, 1))) if N >= 512 else step
        else:
            kstep = 1
        for k in range(kt0, kt1, kstep):
            k1 = min(k + kstep, kt1)
            for c0 in range(0, N, cw):
                c1 = min(c0 + cw, N)
                fw.dma(q, dst[:, k:k1, c0:c1], src[k * 128:k1 * 128, c0:c1].rearrange("(k p) n -> p k n", p=128))

    def rmsnorm(x, g, out, M, junk):
        fw.act(junk[0:M, :], x, AF.Square, accum_out=nst[0:M, 0:1])
        fw.ts("dve", nst[0:M, 1:2], nst[0:M, 0:1], 1.0 / D, EPS, ALU.mult, ALU.add)
        fw.act(nst[0:M, 2:3], nst[0:M, 1:2], AF.Ln)
        fw.act(nst[0:M, 3:4], nst[0:M, 2:3], AF.Exp, scale=-0.5)
        fw.stt(out, x, nst[0:M, 3:4], g[0:M, :], ALU.mult, ALU.mult)

    def to_feat(src, dst, M):
        for kt in range(8):
            fw.tr(PT3[:, kt, 0:M], src[0:M, kt * 128:(kt + 1) * 128], identb[0:M, 0:M])
        fw.cp("dve", dst[:, :, 0:M], PT3[:, :, 0:M])

    def grp_rstd(src, ncol, dst, junk, M=128):
        fw.act(junk[0:M, 0:ncol], src, AF.Square, accum_out=nst[0:M, 4:5])
        fw.ts("dve", nst[0:M, 5:6], nst[0:M, 4:5], 1.0 / ncol, EPS, ALU.mult, ALU.add)
        fw.act(nst[0:M, 6:7], nst[0:M, 5:6], AF.Ln)
        fw.act(dst, nst[0:M, 6:7], AF.Exp, scale=-0.5)

    def proj_feat(W, col0, ntile, xT, M, evac):
        for gi, g0 in enumerate(range(0, ntile, 4)):
            n = min(4, ntile - g0)
            ps3 = pcd[gi % 2][:, :].rearrange("p (a m) -> p a m", a=4)
            for i in range(n):
                col = col0 + (g0 + i) * 128
                for kt in range(8):
                    fw.mm(ps3[:, i, 0:M], W[:, kt, col:col + 128], xT[:, kt, 0:M], start=kt == 0, stop=kt == 7)
            evac(g0, n, ps3[:, 0:n, 0:M])

    def conv_tiles(convin, convp, accs, ct0, n):
        for i in range(n):
            ct = ct0 + i
            fw.act(accs[i][:, :], convin[:, i, 0:128], AF.Identity, scale=convp[:, ct, 0:1], bias=convp[:, ct, 4:5])
        for j in range(1, 4):
            for i in range(n):
                ct = ct0 + i
                fw.stt(accs[i][:, :], convin[:, i, j:j + 128], convp[:, ct, j:j + 1], accs[i][:, :], ALU.mult, ALU.add)

    base_mark = fw.mark()

    if "0a" in phases:
        Wc = fw.sb([128, 8, 1536], BF16, "Wc")
        load_w(Wc, io.w_in0[:, 1024:2560], 0, 8)
        Wz = fw.sb([128, 8, 1024], BF16, "Wz")
        load_w(Wz, io.w_in0[:, 0:1024], 0, 8)
        Wdt = fw.sb([128, 8, 16], BF16, "Wdt")
        load_w(Wdt, io.w_in0[:, 3584:3600], 0, 8, step=8)
        Wo = fw.sb([128, 8, D], BF16, "Wo")
        load_w(Wo, io.w_out0[0:1024, :], 0, 8)
        gmix = fw.sb([128, D], F32, "gmix")
        fw.dma("sp", gmix[:], io.norm_mix[0, :].partition_broadcast(128))
        gssd = fw.sb([128, D], F32, "gssd")
        fw.dma("sp", gssd[:], io.ssd_norm.partition_broadcast(128))
        sm0 = fw.sb([128, 64], F32, "sm0")
        fw.dma("sp", sm0[:], io.small0.partition_broadcast(128))
        dtb_bc, D_bc = sm0[:, 0:16], sm0[:, 32:48]
        A_t = fw.sb([128, 16], F32, "A_t")
        fw.act(A_t[:], sm0[:, 16:32], AF.Exp)
        fw.ts("dve", A_t[:], A_t[:], -1.0, None, ALU.mult)
        convp = fw.sb([128, 20, 5], F32, "convp")
        fw.dma("sp", convp[:], io.convp[:, :, :])
        convin = fw.sb([128, 12, 131], F32, "convin")
        fw.memset("pool", convin[:], 0.0)
        ST = fw.sb([128, D], F32, "ST")
        fw.memset("pool", ST[:], 0.0)
        STb = fw.sb([128, D], BF16, "STb")
        fw.memset("pool", STb[:], 0.0)
        xt = fw.sb([128, D], F32, "xt")
        junk = fw.sb([128, D], F32, "junk")
        xn = fw.sb([128, D], BF16, "xn")
        xnT = fw.sb([128, 8, 128], BF16, "xnT")
        acc = fw.sb([128, 12, 128], F32, "acc")
        accv = [fw.view(acc[:, i, :], f"acc{i}") for i in range(12)]
        cact = fw.sb([128, 12, 128], BF16, "cact")
        zs = fw.sb([128, D], F32, "zs")
        xtok = fw.sb([128, D], BF16, "xtok")
        Btok = fw.sb([128, 256], BF16, "Btok")
        sm = fw.sb([128, 128], F32, "sm")
        Lh = [fw.sb([128, 4, 128], F32, f"Lh{i}") for i in range(2)]
        Eh = fw.sb([128, 4, 128], F32, "Eh")
        CBm = fw.sb([128, 2, 128], F32, "CBm")
        Wt = fw.sb([128, 16, 128], BF16, "Wt")
        t1 = fw.sb([128, D], F32, "t1")
        yn = fw.sb([128, D], BF16, "yn")
        ynT = fw.sb([128, 8, 128], BF16, "ynT")
        xw = fw.sb([128, D], BF16, "xw")
        x1 = fw.sb([128, D], F32, "x1")

        for c in range(NCH):
            fw.dma("sp", xt[:], io.xp[c * 128:(c + 1) * 128, :])
            rmsnorm(xt[:], gmix, xn[:], 128, junk)
            to_feat(xn, xnT, 128)
            proj_feat(Wc, 0, 12, xnT, 128, lambda g0, n, ps: fw.cp("act", convin[:, g0:g0 + n, 3:131], ps))
            for half in range(2):
                for kt in range(8):
                    fw.mm(PA[:, half * 512:(half + 1) * 512], xnT[:, kt, :], Wz[:, kt, half * 512:(half + 1) * 512],
                          start=kt == 0, stop=kt == 7)
            fw.act(zs[:], PA[:, :], AF.Silu)
            for kt in range(8):
                fw.mm(PE[:, 0:16], xnT[:, kt, :], Wdt[:, kt, :], start=kt == 0, stop=kt == 7)
            fw.tt("dve", sm[:, 0:16], PE[:, 0:16], dtb_bc, ALU.add)
            conv_tiles(convin, convp, accv, 0, 12)
            for i in range(12):
                fw.act(cact[:, i, :], accv[i][:, :], AF.Silu)
            fw.cp("pool", convin[:, :, 0:3], convin[:, :, 128:131])
            for kt in range(8):
                fw.tr(PT3[:, kt, :], cact[:, kt, :], identb[:, :])
            fw.cp("dve", xtok[:], PT[:, :])
            for g in range(2):
                fw.tr(PT[:, g * 128:(g + 1) * 128], cact[:, 8 + g, :], identb[:, :])
            fw.cp("dve", Btok[:], PT[:, 0:256])
            fw.act(sm[:, 0:16], sm[:, 0:16], AF.Exp)
            fw.act(sm[:, 0:16], sm[:, 0:16], AF.Ln, bias=1.0)
            fw.tt("dve", sm[:, 16:32], sm[:, 0:16], A_t[:], ALU.mult)
            fw.mm(PE[:, 32:48], tri_le, sm[:, 16:32])
            fw.mm(PE[:, 48:64], ones, sm[:, 16:32])
            fw.act(sm[:, 32:48], PE[:, 32:48], AF.Exp)
            fw.cp("dve", sm[:, 64:80], PE[:, 32:48])
            fw.tt("dve", sm[:, 48:64], PE[:, 48:64], sm[:, 64:80], ALU.subtract)
            fw.act(sm[:, 48:64], sm[:, 48:64], AF.Exp)
            fw.tt("dve", sm[:, 48:64], sm[:, 48:64], sm[:, 0:16], ALU.mult)
            fw.act(sm[:, 80:96], PE[:, 48:64], AF.Exp)
            for g in range(2):
                fw.mm(PE[:, 128 + g * 128:256 + g * 128], cact[:, 8 + g, :], cact[:, 10 + g, :])
                fw.tt("dve", CBm[:, g, :], PE[:, 128 + g * 128:256 + g * 128], tri_le, ALU.mult)
            for hq in range(4):
                L = Lh[hq % 2]
                ps3 = pcd[hq % 2][:, :].rearrange("p (a m) -> p a m", a=4)
                for i in range(4):
                    h = hq * 4 + i
                    fw.ts("dve", L[:, i, :], mask_gt, sm[:, 16 + h:17 + h], None, ALU.mult)
                    fw.mm(ps3[:, i, :], L[:, i, :], tri_le)
                fw.act(Eh[:], ps3, AF.Exp)
                for i in range(4):
                    h = hq * 4 + i
                    fw.stt(Wt[:, h, :], Eh[:, i, :], sm[:, h:h + 1], CBm[:, h // 8, :], ALU.mult, ALU.mult)
            for h in range(16):
                fw.mm(PA[:, h * 64:(h + 1) * 64], Wt[:, h, :], xtok[:, h * 64:(h + 1) * 64])
            for g in range(2):
                fw.mm(PB[:, g * 512:(g + 1) * 512], cact[:, 10 + g, :], STb[:, g * 512:(g + 1) * 512])
            fw.tt("dve", v16(t1[:, :]), v16(PB[:, :]), sm[:, 32:48].bc(2, 64), ALU.mult)
            fw.tt("dve", t1[:], t1[:], PA[:, :], ALU.add)
            fw.tt("pool", v16(junk[:, :]), v16(xtok[:, :]), D_bc.bc(2, 64), ALU.mult)
            fw.tt("dve", t1[:], t1[:], junk[:], ALU.add)
            fw.tt("dve", t1[:], t1[:], zs[:], ALU.mult)
            for g in range(2):
                grp_rstd(t1[:, g * 512:(g + 1) * 512], 512, nst[:, 7:8], junk)
                fw.stt(yn[:, g * 512:(g + 1) * 512], t1[:, g * 512:(g + 1) * 512], nst[:, 7:8],
                       gssd[:, g * 512:(g + 1) * 512], ALU.mult, ALU.mult)
            to_feat(yn, ynT, 128)
            fw.tt("pool", v16(xw[:, :]), v16(xtok[:, :]), sm[:, 48:64].bc(2, 64), ALU.mult)
            for g in range(2):
                fw.mm(PB[:, g * 512:(g + 1) * 512], Btok[:, g * 128:(g + 1) * 128], xw[:, g * 512:(g + 1) * 512])
            fw.tt("dve", v16(ST[:, :]), v16(ST[:, :]), sm[:, 80:96].bc(2, 64), ALU.mult)
            fw.tt("dve", ST[:], ST[:], PB[:, :], ALU.add)
            fw.cp("act", STb[:], ST[:])
            for half in range(2):
                for kt in range(8):
                    fw.mm(PA[:, half * 512:(half + 1) * 512], ynT[:, kt, :], Wo[:, kt, half * 512:(half + 1) * 512],
                          start=kt == 0, stop=kt == 7)
            fw.tt("dve", x1[:], xt[:], PA[:, :], ALU.add)
            fw.dma("sp", io.s1[c * 128:(c + 1) * 128, :], x1[:])

        if SAMPLE:
            dtcol = fw.sb([16, 4], F32, "dtcol")
            fw.dma("sp", dtcol[:], io.dtcol[:, :])
            fw.act(dtcol[:, 3:4], dtcol[:, 1:2], AF.Exp)
            fw.ts("dve", dtcol[:, 3:4], dtcol[:, 3:4], -1.0, None, ALU.mult)
            cst_s = fw.sb([128, 12, 3, NB], F32, "cst_s")
            fw.dma("sp", cst_s[:], io.conv_s_in[:, 0:12, :, :])
            uS = fw.sb([128, 12, NB], F32, "uS")
            accs = fw.sb([128, 12, NB], F32, "accs")
            tmps = fw.sb([128, 12, NB], F32, "tmps")
            cs = fw.sb([128, 12, NB], F32, "cs")
            zsT = fw.sb([128, 8, NB], F32, "zsT")
            dd = fw.sb([16, 48], F32, "dd")
            dx = fw.sb([128, 8, 48], F32, "dx")
            dtx = fw.sb([128, 8, NB], F32, "dtx")
            BCtok = fw.sb([16, 512], F32, "BCtok")
            Sb = [fw.sb([128, 8, 128], F32, f"Sb{i}") for i in range(2)]
            T1s = fw.sb([128, 8, 128], F32, "T1s")
            ysT = fw.sb([128, 8, NB], F32, "ysT")
            fw.dma("sp", xt[0:NB, :], io.xs[:, :])
            rmsnorm(xt[0:NB, :], gmix, xn[0:NB, :], NB, junk)
            to_feat(xn, xnT, NB)
            proj_feat(Wc, 0, 12, xnT, NB, lambda g0, n, ps: fw.cp("act", uS[:, g0:g0 + n, :], ps))
            proj_feat(Wz, 0, 8, xnT, NB, lambda g0, n, ps: fw.act(zsT[:, g0:g0 + n, :], ps, AF.Silu))
            wv = lambda j: convp[:, 0:12, j].bc(2, NB)
            fw.tt("dve", accs[:], cst_s[:, :, 0, :], wv(0), ALU.mult)
            fw.tt("dve", accs[:], accs[:], wv(4), ALU.add)
            for j in (1, 2):
                fw.tt("dve", tmps[:], cst_s[:, :, j, :], wv(j), ALU.mult)
                fw.tt("dve", accs[:], accs[:], tmps[:], ALU.add)
            fw.tt("dve", tmps[:], uS[:], wv(3), ALU.mult)
            fw.tt("dve", accs[:], accs[:], tmps[:], ALU.add)
            fw.act(cs[:], accs[:], AF.Silu)
            fw.dma("sp", io.conv_s[:, 0:12, 0:2, :], cst_s[:, :, 1:3, :])
            fw.dma("sp", io.conv_s[:, 0:12, 2, :], uS[:])
            for kt in range(8):
                fw.mm(PE[0:16, 0:16], Wdt[:, kt, :], xnT[:, kt, 0:NB], start=kt == 0, stop=kt == 7)
            fw.ts("dve", dd[:, 0:16], PE[0:16, 0:16], dtcol[:, 0:1], None, ALU.add)
            fw.act(dd[:, 0:16], dd[:, 0:16], AF.Exp)
            fw.act(dd[:, 0:16], dd[:, 0:16], AF.Ln, bias=1.0)
            fw.ts("dve", dd[:, 16:32], dd[:, 0:16], dtcol[:, 3:4], None, ALU.mult)
            fw.act(dd[:, 16:32], dd[:, 16:32], AF.Exp)
            fw.ts("dve", dd[:, 32:48], ones[0:16, 0:16], dtcol[:, 2:3], None, ALU.mult)
            for j in range(8):
                fw.mm(PE[:, 128 + j * 48:128 + (j + 1) * 48], exp16[:, j, :], dd[:, :])
            fw.cp("dve", dx[:], PE[:, 128:512].rearrange("p (j c) -> p j c", j=8))
            fw.tt("dve", dtx[:], dx[:, :, 0:16], cs[:, 0:8, :], ALU.mult)
            for i in range(4):
                fw.tr(PD[0:16, i * 128:(i + 1) * 128], cs[:, 8 + i, :], ident)
            fw.cp("dve", BCtok[:], PD[0:16, :])
            for b in range(NB):
                S = Sb[b % 2]
                fw.dma("sp", S[:], io.ssm_s_in[b].rearrange("(j q) n -> q j n", q=128))
                fw.mm(PC[:, :], sel16[:, b, :], BCtok[:, :])
                for g in range(2):
                    fw.tt("dve", T1s[:, 4 * g:4 * g + 4, :], PC[:, g * 128:(g + 1) * 128].bc(1, 4),
                          dtx[:, 4 * g:4 * g + 4, b].bc(2, 128), ALU.mult)
                fw.tt("pool", S[:], S[:], dx[:, :, 16 + b].bc(2, 128), ALU.mult)
                fw.tt("dve", S[:], S[:], T1s[:], ALU.add)
                fw.dma("sp", io.ssm_s[b].rearrange("(j q) n -> q j n", q=128), S[:])
                for g in range(2):
                    fw.tt("dve", T1s[:, 4 * g:4 * g + 4, :], S[:, 4 * g:4 * g + 4, :],
                          PC[:, 256 + g * 128:256 + (g + 1) * 128].bc(1, 4), ALU.mult)
                fw.red(ysT[:, :, b], T1s[:], ALU.add)
            fw.tt("dve", dtx[:], dx[:, :, 32:48], cs[:, 0:8, :], ALU.mult)
            fw.tt("dve", ysT[:], ysT[:], dtx[:], ALU.add)
            fw.tt("dve", ysT[:], ysT[:], zsT[:], ALU.mult)
            for j in range(8):
                fw.tr(PA[0:16, j * 128:(j + 1) * 128], ysT[:, j, :], ident)
            fw.cp("dve", t1[0:NB, :], PA[0:NB, :])
            for g in range(2):
                grp_rstd(t1[0:NB, g * 512:(g + 1) * 512], 512, nst[0:NB, 7:8], junk, NB)
                fw.stt(yn[0:NB, g * 512:(g + 1) * 512], t1[0:NB, g * 512:(g + 1) * 512], nst[0:NB, 7:8],
                       gssd[0:NB, g * 512:(g + 1) * 512], ALU.mult, ALU.mult)
            to_feat(yn, ynT, NB)
            for half in range(2):
                for kt in range(8):
                    fw.mm(PA[0:NB, half * 512:(half + 1) * 512], ynT[:, kt, 0:NB], Wo[:, kt, half * 512:(half + 1) * 512],
                          start=kt == 0, stop=kt == 7)
            fw.tt("dve", x1[0:NB, :], xt[0:NB, :], PA[0:NB, :], ALU.add)
            fw.dma("sp", io.s1s[:, :], x1[0:NB, :])
        fw.dma("sp", io.ssm_p[:, :], ST[:])
        fw.dma("sp", io.conv_p[:, 0:12, :], convin[:, :, 0:3])
        fw.release(base_mark)

    if "0b" in phases:
        Wx = fw.sb([128, 8, 1024], BF16, "Wx")
        load_w(Wx, io.w_in0[:, 2560:3584], 0, 8)
        Wg = fw.sb([128, 8, 1024], BF16, "Wg")
        load_w(Wg, io.w_in0[:, 3600:4624], 0, 8)
        Wif = fw.sb([128, 8, 16], BF16, "Wif")
        load_w(Wif, io.w_in0[:, 4616:4632], 0, 8, step=8)
        Wo = fw.sb([128, 8, D], BF16, "Wo")
        load_w(Wo, io.w_out0[1024:2048, :], 0, 8)
        BDq = fw.sb([128, 8, 128], BF16, "BDq")
        BDk = fw.sb([128, 8, 128], BF16, "BDk")
        BDv = fw.sb([128, 8, 128], BF16, "BDv")
        fw.dma("pool", BDq[:], io.bdq[:, :, :])
        fw.dma("pool", BDk[:], io.bdk[:, :, :])
        fw.dma("pool", BDv[:], io.bdv[:, :, :])
        gmix = fw.sb([128, D], F32, "gmix")
        fw.dma("sp", gmix[:], io.norm_mix[0, :].partition_broadcast(128))
        sm0 = fw.sb([128, 64], F32, "sm0")
        fw.dma("sp", sm0[:], io.small0.partition_broadcast(128))
        ib_bc, fb_bc = sm0[:, 48:52], sm0[:, 52:56]
        convp = fw.sb([128, 20, 5], F32, "convp")
        fw.dma("sp", convp[:], io.convp[:, :, :])
        mlcol = fw.sb([128, 8, 2], F32, "mlcol")
        fw.dma("sp", mlcol[:], io.mlcol[:, :, :])
        convin = fw.sb([128, 8, 131], F32, "convin")
        fw.memset("pool", convin[:], 0.0)
        Cst = fw.sb([128, 2, 4, 264], F32, "Cst")
        fw.memset("pool", Cst[:], 0.0)
        Cb = fw.sb([128, 2, 4, 264], BF16, "Cb")
        fw.memset("pool", Cb[:], 0.0)
        mprev = fw.sb([128, 4], F32, "mprev")
        fw.memset("pool", mprev[:], 0.0)
        xt = fw.sb([128, D], F32, "xt")
        junk = fw.sb([128, D], F32, "junk")
        xn = fw.sb([128, D], BF16, "xn")
        xnT = fw.sb([128, 8, 128], BF16, "xnT")
        acc = fw.sb([128, 8, 128], F32, "acc")
        accv = [fw.view(acc[:, i, :], f"acc{i}") for i in range(8)]
        cact = fw.sb([128, 8, 128], BF16, "cact")
        xmraw = fw.sb([128, 8, 128], BF16, "xmraw")
        sigoT = fw.sb([128, 8, 128], BF16, "sigoT")
        sm2 = fw.sb([128, 64], F32, "sm2")
        qT = fw.sb([128, 8, 128], BF16, "qT")
        kT = fw.sb([128, 8, 128], BF16, "kT")
        vtok = fw.sb([128, 4, 264], BF16, "vtok")
        fw.memset("pool", vtok[:], 1.0)
        kw_ = fw.sb([128, 4, 256], BF16, "kw")
        HT = [(fw.sb([128, 128], F32, f"Rh{i}"), fw.sb([128, 128], F32, f"dlm{i}"), fw.sb([128, 128], F32, f"Dm{i}"),
               fw.sb([128, 128], BF16, f"Sg{i}"), fw.sb([128, 128], BF16, f"SgT{i}"), fw.sb([128, 16], F32, f"hs{i}"),
               fw.sb([128, 258], F32, f"comb{i}"), fw.sb([128, 256], F32, f"hh{i}")) for i in range(2)]
        junk2 = [fw.sb([128, 256], F32, f"jk{i}") for i in range(2)]
        ktb = fw.sb([128, D], BF16, "ktb")
        mt = fw.sb([128, 16], F32, "mt")
        fw.memset("pool", mt[:], 0.0)
        fw.memset("pool", sm2[:], 0.0)
        hmn = fw.sb([128, D], BF16, "hmn")
        hmnT = fw.sb([128, 8, 128], BF16, "hmnT")
        hmfT = fw.sb([128, 8, 128], BF16, "hmfT")
        x1 = fw.sb([128, D], F32, "x1")

        lvl = cfg.get('lvl', 99)
        for c in range(NCH):
            fw.dma("sp", xt[:], io.xp[c * 128:(c + 1) * 128, :])
            fw.dma("sp", x1[:], io.s1[c * 128:(c + 1) * 128, :])
            rmsnorm(xt[:], gmix, xn[:], 128, junk)
            to_feat(xn, xnT, 128)
            proj_feat(Wx, 0, 8, xnT, 128, lambda g0, n, ps: fw.cp("act", convin[:, g0:g0 + n, 3:131], ps))
            proj_feat(Wg, 0, 8, xnT, 128, lambda g0, n, ps: fw.act(sigoT[:, g0:g0 + n, :], ps, AF.Sigmoid))
            for kt in range(8):
                fw.mm(PE[:, 16:32], xnT[:, kt, :], Wif[:, kt, :], start=kt == 0, stop=kt == 7)
            fw.tt("dve", sm2[:, 0:4], PE[:, 24:28], ib_bc, ALU.add)
            fw.tt("dve", sm2[:, 4:8], PE[:, 28:32], fb_bc, ALU.add)
            conv_tiles(convin, convp, accv, 12, 8)
            for i in range(8):
                fw.act(cact[:, i, :], accv[i][:, :], AF.Silu)
            fw.cp("pool", xmraw[:], convin[:, :, 3:131])
            fw.cp("pool", convin[:, :, 0:3], convin[:, :, 128:131])
            if lvl < 2:
                continue
            for tile in range(8):
                ps = pcd[tile % 2]
                fw.mm(ps[:, 0:128], (Wx[:, tile, 0:128] if cfg.get('alt') else BDq[:, tile, :]), cact[:, tile, :])
                fw.mm(ps[:, 128:256], (Wx[:, tile, 0:128] if cfg.get('alt') else BDk[:, tile, :]), cact[:, tile, :])
                if cfg.get('alt') != 2:
                    fw.cp("dve", qT[:, tile, :], ps[:, 0:128])
                if cfg.get('alt') not in (2, 3):
                    fw.ts("dve", kT[:, tile, :], ps[:, 128:256], 0.0625, None, ALU.mult)
            if lvl < 2.1:
                continue
            for tile in range(8):
                fw.mm(PA[:, tile * 128:(tile + 1) * 128], xmraw[:, tile, :], BDv[:, tile, :])
                fw.mm(PB[:, tile * 128:(tile + 1) * 128], cact[:, tile, :], BDk[:, tile, :])
            if lvl < 2.2:
                continue
            fw.cp("act", vtok[:, :, 0:256], PA[:, :].rearrange("p (h v) -> p h v", h=4))
            fw.cp("dve", ktb[:], PB[:, :])
            if lvl < 2.3:
                continue
            fw.act(sm2[:, 4:8], sm2[:, 4:8], AF.Exp, scale=-1.0)
            fw.act(sm2[:, 4:8], sm2[:, 4:8], AF.Ln, bias=1.0)
            fw.ts("dve", sm2[:, 4:8], sm2[:, 4:8], -1.0, None, ALU.mult)
            fw.mm(PE[:, 64:80], tri_le, sm2[:, 0:16])
            fw.mm(PE[:, 96:112], ones, sm2[:, 0:16])
            fw.cp("dve", sm2[:, 8:12], PE[:, 68:72])
            fw.cp("dve", sm2[:, 12:16], PE[:, 100:104])
            fw.tt("dve", sm2[:, 16:20], sm2[:, 8:12], mprev[:], ALU.add)
            if lvl < 3:
                continue
            def head_gen(h, pi):
                Rh, dlm, Dm, Sg, SgT, hs, comb, hh = HT[pi]
                Pd = pcd[pi]
                Pn = [PA, PB][pi]
                fw.ts("dve", Rh[:], mask_gt, sm2[:, 4 + h:5 + h], None, ALU.mult)
                fw.stt(Rh[:], ident, sm2[:, h:h + 1], Rh[:], ALU.mult, ALU.add)
                yield
                fw.mm(Pd[:, 0:128], tri_le, Rh[:])
                fw.mm(Pd[:, 128:256], qT[:, 2 * h, :], kT[:, 2 * h, :], start=True, stop=False)
                fw.mm(Pd[:, 128:256], qT[:, 2 * h + 1, :], kT[:, 2 * h + 1, :], start=False, stop=True)
                fw.mm(Pn[:, 512:770], qT[:, 2 * h, :], Cb[:, 0, h, 0:258], start=True, stop=False)
                fw.mm(Pn[:, 512:770], qT[:, 2 * h + 1, :], Cb[:, 1, h, 0:258], start=False, stop=True)
                yield
                fw.tt("dve", dlm[:], Pd[:, 0:128], negmask, ALU.add)
                yield
                fw.red(hs[:, 0:1], dlm[:], ALU.max)
                yield
                fw.tt("dve", mt[:, h:h + 1], hs[:, 0:1], sm2[:, 16 + h:17 + h], ALU.max)
                yield
                fw.ts("dve", hs[:, 1:2], mt[:, h:h + 1], -1.0, None, ALU.mult)
                yield
                fw.act(Dm[:], dlm[:], AF.Exp, bias=hs[:, 1:2])
                fw.act(hs[:, 2:3], sm2[:, 16 + h:17 + h], AF.Exp, bias=hs[:, 1:2])
                fw.act(hs[:, 3:4], mt[:, h:h + 1], AF.Exp, scale=-1.0)
                yield
                fw.tt("dve", Sg[:], Pd[:, 128:256], Dm[:], ALU.mult)
                yield
                fw.tr(PT[:, pi * 128:(pi + 1) * 128], Sg[:], identb[:, :])
                yield
                fw.cp("dve", SgT[:], PT[:, pi * 128:(pi + 1) * 128])
                fw.act(comb[:], Pn[:, 512:770], AF.Copy, scale=hs[:, 2:3])
                yield
                fw.mm(Pn[:, 0:258], SgT[:], vtok[:, h, 0:258])
                yield
                fw.tt("dve", comb[:], comb[:], Pn[:, 0:258], ALU.add)
                yield
                fw.ts("dve", hs[:, 6:7], comb[:, 256:257], -1.0, None, ALU.mult)
                yield
                fw.tt("dve", hs[:, 6:7], hs[:, 6:7], comb[:, 256:257], ALU.max)
                yield
                fw.tt("dve", hs[:, 4:5], hs[:, 6:7], hs[:, 3:4], ALU.max)
                yield
                fw.recip(hs[:, 5:6], hs[:, 4:5])
                yield
                fw.ts("dve", hh[:], comb[:, 0:256], hs[:, 5:6], None, ALU.mult)
                yield
                fw.act(junk2[pi][:, 0:256], hh[:], AF.Square, accum_out=hs[:, 8:9])
                yield
                fw.ts("dve", hs[:, 9:10], hs[:, 8:9], 1.0 / 256, EPS, ALU.mult, ALU.add)
                yield
                fw.act(hs[:, 10:11], hs[:, 9:10], AF.Ln)
                fw.act(hs[:, 11:12], hs[:, 10:11], AF.Exp, scale=-0.5)
                yield
                fw.ts("dve", hmn[:, h * 256:(h + 1) * 256], hh[:], hs[:, 11:12], None, ALU.mult)
                yield

            for h0 in (0, 2):
                for _ in zip(head_gen(h0, 0), head_gen(h0 + 1, 1)):
                    pass
            to_feat(hmn, hmnT, 128)
            for tile in range(8):
                fw.ts("dve", hmfT[:, tile, :], hmnT[:, tile, :], mlcol[:, tile, 0:1], None, ALU.mult)
                fw.stt(hmfT[:, tile, :], cact[:, tile, :], mlcol[:, tile, 1:2], hmfT[:, tile, :], ALU.mult, ALU.add)
            fw.tt("dve", hmfT[:], hmfT[:], sigoT[:], ALU.mult)
            if lvl < 5:
                continue
            fw.mm(PE[:, 112:128], sel127, mt[:])
            fw.cp("dve", sm2[:, 20:24], PE[:, 112:116])
            fw.tt("dve", sm2[:, 24:28], sm2[:, 12:16], sm2[:, 8:12], ALU.subtract)
            fw.tt("dve", sm2[:, 24:28], sm2[:, 24:28], sm2[:, 0:4], ALU.add)
            fw.tt("dve", sm2[:, 24:28], sm2[:, 24:28], sm2[:, 20:24], ALU.subtract)
            fw.act(sm2[:, 28:32], sm2[:, 24:28], AF.Exp)
            fw.ts("dve", sm2[:, 28:32], sm2[:, 28:32], 0.0625, None, ALU.mult)
            fw.tt("dve", sm2[:, 32:36], sm2[:, 12:16], mprev[:], ALU.add)
            fw.tt("dve", sm2[:, 32:36], sm2[:, 32:36], sm2[:, 20:24], ALU.subtract)
            fw.act(sm2[:, 32:36], sm2[:, 32:36], AF.Exp)
            fw.tt("dve", kw_[:], ktb[:, :].rearrange("p (h d) -> p h d", h=4), sm2[:, 28:32].bc(2, 256), ALU.mult)
            for kt in range(2):
                for h in range(4):
                    fw.mm(PB[:, h * 256:(h + 1) * 256], kw_[:, h, kt * 128:(kt + 1) * 128], vtok[:, h, 0:256])
                    fw.mm(PE[:, 80 + 2 * h:82 + 2 * h], kw_[:, h, kt * 128:(kt + 1) * 128], onesb[:, 0:2])
                for h in range(4):
                    fw.stt(Cst[:, kt, h, 0:256], Cst[:, kt, h, 0:256], sm2[:, 32 + h:33 + h],
                           PB[:, h * 256:(h + 1) * 256], ALU.mult, ALU.add)
                    fw.stt(Cst[:, kt, h, 256:257], Cst[:, kt, h, 256:257], sm2[:, 32 + h:33 + h],
                           PE[:, 80 + 2 * h:81 + 2 * h], ALU.mult, ALU.add)
            fw.cp("act", Cb[:], Cst[:])
            fw.cp("dve", mprev[:], sm2[:, 20:24])
            if lvl < 6:
                continue
            for half in range(2):
                for kt in range(8):
                    fw.mm(PA[:, half * 512:(half + 1) * 512], hmfT[:, kt, :], Wo[:, kt, half * 512:(half + 1) * 512],
                          start=kt == 0, stop=kt == 7)
            fw.tt("dve", x1[:], x1[:], PA[:, :], ALU.add)
            fw.dma("sp", io.s1[c * 128:(c + 1) * 128, :], x1[:])

        if SAMPLE:
            cst_s = fw.sb([128, 8, 3, NB], F32, "cst_s")
            fw.dma("sp", cst_s[:], io.conv_s_in[:, 12:20, :, :])
            uS = fw.sb([128, 8, NB], F32, "uS")
            accs = fw.sb([128, 8, NB], F32, "accs")
            tmps = fw.sb([128, 8, NB], F32, "tmps")
            cs = fw.sb([128, 8, NB], F32, "cs")
            cs_bf = fw.sb([128, 8, NB], BF16, "cs_bf")
            us_bf = fw.sb([128, 8, NB], BF16, "us_bf")
            qTs = fw.sb([128, 8, NB], F32, "qTs")
            kTs = fw.sb([128, 8, NB], F32, "kTs")
            kws = fw.sb([128, 8, NB], F32, "kws")
            nS = fw.sb([128, 8, NB], F32, "nS")
            vtoks = fw.sb([16, D], F32, "vtoks")
            g16 = fw.sb([16, 64], F32, "g16")
            Zd = fw.sb([16, 128], F32, "Zd")
            wd = fw.sb([128, 2, 4, NB], F32, "wd")
            qmask = fw.sb([128, 8, NB, NB], F32, "qmask")
            Cs = [fw.sb([128, 8, 256], F32, f"Cs{i}") for i in range(2)]
            Tt = fw.sb([128, 8, 256], F32, "Tt")
            numt = fw.sb([16, D], F32, "numt")
            fw.dma("sp", xt[0:NB, :], io.xs[:, :])
            fw.dma("sp", x1[0:NB, :], io.s1s[:, :])
            fw.dma("sp", g16[:, 8:12], io.mm_s_in[:, :])
            fw.dma("sp", nS[:], io.mn_s_in[:, :, :])
            rmsnorm(xt[0:NB, :], gmix, xn[0:NB, :], NB, junk)
            to_feat(xn, xnT, NB)
            proj_feat(Wx, 0, 8, xnT, NB, lambda g0, n, ps: fw.cp("act", uS[:, g0:g0 + n, :], ps))
            proj_feat(Wg, 0, 8, xnT, NB, lambda g0, n, ps: fw.act(sigoT[:, g0:g0 + n, 0:NB], ps, AF.Sigmoid))
            for kt in range(8):
                fw.mm(PE[0:NB, 16:32], xnT[:, kt, 0:NB], Wif[:, kt, :], start=kt == 0, stop=kt == 7)
            fw.tt("dve", g16[:, 0:4], PE[0:NB, 24:28], ib_bc[0:NB, :], ALU.add)
            fw.tt("dve", g16[:, 4:8], PE[0:NB, 28:32], fb_bc[0:NB, :], ALU.add)
            fw.act(g16[:, 4:8], g16[:, 4:8], AF.Exp, scale=-1.0)
            fw.act(g16[:, 4:8], g16[:, 4:8], AF.Ln, bias=1.0)
            fw.ts("dve", g16[:, 4:8], g16[:, 4:8], -1.0, None, ALU.mult)
            wv = lambda j: convp[:, 12:20, j].bc(2, NB)
            fw.tt("dve", accs[:], cst_s[:, :, 0, :], wv(0), ALU.mult)
            fw.tt("dve", accs[:], accs[:], wv(4), ALU.add)
            for j in (1, 2):
                fw.tt("dve", tmps[:], cst_s[:, :, j, :], wv(j), ALU.mult)
                fw.tt("dve", accs[:], accs[:], tmps[:], ALU.add)
            fw.tt("dve", tmps[:], uS[:], wv(3), ALU.mult)
            fw.tt("dve", accs[:], accs[:], tmps[:], ALU.add)
            fw.act(cs[:], accs[:], AF.Silu)
            fw.dma("sp", io.conv_s[:, 12:20, 0:2, :], cst_s[:, :, 1:3, :])
            fw.dma("sp", io.conv_s[:, 12:20, 2, :], uS[:])
            fw.cp("dve", cs_bf[:], cs[:])
            fw.cp("dve", us_bf[:], uS[:])
            for tile in range(8):
                ps = pcd[tile % 2]
                fw.mm(ps[:, 0:NB], BDq[:, tile, :], cs_bf[:, tile, :])
                fw.mm(ps[:, 16:16 + NB], BDk[:, tile, :], cs_bf[:, tile, :])
                fw.cp("dve", qTs[:, tile, :], ps[:, 0:NB])
                fw.ts("dve", kTs[:, tile, :], ps[:, 16:16 + NB], 0.0625, None, ALU.mult)
            for tile in range(8):
                fw.mm(PA[0:NB, tile * 128:(tile + 1) * 128], us_bf[:, tile, :], BDv[:, tile, :])
            fw.cp("act", vtoks[:], PA[0:NB, :])
            fw.tt("dve", g16[:, 16:20], g16[:, 4:8], g16[:, 8:12], ALU.add)
            fw.tt("dve", g16[:, 12:16], g16[:, 16:20], g16[:, 0:4], ALU.max)
            fw.dma("sp", io.mm_s[:, :], g16[:, 12:16])
            fw.tt("dve", g16[:, 20:24], g16[:, 0:4], g16[:, 12:16], ALU.subtract)
            fw.act(g16[:, 20:24], g16[:, 20:24], AF.Exp)
            fw.tt("dve", g16[:, 24:28], g16[:, 16:20], g16[:, 12:16], ALU.subtract)
            fw.act(g16[:, 24:28], g16[:, 24:28], AF.Exp)
            fw.act(g16[:, 28:32], g16[:, 12:16], AF.Exp, scale=-1.0)
            z3 = lambda r: r.rearrange("p (h b) -> p h b", h=4)
            fw.tt("dve", z3(Zd[:, 0:64]), g16[:, 20:24].bc(2, NB), ident[0:NB, 0:NB].bc(1, 4), ALU.mult)
            fw.tt("dve", z3(Zd[:, 64:128]), g16[:, 24:28].bc(2, NB), ident[0:NB, 0:NB].bc(1, 4), ALU.mult)
            fw.mm(PE[:, 128:256], ones[0:NB, :], Zd[:, :])
            fw.cp("dve", wd[:], PE[:, 128:256].rearrange("p (w h b) -> p w h b", w=2, h=4))
            k4 = lambda r: r.rearrange("p (h k) b -> p h k b", h=4)
            fw.tt("dve", k4(kws[:, :, :]), k4(kTs[:, :, :]), wd[:, 0, :, :].bc(2, 2), ALU.mult)
            fw.tt("dve", k4(nS[:, :, :]), k4(nS[:, :, :]), wd[:, 1, :, :].bc(2, 2), ALU.mult)
            fw.tt("dve", nS[:], nS[:], kws[:], ALU.add)
            fw.dma("sp", io.mn_s[:, :, :], nS[:])
            fw.tt("dve", tmps[:], qTs[:], nS[:], ALU.mult)
            for h in range(4):
                for kt in range(2):
                    fw.mm(PE[0:NB, 256 + 2 * h:258 + 2 * h], tmps[:, 2 * h + kt, :], ones[:, 0:2], start=kt == 0, stop=kt == 1)
            fw.tt("dve", qmask[:], qTs[:, :, :].bc(2, NB), eye16[:, :, :].bc(1, 8), ALU.mult)
            for b in range(NB):
                Cc = Cs[b % 2]
                fw.dma("sp", Cc[:], io.mc_s_in[b].rearrange("h (k p) v -> p (h k) v", p=128))
                fw.mm(PA[:, 0:512], sel16[:, b, :], vtoks[:, 0:512])
                fw.mm(PA[:, 512:1024], sel16[:, b, :], vtoks[:, 512:1024])
                fw.tt("dve", Tt[:, :, :].rearrange("p (h k) v -> p h k v", h=4),
                      PA[:, :].rearrange("p (h v) -> p h v", h=4).bc(2, 2),
                      kws[:, :, b].rearrange("p (h k) -> p h k", h=4).bc(3, 256), ALU.mult)
                fw.tt("pool", Cc[:, :, :].rearrange("p (h k) v -> p h (k v)", h=4),
                      Cc[:, :, :].rearrange("p (h k) v -> p h (k v)", h=4), wd[:, 1, :, b].bc(2, 512), ALU.mult)
                fw.tt("dve", Cc[:], Cc[:], Tt[:], ALU.add)
                fw.dma("sp", io.mc_s[b].rearrange("h (k p) v -> p (h k) v", p=128), Cc[:])
                for tile in range(8):
                    h, kt = tile // 2, tile % 2
                    fw.mm(PB[0:NB, h * 256:(h + 1) * 256], qmask[:, tile, b, :], Cc[:, tile, :],
                          start=(b == 0 and tile in (0, 4)), stop=(b == NB - 1 and kt == 1), skip=True)
            fw.cp("act", numt[:], PB[0:NB, :])
            if "dbg_a" in dbg:
                fw.dma("sp", io.dbg_a[:, :], numt[:])
                fw.cp("dve", g16[:, 40:44], PE[0:NB, 256:264].rearrange("p (h t) -> p h t", t=2)[:, :, 0])
                fw.dma("sp", io.dbg_b[:, :], g16[:])
            dn = PE[0:NB, 256:264].rearrange("p (h t) -> p h t", t=2)[:, :, 0]
            fw.ts("dve", g16[:, 32:36], dn, -1.0, None, ALU.mult)
            fw.tt("dve", g16[:, 32:36], g16[:, 32:36], dn, ALU.max)
            fw.tt("dve", g16[:, 32:36], g16[:, 32:36], g16[:, 28:32], ALU.max)
            fw.recip(g16[:, 36:40], g16[:, 32:36])
            fw.tt("dve", numt[:, :].rearrange("p (h v) -> p h v", h=4), numt[:, :].rearrange("p (h v) -> p h v", h=4),
                  g16[:, 36:40].bc(2, 256), ALU.mult)
            for h in range(4):
                grp_rstd(numt[:, h * 256:(h + 1) * 256], 256, nst[0:NB, 7:8], junk, NB)
                fw.ts("dve", hmn[0:NB, h * 256:(h + 1) * 256], numt[:, h * 256:(h + 1) * 256], nst[0:NB, 7:8], None, ALU.mult)
            to_feat(hmn, hmnT, NB)
            for tile in range(8):
                fw.ts("dve", hmfT[:, tile, 0:NB], hmnT[:, tile, 0:NB], mlcol[:, tile, 0:1], None, ALU.mult)
                fw.stt(hmfT[:, tile, 0:NB], cs_bf[:, tile, :], mlcol[:, tile, 1:2], hmfT[:, tile, 0:NB], ALU.mult, ALU.add)
            fw.tt("dve", hmfT[:, :, 0:NB], hmfT[:, :, 0:NB], sigoT[:, :, 0:NB], ALU.mult)
            for half in range(2):
                for kt in range(8):
                    fw.mm(PA[0:NB, half * 512:(half + 1) * 512], hmfT[:, kt, 0:NB], Wo[:, kt, half * 512:(half + 1) * 512],
                          start=kt == 0, stop=kt == 7)
            fw.tt("dve", x1[0:NB, :], x1[0:NB, :], PA[0:NB, :], ALU.add)
            fw.dma("sp", io.s1s[:, :], x1[0:NB, :])
        fw.dma("sp", io.mc_p[:, :, :, :], Cst[:])
        fw.dma("sp", io.mm_p[:, :], mprev[0:1, :])
        fw.dma("sp", io.conv_p[:, 12:20, :], convin[:, :, 0:3])
        fw.release(base_mark)

    def ffn_phase(layer, src, dst, ssrc, sdst, final):
        Wgu = fw.sb([128, 8, 2 * DFF], BF16, "Wgu")
        load_w(Wgu, io.w_gu[layer], 0, 8, step=1)
        Wd = fw.sb([128, 22, D], BF16, "Wd")
        load_w(Wd, io.w_dn[layer], 0, 22)
        gf = fw.sb([128, D], F32, "gf")
        fw.dma("sp", gf[:], io.norm_ffn[layer, :].partition_broadcast(128))
        if final:
            gfin = fw.sb([128, D], F32, "gfin")
            fw.dma("sp", gfin[:], io.norm_final.partition_broadcast(128))
        GB = 4
        xt = fw.sb([128, D], F32, "xt")
        junk = fw.sb([128, D], F32, "junk")
        xn = fw.sb([128, D], BF16, "xn")
        xnT = fw.sb([128, 8, GB * 128], BF16, "xnT")
        hT = fw.sb([128, 22, GB * 128], BF16, "hT")
        sg = [fw.sb([128, GB * 128], F32, f"sg{i}") for i in range(2)]
        x2 = fw.sb([128, D], F32, "x2")
        yo = junk
        groups = [list(range(g, min(g + GB, NCH))) for g in range(0, NCH, GB)]
        if SAMPLE:
            groups.append([NCH])
        for grp in groups:
            samp = grp[0] == NCH
            M = NB if samp else 128
            W = M * len(grp)
            rows = lambda ap, c: (ap[:, :] if samp else ap[c * 128:(c + 1) * 128, :])
            for gi, c in enumerate(grp):
                fw.dma("sp", xt[0:M, :], rows(ssrc if samp else src, c))
                rmsnorm(xt[0:M, :], gf, xn[0:M, :], M, junk)
                for kt in range(8):
                    fw.tr(PT3[:, kt, 0:M], xn[0:M, kt * 128:(kt + 1) * 128], identb[0:M, 0:M])
                fw.cp("dve", xnT[:, :, gi * M:(gi + 1) * M], PT3[:, :, 0:M])
            for j in range(22):
                psg = pcd[j % 2]
                for kt in range(8):
                    fw.mm(psg[:, 0:W], Wgu[:, kt, j * 128:(j + 1) * 128], xnT[:, kt, 0:W], start=kt == 0, stop=kt == 7)
                psu = PB0f if j % 2 == 0 else PB1f
                for kt in range(8):
                    fw.mm(psu[:, 0:W], Wgu[:, kt, DFF + j * 128:DFF + (j + 1) * 128], xnT[:, kt, 0:W],
                          start=kt == 0, stop=kt == 7)
                fw.act(sg[j % 2][:, 0:W], psg[:, 0:W], AF.Silu)
                fw.tt("dve", hT[:, j, 0:W], sg[j % 2][:, 0:W], psu[:, 0:W], ALU.mult)
            for gi, c in enumerate(grp):
                for half in range(2):
                    for j in range(22):
                        fw.mm(PA[0:M, half * 512:(half + 1) * 512], hT[:, j, gi * M:(gi + 1) * M],
                              Wd[:, j, half * 512:(half + 1) * 512], start=j == 0, stop=j == 21)
                fw.dma("sp", xt[0:M, :], rows(ssrc if samp else src, c))
                fw.tt("dve", x2[0:M, :], xt[0:M, :], PA[0:M, :], ALU.add)
                if final:
                    rmsnorm(x2[0:M, :], gfin, yo[0:M, :], M, junk)
                    fw.dma("sp", rows(sdst if samp else dst, c), yo[0:M, :])
                else:
                    fw.dma("sp", rows(sdst if samp else dst, c), x2[0:M, :])
        fw.release(base_mark)

    if "1" in phases:
        ffn_phase(0, io.s1, io.s2, io.s1s, io.s2s, False)

    if "2" in phases:
        Wr = fw.sb([128, 8, D], BF16, "Wr"); load_w(Wr, io.rw_wr, 0, 8)
        Wk = fw.sb([128, 8, D], BF16, "Wk"); load_w(Wk, io.rw_wk, 0, 8)
        A1 = fw.sb([128, 8, 64], BF16, "A1"); load_w(A1, io.rw_a1, 0, 8, step=8)
        A2 = fw.sb([128, D], BF16, "A2"); fw.dma("pool", A2[0:64, :], io.rw_a2[:, :])
        Wv = fw.sb([128, 8, D], BF16, "Wv"); load_w(Wv, io.rw_wv, 0, 8)
        G1 = fw.sb([128, 8, 160], BF16, "G1"); load_w(G1, io.rw_g1, 0, 8, step=8)
        G2a = fw.sb([128, D], BF16, "G2a"); fw.dma("pool", G2a[:], io.rw_g2[0:128, :])
        G2b = fw.sb([128, D], BF16, "G2b"); fw.dma("pool", G2b[0:32, :], io.rw_g2[128:160, :])
        W1 = fw.sb([128, 8, 64], BF16, "W1"); load_w(W1, io.rw_w1, 0, 8, step=8)
        W2 = fw.sb([128, D], BF16, "W2"); fw.dma("pool", W2[0:64, :], io.rw_w2[:, :])
        Wo = fw.sb([128, 8, D], BF16, "Wo"); load_w(Wo, io.rw_wo, 0, 8)
        gm1 = fw.sb([128, D], F32, "gm1")
        fw.dma("sp", gm1[:], io.norm_mix[1, :].partition_broadcast(128))
        rows = []
        for i in range(7):
            rt = fw.sb([128, D], F32, f"row{i}")
            fw.dma("sp", rt[:], io.rw_rows[i, :].partition_broadcast(128))
            rows.append(rt)
        w0b, a0b, kkb_, kab, rkb, lnw, lnb = rows
        mu = fw.sb([128, 8, 6], F32, "mu")
        fw.dma("sp", mu[:], io.rw_mu[:, :, :])
        PB0 = fw.view(PB[:, 0:512], "PB0")
        PB1 = fw.view(PB[:, 512:1024], "PB1")
        NPS = [PB0, PB1, PC, PD]
        PAh = [PA[:, 0:512], PA[:, 512:1024]]
        PBh = [PB0[:, :], PB1[:, :]]
        h1 = fw.sb([128, 2, 128], BF16, "h1")

        def proj_tok(xT, W, Ph, M=128):
            for half in range(2):
                for kt in range(8):
                    fw.mm(Ph[half][0:M, :], xT[:, kt, 0:M], W[:, kt, half * 512:(half + 1) * 512],
                          start=kt == 0, stop=kt == 7)

        def lora(xT, Wa, nh, Wb_list, func, P, M=128):
            widths = [min(128, nh), nh - 128] if nh > 128 else [nh]
            for wi, wd in enumerate(widths):
                for kt in range(8):
                    fw.mm(PE[0:wd, wi * 128:wi * 128 + M], Wa[:, kt, wi * 128:wi * 128 + wd], xT[:, kt, 0:M],
                          start=kt == 0, stop=kt == 7)
                fw.act(h1[0:wd, wi, 0:M], PE[0:wd, wi * 128:wi * 128 + M], func)
            for half in range(2):
                for wi, wd in enumerate(widths):
                    fw.mm(P[half][0:M, :], h1[0:wd, wi, 0:M], Wb_list[wi][0:wd, half * 512:(half + 1) * 512],
                          start=wi == 0, stop=wi == len(widths) - 1)

        def rstd16(src16, dst16, mult_, eps, floor=None):
            if floor is not None:
                fw.ts("dve", dst16, src16, floor, None, ALU.max)
            else:
                fw.ts("dve", dst16, src16, mult_, eps, ALU.mult, ALU.add)
            fw.act(dst16, dst16, AF.Ln)
            fw.act(dst16, dst16, AF.Exp, scale=-0.5)

        mark2 = fw.mark()
        xt = fw.sb([128, D], F32, "xt")
        junk = fw.sb([128, D], F32, "junk")
        tmpA = fw.sb([128, D], F32, "tmpA")
        tmpB = fw.sb([128, D], F32, "tmpB")
        Et = fw.sb([128, D], F32, "Et")
        SB = [fw.sb([128, D], BF16, f"S{i}") for i in range(13)]
        xn = SB[0]; r_bf = SB[1]; kkn = SB[2]; kf_bf = SB[3]; b_bf = SB[4]; v_bf = SB[5]; bv = SB[6]
        g_bf = SB[7]; abar = SB[8]; bbar = SB[9]; kbar = SB[10]; btil = SB[11]; ktil = SB[12]
        rbar = SB[0]; yo = SB[8]
        xnTe = fw.sb([128, 8, 130], BF16, "xnTe")
        fw.memset("pool", xnTe[:], 0.0)
        xx = fw.sb([128, 8, 128], BF16, "xx")
        mixb = [fw.sb([128, 8, 128], BF16, f"mix{i}") for i in range(2)]
        arT = fw.sb([128, 8, 2, 128], BF16, "arT")
        bT = fw.sb([128, 8, 128], BF16, "bT")
        kT = fw.sb([128, 8, 128], BF16, "kT")
        yoT = fw.sb([128, 8, 128], BF16, "yoT")
        Ms = [fw.sb([128, 512], BF16, f"Ms{i}") for i in range(4)]
        Q0 = [fw.sb([128, 128], BF16, f"Q0{i}") for i in range(4)]
        PQ = [[fw.sb([128, 384], BF16, f"PQ{i}{k}") for k in range(2)] for i in range(4)]
        RHSb = [fw.sb([128, 64], BF16, f"RHS{i}") for i in range(4)]
        Ubp = [fw.sb([128, 2, 64], BF16, f"Ubp{i}") for i in range(2)]
        Hst = fw.sb([128, 8, 64], F32, "Hst")
        fw.memset("pool", Hst[:], 0.0)
        Hb = fw.sb([128, 8, 64], BF16, "Hb")
        fw.memset("pool", Hb[:], 0.0)
        eLT = fw.sb([128, 8], F32, "eLT")
        s16 = fw.sb([128, 64], F32, "s16")
        x3 = tmpB
        mcount = [0]

        def mix(cidx):
            dst = mixb[mcount[0] % 2]
            mcount[0] += 1
            for kt in range(8):
                fw.stt(dst[:, kt, :], xx[:, kt, :], mu[:, kt, cidx:cidx + 1], xnTe[:, kt, 1:129], ALU.mult, ALU.add)
            return dst

        for c in range(NCH):
            fw.dma("sp", xt[:], io.s2[c * 128:(c + 1) * 128, :])
            if c == NCH - 1:
                fw.act(junk[:, :], xt[:], AF.Square, accum_out=nst[:, 0:1])
                fw.ts("dve", nst[:, 1:2], nst[:, 0:1], 1.0 / D, EPS, ALU.mult, ALU.add)
                fw.act(nst[:, 2:3], nst[:, 1:2], AF.Ln)
                fw.act(nst[:, 3:4], nst[:, 2:3], AF.Exp, scale=-0.5)
                fw.stt(tmpA[:], xt[:], nst[:, 3:4], gm1[:], ALU.mult, ALU.mult)
                fw.dma("sp", io.shift_p[:, :], tmpA[127:128, :])
                fw.cp("dve", xn[:], tmpA[:])
            else:
                rmsnorm(xt[:], gm1, xn[:], 128, junk)
            for kt in range(8):
                fw.tr(PT3[:, kt, :], xn[:, kt * 128:(kt + 1) * 128], identb[:, :])
            fw.cp("dve", xnTe[:, :, 1:129], PT3)
            fw.tt("pool", xx[:], xnTe[:, :, 0:128], xnTe[:, :, 1:129], ALU.subtract)
            proj_tok(mix(0), Wr, PAh)
            fw.cp("act", r_bf[:], PA[:, :])
            proj_tok(mix(2), Wk, PAh)
            fw.tt("dve", tmpA[:], PA[:, :], kkb_[:], ALU.mult)
            fw.tt("pool", junk[:], tmpA[:], tmpA[:], ALU.mult)
            fw.red(s16[:, 0:16], v16(junk[:, :]), ALU.add)
            rstd16(s16[:, 0:16], s16[:, 16:32], None, None, floor=1e-24)
            fw.tt("dve", v16(kkn[:, :]), v16(tmpA[:, :]), s16[:, 16:32].bc(2, 64), ALU.mult)
            lora(mix(4), A1, 64, [A2], AF.Copy, PBh)
            for i in range(2):
                fw.tt("dve", tmpB[:, i * 512:(i + 1) * 512], PBh[i], a0b[:, i * 512:(i + 1) * 512], ALU.add)
            fw.act(tmpB[:], tmpB[:], AF.Sigmoid)
            fw.stt(junk[:], tmpB[:], 1.0, kab[:], ALU.subtract, ALU.mult)
            fw.ts("dve", junk[:], junk[:], 1.0, None, ALU.add)
            fw.tt("dve", kf_bf[:], PA[:, :], junk[:], ALU.mult)
            fw.tt("pool", b_bf[:], kkn[:], tmpB[:], ALU.mult)
            fw.tt("pool", junk[:], r_bf[:], kf_bf[:], ALU.mult)
            fw.tt("pool", junk[:], junk[:], rkb[:], ALU.mult)
            fw.red(s16[:, 32:48], v16(junk[:, :]), ALU.add)
            proj_tok(mix(3), Wv, PAh)
            fw.cp("act", v_bf[:], PA[:, :])
            fw.tt("dve", v16(bv[:, :]), v16(PA[:, :]), s16[:, 32:48].bc(2, 64), ALU.mult)
            lora(mix(5), G1, 160, [G2a, G2b], AF.Sigmoid, PBh)
            for i in range(2):
                fw.cp("act", g_bf[:, i * 512:(i + 1) * 512], PBh[i])
            lora(mix(1), W1, 64, [W2], AF.Tanh, PAh)
            fw.tt("dve", tmpA[:], PA[:, :], w0b[:], ALU.add)
            fw.act(tmpA[:], tmpA[:], AF.Exp, scale=-1.0)
            fw.act(tmpA[:], tmpA[:], AF.Ln, bias=1.0)
            fw.ts("dve", tmpA[:], tmpA[:], -1.0, -0.5, ALU.mult, ALU.add)
            fw.act(Et[:], tmpA[:], AF.Exp)
            fw.mm(PB0[:, :], tri_le, Et[:, 0:512])
            fw.mm(PB1[:, :], tri_le, Et[:, 512:1024])
            fw.mm(PA[:, 0:512], ones, Et[:, 0:512])
            fw.mm(PA[:, 512:1024], ones, Et[:, 512:1024])
            for kt in range(8):
                fw.mm(PE[:, kt * 16:(kt + 1) * 16], Et[:, kt * 128:(kt + 1) * 128], ones[:, 0:16])
            fw.act(eLT[:], PE[:, 0:128].rearrange("p (k s) -> p k s", s=16)[:, :, 0], AF.Exp, scale=-1.0)
            hv = lambda r, i: r[:, i * 512:(i + 1) * 512]
            for i, PBi in enumerate((PB0, PB1)):
                fw.act(hv(tmpA, i), PBi[:, :], AF.Exp, scale=-1.0)
                fw.tt("pool", hv(rbar, i), hv(r_bf, i), hv(tmpA, i), ALU.mult)
                fw.tt("dve", hv(tmpB, i), hv(Et, i), PBi[:, :], ALU.subtract)
                fw.act(hv(tmpB, i), hv(tmpB, i), AF.Exp)
                fw.stt(hv(abar, i), hv(kkn, i), -1.0, hv(tmpB, i), ALU.mult, ALU.mult)
            fw.cp("act", junk[:], PA[:, :])
            for i, PBi in enumerate((PB0, PB1)):
                fw.act(hv(tmpA, i), PBi[:, :], AF.Exp)
                fw.tt("pool", hv(bbar, i), hv(b_bf, i), hv(tmpA, i), ALU.mult)
                fw.tt("pool", hv(kbar, i), hv(kf_bf, i), hv(tmpA, i), ALU.mult)
                fw.tt("dve", hv(tmpB, i), PBi[:, :], hv(junk, i), ALU.subtract)
                fw.act(hv(tmpB, i), hv(tmpB, i), AF.Exp)
                fw.tt("pool", hv(btil, i), hv(b_bf, i), hv(tmpB, i), ALU.mult)
                fw.tt("dve", hv(ktil, i), hv(kf_bf, i), hv(tmpB, i), ALU.mult)
            for src, dst in ((abar, arT[:, :, 0, :]), (rbar, arT[:, :, 1, :]), (bbar, bT[:, :, :]), (kbar, kT[:, :, :])):
                for kt in range(8):
                    fw.tr(PT3[:, kt, :], src[:, kt * 128:(kt + 1) * 128], identb[:, :])
                fw.cp("dve", dst, PT3)
            for h0 in range(0, 16, 4):
                hd = []
                for i in range(4):
                    h = h0 + i
                    j, e = h // 2, h % 2
                    p0 = 64 * e
                    hd.append(dict(h=h, j=j, e=e, p0=p0, NP=NPS[i],
                                   aT=arT[p0:p0 + 64, j, 0, :], rT=arT[p0:p0 + 64, j, 1, :],
                                   ar=arT[p0:p0 + 64, j, :, :].rearrange("p a t -> p (a t)"),
                                   bT=bT[p0:p0 + 64, j, :], kT=kT[p0:p0 + 64, j, :]))
                for i, d in enumerate(hd):
                    fw.mm(PE[:, 0:256], d["bT"], d["ar"])
                    fw.mm(PE[:, 256:512], d["kT"], d["ar"])
                    fw.tt("dve", Ms[i][:], PE[:, :], m4, ALU.mult)
                    fw.mm(d["NP"][:, 0:128], d["aT"], d["bT"])
                    fw.tt("dve", Q0[i][:], d["NP"][:, 0:128], mask_gt, ALU.mult)
                    d["P"], d["Q"], d["Z"] = Ms[i][:, 0:128], Q0[i][:], identb[:, :]
                for k in range(7):
                    for i, d in enumerate(hd):
                        NP = d["NP"]
                        if k < 6:
                            fw.mm(NP[:, 0:128], d["Q"], d["P"])
                            fw.mm(NP[:, 128:256], d["P"], d["Q"])
                        fw.mm(NP[:, 256:384], identb[:, :], d["Z"], start=True, stop=False)
                        fw.mm(NP[:, 256:384], d["Q"], d["Z"], start=False, stop=True)
                    for i, d in enumerate(hd):
                        NP = d["NP"]
                        pq = PQ[i][k % 2]
                        lo = 0 if k < 6 else 256
                        fw.cp("act" if i % 2 == 0 else "dve", pq[:, lo:384], NP[:, lo:384])
                        d["P"], d["Q"], d["Z"] = pq[:, 0:128], pq[:, 128:256], pq[:, 256:384]
                for i, d in enumerate(hd):
                    NP, h, j, p0 = d["NP"], d["h"], d["j"], d["p0"]
                    fw.mm(NP[:, 384:448], d["aT"], Hb[p0:p0 + 64, j, :], start=True, stop=False)
                    fw.mm(NP[:, 384:448], Ms[i][:, 256:384], v_bf[:, h * 64:(h + 1) * 64], start=False, stop=True)
                    fw.cp("act", RHSb[i][:], NP[:, 384:448])
                for i, d in enumerate(hd):
                    NP, h, j, e = d["NP"], d["h"], d["j"], d["e"]
                    fw.mm(NP[:, 448:512], d["Z"], RHSb[i][:])
                    fw.cp("dve", Ubp[j % 2][:, e, :], NP[:, 448:512])
                for i, d in enumerate(hd):
                    h, j, e, p0 = d["h"], d["j"], d["e"], d["p0"]
                    ysl = PA[:, h * 64:(h + 1) * 64]
                    fw.mm(ysl, d["rT"], Hb[p0:p0 + 64, j, :], start=True, stop=False)
                    fw.mm(ysl, Ms[i][:, 128:256], Ubp[j % 2][:, e, :], start=False, stop=False)
                    fw.mm(ysl, Ms[i][:, 384:512], v_bf[:, h * 64:(h + 1) * 64], start=False, stop=True)
                for jj in range(2):
                    j = h0 // 2 + jj
                    NP = hd[2 * jj]["NP"]
                    fw.mm(NP[:, 0:128], btil[:, j * 128:(j + 1) * 128], Ubp[j % 2][:, :, :].rearrange("p e v -> p (e v)"),
                          start=True, stop=False)
                    fw.mm(NP[:, 0:128], ktil[:, j * 128:(j + 1) * 128], v_bf[:, j * 128:(j + 1) * 128],
                          start=False, stop=True)
                    for e in range(2):
                        p0 = 64 * e
                        fw.stt(Hst[p0:p0 + 64, j, :], Hst[p0:p0 + 64, j, :], eLT[p0:p0 + 64, j:j + 1],
                               NP[p0:p0 + 64, p0:p0 + 64], ALU.mult, ALU.add)
                    fw.cp("pool", Hb[:, j, :], Hst[:, j, :])
            fw.cp("act", tmpA[:], PA[:, :])
            fw.red(s16[:, 0:16], v16(tmpA[:, :]), ALU.add)
            fw.ts("dve", s16[:, 0:16], s16[:, 0:16], 1.0 / 64, None, ALU.mult)
            fw.tt("dve", v16(tmpA[:, :]), v16(tmpA[:, :]), s16[:, 0:16].bc(2, 64), ALU.subtract)
            fw.tt("pool", junk[:], tmpA[:], tmpA[:], ALU.mult)
            fw.red(s16[:, 16:32], v16(junk[:, :]), ALU.add)
            rstd16(s16[:, 16:32], s16[:, 48:64], 1.0 / 64, 64e-5)
            fw.tt("dve", v16(tmpA[:, :]), v16(tmpA[:, :]), s16[:, 48:64].bc(2, 64), ALU.mult)
            fw.tt("dve", tmpA[:], tmpA[:], lnw[:], ALU.mult)
            fw.tt("dve", tmpA[:], tmpA[:], lnb[:], ALU.add)
            fw.tt("dve", tmpA[:], tmpA[:], bv[:], ALU.add)
            fw.tt("dve", yo[:], tmpA[:], g_bf[:], ALU.mult)
            to_feat(yo, yoT, 128)
            for half in range(2):
                for kt in range(8):
                    fw.mm(PA[:, half * 512:(half + 1) * 512], yoT[:, kt, :], Wo[:, kt, half * 512:(half + 1) * 512],
                          start=kt == 0, stop=kt == 7)
            fw.tt("dve", x3[:], xt[:], PA[:, :], ALU.add)
            fw.dma("sp", io.s3[c * 128:(c + 1) * 128, :], x3[:])
            fw.cp("pool", xnTe[:, :, 0:1], xnTe[:, :, 128:129])
        fw.dma("sp", io.wkv_p[:, :, :], Hst[:])
        fw.release(mark2)
        if SAMPLE:
            M = NB
            f16 = lambda nm, dt=F32: fw.sb([NB, D], dt, nm)
            xts = f16("xts"); jk = f16("jk"); tA = f16("tA"); tB = f16("tB"); Es = f16("Es")
            r_s = f16("r_s", BF16); kk_s = f16("kk_s", BF16); kf_s = f16("kf_s", BF16); b_s = f16("b_s", BF16)
            v_s = f16("v_s"); bv_s = f16("bv_s", BF16); g_s = f16("g_s", BF16); xn_s = f16("xn_s", BF16)
            sa_tok = f16("sa_tok"); yo_s = f16("yo_s", BF16)
            xprev = fw.sb([128, 8, NB], F32, "xprev")
            xsT = fw.sb([128, 8, NB], BF16, "xsT")
            xxs = fw.sb([128, 8, NB], BF16, "xxs")
            mixs = [fw.sb([128, 8, NB], BF16, f"mixs{i}") for i in range(2)]
            featT = {nm: fw.sb([128, 8, NB], F32, nm) for nm in ("aT", "wT", "bTs", "kTs", "rTs")}
            amask = fw.sb([128, 8, NB, NB], F32, "amask")
            rmask = fw.sb([128, 8, NB, NB], F32, "rmask")
            Hs = [fw.sb([128, 8, 64], F32, f"Hs{i}") for i in range(2)]
            Tt = fw.sb([128, 8, 64], F32, "Tt")
            yoTs = fw.sb([128, 8, NB], BF16, "yoTs")
            s16 = fw.sb([NB, 64], F32, "s16s")
            v16s = lambda r: r.rearrange("p (h q) -> p h q", h=16)
            mc2 = [0]

            def mix_s(cidx):
                dst = mixs[mc2[0] % 2]
                mc2[0] += 1
                for kt in range(8):
                    fw.stt(dst[:, kt, :], xxs[:, kt, :], mu[:, kt, cidx:cidx + 1], xsT[:, kt, :], ALU.mult, ALU.add)
                return dst

            fw.dma("sp", xts[:], io.s2s[:, :])
            fw.dma("sp", xprev[:], io.shift_s_in[:, :, :])
            fw.act(jk[:], xts[:], AF.Square, accum_out=nst[0:M, 0:1])
            fw.ts("dve", nst[0:M, 1:2], nst[0:M, 0:1], 1.0 / D, EPS, ALU.mult, ALU.add)
            fw.act(nst[0:M, 2:3], nst[0:M, 1:2], AF.Ln)
            fw.act(nst[0:M, 3:4], nst[0:M, 2:3], AF.Exp, scale=-0.5)
            fw.stt(tA[:], xts[:], nst[0:M, 3:4], gm1[0:M, :], ALU.mult, ALU.mult)
            fw.dma("sp", io.shift_s[:, :], tA[:])
            fw.cp("dve", xn_s[:], tA[:])
            for kt in range(8):
                fw.tr(PT3[:, kt, 0:M], xn_s[:, kt * 128:(kt + 1) * 128], identb[0:M, 0:M])
            fw.cp("dve", xsT[:], PT3[:, :, 0:M])
            fw.tt("dve", xxs[:], xprev[:], xsT[:], ALU.subtract)
            PAm = [PA[0:M, 0:512], PA[0:M, 512:1024]]
            PBm = [PB0[0:M, :], PB1[0:M, :]]
            PAf = PA[0:M, :]
            proj_tok(mix_s(0), Wr, PAh, M)
            fw.cp("act", r_s[:], PAf)
            proj_tok(mix_s(2), Wk, PAh, M)
            fw.tt("dve", tA[:], PAf, kkb_[0:M, :], ALU.mult)
            fw.tt("dve", jk[:], tA[:], tA[:], ALU.mult)
            fw.red(s16[:, 0:16], v16s(jk[:, :]), ALU.add)
            rstd16(s16[:, 0:16], s16[:, 16:32], None, None, floor=1e-24)
            fw.tt("dve", v16s(kk_s[:, :]), v16s(tA[:, :]), s16[:, 16:32].bc(2, 64), ALU.mult)
            lora(mix_s(4), A1, 64, [A2], AF.Copy, PBh, M)
            for i in range(2):
                fw.tt("dve", tB[:, i * 512:(i + 1) * 512], PBm[i], a0b[0:M, i * 512:(i + 1) * 512], ALU.add)
            fw.act(tB[:], tB[:], AF.Sigmoid)
            fw.stt(jk[:], tB[:], 1.0, kab[0:M, :], ALU.subtract, ALU.mult)
            fw.ts("dve", jk[:], jk[:], 1.0, None, ALU.add)
            fw.tt("dve", kf_s[:], PAf, jk[:], ALU.mult)
            fw.tt("dve", b_s[:], kk_s[:], tB[:], ALU.mult)
            fw.tt("dve", jk[:], r_s[:], kf_s[:], ALU.mult)
            fw.tt("dve", jk[:], jk[:], rkb[0:M, :], ALU.mult)
            fw.red(s16[:, 32:48], v16s(jk[:, :]), ALU.add)
            proj_tok(mix_s(3), Wv, PAh, M)
            fw.cp("act", v_s[:], PAf)
            fw.tt("dve", v16s(bv_s[:, :]), v16s(PAf), s16[:, 32:48].bc(2, 64), ALU.mult)
            lora(mix_s(5), G1, 160, [G2a, G2b], AF.Sigmoid, PBh, M)
            for i in range(2):
                fw.cp("act", g_s[:, i * 512:(i + 1) * 512], PBm[i])
            lora(mix_s(1), W1, 64, [W2], AF.Tanh, PAh, M)
            fw.tt("dve", tA[:], PAf, w0b[0:M, :], ALU.add)
            fw.act(tA[:], tA[:], AF.Exp, scale=-1.0)
            fw.act(tA[:], tA[:], AF.Ln, bias=1.0)
            fw.ts("dve", tA[:], tA[:], -1.0, -0.5, ALU.mult, ALU.add)
            fw.act(Es[:], tA[:], AF.Exp)
            fw.act(Es[:], Es[:], AF.Exp, scale=-1.0)
            fw.ts("dve", tB[:], kk_s[:], -1.0, None, ALU.mult)
            for nm, src in (("aT", tB), ("wT", Es)):
                for kt in range(8):
                    fw.tr(PE[:, kt * 16:(kt + 1) * 16], src[:, kt * 128:(kt + 1) * 128], ident[0:M, 0:M])
                fw.cp("dve", featT[nm][:], PE[:, 0:128].rearrange("p (k b) -> p k b", k=8))
            for nm, src in (("bTs", b_s), ("kTs", kf_s), ("rTs", r_s)):
                for kt in range(8):
                    fw.tr(PT3[:, kt, 0:M], src[:, kt * 128:(kt + 1) * 128], identb[0:M, 0:M])
                fw.cp("dve", featT[nm][:], PT3[:, :, 0:M])
            fw.tt("dve", amask[:], featT["aT"][:, :, :].bc(2, NB), eye16[:, :, :].bc(1, 8), ALU.mult)
            fw.tt("dve", rmask[:], featT["rTs"][:, :, :].bc(2, NB), eye16[:, :, :].bc(1, 8), ALU.mult)
            SY = [PC, PD]
            for b in range(NB):
                H = Hs[b % 2]
                fw.dma("sp", H[:], io.wkv_s_in[b])
                for h in range(16):
                    j, e = h // 2, h % 2
                    p0 = 64 * e
                    fw.mm(SY[e][0:M, j * 64:(j + 1) * 64], amask[p0:p0 + 64, j, b, :], H[p0:p0 + 64, j, :],
                          start=(b == 0 and j == 0), stop=(b == NB - 1), skip=True)
            je = lambda r: r.rearrange("p (j e v) -> p j e v", j=8, e=2)
            fw.cp("dve", je(sa_tok[:, :])[:, :, 0, :], PC[0:M, :].rearrange("p (j v) -> p j v", j=8))
            fw.cp("dve", je(sa_tok[:, :])[:, :, 1, :], PD[0:M, :].rearrange("p (j v) -> p j v", j=8))
            e4 = lambda r, e: r.rearrange("p (j e v) -> p j e v", j=8, e=2)[64 * e:64 * e + 64, :, e, :]
            for b in range(NB):
                H = Hs[b % 2]
                fw.dma("sp", H[:], io.wkv_s_in[b])
                fw.mm(PA[:, 0:512], sel16[:, b, :], sa_tok[:, 0:512])
                fw.mm(PA[:, 512:1024], sel16[:, b, :], sa_tok[:, 512:1024])
                fw.mm(PB0[:, :], sel16[:, b, :], v_s[:, 0:512])
                fw.mm(PB1[:, :], sel16[:, b, :], v_s[:, 512:1024])
                fw.tt("pool", H[:], H[:], featT["wT"][:, :, b].bc(2, 64), ALU.mult)
                for e in range(2):
                    p0 = 64 * e
                    fw.tt("dve", Tt[p0:p0 + 64, :, :], e4(PA[:, :], e), featT["bTs"][p0:p0 + 64, :, b].bc(2, 64), ALU.mult)
                fw.tt("dve", H[:], H[:], Tt[:], ALU.add)
                for e in range(2):
                    p0 = 64 * e
                    for half, PBx in enumerate((PB0, PB1)):
                        src = PBx[:, :].rearrange("p (j e v) -> p j e v", j=4, e=2)[p0:p0 + 64, :, e, :]
                        fw.tt("dve", Tt[p0:p0 + 64, 4 * half:4 * half + 4, :], src,
                              featT["kTs"][p0:p0 + 64, 4 * half:4 * half + 4, b].bc(2, 64), ALU.mult)
                fw.tt("dve", H[:], H[:], Tt[:], ALU.add)
                fw.dma("sp", io.wkv_s[b], H[:])
                for h in range(16):
                    j, e = h // 2, h % 2
                    p0 = 64 * e
                    fw.mm(SY[e][0:M, j * 64:(j + 1) * 64], rmask[p0:p0 + 64, j, b, :], H[p0:p0 + 64, j, :],
                          start=(b == 0 and j == 0), stop=(b == NB - 1), skip=True)
            fw.cp("dve", je(tA[:, :])[:, :, 0, :], PC[0:M, :].rearrange("p (j v) -> p j v", j=8))
            fw.cp("dve", je(tA[:, :])[:, :, 1, :], PD[0:M, :].rearrange("p (j v) -> p j v", j=8))
            fw.red(s16[:, 0:16], v16s(tA[:, :]), ALU.add)
            fw.ts("dve", s16[:, 0:16], s16[:, 0:16], 1.0 / 64, None, ALU.mult)
            fw.tt("dve", v16s(tA[:, :]), v16s(tA[:, :]), s16[:, 0:16].bc(2, 64), ALU.subtract)
            fw.tt("dve", jk[:], tA[:], tA[:], ALU.mult)
            fw.red(s16[:, 16:32], v16s(jk[:, :]), ALU.add)
            rstd16(s16[:, 16:32], s16[:, 48:64], 1.0 / 64, 64e-5)
            fw.tt("dve", v16s(tA[:, :]), v16s(tA[:, :]), s16[:, 48:64].bc(2, 64), ALU.mult)
            fw.tt("dve", tA[:], tA[:], lnw[0:M, :], ALU.mult)
            fw.tt("dve", tA[:], tA[:], lnb[0:M, :], ALU.add)
            fw.tt("dve", tA[:], tA[:], bv_s[:], ALU.add)
            fw.tt("dve", yo_s[:], tA[:], g_s[:], ALU.mult)
            for kt in range(8):
                fw.tr(PT3[:, kt, 0:M], yo_s[:, kt * 128:(kt + 1) * 128], identb[0:M, 0:M])
            fw.cp("dve", yoTs[:], PT3[:, :, 0:M])
            for half in range(2):
                for kt in range(8):
                    fw.mm(PA[0:M, half * 512:(half + 1) * 512], yoTs[:, kt, :], Wo[:, kt, half * 512:(half + 1) * 512],
                          start=kt == 0, stop=kt == 7)
            fw.tt("dve", tB[:], xts[:], PA[0:M, :], ALU.add)
            fw.dma("sp", io.s3s[:, :], tB[:])
        fw.release(base_mark)

    if "3" in phases:
        ffn_phase(1, io.s3, io.y_p, io.s3s, io.y_s, True)

    fw.finish()
    fw.close()
    return nc, fw


def prep_common(inp):
    f = lambda k: np.ascontiguousarray(np.asarray(inp[k], np.float32))
    m = {}
    m["cst"] = host_consts()
    m["w_in0"] = f("w_in0")[0]
    m["w_out0"] = f("w_out0")[0]
    m["norm_mix"] = f("norm_mix")
    m["norm_ffn"] = f("norm_ffn")
    m["norm_final"] = f("norm_final")
    m["ssd_norm"] = f("ssd_norm")[0]
    small0 = np.zeros(64, np.float32)
    small0[0:16] = f("ssd_dt_bias")[0]
    small0[16:32] = f("ssd_a_log")[0]
    small0[32:48] = f("ssd_d")[0]
    small0[48:52] = f("ml_i_bias")[0]
    small0[52:56] = f("ml_f_bias")[0]
    m["small0"] = small0
    cw = f("conv_w")[0].reshape(4, 20, 128).transpose(2, 1, 0)
    cb = f("conv_b")[0].reshape(20, 128).T[:, :, None]
    m["convp"] = np.ascontiguousarray(np.concatenate([cw, cb], axis=2))
    m["mlcol"] = np.ascontiguousarray(np.stack([f("ml_norm")[0].reshape(8, 128).T, f("ml_skip")[0].reshape(8, 128).T], axis=2))
    m["bdq"] = blockdiag(f("ml_wq")[0])
    m["bdk"] = blockdiag(f("ml_wk")[0])
    m["bdv"] = blockdiag(f("ml_wv")[0])
    for nm in ("rw_wr", "rw_wk", "rw_wv", "rw_wo", "rw_w1", "rw_w2", "rw_a1", "rw_a2", "rw_g1", "rw_g2"):
        m[nm] = f(nm)[0]
    m["rw_rows"] = np.ascontiguousarray(np.stack([f(k)[0] for k in ("rw_w0", "rw_a0", "rw_k_k", "rw_k_a", "rw_r_k", "rw_ln_w", "rw_ln_b")]))
    m["rw_mu"] = np.ascontiguousarray(f("rw_mu")[0].reshape(6, 8, 128).transpose(2, 1, 0))
    m["w_gu"] = f("ffn_w_gate_up")
    m["w_dn"] = f("ffn_w_down")
    return m


def prep_core(inp, core):
    f = lambda k: np.asarray(inp[k], np.float32)
    b0 = core * NB
    m = {}
    m["xp"] = np.ascontiguousarray(f("x_prompt")[core])
    m["xs"] = np.ascontiguousarray(f("x_sample")[b0:b0 + NB, 0, :])
    m["conv_s_in"] = np.ascontiguousarray(f("state_conv")[0, b0:b0 + NB].reshape(NB, 3, 20, 128).transpose(3, 2, 1, 0))
    m["ssm_s_in"] = np.ascontiguousarray(f("state_ssm")[0, b0:b0 + NB].reshape(NB, D, 128))
    m["mc_s_in"] = np.ascontiguousarray(f("state_mlstm_c")[0, b0:b0 + NB])
    m["mn_s_in"] = np.ascontiguousarray(f("state_mlstm_n")[0, b0:b0 + NB].reshape(NB, 4, 2, 128).transpose(3, 1, 2, 0).reshape(128, 8, NB))
    m["mm_s_in"] = np.ascontiguousarray(f("state_mlstm_m")[0, b0:b0 + NB])
    m["shift_s_in"] = np.ascontiguousarray(f("state_shift")[0, b0:b0 + NB].reshape(NB, 8, 128).transpose(2, 1, 0))
    m["wkv_s_in"] = np.ascontiguousarray(f("state_wkv")[0, b0:b0 + NB].reshape(NB, 8, 2, 64, 64).transpose(0, 2, 4, 1, 3).reshape(NB, 128, 8, 64))
    return m


def prep_consts(inp):
    f = lambda k: np.asarray(inp[k], np.float32)
    m = prep_common(inp)
    m["c16"] = host_consts16()
    m["eye16"] = np.ascontiguousarray(np.broadcast_to(np.eye(16, dtype=np.float32), (128, 16, 16)))
    dtcol = np.zeros((16, 4), np.float32)
    dtcol[:, 0] = f("ssd_dt_bias")[0]
    dtcol[:, 1] = f("ssd_a_log")[0]
    dtcol[:, 2] = f("ssd_d")[0]
    m["dtcol"] = dtcol
    return m


_NC_CACHE = {}


def kernel(**inp):
    if "nc" not in _NC_CACHE:
        _NC_CACHE["nc"] = build({})
    nc = _NC_CACHE["nc"]
    cm = prep_consts(inp)
    in_maps = [dict(cm, **prep_core(inp, c)) for c in range(NCORE)]
    res = run_bass_kernel_spmd(nc, in_maps, core_ids=list(range(NCORE)))
    R = res.results
    BT = NCORE * NB
    y_p = np.zeros((NCORE, T, D), np.float32)
    y_s = np.zeros((BT, 1, D), np.float32)
    conv_p = np.zeros((1, NCORE, 3, 2560), np.float32)
    conv_s = np.zeros((1, BT, 3, 2560), np.float32)
    ssm_p = np.zeros((1, NCORE, 16, 64, 128), np.float32)
    ssm_s = np.zeros((1, BT, 16, 64, 128), np.float32)
    mc_p = np.zeros((1, NCORE, 4, 256, 256), np.float32)
    mc_s = np.zeros((1, BT, 4, 256, 256), np.float32)
    mn_p = np.zeros((1, NCORE, 4, 256), np.float32)
    mn_s = np.zeros((1, BT, 4, 256), np.float32)
    mm_p = np.zeros((1, NCORE, 4), np.float32)
    mm_s = np.zeros((1, BT, 4), np.float32)
    sh_p = np.zeros((1, NCORE, D), np.float32)
    sh_s = np.zeros((1, BT, D), np.float32)
    wkv_p = np.zeros((1, NCORE, 16, 64, 64), np.float32)
    wkv_s = np.zeros((1, BT, 16, 64, 64), np.float32)
    for c in range(NCORE):
        r = R[c]
        sl = slice(c * NB, (c + 1) * NB)
        y_p[c] = r["y_p"]
        y_s[sl, 0] = r["y_s"]
        conv_p[0, c] = r["conv_p"].transpose(2, 1, 0).reshape(3, 2560)
        conv_s[0, sl] = r["conv_s"].transpose(3, 2, 1, 0).reshape(NB, 3, 2560)
        ssm_p[0, c] = r["ssm_p"].reshape(128, 16, 64).transpose(1, 2, 0)
        ssm_s[0, sl] = r["ssm_s"].reshape(NB, 16, 64, 128)
        mc = r["mc_p"]
        mc_p[0, c] = mc[:, :, :, :256].transpose(2, 1, 0, 3).reshape(4, 256, 256)
        mn_p[0, c] = mc[:, :, :, 256].transpose(2, 1, 0).reshape(4, 256)
        mc_s[0, sl] = r["mc_s"]
        mn_s[0, sl] = r["mn_s"].reshape(128, 4, 2, NB).transpose(3, 1, 2, 0).reshape(NB, 4, 256)
        mm_p[0, c] = r["mm_p"][0]
        mm_s[0, sl] = r["mm_s"]
        sh_p[0, c] = r["shift_p"][0]
        sh_s[0, sl] = r["shift_s"]
        wkv_p[0, c] = r["wkv_p"].reshape(2, 64, 8, 64).transpose(2, 0, 3, 1).reshape(16, 64, 64)
        wkv_s[0, sl] = r["wkv_s"].reshape(NB, 2, 64, 8, 64).transpose(0, 3, 1, 4, 2).reshape(NB, 16, 64, 64)
    return (y_p, y_s, conv_p, conv_s, ssm_p, ssm_s, mc_p, mc_s, mn_p, mn_s, mm_p, mm_s, sh_p, sh_s, wkv_p, wkv_s)
```

```python
import numpy as np
import concourse.bass as bass
import concourse.mybir as mybir
from concourse.bass_utils import run_bass_kernel_spmd

F32 = mybir.dt.float32
BF16 = mybir.dt.bfloat16
ALU = mybir.AluOpType
AF = mybir.ActivationFunctionType
AX = mybir.AxisListType

NCORE = 8
D = 1024
T = 2048
NB = 16
IN0 = 4632
DFF = 2816
EPS = 1e-5


class Tok:
    __slots__ = ("sem", "val", "eng", "seq")

    def __init__(self, sem, val, eng, seq=None):
        self.sem, self.val, self.eng, self.seq = sem, val, eng, seq


class Ref:
    __slots__ = ("T", "ap")

    def __init__(self, T_, ap):
        self.T, self.ap = T_, ap

    def __getitem__(self, k):
        return Ref(self.T, self.ap[k])

    def rearrange(self, p, **kw):
        return Ref(self.T, self.ap.rearrange(p, **kw))

    def unsqueeze(self, a):
        return Ref(self.T, self.ap.unsqueeze(a))

    def to_broadcast(self, shp):
        return Ref(self.T, self.ap.to_broadcast(list(shp)))

    def bc(self, axis, n):
        ap = self.ap.unsqueeze(axis)
        shp = list(ap.shape)
        shp[axis] = n
        return Ref(self.T, ap.to_broadcast(shp))


class TT:
    __slots__ = ("t", "name", "lw", "rd", "psum")

    def __init__(self, t, name, psum=False):
        self.t, self.name, self.lw, self.rd, self.psum = t, name, None, [], psum

    def __getitem__(self, k):
        return Ref(self, self.t[k])


def _Ts(*xs):
    return [x.T for x in xs if isinstance(x, Ref)]


def _a(x):
    return x.ap if isinstance(x, Ref) else x


class Eng:
    def __init__(self, fw, name, h):
        self.fw, self.name, self.h = fw, name, h
        self.sems, self.n, self.waited, self.nsig = [], 0, {}, 0


class Fw:
    EPOCH = 30000
    NDMA = 10

    def __init__(self, nc, need=None):
        self.nc = nc
        self.need = need
        self.waited_on = set()
        self._ctx = []
        self.E = {}
        for name, h in (("pe", nc.tensor), ("dve", nc.vector), ("act", nc.scalar),
                        ("pool", nc.gpsimd), ("sp", nc.sync)):
            self.E[name] = Eng(self, name, h)
        self.dma_sems, self.dma_i = {}, {}
        self.ntile = 0
        self.sb_bytes = 0

    def enter(self, cm):
        v = cm.__enter__()
        self._ctx.append(cm)
        return v

    def close(self):
        for cm in reversed(self._ctx):
            cm.__exit__(None, None, None)
        self._ctx = []

    def new_sem(self, name):
        return self.enter(self.nc.semaphore(name))

    def presem(self, queues=("sp", "pool", "act"), epochs=3):
        for e in self.E.values():
            while len(e.sems) < epochs:
                e.sems.append(self.new_sem(f"e_{e.name}_{len(e.sems)}"))
        for q in queues:
            self.dma_sems[q] = [[self.new_sem(f"d_{q}_{i}"), 0] for i in range(Fw.NDMA)]
            self.dma_i[q] = 0

    def mark(self):
        return len(self._ctx)

    def release(self, mark):
        self.barrier()
        while len(self._ctx) > mark:
            self._ctx.pop().__exit__(None, None, None)

    def _last_tok(self, e):
        return e.last

    def barrier(self):
        for eng in self.E.values():
            for q, slots in self.dma_sems.items():
                for sem, cnt in slots:
                    if cnt > 0:
                        self._wait(eng, Tok(sem, cnt, "dma"))
            for name, e in self.E.items():
                if e is eng or e.n == 0:
                    continue
                self._wait(eng, self._last_tok(e))

    def sb(self, shape, dt=F32, name="t"):
        self.ntile += 1
        n = 1
        for s in shape[1:]:
            n *= s
        self.sb_bytes += n * (2 if dt == BF16 else 4)
        return TT(self.enter(self.nc.sbuf_tensor(f"{name}_{self.ntile}", list(shape), dt)), name)

    def ps(self, shape, dt=F32, name="p"):
        self.ntile += 1
        return TT(self.enter(self.nc.psum_tensor(f"{name}_{self.ntile}", list(shape), dt)), name, psum=True)

    def view(self, ref, name="v"):
        return TT(ref.ap, name, psum=ref.T.psum)

    def _wait(self, eng, tok):
        if tok is None:
            return
        key = id(tok.sem)
        if eng.waited.get(key, 0) >= tok.val:
            return
        if tok.seq is not None:
            self.waited_on.add((tok.eng, tok.seq))
            assert tok.val == int(tok.val), "wait on a non-signalling instruction (two-pass mismatch)"
        eng.h.wait_ge(tok.sem, int(tok.val))
        eng.waited[key] = tok.val

    def _deps(self, eng, reads, writes):
        for t in reads:
            if t.lw is not None:
                self._wait(eng, t.lw)
            if t.psum:
                for r in t.rd:
                    if r.eng != eng.name:
                        self._wait(eng, r)
        strict = eng.name != "pe"
        for t in writes:
            if t.lw is not None and (strict or t.lw.eng != eng.name):
                self._wait(eng, t.lw)
            for r in t.rd:
                if strict or r.eng != eng.name:
                    self._wait(eng, r)

    def _mark(self, tok, reads, writes):
        for t in reads:
            t.rd.append(tok)
        for t in writes:
            t.lw = tok
            t.rd = []

    def op(self, e, fn, reads=(), writes=()):
        eng = self.E[e]
        self._deps(eng, reads, writes)
        seq = eng.n
        eng.n += 1
        signal = self.need is None or (eng.name, seq) in self.need
        ep = eng.nsig // Fw.EPOCH
        while len(eng.sems) <= ep:
            eng.sems.append(self.new_sem(f"e_{eng.name}_{len(eng.sems)}"))
        sem = eng.sems[ep]
        inst = fn(eng.h)
        if signal:
            val = eng.nsig % Fw.EPOCH + 1
            eng.nsig += 1
            inst.then_inc(sem, 1)
        else:
            val = eng.nsig % Fw.EPOCH + 0.5
        tok = Tok(sem, val, eng.name, seq)
        eng.last = tok
        self._mark(tok, reads, writes)
        return tok

    def dma(self, q, out, in_, **kw):
        eng = self.E[q]
        if q not in self.dma_sems:
            self.dma_sems[q] = [[self.new_sem(f"d_{q}_{i}"), 0] for i in range(Fw.NDMA)]
            self.dma_i[q] = 0
        slot = self.dma_sems[q][self.dma_i[q] % Fw.NDMA]
        self.dma_i[q] += 1
        sem, cnt = slot
        if cnt > 0:
            self._wait(eng, Tok(sem, cnt, "dma"))
        reads, writes = _Ts(in_), _Ts(out)
        self._deps(eng, reads, writes)
        inst = eng.h.dma_start(out=_a(out), in_=_a(in_), **kw)
        slot[1] = cnt + 16
        inst.then_inc(sem, 16)
        tok = Tok(sem, cnt + 16, "dma")
        self._mark(tok, reads, writes)
        return tok

    def finish(self):
        eng = self.E["sp"]
        for q, slots in self.dma_sems.items():
            for sem, cnt in slots:
                if cnt > 0:
                    self._wait(eng, Tok(sem, cnt, "dma"))
        for name, e in self.E.items():
            if name == "sp" or e.n == 0:
                continue
            self._wait(eng, self._last_tok(e))

    def mm(self, out, lhsT, rhs, start=True, stop=True, skip=False):
        kw = {"skip_group_check": True} if skip else {}
        return self.op("pe", lambda e: e.matmul(_a(out), _a(lhsT), _a(rhs), start=start, stop=stop, **kw),
                       _Ts(lhsT, rhs), _Ts(out))

    def tr(self, out, in_, ident):
        return self.op("pe", lambda e: e.transpose(_a(out), _a(in_), _a(ident)), _Ts(in_, ident), _Ts(out))

    def act(self, out, in_, func, bias=None, scale=None, accum_out=None):
        kw = {}
        if bias is not None:
            kw["bias"] = _a(bias)
        if scale is not None:
            kw["scale"] = _a(scale)
        if accum_out is not None:
            kw["accum_out"] = _a(accum_out)
        return self.op("act", lambda e: e.activation(out=_a(out), in_=_a(in_), func=func, **kw),
                       _Ts(in_, bias, scale), _Ts(out, accum_out))

    def tt(self, e, out, in0, in1, op):
        return self.op(e, lambda h: h.tensor_tensor(out=_a(out), in0=_a(in0), in1=_a(in1), op=op),
                       _Ts(in0, in1), _Ts(out))

    def ts(self, e, out, in0, s1, s2, op0, op1=None, accum_out=None):
        kw = {}
        if op1 is not None:
            kw["op1"] = op1
        if accum_out is not None:
            kw["accum_out"] = _a(accum_out)
        return self.op(e, lambda h: h.tensor_scalar(out=_a(out), in0=_a(in0), scalar1=_a(s1), scalar2=_a(s2),
                                                    op0=op0, **kw),
                       _Ts(in0, s1, s2), _Ts(out, accum_out))

    def stt(self, out, in0, scalar, in1, op0, op1, accum_out=None):
        kw = {}
        if accum_out is not None:
            kw["accum_out"] = _a(accum_out)
        return self.op("dve", lambda h: h.scalar_tensor_tensor(out=_a(out), in0=_a(in0), scalar=_a(scalar),
                                                               in1=_a(in1), op0=op0, op1=op1, **kw),
                       _Ts(in0, scalar, in1), _Ts(out, accum_out))

    def cp(self, e, out, in_):
        if e == "act":
            return self.act(out, in_, AF.Copy)
        return self.op(e, lambda h: h.tensor_copy(out=_a(out), in_=_a(in_)), _Ts(in_), _Ts(out))

    def red(self, out, in_, op, axis=AX.X):
        return self.op("dve", lambda h: h.tensor_reduce(out=_a(out), in_=_a(in_), axis=axis, op=op),
                       _Ts(in_), _Ts(out))

    def recip(self, out, in_):
        return self.op("dve", lambda h: h.reciprocal(out=_a(out), in_=_a(in_)), _Ts(in_), _Ts(out))

    def memset(self, e, out, val):
        return self.op(e, lambda h: h.memset(_a(out), val), [], _Ts(out))


def host_consts():
    j = np.arange(128)
    c = np.zeros((128, 10, 128), np.float32)
    c[:, 0, :] = (j[:, None] == j[None, :])
    c[:, 1, :] = (j[:, None] <= j[None, :])
    c[:, 2, :] = (j[:, None] > j[None, :])
    c[:, 3, :] = np.where(j[None, :] <= j[:, None], 0.0, -30000.0)
    c[:, 4, :] = 1.0
    c[:, 5, :] = (j[:, None] == 127)
    c[:, 6, :] = (j[:, None] < j[None, :])
    c[:, 7, :] = c[:, 1, :]
    c[:, 8, :] = c[:, 6, :]
    c[:, 9, :] = c[:, 1, :]
    return c


def blockdiag(w):
    out = np.zeros((8, 128, 128), np.float32)
    w = w.reshape(8, 32, 4, 4)
    for nl in range(32):
        out[:, nl * 4:(nl + 1) * 4, nl * 4:(nl + 1) * 4] = w[:, nl]
    return np.ascontiguousarray(out.transpose(1, 0, 2))


def host_consts16():
    h = np.arange(16)
    q = np.arange(128)
    j = np.arange(8)
    e = (h[:, None, None] == (2 * j[None, :, None] + q[None, None, :] // 64)).astype(np.float32)
    sel = np.broadcast_to((h[:, None, None] == h[None, :, None]), (16, 16, 128)).astype(np.float32)
    return np.ascontiguousarray(np.concatenate([e.reshape(16, -1), sel.reshape(16, -1)], axis=1))


class IO:
    pass


def build(cfg):
    _, fw1 = _build(cfg, None)
    nc, fw2 = _build(cfg, fw1.waited_on)
    return nc


def _build(cfg, need):
    nc = bass.Bass("TRN2", target_bir_lowering=False)
    fw = Fw(nc, need)
    io = IO()
    NCH = cfg.get("nch", 16)
    dbg = cfg.get("dbg", ())
    phases = cfg.get("phases", ("0a", "0b", "1", "2", "3"))

    def din(name, shape):
        return nc.dram_tensor(name, list(shape), F32, kind="ExternalInput").ap()

    def dout(name, shape):
        return nc.dram_tensor(name, list(shape), F32, kind="ExternalOutput").ap()

    def dscr(name, shape):
        if name in dbg:
            return dout(name, shape)
        return nc.dram_tensor(name, list(shape), F32).ap()

    io.xp = din("xp", [T, D])
    io.cst = din("cst", [128, 10, 128])
    io.w_in0 = din("w_in0", [D, IN0])
    io.w_out0 = din("w_out0", [2 * D, D])
    io.norm_mix = din("norm_mix", [2, D])
    io.norm_ffn = din("norm_ffn", [2, D])
    io.norm_final = din("norm_final", [D])
    io.ssd_norm = din("ssd_norm", [D])
    io.small0 = din("small0", [64])
    io.convp = din("convp", [128, 20, 5])
    io.mlcol = din("mlcol", [128, 8, 2])
    io.bdq = din("bdq", [128, 8, 128])
    io.bdk = din("bdk", [128, 8, 128])
    io.bdv = din("bdv", [128, 8, 128])
    io.w_gu = din("w_gu", [2, D, 2 * DFF])
    io.w_dn = din("w_dn", [2, DFF, D])
    for nm in ("rw_wr", "rw_wk", "rw_wv", "rw_wo"):
        setattr(io, nm, din(nm, [D, D]))
    io.rw_w1 = din("rw_w1", [D, 64]); io.rw_w2 = din("rw_w2", [64, D])
    io.rw_a1 = din("rw_a1", [D, 64]); io.rw_a2 = din("rw_a2", [64, D])
    io.rw_g1 = din("rw_g1", [D, 160]); io.rw_g2 = din("rw_g2", [160, D])
    io.rw_rows = din("rw_rows", [7, D])
    io.rw_mu = din("rw_mu", [128, 8, 6])
    io.wkv_p = dout("wkv_p", [128, 8, 64])
    io.shift_p = dout("shift_p", [1, D])
    io.xs = din("xs", [NB, D])
    io.c16 = din("c16", [16, 8 * 128 + 16 * 128])
    io.eye16 = din("eye16", [128, 16, 16])
    io.dtcol = din("dtcol", [16, 4])
    io.conv_s_in = din("conv_s_in", [128, 20, 3, NB])
    io.ssm_s_in = din("ssm_s_in", [NB, D, 128])
    io.mc_s_in = din("mc_s_in", [NB, 4, 256, 256])
    io.mn_s_in = din("mn_s_in", [128, 8, NB])
    io.mm_s_in = din("mm_s_in", [NB, 4])
    io.shift_s_in = din("shift_s_in", [128, 8, NB])
    io.wkv_s_in = din("wkv_s_in", [NB, 128, 8, 64])
    io.y_s = dout("y_s", [NB, D])
    io.conv_s = dout("conv_s", [128, 20, 3, NB])
    io.ssm_s = dout("ssm_s", [NB, D, 128])
    io.mc_s = dout("mc_s", [NB, 4, 256, 256])
    io.mn_s = dout("mn_s", [128, 8, NB])
    io.mm_s = dout("mm_s", [NB, 4])
    io.shift_s = dout("shift_s", [NB, D])
    io.wkv_s = dout("wkv_s", [NB, 128, 8, 64])
    io.s1s = dscr("s1s", [NB, D])
    io.s2s = dscr("s2s", [NB, D])
    io.s3s = dscr("s3s", [NB, D])
    if "dbg_a" in dbg:
        io.dbg_a = dout("dbg_a", [NB, D]); io.dbg_b = dout("dbg_b", [NB, 64])
    io.s1 = dscr("s1", [T, D])
    io.s2 = dscr("s2", [T, D])
    io.s3 = dscr("s3", [T, D])
    io.y_p = dout("y_p", [T, D])
    io.ssm_p = dout("ssm_p", [128, D])
    io.mc_p = dout("mc_p", [128, 2, 4, 264])
    io.mm_p = dout("mm_p", [1, 4])
    io.conv_p = dout("conv_p", [128, 20, 3])

    fw.presem(epochs=5)

    cst = fw.sb([128, 10, 128], F32, "cst")
    fw.dma("sp", cst[:], io.cst[:, :, :])
    ident, tri_le, mask_gt, negmask, ones = (cst[:, i, :] for i in range(5))
    sel127 = cst[:, 5, :]
    m4 = cst[:, 6:10, :].rearrange("p a t -> p (a t)")
    identb = fw.sb([128, 128], BF16, "identb")
    fw.cp("dve", identb[:], ident)
    onesb = fw.sb([128, 128], BF16, "onesb")
    fw.cp("dve", onesb[:], ones)
    nst = fw.sb([128, 8], F32, "nst")
    c16 = fw.sb([16, 8 * 128 + 16 * 128], F32, "c16")
    fw.dma("sp", c16[:], io.c16[:, :])
    exp16 = c16[:, 0:1024].rearrange("p (j q) -> p j q", j=8)
    sel16 = c16[:, 1024:3072].rearrange("p (b q) -> p b q", b=16)
    eye16 = fw.sb([128, 16, 16], F32, "eye16")
    fw.dma("sp", eye16[:], io.eye16[:, :, :])
    SAMPLE = cfg.get("sample", True)

    PA = fw.ps([128, 1024], F32, "PA")
    PB = fw.ps([128, 1024], F32, "PB")
    PC = fw.ps([128, 512], F32, "PC")
    PD = fw.ps([128, 512], F32, "PD")
    PE = fw.ps([128, 512], F32, "PE")
    PT = fw.ps([128, 1024], BF16, "PT")
    pcd = [PC, PD]
    PB0f = fw.view(PB[:, 0:512], "PB0f")
    PB1f = fw.view(PB[:, 512:1024], "PB1f")
    PT3 = PT[:, :].rearrange("p (k m) -> p k m", k=8)
    v16 = lambda r: r.rearrange("p (h q) -> p h q", h=16)

    def load_w(dst, src, kt0, kt1, q="pool", step=2):
        N = src.shape[1]
        cw = 1024 if N > 1024 else N
        if N <= 1024:
            kstep = max(1, min(step, 2048 // max(N, 1))) if N >= 512 else step
        else:
            kstep = 1
        for k in range(kt0, kt1, kstep):
            k1 = min(k + kstep, kt1)
            for c0 in range(0, N, cw):
                c1 = min(c0 + cw, N)
                fw.dma(q, dst[:, k:k1, c0:c1], src[k * 128:k1 * 128, c0:c1].rearrange("(k p) n -> p k n", p=128))

    def rmsnorm(x, g, out, M, junk):
        fw.act(junk[0:M, :], x, AF.Square, accum_out=nst[0:M, 0:1])
        fw.ts("dve", nst[0:M, 1:2], nst[0:M, 0:1], 1.0 / D, EPS, ALU.mult, ALU.add)
        fw.act(nst[0:M, 2:3], nst[0:M, 1:2], AF.Ln)
        fw.act(nst[0:M, 3:4], nst[0:M, 2:3], AF.Exp, scale=-0.5)
        fw.stt(out, x, nst[0:M, 3:4], g[0:M, :], ALU.mult, ALU.mult)

    def to_feat(src, dst, M):
        for kt in range(8):
            fw.tr(PT3[:, kt, 0:M], src[0:M, kt * 128:(kt + 1) * 128], identb[0:M, 0:M])
        fw.cp("dve", dst[:, :, 0:M], PT3[:, :, 0:M])

    def grp_rstd(src, ncol, dst, junk, M=128):
        fw.act(junk[0:M, 0:ncol], src, AF.Square, accum_out=nst[0:M, 4:5])
        fw.ts("dve", nst[0:M, 5:6], nst[0:M, 4:5], 1.0 / ncol, EPS, ALU.mult, ALU.add)
        fw.act(nst[0:M, 6:7], nst[0:M, 5:6], AF.Ln)
        fw.act(dst, nst[0:M, 6:7], AF.Exp, scale=-0.5)

    def proj_feat(W, col0, ntile, xT, M, evac):
        for gi, g0 in enumerate(range(0, ntile, 4)):
            n = min(4, ntile - g0)
            ps3 = pcd[gi % 2][:, :].rearrange("p (a m) -> p a m", a=4)
            for i in range(n):
                col = col0 + (g0 + i) * 128
                for kt in range(8):
                    fw.mm(ps3[:, i, 0:M], W[:, kt, col:col + 128], xT[:, kt, 0:M], start=kt == 0, stop=kt == 7)
            evac(g0, n, ps3[:, 0:n, 0:M])

    def conv_tiles(convin, convp, accs, ct0, n):
        for i in range(n):
            ct = ct0 + i
            fw.act(accs[i][:, :], convin[:, i, 0:128], AF.Identity, scale=convp[:, ct, 0:1], bias=convp[:, ct, 4:5])
        for j in range(1, 4):
            for i in range(n):
                ct = ct0 + i
                fw.stt(accs[i][:, :], convin[:, i, j:j + 128], convp[:, ct, j:j + 1], accs[i][:, :], ALU.mult, ALU.add)

    base_mark = fw.mark()

    if "0a" in phases:
        Wc = fw.sb([128, 8, 1536], BF16, "Wc")
        load_w(Wc, io.w_in0[:, 1024:2560], 0, 8)
        Wz = fw.sb([128, 8, 1024], BF16, "Wz")
        load_w(Wz, io.w_in0[:, 0:1024], 0, 8)
        Wdt = fw.sb([128, 8, 16], BF16, "Wdt")
        load_w(Wdt, io.w_in0[:, 3584:3600], 0, 8, step=8)
        Wo = fw.sb([128, 8, D], BF16, "Wo")
        load_w(Wo, io.w_out0[0:1024, :], 0, 8)
        gmix = fw.sb([128, D], F32, "gmix")
        fw.dma("sp", gmix[:], io.norm_mix[0, :].partition_broadcast(128))
        gssd = fw.sb([128, D], F32, "gssd")
        fw.dma("sp", gssd[:], io.ssd_norm.partition_broadcast(128))
        sm0 = fw.sb([128, 64], F32, "sm0")
        fw.dma("sp", sm0[:], io.small0.partition_broadcast(128))
        dtb_bc, D_bc = sm0[:, 0:16], sm0[:, 32:48]
        A_t = fw.sb([128, 16], F32, "A_t")
        fw.act(A_t[:], sm0[:, 16:32], AF.Exp)
        fw.ts("dve", A_t[:], A_t[:], -1.0, None, ALU.mult)
        convp = fw.sb([128, 20, 5], F32, "convp")
        fw.dma("sp", convp[:], io.convp[:, :, :])
        convin = fw.sb([128, 12, 131], F32, "convin")
        fw.memset("pool", convin[:], 0.0)
        ST = fw.sb([128, D], F32, "ST")
        fw.memset("pool", ST[:], 0.0)
        STb = fw.sb([128, D], BF16, "STb")
        fw.memset("pool", STb[:], 0.0)
        xt = fw.sb([128, D], F32, "xt")
        junk = fw.sb([128, D], F32, "junk")
        xn = fw.sb([128, D], BF16, "xn")
        xnT = fw.sb([128, 8, 128], BF16, "xnT")
        acc = fw.sb([128, 12, 128], F32, "acc")
        accv = [fw.view(acc[:, i, :], f"acc{i}") for i in range(12)]
        cact = fw.sb([128, 12, 128], BF16, "cact")
        zs = fw.sb([128, D], F32, "zs")
        xtok = fw.sb([128, D], BF16, "xtok")
        Btok = fw.sb([128, 256], BF16, "Btok")
        sm = fw.sb([128, 128], F32, "sm")
        Lh = [fw.sb([128, 4, 128], F32, f"Lh{i}") for i in range(2)]
        Eh = fw.sb([128, 4, 128], F32, "Eh")
        CBm = fw.sb([128, 2, 128], F32, "CBm")
        Wt = fw.sb([128, 16, 128], BF16, "Wt")
        t1 = fw.sb([128, D], F32, "t1")
        yn = fw.sb([128, D], BF16, "yn")
        ynT = fw.sb([128, 8, 128], BF16, "ynT")
        xw = fw.sb([128, D], BF16, "xw")
        x1 = fw.sb([128, D], F32, "x1")

        for c in range(NCH):
            fw.dma("sp", xt[:], io.xp[c * 128:(c + 1) * 128, :])
            rmsnorm(xt[:], gmix, xn[:], 128, junk)
            to_feat(xn, xnT, 128)
            proj_feat(Wc, 0, 12, xnT, 128, lambda g0, n, ps: fw.cp("act", convin[:, g0:g0 + n, 3:131], ps))
            for half in range(2):
                for kt in range(8):
                    fw.mm(PA[:, half * 512:(half + 1) * 512], xnT[:, kt, :], Wz[:, kt, half * 512:(half + 1) * 512],
                          start=kt == 0, stop=kt == 7)
            fw.act(zs[:], PA[:, :], AF.Silu)
            for kt in range(8):
                fw.mm(PE[:, 0:16], xnT[:, kt, :], Wdt[:, kt, :], start=kt == 0, stop=kt == 7)
            fw.tt("dve", sm[:, 0:16], PE[:, 0:16], dtb_bc, ALU.add)
            conv_tiles(convin, convp, accv, 0, 12)
            for i in range(12):
                fw.act(cact[:, i, :], accv[i][:, :], AF.Silu)
            fw.cp("pool", convin[:, :, 0:3], convin[:, :, 128:131])
            for kt in range(8):
                fw.tr(PT3[:, kt, :], cact[:, kt, :], identb[:, :])
            fw.cp("dve", xtok[:], PT[:, :])
            for g in range(2):
                fw.tr(PT[:, g * 128:(g + 1) * 128], cact[:, 8 + g, :], identb[:, :])
            fw.cp("dve", Btok[:], PT[:, 0:256])
            fw.act(sm[:, 0:16], sm[:, 0:16], AF.Exp)
            fw.act(sm[:, 0:16], sm[:, 0:16], AF.Ln, bias=1.0)
            fw.tt("dve", sm[:, 16:32], sm[:, 0:16], A_t[:], ALU.mult)
            fw.mm(PE[:, 32:48], tri_le, sm[:, 16:32])
            fw.mm(PE[:, 48:64], ones, sm[:, 16:32])
            fw.act(sm[:, 32:48], PE[:, 32:48], AF.Exp)
            fw.cp("dve", sm[:, 64:80], PE[:, 32:48])
            fw.tt("dve", sm[:, 48:64], PE[:, 48:64], sm[:, 64:80], ALU.subtract)
            fw.act(sm[:, 48:64], sm[:, 48:64], AF.Exp)
            fw.tt("dve", sm[:, 48:64], sm[:, 48:64], sm[:, 0:16], ALU.mult)
            fw.act(sm[:, 80:96], PE[:, 48:64], AF.Exp)
            for g in range(2):
                fw.mm(PE[:, 128 + g * 128:256 + g * 128], cact[:, 8 + g, :], cact[:, 10 + g, :])
                fw.tt("dve", CBm[:, g, :], PE[:, 128 + g * 128:256 + g * 128], tri_le, ALU.mult)
            for hq in range(4):
                L = Lh[hq % 2]
                ps3 = pcd[hq % 2][:, :].rearrange("p (a m) -> p a m", a=4)
                for i in range(4):
                    h = hq * 4 + i
                    fw.ts("dve", L[:, i, :], mask_gt, sm[:, 16 + h:17 + h], None, ALU.mult)
                    fw.mm(ps3[:, i, :], L[:, i, :], tri_le)
                fw.act(Eh[:], ps3, AF.Exp)
                for i in range(4):
                    h = hq * 4 + i
                    fw.stt(Wt[:, h, :], Eh[:, i, :], sm[:, h:h + 1], CBm[:, h // 8, :], ALU.mult, ALU.mult)
            for h in range(16):
                fw.mm(PA[:, h * 64:(h + 1) * 64], Wt[:, h, :], xtok[:, h * 64:(h + 1) * 64])
            for g in range(2):
                fw.mm(PB[:, g * 512:(g + 1) * 512], cact[:, 10 + g, :], STb[:, g * 512:(g + 1) * 512])
            fw.tt("dve", v16(t1[:, :]), v16(PB[:, :]), sm[:, 32:48].bc(2, 64), ALU.mult)
            fw.tt("dve", t1[:], t1[:], PA[:, :], ALU.add)
            fw.tt("pool", v16(junk[:, :]), v16(xtok[:, :]), D_bc.bc(2, 64), ALU.mult)
            fw.tt("dve", t1[:], t1[:], junk[:], ALU.add)
            fw.tt("dve", t1[:], t1[:], zs[:], ALU.mult)
            for g in range(2):
                grp_rstd(t1[:, g * 512:(g + 1) * 512], 512, nst[:, 7:8], junk)
                fw.stt(yn[:, g * 512:(g + 1) * 512], t1[:, g * 512:(g + 1) * 512], nst[:, 7:8],
                       gssd[:, g * 512:(g + 1) * 512], ALU.mult, ALU.mult)
            to_feat(yn, ynT, 128)
            fw.tt("pool", v16(xw[:, :]), v16(xtok[:, :]), sm[:, 48:64].bc(2, 64), ALU.mult)
            for g in range(2):
                fw.mm(PB[:, g * 512:(g + 1) * 512], Btok[:, g * 128:(g + 1) * 128], xw[:, g * 512:(g + 1) * 512])
            fw.tt("dve", v16(ST[:, :]), v16(ST[:, :]), sm[:, 80:96].bc(2, 64), ALU.mult)
            fw.tt("dve", ST[:], ST[:], PB[:, :], ALU.add)
            fw.cp("act", STb[:], ST[:])
            for half in range(2):
                for kt in range(8):
                    fw.mm(PA[:, half * 512:(half + 1) * 512], ynT[:, kt, :], Wo[:, kt, half * 512:(half + 1) * 512],
                          start=kt == 0, stop=kt == 7)
            fw.tt("dve", x1[:], xt[:], PA[:, :], ALU.add)
            fw.dma("sp", io.s1[c * 128:(c + 1) * 128, :], x1[:])

        if SAMPLE:
            dtcol = fw.sb([16, 4], F32, "dtcol")
            fw.dma("sp", dtcol[:], io.dtcol[:, :])
            fw.act(dtcol[:, 3:4], dtcol[:, 1:2], AF.Exp)
            fw.ts("dve", dtcol[:, 3:4], dtcol[:, 3:4], -1.0, None, ALU.mult)
            cst_s = fw.sb([128, 12, 3, NB], F32, "cst_s")
            fw.dma("sp", cst_s[:], io.conv_s_in[:, 0:12, :, :])
            uS = fw.sb([128, 12, NB], F32, "uS")
            accs = fw.sb([128, 12, NB], F32, "accs")
            tmps = fw.sb([128, 12, NB], F32, "tmps")
            cs = fw.sb([128, 12, NB], F32, "cs")
            zsT = fw.sb([128, 8, NB], F32, "zsT")
            dd = fw.sb([16, 48], F32, "dd")
            dx = fw.sb([128, 8, 48], F32, "dx")
            dtx = fw.sb([128, 8, NB], F32, "dtx")
            BCtok = fw.sb([16, 512], F32, "BCtok")
            Sb = [fw.sb([128, 8, 128], F32, f"Sb{i}") for i in range(2)]
            T1s = fw.sb([128, 8, 128], F32, "T1s")
            ysT = fw.sb([128, 8, NB], F32, "ysT")
            fw.dma("sp", xt[0:NB, :], io.xs[:, :])
            rmsnorm(xt[0:NB, :], gmix, xn[0:NB, :], NB, junk)
            to_feat(xn, xnT, NB)
            proj_feat(Wc, 0, 12, xnT, NB, lambda g0, n, ps: fw.cp("act", uS[:, g0:g0 + n, :], ps))
            proj_feat(Wz, 0, 8, xnT, NB, lambda g0, n, ps: fw.act(zsT[:, g0:g0 + n, :], ps, AF.Silu))
            wv = lambda j: convp[:, 0:12, j].bc(2, NB)
            fw.tt("dve", accs[:], cst_s[:, :, 0, :], wv(0), ALU.mult)
            fw.tt("dve", accs[:], accs[:], wv(4), ALU.add)
            for j in (1, 2):
                fw.tt("dve", tmps[:], cst_s[:, :, j, :], wv(j), ALU.mult)
                fw.tt("dve", accs[:], accs[:], tmps[:], ALU.add)
            fw.tt("dve", tmps[:], uS[:], wv(3), ALU.mult)
            fw.tt("dve", accs[:], accs[:], tmps[:], ALU.add)
            fw.act(cs[:], accs[:], AF.Silu)
            fw.dma("sp", io.conv_s[:, 0:12, 0:2, :], cst_s[:, :, 1:3, :])
            fw.dma("sp", io.conv_s[:, 0:12, 2, :], uS[:])
            for kt in range(8):
                fw.mm(PE[0:16, 0:16], Wdt[:, kt, :], xnT[:, kt, 0:NB], start=kt == 0, stop=kt == 7)
            fw.ts("dve", dd[:, 0:16], PE[0:16, 0:16], dtcol[:, 0:1], None, ALU.add)
            fw.act(dd[:, 0:16], dd[:, 0:16], AF.Exp)
            fw.act(dd[:, 0:16], dd[:, 0:16], AF.Ln, bias=1.0)
            fw.ts("dve", dd[:, 16:32], dd[:, 0:16], dtcol[:, 3:4], None, ALU.mult)
            fw.act(dd[:, 16:32], dd[:, 16:32], AF.Exp)
            fw.ts("dve", dd[:, 32:48], ones[0:16, 0:16], dtcol[:, 2:3], None, ALU.mult)
            for j in range(8):
                fw.mm(PE[:, 128 + j * 48:128 + (j + 1) * 48], exp16[:, j, :], dd[:, :])
            fw.cp("dve", dx[:], PE[:, 128:512].rearrange("p (j c) -> p j c", j=8))
            fw.tt("dve", dtx[:], dx[:, :, 0:16], cs[:, 0:8, :], ALU.mult)
            for i in range(4):
                fw.tr(PD[0:16, i * 128:(i + 1) * 128], cs[:, 8 + i, :], ident)
            fw.cp("dve", BCtok[:], PD[0:16, :])
            for b in range(NB):
                S = Sb[b % 2]
                fw.dma("sp", S[:], io.ssm_s_in[b].rearrange("(j q) n -> q j n", q=128))
                fw.mm(PC[:, :], sel16[:, b, :], BCtok[:, :])
                for g in range(2):
                    fw.tt("dve", T1s[:, 4 * g:4 * g + 4, :], PC[:, g * 128:(g + 1) * 128].bc(1, 4),
                          dtx[:, 4 * g:4 * g + 4, b].bc(2, 128), ALU.mult)
                fw.tt("pool", S[:], S[:], dx[:, :, 16 + b].bc(2, 128), ALU.mult)
                fw.tt("dve", S[:], S[:], T1s[:], ALU.add)
                fw.dma("sp", io.ssm_s[b].rearrange("(j q) n -> q j n", q=128), S[:])
                for g in range(2):
                    fw.tt("dve", T1s[:, 4 * g:4 * g + 4, :], S[:, 4 * g:4 * g + 4, :],
                          PC[:, 256 + g * 128:256 + (g + 1) * 128].bc(1, 4), ALU.mult)
                fw.red(ysT[:, :, b], T1s[:], ALU.add)
            fw.tt("dve", dtx[:], dx[:, :, 32:48], cs[:, 0:8, :], ALU.mult)
            fw.tt("dve", ysT[:], ysT[:], dtx[:], ALU.add)
            fw.tt("dve", ysT[:], ysT[:], zsT[:], ALU.mult)
            for j in range(8):
                fw.tr(PA[0:16, j * 128:(j + 1) * 128], ysT[:, j, :], ident)
            fw.cp("dve", t1[0:NB, :], PA[0:NB, :])
            for g in range(2):
                grp_rstd(t1[0:NB, g * 512:(g + 1) * 512], 512, nst[0:NB, 7:8], junk, NB)
                fw.stt(yn[0:NB, g * 512:(g + 1) * 512], t1[0:NB, g * 512:(g + 1) * 512], nst[0:NB, 7:8],
                       gssd[0:NB, g * 512:(g + 1) * 512], ALU.mult, ALU.mult)
            to_feat(yn, ynT, NB)
            for half in range(2):
                for kt in range(8):
                    fw.mm(PA[0:NB, half * 512:(half + 1) * 512], ynT[:, kt, 0:NB], Wo[:, kt, half * 512:(half + 1) * 512],
                          start=kt == 0, stop=kt == 7)
            fw.tt("dve", x1[0:NB, :], xt[0:NB, :], PA[0:NB, :], ALU.add)
            fw.dma("sp", io.s1s[:, :], x1[0:NB, :])
        fw.dma("sp", io.ssm_p[:, :], ST[:])
        fw.dma("sp", io.conv_p[:, 0:12, :], convin[:, :, 0:3])
        fw.release(base_mark)

    if "0b" in phases:
        Wx = fw.sb([128, 8, 1024], BF16, "Wx")
        load_w(Wx, io.w_in0[:, 2560:3584], 0, 8)
        Wg = fw.sb([128, 8, 1024], BF16, "Wg")
        load_w(Wg, io.w_in0[:, 3600:4624], 0, 8)
        Wif = fw.sb([128, 8, 16], BF16, "Wif")
        load_w(Wif, io.w_in0[:, 4616:4632], 0, 8, step=8)
        Wo = fw.sb([128, 8, D], BF16, "Wo")
        load_w(Wo, io.w_out0[1024:2048, :], 0, 8)
        BDq = fw.sb([128, 8, 128], BF16, "BDq")
        BDk = fw.sb([128, 8, 128], BF16, "BDk")
        BDv = fw.sb([128, 8, 128], BF16, "BDv")
        fw.dma("pool", BDq[:], io.bdq[:, :, :])
        fw.dma("pool", BDk[:], io.bdk[:, :, :])
        fw.dma("pool", BDv[:], io.bdv[:, :, :])
        gmix = fw.sb([128, D], F32, "gmix")
        fw.dma("sp", gmix[:], io.norm_mix[0, :].partition_broadcast(128))
        sm0 = fw.sb([128, 64], F32, "sm0")
        fw.dma("sp", sm0[:], io.small0.partition_broadcast(128))
        ib_bc, fb_bc = sm0[:, 48:52], sm0[:, 52:56]
        convp = fw.sb([128, 20, 5], F32, "convp")
        fw.dma("sp", convp[:], io.convp[:, :, :])
        mlcol = fw.sb([128, 8, 2], F32, "mlcol")
        fw.dma("sp", mlcol[:], io.mlcol[:, :, :])
        convin = fw.sb([128, 8, 131], F32, "convin")
        fw.memset("pool", convin[:], 0.0)
        Cst = fw.sb([128, 2, 4, 264], F32, "Cst")
        fw.memset("pool", Cst[:], 0.0)
        Cb = fw.sb([128, 2, 4, 264], BF16, "Cb")
        fw.memset("pool", Cb[:], 0.0)
        mprev = fw.sb([128, 4], F32, "mprev")
        fw.memset("pool", mprev[:], 0.0)
        xt = fw.sb([128, D], F32, "xt")
        junk = fw.sb([128, D], F32, "junk")
        xn = fw.sb([128, D], BF16, "xn")
        xnT = fw.sb([128, 8, 128], BF16, "xnT")
        acc = fw.sb([128, 8, 128], F32, "acc")
        accv = [fw.view(acc[:, i, :], f"acc{i}") for i in range(8)]
        cact = fw.sb([128, 8, 128], BF16, "cact")
        xmraw = fw.sb([128, 8, 128], BF16, "xmraw")
        sigoT = fw.sb([128, 8, 128], BF16, "sigoT")
        sm2 = fw.sb([128, 64], F32, "sm2")
        qT = fw.sb([128, 8, 128], BF16, "qT")
        kT = fw.sb([128, 8, 128], BF16, "kT")
        vtok = fw.sb([128, 4, 264], BF16, "vtok")
        fw.memset("pool", vtok[:], 1.0)
        kw_ = fw.sb([128, 4, 256], BF16, "kw")
        HT = [(fw.sb([128, 128], F32, f"Rh{i}"), fw.sb([128, 128], F32, f"dlm{i}"), fw.sb([128, 128], F32, f"Dm{i}"),
               fw.sb([128, 128], BF16, f"Sg{i}"), fw.sb([128, 128], BF16, f"SgT{i}"), fw.sb([128, 16], F32, f"hs{i}"),
               fw.sb([128, 258], F32, f"comb{i}"), fw.sb([128, 256], F32, f"hh{i}")) for i in range(2)]
        junk2 = [fw.sb([128, 256], F32, f"jk{i}") for i in range(2)]
        ktb = fw.sb([128, D], BF16, "ktb")
        mt = fw.sb([128, 16], F32, "mt")
        fw.memset("pool", mt[:], 0.0)
        fw.memset("pool", sm2[:], 0.0)
        hmn = fw.sb([128, D], BF16, "hmn")
        hmnT = fw.sb([128, 8, 128], BF16, "hmnT")
        hmfT = fw.sb([128, 8, 128], BF16, "hmfT")
        x1 = fw.sb([128, D], F32, "x1")

        lvl = cfg.get('lvl', 99)
        for c in range(NCH):
            fw.dma("sp", xt[:], io.xp[c * 128:(c + 1) * 128, :])
            fw.dma("sp", x1[:], io.s1[c * 128:(c + 1) * 128, :])
            rmsnorm(xt[:], gmix, xn[:], 128, junk)
            to_feat(xn, xnT, 128)
            proj_feat(Wx, 0, 8, xnT, 128, lambda g0, n, ps: fw.cp("act", convin[:, g0:g0 + n, 3:131], ps))
            proj_feat(Wg, 0, 8, xnT, 128, lambda g0, n, ps: fw.act(sigoT[:, g0:g0 + n, :], ps, AF.Sigmoid))
            for kt in range(8):
                fw.mm(PE[:, 16:32], xnT[:, kt, :], Wif[:, kt, :], start=kt == 0, stop=kt == 7)
            fw.tt("dve", sm2[:, 0:4], PE[:, 24:28], ib_bc, ALU.add)
            fw.tt("dve", sm2[:, 4:8], PE[:, 28:32], fb_bc, ALU.add)
            conv_tiles(convin, convp, accv, 12, 8)
            for i in range(8):
                fw.act(cact[:, i, :], accv[i][:, :], AF.Silu)
            fw.cp("pool", xmraw[:], convin[:, :, 3:131])
            fw.cp("pool", convin[:, :, 0:3], convin[:, :, 128:131])
            if lvl < 2:
                continue
            for tile in range(8):
                ps = pcd[tile % 2]
                fw.mm(ps[:, 0:128], (Wx[:, tile, 0:128] if cfg.get('alt') else BDq[:, tile, :]), cact[:, tile, :])
                fw.mm(ps[:, 128:256], (Wx[:, tile, 0:128] if cfg.get('alt') else BDk[:, tile, :]), cact[:, tile, :])
                if cfg.get('alt') != 2:
                    fw.cp("dve", qT[:, tile, :], ps[:, 0:128])
                if cfg.get('alt') not in (2, 3):
                    fw.ts("dve", kT[:, tile, :], ps[:, 128:256], 0.0625, None, ALU.mult)
            if lvl < 2.1:
                continue
            for tile in range(8):
                fw.mm(PA[:, tile * 128:(tile + 1) * 128], xmraw[:, tile, :], BDv[:, tile, :])
                fw.mm(PB[:, tile * 128:(tile + 1) * 128], cact[:, tile, :], BDk[:, tile, :])
            if lvl < 2.2:
                continue
            fw.cp("act", vtok[:, :, 0:256], PA[:, :].rearrange("p (h v) -> p h v", h=4))
            fw.cp("dve", ktb[:], PB[:, :])
            if lvl < 2.3:
                continue
            fw.act(sm2[:, 4:8], sm2[:, 4:8], AF.Exp, scale=-1.0)
            fw.act(sm2[:, 4:8], sm2[:, 4:8], AF.Ln, bias=1.0)
            fw.ts("dve", sm2[:, 4:8], sm2[:, 4:8], -1.0, None, ALU.mult)
            fw.mm(PE[:, 64:80], tri_le, sm2[:, 0:16])
            fw.mm(PE[:, 96:112], ones, sm2[:, 0:16])
            fw.cp("dve", sm2[:, 8:12], PE[:, 68:72])
            fw.cp("dve", sm2[:, 12:16], PE[:, 100:104])
            fw.tt("dve", sm2[:, 16:20], sm2[:, 8:12], mprev[:], ALU.add)
            if lvl < 3:
                continue
            def head_gen(h, pi):
                Rh, dlm, Dm, Sg, SgT, hs, comb, hh = HT[pi]
                Pd = pcd[pi]
                Pn = [PA, PB][pi]
                fw.ts("dve", Rh[:], mask_gt, sm2[:, 4 + h:5 + h], None, ALU.mult)
                fw.stt(Rh[:], ident, sm2[:, h:h + 1], Rh[:], ALU.mult, ALU.add)
                yield
                fw.mm(Pd[:, 0:128], tri_le, Rh[:])
                fw.mm(Pd[:, 128:256], qT[:, 2 * h, :], kT[:, 2 * h, :], start=True, stop=False)
                fw.mm(Pd[:, 128:256], qT[:, 2 * h + 1, :], kT[:, 2 * h + 1, :], start=False, stop=True)
                fw.mm(Pn[:, 512:770], qT[:, 2 * h, :], Cb[:, 0, h, 0:258], start=True, stop=False)
                fw.mm(Pn[:, 512:770], qT[:, 2 * h + 1, :], Cb[:, 1, h, 0:258], start=False, stop=True)
                yield
                fw.tt("dve", dlm[:], Pd[:, 0:128], negmask, ALU.add)
                yield
                fw.red(hs[:, 0:1], dlm[:], ALU.max)
                yield
                fw.tt("dve", mt[:, h:h + 1], hs[:, 0:1], sm2[:, 16 + h:17 + h], ALU.max)
                yield
                fw.ts("dve", hs[:, 1:2], mt[:, h:h + 1], -1.0, None, ALU.mult)
                yield
                fw.act(Dm[:], dlm[:], AF.Exp, bias=hs[:, 1:2])
                fw.act(hs[:, 2:3], sm2[:, 16 + h:17 + h], AF.Exp, bias=hs[:, 1:2])
                fw.act(hs[:, 3:4], mt[:, h:h + 1], AF.Exp, scale=-1.0)
                yield
                fw.tt("dve", Sg[:], Pd[:, 128:256], Dm[:], ALU.mult)
                yield
                fw.tr(PT[:, pi * 128:(pi + 1) * 128], Sg[:], identb[:, :])
                yield
                fw.cp("dve", SgT[:], PT[:, pi * 128:(pi + 1) * 128])
                fw.act(comb[:], Pn[:, 512:770], AF.Copy, scale=hs[:, 2:3])
                yield
                fw.mm(Pn[:, 0:258], SgT[:], vtok[:, h, 0:258])
                yield
                fw.tt("dve", comb[:], comb[:], Pn[:, 0:258], ALU.add)
                yield
                fw.ts("dve", hs[:, 6:7], comb[:, 256:257], -1.0, None, ALU.mult)
                yield
                fw.tt("dve", hs[:, 6:7], hs[:, 6:7], comb[:, 256:257], ALU.max)
                yield
                fw.tt("dve", hs[:, 4:5], hs[:, 6:7], hs[:, 3:4], ALU.max)
                yield
                fw.recip(hs[:, 5:6], hs[:, 4:5])
                yield
                fw.ts("dve", hh[:], comb[:, 0:256], hs[:, 5:6], None, ALU.mult)
                yield
                fw.act(junk2[pi][:, 0:256], hh[:], AF.Square, accum_out=hs[:, 8:9])
                yield
                fw.ts("dve", hs[:, 9:10], hs[:, 8:9], 1.0 / 256, EPS, ALU.mult, ALU.add)
                yield
                fw.act(hs[:, 10:11], hs[:, 9:10], AF.Ln)
                fw.act(hs[:, 11:12], hs[:, 10:11], AF.Exp, scale=-0.5)
                yield
                fw.ts("dve", hmn[:, h * 256:(h + 1) * 256], hh[:], hs[:, 11:12], None, ALU.mult)
                yield

            for h0 in (0, 2):
                for _ in zip(head_gen(h0, 0), head_gen(h0 + 1, 1)):
                    pass
            to_feat(hmn, hmnT, 128)
            for tile in range(8):
                fw.ts("dve", hmfT[:, tile, :], hmnT[:, tile, :], mlcol[:, tile, 0:1], None, ALU.mult)
                fw.stt(hmfT[:, tile, :], cact[:, tile, :], mlcol[:, tile, 1:2], hmfT[:, tile, :], ALU.mult, ALU.add)
            fw.tt("dve", hmfT[:], hmfT[:], sigoT[:], ALU.mult)
            if lvl < 5:
                continue
            fw.mm(PE[:, 112:128], sel127, mt[:])
            fw.cp("dve", sm2[:, 20:24], PE[:, 112:116])
            fw.tt("dve", sm2[:, 24:28], sm2[:, 12:16], sm2[:, 8:12], ALU.subtract)
            fw.tt("dve", sm2[:, 24:28], sm2[:, 24:28], sm2[:, 0:4], ALU.add)
            fw.tt("dve", sm2[:, 24:28], sm2[:, 24:28], sm2[:, 20:24], ALU.subtract)
            fw.act(sm2[:, 28:32], sm2[:, 24:28], AF.Exp)
            fw.ts("dve", sm2[:, 28:32], sm2[:, 28:32], 0.0625, None, ALU.mult)
            fw.tt("dve", sm2[:, 32:36], sm2[:, 12:16], mprev[:], ALU.add)
            fw.tt("dve", sm2[:, 32:36], sm2[:, 32:36], sm2[:, 20:24], ALU.subtract)
            fw.act(sm2[:, 32:36], sm2[:, 32:36], AF.Exp)
            fw.tt("dve", kw_[:], ktb[:, :].rearrange("p (h d) -> p h d", h=4), sm2[:, 28:32].bc(2, 256), ALU.mult)
            for kt in range(2):
                for h in range(4):
                    fw.mm(PB[:, h * 256:(h + 1) * 256], kw_[:, h, kt * 128:(kt + 1) * 128], vtok[:, h, 0:256])
                    fw.mm(PE[:, 80 + 2 * h:82 + 2 * h], kw_[:, h, kt * 128:(kt + 1) * 128], onesb[:, 0:2])
                for h in range(4):
                    fw.stt(Cst[:, kt, h, 0:256], Cst[:, kt, h, 0:256], sm2[:, 32 + h:33 + h],
                           PB[:, h * 256:(h + 1) * 256], ALU.mult, ALU.add)
                    fw.stt(Cst[:, kt, h, 256:257], Cst[:, kt, h, 256:257], sm2[:, 32 + h:33 + h],
                           PE[:, 80 + 2 * h:81 + 2 * h], ALU.mult, ALU.add)
            fw.cp("act", Cb[:], Cst[:])
            fw.cp("dve", mprev[:], sm2[:, 20:24])
            if lvl < 6:
                continue
            for half in range(2):
                for kt in range(8):
                    fw.mm(PA[:, half * 512:(half + 1) * 512], hmfT[:, kt, :], Wo[:, kt, half * 512:(half + 1) * 512],
                          start=kt == 0, stop=kt == 7)
            fw.tt("dve", x1[:], x1[:], PA[:, :], ALU.add)
            fw.dma("sp", io.s1[c * 128:(c + 1) * 128, :], x1[:])

        if SAMPLE:
            cst_s = fw.sb([128, 8, 3, NB], F32, "cst_s")
            fw.dma("sp", cst_s[:], io.conv_s_in[:, 12:20, :, :])
            uS = fw.sb([128, 8, NB], F32, "uS")
            accs = fw.sb([128, 8, NB], F32, "accs")
            tmps = fw.sb([128, 8, NB], F32, "tmps")
            cs = fw.sb([128, 8, NB], F32, "cs")
            cs_bf = fw.sb([128, 8, NB], BF16, "cs_bf")
            us_bf = fw.sb([128, 8, NB], BF16, "us_bf")
            qTs = fw.sb([128, 8, NB], F32, "qTs")
            kTs = fw.sb([128, 8, NB], F32, "kTs")
            kws = fw.sb([128, 8, NB], F32, "kws")
            nS = fw.sb([128, 8, NB], F32, "nS")
            vtoks = fw.sb([16, D], F32, "vtoks")
            g16 = fw.sb([16, 64], F32, "g16")
            Zd = fw.sb([16, 128], F32, "Zd")
            wd = fw.sb([128, 2, 4, NB], F32, "wd")
            qmask = fw.sb([128, 8, NB, NB], F32, "qmask")
            Cs = [fw.sb([128, 8, 256], F32, f"Cs{i}") for i in range(2)]
            Tt = fw.sb([128, 8, 256], F32, "Tt")
            numt = fw.sb([16, D], F32, "numt")
            fw.dma("sp", xt[0:NB, :], io.xs[:, :])
            fw.dma("sp", x1[0:NB, :], io.s1s[:, :])
            fw.dma("sp", g16[:, 8:12], io.mm_s_in[:, :])
            fw.dma("sp", nS[:], io.mn_s_in[:, :, :])
            rmsnorm(xt[0:NB, :], gmix, xn[0:NB, :], NB, junk)
            to_feat(xn, xnT, NB)
            proj_feat(Wx, 0, 8, xnT, NB, lambda g0, n, ps: fw.cp("act", uS[:, g0:g0 + n, :], ps))
            proj_feat(Wg, 0, 8, xnT, NB, lambda g0, n, ps: fw.act(sigoT[:, g0:g0 + n, 0:NB], ps, AF.Sigmoid))
            for kt in range(8):
                fw.mm(PE[0:NB, 16:32], xnT[:, kt, 0:NB], Wif[:, kt, :], start=kt == 0, stop=kt == 7)
            fw.tt("dve", g16[:, 0:4], PE[0:NB, 24:28], ib_bc[0:NB, :], ALU.add)
            fw.tt("dve", g16[:, 4:8], PE[0:NB, 28:32], fb_bc[0:NB, :], ALU.add)
            fw.act(g16[:, 4:8], g16[:, 4:8], AF.Exp, scale=-1.0)
            fw.act(g16[:, 4:8], g16[:, 4:8], AF.Ln, bias=1.0)
            fw.ts("dve", g16[:, 4:8], g16[:, 4:8], -1.0, None, ALU.mult)
            wv = lambda j: convp[:, 12:20, j].bc(2, NB)
            fw.tt("dve", accs[:], cst_s[:, :, 0, :], wv(0), ALU.mult)
            fw.tt("dve", accs[:], accs[:], wv(4), ALU.add)
            for j in (1, 2):
                fw.tt("dve", tmps[:], cst_s[:, :, j, :], wv(j), ALU.mult)
                fw.tt("dve", accs[:], accs[:], tmps[:], ALU.add)
            fw.tt("dve", tmps[:], uS[:], wv(3), ALU.mult)
            fw.tt("dve", accs[:], accs[:], tmps[:], ALU.add)
            fw.act(cs[:], accs[:], AF.Silu)
            fw.dma("sp", io.conv_s[:, 12:20, 0:2, :], cst_s[:, :, 1:3, :])
            fw.dma("sp", io.conv_s[:, 12:20, 2, :], uS[:])
            fw.cp("dve", cs_bf[:], cs[:])
            fw.cp("dve", us_bf[:], uS[:])
            for tile in range(8):
                ps = pcd[tile % 2]
                fw.mm(ps[:, 0:NB], BDq[:, tile, :], cs_bf[:, tile, :])
                fw.mm(ps[:, 16:16 + NB], BDk[:, tile, :], cs_bf[:, tile, :])
                fw.cp("dve", qTs[:, tile, :], ps[:, 0:NB])
                fw.ts("dve", kTs[:, tile, :], ps[:, 16:16 + NB], 0.0625, None, ALU.mult)
            for tile in range(8):
                fw.mm(PA[0:NB, tile * 128:(tile + 1) * 128], us_bf[:, tile, :], BDv[:, tile, :])
            fw.cp("act", vtoks[:], PA[0:NB, :])
            fw.tt("dve", g16[:, 16:20], g16[:, 4:8], g16[:, 8:12], ALU.add)
            fw.tt("dve", g16[:, 12:16], g16[:, 16:20], g16[:, 0:4], ALU.max)
            fw.dma("sp", io.mm_s[:, :], g16[:, 12:16])
            fw.tt("dve", g16[:, 20:24], g16[:, 0:4], g16[:, 12:16], ALU.subtract)
            fw.act(g16[:, 20:24], g16[:, 20:24], AF.Exp)
            fw.tt("dve", g16[:, 24:28], g16[:, 16:20], g16[:, 12:16], ALU.subtract)
            fw.act(g16[:, 24:28], g16[:, 24:28], AF.Exp)
            fw.act(g16[:, 28:32], g16[:, 12:16], AF.Exp, scale=-1.0)
            z3 = lambda r: r.rearrange("p (h b) -> p h b", h=4)
            fw.tt("dve", z3(Zd[:, 0:64]), g16[:, 20:24].bc(2, NB), ident[0:NB, 0:NB].bc(1, 4), ALU.mult)
            fw.tt("dve", z3(Zd[:, 64:128]), g16[:, 24:28].bc(2, NB), ident[0:NB, 0:NB].bc(1, 4), ALU.mult)
            fw.mm(PE[:, 128:256], ones[0:NB, :], Zd[:, :])
            fw.cp("dve", wd[:], PE[:, 128:256].rearrange("p (w h b) -> p w h b", w=2, h=4))
            k4 = lambda r: r.rearrange("p (h k) b -> p h k b", h=4)
            fw.tt("dve", k4(kws[:, :, :]), k4(kTs[:, :, :]), wd[:, 0, :, :].bc(2, 2), ALU.mult)
            fw.tt("dve", k4(nS[:, :, :]), k4(nS[:, :, :]), wd[:, 1, :, :].bc(2, 2), ALU.mult)
            fw.tt("dve", nS[:], nS[:], kws[:], ALU.add)
            fw.dma("sp", io.mn_s[:, :, :], nS[:])
            fw.tt("dve", tmps[:], qTs[:], nS[:], ALU.mult)
            for h in range(4):
                for kt in range(2):
                    fw.mm(PE[0:NB, 256 + 2 * h:258 + 2 * h], tmps[:, 2 * h + kt, :], ones[:, 0:2], start=kt == 0, stop=kt == 1)
            fw.tt("dve", qmask[:], qTs[:, :, :].bc(2, NB), eye16[:, :, :].bc(1, 8), ALU.mult)
            for b in range(NB):
                Cc = Cs[b % 2]
                fw.dma("sp", Cc[:], io.mc_s_in[b].rearrange("h (k p) v -> p (h k) v", p=128))
                fw.mm(PA[:, 0:512], sel16[:, b, :], vtoks[:, 0:512])
                fw.mm(PA[:, 512:1024], sel16[:, b, :], vtoks[:, 512:1024])
                fw.tt("dve", Tt[:, :, :].rearrange("p (h k) v -> p h k v", h=4),
                      PA[:, :].rearrange("p (h v) -> p h v", h=4).bc(2, 2),
                      kws[:, :, b].rearrange("p (h k) -> p h k", h=4).bc(3, 256), ALU.mult)
                fw.tt("pool", Cc[:, :, :].rearrange("p (h k) v -> p h (k v)", h=4),
                      Cc[:, :, :].rearrange("p (h k) v -> p h (k v)", h=4), wd[:, 1, :, b].bc(2, 512), ALU.mult)
                fw.tt("dve", Cc[:], Cc[:], Tt[:], ALU.add)
                fw.dma("sp", io.mc_s[b].rearrange("h (k p) v -> p (h k) v", p=128), Cc[:])
                for tile in range(8):
                    h, kt = tile // 2, tile % 2
                    fw.mm(PB[0:NB, h * 256:(h + 1) * 256], qmask[:, tile, b, :], Cc[:, tile, :],
                          start=(b == 0 and tile in (0, 4)), stop=(b == NB - 1 and kt == 1), skip=True)
            fw.cp("act", numt[:], PB[0:NB, :])
            if "dbg_a" in dbg:
                fw.dma("sp", io.dbg_a[:, :], numt[:])
                fw.cp("dve", g16[:, 40:44], PE[0:NB, 256:264].rearrange("p (h t) -> p h t", t=2)[:, :, 0])
                fw.dma("sp", io.dbg_b[:, :], g16[:])
            dn = PE[0:NB, 256:264].rearrange("p (h t) -> p h t", t=2)[:, :, 0]
            fw.ts("dve", g16[:, 32:36], dn, -1.0, None, ALU.mult)
            fw.tt("dve", g16[:, 32:36], g16[:, 32:36], dn, ALU.max)
            fw.tt("dve", g16[:, 32:36], g16[:, 32:36], g16[:, 28:32], ALU.max)
            fw.recip(g16[:, 36:40], g16[:, 32:36])
            fw.tt("dve", numt[:, :].rearrange("p (h v) -> p h v", h=4), numt[:, :].rearrange("p (h v) -> p h v", h=4),
                  g16[:, 36:40].bc(2, 256), ALU.mult)
            for h in range(4):
                grp_rstd(numt[:, h * 256:(h + 1) * 256], 256, nst[0:NB, 7:8], junk, NB)
                fw.ts("dve", hmn[0:NB, h * 256:(h + 1) * 256], numt[:, h * 256:(h + 1) * 256], nst[0:NB, 7:8], None, ALU.mult)
            to_feat(hmn, hmnT, NB)
            for tile in range(8):
                fw.ts("dve", hmfT[:, tile, 0:NB], hmnT[:, tile, 0:NB], mlcol[:, tile, 0:1], None, ALU.mult)
                fw.stt(hmfT[:, tile, 0:NB], cs_bf[:, tile, :], mlcol[:, tile, 1:2], hmfT[:, tile, 0:NB], ALU.mult, ALU.add)
            fw.tt("dve", hmfT[:, :, 0:NB], hmfT[:, :, 0:NB], sigoT[:, :, 0:NB], ALU.mult)
            for half in range(2):
                for kt in range(8):
                    fw.mm(PA[0:NB, half * 512:(half + 1) * 512], hmfT[:, kt, 0:NB], Wo[:, kt, half * 512:(half + 1) * 512],
                          start=kt == 0, stop=kt == 7)
            fw.tt("dve", x1[0:NB, :], x1[0:NB, :], PA[0:NB, :], ALU.add)
            fw.dma("sp", io.s1s[:, :], x1[0:NB, :])
        fw.dma("sp", io.mc_p[:, :, :, :], Cst[:])
        fw.dma("sp", io.mm_p[:, :], mprev[0:1, :])
        fw.dma("sp", io.conv_p[:, 12:20, :], convin[:, :, 0:3])
        fw.release(base_mark)

    def ffn_phase(layer, src, dst, ssrc, sdst, final):
        Wgu = fw.sb([128, 8, 2 * DFF], BF16, "Wgu")
        load_w(Wgu, io.w_gu[layer], 0, 8, step=1)
        Wd = fw.sb([128, 22, D], BF16, "Wd")
        load_w(Wd, io.w_dn[layer], 0, 22)
        gf = fw.sb([128, D], F32, "gf")
        fw.dma("sp", gf[:], io.norm_ffn[layer, :].partition_broadcast(128))
        if final:
            gfin = fw.sb([128, D], F32, "gfin")
            fw.dma("sp", gfin[:], io.norm_final.partition_broadcast(128))
        GB = 4
        xt = fw.sb([128, D], F32, "xt")
        junk = fw.sb([128, D], F32, "junk")
        xn = fw.sb([128, D], BF16, "xn")
        xnT = fw.sb([128, 8, GB * 128], BF16, "xnT")
        hT = fw.sb([128, 22, GB * 128], BF16, "hT")
        sg = [fw.sb([128, GB * 128], F32, f"sg{i}") for i in range(2)]
        x2 = fw.sb([128, D], F32, "x2")
        yo = junk
        groups = [list(range(g, min(g + GB, NCH))) for g in range(0, NCH, GB)]
        if SAMPLE:
            groups.append([NCH])
        for grp in groups:
            samp = grp[0] == NCH
            M = NB if samp else 128
            W = M * len(grp)
            rows = lambda ap, c: (ap[:, :] if samp else ap[c * 128:(c + 1) * 128, :])
            for gi, c in enumerate(grp):
                fw.dma("sp", xt[0:M, :], rows(ssrc if samp else src, c))
                rmsnorm(xt[0:M, :], gf, xn[0:M, :], M, junk)
                for kt in range(8):
                    fw.tr(PT3[:, kt, 0:M], xn[0:M, kt * 128:(kt + 1) * 128], identb[0:M, 0:M])
                fw.cp("dve", xnT[:, :, gi * M:(gi + 1) * M], PT3[:, :, 0:M])
            for j in range(22):
                psg = pcd[j % 2]
                for kt in range(8):
                    fw.mm(psg[:, 0:W], Wgu[:, kt, j * 128:(j + 1) * 128], xnT[:, kt, 0:W], start=kt == 0, stop=kt == 7)
                psu = PB0f if j % 2 == 0 else PB1f
                for kt in range(8):
                    fw.mm(psu[:, 0:W], Wgu[:, kt, DFF + j * 128:DFF + (j + 1) * 128], xnT[:, kt, 0:W],
                          start=kt == 0, stop=kt == 7)
                fw.act(sg[j % 2][:, 0:W], psg[:, 0:W], AF.Silu)
                fw.tt("dve", hT[:, j, 0:W], sg[j % 2][:, 0:W], psu[:, 0:W], ALU.mult)
            for gi, c in enumerate(grp):
                for half in range(2):
                    for j in range(22):
                        fw.mm(PA[0:M, half * 512:(half + 1) * 512], hT[:, j, gi * M:(gi + 1) * M],
                              Wd[:, j, half * 512:(half + 1) * 512], start=j == 0, stop=j == 21)
                fw.dma("sp", xt[0:M, :], rows(ssrc if samp else src, c))
                fw.tt("dve", x2[0:M, :], xt[0:M, :], PA[0:M, :], ALU.add)
                if final:
                    rmsnorm(x2[0:M, :], gfin, yo[0:M, :], M, junk)
                    fw.dma("sp", rows(sdst if samp else dst, c), yo[0:M, :])
                else:
                    fw.dma("sp", rows(sdst if samp else dst, c), x2[0:M, :])
        fw.release(base_mark)

    if "1" in phases:
        ffn_phase(0, io.s1, io.s2, io.s1s, io.s2s, False)

    if "2" in phases:
        Wr = fw.sb([128, 8, D], BF16, "Wr"); load_w(Wr, io.rw_wr, 0, 8)
        Wk = fw.sb([128, 8, D], BF16, "Wk"); load_w(Wk, io.rw_wk, 0, 8)
        A1 = fw.sb([128, 8, 64], BF16, "A1"); load_w(A1, io.rw_a1, 0, 8, step=8)
        A2 = fw.sb([128, D], BF16, "A2"); fw.dma("pool", A2[0:64, :], io.rw_a2[:, :])
        Wv = fw.sb([128, 8, D], BF16, "Wv"); load_w(Wv, io.rw_wv, 0, 8)
        G1 = fw.sb([128, 8, 160], BF16, "G1"); load_w(G1, io.rw_g1, 0, 8, step=8)
        G2a = fw.sb([128, D], BF16, "G2a"); fw.dma("pool", G2a[:], io.rw_g2[0:128, :])
        G2b = fw.sb([128, D], BF16, "G2b"); fw.dma("pool", G2b[0:32, :], io.rw_g2[128:160, :])
        W1 = fw.sb([128, 8, 64], BF16, "W1"); load_w(W1, io.rw_w1, 0, 8, step=8)
        W2 = fw.sb([128, D], BF16, "W2"); fw.dma("pool", W2[0:64, :], io.rw_w2[:, :])
        Wo = fw.sb([128, 8, D], BF16, "Wo"); load_w(Wo, io.rw_wo, 0, 8)
        gm1 = fw.sb([128, D], F32, "gm1")
        fw.dma("sp", gm1[:], io.norm_mix[1, :].partition_broadcast(128))
        rows = []
        for i in range(7):
            rt = fw.sb([128, D], F32, f"row{i}")
            fw.dma("sp", rt[:], io.rw_rows[i, :].partition_broadcast(128))
            rows.append(rt)
        w0b, a0b, kkb_, kab, rkb, lnw, lnb = rows
        mu = fw.sb([128, 8, 6], F32, "mu")
        fw.dma("sp", mu[:], io.rw_mu[:, :, :])
        PB0 = fw.view(PB[:, 0:512], "PB0")
        PB1 = fw.view(PB[:, 512:1024], "PB1")
        NPS = [PB0, PB1, PC, PD]
        PAh = [PA[:, 0:512], PA[:, 512:1024]]
        PBh = [PB0[:, :], PB1[:, :]]
        h1 = fw.sb([128, 2, 128], BF16, "h1")

        def proj_tok(xT, W, Ph, M=128):
            for half in range(2):
                for kt in range(8):
                    fw.mm(Ph[half][0:M, :], xT[:, kt, 0:M], W[:, kt, half * 512:(half + 1) * 512],
                          start=kt == 0, stop=kt == 7)

        def lora(xT, Wa, nh, Wb_list, func, P, M=128):
            widths = [min(128, nh), nh - 128] if nh > 128 else [nh]
            for wi, wd in enumerate(widths):
                for kt in range(8):
                    fw.mm(PE[0:wd, wi * 128:wi * 128 + M], Wa[:, kt, wi * 128:wi * 128 + wd], xT[:, kt, 0:M],
                          start=kt == 0, stop=kt == 7)
                fw.act(h1[0:wd, wi, 0:M], PE[0:wd, wi * 128:wi * 128 + M], func)
            for half in range(2):
                for wi, wd in enumerate(widths):
                    fw.mm(P[half][0:M, :], h1[0:wd, wi, 0:M], Wb_list[wi][0:wd, half * 512:(half + 1) * 512],
                          start=wi == 0, stop=wi == len(widths) - 1)

        def rstd16(src16, dst16, mult_, eps, floor=None):
            if floor is not None:
                fw.ts("dve", dst16, src16, floor, None, ALU.max)
            else:
                fw.ts("dve", dst16, src16, mult_, eps, ALU.mult, ALU.add)
            fw.act(dst16, dst16, AF.Ln)
            fw.act(dst16, dst16, AF.Exp, scale=-0.5)

        mark2 = fw.mark()
        xt = fw.sb([128, D], F32, "xt")
        junk = fw.sb([128, D], F32, "junk")
        tmpA = fw.sb([128, D], F32, "tmpA")
        tmpB = fw.sb([128, D], F32, "tmpB")
        Et = fw.sb([128, D], F32, "Et")
        SB = [fw.sb([128, D], BF16, f"S{i}") for i in range(13)]
        xn = SB[0]; r_bf = SB[1]; kkn = SB[2]; kf_bf = SB[3]; b_bf = SB[4]; v_bf = SB[5]; bv = SB[6]
        g_bf = SB[7]; abar = SB[8]; bbar = SB[9]; kbar = SB[10]; btil = SB[11]; ktil = SB[12]
        rbar = SB[0]; yo = SB[8]
        xnTe = fw.sb([128, 8, 130], BF16, "xnTe")
        fw.memset("pool", xnTe[:], 0.0)
        xx = fw.sb([128, 8, 128], BF16, "xx")
        mixb = [fw.sb([128, 8, 128], BF16, f"mix{i}") for i in range(2)]
        arT = fw.sb([128, 8, 2, 128], BF16, "arT")
        bT = fw.sb([128, 8, 128], BF16, "bT")
        kT = fw.sb([128, 8, 128], BF16, "kT")
        yoT = fw.sb([128, 8, 128], BF16, "yoT")
        Ms = [fw.sb([128, 512], BF16, f"Ms{i}") for i in range(4)]
        Q0 = [fw.sb([128, 128], BF16, f"Q0{i}") for i in range(4)]
        PQ = [[fw.sb([128, 384], BF16, f"PQ{i}{k}") for k in range(2)] for i in range(4)]
        RHSb = [fw.sb([128, 64], BF16, f"RHS{i}") for i in range(4)]
        Ubp = [fw.sb([128, 2, 64], BF16, f"Ubp{i}") for i in range(2)]
        Hst = fw.sb([128, 8, 64], F32, "Hst")
        fw.memset("pool", Hst[:], 0.0)
        Hb = fw.sb([128, 8, 64], BF16, "Hb")
        fw.memset("pool", Hb[:], 0.0)
        eLT = fw.sb([128, 8], F32, "eLT")
        s16 = fw.sb([128, 64], F32, "s16")
        x3 = tmpB
        mcount = [0]

        def mix(cidx):
            dst = mixb[mcount[0] % 2]
            mcount[0] += 1
            for kt in range(8):
                fw.stt(dst[:, kt, :], xx[:, kt, :], mu[:, kt, cidx:cidx + 1], xnTe[:, kt, 1:129], ALU.mult, ALU.add)
            return dst

        for c in range(NCH):
            fw.dma("sp", xt[:], io.s2[c * 128:(c + 1) * 128, :])
            if c == NCH - 1:
                fw.act(junk[:, :], xt[:], AF.Square, accum_out=nst[:, 0:1])
                fw.ts("dve", nst[:, 1:2], nst[:, 0:1], 1.0 / D, EPS, ALU.mult, ALU.add)
                fw.act(nst[:, 2:3], nst[:, 1:2], AF.Ln)
                fw.act(nst[:, 3:4], nst[:, 2:3], AF.Exp, scale=-0.5)
                fw.stt(tmpA[:], xt[:], nst[:, 3:4], gm1[:], ALU.mult, ALU.mult)
                fw.dma("sp", io.shift_p[:, :], tmpA[127:128, :])
                fw.cp("dve", xn[:], tmpA[:])
            else:
                rmsnorm(xt[:], gm1, xn[:], 128, junk)
            for kt in range(8):
                fw.tr(PT3[:, kt, :], xn[:, kt * 128:(kt + 1) * 128], identb[:, :])
            fw.cp("dve", xnTe[:, :, 1:129], PT3)
            fw.tt("pool", xx[:], xnTe[:, :, 0:128], xnTe[:, :, 1:129], ALU.subtract)
            PCDh = [PC[:, :], PD[:, :]]
            hv2 = lambda r, i: r[:, i * 512:(i + 1) * 512]
            proj_tok(mix(0), Wr, PAh)
            proj_tok(mix(2), Wk, PBh)
            fw.cp("act", r_bf[:], PA[:, :])
            lora(mix(4), A1, 64, [A2], AF.Copy, PCDh)
            for i in range(2):
                fw.tt("dve", hv2(tmpA, i), PBh[i], hv2(kkb_, i), ALU.mult)
            fw.tt("pool", junk[:], tmpA[:], tmpA[:], ALU.mult)
            fw.red(s16[:, 0:16], v16(junk[:, :]), ALU.add)
            rstd16(s16[:, 0:16], s16[:, 16:32], None, None, floor=1e-24)
            fw.tt("dve", v16(kkn[:, :]), v16(tmpA[:, :]), s16[:, 16:32].bc(2, 64), ALU.mult)
            proj_tok(mix(3), Wv, PAh)
            for i in range(2):
                fw.tt("dve", hv2(tmpB, i), PCDh[i], hv2(a0b, i), ALU.add)
            fw.act(tmpB[:], tmpB[:], AF.Sigmoid)
            fw.stt(junk[:], tmpB[:], 1.0, kab[:], ALU.subtract, ALU.mult)
            fw.ts("dve", junk[:], junk[:], 1.0, None, ALU.add)
            for i in range(2):
                fw.tt("dve", hv2(kf_bf, i), PBh[i], hv2(junk, i), ALU.mult)
            fw.tt("pool", b_bf[:], kkn[:], tmpB[:], ALU.mult)
            lora(mix(5), G1, 160, [G2a, G2b], AF.Sigmoid, PBh)
            fw.tt("pool", junk[:], r_bf[:], kf_bf[:], ALU.mult)
            fw.tt("pool", junk[:], junk[:], rkb[:], ALU.mult)
            fw.red(s16[:, 32:48], v16(junk[:, :]), ALU.add)
            fw.cp("act", v_bf[:], PA[:, :])
            fw.tt("dve", v16(bv[:, :]), v16(PA[:, :]), s16[:, 32:48].bc(2, 64), ALU.mult)
            lora(mix(1), W1, 64, [W2], AF.Tanh, PCDh)
            for i in range(2):
                fw.cp("act", g_bf[:, i * 512:(i + 1) * 512], PBh[i])
            for i in range(2):
                fw.tt("dve", hv2(tmpA, i), PCDh[i], hv2(w0b, i), ALU.add)
            fw.act(tmpA[:], tmpA[:], AF.Exp, scale=-1.0)
            fw.act(tmpA[:], tmpA[:], AF.Ln, bias=1.0)
            fw.ts("dve", tmpA[:], tmpA[:], -1.0, -0.5, ALU.mult, ALU.add)
            fw.act(Et[:], tmpA[:], AF.Exp)
            fw.mm(PB0[:, :], tri_le, Et[:, 0:512])
            fw.mm(PB1[:, :], tri_le, Et[:, 512:1024])
            fw.mm(PA[:, 0:512], ones, Et[:, 0:512])
            fw.mm(PA[:, 512:1024], ones, Et[:, 512:1024])
            for kt in range(8):
                fw.mm(PE[:, kt * 16:(kt + 1) * 16], Et[:, kt * 128:(kt + 1) * 128], ones[:, 0:16])
            fw.act(eLT[:], PE[:, 0:128].rearrange("p (k s) -> p k s", s=16)[:, :, 0], AF.Exp, scale=-1.0)
            hv = lambda r, i: r[:, i * 512:(i + 1) * 512]
            for i, PBi in enumerate((PB0, PB1)):
                fw.act(hv(tmpA, i), PBi[:, :], AF.Exp, scale=-1.0)
                fw.tt("pool", hv(rbar, i), hv(r_bf, i), hv(tmpA, i), ALU.mult)
                fw.tt("dve", hv(tmpB, i), hv(Et, i), PBi[:, :], ALU.subtract)
                fw.act(hv(tmpB, i), hv(tmpB, i), AF.Exp)
                fw.stt(hv(abar, i), hv(kkn, i), -1.0, hv(tmpB, i), ALU.mult, ALU.mult)
            fw.cp("act", junk[:], PA[:, :])
            for i, PBi in enumerate((PB0, PB1)):
                fw.act(hv(tmpA, i), PBi[:, :], AF.Exp)
                fw.tt("pool", hv(bbar, i), hv(b_bf, i), hv(tmpA, i), ALU.mult)
                fw.tt("pool", hv(kbar, i), hv(kf_bf, i), hv(tmpA, i), ALU.mult)
                fw.tt("dve", hv(tmpB, i), PBi[:, :], hv(junk, i), ALU.subtract)
                fw.act(hv(tmpB, i), hv(tmpB, i), AF.Exp)
                fw.tt("pool", hv(btil, i), hv(b_bf, i), hv(tmpB, i), ALU.mult)
                fw.tt("dve", hv(ktil, i), hv(kf_bf, i), hv(tmpB, i), ALU.mult)
            for src, dst in ((abar, arT[:, :, 0, :]), (rbar, arT[:, :, 1, :]), (bbar, bT[:, :, :]), (kbar, kT[:, :, :])):
                for kt in range(8):
                    fw.tr(PT3[:, kt, :], src[:, kt * 128:(kt + 1) * 128], identb[:, :])
                fw.cp("dve", dst, PT3)
            for h0 in range(0, 16, 4):
                hd = []
                for i in range(4):
                    h = h0 + i
                    j, e = h // 2, h % 2
                    p0 = 64 * e
                    hd.append(dict(h=h, j=j, e=e, p0=p0, NP=NPS[i],
                                   aT=arT[p0:p0 + 64, j, 0, :], rT=arT[p0:p0 + 64, j, 1, :],
                                   ar=arT[p0:p0 + 64, j, :, :].rearrange("p a t -> p (a t)"),
                                   bT=bT[p0:p0 + 64, j, :], kT=kT[p0:p0 + 64, j, :]))
                for i, d in enumerate(hd):
                    fw.mm(PE[:, 0:256], d["bT"], d["ar"])
                    fw.mm(PE[:, 256:512], d["kT"], d["ar"])
                    fw.tt("dve", Ms[i][:], PE[:, :], m4, ALU.mult)
                    fw.mm(d["NP"][:, 0:128], d["aT"], d["bT"])
                    fw.tt("dve", Q0[i][:], d["NP"][:, 0:128], mask_gt, ALU.mult)
                    d["P"], d["Q"], d["Z"] = Ms[i][:, 0:128], Q0[i][:], identb[:, :]
                for k in range(7):
                    for i, d in enumerate(hd):
                        NP = d["NP"]
                        if k < 6:
                            fw.mm(NP[:, 0:128], d["Q"], d["P"])
                            fw.mm(NP[:, 128:256], d["P"], d["Q"])
                        fw.mm(NP[:, 256:384], identb[:, :], d["Z"], start=True, stop=False)
                        fw.mm(NP[:, 256:384], d["Q"], d["Z"], start=False, stop=True)
                    for i, d in enumerate(hd):
                        NP = d["NP"]
                        pq = PQ[i][k % 2]
                        lo = 0 if k < 6 else 256
                        fw.cp("act" if i % 2 == 0 else "dve", pq[:, lo:384], NP[:, lo:384])
                        d["P"], d["Q"], d["Z"] = pq[:, 0:128], pq[:, 128:256], pq[:, 256:384]
                for i, d in enumerate(hd):
                    NP, h, j, p0 = d["NP"], d["h"], d["j"], d["p0"]
                    fw.mm(NP[:, 384:448], d["aT"], Hb[p0:p0 + 64, j, :], start=True, stop=False)
                    fw.mm(NP[:, 384:448], Ms[i][:, 256:384], v_bf[:, h * 64:(h + 1) * 64], start=False, stop=True)
                    fw.cp("act", RHSb[i][:], NP[:, 384:448])
                for i, d in enumerate(hd):
                    NP, h, j, e = d["NP"], d["h"], d["j"], d["e"]
                    fw.mm(NP[:, 448:512], d["Z"], RHSb[i][:])
                    fw.cp("dve", Ubp[j % 2][:, e, :], NP[:, 448:512])
                for i, d in enumerate(hd):
                    h, j, e, p0 = d["h"], d["j"], d["e"], d["p0"]
                    ysl = PA[:, h * 64:(h + 1) * 64]
                    fw.mm(ysl, d["rT"], Hb[p0:p0 + 64, j, :], start=True, stop=False)
                    fw.mm(ysl, Ms[i][:, 128:256], Ubp[j % 2][:, e, :], start=False, stop=False)
                    fw.mm(ysl, Ms[i][:, 384:512], v_bf[:, h * 64:(h + 1) * 64], start=False, stop=True)
                for jj in range(2):
                    j = h0 // 2 + jj
                    NP = hd[2 * jj]["NP"]
                    fw.mm(NP[:, 0:128], btil[:, j * 128:(j + 1) * 128], Ubp[j % 2][:, :, :].rearrange("p e v -> p (e v)"),
                          start=True, stop=False)
                    fw.mm(NP[:, 0:128], ktil[:, j * 128:(j + 1) * 128], v_bf[:, j * 128:(j + 1) * 128],
                          start=False, stop=True)
                    for e in range(2):
                        p0 = 64 * e
                        fw.stt(Hst[p0:p0 + 64, j, :], Hst[p0:p0 + 64, j, :], eLT[p0:p0 + 64, j:j + 1],
                               NP[p0:p0 + 64, p0:p0 + 64], ALU.mult, ALU.add)
                    fw.cp("pool", Hb[:, j, :], Hst[:, j, :])
            fw.cp("act", tmpA[:], PA[:, :])
            fw.red(s16[:, 0:16], v16(tmpA[:, :]), ALU.add)
            fw.ts("dve", s16[:, 0:16], s16[:, 0:16], 1.0 / 64, None, ALU.mult)
            fw.tt("dve", v16(tmpA[:, :]), v16(tmpA[:, :]), s16[:, 0:16].bc(2, 64), ALU.subtract)
            fw.tt("pool", junk[:], tmpA[:], tmpA[:], ALU.mult)
            fw.red(s16[:, 16:32], v16(junk[:, :]), ALU.add)
            rstd16(s16[:, 16:32], s16[:, 48:64], 1.0 / 64, 64e-5)
            fw.tt("dve", v16(tmpA[:, :]), v16(tmpA[:, :]), s16[:, 48:64].bc(2, 64), ALU.mult)
            fw.tt("dve", tmpA[:], tmpA[:], lnw[:], ALU.mult)
            fw.tt("dve", tmpA[:], tmpA[:], lnb[:], ALU.add)
            fw.tt("dve", tmpA[:], tmpA[:], bv[:], ALU.add)
            fw.tt("dve", yo[:], tmpA[:], g_bf[:], ALU.mult)
            to_feat(yo, yoT, 128)
            for half in range(2):
                for kt in range(8):
                    fw.mm(PA[:, half * 512:(half + 1) * 512], yoT[:, kt, :], Wo[:, kt, half * 512:(half + 1) * 512],
                          start=kt == 0, stop=kt == 7)
            fw.tt("dve", x3[:], xt[:], PA[:, :], ALU.add)
            fw.dma("sp", io.s3[c * 128:(c + 1) * 128, :], x3[:])
            fw.cp("pool", xnTe[:, :, 0:1], xnTe[:, :, 128:129])
        fw.dma("sp", io.wkv_p[:, :, :], Hst[:])
        fw.release(mark2)
        if SAMPLE:
            M = NB
            f16 = lambda nm, dt=F32: fw.sb([NB, D], dt, nm)
            xts = f16("xts"); jk = f16("jk"); tA = f16("tA"); tB = f16("tB"); Es = f16("Es")
            r_s = f16("r_s", BF16); kk_s = f16("kk_s", BF16); kf_s = f16("kf_s", BF16); b_s = f16("b_s", BF16)
            v_s = f16("v_s"); bv_s = f16("bv_s", BF16); g_s = f16("g_s", BF16); xn_s = f16("xn_s", BF16)
            sa_tok = f16("sa_tok"); yo_s = f16("yo_s", BF16)
            xprev = fw.sb([128, 8, NB], F32, "xprev")
            xsT = fw.sb([128, 8, NB], BF16, "xsT")
            xxs = fw.sb([128, 8, NB], BF16, "xxs")
            mixs = [fw.sb([128, 8, NB], BF16, f"mixs{i}") for i in range(2)]
            featT = {nm: fw.sb([128, 8, NB], F32, nm) for nm in ("aT", "wT", "bTs", "kTs", "rTs")}
            amask = fw.sb([128, 8, NB, NB], F32, "amask")
            rmask = fw.sb([128, 8, NB, NB], F32, "rmask")
            Hs = [fw.sb([128, 8, 64], F32, f"Hs{i}") for i in range(2)]
            Tt = fw.sb([128, 8, 64], F32, "Tt")
            yoTs = fw.sb([128, 8, NB], BF16, "yoTs")
            s16 = fw.sb([NB, 64], F32, "s16s")
            v16s = lambda r: r.rearrange("p (h q) -> p h q", h=16)
            mc2 = [0]

            def mix_s(cidx):
                dst = mixs[mc2[0] % 2]
                mc2[0] += 1
                for kt in range(8):
                    fw.stt(dst[:, kt, :], xxs[:, kt, :], mu[:, kt, cidx:cidx + 1], xsT[:, kt, :], ALU.mult, ALU.add)
                return dst

            fw.dma("sp", xts[:], io.s2s[:, :])
            fw.dma("sp", xprev[:], io.shift_s_in[:, :, :])
            fw.act(jk[:], xts[:], AF.Square, accum_out=nst[0:M, 0:1])
            fw.ts("dve", nst[0:M, 1:2], nst[0:M, 0:1], 1.0 / D, EPS, ALU.mult, ALU.add)
            fw.act(nst[0:M, 2:3], nst[0:M, 1:2], AF.Ln)
            fw.act(nst[0:M, 3:4], nst[0:M, 2:3], AF.Exp, scale=-0.5)
            fw.stt(tA[:], xts[:], nst[0:M, 3:4], gm1[0:M, :], ALU.mult, ALU.mult)
            fw.dma("sp", io.shift_s[:, :], tA[:])
            fw.cp("dve", xn_s[:], tA[:])
            for kt in range(8):
                fw.tr(PT3[:, kt, 0:M], xn_s[:, kt * 128:(kt + 1) * 128], identb[0:M, 0:M])
            fw.cp("dve", xsT[:], PT3[:, :, 0:M])
            fw.tt("dve", xxs[:], xprev[:], xsT[:], ALU.subtract)
            PAm = [PA[0:M, 0:512], PA[0:M, 512:1024]]
            PBm = [PB0[0:M, :], PB1[0:M, :]]
            PAf = PA[0:M, :]
            proj_tok(mix_s(0), Wr, PAh, M)
            fw.cp("act", r_s[:], PAf)
            proj_tok(mix_s(2), Wk, PAh, M)
            fw.tt("dve", tA[:], PAf, kkb_[0:M, :], ALU.mult)
            fw.tt("dve", jk[:], tA[:], tA[:], ALU.mult)
            fw.red(s16[:, 0:16], v16s(jk[:, :]), ALU.add)
            rstd16(s16[:, 0:16], s16[:, 16:32], None, None, floor=1e-24)
            fw.tt("dve", v16s(kk_s[:, :]), v16s(tA[:, :]), s16[:, 16:32].bc(2, 64), ALU.mult)
            lora(mix_s(4), A1, 64, [A2], AF.Copy, PBh, M)
            for i in range(2):
                fw.tt("dve", tB[:, i * 512:(i + 1) * 512], PBm[i], a0b[0:M, i * 512:(i + 1) * 512], ALU.add)
            fw.act(tB[:], tB[:], AF.Sigmoid)
            fw.stt(jk[:], tB[:], 1.0, kab[0:M, :], ALU.subtract, ALU.mult)
            fw.ts("dve", jk[:], jk[:], 1.0, None, ALU.add)
            fw.tt("dve", kf_s[:], PAf, jk[:], ALU.mult)
            fw.tt("dve", b_s[:], kk_s[:], tB[:], ALU.mult)
            fw.tt("dve", jk[:], r_s[:], kf_s[:], ALU.mult)
            fw.tt("dve", jk[:], jk[:], rkb[0:M, :], ALU.mult)
            fw.red(s16[:, 32:48], v16s(jk[:, :]), ALU.add)
            proj_tok(mix_s(3), Wv, PAh, M)
            fw.cp("act", v_s[:], PAf)
            fw.tt("dve", v16s(bv_s[:, :]), v16s(PAf), s16[:, 32:48].bc(2, 64), ALU.mult)
            lora(mix_s(5), G1, 160, [G2a, G2b], AF.Sigmoid, PBh, M)
            for i in range(2):
                fw.cp("act", g_s[:, i * 512:(i + 1) * 512], PBm[i])
            lora(mix_s(1), W1, 64, [W2], AF.Tanh, PAh, M)
            fw.tt("dve", tA[:], PAf, w0b[0:M, :], ALU.add)
            fw.act(tA[:], tA[:], AF.Exp, scale=-1.0)
            fw.act(tA[:], tA[:], AF.Ln, bias=1.0)
            fw.ts("dve", tA[:], tA[:], -1.0, -0.5, ALU.mult, ALU.add)
            fw.act(Es[:], tA[:], AF.Exp)
            fw.act(Es[:], Es[:], AF.Exp, scale=-1.0)
            fw.ts("dve", tB[:], kk_s[:], -1.0, None, ALU.mult)
            for nm, src in (("aT", tB), ("wT", Es)):
                for kt in range(8):
                    fw.tr(PE[:, kt * 16:(kt + 1) * 16], src[:, kt * 128:(kt + 1) * 128], ident[0:M, 0:M])
                fw.cp("dve", featT[nm][:], PE[:, 0:128].rearrange("p (k b) -> p k b", k=8))
            for nm, src in (("bTs", b_s), ("kTs", kf_s), ("rTs", r_s)):
                for kt in range(8):
                    fw.tr(PT3[:, kt, 0:M], src[:, kt * 128:(kt + 1) * 128], identb[0:M, 0:M])
                fw.cp("dve", featT[nm][:], PT3[:, :, 0:M])
            fw.tt("dve", amask[:], featT["aT"][:, :, :].bc(2, NB), eye16[:, :, :].bc(1, 8), ALU.mult)
            fw.tt("dve", rmask[:], featT["rTs"][:, :, :].bc(2, NB), eye16[:, :, :].bc(1, 8), ALU.mult)
            SY = [PC, PD]
            for b in range(NB):
                H = Hs[b % 2]
                fw.dma("sp", H[:], io.wkv_s_in[b])
                for h in range(16):
                    j, e = h // 2, h % 2
                    p0 = 64 * e
                    fw.mm(SY[e][0:M, j * 64:(j + 1) * 64], amask[p0:p0 + 64, j, b, :], H[p0:p0 + 64, j, :],
                          start=(b == 0 and j == 0), stop=(b == NB - 1), skip=True)
            je = lambda r: r.rearrange("p (j e v) -> p j e v", j=8, e=2)
            fw.cp("dve", je(sa_tok[:, :])[:, :, 0, :], PC[0:M, :].rearrange("p (j v) -> p j v", j=8))
            fw.cp("dve", je(sa_tok[:, :])[:, :, 1, :], PD[0:M, :].rearrange("p (j v) -> p j v", j=8))
            e4 = lambda r, e: r.rearrange("p (j e v) -> p j e v", j=8, e=2)[64 * e:64 * e + 64, :, e, :]
            for b in range(NB):
                H = Hs[b % 2]
                fw.dma("sp", H[:], io.wkv_s_in[b])
                fw.mm(PA[:, 0:512], sel16[:, b, :], sa_tok[:, 0:512])
                fw.mm(PA[:, 512:1024], sel16[:, b, :], sa_tok[:, 512:1024])
                fw.mm(PB0[:, :], sel16[:, b, :], v_s[:, 0:512])
                fw.mm(PB1[:, :], sel16[:, b, :], v_s[:, 512:1024])
                fw.tt("pool", H[:], H[:], featT["wT"][:, :, b].bc(2, 64), ALU.mult)
                for e in range(2):
                    p0 = 64 * e
                    fw.tt("dve", Tt[p0:p0 + 64, :, :], e4(PA[:, :], e), featT["bTs"][p0:p0 + 64, :, b].bc(2, 64), ALU.mult)
                fw.tt("dve", H[:], H[:], Tt[:], ALU.add)
                for e in range(2):
                    p0 = 64 * e
                    for half, PBx in enumerate((PB0, PB1)):
                        src = PBx[:, :].rearrange("p (j e v) -> p j e v", j=4, e=2)[p0:p0 + 64, :, e, :]
                        fw.tt("dve", Tt[p0:p0 + 64, 4 * half:4 * half + 4, :], src,
                              featT["kTs"][p0:p0 + 64, 4 * half:4 * half + 4, b].bc(2, 64), ALU.mult)
                fw.tt("dve", H[:], H[:], Tt[:], ALU.add)
                fw.dma("sp", io.wkv_s[b], H[:])
                for h in range(16):
                    j, e = h // 2, h % 2
                    p0 = 64 * e
                    fw.mm(SY[e][0:M, j * 64:(j + 1) * 64], rmask[p0:p0 + 64, j, b, :], H[p0:p0 + 64, j, :],
                          start=(b == 0 and j == 0), stop=(b == NB - 1), skip=True)
            fw.cp("dve", je(tA[:, :])[:, :, 0, :], PC[0:M, :].rearrange("p (j v) -> p j v", j=8))
            fw.cp("dve", je(tA[:, :])[:, :, 1, :], PD[0:M, :].rearrange("p (j v) -> p j v", j=8))
            fw.red(s16[:, 0:16], v16s(tA[:, :]), ALU.add)
            fw.ts("dve", s16[:, 0:16], s16[:, 0:16], 1.0 / 64, None, ALU.mult)
            fw.tt("dve", v16s(tA[:, :]), v16s(tA[:, :]), s16[:, 0:16].bc(2, 64), ALU.subtract)
            fw.tt("dve", jk[:], tA[:], tA[:], ALU.mult)
            fw.red(s16[:, 16:32], v16s(jk[:, :]), ALU.add)
            rstd16(s16[:, 16:32], s16[:, 48:64], 1.0 / 64, 64e-5)
            fw.tt("dve", v16s(tA[:, :]), v16s(tA[:, :]), s16[:, 48:64].bc(2, 64), ALU.mult)
            fw.tt("dve", tA[:], tA[:], lnw[0:M, :], ALU.mult)
            fw.tt("dve", tA[:], tA[:], lnb[0:M, :], ALU.add)
            fw.tt("dve", tA[:], tA[:], bv_s[:], ALU.add)
            fw.tt("dve", yo_s[:], tA[:], g_s[:], ALU.mult)
            for kt in range(8):
                fw.tr(PT3[:, kt, 0:M], yo_s[:, kt * 128:(kt + 1) * 128], identb[0:M, 0:M])
            fw.cp("dve", yoTs[:], PT3[:, :, 0:M])
            for half in range(2):
                for kt in range(8):
                    fw.mm(PA[0:M, half * 512:(half + 1) * 512], yoTs[:, kt, :], Wo[:, kt, half * 512:(half + 1) * 512],
                          start=kt == 0, stop=kt == 7)
            fw.tt("dve", tB[:], xts[:], PA[0:M, :], ALU.add)
            fw.dma("sp", io.s3s[:, :], tB[:])
        fw.release(base_mark)

    if "3" in phases:
        ffn_phase(1, io.s3, io.y_p, io.s3s, io.y_s, True)

    fw.finish()
    fw.close()
    return nc, fw


def prep_common(inp):
    f = lambda k: np.ascontiguousarray(np.asarray(inp[k], np.float32))
    m = {}
    m["cst"] = host_consts()
    m["w_in0"] = f("w_in0")[0]
    m["w_out0"] = f("w_out0")[0]
    m["norm_mix"] = f("norm_mix")
    m["norm_ffn"] = f("norm_ffn")
    m["norm_final"] = f("norm_final")
    m["ssd_norm"] = f("ssd_norm")[0]
    small0 = np.zeros(64, np.float32)
    small0[0:16] = f("ssd_dt_bias")[0]
    small0[16:32] = f("ssd_a_log")[0]
    small0[32:48] = f("ssd_d")[0]
    small0[48:52] = f("ml_i_bias")[0]
    small0[52:56] = f("ml_f_bias")[0]
    m["small0"] = small0
    cw = f("conv_w")[0].reshape(4, 20, 128).transpose(2, 1, 0)
    cb = f("conv_b")[0].reshape(20, 128).T[:, :, None]
    m["convp"] = np.ascontiguousarray(np.concatenate([cw, cb], axis=2))
    m["mlcol"] = np.ascontiguousarray(np.stack([f("ml_norm")[0].reshape(8, 128).T, f("ml_skip")[0].reshape(8, 128).T], axis=2))
    m["bdq"] = blockdiag(f("ml_wq")[0])
    m["bdk"] = blockdiag(f("ml_wk")[0])
    m["bdv"] = blockdiag(f("ml_wv")[0])
    for nm in ("rw_wr", "rw_wk", "rw_wv", "rw_wo", "rw_w1", "rw_w2", "rw_a1", "rw_a2", "rw_g1", "rw_g2"):
        m[nm] = f(nm)[0]
    m["rw_rows"] = np.ascontiguousarray(np.stack([f(k)[0] for k in ("rw_w0", "rw_a0", "rw_k_k", "rw_k_a", "rw_r_k", "rw_ln_w", "rw_ln_b")]))
    m["rw_mu"] = np.ascontiguousarray(f("rw_mu")[0].reshape(6, 8, 128).transpose(2, 1, 0))
    m["w_gu"] = f("ffn_w_gate_up")
    m["w_dn"] = f("ffn_w_down")
    return m


def prep_core(inp, core):
    f = lambda k: np.asarray(inp[k], np.float32)
    b0 = core * NB
    m = {}
    m["xp"] = np.ascontiguousarray(f("x_prompt")[core])
    m["xs"] = np.ascontiguousarray(f("x_sample")[b0:b0 + NB, 0, :])
    m["conv_s_in"] = np.ascontiguousarray(f("state_conv")[0, b0:b0 + NB].reshape(NB, 3, 20, 128).transpose(3, 2, 1, 0))
    m["ssm_s_in"] = np.ascontiguousarray(f("state_ssm")[0, b0:b0 + NB].reshape(NB, D, 128))
    m["mc_s_in"] = np.ascontiguousarray(f("state_mlstm_c")[0, b0:b0 + NB])
    m["mn_s_in"] = np.ascontiguousarray(f("state_mlstm_n")[0, b0:b0 + NB].reshape(NB, 4, 2, 128).transpose(3, 1, 2, 0).reshape(128, 8, NB))
    m["mm_s_in"] = np.ascontiguousarray(f("state_mlstm_m")[0, b0:b0 + NB])
    m["shift_s_in"] = np.ascontiguousarray(f("state_shift")[0, b0:b0 + NB].reshape(NB, 8, 128).transpose(2, 1, 0))
    m["wkv_s_in"] = np.ascontiguousarray(f("state_wkv")[0, b0:b0 + NB].reshape(NB, 8, 2, 64, 64).transpose(0, 2, 4, 1, 3).reshape(NB, 128, 8, 64))
    return m


def prep_consts(inp):
    f = lambda k: np.asarray(inp[k], np.float32)
    m = prep_common(inp)
    m["c16"] = host_consts16()
    m["eye16"] = np.ascontiguousarray(np.broadcast_to(np.eye(16, dtype=np.float32), (128, 16, 16)))
    dtcol = np.zeros((16, 4), np.float32)
    dtcol[:, 0] = f("ssd_dt_bias")[0]
    dtcol[:, 1] = f("ssd_a_log")[0]
    dtcol[:, 2] = f("ssd_d")[0]
    m["dtcol"] = dtcol
    return m


_NC_CACHE = {}


def kernel(**inp):
    if "nc" not in _NC_CACHE:
        _NC_CACHE["nc"] = build({})
    nc = _NC_CACHE["nc"]
    cm = prep_consts(inp)
    in_maps = [dict(cm, **prep_core(inp, c)) for c in range(NCORE)]
    res = run_bass_kernel_spmd(nc, in_maps, core_ids=list(range(NCORE)))
    R = res.results
    BT = NCORE * NB
    y_p = np.zeros((NCORE, T, D), np.float32)
    y_s = np.zeros((BT, 1, D), np.float32)
    conv_p = np.zeros((1, NCORE, 3, 2560), np.float32)
    conv_s = np.zeros((1, BT, 3, 2560), np.float32)
    ssm_p = np.zeros((1, NCORE, 16, 64, 128), np.float32)
    ssm_s = np.zeros((1, BT, 16, 64, 128), np.float32)
    mc_p = np.zeros((1, NCORE, 4, 256, 256), np.float32)
    mc_s = np.zeros((1, BT, 4, 256, 256), np.float32)
    mn_p = np.zeros((1, NCORE, 4, 256), np.float32)
    mn_s = np.zeros((1, BT, 4, 256), np.float32)
    mm_p = np.zeros((1, NCORE, 4), np.float32)
    mm_s = np.zeros((1, BT, 4), np.float32)
    sh_p = np.zeros((1, NCORE, D), np.float32)
    sh_s = np.zeros((1, BT, D), np.float32)
    wkv_p = np.zeros((1, NCORE, 16, 64, 64), np.float32)
    wkv_s = np.zeros((1, BT, 16, 64, 64), np.float32)
    for c in range(NCORE):
        r = R[c]
        sl = slice(c * NB, (c + 1) * NB)
        y_p[c] = r["y_p"]
        y_s[sl, 0] = r["y_s"]
        conv_p[0, c] = r["conv_p"].transpose(2, 1, 0).reshape(3, 2560)
        conv_s[0, sl] = r["conv_s"].transpose(3, 2, 1, 0).reshape(NB, 3, 2560)
        ssm_p[0, c] = r["ssm_p"].reshape(128, 16, 64).transpose(1, 2, 0)
        ssm_s[0, sl] = r["ssm_s"].reshape(NB, 16, 64, 128)
        mc = r["mc_p"]
        mc_p[0, c] = mc[:, :, :, :256].transpose(2, 1, 0, 3).reshape(4, 256, 256)
        mn_p[0, c] = mc[:, :, :, 256].transpose(2, 1, 0).reshape(4, 256)
        mc_s[0, sl] = r["mc_s"]
        mn_s[0, sl] = r["mn_s"].reshape(128, 4, 2, NB).transpose(3, 1, 2, 0).reshape(NB, 4, 256)
        mm_p[0, c] = r["mm_p"][0]
        mm_s[0, sl] = r["mm_s"]
        sh_p[0, c] = r["shift_p"][0]
        sh_s[0, sl] = r["shift_s"]
        wkv_p[0, c] = r["wkv_p"].reshape(2, 64, 8, 64).transpose(2, 0, 3, 1).reshape(16, 64, 64)
        wkv_s[0, sl] = r["wkv_s"].reshape(NB, 2, 64, 8, 64).transpose(0, 3, 1, 4, 2).reshape(NB, 16, 64, 64)
    return (y_p, y_s, conv_p, conv_s, ssm_p, ssm_s, mc_p, mc_s, mn_p, mn_s, mm_p, mm_s, sh_p, sh_s, wkv_p, wkv_s)
```

```python
import numpy as np
import concourse.bass as bass
import concourse.mybir as mybir
from concourse.bass_utils import run_bass_kernel_spmd

F32 = mybir.dt.float32
BF16 = mybir.dt.bfloat16
ALU = mybir.AluOpType
AF = mybir.ActivationFunctionType
AX = mybir.AxisListType

NCORE = 8
D = 1024
T = 2048
NB = 16
IN0 = 4632
DFF = 2816
EPS = 1e-5


class Tok:
    __slots__ = ("sem", "val", "eng", "seq")

    def __init__(self, sem, val, eng, seq=None):
        self.sem, self.val, self.eng, self.seq = sem, val, eng, seq


class Ref:
    __slots__ = ("T", "ap")

    def __init__(self, T_, ap):
        self.T, self.ap = T_, ap

    def __getitem__(self, k):
        return Ref(self.T, self.ap[k])

    def rearrange(self, p, **kw):
        return Ref(self.T, self.ap.rearrange(p, **kw))

    def unsqueeze(self, a):
        return Ref(self.T, self.ap.unsqueeze(a))

    def to_broadcast(self, shp):
        return Ref(self.T, self.ap.to_broadcast(list(shp)))

    def bc(self, axis, n):
        ap = self.ap.unsqueeze(axis)
        shp = list(ap.shape)
        shp[axis] = n
        return Ref(self.T, ap.to_broadcast(shp))


class TT:
    __slots__ = ("t", "name", "lw", "rd", "psum")

    def __init__(self, t, name, psum=False):
        self.t, self.name, self.lw, self.rd, self.psum = t, name, None, [], psum

    def __getitem__(self, k):
        return Ref(self, self.t[k])


def _Ts(*xs):
    return [x.T for x in xs if isinstance(x, Ref)]


def _a(x):
    return x.ap if isinstance(x, Ref) else x


class Eng:
    def __init__(self, fw, name, h):
        self.fw, self.name, self.h = fw, name, h
        self.sems, self.n, self.waited, self.nsig = [], 0, {}, 0


class Fw:
    EPOCH = 30000
    NDMA = 10

    def __init__(self, nc, need=None):
        self.nc = nc
        self.need = need
        self.waited_on = set()
        self._ctx = []
        self.E = {}
        for name, h in (("pe", nc.tensor), ("dve", nc.vector), ("act", nc.scalar),
                        ("pool", nc.gpsimd), ("sp", nc.sync)):
            self.E[name] = Eng(self, name, h)
        self.dma_sems, self.dma_i = {}, {}
        self.ntile = 0
        self.sb_bytes = 0

    def enter(self, cm):
        v = cm.__enter__()
        self._ctx.append(cm)
        return v

    def close(self):
        for cm in reversed(self._ctx):
            cm.__exit__(None, None, None)
        self._ctx = []

    def new_sem(self, name):
        return self.enter(self.nc.semaphore(name))

    def presem(self, queues=("sp", "pool", "act"), epochs=3):
        for e in self.E.values():
            while len(e.sems) < epochs:
                e.sems.append(self.new_sem(f"e_{e.name}_{len(e.sems)}"))
        for q in queues:
            self.dma_sems[q] = [[self.new_sem(f"d_{q}_{i}"), 0] for i in range(Fw.NDMA)]
            self.dma_i[q] = 0

    def mark(self):
        return len(self._ctx)

    def release(self, mark):
        self.barrier()
        while len(self._ctx) > mark:
            self._ctx.pop().__exit__(None, None, None)

    def _last_tok(self, e):
        return e.last

    def barrier(self):
        for eng in self.E.values():
            for q, slots in self.dma_sems.items():
                for sem, cnt in slots:
                    if cnt > 0:
                        self._wait(eng, Tok(sem, cnt, "dma"))
            for name, e in self.E.items():
                if e is eng or e.n == 0:
                    continue
                self._wait(eng, self._last_tok(e))

    def sb(self, shape, dt=F32, name="t"):
        self.ntile += 1
        n = 1
        for s in shape[1:]:
            n *= s
        self.sb_bytes += n * (2 if dt == BF16 else 4)
        return TT(self.enter(self.nc.sbuf_tensor(f"{name}_{self.ntile}", list(shape), dt)), name)

    def ps(self, shape, dt=F32, name="p"):
        self.ntile += 1
        return TT(self.enter(self.nc.psum_tensor(f"{name}_{self.ntile}", list(shape), dt)), name, psum=True)

    def view(self, ref, name="v"):
        return TT(ref.ap, name, psum=ref.T.psum)

    def _wait(self, eng, tok):
        if tok is None:
            return
        key = id(tok.sem)
        if eng.waited.get(key, 0) >= tok.val:
            return
        if tok.seq is not None:
            self.waited_on.add((tok.eng, tok.seq))
            assert tok.val == int(tok.val), "wait on a non-signalling instruction (two-pass mismatch)"
        eng.h.wait_ge(tok.sem, int(tok.val))
        eng.waited[key] = tok.val

    def _deps(self, eng, reads, writes):
        for t in reads:
            if t.lw is not None:
                self._wait(eng, t.lw)
            if t.psum:
                for r in t.rd:
                    if r.eng != eng.name:
                        self._wait(eng, r)
        strict = eng.name != "pe"
        for t in writes:
            if t.lw is not None and (strict or t.lw.eng != eng.name):
                self._wait(eng, t.lw)
            for r in t.rd:
                if strict or r.eng != eng.name:
                    self._wait(eng, r)

    def _mark(self, tok, reads, writes):
        for t in reads:
            t.rd.append(tok)
        for t in writes:
            t.lw = tok
            t.rd = []

    def op(self, e, fn, reads=(), writes=()):
        eng = self.E[e]
        self._deps(eng, reads, writes)
        seq = eng.n
        eng.n += 1
        signal = self.need is None or (eng.name, seq) in self.need
        ep = eng.nsig // Fw.EPOCH
        while len(eng.sems) <= ep:
            eng.sems.append(self.new_sem(f"e_{eng.name}_{len(eng.sems)}"))
        sem = eng.sems[ep]
        inst = fn(eng.h)
        if signal:
            val = eng.nsig % Fw.EPOCH + 1
            eng.nsig += 1
            inst.then_inc(sem, 1)
        else:
            val = eng.nsig % Fw.EPOCH + 0.5
        tok = Tok(sem, val, eng.name, seq)
        eng.last = tok
        self._mark(tok, reads, writes)
        return tok

    def dma(self, q, out, in_, **kw):
        eng = self.E[q]
        if q not in self.dma_sems:
            self.dma_sems[q] = [[self.new_sem(f"d_{q}_{i}"), 0] for i in range(Fw.NDMA)]
            self.dma_i[q] = 0
        slot = self.dma_sems[q][self.dma_i[q] % Fw.NDMA]
        self.dma_i[q] += 1
        sem, cnt = slot
        if cnt > 0:
            self._wait(eng, Tok(sem, cnt, "dma"))
        reads, writes = _Ts(in_), _Ts(out)
        self._deps(eng, reads, writes)
        inst = eng.h.dma_start(out=_a(out), in_=_a(in_), **kw)
        slot[1] = cnt + 16
        inst.then_inc(sem, 16)
        tok = Tok(sem, cnt + 16, "dma")
        self._mark(tok, reads, writes)
        return tok

    def finish(self):
        eng = self.E["sp"]
        for q, slots in self.dma_sems.items():
            for sem, cnt in slots:
                if cnt > 0:
                    self._wait(eng, Tok(sem, cnt, "dma"))
        for name, e in self.E.items():
            if name == "sp" or e.n == 0:
                continue
            self._wait(eng, self._last_tok(e))

    def mm(self, out, lhsT, rhs, start=True, stop=True, skip=False):
        kw = {"skip_group_check": True} if skip else {}
        return self.op("pe", lambda e: e.matmul(_a(out), _a(lhsT), _a(rhs), start=start, stop=stop, **kw),
                       _Ts(lhsT, rhs), _Ts(out))

    def tr(self, out, in_, ident):
        return self.op("pe", lambda e: e.transpose(_a(out), _a(in_), _a(ident)), _Ts(in_, ident), _Ts(out))

    def act(self, out, in_, func, bias=None, scale=None, accum_out=None):
        kw = {}
        if bias is not None:
            kw["bias"] = _a(bias)
        if scale is not None:
            kw["scale"] = _a(scale)
        if accum_out is not None:
            kw["accum_out"] = _a(accum_out)
        return self.op("act", lambda e: e.activation(out=_a(out), in_=_a(in_), func=func, **kw),
                       _Ts(in_, bias, scale), _Ts(out, accum_out))

    def tt(self, e, out, in0, in1, op):
        return self.op(e, lambda h: h.tensor_tensor(out=_a(out), in0=_a(in0), in1=_a(in1), op=op),
                       _Ts(in0, in1), _Ts(out))

    def ts(self, e, out, in0, s1, s2, op0, op1=None, accum_out=None):
        kw = {}
        if op1 is not None:
            kw["op1"] = op1
        if accum_out is not None:
            kw["accum_out"] = _a(accum_out)
        return self.op(e, lambda h: h.tensor_scalar(out=_a(out), in0=_a(in0), scalar1=_a(s1), scalar2=_a(s2),
                                                    op0=op0, **kw),
                       _Ts(in0, s1, s2), _Ts(out, accum_out))

    def stt(self, out, in0, scalar, in1, op0, op1, accum_out=None):
        kw = {}
        if accum_out is not None:
            kw["accum_out"] = _a(accum_out)
        return self.op("dve", lambda h: h.scalar_tensor_tensor(out=_a(out), in0=_a(in0), scalar=_a(scalar),
                                                               in1=_a(in1), op0=op0, op1=op1, **kw),
                       _Ts(in0, scalar, in1), _Ts(out, accum_out))

    def cp(self, e, out, in_):
        if e == "act":
            return self.act(out, in_, AF.Copy)
        return self.op(e, lambda h: h.tensor_copy(out=_a(out), in_=_a(in_)), _Ts(in_), _Ts(out))

    def red(self, out, in_, op, axis=AX.X):
        return self.op("dve", lambda h: h.tensor_reduce(out=_a(out), in_=_a(in_), axis=axis, op=op),
                       _Ts(in_), _Ts(out))

    def recip(self, out, in_):
        return self.op("dve", lambda h: h.reciprocal(out=_a(out), in_=_a(in_)), _Ts(in_), _Ts(out))

    def memset(self, e, out, val):
        return self.op(e, lambda h: h.memset(_a(out), val), [], _Ts(out))


def host_consts():
    j = np.arange(128)
    c = np.zeros((128, 10, 128), np.float32)
    c[:, 0, :] = (j[:, None] == j[None, :])
    c[:, 1, :] = (j[:, None] <= j[None, :])
    c[:, 2, :] = (j[:, None] > j[None, :])
    c[:, 3, :] = np.where(j[None, :] <= j[:, None], 0.0, -30000.0)
    c[:, 4, :] = 1.0
    c[:, 5, :] = (j[:, None] == 127)
    c[:, 6, :] = (j[:, None] < j[None, :])
    c[:, 7, :] = c[:, 1, :]
    c[:, 8, :] = c[:, 6, :]
    c[:, 9, :] = c[:, 1, :]
    return c


def blockdiag(w):
    out = np.zeros((8, 128, 128), np.float32)
    w = w.reshape(8, 32, 4, 4)
    for nl in range(32):
        out[:, nl * 4:(nl + 1) * 4, nl * 4:(nl + 1) * 4] = w[:, nl]
    return np.ascontiguousarray(out.transpose(1, 0, 2))


def host_consts16():
    h = np.arange(16)
    q = np.arange(128)
    j = np.arange(8)
    e = (h[:, None, None] == (2 * j[None, :, None] + q[None, None, :] // 64)).astype(np.float32)
    sel = np.broadcast_to((h[:, None, None] == h[None, :, None]), (16, 16, 128)).astype(np.float32)
    return np.ascontiguousarray(np.concatenate([e.reshape(16, -1), sel.reshape(16, -1)], axis=1))


class IO:
    pass


def build(cfg):
    _, fw1 = _build(cfg, None)
    nc, fw2 = _build(cfg, fw1.waited_on)
    return nc


def _build(cfg, need):
    nc = bass.Bass("TRN2", target_bir_lowering=False)
    fw = Fw(nc, need)
    io = IO()
    NCH = cfg.get("nch", 16)
    dbg = cfg.get("dbg", ())
    phases = cfg.get("phases", ("0a", "0b", "1", "2", "3"))

    def din(name, shape):
        return nc.dram_tensor(name, list(shape), F32, kind="ExternalInput").ap()

    def dout(name, shape):
        return nc.dram_tensor(name, list(shape), F32, kind="ExternalOutput").ap()

    def dscr(name, shape):
        if name in dbg:
            return dout(name, shape)
        return nc.dram_tensor(name, list(shape), F32).ap()

    io.xp = din("xp", [T, D])
    io.cst = din("cst", [128, 10, 128])
    io.w_in0 = din("w_in0", [D, IN0])
    io.w_out0 = din("w_out0", [2 * D, D])
    io.norm_mix = din("norm_mix", [2, D])
    io.norm_ffn = din("norm_ffn", [2, D])
    io.norm_final = din("norm_final", [D])
    io.ssd_norm = din("ssd_norm", [D])
    io.small0 = din("small0", [64])
    io.convp = din("convp", [128, 20, 5])
    io.mlcol = din("mlcol", [128, 8, 2])
    io.bdq = din("bdq", [128, 8, 128])
    io.bdk = din("bdk", [128, 8, 128])
    io.bdv = din("bdv", [128, 8, 128])
    io.w_gu = din("w_gu", [2, D, 2 * DFF])
    io.w_dn = din("w_dn", [2, DFF, D])
    for nm in ("rw_wr", "rw_wk", "rw_wv", "rw_wo"):
        setattr(io, nm, din(nm, [D, D]))
    io.rw_w1 = din("rw_w1", [D, 64]); io.rw_w2 = din("rw_w2", [64, D])
    io.rw_a1 = din("rw_a1", [D, 64]); io.rw_a2 = din("rw_a2", [64, D])
    io.rw_g1 = din("rw_g1", [D, 160]); io.rw_g2 = din("rw_g2", [160, D])
    io.rw_rows = din("rw_rows", [7, D])
    io.rw_mu = din("rw_mu", [128, 8, 6])
    io.wkv_p = dout("wkv_p", [128, 8, 64])
    io.shift_p = dout("shift_p", [1, D])
    io.xs = din("xs", [NB, D])
    io.c16 = din("c16", [16, 8 * 128 + 16 * 128])
    io.eye16 = din("eye16", [128, 16, 16])
    io.dtcol = din("dtcol", [16, 4])
    io.conv_s_in = din("conv_s_in", [128, 20, 3, NB])
    io.ssm_s_in = din("ssm_s_in", [NB, D, 128])
    io.mc_s_in = din("mc_s_in", [NB, 4, 256, 256])
    io.mn_s_in = din("mn_s_in", [128, 8, NB])
    io.mm_s_in = din("mm_s_in", [NB, 4])
    io.shift_s_in = din("shift_s_in", [128, 8, NB])
    io.wkv_s_in = din("wkv_s_in", [NB, 128, 8, 64])
    io.y_s = dout("y_s", [NB, D])
    io.conv_s = dout("conv_s", [128, 20, 3, NB])
    io.ssm_s = dout("ssm_s", [NB, D, 128])
    io.mc_s = dout("mc_s", [NB, 4, 256, 256])
    io.mn_s = dout("mn_s", [128, 8, NB])
    io.mm_s = dout("mm_s", [NB, 4])
    io.shift_s = dout("shift_s", [NB, D])
    io.wkv_s = dout("wkv_s", [NB, 128, 8, 64])
    io.s1s = dscr("s1s", [NB, D])
    io.s2s = dscr("s2s", [NB, D])
    io.s3s = dscr("s3s", [NB, D])
    if "dbg_a" in dbg:
        io.dbg_a = dout("dbg_a", [NB, D]); io.dbg_b = dout("dbg_b", [NB, 64])
    io.s1 = dscr("s1", [T, D])
    io.s2 = dscr("s2", [T, D])
    io.s3 = dscr("s3", [T, D])
    io.y_p = dout("y_p", [T, D])
    io.ssm_p = dout("ssm_p", [128, D])
    io.mc_p = dout("mc_p", [128, 2, 4, 264])
    io.mm_p = dout("mm_p", [1, 4])
    io.conv_p = dout("conv_p", [128, 20, 3])

    fw.presem(epochs=5)

    cst = fw.sb([128, 10, 128], F32, "cst")
    fw.dma("sp", cst[:], io.cst[:, :, :])
    ident, tri_le, mask_gt, negmask, ones = (cst[:, i, :] for i in range(5))
    sel127 = cst[:, 5, :]
    m4 = cst[:, 6:10, :].rearrange("p a t -> p (a t)")
    identb = fw.sb([128, 128], BF16, "identb")
    fw.cp("dve", identb[:], ident)
    onesb = fw.sb([128, 128], BF16, "onesb")
    fw.cp("dve", onesb[:], ones)
    nst = fw.sb([128, 8], F32, "nst")
    c16 = fw.sb([16, 8 * 128 + 16 * 128], F32, "c16")
    fw.dma("sp", c16[:], io.c16[:, :])
    exp16 = c16[:, 0:1024].rearrange("p (j q) -> p j q", j=8)
    sel16 = c16[:, 1024:3072].rearrange("p (b q) -> p b q", b=16)
    eye16 = fw.sb([128, 16, 16], F32, "eye16")
    fw.dma("sp", eye16[:], io.eye16[:, :, :])
    SAMPLE = cfg.get("sample", True)

    PA = fw.ps([128, 1024], F32, "PA")
    PB = fw.ps([128, 1024], F32, "PB")
    PC = fw.ps([128, 512], F32, "PC")
    PD = fw.ps([128, 512], F32, "PD")
    PE = fw.ps([128, 512], F32, "PE")
    PT = fw.ps([128, 1024], BF16, "PT")
    pcd = [PC, PD]
    PB0f = fw.view(PB[:, 0:512], "PB0f")
    PB1f = fw.view(PB[:, 512:1024], "PB1f")
    PT3 = PT[:, :].rearrange("p (k m) -> p k m", k=8)
    v16 = lambda r: r.rearrange("p (h q) -> p h q", h=16)

    def load_w(dst, src, kt0, kt1, q="pool", step=2):
        N = src.shape[1]
        cw = 1024 if N > 1024 else N
        if N <= 1024:
            kstep = max(1, min(step, 2048 // max(N, 1))) if N >= 512 else step
        else:
            kstep = 1
        for k in range(kt0, kt1, kstep):
            k1 = min(k + kstep, kt1)
            for c0 in range(0, N, cw):
                c1 = min(c0 + cw, N)
                fw.dma(q, dst[:, k:k1, c0:c1], src[k * 128:k1 * 128, c0:c1].rearrange("(k p) n -> p k n", p=128))

    def rmsnorm(x, g, out, M, junk):
        fw.act(junk[0:M, :], x, AF.Square, accum_out=nst[0:M, 0:1])
        fw.ts("dve", nst[0:M, 1:2], nst[0:M, 0:1], 1.0 / D, EPS, ALU.mult, ALU.add)
        fw.act(nst[0:M, 2:3], nst[0:M, 1:2], AF.Ln)
        fw.act(nst[0:M, 3:4], nst[0:M, 2:3], AF.Exp, scale=-0.5)
        fw.stt(out, x, nst[0:M, 3:4], g[0:M, :], ALU.mult, ALU.mult)

    def to_feat(src, dst, M):
        for kt in range(8):
            fw.tr(PT3[:, kt, 0:M], src[0:M, kt * 128:(kt + 1) * 128], identb[0:M, 0:M])
        fw.cp("dve", dst[:, :, 0:M], PT3[:, :, 0:M])

    def grp_rstd(src, ncol, dst, junk, M=128):
        fw.act(junk[0:M, 0:ncol], src, AF.Square, accum_out=nst[0:M, 4:5])
        fw.ts("dve", nst[0:M, 5:6], nst[0:M, 4:5], 1.0 / ncol, EPS, ALU.mult, ALU.add)
        fw.act(nst[0:M, 6:7], nst[0:M, 5:6], AF.Ln)
        fw.act(dst, nst[0:M, 6:7], AF.Exp, scale=-0.5)

    def proj_feat(W, col0, ntile, xT, M, evac):
        for gi, g0 in enumerate(range(0, ntile, 4)):
            n = min(4, ntile - g0)
            ps3 = pcd[gi % 2][:, :].rearrange("p (a m) -> p a m", a=4)
            for i in range(n):
                col = col0 + (g0 + i) * 128
                for kt in range(8):
                    fw.mm(ps3[:, i, 0:M], W[:, kt, col:col + 128], xT[:, kt, 0:M], start=kt == 0, stop=kt == 7)
            evac(g0, n, ps3[:, 0:n, 0:M])

    def conv_tiles(convin, convp, accs, ct0, n):
        for i in range(n):
            ct = ct0 + i
            fw.act(accs[i][:, :], convin[:, i, 0:128], AF.Identity, scale=convp[:, ct, 0:1], bias=convp[:, ct, 4:5])
        for j in range(1, 4):
            for i in range(n):
                ct = ct0 + i
                fw.stt(accs[i][:, :], convin[:, i, j:j + 128], convp[:, ct, j:j + 1], accs[i][:, :], ALU.mult, ALU.add)

    base_mark = fw.mark()

    if "0a" in phases:
        Wc = fw.sb([128, 8, 1536], BF16, "Wc")
        load_w(Wc, io.w_in0[:, 1024:2560], 0, 8)
        Wz = fw.sb([128, 8, 1024], BF16, "Wz")
        load_w(Wz, io.w_in0[:, 0:1024], 0, 8)
        Wdt = fw.sb([128, 8, 16], BF16, "Wdt")
        load_w(Wdt, io.w_in0[:, 3584:3600], 0, 8, step=8)
        Wo = fw.sb([128, 8, D], BF16, "Wo")
        load_w(Wo, io.w_out0[0:1024, :], 0, 8)
        gmix = fw.sb([128, D], F32, "gmix")
        fw.dma("sp", gmix[:], io.norm_mix[0, :].partition_broadcast(128))
        gssd = fw.sb([128, D], F32, "gssd")
        fw.dma("sp", gssd[:], io.ssd_norm.partition_broadcast(128))
        sm0 = fw.sb([128, 64], F32, "sm0")
        fw.dma("sp", sm0[:], io.small0.partition_broadcast(128))
        dtb_bc, D_bc = sm0[:, 0:16], sm0[:, 32:48]
        A_t = fw.sb([128, 16], F32, "A_t")
        fw.act(A_t[:], sm0[:, 16:32], AF.Exp)
        fw.ts("dve", A_t[:], A_t[:], -1.0, None, ALU.mult)
        convp = fw.sb([128, 20, 5], F32, "convp")
        fw.dma("sp", convp[:], io.convp[:, :, :])
        convin = fw.sb([128, 12, 131], F32, "convin")
        fw.memset("pool", convin[:], 0.0)
        ST = fw.sb([128, D], F32, "ST")
        fw.memset("pool", ST[:], 0.0)
        STb = fw.sb([128, D], BF16, "STb")
        fw.memset("pool", STb[:], 0.0)
        xt = fw.sb([128, D], F32, "xt")
        junk = fw.sb([128, D], F32, "junk")
        xn = fw.sb([128, D], BF16, "xn")
        xnT = fw.sb([128, 8, 128], BF16, "xnT")
        acc = fw.sb([128, 12, 128], F32, "acc")
        accv = [fw.view(acc[:, i, :], f"acc{i}") for i in range(12)]
        cact = fw.sb([128, 12, 128], BF16, "cact")
        zs = fw.sb([128, D], F32, "zs")
        xtok = fw.sb([128, D], BF16, "xtok")
        Btok = fw.sb([128, 256], BF16, "Btok")
        sm = fw.sb([128, 128], F32, "sm")
        Lh = [fw.sb([128, 4, 2, 128], BF16, f"Lh{i}") for i in range(2)]
        dsp = fw.sb([128, 32], BF16, "dsp")
        mgt_b = fw.sb([128, 128], BF16, "mgt_b")
        fw.cp("dve", mgt_b[:], mask_gt)
        tri_b = fw.sb([128, 128], BF16, "tri_b")
        fw.cp("dve", tri_b[:], tri_le)
        Eh = fw.sb([128, 4, 128], F32, "Eh")
        CBm = fw.sb([128, 2, 128], F32, "CBm")
        Wt = fw.sb([128, 16, 128], BF16, "Wt")
        t1 = fw.sb([128, D], F32, "t1")
        yn = fw.sb([128, D], BF16, "yn")
        ynT = fw.sb([128, 8, 128], BF16, "ynT")
        xw = fw.sb([128, D], BF16, "xw")
        x1 = fw.sb([128, D], F32, "x1")

        xtB = [xt, fw.sb([128, D], F32, "xt_b")]
        xnB = [xn, fw.sb([128, D], BF16, "xn_b")]
        xnTB = [xnT, fw.sb([128, 8, 128], BF16, "xnT_b")]
        junkB = fw.sb([128, D], F32, "junk_b")

        def front0(c):
            fw.dma("sp", xtB[c % 2][:], io.xp[c * 128:(c + 1) * 128, :])
            rmsnorm(xtB[c % 2][:], gmix, xnB[c % 2][:], 128, junkB)
            to_feat(xnB[c % 2], xnTB[c % 2], 128)

        front0(0)
        for c in range(NCH):
            xt, xn, xnT = xtB[c % 2], xnB[c % 2], xnTB[c % 2]
            proj_feat(Wc, 0, 12, xnT, 128, lambda g0, n, ps: fw.cp("act", convin[:, g0:g0 + n, 3:131], ps))
            for half in range(2):
                for kt in range(8):
                    fw.mm(PA[:, half * 512:(half + 1) * 512], xnT[:, kt, :], Wz[:, kt, half * 512:(half + 1) * 512],
                          start=kt == 0, stop=kt == 7)
            fw.act(zs[:], PA[:, :], AF.Silu)
            for kt in range(8):
                fw.mm(PE[:, 0:16], xnT[:, kt, :], Wdt[:, kt, :], start=kt == 0, stop=kt == 7)
            fw.tt("dve", sm[:, 0:16], PE[:, 0:16], dtb_bc, ALU.add)
            conv_tiles(convin, convp, accv, 0, 12)
            for i in range(12):
                fw.act(cact[:, i, :], accv[i][:, :], AF.Silu)
            if c + 1 < NCH:
                front0(c + 1)
            fw.cp("pool", convin[:, :, 0:3], convin[:, :, 128:131])
            for kt in range(8):
                fw.tr(PT3[:, kt, :], cact[:, kt, :], identb[:, :])
            fw.cp("dve", xtok[:], PT[:, :])
            for g in range(2):
                fw.tr(PT[:, g * 128:(g + 1) * 128], cact[:, 8 + g, :], identb[:, :])
            fw.cp("dve", Btok[:], PT[:, 0:256])
            fw.act(sm[:, 0:16], sm[:, 0:16], AF.Exp)
            fw.act(sm[:, 0:16], sm[:, 0:16], AF.Ln, bias=1.0)
            fw.tt("dve", sm[:, 16:32], sm[:, 0:16], A_t[:], ALU.mult)
            fw.mm(PE[:, 32:48], tri_le, sm[:, 16:32])
            fw.mm(PE[:, 48:64], ones, sm[:, 16:32])
            fw.act(sm[:, 32:48], PE[:, 32:48], AF.Exp)
            fw.cp("dve", sm[:, 64:80], PE[:, 32:48])
            fw.tt("dve", sm[:, 48:64], PE[:, 48:64], sm[:, 64:80], ALU.subtract)
            fw.act(sm[:, 48:64], sm[:, 48:64], AF.Exp)
            fw.tt("dve", sm[:, 48:64], sm[:, 48:64], sm[:, 0:16], ALU.mult)
            fw.act(sm[:, 80:96], PE[:, 48:64], AF.Exp)
            for g in range(2):
                fw.mm(PE[:, 128 + g * 128:256 + g * 128], cact[:, 8 + g, :], cact[:, 10 + g, :])
                fw.tt("dve", CBm[:, g, :], PE[:, 128 + g * 128:256 + g * 128], tri_le, ALU.mult)
            fw.cp("dve", dsp[:, 0:16], sm[:, 16:32])
            fw.tt("dve", dsp[:, 16:32], sm[:, 16:32], dsp[:, 0:16], ALU.subtract)
            for hq in range(4):
                L = Lh[hq % 2]
                ps3 = pcd[hq % 2][:, :].rearrange("p (a m) -> p a m", a=4)
                for i in range(4):
                    h = hq * 4 + i
                    fw.ts("dve", L[:, i, 0, :], mgt_b[:, :], dsp[:, h:h + 1], None, ALU.mult)
                    fw.ts("dve", L[:, i, 1, :], mgt_b[:, :], dsp[:, 16 + h:17 + h], None, ALU.mult)
                    fw.mm(ps3[:, i, :], L[:, i, 0, :], tri_b[:, :], start=True, stop=False)
                    fw.mm(ps3[:, i, :], L[:, i, 1, :], tri_b[:, :], start=False, stop=True)
                fw.act(Eh[:], ps3, AF.Exp)
                for i in range(4):
                    h = hq * 4 + i
                    fw.stt(Wt[:, h, :], Eh[:, i, :], sm[:, h:h + 1], CBm[:, h // 8, :], ALU.mult, ALU.mult)
            for h in range(16):
                fw.mm(PA[:, h * 64:(h + 1) * 64], Wt[:, h, :], xtok[:, h * 64:(h + 1) * 64])
            for g in range(2):
                fw.mm(PB[:, g * 512:(g + 1) * 512], cact[:, 10 + g, :], STb[:, g * 512:(g + 1) * 512])
            fw.tt("dve", v16(t1[:, :]), v16(PB[:, :]), sm[:, 32:48].bc(2, 64), ALU.mult)
            fw.tt("dve", t1[:], t1[:], PA[:, :], ALU.add)
            fw.tt("pool", v16(junk[:, :]), v16(xtok[:, :]), D_bc.bc(2, 64), ALU.mult)
            fw.tt("dve", t1[:], t1[:], junk[:], ALU.add)
            fw.tt("dve", t1[:], t1[:], zs[:], ALU.mult)
            for g in range(2):
                grp_rstd(t1[:, g * 512:(g + 1) * 512], 512, nst[:, 7:8], junk)
                fw.stt(yn[:, g * 512:(g + 1) * 512], t1[:, g * 512:(g + 1) * 512], nst[:, 7:8],
                       gssd[:, g * 512:(g + 1) * 512], ALU.mult, ALU.mult)
            to_feat(yn, ynT, 128)
            fw.tt("pool", v16(xw[:, :]), v16(xtok[:, :]), sm[:, 48:64].bc(2, 64), ALU.mult)
            for g in range(2):
                fw.mm(PB[:, g * 512:(g + 1) * 512], Btok[:, g * 128:(g + 1) * 128], xw[:, g * 512:(g + 1) * 512])
            fw.tt("dve", v16(ST[:, :]), v16(ST[:, :]), sm[:, 80:96].bc(2, 64), ALU.mult)
            fw.tt("dve", ST[:], ST[:], PB[:, :], ALU.add)
            fw.cp("act", STb[:], ST[:])
            for half in range(2):
                for kt in range(8):
                    fw.mm(PA[:, half * 512:(half + 1) * 512], ynT[:, kt, :], Wo[:, kt, half * 512:(half + 1) * 512],
                          start=kt == 0, stop=kt == 7)
            fw.tt("dve", x1[:], xt[:], PA[:, :], ALU.add)
            fw.dma("sp", io.s1[c * 128:(c + 1) * 128, :], x1[:])

        xt, xn, xnT = xtB[0], xnB[0], xnTB[0]
        if SAMPLE:
            dtcol = fw.sb([16, 4], F32, "dtcol")
            fw.dma("sp", dtcol[:], io.dtcol[:, :])
            fw.act(dtcol[:, 3:4], dtcol[:, 1:2], AF.Exp)
            fw.ts("dve", dtcol[:, 3:4], dtcol[:, 3:4], -1.0, None, ALU.mult)
            cst_s = fw.sb([128, 12, 3, NB], F32, "cst_s")
            fw.dma("sp", cst_s[:], io.conv_s_in[:, 0:12, :, :])
            uS = fw.sb([128, 12, NB], F32, "uS")
            accs = fw.sb([128, 12, NB], F32, "accs")
            tmps = fw.sb([128, 12, NB], F32, "tmps")
            cs = fw.sb([128, 12, NB], F32, "cs")
            zsT = fw.sb([128, 8, NB], F32, "zsT")
            dd = fw.sb([16, 48], F32, "dd")
            dx = fw.sb([128, 8, 48], F32, "dx")
            dtx = fw.sb([128, 8, NB], F32, "dtx")
            BCtok = fw.sb([16, 512], F32, "BCtok")
            Sb = [fw.sb([128, 8, 128], F32, f"Sb{i}") for i in range(2)]
            T1s = fw.sb([128, 8, 128], F32, "T1s")
            ysT = fw.sb([128, 8, NB], F32, "ysT")
            fw.dma("sp", xt[0:NB, :], io.xs[:, :])
            rmsnorm(xt[0:NB, :], gmix, xn[0:NB, :], NB, junk)
            to_feat(xn, xnT, NB)
            proj_feat(Wc, 0, 12, xnT, NB, lambda g0, n, ps: fw.cp("act", uS[:, g0:g0 + n, :], ps))
            proj_feat(Wz, 0, 8, xnT, NB, lambda g0, n, ps: fw.act(zsT[:, g0:g0 + n, :], ps, AF.Silu))
            wv = lambda j: convp[:, 0:12, j].bc(2, NB)
            fw.tt("dve", accs[:], cst_s[:, :, 0, :], wv(0), ALU.mult)
            fw.tt("dve", accs[:], accs[:], wv(4), ALU.add)
            for j in (1, 2):
                fw.tt("dve", tmps[:], cst_s[:, :, j, :], wv(j), ALU.mult)
                fw.tt("dve", accs[:], accs[:], tmps[:], ALU.add)
            fw.tt("dve", tmps[:], uS[:], wv(3), ALU.mult)
            fw.tt("dve", accs[:], accs[:], tmps[:], ALU.add)
            fw.act(cs[:], accs[:], AF.Silu)
            fw.dma("sp", io.conv_s[:, 0:12, 0:2, :], cst_s[:, :, 1:3, :])
            fw.dma("sp", io.conv_s[:, 0:12, 2, :], uS[:])
            for kt in range(8):
                fw.mm(PE[0:16, 0:16], Wdt[:, kt, :], xnT[:, kt, 0:NB], start=kt == 0, stop=kt == 7)
            fw.ts("dve", dd[:, 0:16], PE[0:16, 0:16], dtcol[:, 0:1], None, ALU.add)
            fw.act(dd[:, 0:16], dd[:, 0:16], AF.Exp)
            fw.act(dd[:, 0:16], dd[:, 0:16], AF.Ln, bias=1.0)
            fw.ts("dve", dd[:, 16:32], dd[:, 0:16], dtcol[:, 3:4], None, ALU.mult)
            fw.act(dd[:, 16:32], dd[:, 16:32], AF.Exp)
            fw.ts("dve", dd[:, 32:48], ones[0:16, 0:16], dtcol[:, 2:3], None, ALU.mult)
            for j in range(8):
                fw.mm(PE[:, 128 + j * 48:128 + (j + 1) * 48], exp16[:, j, :], dd[:, :])
            fw.cp("dve", dx[:], PE[:, 128:512].rearrange("p (j c) -> p j c", j=8))
            fw.tt("dve", dtx[:], dx[:, :, 0:16], cs[:, 0:8, :], ALU.mult)
            for i in range(4):
                fw.tr(PD[0:16, i * 128:(i + 1) * 128], cs[:, 8 + i, :], ident)
            fw.cp("dve", BCtok[:], PD[0:16, :])
            for b in range(NB):
                S = Sb[b % 2]
                fw.dma("sp", S[:], io.ssm_s_in[b].rearrange("(j q) n -> q j n", q=128))
                fw.mm(PC[:, :], sel16[:, b, :], BCtok[:, :])
                for g in range(2):
                    fw.tt("dve", T1s[:, 4 * g:4 * g + 4, :], PC[:, g * 128:(g + 1) * 128].bc(1, 4),
                          dtx[:, 4 * g:4 * g + 4, b].bc(2, 128), ALU.mult)
                fw.tt("pool", S[:], S[:], dx[:, :, 16 + b].bc(2, 128), ALU.mult)
                fw.tt("dve", S[:], S[:], T1s[:], ALU.add)
                fw.dma("sp", io.ssm_s[b].rearrange("(j q) n -> q j n", q=128), S[:])
                for g in range(2):
                    fw.tt("dve", T1s[:, 4 * g:4 * g + 4, :], S[:, 4 * g:4 * g + 4, :],
                          PC[:, 256 + g * 128:256 + (g + 1) * 128].bc(1, 4), ALU.mult)
                fw.red(ysT[:, :, b], T1s[:], ALU.add)
            fw.tt("dve", dtx[:], dx[:, :, 32:48], cs[:, 0:8, :], ALU.mult)
            fw.tt("dve", ysT[:], ysT[:], dtx[:], ALU.add)
            fw.tt("dve", ysT[:], ysT[:], zsT[:], ALU.mult)
            for j in range(8):
                fw.tr(PA[0:16, j * 128:(j + 1) * 128], ysT[:, j, :], ident)
            fw.cp("dve", t1[0:NB, :], PA[0:NB, :])
            for g in range(2):
                grp_rstd(t1[0:NB, g * 512:(g + 1) * 512], 512, nst[0:NB, 7:8], junk, NB)
                fw.stt(yn[0:NB, g * 512:(g + 1) * 512], t1[0:NB, g * 512:(g + 1) * 512], nst[0:NB, 7:8],
                       gssd[0:NB, g * 512:(g + 1) * 512], ALU.mult, ALU.mult)
            to_feat(yn, ynT, NB)
            for half in range(2):
                for kt in range(8):
                    fw.mm(PA[0:NB, half * 512:(half + 1) * 512], ynT[:, kt, 0:NB], Wo[:, kt, half * 512:(half + 1) * 512],
                          start=kt == 0, stop=kt == 7)
            fw.tt("dve", x1[0:NB, :], xt[0:NB, :], PA[0:NB, :], ALU.add)
            fw.dma("sp", io.s1s[:, :], x1[0:NB, :])
        fw.dma("sp", io.ssm_p[:, :], ST[:])
        fw.dma("sp", io.conv_p[:, 0:12, :], convin[:, :, 0:3])
        fw.release(base_mark)

    if "0b" in phases:
        Wx = fw.sb([128, 8, 1024], BF16, "Wx")
        load_w(Wx, io.w_in0[:, 2560:3584], 0, 8)
        Wg = fw.sb([128, 8, 1024], BF16, "Wg")
        load_w(Wg, io.w_in0[:, 3600:4624], 0, 8)
        Wif = fw.sb([128, 8, 16], BF16, "Wif")
        load_w(Wif, io.w_in0[:, 4616:4632], 0, 8, step=8)
        Wo = fw.sb([128, 8, D], BF16, "Wo")
        load_w(Wo, io.w_out0[1024:2048, :], 0, 8)
        BDq = fw.sb([128, 8, 128], BF16, "BDq")
        BDk = fw.sb([128, 8, 128], BF16, "BDk")
        BDv = fw.sb([128, 8, 128], BF16, "BDv")
        fw.dma("pool", BDq[:], io.bdq[:, :, :])
        fw.dma("pool", BDk[:], io.bdk[:, :, :])
        fw.dma("pool", BDv[:], io.bdv[:, :, :])
        gmix = fw.sb([128, D], F32, "gmix")
        fw.dma("sp", gmix[:], io.norm_mix[0, :].partition_broadcast(128))
        sm0 = fw.sb([128, 64], F32, "sm0")
        fw.dma("sp", sm0[:], io.small0.partition_broadcast(128))
        ib_bc, fb_bc = sm0[:, 48:52], sm0[:, 52:56]
        convp = fw.sb([128, 20, 5], F32, "convp")
        fw.dma("sp", convp[:], io.convp[:, :, :])
        mlcol = fw.sb([128, 8, 2], F32, "mlcol")
        fw.dma("sp", mlcol[:], io.mlcol[:, :, :])
        convin = fw.sb([128, 8, 131], F32, "convin")
        fw.memset("pool", convin[:], 0.0)
        Cst = fw.sb([128, 2, 4, 264], F32, "Cst")
        fw.memset("pool", Cst[:], 0.0)
        Cb = fw.sb([128, 2, 4, 264], BF16, "Cb")
        fw.memset("pool", Cb[:], 0.0)
        mprev = fw.sb([128, 4], F32, "mprev")
        fw.memset("pool", mprev[:], 0.0)
        xt = fw.sb([128, D], F32, "xt")
        junk = fw.sb([128, D], F32, "junk")
        xn = fw.sb([128, D], BF16, "xn")
        xnT = fw.sb([128, 8, 128], BF16, "xnT")
        acc = fw.sb([128, 8, 128], F32, "acc")
        accv = [fw.view(acc[:, i, :], f"acc{i}") for i in range(8)]
        cact = fw.sb([128, 8, 128], BF16, "cact")
        xmraw = fw.sb([128, 8, 128], BF16, "xmraw")
        sigoT = fw.sb([128, 8, 128], BF16, "sigoT")
        sm2 = fw.sb([128, 64], F32, "sm2")
        qT = fw.sb([128, 8, 128], BF16, "qT")
        kT = fw.sb([128, 8, 128], BF16, "kT")
        vtok = fw.sb([128, 4, 264], BF16, "vtok")
        fw.memset("pool", vtok[:], 1.0)
        kw_ = fw.sb([128, 4, 256], BF16, "kw")
        HT = [(fw.sb([128, 128], F32, f"Rh{i}"), fw.sb([128, 128], F32, f"dlm{i}"), fw.sb([128, 128], F32, f"Dm{i}"),
               fw.sb([128, 128], BF16, f"Sg{i}"), fw.sb([128, 128], BF16, f"SgT{i}"), fw.sb([128, 16], F32, f"hs{i}"),
               fw.sb([128, 258], F32, f"comb{i}"), fw.sb([128, 256], F32, f"hh{i}")) for i in range(2)]
        junk2 = [fw.sb([128, 256], F32, f"jk{i}") for i in range(2)]
        ktb = fw.sb([128, D], BF16, "ktb")
        mt = fw.sb([128, 16], F32, "mt")
        fw.memset("pool", mt[:], 0.0)
        fw.memset("pool", sm2[:], 0.0)
        hmn = fw.sb([128, D], BF16, "hmn")
        hmnT = fw.sb([128, 8, 128], BF16, "hmnT")
        hmfT = fw.sb([128, 8, 128], BF16, "hmfT")
        x1 = fw.sb([128, D], F32, "x1")

        lvl = cfg.get('lvl', 99)
        xnTB = [xnT, fw.sb([128, 8, 128], BF16, "xnT_b")]

        def front0(c):
            fw.dma("sp", xt[:], io.xp[c * 128:(c + 1) * 128, :])
            rmsnorm(xt[:], gmix, xn[:], 128, junk)
            to_feat(xn, xnTB[c % 2], 128)

        front0(0)
        for c in range(NCH):
            xnT = xnTB[c % 2]
            fw.dma("sp", x1[:], io.s1[c * 128:(c + 1) * 128, :])
            proj_feat(Wx, 0, 8, xnT, 128, lambda g0, n, ps: fw.cp("act", convin[:, g0:g0 + n, 3:131], ps))
            proj_feat(Wg, 0, 8, xnT, 128, lambda g0, n, ps: fw.act(sigoT[:, g0:g0 + n, :], ps, AF.Sigmoid))
            for kt in range(8):
                fw.mm(PE[:, 16:32], xnT[:, kt, :], Wif[:, kt, :], start=kt == 0, stop=kt == 7)
            fw.tt("dve", sm2[:, 0:4], PE[:, 24:28], ib_bc, ALU.add)
            fw.tt("dve", sm2[:, 4:8], PE[:, 28:32], fb_bc, ALU.add)
            conv_tiles(convin, convp, accv, 12, 8)
            for i in range(8):
                fw.act(cact[:, i, :], accv[i][:, :], AF.Silu)
            if c + 1 < NCH:
                front0(c + 1)
            fw.cp("pool", xmraw[:], convin[:, :, 3:131])
            fw.cp("pool", convin[:, :, 0:3], convin[:, :, 128:131])
            if lvl < 2:
                continue
            for tile in range(8):
                ps = pcd[tile % 2]
                fw.mm(ps[:, 0:128], (Wx[:, tile, 0:128] if cfg.get('alt') else BDq[:, tile, :]), cact[:, tile, :])
                fw.mm(ps[:, 128:256], (Wx[:, tile, 0:128] if cfg.get('alt') else BDk[:, tile, :]), cact[:, tile, :])
                if cfg.get('alt') != 2:
                    fw.cp("dve", qT[:, tile, :], ps[:, 0:128])
                if cfg.get('alt') not in (2, 3):
                    fw.ts("dve", kT[:, tile, :], ps[:, 128:256], 0.0625, None, ALU.mult)
            if lvl < 2.1:
                continue
            for tile in range(8):
                fw.mm(PA[:, tile * 128:(tile + 1) * 128], xmraw[:, tile, :], BDv[:, tile, :])
                fw.mm(PB[:, tile * 128:(tile + 1) * 128], cact[:, tile, :], BDk[:, tile, :])
            if lvl < 2.2:
                continue
            fw.cp("act", vtok[:, :, 0:256], PA[:, :].rearrange("p (h v) -> p h v", h=4))
            fw.cp("dve", ktb[:], PB[:, :])
            if lvl < 2.3:
                continue
            fw.act(sm2[:, 4:8], sm2[:, 4:8], AF.Exp, scale=-1.0)
            fw.act(sm2[:, 4:8], sm2[:, 4:8], AF.Ln, bias=1.0)
            fw.ts("dve", sm2[:, 4:8], sm2[:, 4:8], -1.0, None, ALU.mult)
            fw.mm(PE[:, 64:80], tri_le, sm2[:, 0:16])
            fw.mm(PE[:, 96:112], ones, sm2[:, 0:16])
            fw.cp("dve", sm2[:, 8:12], PE[:, 68:72])
            fw.cp("dve", sm2[:, 12:16], PE[:, 100:104])
            fw.tt("dve", sm2[:, 16:20], sm2[:, 8:12], mprev[:], ALU.add)
            if lvl < 3:
                continue
            def head_gen(h, pi):
                Rh, dlm, Dm, Sg, SgT, hs, comb, hh = HT[pi]
                Pd = pcd[pi]
                Pn = [PA, PB][pi]
                fw.ts("dve", Rh[:], mask_gt, sm2[:, 4 + h:5 + h], None, ALU.mult)
                fw.stt(Rh[:], ident, sm2[:, h:h + 1], Rh[:], ALU.mult, ALU.add)
                yield
                fw.mm(Pd[:, 0:128], tri_le, Rh[:])
                fw.mm(Pd[:, 128:256], qT[:, 2 * h, :], kT[:, 2 * h, :], start=True, stop=False)
                fw.mm(Pd[:, 128:256], qT[:, 2 * h + 1, :], kT[:, 2 * h + 1, :], start=False, stop=True)
                fw.mm(Pn[:, 512:770], qT[:, 2 * h, :], Cb[:, 0, h, 0:258], start=True, stop=False)
                fw.mm(Pn[:, 512:770], qT[:, 2 * h + 1, :], Cb[:, 1, h, 0:258], start=False, stop=True)
                yield
                fw.tt("dve", dlm[:], Pd[:, 0:128], negmask, ALU.add)
                yield
                fw.red(hs[:, 0:1], dlm[:], ALU.max)
                yield
                fw.tt("dve", mt[:, h:h + 1], hs[:, 0:1], sm2[:, 16 + h:17 + h], ALU.max)
                yield
                fw.ts("dve", hs[:, 1:2], mt[:, h:h + 1], -1.0, None, ALU.mult)
                yield
                fw.act(Dm[:], dlm[:], AF.Exp, bias=hs[:, 1:2])
                fw.act(hs[:, 2:3], sm2[:, 16 + h:17 + h], AF.Exp, bias=hs[:, 1:2])
                fw.act(hs[:, 3:4], mt[:, h:h + 1], AF.Exp, scale=-1.0)
                yield
                fw.tt("dve", Sg[:], Pd[:, 128:256], Dm[:], ALU.mult)
                yield
                fw.tr(PT[:, pi * 128:(pi + 1) * 128], Sg[:], identb[:, :])
                yield
                fw.cp("dve", SgT[:], PT[:, pi * 128:(pi + 1) * 128])
                fw.act(comb[:], Pn[:, 512:770], AF.Copy, scale=hs[:, 2:3])
                yield
                fw.mm(Pn[:, 0:258], SgT[:], vtok[:, h, 0:258])
                yield
                fw.tt("dve", comb[:], comb[:], Pn[:, 0:258], ALU.add)
                yield
                fw.ts("dve", hs[:, 6:7], comb[:, 256:257], -1.0, None, ALU.mult)
                yield
                fw.tt("dve", hs[:, 6:7], hs[:, 6:7], comb[:, 256:257], ALU.max)
                yield
                fw.tt("dve", hs[:, 4:5], hs[:, 6:7], hs[:, 3:4], ALU.max)
                yield
                fw.recip(hs[:, 5:6], hs[:, 4:5])
                yield
                fw.ts("dve", hh[:], comb[:, 0:256], hs[:, 5:6], None, ALU.mult)
                yield
                fw.act(junk2[pi][:, 0:256], hh[:], AF.Square, accum_out=hs[:, 8:9])
                yield
                fw.ts("dve", hs[:, 9:10], hs[:, 8:9], 1.0 / 256, EPS, ALU.mult, ALU.add)
                yield
                fw.act(hs[:, 10:11], hs[:, 9:10], AF.Ln)
                fw.act(hs[:, 11:12], hs[:, 10:11], AF.Exp, scale=-0.5)
                yield
                fw.ts("dve", hmn[:, h * 256:(h + 1) * 256], hh[:], hs[:, 11:12], None, ALU.mult)
                yield

            for h0 in (0, 2):
                for _ in zip(head_gen(h0, 0), head_gen(h0 + 1, 1)):
                    pass
            to_feat(hmn, hmnT, 128)
            for tile in range(8):
                fw.ts("dve", hmfT[:, tile, :], hmnT[:, tile, :], mlcol[:, tile, 0:1], None, ALU.mult)
                fw.stt(hmfT[:, tile, :], cact[:, tile, :], mlcol[:, tile, 1:2], hmfT[:, tile, :], ALU.mult, ALU.add)
            fw.tt("dve", hmfT[:], hmfT[:], sigoT[:], ALU.mult)
            if lvl < 5:
                continue
            fw.mm(PE[:, 112:128], sel127, mt[:])
            fw.cp("dve", sm2[:, 20:24], PE[:, 112:116])
            fw.tt("dve", sm2[:, 24:28], sm2[:, 12:16], sm2[:, 8:12], ALU.subtract)
            fw.tt("dve", sm2[:, 24:28], sm2[:, 24:28], sm2[:, 0:4], ALU.add)
            fw.tt("dve", sm2[:, 24:28], sm2[:, 24:28], sm2[:, 20:24], ALU.subtract)
            fw.act(sm2[:, 28:32], sm2[:, 24:28], AF.Exp)
            fw.ts("dve", sm2[:, 28:32], sm2[:, 28:32], 0.0625, None, ALU.mult)
            fw.tt("dve", sm2[:, 32:36], sm2[:, 12:16], mprev[:], ALU.add)
            fw.tt("dve", sm2[:, 32:36], sm2[:, 32:36], sm2[:, 20:24], ALU.subtract)
            fw.act(sm2[:, 32:36], sm2[:, 32:36], AF.Exp)
            fw.tt("dve", kw_[:], ktb[:, :].rearrange("p (h d) -> p h d", h=4), sm2[:, 28:32].bc(2, 256), ALU.mult)
            for kt in range(2):
                for h in range(4):
                    fw.mm(PB[:, h * 256:(h + 1) * 256], kw_[:, h, kt * 128:(kt + 1) * 128], vtok[:, h, 0:256])
                    fw.mm(PE[:, 80 + 2 * h:82 + 2 * h], kw_[:, h, kt * 128:(kt + 1) * 128], onesb[:, 0:2])
                for h in range(4):
                    fw.stt(Cst[:, kt, h, 0:256], Cst[:, kt, h, 0:256], sm2[:, 32 + h:33 + h],
                           PB[:, h * 256:(h + 1) * 256], ALU.mult, ALU.add)
                    fw.stt(Cst[:, kt, h, 256:257], Cst[:, kt, h, 256:257], sm2[:, 32 + h:33 + h],
                           PE[:, 80 + 2 * h:81 + 2 * h], ALU.mult, ALU.add)
            fw.cp("act", Cb[:], Cst[:])
            fw.cp("dve", mprev[:], sm2[:, 20:24])
            if lvl < 6:
                continue
            for half in range(2):
                for kt in range(8):
                    fw.mm(PA[:, half * 512:(half + 1) * 512], hmfT[:, kt, :], Wo[:, kt, half * 512:(half + 1) * 512],
                          start=kt == 0, stop=kt == 7)
            fw.tt("dve", x1[:], x1[:], PA[:, :], ALU.add)
            fw.dma("sp", io.s1[c * 128:(c + 1) * 128, :], x1[:])

        xnT = xnTB[0]
        if SAMPLE:
            cst_s = fw.sb([128, 8, 3, NB], F32, "cst_s")
            fw.dma("sp", cst_s[:], io.conv_s_in[:, 12:20, :, :])
            uS = fw.sb([128, 8, NB], F32, "uS")
            accs = fw.sb([128, 8, NB], F32, "accs")
            tmps = fw.sb([128, 8, NB], F32, "tmps")
            cs = fw.sb([128, 8, NB], F32, "cs")
            cs_bf = fw.sb([128, 8, NB], BF16, "cs_bf")
            us_bf = fw.sb([128, 8, NB], BF16, "us_bf")
            qTs = fw.sb([128, 8, NB], F32, "qTs")
            kTs = fw.sb([128, 8, NB], F32, "kTs")
            kws = fw.sb([128, 8, NB], F32, "kws")
            nS = fw.sb([128, 8, NB], F32, "nS")
            vtoks = fw.sb([16, D], F32, "vtoks")
            g16 = fw.sb([16, 64], F32, "g16")
            Zd = fw.sb([16, 128], F32, "Zd")
            wd = fw.sb([128, 2, 4, NB], F32, "wd")
            qmask = fw.sb([128, 8, NB, NB], F32, "qmask")
            Cs = [fw.sb([128, 8, 256], F32, f"Cs{i}") for i in range(2)]
            Tt = fw.sb([128, 8, 256], F32, "Tt")
            numt = fw.sb([16, D], F32, "numt")
            fw.dma("sp", xt[0:NB, :], io.xs[:, :])
            fw.dma("sp", x1[0:NB, :], io.s1s[:, :])
            fw.dma("sp", g16[:, 8:12], io.mm_s_in[:, :])
            fw.dma("sp", nS[:], io.mn_s_in[:, :, :])
            rmsnorm(xt[0:NB, :], gmix, xn[0:NB, :], NB, junk)
            to_feat(xn, xnT, NB)
            proj_feat(Wx, 0, 8, xnT, NB, lambda g0, n, ps: fw.cp("act", uS[:, g0:g0 + n, :], ps))
            proj_feat(Wg, 0, 8, xnT, NB, lambda g0, n, ps: fw.act(sigoT[:, g0:g0 + n, 0:NB], ps, AF.Sigmoid))
            for kt in range(8):
                fw.mm(PE[0:NB, 16:32], xnT[:, kt, 0:NB], Wif[:, kt, :], start=kt == 0, stop=kt == 7)
            fw.tt("dve", g16[:, 0:4], PE[0:NB, 24:28], ib_bc[0:NB, :], ALU.add)
            fw.tt("dve", g16[:, 4:8], PE[0:NB, 28:32], fb_bc[0:NB, :], ALU.add)
            fw.act(g16[:, 4:8], g16[:, 4:8], AF.Exp, scale=-1.0)
            fw.act(g16[:, 4:8], g16[:, 4:8], AF.Ln, bias=1.0)
            fw.ts("dve", g16[:, 4:8], g16[:, 4:8], -1.0, None, ALU.mult)
            wv = lambda j: convp[:, 12:20, j].bc(2, NB)
            fw.tt("dve", accs[:], cst_s[:, :, 0, :], wv(0), ALU.mult)
            fw.tt("dve", accs[:], accs[:], wv(4), ALU.add)
            for j in (1, 2):
                fw.tt("dve", tmps[:], cst_s[:, :, j, :], wv(j), ALU.mult)
                fw.tt("dve", accs[:], accs[:], tmps[:], ALU.add)
            fw.tt("dve", tmps[:], uS[:], wv(3), ALU.mult)
            fw.tt("dve", accs[:], accs[:], tmps[:], ALU.add)
            fw.act(cs[:], accs[:], AF.Silu)
            fw.dma("sp", io.conv_s[:, 12:20, 0:2, :], cst_s[:, :, 1:3, :])
            fw.dma("sp", io.conv_s[:, 12:20, 2, :], uS[:])
            fw.cp("dve", cs_bf[:], cs[:])
            fw.cp("dve", us_bf[:], uS[:])
            for tile in range(8):
                ps = pcd[tile % 2]
                fw.mm(ps[:, 0:NB], BDq[:, tile, :], cs_bf[:, tile, :])
                fw.mm(ps[:, 16:16 + NB], BDk[:, tile, :], cs_bf[:, tile, :])
                fw.cp("dve", qTs[:, tile, :], ps[:, 0:NB])
                fw.ts("dve", kTs[:, tile, :], ps[:, 16:16 + NB], 0.0625, None, ALU.mult)
            for tile in range(8):
                fw.mm(PA[0:NB, tile * 128:(tile + 1) * 128], us_bf[:, tile, :], BDv[:, tile, :])
            fw.cp("act", vtoks[:], PA[0:NB, :])
            fw.tt("dve", g16[:, 16:20], g16[:, 4:8], g16[:, 8:12], ALU.add)
            fw.tt("dve", g16[:, 12:16], g16[:, 16:20], g16[:, 0:4], ALU.max)
            fw.dma("sp", io.mm_s[:, :], g16[:, 12:16])
            fw.tt("dve", g16[:, 20:24], g16[:, 0:4], g16[:, 12:16], ALU.subtract)
            fw.act(g16[:, 20:24], g16[:, 20:24], AF.Exp)
            fw.tt("dve", g16[:, 24:28], g16[:, 16:20], g16[:, 12:16], ALU.subtract)
            fw.act(g16[:, 24:28], g16[:, 24:28], AF.Exp)
            fw.act(g16[:, 28:32], g16[:, 12:16], AF.Exp, scale=-1.0)
            z3 = lambda r: r.rearrange("p (h b) -> p h b", h=4)
            fw.tt("dve", z3(Zd[:, 0:64]), g16[:, 20:24].bc(2, NB), ident[0:NB, 0:NB].bc(1, 4), ALU.mult)
            fw.tt("dve", z3(Zd[:, 64:128]), g16[:, 24:28].bc(2, NB), ident[0:NB, 0:NB].bc(1, 4), ALU.mult)
            fw.mm(PE[:, 128:256], ones[0:NB, :], Zd[:, :])
            fw.cp("dve", wd[:], PE[:, 128:256].rearrange("p (w h b) -> p w h b", w=2, h=4))
            k4 = lambda r: r.rearrange("p (h k) b -> p h k b", h=4)
            fw.tt("dve", k4(kws[:, :, :]), k4(kTs[:, :, :]), wd[:, 0, :, :].bc(2, 2), ALU.mult)
            fw.tt("dve", k4(nS[:, :, :]), k4(nS[:, :, :]), wd[:, 1, :, :].bc(2, 2), ALU.mult)
            fw.tt("dve", nS[:], nS[:], kws[:], ALU.add)
            fw.dma("sp", io.mn_s[:, :, :], nS[:])
            fw.tt("dve", tmps[:], qTs[:], nS[:], ALU.mult)
            for h in range(4):
                for kt in range(2):
                    fw.mm(PE[0:NB, 256 + 2 * h:258 + 2 * h], tmps[:, 2 * h + kt, :], ones[:, 0:2], start=kt == 0, stop=kt == 1)
            fw.tt("dve", qmask[:], qTs[:, :, :].bc(2, NB), eye16[:, :, :].bc(1, 8), ALU.mult)
            for b in range(NB):
                Cc = Cs[b % 2]
                fw.dma("sp", Cc[:], io.mc_s_in[b].rearrange("h (k p) v -> p (h k) v", p=128))
                fw.mm(PA[:, 0:512], sel16[:, b, :], vtoks[:, 0:512])
                fw.mm(PA[:, 512:1024], sel16[:, b, :], vtoks[:, 512:1024])
                fw.tt("dve", Tt[:, :, :].rearrange("p (h k) v -> p h k v", h=4),
                      PA[:, :].rearrange("p (h v) -> p h v", h=4).bc(2, 2),
                      kws[:, :, b].rearrange("p (h k) -> p h k", h=4).bc(3, 256), ALU.mult)
                fw.tt("pool", Cc[:, :, :].rearrange("p (h k) v -> p h (k v)", h=4),
                      Cc[:, :, :].rearrange("p (h k) v -> p h (k v)", h=4), wd[:, 1, :, b].bc(2, 512), ALU.mult)
                fw.tt("dve", Cc[:], Cc[:], Tt[:], ALU.add)
                fw.dma("sp", io.mc_s[b].rearrange("h (k p) v -> p (h k) v", p=128), Cc[:])
                for tile in range(8):
                    h, kt = tile // 2, tile % 2
                    fw.mm(PB[0:NB, h * 256:(h + 1) * 256], qmask[:, tile, b, :], Cc[:, tile, :],
                          start=(b == 0 and tile in (0, 4)), stop=(b == NB - 1 and kt == 1), skip=True)
            fw.cp("act", numt[:], PB[0:NB, :])
            if "dbg_a" in dbg:
                fw.dma("sp", io.dbg_a[:, :], numt[:])
                fw.cp("dve", g16[:, 40:44], PE[0:NB, 256:264].rearrange("p (h t) -> p h t", t=2)[:, :, 0])
                fw.dma("sp", io.dbg_b[:, :], g16[:])
            dn = PE[0:NB, 256:264].rearrange("p (h t) -> p h t", t=2)[:, :, 0]
            fw.ts("dve", g16[:, 32:36], dn, -1.0, None, ALU.mult)
            fw.tt("dve", g16[:, 32:36], g16[:, 32:36], dn, ALU.max)
            fw.tt("dve", g16[:, 32:36], g16[:, 32:36], g16[:, 28:32], ALU.max)
            fw.recip(g16[:, 36:40], g16[:, 32:36])
            fw.tt("dve", numt[:, :].rearrange("p (h v) -> p h v", h=4), numt[:, :].rearrange("p (h v) -> p h v", h=4),
                  g16[:, 36:40].bc(2, 256), ALU.mult)
            for h in range(4):
                grp_rstd(numt[:, h * 256:(h + 1) * 256], 256, nst[0:NB, 7:8], junk, NB)
                fw.ts("dve", hmn[0:NB, h * 256:(h + 1) * 256], numt[:, h * 256:(h + 1) * 256], nst[0:NB, 7:8], None, ALU.mult)
            to_feat(hmn, hmnT, NB)
            for tile in range(8):
                fw.ts("dve", hmfT[:, tile, 0:NB], hmnT[:, tile, 0:NB], mlcol[:, tile, 0:1], None, ALU.mult)
                fw.stt(hmfT[:, tile, 0:NB], cs_bf[:, tile, :], mlcol[:, tile, 1:2], hmfT[:, tile, 0:NB], ALU.mult, ALU.add)
            fw.tt("dve", hmfT[:, :, 0:NB], hmfT[:, :, 0:NB], sigoT[:, :, 0:NB], ALU.mult)
            for half in range(2):
                for kt in range(8):
                    fw.mm(PA[0:NB, half * 512:(half + 1) * 512], hmfT[:, kt, 0:NB], Wo[:, kt, half * 512:(half + 1) * 512],
                          start=kt == 0, stop=kt == 7)
            fw.tt("dve", x1[0:NB, :], x1[0:NB, :], PA[0:NB, :], ALU.add)
            fw.dma("sp", io.s1s[:, :], x1[0:NB, :])
        fw.dma("sp", io.mc_p[:, :, :, :], Cst[:])
        fw.dma("sp", io.mm_p[:, :], mprev[0:1, :])
        fw.dma("sp", io.conv_p[:, 12:20, :], convin[:, :, 0:3])
        fw.release(base_mark)

    def ffn_phase(layer, src, dst, ssrc, sdst, final):
        Wgu = fw.sb([128, 8, 2 * DFF], BF16, "Wgu")
        load_w(Wgu, io.w_gu[layer], 0, 8, step=1)
        Wd = fw.sb([128, 22, D], BF16, "Wd")
        load_w(Wd, io.w_dn[layer], 0, 22)
        gf = fw.sb([128, D], F32, "gf")
        fw.dma("sp", gf[:], io.norm_ffn[layer, :].partition_broadcast(128))
        if final:
            gfin = fw.sb([128, D], F32, "gfin")
            fw.dma("sp", gfin[:], io.norm_final.partition_broadcast(128))
        GB = 4
        xt = fw.sb([128, D], F32, "xt")
        junk = fw.sb([128, D], F32, "junk")
        xn = fw.sb([128, D], BF16, "xn")
        xnT = fw.sb([128, 8, GB * 128], BF16, "xnT")
        hT = fw.sb([128, 22, GB * 128], BF16, "hT")
        sg = [fw.sb([128, GB * 128], F32, f"sg{i}") for i in range(2)]
        x2 = fw.sb([128, D], F32, "x2")
        yo = junk
        groups = [list(range(g, min(g + GB, NCH))) for g in range(0, NCH, GB)]
        if SAMPLE:
            groups.append([NCH])
        def ffn_front(grp):
            samp = grp[0] == NCH
            M = NB if samp else 128
            rows = lambda ap, c: (ap[:, :] if samp else ap[c * 128:(c + 1) * 128, :])
            for gi, c in enumerate(grp):
                fw.dma("sp", xt[0:M, :], rows(ssrc if samp else src, c))
                rmsnorm(xt[0:M, :], gf, xn[0:M, :], M, junk)
                for kt in range(8):
                    fw.tr(PT3[:, kt, 0:M], xn[0:M, kt * 128:(kt + 1) * 128], identb[0:M, 0:M])
                fw.cp("dve", xnT[:, :, gi * M:(gi + 1) * M], PT3[:, :, 0:M])

        ffn_front(groups[0])
        for gidx, grp in enumerate(groups):
            samp = grp[0] == NCH
            M = NB if samp else 128
            W = M * len(grp)
            rows = lambda ap, c: (ap[:, :] if samp else ap[c * 128:(c + 1) * 128, :])
            for j in range(22):
                psg = pcd[j % 2]
                for kt in range(8):
                    fw.mm(psg[:, 0:W], Wgu[:, kt, j * 128:(j + 1) * 128], xnT[:, kt, 0:W], start=kt == 0, stop=kt == 7)
                psu = PB0f if j % 2 == 0 else PB1f
                for kt in range(8):
                    fw.mm(psu[:, 0:W], Wgu[:, kt, DFF + j * 128:DFF + (j + 1) * 128], xnT[:, kt, 0:W],
                          start=kt == 0, stop=kt == 7)
                fw.act(sg[j % 2][:, 0:W], psg[:, 0:W], AF.Silu)
                fw.tt("dve", hT[:, j, 0:W], sg[j % 2][:, 0:W], psu[:, 0:W], ALU.mult)
            if gidx + 1 < len(groups):
                ffn_front(groups[gidx + 1])
            for gi, c in enumerate(grp):
                for half in range(2):
                    for j in range(22):
                        fw.mm(PA[0:M, half * 512:(half + 1) * 512], hT[:, j, gi * M:(gi + 1) * M],
                              Wd[:, j, half * 512:(half + 1) * 512], start=j == 0, stop=j == 21)
                fw.dma("sp", x2[0:M, :], rows(ssrc if samp else src, c))
                fw.tt("dve", x2[0:M, :], x2[0:M, :], PA[0:M, :], ALU.add)
                if final:
                    rmsnorm(x2[0:M, :], gfin, yo[0:M, :], M, junk)
                    fw.dma("sp", rows(sdst if samp else dst, c), yo[0:M, :])
                else:
                    fw.dma("sp", rows(sdst if samp else dst, c), x2[0:M, :])
        fw.release(base_mark)

    if "1" in phases:
        ffn_phase(0, io.s1, io.s2, io.s1s, io.s2s, False)

    if "2" in phases:
        Wr = fw.sb([128, 8, D], BF16, "Wr"); load_w(Wr, io.rw_wr, 0, 8)
        Wk = fw.sb([128, 8, D], BF16, "Wk"); load_w(Wk, io.rw_wk, 0, 8)
        A1 = fw.sb([128, 8, 64], BF16, "A1"); load_w(A1, io.rw_a1, 0, 8, step=8)
        A2 = fw.sb([128, D], BF16, "A2"); fw.dma("pool", A2[0:64, :], io.rw_a2[:, :])
        Wv = fw.sb([128, 8, D], BF16, "Wv"); load_w(Wv, io.rw_wv, 0, 8)
        G1 = fw.sb([128, 8, 160], BF16, "G1"); load_w(G1, io.rw_g1, 0, 8, step=8)
        G2a = fw.sb([128, D], BF16, "G2a"); fw.dma("pool", G2a[:], io.rw_g2[0:128, :])
        G2b = fw.sb([128, D], BF16, "G2b"); fw.dma("pool", G2b[0:32, :], io.rw_g2[128:160, :])
        W1 = fw.sb([128, 8, 64], BF16, "W1"); load_w(W1, io.rw_w1, 0, 8, step=8)
        W2 = fw.sb([128, D], BF16, "W2"); fw.dma("pool", W2[0:64, :], io.rw_w2[:, :])
        Wo = fw.sb([128, 8, D], BF16, "Wo"); load_w(Wo, io.rw_wo, 0, 8)
        gm1 = fw.sb([128, D], F32, "gm1")
        fw.dma("sp", gm1[:], io.norm_mix[1, :].partition_broadcast(128))
        rows = []
        for i in range(7):
            rt = fw.sb([128, D], F32, f"row{i}")
            fw.dma("sp", rt[:], io.rw_rows[i, :].partition_broadcast(128))
            rows.append(rt)
        w0b, a0b, kkb_, kab, rkb, lnw, lnb = rows
        mu = fw.sb([128, 8, 6], F32, "mu")
        fw.dma("sp", mu[:], io.rw_mu[:, :, :])
        PB0 = fw.view(PB[:, 0:512], "PB0")
        PB1 = fw.view(PB[:, 512:1024], "PB1")
        NPS = [PB0, PB1, PC, PD]
        PAh = [PA[:, 0:512], PA[:, 512:1024]]
        PBh = [PB0[:, :], PB1[:, :]]
        h1 = fw.sb([128, 2, 128], BF16, "h1")

        def proj_tok(xT, W, Ph, M=128):
            for half in range(2):
                for kt in range(8):
                    fw.mm(Ph[half][0:M, :], xT[:, kt, 0:M], W[:, kt, half * 512:(half + 1) * 512],
                          start=kt == 0, stop=kt == 7)

        def lora(xT, Wa, nh, Wb_list, func, P, M=128):
            widths = [min(128, nh), nh - 128] if nh > 128 else [nh]
            for wi, wd in enumerate(widths):
                for kt in range(8):
                    fw.mm(PE[0:wd, wi * 128:wi * 128 + M], Wa[:, kt, wi * 128:wi * 128 + wd], xT[:, kt, 0:M],
                          start=kt == 0, stop=kt == 7)
                fw.act(h1[0:wd, wi, 0:M], PE[0:wd, wi * 128:wi * 128 + M], func)
            for half in range(2):
                for wi, wd in enumerate(widths):
                    fw.mm(P[half][0:M, :], h1[0:wd, wi, 0:M], Wb_list[wi][0:wd, half * 512:(half + 1) * 512],
                          start=wi == 0, stop=wi == len(widths) - 1)

        def rstd16(src16, dst16, mult_, eps, floor=None):
            if floor is not None:
                fw.ts("dve", dst16, src16, floor, None, ALU.max)
            else:
                fw.ts("dve", dst16, src16, mult_, eps, ALU.mult, ALU.add)
            fw.act(dst16, dst16, AF.Ln)
            fw.act(dst16, dst16, AF.Exp, scale=-0.5)

        mark2 = fw.mark()
        xt = fw.sb([128, D], F32, "xt")
        junk = fw.sb([128, D], F32, "junk")
        tmpA = fw.sb([128, D], F32, "tmpA")
        tmpB = fw.sb([128, D], F32, "tmpB")
        Et = fw.sb([128, D], F32, "Et")
        SB = [fw.sb([128, D], BF16, f"S{i}") for i in range(13)]
        xn = SB[0]; r_bf = SB[1]; kkn = SB[2]; kf_bf = SB[3]; b_bf = SB[4]; v_bf = SB[5]; bv = SB[6]
        g_bf = SB[7]; abar = SB[8]; bbar = SB[9]; kbar = SB[10]; btil = SB[11]; ktil = SB[12]
        rbar = SB[0]; yo = SB[8]
        xnTe = fw.sb([128, 8, 130], BF16, "xnTe")
        fw.memset("pool", xnTe[:], 0.0)
        xx = fw.sb([128, 8, 128], BF16, "xx")
        mixb = [fw.sb([128, 8, 128], BF16, f"mix{i}") for i in range(2)]
        arT = fw.sb([128, 8, 2, 128], BF16, "arT")
        bT = fw.sb([128, 8, 128], BF16, "bT")
        kT = fw.sb([128, 8, 128], BF16, "kT")
        yoT = fw.sb([128, 8, 128], BF16, "yoT")
        Ms = [fw.sb([128, 512], BF16, f"Ms{i}") for i in range(4)]
        Q0 = [fw.sb([128, 128], BF16, f"Q0{i}") for i in range(4)]
        PQ = [[fw.sb([128, 384], BF16, f"PQ{i}{k}") for k in range(2)] for i in range(4)]
        RHSb = [fw.sb([128, 64], BF16, f"RHS{i}") for i in range(4)]
        Ubp = [fw.sb([128, 2, 64], BF16, f"Ubp{i}") for i in range(2)]
        Hst = fw.sb([128, 8, 64], F32, "Hst")
        fw.memset("pool", Hst[:], 0.0)
        Hb = fw.sb([128, 8, 64], BF16, "Hb")
        fw.memset("pool", Hb[:], 0.0)
        eLT = fw.sb([128, 8], F32, "eLT")
        s16 = fw.sb([128, 64], F32, "s16")
        x3 = tmpB
        mcount = [0]

        def mix(cidx):
            dst = mixb[mcount[0] % 2]
            mcount[0] += 1
            for kt in range(8):
                fw.stt(dst[:, kt, :], xx[:, kt, :], mu[:, kt, cidx:cidx + 1], xnTe[:, kt, 1:129], ALU.mult, ALU.add)
            return dst

        for c in range(NCH):
            fw.dma("sp", xt[:], io.s2[c * 128:(c + 1) * 128, :])
            if c == NCH - 1:
                fw.act(junk[:, :], xt[:], AF.Square, accum_out=nst[:, 0:1])
                fw.ts("dve", nst[:, 1:2], nst[:, 0:1], 1.0 / D, EPS, ALU.mult, ALU.add)
                fw.act(nst[:, 2:3], nst[:, 1:2], AF.Ln)
                fw.act(nst[:, 3:4], nst[:, 2:3], AF.Exp, scale=-0.5)
                fw.stt(tmpA[:], xt[:], nst[:, 3:4], gm1[:], ALU.mult, ALU.mult)
                fw.dma("sp", io.shift_p[:, :], tmpA[127:128, :])
                fw.cp("dve", xn[:], tmpA[:])
            else:
                rmsnorm(xt[:], gm1, xn[:], 128, junk)
            for kt in range(8):
                fw.tr(PT3[:, kt, :], xn[:, kt * 128:(kt + 1) * 128], identb[:, :])
            fw.cp("dve", xnTe[:, :, 1:129], PT3)
            fw.tt("pool", xx[:], xnTe[:, :, 0:128], xnTe[:, :, 1:129], ALU.subtract)
            PCDh = [PC[:, :], PD[:, :]]
            hv2 = lambda r, i: r[:, i * 512:(i + 1) * 512]
            proj_tok(mix(0), Wr, PAh)
            proj_tok(mix(2), Wk, PBh)
            fw.cp("act", r_bf[:], PA[:, :])
            lora(mix(4), A1, 64, [A2], AF.Copy, PCDh)
            for i in range(2):
                fw.tt("dve", hv2(tmpA, i), PBh[i], hv2(kkb_, i), ALU.mult)
            fw.tt("pool", junk[:], tmpA[:], tmpA[:], ALU.mult)
            fw.red(s16[:, 0:16], v16(junk[:, :]), ALU.add)
            rstd16(s16[:, 0:16], s16[:, 16:32], None, None, floor=1e-24)
            fw.tt("dve", v16(kkn[:, :]), v16(tmpA[:, :]), s16[:, 16:32].bc(2, 64), ALU.mult)
            proj_tok(mix(3), Wv, PAh)
            for i in range(2):
                fw.tt("dve", hv2(tmpB, i), PCDh[i], hv2(a0b, i), ALU.add)
            fw.act(tmpB[:], tmpB[:], AF.Sigmoid)
            fw.stt(junk[:], tmpB[:], 1.0, kab[:], ALU.subtract, ALU.mult)
            fw.ts("dve", junk[:], junk[:], 1.0, None, ALU.add)
            for i in range(2):
                fw.tt("dve", hv2(kf_bf, i), PBh[i], hv2(junk, i), ALU.mult)
            fw.tt("pool", b_bf[:], kkn[:], tmpB[:], ALU.mult)
            lora(mix(5), G1, 160, [G2a, G2b], AF.Sigmoid, PBh)
            fw.tt("pool", junk[:], r_bf[:], kf_bf[:], ALU.mult)
            fw.tt("pool", junk[:], junk[:], rkb[:], ALU.mult)
            fw.red(s16[:, 32:48], v16(junk[:, :]), ALU.add)
            fw.cp("act", v_bf[:], PA[:, :])
            fw.tt("dve", v16(bv[:, :]), v16(PA[:, :]), s16[:, 32:48].bc(2, 64), ALU.mult)
            lora(mix(1), W1, 64, [W2], AF.Tanh, PCDh)
            for i in range(2):
                fw.cp("act", g_bf[:, i * 512:(i + 1) * 512], PBh[i])
            for i in range(2):
                fw.tt("dve", hv2(tmpA, i), PCDh[i], hv2(w0b, i), ALU.add)
            fw.act(tmpA[:], tmpA[:], AF.Exp, scale=-1.0)
            fw.act(tmpA[:], tmpA[:], AF.Ln, bias=1.0)
            fw.ts("dve", tmpA[:], tmpA[:], -1.0, -0.5, ALU.mult, ALU.add)
            fw.act(Et[:], tmpA[:], AF.Exp)
            fw.mm(PB0[:, :], tri_le, Et[:, 0:512])
            fw.mm(PB1[:, :], tri_le, Et[:, 512:1024])
            fw.mm(PA[:, 0:512], ones, Et[:, 0:512])
            fw.mm(PA[:, 512:1024], ones, Et[:, 512:1024])
            for kt in range(8):
                fw.mm(PE[:, kt * 16:(kt + 1) * 16], Et[:, kt * 128:(kt + 1) * 128], ones[:, 0:16])
            fw.act(eLT[:], PE[:, 0:128].rearrange("p (k s) -> p k s", s=16)[:, :, 0], AF.Exp, scale=-1.0)
            hv = lambda r, i: r[:, i * 512:(i + 1) * 512]
            for i, PBi in enumerate((PB0, PB1)):
                fw.act(hv(tmpA, i), PBi[:, :], AF.Exp, scale=-1.0)
                fw.tt("pool", hv(rbar, i), hv(r_bf, i), hv(tmpA, i), ALU.mult)
                fw.tt("dve", hv(tmpB, i), hv(Et, i), PBi[:, :], ALU.subtract)
                fw.act(hv(tmpB, i), hv(tmpB, i), AF.Exp)
                fw.stt(hv(abar, i), hv(kkn, i), -1.0, hv(tmpB, i), ALU.mult, ALU.mult)
            fw.cp("act", junk[:], PA[:, :])
            for i, PBi in enumerate((PB0, PB1)):
                fw.act(hv(tmpA, i), PBi[:, :], AF.Exp)
                fw.tt("pool", hv(bbar, i), hv(b_bf, i), hv(tmpA, i), ALU.mult)
                fw.tt("pool", hv(kbar, i), hv(kf_bf, i), hv(tmpA, i), ALU.mult)
                fw.tt("dve", hv(tmpB, i), PBi[:, :], hv(junk, i), ALU.subtract)
                fw.act(hv(tmpB, i), hv(tmpB, i), AF.Exp)
                fw.tt("pool", hv(btil, i), hv(b_bf, i), hv(tmpB, i), ALU.mult)
                fw.tt("dve", hv(ktil, i), hv(kf_bf, i), hv(tmpB, i), ALU.mult)
            for src, dst in ((abar, arT[:, :, 0, :]), (rbar, arT[:, :, 1, :]), (bbar, bT[:, :, :]), (kbar, kT[:, :, :])):
                for kt in range(8):
                    fw.tr(PT3[:, kt, :], src[:, kt * 128:(kt + 1) * 128], identb[:, :])
                fw.cp("dve", dst, PT3)
            for h0 in range(0, 16, 4):
                hd = []
                for i in range(4):
                    h = h0 + i
                    j, e = h // 2, h % 2
                    p0 = 64 * e
                    hd.append(dict(h=h, j=j, e=e, p0=p0, NP=NPS[i],
                                   aT=arT[p0:p0 + 64, j, 0, :], rT=arT[p0:p0 + 64, j, 1, :],
                                   ar=arT[p0:p0 + 64, j, :, :].rearrange("p a t -> p (a t)"),
                                   bT=bT[p0:p0 + 64, j, :], kT=kT[p0:p0 + 64, j, :]))
                for i, d in enumerate(hd):
                    fw.mm(PE[:, 0:256], d["bT"], d["ar"])
                    fw.mm(PE[:, 256:512], d["kT"], d["ar"])
                    fw.tt("dve", Ms[i][:], PE[:, :], m4, ALU.mult)
                    fw.mm(d["NP"][:, 0:128], d["aT"], d["bT"])
                    fw.tt("dve", Q0[i][:], d["NP"][:, 0:128], mask_gt, ALU.mult)
                    d["P"], d["Q"], d["Z"] = Ms[i][:, 0:128], Q0[i][:], identb[:, :]
                for k in range(7):
                    for i, d in enumerate(hd):
                        NP = d["NP"]
                        if k < 6:
                            fw.mm(NP[:, 0:128], d["Q"], d["P"])
                            fw.mm(NP[:, 128:256], d["P"], d["Q"])
                        fw.mm(NP[:, 256:384], identb[:, :], d["Z"], start=True, stop=False)
                        fw.mm(NP[:, 256:384], d["Q"], d["Z"], start=False, stop=True)
                    for i, d in enumerate(hd):
                        NP = d["NP"]
                        pq = PQ[i][k % 2]
                        lo = 0 if k < 6 else 256
                        fw.cp("act" if i % 2 == 0 else "dve", pq[:, lo:384], NP[:, lo:384])
                        d["P"], d["Q"], d["Z"] = pq[:, 0:128], pq[:, 128:256], pq[:, 256:384]
                for i, d in enumerate(hd):
                    NP, h, j, p0 = d["NP"], d["h"], d["j"], d["p0"]
                    fw.mm(NP[:, 384:448], d["aT"], Hb[p0:p0 + 64, j, :], start=True, stop=False)
                    fw.mm(NP[:, 384:448], Ms[i][:, 256:384], v_bf[:, h * 64:(h + 1) * 64], start=False, stop=True)
                    fw.cp("act", RHSb[i][:], NP[:, 384:448])
                for i, d in enumerate(hd):
                    NP, h, j, e = d["NP"], d["h"], d["j"], d["e"]
                    fw.mm(NP[:, 448:512], d["Z"], RHSb[i][:])
                    fw.cp("dve", Ubp[j % 2][:, e, :], NP[:, 448:512])
                for i, d in enumerate(hd):
                    h, j, e, p0 = d["h"], d["j"], d["e"], d["p0"]
                    ysl = PA[:, h * 64:(h + 1) * 64]
                    fw.mm(ysl, d["rT"], Hb[p0:p0 + 64, j, :], start=True, stop=False)
                    fw.mm(ysl, Ms[i][:, 128:256], Ubp[j % 2][:, e, :], start=False, stop=False)
                    fw.mm(ysl, Ms[i][:, 384:512], v_bf[:, h * 64:(h + 1) * 64], start=False, stop=True)
                for jj in range(2):
                    j = h0 // 2 + jj
                    NP = hd[2 * jj]["NP"]
                    fw.mm(NP[:, 0:128], btil[:, j * 128:(j + 1) * 128], Ubp[j % 2][:, :, :].rearrange("p e v -> p (e v)"),
                          start=True, stop=False)
                    fw.mm(NP[:, 0:128], ktil[:, j * 128:(j + 1) * 128], v_bf[:, j * 128:(j + 1) * 128],
                          start=False, stop=True)
                    for e in range(2):
                        p0 = 64 * e
                        fw.stt(Hst[p0:p0 + 64, j, :], Hst[p0:p0 + 64, j, :], eLT[p0:p0 + 64, j:j + 1],
                               NP[p0:p0 + 64, p0:p0 + 64], ALU.mult, ALU.add)
                    fw.cp("pool", Hb[:, j, :], Hst[:, j, :])
            fw.cp("act", tmpA[:], PA[:, :])
            fw.red(s16[:, 0:16], v16(tmpA[:, :]), ALU.add)
            fw.ts("dve", s16[:, 0:16], s16[:, 0:16], 1.0 / 64, None, ALU.mult)
            fw.tt("dve", v16(tmpA[:, :]), v16(tmpA[:, :]), s16[:, 0:16].bc(2, 64), ALU.subtract)
            fw.tt("pool", junk[:], tmpA[:], tmpA[:], ALU.mult)
            fw.red(s16[:, 16:32], v16(junk[:, :]), ALU.add)
            rstd16(s16[:, 16:32], s16[:, 48:64], 1.0 / 64, 64e-5)
            fw.tt("dve", v16(tmpA[:, :]), v16(tmpA[:, :]), s16[:, 48:64].bc(2, 64), ALU.mult)
            fw.tt("dve", tmpA[:], tmpA[:], lnw[:], ALU.mult)
            fw.tt("dve", tmpA[:], tmpA[:], lnb[:], ALU.add)
            fw.tt("dve", tmpA[:], tmpA[:], bv[:], ALU.add)
            fw.tt("dve", yo[:], tmpA[:], g_bf[:], ALU.mult)
            to_feat(yo, yoT, 128)
            for half in range(2):
                for kt in range(8):
                    fw.mm(PA[:, half * 512:(half + 1) * 512], yoT[:, kt, :], Wo[:, kt, half * 512:(half + 1) * 512],
                          start=kt == 0, stop=kt == 7)
            fw.tt("dve", x3[:], xt[:], PA[:, :], ALU.add)
            fw.dma("sp", io.s3[c * 128:(c + 1) * 128, :], x3[:])
            fw.cp("pool", xnTe[:, :, 0:1], xnTe[:, :, 128:129])
        fw.dma("sp", io.wkv_p[:, :, :], Hst[:])
        fw.release(mark2)
        if SAMPLE:
            M = NB
            f16 = lambda nm, dt=F32: fw.sb([NB, D], dt, nm)
            xts = f16("xts"); jk = f16("jk"); tA = f16("tA"); tB = f16("tB"); Es = f16("Es")
            r_s = f16("r_s", BF16); kk_s = f16("kk_s", BF16); kf_s = f16("kf_s", BF16); b_s = f16("b_s", BF16)
            v_s = f16("v_s"); bv_s = f16("bv_s", BF16); g_s = f16("g_s", BF16); xn_s = f16("xn_s", BF16)
            sa_tok = f16("sa_tok"); yo_s = f16("yo_s", BF16)
            xprev = fw.sb([128, 8, NB], F32, "xprev")
            xsT = fw.sb([128, 8, NB], BF16, "xsT")
            xxs = fw.sb([128, 8, NB], BF16, "xxs")
            mixs = [fw.sb([128, 8, NB], BF16, f"mixs{i}") for i in range(2)]
            featT = {nm: fw.sb([128, 8, NB], F32, nm) for nm in ("aT", "wT", "bTs", "kTs", "rTs")}
            amask = fw.sb([128, 8, NB, NB], F32, "amask")
            rmask = fw.sb([128, 8, NB, NB], F32, "rmask")
            Hs = [fw.sb([128, 8, 64], F32, f"Hs{i}") for i in range(2)]
            Tt = fw.sb([128, 8, 64], F32, "Tt")
            yoTs = fw.sb([128, 8, NB], BF16, "yoTs")
            s16 = fw.sb([NB, 64], F32, "s16s")
            v16s = lambda r: r.rearrange("p (h q) -> p h q", h=16)
            mc2 = [0]

            def mix_s(cidx):
                dst = mixs[mc2[0] % 2]
                mc2[0] += 1
                for kt in range(8):
                    fw.stt(dst[:, kt, :], xxs[:, kt, :], mu[:, kt, cidx:cidx + 1], xsT[:, kt, :], ALU.mult, ALU.add)
                return dst

            fw.dma("sp", xts[:], io.s2s[:, :])
            fw.dma("sp", xprev[:], io.shift_s_in[:, :, :])
            fw.act(jk[:], xts[:], AF.Square, accum_out=nst[0:M, 0:1])
            fw.ts("dve", nst[0:M, 1:2], nst[0:M, 0:1], 1.0 / D, EPS, ALU.mult, ALU.add)
            fw.act(nst[0:M, 2:3], nst[0:M, 1:2], AF.Ln)
            fw.act(nst[0:M, 3:4], nst[0:M, 2:3], AF.Exp, scale=-0.5)
            fw.stt(tA[:], xts[:], nst[0:M, 3:4], gm1[0:M, :], ALU.mult, ALU.mult)
            fw.dma("sp", io.shift_s[:, :], tA[:])
            fw.cp("dve", xn_s[:], tA[:])
            for kt in range(8):
                fw.tr(PT3[:, kt, 0:M], xn_s[:, kt * 128:(kt + 1) * 128], identb[0:M, 0:M])
            fw.cp("dve", xsT[:], PT3[:, :, 0:M])
            fw.tt("dve", xxs[:], xprev[:], xsT[:], ALU.subtract)
            PAm = [PA[0:M, 0:512], PA[0:M, 512:1024]]
            PBm = [PB0[0:M, :], PB1[0:M, :]]
            PAf = PA[0:M, :]
            proj_tok(mix_s(0), Wr, PAh, M)
            fw.cp("act", r_s[:], PAf)
            proj_tok(mix_s(2), Wk, PAh, M)
            fw.tt("dve", tA[:], PAf, kkb_[0:M, :], ALU.mult)
            fw.tt("dve", jk[:], tA[:], tA[:], ALU.mult)
            fw.red(s16[:, 0:16], v16s(jk[:, :]), ALU.add)
            rstd16(s16[:, 0:16], s16[:, 16:32], None, None, floor=1e-24)
            fw.tt("dve", v16s(kk_s[:, :]), v16s(tA[:, :]), s16[:, 16:32].bc(2, 64), ALU.mult)
            lora(mix_s(4), A1, 64, [A2], AF.Copy, PBh, M)
            for i in range(2):
                fw.tt("dve", tB[:, i * 512:(i + 1) * 512], PBm[i], a0b[0:M, i * 512:(i + 1) * 512], ALU.add)
            fw.act(tB[:], tB[:], AF.Sigmoid)
            fw.stt(jk[:], tB[:], 1.0, kab[0:M, :], ALU.subtract, ALU.mult)
            fw.ts("dve", jk[:], jk[:], 1.0, None, ALU.add)
            fw.tt("dve", kf_s[:], PAf, jk[:], ALU.mult)
            fw.tt("dve", b_s[:], kk_s[:], tB[:], ALU.mult)
            fw.tt("dve", jk[:], r_s[:], kf_s[:], ALU.mult)
            fw.tt("dve", jk[:], jk[:], rkb[0:M, :], ALU.mult)
            fw.red(s16[:, 32:48], v16s(jk[:, :]), ALU.add)
            proj_tok(mix_s(3), Wv, PAh, M)
            fw.cp("act", v_s[:], PAf)
            fw.tt("dve", v16s(bv_s[:, :]), v16s(PAf), s16[:, 32:48].bc(2, 64), ALU.mult)
            lora(mix_s(5), G1, 160, [G2a, G2b], AF.Sigmoid, PBh, M)
            for i in range(2):
                fw.cp("act", g_s[:, i * 512:(i + 1) * 512], PBm[i])
            lora(mix_s(1), W1, 64, [W2], AF.Tanh, PAh, M)
            fw.tt("dve", tA[:], PAf, w0b[0:M, :], ALU.add)
            fw.act(tA[:], tA[:], AF.Exp, scale=-1.0)
            fw.act(tA[:], tA[:], AF.Ln, bias=1.0)
            fw.ts("dve", tA[:], tA[:], -1.0, -0.5, ALU.mult, ALU.add)
            fw.act(Es[:], tA[:], AF.Exp)
            fw.act(Es[:], Es[:], AF.Exp, scale=-1.0)
            fw.ts("dve", tB[:], kk_s[:], -1.0, None, ALU.mult)
            for nm, src in (("aT", tB), ("wT", Es)):
                for kt in range(8):
                    fw.tr(PE[:, kt * 16:(kt + 1) * 16], src[:, kt * 128:(kt + 1) * 128], ident[0:M, 0:M])
                fw.cp("dve", featT[nm][:], PE[:, 0:128].rearrange("p (k b) -> p k b", k=8))
            for nm, src in (("bTs", b_s), ("kTs", kf_s), ("rTs", r_s)):
                for kt in range(8):
                    fw.tr(PT3[:, kt, 0:M], src[:, kt * 128:(kt + 1) * 128], identb[0:M, 0:M])
                fw.cp("dve", featT[nm][:], PT3[:, :, 0:M])
            fw.tt("dve", amask[:], featT["aT"][:, :, :].bc(2, NB), eye16[:, :, :].bc(1, 8), ALU.mult)
            fw.tt("dve", rmask[:], featT["rTs"][:, :, :].bc(2, NB), eye16[:, :, :].bc(1, 8), ALU.mult)
            SY = [PC, PD]
            for b in range(NB):
                H = Hs[b % 2]
                fw.dma("sp", H[:], io.wkv_s_in[b])
                for h in range(16):
                    j, e = h // 2, h % 2
                    p0 = 64 * e
                    fw.mm(SY[e][0:M, j * 64:(j + 1) * 64], amask[p0:p0 + 64, j, b, :], H[p0:p0 + 64, j, :],
                          start=(b == 0 and j == 0), stop=(b == NB - 1), skip=True)
            je = lambda r: r.rearrange("p (j e v) -> p j e v", j=8, e=2)
            fw.cp("dve", je(sa_tok[:, :])[:, :, 0, :], PC[0:M, :].rearrange("p (j v) -> p j v", j=8))
            fw.cp("dve", je(sa_tok[:, :])[:, :, 1, :], PD[0:M, :].rearrange("p (j v) -> p j v", j=8))
            e4 = lambda r, e: r.rearrange("p (j e v) -> p j e v", j=8, e=2)[64 * e:64 * e + 64, :, e, :]
            for b in range(NB):
                H = Hs[b % 2]
                fw.dma("sp", H[:], io.wkv_s_in[b])
                fw.mm(PA[:, 0:512], sel16[:, b, :], sa_tok[:, 0:512])
                fw.mm(PA[:, 512:1024], sel16[:, b, :], sa_tok[:, 512:1024])
                fw.mm(PB0[:, :], sel16[:, b, :], v_s[:, 0:512])
                fw.mm(PB1[:, :], sel16[:, b, :], v_s[:, 512:1024])
                fw.tt("pool", H[:], H[:], featT["wT"][:, :, b].bc(2, 64), ALU.mult)
                for e in range(2):
                    p0 = 64 * e
                    fw.tt("dve", Tt[p0:p0 + 64, :, :], e4(PA[:, :], e), featT["bTs"][p0:p0 + 64, :, b].bc(2, 64), ALU.mult)
                fw.tt("dve", H[:], H[:], Tt[:], ALU.add)
                for e in range(2):
                    p0 = 64 * e
                    for half, PBx in enumerate((PB0, PB1)):
                        src = PBx[:, :].rearrange("p (j e v) -> p j e v", j=4, e=2)[p0:p0 + 64, :, e, :]
                        fw.tt("dve", Tt[p0:p0 + 64, 4 * half:4 * half + 4, :], src,
                              featT["kTs"][p0:p0 + 64, 4 * half:4 * half + 4, b].bc(2, 64), ALU.mult)
                fw.tt("dve", H[:], H[:], Tt[:], ALU.add)
                fw.dma("sp", io.wkv_s[b], H[:])
                for h in range(16):
                    j, e = h // 2, h % 2
                    p0 = 64 * e
                    fw.mm(SY[e][0:M, j * 64:(j + 1) * 64], rmask[p0:p0 + 64, j, b, :], H[p0:p0 + 64, j, :],
                          start=(b == 0 and j == 0), stop=(b == NB - 1), skip=True)
            fw.cp("dve", je(tA[:, :])[:, :, 0, :], PC[0:M, :].rearrange("p (j v) -> p j v", j=8))
            fw.cp("dve", je(tA[:, :])[:, :, 1, :], PD[0:M, :].rearrange("p (j v) -> p j v", j=8))
            fw.red(s16[:, 0:16], v16s(tA[:, :]), ALU.add)
            fw.ts("dve", s16[:, 0:16], s16[:, 0:16], 1.0 / 64, None, ALU.mult)
            fw.tt("dve", v16s(tA[:, :]), v16s(tA[:, :]), s16[:, 0:16].bc(2, 64), ALU.subtract)
            fw.tt("dve", jk[:], tA[:], tA[:], ALU.mult)
            fw.red(s16[:, 16:32], v16s(jk[:, :]), ALU.add)
            rstd16(s16[:, 16:32], s16[:, 48:64], 1.0 / 64, 64e-5)
            fw.tt("dve", v16s(tA[:, :]), v16s(tA[:, :]), s16[:, 48:64].bc(2, 64), ALU.mult)
            fw.tt("dve", tA[:], tA[:], lnw[0:M, :], ALU.mult)
            fw.tt("dve", tA[:], tA[:], lnb[0:M, :], ALU.add)
            fw.tt("dve", tA[:], tA[:], bv_s[:], ALU.add)
            fw.tt("dve", yo_s[:], tA[:], g_s[:], ALU.mult)
            for kt in range(8):
                fw.tr(PT3[:, kt, 0:M], yo_s[:, kt * 128:(kt + 1) * 128], identb[0:M, 0:M])
            fw.cp("dve", yoTs[:], PT3[:, :, 0:M])
            for half in range(2):
                for kt in range(8):
                    fw.mm(PA[0:M, half * 512:(half + 1) * 512], yoTs[:, kt, :], Wo[:, kt, half * 512:(half + 1) * 512],
                          start=kt == 0, stop=kt == 7)
            fw.tt("dve", tB[:], xts[:], PA[0:M, :], ALU.add)
            fw.dma("sp", io.s3s[:, :], tB[:])
        fw.release(base_mark)

    if "3" in phases:
        ffn_phase(1, io.s3, io.y_p, io.s3s, io.y_s, True)

    fw.finish()
    fw.close()
    return nc, fw


def prep_common(inp):
    f = lambda k: np.ascontiguousarray(np.asarray(inp[k], np.float32))
    m = {}
    m["cst"] = host_consts()
    m["w_in0"] = f("w_in0")[0]
    m["w_out0"] = f("w_out0")[0]
    m["norm_mix"] = f("norm_mix")
    m["norm_ffn"] = f("norm_ffn")
    m["norm_final"] = f("norm_final")
    m["ssd_norm"] = f("ssd_norm")[0]
    small0 = np.zeros(64, np.float32)
    small0[0:16] = f("ssd_dt_bias")[0]
    small0[16:32] = f("ssd_a_log")[0]
    small0[32:48] = f("ssd_d")[0]
    small0[48:52] = f("ml_i_bias")[0]
    small0[52:56] = f("ml_f_bias")[0]
    m["small0"] = small0
    cw = f("conv_w")[0].reshape(4, 20, 128).transpose(2, 1, 0)
    cb = f("conv_b")[0].reshape(20, 128).T[:, :, None]
    m["convp"] = np.ascontiguousarray(np.concatenate([cw, cb], axis=2))
    m["mlcol"] = np.ascontiguousarray(np.stack([f("ml_norm")[0].reshape(8, 128).T, f("ml_skip")[0].reshape(8, 128).T], axis=2))
    m["bdq"] = blockdiag(f("ml_wq")[0])
    m["bdk"] = blockdiag(f("ml_wk")[0])
    m["bdv"] = blockdiag(f("ml_wv")[0])
    for nm in ("rw_wr", "rw_wk", "rw_wv", "rw_wo", "rw_w1", "rw_w2", "rw_a1", "rw_a2", "rw_g1", "rw_g2"):
        m[nm] = f(nm)[0]
    m["rw_rows"] = np.ascontiguousarray(np.stack([f(k)[0] for k in ("rw_w0", "rw_a0", "rw_k_k", "rw_k_a", "rw_r_k", "rw_ln_w", "rw_ln_b")]))
    m["rw_mu"] = np.ascontiguousarray(f("rw_mu")[0].reshape(6, 8, 128).transpose(2, 1, 0))
    m["w_gu"] = f("ffn_w_gate_up")
    m["w_dn"] = f("ffn_w_down")
    return m


def prep_core(inp, core):
    f = lambda k: np.asarray(inp[k], np.float32)
    b0 = core * NB
    m = {}
    m["xp"] = np.ascontiguousarray(f("x_prompt")[core])
    m["xs"] = np.ascontiguousarray(f("x_sample")[b0:b0 + NB, 0, :])
    m["conv_s_in"] = np.ascontiguousarray(f("state_conv")[0, b0:b0 + NB].reshape(NB, 3, 20, 128).transpose(3, 2, 1, 0))
    m["ssm_s_in"] = np.ascontiguousarray(f("state_ssm")[0, b0:b0 + NB].reshape(NB, D, 128))
    m["mc_s_in"] = np.ascontiguousarray(f("state_mlstm_c")[0, b0:b0 + NB])
    m["mn_s_in"] = np.ascontiguousarray(f("state_mlstm_n")[0, b0:b0 + NB].reshape(NB, 4, 2, 128).transpose(3, 1, 2, 0).reshape(128, 8, NB))
    m["mm_s_in"] = np.ascontiguousarray(f("state_mlstm_m")[0, b0:b0 + NB])
    m["shift_s_in"] = np.ascontiguousarray(f("state_shift")[0, b0:b0 + NB].reshape(NB, 8, 128).transpose(2, 1, 0))
    m["wkv_s_in"] = np.ascontiguousarray(f("state_wkv")[0, b0:b0 + NB].reshape(NB, 8, 2, 64, 64).transpose(0, 2, 4, 1, 3).reshape(NB, 128, 8, 64))
    return m


def prep_consts(inp):
    f = lambda k: np.asarray(inp[k], np.float32)
    m = prep_common(inp)
    m["c16"] = host_consts16()
    m["eye16"] = np.ascontiguousarray(np.broadcast_to(np.eye(16, dtype=np.float32), (128, 16, 16)))
    dtcol = np.zeros((16, 4), np.float32)
    dtcol[:, 0] = f("ssd_dt_bias")[0]
    dtcol[:, 1] = f("ssd_a_log")[0]
    dtcol[:, 2] = f("ssd_d")[0]
    m["dtcol"] = dtcol
    return m


_NC_CACHE = {}


def kernel(**inp):
    if "nc" not in _NC_CACHE:
        _NC_CACHE["nc"] = build({})
    nc = _NC_CACHE["nc"]
    cm = prep_consts(inp)
    in_maps = [dict(cm, **prep_core(inp, c)) for c in range(NCORE)]
    res = run_bass_kernel_spmd(nc, in_maps, core_ids=list(range(NCORE)))
    R = res.results
    BT = NCORE * NB
    y_p = np.zeros((NCORE, T, D), np.float32)
    y_s = np.zeros((BT, 1, D), np.float32)
    conv_p = np.zeros((1, NCORE, 3, 2560), np.float32)
    conv_s = np.zeros((1, BT, 3, 2560), np.float32)
    ssm_p = np.zeros((1, NCORE, 16, 64, 128), np.float32)
    ssm_s = np.zeros((1, BT, 16, 64, 128), np.float32)
    mc_p = np.zeros((1, NCORE, 4, 256, 256), np.float32)
    mc_s = np.zeros((1, BT, 4, 256, 256), np.float32)
    mn_p = np.zeros((1, NCORE, 4, 256), np.float32)
    mn_s = np.zeros((1, BT, 4, 256), np.float32)
    mm_p = np.zeros((1, NCORE, 4), np.float32)
    mm_s = np.zeros((1, BT, 4), np.float32)
    sh_p = np.zeros((1, NCORE, D), np.float32)
    sh_s = np.zeros((1, BT, D), np.float32)
    wkv_p = np.zeros((1, NCORE, 16, 64, 64), np.float32)
    wkv_s = np.zeros((1, BT, 16, 64, 64), np.float32)
    for c in range(NCORE):
        r = R[c]
        sl = slice(c * NB, (c + 1) * NB)
        y_p[c] = r["y_p"]
        y_s[sl, 0] = r["y_s"]
        conv_p[0, c] = r["conv_p"].transpose(2, 1, 0).reshape(3, 2560)
        conv_s[0, sl] = r["conv_s"].transpose(3, 2, 1, 0).reshape(NB, 3, 2560)
        ssm_p[0, c] = r["ssm_p"].reshape(128, 16, 64).transpose(1, 2, 0)
        ssm_s[0, sl] = r["ssm_s"].reshape(NB, 16, 64, 128)
        mc = r["mc_p"]
        mc_p[0, c] = mc[:, :, :, :256].transpose(2, 1, 0, 3).reshape(4, 256, 256)
        mn_p[0, c] = mc[:, :, :, 256].transpose(2, 1, 0).reshape(4, 256)
        mc_s[0, sl] = r["mc_s"]
        mn_s[0, sl] = r["mn_s"].reshape(128, 4, 2, NB).transpose(3, 1, 2, 0).reshape(NB, 4, 256)
        mm_p[0, c] = r["mm_p"][0]
        mm_s[0, sl] = r["mm_s"]
        sh_p[0, c] = r["shift_p"][0]
        sh_s[0, sl] = r["shift_s"]
        wkv_p[0, c] = r["wkv_p"].reshape(2, 64, 8, 64).transpose(2, 0, 3, 1).reshape(16, 64, 64)
        wkv_s[0, sl] = r["wkv_s"].reshape(NB, 2, 64, 8, 64).transpose(0, 3, 1, 4, 2).reshape(NB, 16, 64, 64)
    return (y_p, y_s, conv_p, conv_s, ssm_p, ssm_s, mc_p, mc_s, mn_p, mn_s, mm_p, mm_s, sh_p, sh_s, wkv_p, wkv_s)
```

```python
import numpy as np
import concourse.bass as bass
import concourse.mybir as mybir
from concourse.bass_utils import run_bass_kernel_spmd

F32 = mybir.dt.float32
BF16 = mybir.dt.bfloat16
ALU = mybir.AluOpType
AF = mybir.ActivationFunctionType
AX = mybir.AxisListType

NCORE = 8
D = 1024
T = 2048
NB = 16
IN0 = 4632
DFF = 2816
EPS = 1e-5


class Tok:
    __slots__ = ("sem", "val", "eng", "seq")

    def __init__(self, sem, val, eng, seq=None):
        self.sem, self.val, self.eng, self.seq = sem, val, eng, seq


class Ref:
    __slots__ = ("T", "ap")

    def __init__(self, T_, ap):
        self.T, self.ap = T_, ap

    def __getitem__(self, k):
        return Ref(self.T, self.ap[k])

    def rearrange(self, p, **kw):
        return Ref(self.T, self.ap.rearrange(p, **kw))

    def unsqueeze(self, a):
        return Ref(self.T, self.ap.unsqueeze(a))

    def to_broadcast(self, shp):
        return Ref(self.T, self.ap.to_broadcast(list(shp)))

    def bc(self, axis, n):
        ap = self.ap.unsqueeze(axis)
        shp = list(ap.shape)
        shp[axis] = n
        return Ref(self.T, ap.to_broadcast(shp))


class TT:
    __slots__ = ("t", "name", "lw", "rd", "psum")

    def __init__(self, t, name, psum=False):
        self.t, self.name, self.lw, self.rd, self.psum = t, name, None, [], psum

    def __getitem__(self, k):
        return Ref(self, self.t[k])


def _Ts(*xs):
    return [x.T for x in xs if isinstance(x, Ref)]


def _a(x):
    return x.ap if isinstance(x, Ref) else x


class Eng:
    def __init__(self, fw, name, h):
        self.fw, self.name, self.h = fw, name, h
        self.sems, self.n, self.waited, self.nsig = [], 0, {}, 0


class Fw:
    EPOCH = 30000
    NDMA = 10

    def __init__(self, nc, need=None):
        self.nc = nc
        self.need = need
        self.waited_on = set()
        self._ctx = []
        self.E = {}
        for name, h in (("pe", nc.tensor), ("dve", nc.vector), ("act", nc.scalar),
                        ("pool", nc.gpsimd), ("sp", nc.sync)):
            self.E[name] = Eng(self, name, h)
        self.dma_sems, self.dma_i = {}, {}
        self.ntile = 0
        self.sb_bytes = 0

    def enter(self, cm):
        v = cm.__enter__()
        self._ctx.append(cm)
        return v

    def close(self):
        for cm in reversed(self._ctx):
            cm.__exit__(None, None, None)
        self._ctx = []

    def new_sem(self, name):
        return self.enter(self.nc.semaphore(name))

    def presem(self, queues=("sp", "pool", "act"), epochs=3):
        for e in self.E.values():
            while len(e.sems) < epochs:
                e.sems.append(self.new_sem(f"e_{e.name}_{len(e.sems)}"))
        for q in queues:
            self.dma_sems[q] = [[self.new_sem(f"d_{q}_{i}"), 0] for i in range(Fw.NDMA)]
            self.dma_i[q] = 0

    def mark(self):
        return len(self._ctx)

    def release(self, mark):
        self.barrier()
        while len(self._ctx) > mark:
            self._ctx.pop().__exit__(None, None, None)

    def _last_tok(self, e):
        return e.last

    def barrier(self):
        for eng in self.E.values():
            for q, slots in self.dma_sems.items():
                for sem, cnt in slots:
                    if cnt > 0:
                        self._wait(eng, Tok(sem, cnt, "dma"))
            for name, e in self.E.items():
                if e is eng or e.n == 0:
                    continue
                self._wait(eng, self._last_tok(e))

    def sb(self, shape, dt=F32, name="t"):
        self.ntile += 1
        n = 1
        for s in shape[1:]:
            n *= s
        self.sb_bytes += n * (2 if dt == BF16 else 4)
        return TT(self.enter(self.nc.sbuf_tensor(f"{name}_{self.ntile}", list(shape), dt)), name)

    def ps(self, shape, dt=F32, name="p"):
        self.ntile += 1
        return TT(self.enter(self.nc.psum_tensor(f"{name}_{self.ntile}", list(shape), dt)), name, psum=True)

    def view(self, ref, name="v"):
        return TT(ref.ap, name, psum=ref.T.psum)

    def _wait(self, eng, tok):
        if tok is None:
            return
        key = id(tok.sem)
        if eng.waited.get(key, 0) >= tok.val:
            return
        if tok.seq is not None:
            self.waited_on.add((tok.eng, tok.seq))
            assert tok.val == int(tok.val), "wait on a non-signalling instruction (two-pass mismatch)"
        eng.h.wait_ge(tok.sem, int(tok.val))
        eng.waited[key] = tok.val

    def _deps(self, eng, reads, writes):
        for t in reads:
            if t.lw is not None:
                self._wait(eng, t.lw)
            if t.psum:
                for r in t.rd:
                    if r.eng != eng.name:
                        self._wait(eng, r)
        strict = eng.name != "pe"
        for t in writes:
            if t.lw is not None and (strict or t.lw.eng != eng.name):
                self._wait(eng, t.lw)
            for r in t.rd:
                if strict or r.eng != eng.name:
                    self._wait(eng, r)

    def _mark(self, tok, reads, writes):
        for t in reads:
            t.rd.append(tok)
        for t in writes:
            t.lw = tok
            t.rd = []

    def op(self, e, fn, reads=(), writes=()):
        eng = self.E[e]
        self._deps(eng, reads, writes)
        seq = eng.n
        eng.n += 1
        signal = self.need is None or (eng.name, seq) in self.need
        ep = eng.nsig // Fw.EPOCH
        while len(eng.sems) <= ep:
            eng.sems.append(self.new_sem(f"e_{eng.name}_{len(eng.sems)}"))
        sem = eng.sems[ep]
        inst = fn(eng.h)
        if signal:
            val = eng.nsig % Fw.EPOCH + 1
            eng.nsig += 1
            inst.then_inc(sem, 1)
        else:
            val = eng.nsig % Fw.EPOCH + 0.5
        tok = Tok(sem, val, eng.name, seq)
        eng.last = tok
        self._mark(tok, reads, writes)
        return tok

    def dma(self, q, out, in_, **kw):
        eng = self.E[q]
        if q not in self.dma_sems:
            self.dma_sems[q] = [[self.new_sem(f"d_{q}_{i}"), 0] for i in range(Fw.NDMA)]
            self.dma_i[q] = 0
        slot = self.dma_sems[q][self.dma_i[q] % Fw.NDMA]
        self.dma_i[q] += 1
        sem, cnt = slot
        if cnt > 0:
            self._wait(eng, Tok(sem, cnt, "dma"))
        reads, writes = _Ts(in_), _Ts(out)
        self._deps(eng, reads, writes)
        inst = eng.h.dma_start(out=_a(out), in_=_a(in_), **kw)
        slot[1] = cnt + 16
        inst.then_inc(sem, 16)
        tok = Tok(sem, cnt + 16, "dma")
        self._mark(tok, reads, writes)
        return tok

    def finish(self):
        eng = self.E["sp"]
        for q, slots in self.dma_sems.items():
            for sem, cnt in slots:
                if cnt > 0:
                    self._wait(eng, Tok(sem, cnt, "dma"))
        for name, e in self.E.items():
            if name == "sp" or e.n == 0:
                continue
            self._wait(eng, self._last_tok(e))

    def mm(self, out, lhsT, rhs, start=True, stop=True, skip=False):
        kw = {"skip_group_check": True} if skip else {}
        return self.op("pe", lambda e: e.matmul(_a(out), _a(lhsT), _a(rhs), start=start, stop=stop, **kw),
                       _Ts(lhsT, rhs), _Ts(out))

    def tr(self, out, in_, ident):
        return self.op("pe", lambda e: e.transpose(_a(out), _a(in_), _a(ident)), _Ts(in_, ident), _Ts(out))

    def act(self, out, in_, func, bias=None, scale=None, accum_out=None):
        kw = {}
        if bias is not None:
            kw["bias"] = _a(bias)
        if scale is not None:
            kw["scale"] = _a(scale)
        if accum_out is not None:
            kw["accum_out"] = _a(accum_out)
        return self.op("act", lambda e: e.activation(out=_a(out), in_=_a(in_), func=func, **kw),
                       _Ts(in_, bias, scale), _Ts(out, accum_out))

    def tt(self, e, out, in0, in1, op):
        return self.op(e, lambda h: h.tensor_tensor(out=_a(out), in0=_a(in0), in1=_a(in1), op=op),
                       _Ts(in0, in1), _Ts(out))

    def ts(self, e, out, in0, s1, s2, op0, op1=None, accum_out=None):
        kw = {}
        if op1 is not None:
            kw["op1"] = op1
        if accum_out is not None:
            kw["accum_out"] = _a(accum_out)
        return self.op(e, lambda h: h.tensor_scalar(out=_a(out), in0=_a(in0), scalar1=_a(s1), scalar2=_a(s2),
                                                    op0=op0, **kw),
                       _Ts(in0, s1, s2), _Ts(out, accum_out))

    def stt(self, out, in0, scalar, in1, op0, op1, accum_out=None):
        kw = {}
        if accum_out is not None:
            kw["accum_out"] = _a(accum_out)
        return self.op("dve", lambda h: h.scalar_tensor_tensor(out=_a(out), in0=_a(in0), scalar=_a(scalar),
                                                               in1=_a(in1), op0=op0, op1=op1, **kw),
                       _Ts(in0, scalar, in1), _Ts(out, accum_out))

    def cp(self, e, out, in_):
        if e == "act":
            return self.act(out, in_, AF.Copy)
        return self.op(e, lambda h: h.tensor_copy(out=_a(out), in_=_a(in_)), _Ts(in_), _Ts(out))

    def red(self, out, in_, op, axis=AX.X):
        return self.op("dve", lambda h: h.tensor_reduce(out=_a(out), in_=_a(in_), axis=axis, op=op),
                       _Ts(in_), _Ts(out))

    def recip(self, out, in_):
        return self.op("dve", lambda h: h.reciprocal(out=_a(out), in_=_a(in_)), _Ts(in_), _Ts(out))

    def memset(self, e, out, val):
        return self.op(e, lambda h: h.memset(_a(out), val), [], _Ts(out))


def host_consts():
    j = np.arange(128)
    c = np.zeros((128, 10, 128), np.float32)
    c[:, 0, :] = (j[:, None] == j[None, :])
    c[:, 1, :] = (j[:, None] <= j[None, :])
    c[:, 2, :] = (j[:, None] > j[None, :])
    c[:, 3, :] = np.where(j[None, :] <= j[:, None], 0.0, -30000.0)
    c[:, 4, :] = 1.0
    c[:, 5, :] = (j[:, None] == 127)
    c[:, 6, :] = (j[:, None] < j[None, :])
    c[:, 7, :] = c[:, 1, :]
    c[:, 8, :] = c[:, 6, :]
    c[:, 9, :] = c[:, 1, :]
    return c


def blockdiag(w):
    out = np.zeros((8, 128, 128), np.float32)
    w = w.reshape(8, 32, 4, 4)
    for nl in range(32):
        out[:, nl * 4:(nl + 1) * 4, nl * 4:(nl + 1) * 4] = w[:, nl]
    return np.ascontiguousarray(out.transpose(1, 0, 2))


def host_consts16():
    h = np.arange(16)
    q = np.arange(128)
    j = np.arange(8)
    e = (h[:, None, None] == (2 * j[None, :, None] + q[None, None, :] // 64)).astype(np.float32)
    sel = np.broadcast_to((h[:, None, None] == h[None, :, None]), (16, 16, 128)).astype(np.float32)
    return np.ascontiguousarray(np.concatenate([e.reshape(16, -1), sel.reshape(16, -1)], axis=1))


class IO:
    pass


def build(cfg):
    _, fw1 = _build(cfg, None)
    nc, fw2 = _build(cfg, fw1.waited_on)
    return nc


def _build(cfg, need):
    nc = bass.Bass("TRN2", target_bir_lowering=False)
    fw = Fw(nc, need)
    io = IO()
    NCH = cfg.get("nch", 16)
    dbg = cfg.get("dbg", ())
    phases = cfg.get("phases", ("0a", "0b", "1", "2", "3"))

    def din(name, shape):
        return nc.dram_tensor(name, list(shape), F32, kind="ExternalInput").ap()

    def dout(name, shape):
        return nc.dram_tensor(name, list(shape), F32, kind="ExternalOutput").ap()

    def dscr(name, shape):
        if name in dbg:
            return dout(name, shape)
        return nc.dram_tensor(name, list(shape), F32).ap()

    io.xp = din("xp", [T, D])
    io.cst = din("cst", [128, 10, 128])
    io.w_in0 = din("w_in0", [D, IN0])
    io.w_out0 = din("w_out0", [2 * D, D])
    io.norm_mix = din("norm_mix", [2, D])
    io.norm_ffn = din("norm_ffn", [2, D])
    io.norm_final = din("norm_final", [D])
    io.ssd_norm = din("ssd_norm", [D])
    io.small0 = din("small0", [64])
    io.convp = din("convp", [128, 20, 5])
    io.mlcol = din("mlcol", [128, 8, 2])
    io.bdq = din("bdq", [128, 8, 128])
    io.bdk = din("bdk", [128, 8, 128])
    io.bdv = din("bdv", [128, 8, 128])
    io.w_gu = din("w_gu", [2, D, 2 * DFF])
    io.w_dn = din("w_dn", [2, DFF, D])
    for nm in ("rw_wr", "rw_wk", "rw_wv", "rw_wo"):
        setattr(io, nm, din(nm, [D, D]))
    io.rw_w1 = din("rw_w1", [D, 64]); io.rw_w2 = din("rw_w2", [64, D])
    io.rw_a1 = din("rw_a1", [D, 64]); io.rw_a2 = din("rw_a2", [64, D])
    io.rw_g1 = din("rw_g1", [D, 160]); io.rw_g2 = din("rw_g2", [160, D])
    io.rw_rows = din("rw_rows", [7, D])
    io.rw_mu = din("rw_mu", [128, 8, 6])
    io.wkv_p = dout("wkv_p", [128, 8, 64])
    io.shift_p = dout("shift_p", [1, D])
    io.xs = din("xs", [NB, D])
    io.c16 = din("c16", [16, 8 * 128 + 16 * 128])
    io.eye16 = din("eye16", [128, 16, 16])
    io.dtcol = din("dtcol", [16, 4])
    io.conv_s_in = din("conv_s_in", [128, 20, 3, NB])
    io.ssm_s_in = din("ssm_s_in", [NB, D, 128])
    io.mc_s_in = din("mc_s_in", [NB, 4, 256, 256])
    io.mn_s_in = din("mn_s_in", [128, 8, NB])
    io.mm_s_in = din("mm_s_in", [NB, 4])
    io.shift_s_in = din("shift_s_in", [128, 8, NB])
    io.wkv_s_in = din("wkv_s_in", [NB, 128, 8, 64])
    io.y_s = dout("y_s", [NB, D])
    io.conv_s = dout("conv_s", [128, 20, 3, NB])
    io.ssm_s = dout("ssm_s", [NB, D, 128])
    io.mc_s = dout("mc_s", [NB, 4, 256, 256])
    io.mn_s = dout("mn_s", [128, 8, NB])
    io.mm_s = dout("mm_s", [NB, 4])
    io.shift_s = dout("shift_s", [NB, D])
    io.wkv_s = dout("wkv_s", [NB, 128, 8, 64])
    io.s1s = dscr("s1s", [NB, D])
    io.s2s = dscr("s2s", [NB, D])
    io.s3s = dscr("s3s", [NB, D])
    if "dbg_a" in dbg:
        io.dbg_a = dout("dbg_a", [NB, D]); io.dbg_b = dout("dbg_b", [NB, 64])
    io.s1 = dscr("s1", [T, D])
    io.s2 = dscr("s2", [T, D])
    io.s3 = dscr("s3", [T, D])
    io.y_p = dout("y_p", [T, D])
    io.ssm_p = dout("ssm_p", [128, D])
    io.mc_p = dout("mc_p", [128, 2, 4, 264])
    io.mm_p = dout("mm_p", [1, 4])
    io.conv_p = dout("conv_p", [128, 20, 3])

    fw.presem(epochs=5)

    cst = fw.sb([128, 10, 128], F32, "cst")
    fw.dma("sp", cst[:], io.cst[:, :, :])
    ident, tri_le, mask_gt, negmask, ones = (cst[:, i, :] for i in range(5))
    sel127 = cst[:, 5, :]
    m4 = cst[:, 6:10, :].rearrange("p a t -> p (a t)")
    identb = fw.sb([128, 128], BF16, "identb")
    fw.cp("dve", identb[:], ident)
    onesb = fw.sb([128, 128], BF16, "onesb")
    fw.cp("dve", onesb[:], ones)
    nst = fw.sb([128, 8], F32, "nst")
    c16 = fw.sb([16, 8 * 128 + 16 * 128], F32, "c16")
    fw.dma("sp", c16[:], io.c16[:, :])
    exp16 = c16[:, 0:1024].rearrange("p (j q) -> p j q", j=8)
    sel16 = c16[:, 1024:3072].rearrange("p (b q) -> p b q", b=16)
    eye16 = fw.sb([128, 16, 16], F32, "eye16")
    fw.dma("sp", eye16[:], io.eye16[:, :, :])
    selb = [None]

    def mk_sel16b():
        selb[0] = fw.sb([16, 16, 128], BF16, "sel16b")
        fw.cp("dve", selb[0][:], sel16)

    def hilo(src, hi, lo):
        fw.cp("dve", hi, src)
        fw.tt("dve", lo, src, hi, ALU.subtract)

    def bcast_rows(P, b, hi, lo):
        fw.mm(P, selb[0][:, b, :], hi, start=True, stop=False)
        fw.mm(P, selb[0][:, b, :], lo, start=False, stop=True)

    SAMPLE = cfg.get("sample", True)

    PA = fw.ps([128, 1024], F32, "PA")
    PB = fw.ps([128, 1024], F32, "PB")
    PC = fw.ps([128, 512], F32, "PC")
    PD = fw.ps([128, 512], F32, "PD")
    PE = fw.ps([128, 512], F32, "PE")
    PT = fw.ps([128, 1024], BF16, "PT")
    pcd = [PC, PD]
    PB0f = fw.view(PB[:, 0:512], "PB0f")
    PB1f = fw.view(PB[:, 512:1024], "PB1f")
    PT3 = PT[:, :].rearrange("p (k m) -> p k m", k=8)
    v16 = lambda r: r.rearrange("p (h q) -> p h q", h=16)

    def load_w(dst, src, kt0, kt1, q="pool", step=2):
        N = src.shape[1]
        cw = 1024 if N > 1024 else N
        if N <= 1024:
            kstep = max(1, min(step, 2048 // max(N, 1))) if N >= 512 else step
        else:
            kstep = 1
        for k in range(kt0, kt1, kstep):
            k1 = min(k + kstep, kt1)
            for c0 in range(0, N, cw):
                c1 = min(c0 + cw, N)
                fw.dma(q, dst[:, k:k1, c0:c1], src[k * 128:k1 * 128, c0:c1].rearrange("(k p) n -> p k n", p=128))

    def rmsnorm(x, g, out, M, junk):
        fw.act(junk[0:M, :], x, AF.Square, accum_out=nst[0:M, 0:1])
        fw.ts("dve", nst[0:M, 1:2], nst[0:M, 0:1], 1.0 / D, EPS, ALU.mult, ALU.add)
        fw.act(nst[0:M, 2:3], nst[0:M, 1:2], AF.Ln)
        fw.act(nst[0:M, 3:4], nst[0:M, 2:3], AF.Exp, scale=-0.5)
        fw.stt(out, x, nst[0:M, 3:4], g[0:M, :], ALU.mult, ALU.mult)

    def to_feat(src, dst, M):
        for kt in range(8):
            fw.tr(PT3[:, kt, 0:M], src[0:M, kt * 128:(kt + 1) * 128], identb[0:M, 0:M])
        fw.cp("dve", dst[:, :, 0:M], PT3[:, :, 0:M])

    def grp_rstd(src, ncol, dst, junk, M=128):
        fw.act(junk[0:M, 0:ncol], src, AF.Square, accum_out=nst[0:M, 4:5])
        fw.ts("dve", nst[0:M, 5:6], nst[0:M, 4:5], 1.0 / ncol, EPS, ALU.mult, ALU.add)
        fw.act(nst[0:M, 6:7], nst[0:M, 5:6], AF.Ln)
        fw.act(dst, nst[0:M, 6:7], AF.Exp, scale=-0.5)

    def proj_feat(W, col0, ntile, xT, M, evac):
        for gi, g0 in enumerate(range(0, ntile, 4)):
            n = min(4, ntile - g0)
            ps3 = pcd[gi % 2][:, :].rearrange("p (a m) -> p a m", a=4)
            for i in range(n):
                col = col0 + (g0 + i) * 128
                for kt in range(8):
                    fw.mm(ps3[:, i, 0:M], W[:, kt, col:col + 128], xT[:, kt, 0:M], start=kt == 0, stop=kt == 7)
            evac(g0, n, ps3[:, 0:n, 0:M])

    def conv_tiles(convin, convp, accs, ct0, n):
        for i in range(n):
            ct = ct0 + i
            fw.act(accs[i][:, :], convin[:, i, 0:128], AF.Identity, scale=convp[:, ct, 0:1], bias=convp[:, ct, 4:5])
        for j in range(1, 4):
            for i in range(n):
                ct = ct0 + i
                fw.stt(accs[i][:, :], convin[:, i, j:j + 128], convp[:, ct, j:j + 1], accs[i][:, :], ALU.mult, ALU.add)

    base_mark = fw.mark()

    if "0a" in phases:
        Wc = fw.sb([128, 8, 1536], BF16, "Wc")
        load_w(Wc, io.w_in0[:, 1024:2560], 0, 8)
        Wz = fw.sb([128, 8, 1024], BF16, "Wz")
        load_w(Wz, io.w_in0[:, 0:1024], 0, 8)
        Wdt = fw.sb([128, 8, 16], BF16, "Wdt")
        load_w(Wdt, io.w_in0[:, 3584:3600], 0, 8, step=8)
        Wo = fw.sb([128, 8, D], BF16, "Wo")
        load_w(Wo, io.w_out0[0:1024, :], 0, 8)
        gmix = fw.sb([128, D], F32, "gmix")
        fw.dma("sp", gmix[:], io.norm_mix[0, :].partition_broadcast(128))
        gssd = fw.sb([128, D], F32, "gssd")
        fw.dma("sp", gssd[:], io.ssd_norm.partition_broadcast(128))
        sm0 = fw.sb([128, 64], F32, "sm0")
        fw.dma("sp", sm0[:], io.small0.partition_broadcast(128))
        dtb_bc, D_bc = sm0[:, 0:16], sm0[:, 32:48]
        A_t = fw.sb([128, 16], F32, "A_t")
        fw.act(A_t[:], sm0[:, 16:32], AF.Exp)
        fw.ts("dve", A_t[:], A_t[:], -1.0, None, ALU.mult)
        convp = fw.sb([128, 20, 5], F32, "convp")
        fw.dma("sp", convp[:], io.convp[:, :, :])
        convin = fw.sb([128, 12, 131], F32, "convin")
        fw.memset("pool", convin[:], 0.0)
        ST = fw.sb([128, D], F32, "ST")
        fw.memset("pool", ST[:], 0.0)
        STb = fw.sb([128, D], BF16, "STb")
        fw.memset("pool", STb[:], 0.0)
        xt = fw.sb([128, D], F32, "xt")
        junk = fw.sb([128, D], F32, "junk")
        xn = fw.sb([128, D], BF16, "xn")
        xnT = fw.sb([128, 8, 128], BF16, "xnT")
        acc = fw.sb([128, 12, 128], F32, "acc")
        accv = [fw.view(acc[:, i, :], f"acc{i}") for i in range(12)]
        cact = fw.sb([128, 12, 128], BF16, "cact")
        zs = fw.sb([128, D], F32, "zs")
        xtok = fw.sb([128, D], BF16, "xtok")
        Btok = fw.sb([128, 256], BF16, "Btok")
        sm = fw.sb([128, 128], F32, "sm")
        Lh = [fw.sb([128, 4, 2, 128], BF16, f"Lh{i}") for i in range(2)]
        dsp = fw.sb([128, 32], BF16, "dsp")
        mgt_b = fw.sb([128, 128], BF16, "mgt_b")
        fw.cp("dve", mgt_b[:], mask_gt)
        tri_b = fw.sb([128, 128], BF16, "tri_b")
        fw.cp("dve", tri_b[:], tri_le)
        Eh = fw.sb([128, 4, 128], F32, "Eh")
        CBm = fw.sb([128, 2, 128], F32, "CBm")
        Wt = fw.sb([128, 16, 128], BF16, "Wt")
        t1 = fw.sb([128, D], F32, "t1")
        yn = fw.sb([128, D], BF16, "yn")
        ynT = fw.sb([128, 8, 128], BF16, "ynT")
        xw = fw.sb([128, D], BF16, "xw")
        x1 = fw.sb([128, D], F32, "x1")

        xtB = [xt, fw.sb([128, D], F32, "xt_b")]
        xnB = [xn, fw.sb([128, D], BF16, "xn_b")]
        xnTB = [xnT, fw.sb([128, 8, 128], BF16, "xnT_b")]
        junkB = fw.sb([128, D], F32, "junk_b")

        def front0(c):
            fw.dma("sp", xtB[c % 2][:], io.xp[c * 128:(c + 1) * 128, :])
            rmsnorm(xtB[c % 2][:], gmix, xnB[c % 2][:], 128, junkB)
            to_feat(xnB[c % 2], xnTB[c % 2], 128)

        front0(0)
        for c in range(NCH):
            xt, xn, xnT = xtB[c % 2], xnB[c % 2], xnTB[c % 2]
            proj_feat(Wc, 0, 12, xnT, 128, lambda g0, n, ps: fw.cp("act", convin[:, g0:g0 + n, 3:131], ps))
            for half in range(2):
                for kt in range(8):
                    fw.mm(PA[:, half * 512:(half + 1) * 512], xnT[:, kt, :], Wz[:, kt, half * 512:(half + 1) * 512],
                          start=kt == 0, stop=kt == 7)
            fw.act(zs[:], PA[:, :], AF.Silu)
            for kt in range(8):
                fw.mm(PE[:, 0:16], xnT[:, kt, :], Wdt[:, kt, :], start=kt == 0, stop=kt == 7)
            fw.tt("dve", sm[:, 0:16], PE[:, 0:16], dtb_bc, ALU.add)
            conv_tiles(convin, convp, accv, 0, 12)
            for i in range(12):
                fw.act(cact[:, i, :], accv[i][:, :], AF.Silu)
            if c + 1 < NCH:
                front0(c + 1)
            fw.cp("pool", convin[:, :, 0:3], convin[:, :, 128:131])
            for kt in range(8):
                fw.tr(PT3[:, kt, :], cact[:, kt, :], identb[:, :])
            fw.cp("dve", xtok[:], PT[:, :])
            for g in range(2):
                fw.tr(PT[:, g * 128:(g + 1) * 128], cact[:, 8 + g, :], identb[:, :])
            fw.cp("dve", Btok[:], PT[:, 0:256])
            fw.act(sm[:, 0:16], sm[:, 0:16], AF.Exp)
            fw.act(sm[:, 0:16], sm[:, 0:16], AF.Ln, bias=1.0)
            fw.tt("dve", sm[:, 16:32], sm[:, 0:16], A_t[:], ALU.mult)
            fw.mm(PE[:, 32:48], tri_le, sm[:, 16:32])
            fw.mm(PE[:, 48:64], ones, sm[:, 16:32])
            fw.act(sm[:, 32:48], PE[:, 32:48], AF.Exp)
            fw.cp("dve", sm[:, 64:80], PE[:, 32:48])
            fw.tt("dve", sm[:, 48:64], PE[:, 48:64], sm[:, 64:80], ALU.subtract)
            fw.act(sm[:, 48:64], sm[:, 48:64], AF.Exp)
            fw.tt("dve", sm[:, 48:64], sm[:, 48:64], sm[:, 0:16], ALU.mult)
            fw.act(sm[:, 80:96], PE[:, 48:64], AF.Exp)
            for g in range(2):
                fw.mm(PE[:, 128 + g * 128:256 + g * 128], cact[:, 8 + g, :], cact[:, 10 + g, :])
                fw.tt("dve", CBm[:, g, :], PE[:, 128 + g * 128:256 + g * 128], tri_le, ALU.mult)
            fw.cp("dve", dsp[:, 0:16], sm[:, 16:32])
            fw.tt("dve", dsp[:, 16:32], sm[:, 16:32], dsp[:, 0:16], ALU.subtract)
            for hq in range(4):
                L = Lh[hq % 2]
                ps3 = pcd[hq % 2][:, :].rearrange("p (a m) -> p a m", a=4)
                for i in range(4):
                    h = hq * 4 + i
                    fw.ts("dve", L[:, i, 0, :], mgt_b[:, :], dsp[:, h:h + 1], None, ALU.mult)
                    fw.ts("dve", L[:, i, 1, :], mgt_b[:, :], dsp[:, 16 + h:17 + h], None, ALU.mult)
                    fw.mm(ps3[:, i, :], L[:, i, 0, :], tri_b[:, :], start=True, stop=False)
                    fw.mm(ps3[:, i, :], L[:, i, 1, :], tri_b[:, :], start=False, stop=True)
                fw.act(Eh[:], ps3, AF.Exp)
                for i in range(4):
                    h = hq * 4 + i
                    fw.stt(Wt[:, h, :], Eh[:, i, :], sm[:, h:h + 1], CBm[:, h // 8, :], ALU.mult, ALU.mult)
            for h in range(16):
                fw.mm(PA[:, h * 64:(h + 1) * 64], Wt[:, h, :], xtok[:, h * 64:(h + 1) * 64])
            for g in range(2):
                fw.mm(PB[:, g * 512:(g + 1) * 512], cact[:, 10 + g, :], STb[:, g * 512:(g + 1) * 512])
            fw.tt("dve", v16(t1[:, :]), v16(PB[:, :]), sm[:, 32:48].bc(2, 64), ALU.mult)
            fw.tt("dve", t1[:], t1[:], PA[:, :], ALU.add)
            fw.tt("pool", v16(junk[:, :]), v16(xtok[:, :]), D_bc.bc(2, 64), ALU.mult)
            fw.tt("dve", t1[:], t1[:], junk[:], ALU.add)
            fw.tt("dve", t1[:], t1[:], zs[:], ALU.mult)
            for g in range(2):
                grp_rstd(t1[:, g * 512:(g + 1) * 512], 512, nst[:, 7:8], junk)
                fw.stt(yn[:, g * 512:(g + 1) * 512], t1[:, g * 512:(g + 1) * 512], nst[:, 7:8],
                       gssd[:, g * 512:(g + 1) * 512], ALU.mult, ALU.mult)
            to_feat(yn, ynT, 128)
            fw.tt("pool", v16(xw[:, :]), v16(xtok[:, :]), sm[:, 48:64].bc(2, 64), ALU.mult)
            for g in range(2):
                fw.mm(PB[:, g * 512:(g + 1) * 512], Btok[:, g * 128:(g + 1) * 128], xw[:, g * 512:(g + 1) * 512])
            fw.tt("dve", v16(ST[:, :]), v16(ST[:, :]), sm[:, 80:96].bc(2, 64), ALU.mult)
            fw.tt("dve", ST[:], ST[:], PB[:, :], ALU.add)
            fw.cp("act", STb[:], ST[:])
            for half in range(2):
                for kt in range(8):
                    fw.mm(PA[:, half * 512:(half + 1) * 512], ynT[:, kt, :], Wo[:, kt, half * 512:(half + 1) * 512],
                          start=kt == 0, stop=kt == 7)
            fw.tt("dve", x1[:], xt[:], PA[:, :], ALU.add)
            fw.dma("sp", io.s1[c * 128:(c + 1) * 128, :], x1[:])

        xt, xn, xnT = xtB[0], xnB[0], xnTB[0]
        if SAMPLE:
            dtcol = fw.sb([16, 4], F32, "dtcol")
            fw.dma("sp", dtcol[:], io.dtcol[:, :])
            fw.act(dtcol[:, 3:4], dtcol[:, 1:2], AF.Exp)
            fw.ts("dve", dtcol[:, 3:4], dtcol[:, 3:4], -1.0, None, ALU.mult)
            cst_s = fw.sb([128, 12, 3, NB], F32, "cst_s")
            fw.dma("sp", cst_s[:], io.conv_s_in[:, 0:12, :, :])
            uS = fw.sb([128, 12, NB], F32, "uS")
            accs = fw.sb([128, 12, NB], F32, "accs")
            tmps = fw.sb([128, 12, NB], F32, "tmps")
            cs = fw.sb([128, 12, NB], F32, "cs")
            zsT = fw.sb([128, 8, NB], F32, "zsT")
            dd = fw.sb([16, 48], F32, "dd")
            dx = fw.sb([128, 8, 48], F32, "dx")
            dtx = fw.sb([128, 8, NB], F32, "dtx")
            BCtok = fw.sb([16, 512], F32, "BCtok")
            Sb = [fw.sb([128, 8, 128], F32, f"Sb{i}") for i in range(2)]
            T1s = fw.sb([128, 8, 128], F32, "T1s")
            ysT = fw.sb([128, 8, NB], F32, "ysT")
            fw.dma("sp", xt[0:NB, :], io.xs[:, :])
            rmsnorm(xt[0:NB, :], gmix, xn[0:NB, :], NB, junk)
            to_feat(xn, xnT, NB)
            proj_feat(Wc, 0, 12, xnT, NB, lambda g0, n, ps: fw.cp("act", uS[:, g0:g0 + n, :], ps))
            proj_feat(Wz, 0, 8, xnT, NB, lambda g0, n, ps: fw.act(zsT[:, g0:g0 + n, :], ps, AF.Silu))
            wv = lambda j: convp[:, 0:12, j].bc(2, NB)
            fw.tt("dve", accs[:], cst_s[:, :, 0, :], wv(0), ALU.mult)
            fw.tt("dve", accs[:], accs[:], wv(4), ALU.add)
            for j in (1, 2):
                fw.tt("dve", tmps[:], cst_s[:, :, j, :], wv(j), ALU.mult)
                fw.tt("dve", accs[:], accs[:], tmps[:], ALU.add)
            fw.tt("dve", tmps[:], uS[:], wv(3), ALU.mult)
            fw.tt("dve", accs[:], accs[:], tmps[:], ALU.add)
            fw.act(cs[:], accs[:], AF.Silu)
            fw.dma("sp", io.conv_s[:, 0:12, 0:2, :], cst_s[:, :, 1:3, :])
            fw.dma("sp", io.conv_s[:, 0:12, 2, :], uS[:])
            for kt in range(8):
                fw.mm(PE[0:16, 0:16], Wdt[:, kt, :], xnT[:, kt, 0:NB], start=kt == 0, stop=kt == 7)
            fw.ts("dve", dd[:, 0:16], PE[0:16, 0:16], dtcol[:, 0:1], None, ALU.add)
            fw.act(dd[:, 0:16], dd[:, 0:16], AF.Exp)
            fw.act(dd[:, 0:16], dd[:, 0:16], AF.Ln, bias=1.0)
            fw.ts("dve", dd[:, 16:32], dd[:, 0:16], dtcol[:, 3:4], None, ALU.mult)
            fw.act(dd[:, 16:32], dd[:, 16:32], AF.Exp)
            fw.ts("dve", dd[:, 32:48], ones[0:16, 0:16], dtcol[:, 2:3], None, ALU.mult)
            for j in range(8):
                fw.mm(PE[:, 128 + j * 48:128 + (j + 1) * 48], exp16[:, j, :], dd[:, :])
            fw.cp("dve", dx[:], PE[:, 128:512].rearrange("p (j c) -> p j c", j=8))
            fw.tt("dve", dtx[:], dx[:, :, 0:16], cs[:, 0:8, :], ALU.mult)
            for i in range(4):
                fw.tr(PD[0:16, i * 128:(i + 1) * 128], cs[:, 8 + i, :], ident)
            fw.cp("dve", BCtok[:], PD[0:16, :])
            mk_sel16b()
            BCh = fw.sb([16, 512], BF16, "BCh")
            BCl = fw.sb([16, 512], BF16, "BCl")
            hilo(BCtok[:], BCh[:], BCl[:])
            for b in range(NB):
                S = Sb[b % 2]
                fw.dma("sp", S[:], io.ssm_s_in[b].rearrange("(j q) n -> q j n", q=128))
                bcast_rows(PC[:, :], b, BCh[:, :], BCl[:, :])
                for g in range(2):
                    fw.tt("dve", T1s[:, 4 * g:4 * g + 4, :], PC[:, g * 128:(g + 1) * 128].bc(1, 4),
                          dtx[:, 4 * g:4 * g + 4, b].bc(2, 128), ALU.mult)
                fw.tt("pool", S[:], S[:], dx[:, :, 16 + b].bc(2, 128), ALU.mult)
                fw.tt("dve", S[:], S[:], T1s[:], ALU.add)
                fw.dma("sp", io.ssm_s[b].rearrange("(j q) n -> q j n", q=128), S[:])
                for g in range(2):
                    fw.tt("dve", T1s[:, 4 * g:4 * g + 4, :], S[:, 4 * g:4 * g + 4, :],
                          PC[:, 256 + g * 128:256 + (g + 1) * 128].bc(1, 4), ALU.mult)
                fw.red(ysT[:, :, b], T1s[:], ALU.add)
            fw.tt("dve", dtx[:], dx[:, :, 32:48], cs[:, 0:8, :], ALU.mult)
            fw.tt("dve", ysT[:], ysT[:], dtx[:], ALU.add)
            fw.tt("dve", ysT[:], ysT[:], zsT[:], ALU.mult)
            for j in range(8):
                fw.tr(PA[0:16, j * 128:(j + 1) * 128], ysT[:, j, :], ident)
            fw.cp("dve", t1[0:NB, :], PA[0:NB, :])
            for g in range(2):
                grp_rstd(t1[0:NB, g * 512:(g + 1) * 512], 512, nst[0:NB, 7:8], junk, NB)
                fw.stt(yn[0:NB, g * 512:(g + 1) * 512], t1[0:NB, g * 512:(g + 1) * 512], nst[0:NB, 7:8],
                       gssd[0:NB, g * 512:(g + 1) * 512], ALU.mult, ALU.mult)
            to_feat(yn, ynT, NB)
            for half in range(2):
                for kt in range(8):
                    fw.mm(PA[0:NB, half * 512:(half + 1) * 512], ynT[:, kt, 0:NB], Wo[:, kt, half * 512:(half + 1) * 512],
                          start=kt == 0, stop=kt == 7)
            fw.tt("dve", x1[0:NB, :], xt[0:NB, :], PA[0:NB, :], ALU.add)
            fw.dma("sp", io.s1s[:, :], x1[0:NB, :])
        fw.dma("sp", io.ssm_p[:, :], ST[:])
        fw.dma("sp", io.conv_p[:, 0:12, :], convin[:, :, 0:3])
        fw.release(base_mark)

    if "0b" in phases:
        Wx = fw.sb([128, 8, 1024], BF16, "Wx")
        load_w(Wx, io.w_in0[:, 2560:3584], 0, 8)
        Wg = fw.sb([128, 8, 1024], BF16, "Wg")
        load_w(Wg, io.w_in0[:, 3600:4624], 0, 8)
        Wif = fw.sb([128, 8, 16], BF16, "Wif")
        load_w(Wif, io.w_in0[:, 4616:4632], 0, 8, step=8)
        Wo = fw.sb([128, 8, D], BF16, "Wo")
        load_w(Wo, io.w_out0[1024:2048, :], 0, 8)
        BDq = fw.sb([128, 8, 128], BF16, "BDq")
        BDk = fw.sb([128, 8, 128], BF16, "BDk")
        BDv = fw.sb([128, 8, 128], BF16, "BDv")
        fw.dma("pool", BDq[:], io.bdq[:, :, :])
        fw.dma("pool", BDk[:], io.bdk[:, :, :])
        fw.dma("pool", BDv[:], io.bdv[:, :, :])
        gmix = fw.sb([128, D], F32, "gmix")
        fw.dma("sp", gmix[:], io.norm_mix[0, :].partition_broadcast(128))
        sm0 = fw.sb([128, 64], F32, "sm0")
        fw.dma("sp", sm0[:], io.small0.partition_broadcast(128))
        ib_bc, fb_bc = sm0[:, 48:52], sm0[:, 52:56]
        convp = fw.sb([128, 20, 5], F32, "convp")
        fw.dma("sp", convp[:], io.convp[:, :, :])
        mlcol = fw.sb([128, 8, 2], F32, "mlcol")
        fw.dma("sp", mlcol[:], io.mlcol[:, :, :])
        convin = fw.sb([128, 8, 131], F32, "convin")
        fw.memset("pool", convin[:], 0.0)
        Cst = fw.sb([128, 2, 4, 264], F32, "Cst")
        fw.memset("pool", Cst[:], 0.0)
        Cb = fw.sb([128, 2, 4, 264], BF16, "Cb")
        fw.memset("pool", Cb[:], 0.0)
        mprev = fw.sb([128, 4], F32, "mprev")
        fw.memset("pool", mprev[:], 0.0)
        xt = fw.sb([128, D], F32, "xt")
        junk = fw.sb([128, D], F32, "junk")
        xn = fw.sb([128, D], BF16, "xn")
        xnT = fw.sb([128, 8, 128], BF16, "xnT")
        acc = fw.sb([128, 8, 128], F32, "acc")
        accv = [fw.view(acc[:, i, :], f"acc{i}") for i in range(8)]
        cact = fw.sb([128, 8, 128], BF16, "cact")
        xmraw = fw.sb([128, 8, 128], BF16, "xmraw")
        sigoT = fw.sb([128, 8, 128], BF16, "sigoT")
        sm2 = fw.sb([128, 64], F32, "sm2")
        qT = fw.sb([128, 8, 128], BF16, "qT")
        kT = fw.sb([128, 8, 128], BF16, "kT")
        vtok = fw.sb([128, 4, 264], BF16, "vtok")
        fw.memset("pool", vtok[:], 1.0)
        kw_ = fw.sb([128, 4, 256], BF16, "kw")
        HT = [(fw.sb([128, 128], F32, f"Rh{i}"), fw.sb([128, 128], F32, f"dlm{i}"), fw.sb([128, 128], F32, f"Dm{i}"),
               fw.sb([128, 128], BF16, f"Sg{i}"), fw.sb([128, 128], BF16, f"SgT{i}"), fw.sb([128, 16], F32, f"hs{i}"),
               fw.sb([128, 258], F32, f"comb{i}"), fw.sb([128, 256], F32, f"hh{i}")) for i in range(2)]
        junk2 = [fw.sb([128, 256], F32, f"jk{i}") for i in range(2)]
        ktb = fw.sb([128, D], BF16, "ktb")
        mt = fw.sb([128, 16], F32, "mt")
        fw.memset("pool", mt[:], 0.0)
        fw.memset("pool", sm2[:], 0.0)
        hmn = fw.sb([128, D], BF16, "hmn")
        hmnT = fw.sb([128, 8, 128], BF16, "hmnT")
        hmfT = fw.sb([128, 8, 128], BF16, "hmfT")
        x1 = fw.sb([128, D], F32, "x1")

        lvl = cfg.get('lvl', 99)
        xnTB = [xnT, fw.sb([128, 8, 128], BF16, "xnT_b")]

        def front0(c):
            fw.dma("sp", xt[:], io.xp[c * 128:(c + 1) * 128, :])
            rmsnorm(xt[:], gmix, xn[:], 128, junk)
            to_feat(xn, xnTB[c % 2], 128)

        front0(0)
        for c in range(NCH):
            xnT = xnTB[c % 2]
            fw.dma("sp", x1[:], io.s1[c * 128:(c + 1) * 128, :])
            proj_feat(Wx, 0, 8, xnT, 128, lambda g0, n, ps: fw.cp("act", convin[:, g0:g0 + n, 3:131], ps))
            proj_feat(Wg, 0, 8, xnT, 128, lambda g0, n, ps: fw.act(sigoT[:, g0:g0 + n, :], ps, AF.Sigmoid))
            for kt in range(8):
                fw.mm(PE[:, 16:32], xnT[:, kt, :], Wif[:, kt, :], start=kt == 0, stop=kt == 7)
            fw.tt("dve", sm2[:, 0:4], PE[:, 24:28], ib_bc, ALU.add)
            fw.tt("dve", sm2[:, 4:8], PE[:, 28:32], fb_bc, ALU.add)
            conv_tiles(convin, convp, accv, 12, 8)
            for i in range(8):
                fw.act(cact[:, i, :], accv[i][:, :], AF.Silu)
            if c + 1 < NCH:
                front0(c + 1)
            fw.cp("pool", xmraw[:], convin[:, :, 3:131])
            fw.cp("pool", convin[:, :, 0:3], convin[:, :, 128:131])
            if lvl < 2:
                continue
            for tile in range(8):
                ps = pcd[tile % 2]
                fw.mm(ps[:, 0:128], (Wx[:, tile, 0:128] if cfg.get('alt') else BDq[:, tile, :]), cact[:, tile, :])
                fw.mm(ps[:, 128:256], (Wx[:, tile, 0:128] if cfg.get('alt') else BDk[:, tile, :]), cact[:, tile, :])
                if cfg.get('alt') != 2:
                    fw.cp("dve", qT[:, tile, :], ps[:, 0:128])
                if cfg.get('alt') not in (2, 3):
                    fw.ts("dve", kT[:, tile, :], ps[:, 128:256], 0.0625, None, ALU.mult)
            if lvl < 2.1:
                continue
            for tile in range(8):
                fw.mm(PA[:, tile * 128:(tile + 1) * 128], xmraw[:, tile, :], BDv[:, tile, :])
                fw.mm(PB[:, tile * 128:(tile + 1) * 128], cact[:, tile, :], BDk[:, tile, :])
            if lvl < 2.2:
                continue
            fw.cp("act", vtok[:, :, 0:256], PA[:, :].rearrange("p (h v) -> p h v", h=4))
            fw.cp("dve", ktb[:], PB[:, :])
            if lvl < 2.3:
                continue
            fw.act(sm2[:, 4:8], sm2[:, 4:8], AF.Exp, scale=-1.0)
            fw.act(sm2[:, 4:8], sm2[:, 4:8], AF.Ln, bias=1.0)
            fw.ts("dve", sm2[:, 4:8], sm2[:, 4:8], -1.0, None, ALU.mult)
            fw.mm(PE[:, 64:80], tri_le, sm2[:, 0:16])
            fw.mm(PE[:, 96:112], ones, sm2[:, 0:16])
            fw.cp("dve", sm2[:, 8:12], PE[:, 68:72])
            fw.cp("dve", sm2[:, 12:16], PE[:, 100:104])
            fw.tt("dve", sm2[:, 16:20], sm2[:, 8:12], mprev[:], ALU.add)
            if lvl < 3:
                continue
            def head_gen(h, pi):
                Rh, dlm, Dm, Sg, SgT, hs, comb, hh = HT[pi]
                Pd = pcd[pi]
                Pn = [PA, PB][pi]
                fw.ts("dve", Rh[:], mask_gt, sm2[:, 4 + h:5 + h], None, ALU.mult)
                fw.stt(Rh[:], ident, sm2[:, h:h + 1], Rh[:], ALU.mult, ALU.add)
                yield
                fw.mm(Pd[:, 0:128], tri_le, Rh[:])
                fw.mm(Pd[:, 128:256], qT[:, 2 * h, :], kT[:, 2 * h, :], start=True, stop=False)
                fw.mm(Pd[:, 128:256], qT[:, 2 * h + 1, :], kT[:, 2 * h + 1, :], start=False, stop=True)
                fw.mm(Pn[:, 512:770], qT[:, 2 * h, :], Cb[:, 0, h, 0:258], start=True, stop=False)
                fw.mm(Pn[:, 512:770], qT[:, 2 * h + 1, :], Cb[:, 1, h, 0:258], start=False, stop=True)
                yield
                fw.tt("dve", dlm[:], Pd[:, 0:128], negmask, ALU.add)
                yield
                fw.red(hs[:, 0:1], dlm[:], ALU.max)
                yield
                fw.tt("dve", mt[:, h:h + 1], hs[:, 0:1], sm2[:, 16 + h:17 + h], ALU.max)
                yield
                fw.ts("dve", hs[:, 1:2], mt[:, h:h + 1], -1.0, None, ALU.mult)
                yield
                fw.act(Dm[:], dlm[:], AF.Exp, bias=hs[:, 1:2])
                fw.act(hs[:, 2:3], sm2[:, 16 + h:17 + h], AF.Exp, bias=hs[:, 1:2])
                fw.act(hs[:, 3:4], mt[:, h:h + 1], AF.Exp, scale=-1.0)
                yield
                fw.tt("dve", Sg[:], Pd[:, 128:256], Dm[:], ALU.mult)
                yield
                fw.tr(PT[:, pi * 128:(pi + 1) * 128], Sg[:], identb[:, :])
                yield
                fw.cp("dve", SgT[:], PT[:, pi * 128:(pi + 1) * 128])
                fw.act(comb[:], Pn[:, 512:770], AF.Copy, scale=hs[:, 2:3])
                yield
                fw.mm(Pn[:, 0:258], SgT[:], vtok[:, h, 0:258])
                yield
                fw.tt("dve", comb[:], comb[:], Pn[:, 0:258], ALU.add)
                yield
                fw.ts("dve", hs[:, 6:7], comb[:, 256:257], -1.0, None, ALU.mult)
                yield
                fw.tt("dve", hs[:, 6:7], hs[:, 6:7], comb[:, 256:257], ALU.max)
                yield
                fw.tt("dve", hs[:, 4:5], hs[:, 6:7], hs[:, 3:4], ALU.max)
                yield
                fw.recip(hs[:, 5:6], hs[:, 4:5])
                yield
                fw.ts("dve", hh[:], comb[:, 0:256], hs[:, 5:6], None, ALU.mult)
                yield
                fw.act(junk2[pi][:, 0:256], hh[:], AF.Square, accum_out=hs[:, 8:9])
                yield
                fw.ts("dve", hs[:, 9:10], hs[:, 8:9], 1.0 / 256, EPS, ALU.mult, ALU.add)
                yield
                fw.act(hs[:, 10:11], hs[:, 9:10], AF.Ln)
                fw.act(hs[:, 11:12], hs[:, 10:11], AF.Exp, scale=-0.5)
                yield
                fw.ts("dve", hmn[:, h * 256:(h + 1) * 256], hh[:], hs[:, 11:12], None, ALU.mult)
                yield

            for h0 in (0, 2):
                for _ in zip(head_gen(h0, 0), head_gen(h0 + 1, 1)):
                    pass
            to_feat(hmn, hmnT, 128)
            for tile in range(8):
                fw.ts("dve", hmfT[:, tile, :], hmnT[:, tile, :], mlcol[:, tile, 0:1], None, ALU.mult)
                fw.stt(hmfT[:, tile, :], cact[:, tile, :], mlcol[:, tile, 1:2], hmfT[:, tile, :], ALU.mult, ALU.add)
            fw.tt("dve", hmfT[:], hmfT[:], sigoT[:], ALU.mult)
            if lvl < 5:
                continue
            fw.mm(PE[:, 112:128], sel127, mt[:])
            fw.cp("dve", sm2[:, 20:24], PE[:, 112:116])
            fw.tt("dve", sm2[:, 24:28], sm2[:, 12:16], sm2[:, 8:12], ALU.subtract)
            fw.tt("dve", sm2[:, 24:28], sm2[:, 24:28], sm2[:, 0:4], ALU.add)
            fw.tt("dve", sm2[:, 24:28], sm2[:, 24:28], sm2[:, 20:24], ALU.subtract)
            fw.act(sm2[:, 28:32], sm2[:, 24:28], AF.Exp)
            fw.ts("dve", sm2[:, 28:32], sm2[:, 28:32], 0.0625, None, ALU.mult)
            fw.tt("dve", sm2[:, 32:36], sm2[:, 12:16], mprev[:], ALU.add)
            fw.tt("dve", sm2[:, 32:36], sm2[:, 32:36], sm2[:, 20:24], ALU.subtract)
            fw.act(sm2[:, 32:36], sm2[:, 32:36], AF.Exp)
            fw.tt("dve", kw_[:], ktb[:, :].rearrange("p (h d) -> p h d", h=4), sm2[:, 28:32].bc(2, 256), ALU.mult)
            for kt in range(2):
                for h in range(4):
                    fw.mm(PB[:, h * 256:(h + 1) * 256], kw_[:, h, kt * 128:(kt + 1) * 128], vtok[:, h, 0:256])
                    fw.mm(PE[:, 80 + 2 * h:82 + 2 * h], kw_[:, h, kt * 128:(kt + 1) * 128], onesb[:, 0:2])
                for h in range(4):
                    fw.stt(Cst[:, kt, h, 0:256], Cst[:, kt, h, 0:256], sm2[:, 32 + h:33 + h],
                           PB[:, h * 256:(h + 1) * 256], ALU.mult, ALU.add)
                    fw.stt(Cst[:, kt, h, 256:257], Cst[:, kt, h, 256:257], sm2[:, 32 + h:33 + h],
                           PE[:, 80 + 2 * h:81 + 2 * h], ALU.mult, ALU.add)
            fw.cp("act", Cb[:], Cst[:])
            fw.cp("dve", mprev[:], sm2[:, 20:24])
            if lvl < 6:
                continue
            for half in range(2):
                for kt in range(8):
                    fw.mm(PA[:, half * 512:(half + 1) * 512], hmfT[:, kt, :], Wo[:, kt, half * 512:(half + 1) * 512],
                          start=kt == 0, stop=kt == 7)
            fw.tt("dve", x1[:], x1[:], PA[:, :], ALU.add)
            fw.dma("sp", io.s1[c * 128:(c + 1) * 128, :], x1[:])

        xnT = xnTB[0]
        if SAMPLE:
            cst_s = fw.sb([128, 8, 3, NB], F32, "cst_s")
            fw.dma("sp", cst_s[:], io.conv_s_in[:, 12:20, :, :])
            uS = fw.sb([128, 8, NB], F32, "uS")
            accs = fw.sb([128, 8, NB], F32, "accs")
            tmps = fw.sb([128, 8, NB], F32, "tmps")
            cs = fw.sb([128, 8, NB], F32, "cs")
            cs_bf = fw.sb([128, 8, NB], BF16, "cs_bf")
            us_bf = fw.sb([128, 8, NB], BF16, "us_bf")
            qTs = fw.sb([128, 8, NB], F32, "qTs")
            kTs = fw.sb([128, 8, NB], F32, "kTs")
            kws = fw.sb([128, 8, NB], F32, "kws")
            nS = fw.sb([128, 8, NB], F32, "nS")
            vtoks = fw.sb([16, D], F32, "vtoks")
            g16 = fw.sb([16, 64], F32, "g16")
            Zd = fw.sb([16, 128], F32, "Zd")
            wd = fw.sb([128, 2, 4, NB], F32, "wd")
            qmask = fw.sb([128, 8, NB, NB], F32, "qmask")
            Cs = [fw.sb([128, 8, 256], F32, f"Cs{i}") for i in range(2)]
            Tt = fw.sb([128, 8, 256], F32, "Tt")
            numt = fw.sb([16, D], F32, "numt")
            fw.dma("sp", xt[0:NB, :], io.xs[:, :])
            fw.dma("sp", x1[0:NB, :], io.s1s[:, :])
            fw.dma("sp", g16[:, 8:12], io.mm_s_in[:, :])
            fw.dma("sp", nS[:], io.mn_s_in[:, :, :])
            rmsnorm(xt[0:NB, :], gmix, xn[0:NB, :], NB, junk)
            to_feat(xn, xnT, NB)
            proj_feat(Wx, 0, 8, xnT, NB, lambda g0, n, ps: fw.cp("act", uS[:, g0:g0 + n, :], ps))
            proj_feat(Wg, 0, 8, xnT, NB, lambda g0, n, ps: fw.act(sigoT[:, g0:g0 + n, 0:NB], ps, AF.Sigmoid))
            for kt in range(8):
                fw.mm(PE[0:NB, 16:32], xnT[:, kt, 0:NB], Wif[:, kt, :], start=kt == 0, stop=kt == 7)
            fw.tt("dve", g16[:, 0:4], PE[0:NB, 24:28], ib_bc[0:NB, :], ALU.add)
            fw.tt("dve", g16[:, 4:8], PE[0:NB, 28:32], fb_bc[0:NB, :], ALU.add)
            fw.act(g16[:, 4:8], g16[:, 4:8], AF.Exp, scale=-1.0)
            fw.act(g16[:, 4:8], g16[:, 4:8], AF.Ln, bias=1.0)
            fw.ts("dve", g16[:, 4:8], g16[:, 4:8], -1.0, None, ALU.mult)
            wv = lambda j: convp[:, 12:20, j].bc(2, NB)
            fw.tt("dve", accs[:], cst_s[:, :, 0, :], wv(0), ALU.mult)
            fw.tt("dve", accs[:], accs[:], wv(4), ALU.add)
            for j in (1, 2):
                fw.tt("dve", tmps[:], cst_s[:, :, j, :], wv(j), ALU.mult)
                fw.tt("dve", accs[:], accs[:], tmps[:], ALU.add)
            fw.tt("dve", tmps[:], uS[:], wv(3), ALU.mult)
            fw.tt("dve", accs[:], accs[:], tmps[:], ALU.add)
            fw.act(cs[:], accs[:], AF.Silu)
            fw.dma("sp", io.conv_s[:, 12:20, 0:2, :], cst_s[:, :, 1:3, :])
            fw.dma("sp", io.conv_s[:, 12:20, 2, :], uS[:])
            fw.cp("dve", cs_bf[:], cs[:])
            fw.cp("dve", us_bf[:], uS[:])
            for tile in range(8):
                ps = pcd[tile % 2]
                fw.mm(ps[:, 0:NB], BDq[:, tile, :], cs_bf[:, tile, :])
                fw.mm(ps[:, 16:16 + NB], BDk[:, tile, :], cs_bf[:, tile, :])
                fw.cp("dve", qTs[:, tile, :], ps[:, 0:NB])
                fw.ts("dve", kTs[:, tile, :], ps[:, 16:16 + NB], 0.0625, None, ALU.mult)
            for tile in range(8):
                fw.mm(PA[0:NB, tile * 128:(tile + 1) * 128], us_bf[:, tile, :], BDv[:, tile, :])
            fw.cp("act", vtoks[:], PA[0:NB, :])
            mk_sel16b()
            vth = fw.sb([16, D], BF16, "vth")
            vtl = fw.sb([16, D], BF16, "vtl")
            hilo(vtoks[:], vth[:], vtl[:])
            fw.tt("dve", g16[:, 16:20], g16[:, 4:8], g16[:, 8:12], ALU.add)
            fw.tt("dve", g16[:, 12:16], g16[:, 16:20], g16[:, 0:4], ALU.max)
            fw.dma("sp", io.mm_s[:, :], g16[:, 12:16])
            fw.tt("dve", g16[:, 20:24], g16[:, 0:4], g16[:, 12:16], ALU.subtract)
            fw.act(g16[:, 20:24], g16[:, 20:24], AF.Exp)
            fw.tt("dve", g16[:, 24:28], g16[:, 16:20], g16[:, 12:16], ALU.subtract)
            fw.act(g16[:, 24:28], g16[:, 24:28], AF.Exp)
            fw.act(g16[:, 28:32], g16[:, 12:16], AF.Exp, scale=-1.0)
            z3 = lambda r: r.rearrange("p (h b) -> p h b", h=4)
            fw.tt("dve", z3(Zd[:, 0:64]), g16[:, 20:24].bc(2, NB), ident[0:NB, 0:NB].bc(1, 4), ALU.mult)
            fw.tt("dve", z3(Zd[:, 64:128]), g16[:, 24:28].bc(2, NB), ident[0:NB, 0:NB].bc(1, 4), ALU.mult)
            fw.mm(PE[:, 128:256], ones[0:NB, :], Zd[:, :])
            fw.cp("dve", wd[:], PE[:, 128:256].rearrange("p (w h b) -> p w h b", w=2, h=4))
            k4 = lambda r: r.rearrange("p (h k) b -> p h k b", h=4)
            fw.tt("dve", k4(kws[:, :, :]), k4(kTs[:, :, :]), wd[:, 0, :, :].bc(2, 2), ALU.mult)
            fw.tt("dve", k4(nS[:, :, :]), k4(nS[:, :, :]), wd[:, 1, :, :].bc(2, 2), ALU.mult)
            fw.tt("dve", nS[:], nS[:], kws[:], ALU.add)
            fw.dma("sp", io.mn_s[:, :, :], nS[:])
            fw.tt("dve", tmps[:], qTs[:], nS[:], ALU.mult)
            for h in range(4):
                for kt in range(2):
                    fw.mm(PE[0:NB, 256 + 2 * h:258 + 2 * h], tmps[:, 2 * h + kt, :], ones[:, 0:2], start=kt == 0, stop=kt == 1)
            fw.tt("dve", qmask[:], qTs[:, :, :].bc(2, NB), eye16[:, :, :].bc(1, 8), ALU.mult)
            for b in range(NB):
                Cc = Cs[b % 2]
                fw.dma("sp", Cc[:], io.mc_s_in[b].rearrange("h (k p) v -> p (h k) v", p=128))
                bcast_rows(PA[:, 0:512], b, vth[:, 0:512], vtl[:, 0:512])
                bcast_rows(PA[:, 512:1024], b, vth[:, 512:1024], vtl[:, 512:1024])
                fw.tt("dve", Tt[:, :, :].rearrange("p (h k) v -> p h k v", h=4),
                      PA[:, :].rearrange("p (h v) -> p h v", h=4).bc(2, 2),
                      kws[:, :, b].rearrange("p (h k) -> p h k", h=4).bc(3, 256), ALU.mult)
                fw.tt("pool", Cc[:, :, :].rearrange("p (h k) v -> p h (k v)", h=4),
                      Cc[:, :, :].rearrange("p (h k) v -> p h (k v)", h=4), wd[:, 1, :, b].bc(2, 512), ALU.mult)
                fw.tt("dve", Cc[:], Cc[:], Tt[:], ALU.add)
                fw.dma("sp", io.mc_s[b].rearrange("h (k p) v -> p (h k) v", p=128), Cc[:])
                for tile in range(8):
                    h, kt = tile // 2, tile % 2
                    fw.mm(PB[0:NB, h * 256:(h + 1) * 256], qmask[:, tile, b, :], Cc[:, tile, :],
                          start=(b == 0 and tile in (0, 4)), stop=(b == NB - 1 and kt == 1), skip=True)
            fw.cp("act", numt[:], PB[0:NB, :])
            if "dbg_a" in dbg:
                fw.dma("sp", io.dbg_a[:, :], numt[:])
                fw.cp("dve", g16[:, 40:44], PE[0:NB, 256:264].rearrange("p (h t) -> p h t", t=2)[:, :, 0])
                fw.dma("sp", io.dbg_b[:, :], g16[:])
            dn = PE[0:NB, 256:264].rearrange("p (h t) -> p h t", t=2)[:, :, 0]
            fw.ts("dve", g16[:, 32:36], dn, -1.0, None, ALU.mult)
            fw.tt("dve", g16[:, 32:36], g16[:, 32:36], dn, ALU.max)
            fw.tt("dve", g16[:, 32:36], g16[:, 32:36], g16[:, 28:32], ALU.max)
            fw.recip(g16[:, 36:40], g16[:, 32:36])
            fw.tt("dve", numt[:, :].rearrange("p (h v) -> p h v", h=4), numt[:, :].rearrange("p (h v) -> p h v", h=4),
                  g16[:, 36:40].bc(2, 256), ALU.mult)
            for h in range(4):
                grp_rstd(numt[:, h * 256:(h + 1) * 256], 256, nst[0:NB, 7:8], junk, NB)
                fw.ts("dve", hmn[0:NB, h * 256:(h + 1) * 256], numt[:, h * 256:(h + 1) * 256], nst[0:NB, 7:8], None, ALU.mult)
            to_feat(hmn, hmnT, NB)
            for tile in range(8):
                fw.ts("dve", hmfT[:, tile, 0:NB], hmnT[:, tile, 0:NB], mlcol[:, tile, 0:1], None, ALU.mult)
                fw.stt(hmfT[:, tile, 0:NB], cs_bf[:, tile, :], mlcol[:, tile, 1:2], hmfT[:, tile, 0:NB], ALU.mult, ALU.add)
            fw.tt("dve", hmfT[:, :, 0:NB], hmfT[:, :, 0:NB], sigoT[:, :, 0:NB], ALU.mult)
            for half in range(2):
                for kt in range(8):
                    fw.mm(PA[0:NB, half * 512:(half + 1) * 512], hmfT[:, kt, 0:NB], Wo[:, kt, half * 512:(half + 1) * 512],
                          start=kt == 0, stop=kt == 7)
            fw.tt("dve", x1[0:NB, :], x1[0:NB, :], PA[0:NB, :], ALU.add)
            fw.dma("sp", io.s1s[:, :], x1[0:NB, :])
        fw.dma("sp", io.mc_p[:, :, :, :], Cst[:])
        fw.dma("sp", io.mm_p[:, :], mprev[0:1, :])
        fw.dma("sp", io.conv_p[:, 12:20, :], convin[:, :, 0:3])
        fw.release(base_mark)

    def ffn_phase(layer, src, dst, ssrc, sdst, final):
        Wgu = fw.sb([128, 8, 2 * DFF], BF16, "Wgu")
        load_w(Wgu, io.w_gu[layer], 0, 8, step=1)
        Wd = fw.sb([128, 22, D], BF16, "Wd")
        load_w(Wd, io.w_dn[layer], 0, 22)
        gf = fw.sb([128, D], F32, "gf")
        fw.dma("sp", gf[:], io.norm_ffn[layer, :].partition_broadcast(128))
        if final:
            gfin = fw.sb([128, D], F32, "gfin")
            fw.dma("sp", gfin[:], io.norm_final.partition_broadcast(128))
        GB = 4
        xt = fw.sb([128, D], F32, "xt")
        junk = fw.sb([128, D], F32, "junk")
        xn = fw.sb([128, D], BF16, "xn")
        xnT = fw.sb([128, 8, GB * 128], BF16, "xnT")
        hT = fw.sb([128, 22, GB * 128], BF16, "hT")
        sg = [fw.sb([128, GB * 128], F32, f"sg{i}") for i in range(2)]
        x2 = fw.sb([128, D], F32, "x2")
        yo = junk
        groups = [list(range(g, min(g + GB, NCH))) for g in range(0, NCH, GB)]
        if SAMPLE:
            groups.append([NCH])
        def ffn_front(grp):
            samp = grp[0] == NCH
            M = NB if samp else 128
            rows = lambda ap, c: (ap[:, :] if samp else ap[c * 128:(c + 1) * 128, :])
            for gi, c in enumerate(grp):
                fw.dma("sp", xt[0:M, :], rows(ssrc if samp else src, c))
                rmsnorm(xt[0:M, :], gf, xn[0:M, :], M, junk)
                for kt in range(8):
                    fw.tr(PT3[:, kt, 0:M], xn[0:M, kt * 128:(kt + 1) * 128], identb[0:M, 0:M])
                fw.cp("dve", xnT[:, :, gi * M:(gi + 1) * M], PT3[:, :, 0:M])

        ffn_front(groups[0])
        for gidx, grp in enumerate(groups):
            samp = grp[0] == NCH
            M = NB if samp else 128
            W = M * len(grp)
            rows = lambda ap, c: (ap[:, :] if samp else ap[c * 128:(c + 1) * 128, :])
            for j in range(22):
                psg = pcd[j % 2]
                for kt in range(8):
                    fw.mm(psg[:, 0:W], Wgu[:, kt, j * 128:(j + 1) * 128], xnT[:, kt, 0:W], start=kt == 0, stop=kt == 7)
                psu = PB0f if j % 2 == 0 else PB1f
                for kt in range(8):
                    fw.mm(psu[:, 0:W], Wgu[:, kt, DFF + j * 128:DFF + (j + 1) * 128], xnT[:, kt, 0:W],
                          start=kt == 0, stop=kt == 7)
                fw.act(sg[j % 2][:, 0:W], psg[:, 0:W], AF.Silu)
                fw.tt("dve", hT[:, j, 0:W], sg[j % 2][:, 0:W], psu[:, 0:W], ALU.mult)
            if gidx + 1 < len(groups):
                ffn_front(groups[gidx + 1])
            for gi, c in enumerate(grp):
                for half in range(2):
                    for j in range(22):
                        fw.mm(PA[0:M, half * 512:(half + 1) * 512], hT[:, j, gi * M:(gi + 1) * M],
                              Wd[:, j, half * 512:(half + 1) * 512], start=j == 0, stop=j == 21)
                fw.dma("sp", x2[0:M, :], rows(ssrc if samp else src, c))
                fw.tt("dve", x2[0:M, :], x2[0:M, :], PA[0:M, :], ALU.add)
                if final:
                    rmsnorm(x2[0:M, :], gfin, yo[0:M, :], M, junk)
                    fw.dma("sp", rows(sdst if samp else dst, c), yo[0:M, :])
                else:
                    fw.dma("sp", rows(sdst if samp else dst, c), x2[0:M, :])
        fw.release(base_mark)

    if "1" in phases:
        ffn_phase(0, io.s1, io.s2, io.s1s, io.s2s, False)

    if "2" in phases:
        Wr = fw.sb([128, 8, D], BF16, "Wr"); load_w(Wr, io.rw_wr, 0, 8)
        Wk = fw.sb([128, 8, D], BF16, "Wk"); load_w(Wk, io.rw_wk, 0, 8)
        A1 = fw.sb([128, 8, 64], BF16, "A1"); load_w(A1, io.rw_a1, 0, 8, step=8)
        A2 = fw.sb([128, D], BF16, "A2"); fw.dma("pool", A2[0:64, :], io.rw_a2[:, :])
        Wv = fw.sb([128, 8, D], BF16, "Wv"); load_w(Wv, io.rw_wv, 0, 8)
        G1 = fw.sb([128, 8, 160], BF16, "G1"); load_w(G1, io.rw_g1, 0, 8, step=8)
        G2a = fw.sb([128, D], BF16, "G2a"); fw.dma("pool", G2a[:], io.rw_g2[0:128, :])
        G2b = fw.sb([128, D], BF16, "G2b"); fw.dma("pool", G2b[0:32, :], io.rw_g2[128:160, :])
        W1 = fw.sb([128, 8, 64], BF16, "W1"); load_w(W1, io.rw_w1, 0, 8, step=8)
        W2 = fw.sb([128, D], BF16, "W2"); fw.dma("pool", W2[0:64, :], io.rw_w2[:, :])
        Wo = fw.sb([128, 8, D], BF16, "Wo"); load_w(Wo, io.rw_wo, 0, 8)
        gm1 = fw.sb([128, D], F32, "gm1")
        fw.dma("sp", gm1[:], io.norm_mix[1, :].partition_broadcast(128))
        rows = []
        for i in range(7):
            rt = fw.sb([128, D], F32, f"row{i}")
            fw.dma("sp", rt[:], io.rw_rows[i, :].partition_broadcast(128))
            rows.append(rt)
        w0b, a0b, kkb_, kab, rkb, lnw, lnb = rows
        mu = fw.sb([128, 8, 6], F32, "mu")
        fw.dma("sp", mu[:], io.rw_mu[:, :, :])
        PB0 = fw.view(PB[:, 0:512], "PB0")
        PB1 = fw.view(PB[:, 512:1024], "PB1")
        NPS = [PB0, PB1, PC, PD]
        PAh = [PA[:, 0:512], PA[:, 512:1024]]
        PBh = [PB0[:, :], PB1[:, :]]
        h1 = fw.sb([128, 2, 128], BF16, "h1")

        def proj_tok(xT, W, Ph, M=128):
            for half in range(2):
                for kt in range(8):
                    fw.mm(Ph[half][0:M, :], xT[:, kt, 0:M], W[:, kt, half * 512:(half + 1) * 512],
                          start=kt == 0, stop=kt == 7)

        def lora(xT, Wa, nh, Wb_list, func, P, M=128):
            widths = [min(128, nh), nh - 128] if nh > 128 else [nh]
            for wi, wd in enumerate(widths):
                for kt in range(8):
                    fw.mm(PE[0:wd, wi * 128:wi * 128 + M], Wa[:, kt, wi * 128:wi * 128 + wd], xT[:, kt, 0:M],
                          start=kt == 0, stop=kt == 7)
                fw.act(h1[0:wd, wi, 0:M], PE[0:wd, wi * 128:wi * 128 + M], func)
            for half in range(2):
                for wi, wd in enumerate(widths):
                    fw.mm(P[half][0:M, :], h1[0:wd, wi, 0:M], Wb_list[wi][0:wd, half * 512:(half + 1) * 512],
                          start=wi == 0, stop=wi == len(widths) - 1)

        def rstd16(src16, dst16, mult_, eps, floor=None):
            if floor is not None:
                fw.ts("dve", dst16, src16, floor, None, ALU.max)
            else:
                fw.ts("dve", dst16, src16, mult_, eps, ALU.mult, ALU.add)
            fw.act(dst16, dst16, AF.Ln)
            fw.act(dst16, dst16, AF.Exp, scale=-0.5)

        mark2 = fw.mark()
        xt = fw.sb([128, D], F32, "xt")
        junk = fw.sb([128, D], F32, "junk")
        tmpA = fw.sb([128, D], F32, "tmpA")
        tmpB = fw.sb([128, D], F32, "tmpB")
        Et = fw.sb([128, D], F32, "Et")
        SB = [fw.sb([128, D], BF16, f"S{i}") for i in range(13)]
        xn = SB[0]; r_bf = SB[1]; kkn = SB[2]; kf_bf = SB[3]; b_bf = SB[4]; v_bf = SB[5]; bv = SB[6]
        g_bf = SB[7]; abar = SB[8]; bbar = SB[9]; kbar = SB[10]; btil = SB[11]; ktil = SB[12]
        rbar = SB[0]; yo = SB[8]
        xnTe = fw.sb([128, 8, 130], BF16, "xnTe")
        fw.memset("pool", xnTe[:], 0.0)
        xx = fw.sb([128, 8, 128], BF16, "xx")
        mixb = [fw.sb([128, 8, 128], BF16, f"mix{i}") for i in range(2)]
        arT = fw.sb([128, 8, 2, 128], BF16, "arT")
        bT = fw.sb([128, 8, 128], BF16, "bT")
        kT = fw.sb([128, 8, 128], BF16, "kT")
        yoT = fw.sb([128, 8, 128], BF16, "yoT")
        Ms = [fw.sb([128, 512], BF16, f"Ms{i}") for i in range(4)]
        Q0 = [fw.sb([128, 128], BF16, f"Q0{i}") for i in range(4)]
        PQ = [[fw.sb([128, 384], BF16, f"PQ{i}{k}") for k in range(2)] for i in range(4)]
        RHSb = [fw.sb([128, 64], BF16, f"RHS{i}") for i in range(4)]
        Ubp = [fw.sb([128, 2, 64], BF16, f"Ubp{i}") for i in range(2)]
        Hst = fw.sb([128, 8, 64], F32, "Hst")
        fw.memset("pool", Hst[:], 0.0)
        Hb = fw.sb([128, 8, 64], BF16, "Hb")
        fw.memset("pool", Hb[:], 0.0)
        eLT = fw.sb([128, 8], F32, "eLT")
        s16 = fw.sb([128, 64], F32, "s16")
        x3 = tmpB
        mcount = [0]

        def mix(cidx):
            dst = mixb[mcount[0] % 2]
            mcount[0] += 1
            for kt in range(8):
                fw.stt(dst[:, kt, :], xx[:, kt, :], mu[:, kt, cidx:cidx + 1], xnTe[:, kt, 1:129], ALU.mult, ALU.add)
            return dst

        for c in range(NCH):
            fw.dma("sp", xt[:], io.s2[c * 128:(c + 1) * 128, :])
            if c == NCH - 1:
                fw.act(junk[:, :], xt[:], AF.Square, accum_out=nst[:, 0:1])
                fw.ts("dve", nst[:, 1:2], nst[:, 0:1], 1.0 / D, EPS, ALU.mult, ALU.add)
                fw.act(nst[:, 2:3], nst[:, 1:2], AF.Ln)
                fw.act(nst[:, 3:4], nst[:, 2:3], AF.Exp, scale=-0.5)
                fw.stt(tmpA[:], xt[:], nst[:, 3:4], gm1[:], ALU.mult, ALU.mult)
                fw.dma("sp", io.shift_p[:, :], tmpA[127:128, :])
                fw.cp("dve", xn[:], tmpA[:])
            else:
                rmsnorm(xt[:], gm1, xn[:], 128, junk)
            for kt in range(8):
                fw.tr(PT3[:, kt, :], xn[:, kt * 128:(kt + 1) * 128], identb[:, :])
            fw.cp("dve", xnTe[:, :, 1:129], PT3)
            fw.tt("pool", xx[:], xnTe[:, :, 0:128], xnTe[:, :, 1:129], ALU.subtract)
            PCDh = [PC[:, :], PD[:, :]]
            hv2 = lambda r, i: r[:, i * 512:(i + 1) * 512]
            proj_tok(mix(0), Wr, PAh)
            proj_tok(mix(2), Wk, PBh)
            fw.cp("act", r_bf[:], PA[:, :])
            lora(mix(4), A1, 64, [A2], AF.Copy, PCDh)
            for i in range(2):
                fw.tt("dve", hv2(tmpA, i), PBh[i], hv2(kkb_, i), ALU.mult)
            fw.tt("pool", junk[:], tmpA[:], tmpA[:], ALU.mult)
            fw.red(s16[:, 0:16], v16(junk[:, :]), ALU.add)
            rstd16(s16[:, 0:16], s16[:, 16:32], None, None, floor=1e-24)
            fw.tt("dve", v16(kkn[:, :]), v16(tmpA[:, :]), s16[:, 16:32].bc(2, 64), ALU.mult)
            proj_tok(mix(3), Wv, PAh)
            for i in range(2):
                fw.tt("dve", hv2(tmpB, i), PCDh[i], hv2(a0b, i), ALU.add)
            fw.act(tmpB[:], tmpB[:], AF.Sigmoid)
            fw.stt(junk[:], tmpB[:], 1.0, kab[:], ALU.subtract, ALU.mult)
            fw.ts("dve", junk[:], junk[:], 1.0, None, ALU.add)
            for i in range(2):
                fw.tt("dve", hv2(kf_bf, i), PBh[i], hv2(junk, i), ALU.mult)
            fw.tt("pool", b_bf[:], kkn[:], tmpB[:], ALU.mult)
            lora(mix(5), G1, 160, [G2a, G2b], AF.Sigmoid, PBh)
            fw.tt("pool", junk[:], r_bf[:], kf_bf[:], ALU.mult)
            fw.tt("pool", junk[:], junk[:], rkb[:], ALU.mult)
            fw.red(s16[:, 32:48], v16(junk[:, :]), ALU.add)
            fw.cp("act", v_bf[:], PA[:, :])
            fw.tt("dve", v16(bv[:, :]), v16(PA[:, :]), s16[:, 32:48].bc(2, 64), ALU.mult)
            lora(mix(1), W1, 64, [W2], AF.Tanh, PCDh)
            for i in range(2):
                fw.cp("act", g_bf[:, i * 512:(i + 1) * 512], PBh[i])
            for i in range(2):
                fw.tt("dve", hv2(tmpA, i), PCDh[i], hv2(w0b, i), ALU.add)
            fw.act(tmpA[:], tmpA[:], AF.Exp, scale=-1.0)
            fw.act(tmpA[:], tmpA[:], AF.Ln, bias=1.0)
            fw.ts("dve", tmpA[:], tmpA[:], -1.0, -0.5, ALU.mult, ALU.add)
            fw.act(Et[:], tmpA[:], AF.Exp)
            fw.mm(PB0[:, :], tri_le, Et[:, 0:512])
            fw.mm(PB1[:, :], tri_le, Et[:, 512:1024])
            fw.mm(PA[:, 0:512], ones, Et[:, 0:512])
            fw.mm(PA[:, 512:1024], ones, Et[:, 512:1024])
            for kt in range(8):
                fw.mm(PE[:, kt * 16:(kt + 1) * 16], Et[:, kt * 128:(kt + 1) * 128], ones[:, 0:16])
            fw.act(eLT[:], PE[:, 0:128].rearrange("p (k s) -> p k s", s=16)[:, :, 0], AF.Exp, scale=-1.0)
            hv = lambda r, i: r[:, i * 512:(i + 1) * 512]
            for i, PBi in enumerate((PB0, PB1)):
                fw.act(hv(tmpA, i), PBi[:, :], AF.Exp, scale=-1.0)
                fw.tt("pool", hv(rbar, i), hv(r_bf, i), hv(tmpA, i), ALU.mult)
                fw.tt("dve", hv(tmpB, i), hv(Et, i), PBi[:, :], ALU.subtract)
                fw.act(hv(tmpB, i), hv(tmpB, i), AF.Exp)
                fw.stt(hv(abar, i), hv(kkn, i), -1.0, hv(tmpB, i), ALU.mult, ALU.mult)
            fw.cp("act", junk[:], PA[:, :])
            for i, PBi in enumerate((PB0, PB1)):
                fw.act(hv(tmpA, i), PBi[:, :], AF.Exp)
                fw.tt("pool", hv(bbar, i), hv(b_bf, i), hv(tmpA, i), ALU.mult)
                fw.tt("pool", hv(kbar, i), hv(kf_bf, i), hv(tmpA, i), ALU.mult)
                fw.tt("dve", hv(tmpB, i), PBi[:, :], hv(junk, i), ALU.subtract)
                fw.act(hv(tmpB, i), hv(tmpB, i), AF.Exp)
                fw.tt("pool", hv(btil, i), hv(b_bf, i), hv(tmpB, i), ALU.mult)
                fw.tt("dve", hv(ktil, i), hv(kf_bf, i), hv(tmpB, i), ALU.mult)
            for src, dst in ((abar, arT[:, :, 0, :]), (rbar, arT[:, :, 1, :]), (bbar, bT[:, :, :]), (kbar, kT[:, :, :])):
                for kt in range(8):
                    fw.tr(PT3[:, kt, :], src[:, kt * 128:(kt + 1) * 128], identb[:, :])
                fw.cp("dve", dst, PT3)
            for h0 in range(0, 16, 4):
                hd = []
                for i in range(4):
                    h = h0 + i
                    j, e = h // 2, h % 2
                    p0 = 64 * e
                    hd.append(dict(h=h, j=j, e=e, p0=p0, NP=NPS[i],
                                   aT=arT[p0:p0 + 64, j, 0, :], rT=arT[p0:p0 + 64, j, 1, :],
                                   ar=arT[p0:p0 + 64, j, :, :].rearrange("p a t -> p (a t)"),
                                   bT=bT[p0:p0 + 64, j, :], kT=kT[p0:p0 + 64, j, :]))
                for i, d in enumerate(hd):
                    fw.mm(PE[:, 0:256], d["bT"], d["ar"])
                    fw.mm(PE[:, 256:512], d["kT"], d["ar"])
                    fw.tt("dve", Ms[i][:], PE[:, :], m4, ALU.mult)
                    fw.mm(d["NP"][:, 0:128], d["aT"], d["bT"])
                    fw.tt("dve", Q0[i][:], d["NP"][:, 0:128], mask_gt, ALU.mult)
                    d["P"], d["Q"], d["Z"] = Ms[i][:, 0:128], Q0[i][:], identb[:, :]
                for k in range(7):
                    for i, d in enumerate(hd):
                        NP = d["NP"]
                        if k < 6:
                            fw.mm(NP[:, 0:128], d["Q"], d["P"])
                            fw.mm(NP[:, 128:256], d["P"], d["Q"])
                        fw.mm(NP[:, 256:384], identb[:, :], d["Z"], start=True, stop=False)
                        fw.mm(NP[:, 256:384], d["Q"], d["Z"], start=False, stop=True)
                    for i, d in enumerate(hd):
                        NP = d["NP"]
                        pq = PQ[i][k % 2]
                        lo = 0 if k < 6 else 256
                        fw.cp("act" if i % 2 == 0 else "dve", pq[:, lo:384], NP[:, lo:384])
                        d["P"], d["Q"], d["Z"] = pq[:, 0:128], pq[:, 128:256], pq[:, 256:384]
                for i, d in enumerate(hd):
                    NP, h, j, p0 = d["NP"], d["h"], d["j"], d["p0"]
                    fw.mm(NP[:, 384:448], d["aT"], Hb[p0:p0 + 64, j, :], start=True, stop=False)
                    fw.mm(NP[:, 384:448], Ms[i][:, 256:384], v_bf[:, h * 64:(h + 1) * 64], start=False, stop=True)
                    fw.cp("act", RHSb[i][:], NP[:, 384:448])
                for i, d in enumerate(hd):
                    NP, h, j, e = d["NP"], d["h"], d["j"], d["e"]
                    fw.mm(NP[:, 448:512], d["Z"], RHSb[i][:])
                    fw.cp("dve", Ubp[j % 2][:, e, :], NP[:, 448:512])
                for i, d in enumerate(hd):
                    h, j, e, p0 = d["h"], d["j"], d["e"], d["p0"]
                    ysl = PA[:, h * 64:(h + 1) * 64]
                    fw.mm(ysl, d["rT"], Hb[p0:p0 + 64, j, :], start=True, stop=False)
                    fw.mm(ysl, Ms[i][:, 128:256], Ubp[j % 2][:, e, :], start=False, stop=False)
                    fw.mm(ysl, Ms[i][:, 384:512], v_bf[:, h * 64:(h + 1) * 64], start=False, stop=True)
                for jj in range(2):
                    j = h0 // 2 + jj
                    NP = hd[2 * jj]["NP"]
                    fw.mm(NP[:, 0:128], btil[:, j * 128:(j + 1) * 128], Ubp[j % 2][:, :, :].rearrange("p e v -> p (e v)"),
                          start=True, stop=False)
                    fw.mm(NP[:, 0:128], ktil[:, j * 128:(j + 1) * 128], v_bf[:, j * 128:(j + 1) * 128],
                          start=False, stop=True)
                    for e in range(2):
                        p0 = 64 * e
                        fw.stt(Hst[p0:p0 + 64, j, :], Hst[p0:p0 + 64, j, :], eLT[p0:p0 + 64, j:j + 1],
                               NP[p0:p0 + 64, p0:p0 + 64], ALU.mult, ALU.add)
                    fw.cp("pool", Hb[:, j, :], Hst[:, j, :])
            fw.cp("act", tmpA[:], PA[:, :])
            fw.red(s16[:, 0:16], v16(tmpA[:, :]), ALU.add)
            fw.ts("dve", s16[:, 0:16], s16[:, 0:16], 1.0 / 64, None, ALU.mult)
            fw.tt("dve", v16(tmpA[:, :]), v16(tmpA[:, :]), s16[:, 0:16].bc(2, 64), ALU.subtract)
            fw.tt("pool", junk[:], tmpA[:], tmpA[:], ALU.mult)
            fw.red(s16[:, 16:32], v16(junk[:, :]), ALU.add)
            rstd16(s16[:, 16:32], s16[:, 48:64], 1.0 / 64, 64e-5)
            fw.tt("dve", v16(tmpA[:, :]), v16(tmpA[:, :]), s16[:, 48:64].bc(2, 64), ALU.mult)
            fw.tt("dve", tmpA[:], tmpA[:], lnw[:], ALU.mult)
            fw.tt("dve", tmpA[:], tmpA[:], lnb[:], ALU.add)
            fw.tt("dve", tmpA[:], tmpA[:], bv[:], ALU.add)
            fw.tt("dve", yo[:], tmpA[:], g_bf[:], ALU.mult)
            to_feat(yo, yoT, 128)
            for half in range(2):
                for kt in range(8):
                    fw.mm(PA[:, half * 512:(half + 1) * 512], yoT[:, kt, :], Wo[:, kt, half * 512:(half + 1) * 512],
                          start=kt == 0, stop=kt == 7)
            fw.tt("dve", x3[:], xt[:], PA[:, :], ALU.add)
            fw.dma("sp", io.s3[c * 128:(c + 1) * 128, :], x3[:])
            fw.cp("pool", xnTe[:, :, 0:1], xnTe[:, :, 128:129])
        fw.dma("sp", io.wkv_p[:, :, :], Hst[:])
        fw.release(mark2)
        if SAMPLE:
            M = NB
            f16 = lambda nm, dt=F32: fw.sb([NB, D], dt, nm)
            xts = f16("xts"); jk = f16("jk"); tA = f16("tA"); tB = f16("tB"); Es = f16("Es")
            r_s = f16("r_s", BF16); kk_s = f16("kk_s", BF16); kf_s = f16("kf_s", BF16); b_s = f16("b_s", BF16)
            v_s = f16("v_s"); bv_s = f16("bv_s", BF16); g_s = f16("g_s", BF16); xn_s = f16("xn_s", BF16)
            sa_tok = f16("sa_tok"); yo_s = f16("yo_s", BF16)
            xprev = fw.sb([128, 8, NB], F32, "xprev")
            xsT = fw.sb([128, 8, NB], BF16, "xsT")
            xxs = fw.sb([128, 8, NB], BF16, "xxs")
            mixs = [fw.sb([128, 8, NB], BF16, f"mixs{i}") for i in range(2)]
            featT = {nm: fw.sb([128, 8, NB], F32, nm) for nm in ("aT", "wT", "bTs", "kTs", "rTs")}
            amask = fw.sb([128, 8, NB, NB], F32, "amask")
            rmask = fw.sb([128, 8, NB, NB], F32, "rmask")
            Hs = [fw.sb([128, 8, 64], F32, f"Hs{i}") for i in range(2)]
            Tt = fw.sb([128, 8, 64], F32, "Tt")
            yoTs = fw.sb([128, 8, NB], BF16, "yoTs")
            s16 = fw.sb([NB, 64], F32, "s16s")
            v16s = lambda r: r.rearrange("p (h q) -> p h q", h=16)
            mc2 = [0]

            def mix_s(cidx):
                dst = mixs[mc2[0] % 2]
                mc2[0] += 1
                for kt in range(8):
                    fw.stt(dst[:, kt, :], xxs[:, kt, :], mu[:, kt, cidx:cidx + 1], xsT[:, kt, :], ALU.mult, ALU.add)
                return dst

            fw.dma("sp", xts[:], io.s2s[:, :])
            fw.dma("sp", xprev[:], io.shift_s_in[:, :, :])
            fw.act(jk[:], xts[:], AF.Square, accum_out=nst[0:M, 0:1])
            fw.ts("dve", nst[0:M, 1:2], nst[0:M, 0:1], 1.0 / D, EPS, ALU.mult, ALU.add)
            fw.act(nst[0:M, 2:3], nst[0:M, 1:2], AF.Ln)
            fw.act(nst[0:M, 3:4], nst[0:M, 2:3], AF.Exp, scale=-0.5)
            fw.stt(tA[:], xts[:], nst[0:M, 3:4], gm1[0:M, :], ALU.mult, ALU.mult)
            fw.dma("sp", io.shift_s[:, :], tA[:])
            fw.cp("dve", xn_s[:], tA[:])
            for kt in range(8):
                fw.tr(PT3[:, kt, 0:M], xn_s[:, kt * 128:(kt + 1) * 128], identb[0:M, 0:M])
            fw.cp("dve", xsT[:], PT3[:, :, 0:M])
            fw.tt("dve", xxs[:], xprev[:], xsT[:], ALU.subtract)
            PAm = [PA[0:M, 0:512], PA[0:M, 512:1024]]
            PBm = [PB0[0:M, :], PB1[0:M, :]]
            PAf = PA[0:M, :]
            proj_tok(mix_s(0), Wr, PAh, M)
            fw.cp("act", r_s[:], PAf)
            proj_tok(mix_s(2), Wk, PAh, M)
            fw.tt("dve", tA[:], PAf, kkb_[0:M, :], ALU.mult)
            fw.tt("dve", jk[:], tA[:], tA[:], ALU.mult)
            fw.red(s16[:, 0:16], v16s(jk[:, :]), ALU.add)
            rstd16(s16[:, 0:16], s16[:, 16:32], None, None, floor=1e-24)
            fw.tt("dve", v16s(kk_s[:, :]), v16s(tA[:, :]), s16[:, 16:32].bc(2, 64), ALU.mult)
            lora(mix_s(4), A1, 64, [A2], AF.Copy, PBh, M)
            for i in range(2):
                fw.tt("dve", tB[:, i * 512:(i + 1) * 512], PBm[i], a0b[0:M, i * 512:(i + 1) * 512], ALU.add)
            fw.act(tB[:], tB[:], AF.Sigmoid)
            fw.stt(jk[:], tB[:], 1.0, kab[0:M, :], ALU.subtract, ALU.mult)
            fw.ts("dve", jk[:], jk[:], 1.0, None, ALU.add)
            fw.tt("dve", kf_s[:], PAf, jk[:], ALU.mult)
            fw.tt("dve", b_s[:], kk_s[:], tB[:], ALU.mult)
            fw.tt("dve", jk[:], r_s[:], kf_s[:], ALU.mult)
            fw.tt("dve", jk[:], jk[:], rkb[0:M, :], ALU.mult)
            fw.red(s16[:, 32:48], v16s(jk[:, :]), ALU.add)
            proj_tok(mix_s(3), Wv, PAh, M)
            fw.cp("act", v_s[:], PAf)
            fw.tt("dve", v16s(bv_s[:, :]), v16s(PAf), s16[:, 32:48].bc(2, 64), ALU.mult)
            lora(mix_s(5), G1, 160, [G2a, G2b], AF.Sigmoid, PBh, M)
            for i in range(2):
                fw.cp("act", g_s[:, i * 512:(i + 1) * 512], PBm[i])
            lora(mix_s(1), W1, 64, [W2], AF.Tanh, PAh, M)
            fw.tt("dve", tA[:], PAf, w0b[0:M, :], ALU.add)
            fw.act(tA[:], tA[:], AF.Exp, scale=-1.0)
            fw.act(tA[:], tA[:], AF.Ln, bias=1.0)
            fw.ts("dve", tA[:], tA[:], -1.0, -0.5, ALU.mult, ALU.add)
            fw.act(Es[:], tA[:], AF.Exp)
            fw.act(Es[:], Es[:], AF.Exp, scale=-1.0)
            fw.ts("dve", tB[:], kk_s[:], -1.0, None, ALU.mult)
            for nm, src in (("aT", tB), ("wT", Es)):
                for kt in range(8):
                    fw.tr(PE[:, kt * 16:(kt + 1) * 16], src[:, kt * 128:(kt + 1) * 128], ident[0:M, 0:M])
                fw.cp("dve", featT[nm][:], PE[:, 0:128].rearrange("p (k b) -> p k b", k=8))
            for nm, src in (("bTs", b_s), ("kTs", kf_s), ("rTs", r_s)):
                for kt in range(8):
                    fw.tr(PT3[:, kt, 0:M], src[:, kt * 128:(kt + 1) * 128], identb[0:M, 0:M])
                fw.cp("dve", featT[nm][:], PT3[:, :, 0:M])
            fw.tt("dve", amask[:], featT["aT"][:, :, :].bc(2, NB), eye16[:, :, :].bc(1, 8), ALU.mult)
            fw.tt("dve", rmask[:], featT["rTs"][:, :, :].bc(2, NB), eye16[:, :, :].bc(1, 8), ALU.mult)
            SY = [PC, PD]
            for b in range(NB):
                H = Hs[b % 2]
                fw.dma("sp", H[:], io.wkv_s_in[b])
                for h in range(16):
                    j, e = h // 2, h % 2
                    p0 = 64 * e
                    fw.mm(SY[e][0:M, j * 64:(j + 1) * 64], amask[p0:p0 + 64, j, b, :], H[p0:p0 + 64, j, :],
                          start=(b == 0 and j == 0), stop=(b == NB - 1), skip=True)
            je = lambda r: r.rearrange("p (j e v) -> p j e v", j=8, e=2)
            fw.cp("dve", je(sa_tok[:, :])[:, :, 0, :], PC[0:M, :].rearrange("p (j v) -> p j v", j=8))
            fw.cp("dve", je(sa_tok[:, :])[:, :, 1, :], PD[0:M, :].rearrange("p (j v) -> p j v", j=8))
            mk_sel16b()
            sah = f16("sah", BF16); sal = f16("sal", BF16)
            vsh, vsl = xn_s, yo_s
            hilo(sa_tok[:], sah[:], sal[:])
            hilo(v_s[:], vsh[:], vsl[:])
            e4 = lambda r, e: r.rearrange("p (j e v) -> p j e v", j=8, e=2)[64 * e:64 * e + 64, :, e, :]
            for b in range(NB):
                H = Hs[b % 2]
                fw.dma("sp", H[:], io.wkv_s_in[b])
                bcast_rows(PA[:, 0:512], b, sah[:, 0:512], sal[:, 0:512])
                bcast_rows(PA[:, 512:1024], b, sah[:, 512:1024], sal[:, 512:1024])
                bcast_rows(PB0[:, :], b, vsh[:, 0:512], vsl[:, 0:512])
                bcast_rows(PB1[:, :], b, vsh[:, 512:1024], vsl[:, 512:1024])
                fw.tt("pool", H[:], H[:], featT["wT"][:, :, b].bc(2, 64), ALU.mult)
                for e in range(2):
                    p0 = 64 * e
                    fw.tt("dve", Tt[p0:p0 + 64, :, :], e4(PA[:, :], e), featT["bTs"][p0:p0 + 64, :, b].bc(2, 64), ALU.mult)
                fw.tt("dve", H[:], H[:], Tt[:], ALU.add)
                for e in range(2):
                    p0 = 64 * e
                    for half, PBx in enumerate((PB0, PB1)):
                        src = PBx[:, :].rearrange("p (j e v) -> p j e v", j=4, e=2)[p0:p0 + 64, :, e, :]
                        fw.tt("dve", Tt[p0:p0 + 64, 4 * half:4 * half + 4, :], src,
                              featT["kTs"][p0:p0 + 64, 4 * half:4 * half + 4, b].bc(2, 64), ALU.mult)
                fw.tt("dve", H[:], H[:], Tt[:], ALU.add)
                fw.dma("sp", io.wkv_s[b], H[:])
                for h in range(16):
                    j, e = h // 2, h % 2
                    p0 = 64 * e
                    fw.mm(SY[e][0:M, j * 64:(j + 1) * 64], rmask[p0:p0 + 64, j, b, :], H[p0:p0 + 64, j, :],
                          start=(b == 0 and j == 0), stop=(b == NB - 1), skip=True)
            fw.cp("dve", je(tA[:, :])[:, :, 0, :], PC[0:M, :].rearrange("p (j v) -> p j v", j=8))
            fw.cp("dve", je(tA[:, :])[:, :, 1, :], PD[0:M, :].rearrange("p (j v) -> p j v", j=8))
            fw.red(s16[:, 0:16], v16s(tA[:, :]), ALU.add)
            fw.ts("dve", s16[:, 0:16], s16[:, 0:16], 1.0 / 64, None, ALU.mult)
            fw.tt("dve", v16s(tA[:, :]), v16s(tA[:, :]), s16[:, 0:16].bc(2, 64), ALU.subtract)
            fw.tt("dve", jk[:], tA[:], tA[:], ALU.mult)
            fw.red(s16[:, 16:32], v16s(jk[:, :]), ALU.add)
            rstd16(s16[:, 16:32], s16[:, 48:64], 1.0 / 64, 64e-5)
            fw.tt("dve", v16s(tA[:, :]), v16s(tA[:, :]), s16[:, 48:64].bc(2, 64), ALU.mult)
            fw.tt("dve", tA[:], tA[:], lnw[0:M, :], ALU.mult)
            fw.tt("dve", tA[:], tA[:], lnb[0:M, :], ALU.add)
            fw.tt("dve", tA[:], tA[:], bv_s[:], ALU.add)
            fw.tt("dve", yo_s[:], tA[:], g_s[:], ALU.mult)
            for kt in range(8):
                fw.tr(PT3[:, kt, 0:M], yo_s[:, kt * 128:(kt + 1) * 128], identb[0:M, 0:M])
            fw.cp("dve", yoTs[:], PT3[:, :, 0:M])
            for half in range(2):
                for kt in range(8):
                    fw.mm(PA[0:M, half * 512:(half + 1) * 512], yoTs[:, kt, :], Wo[:, kt, half * 512:(half + 1) * 512],
                          start=kt == 0, stop=kt == 7)
            fw.tt("dve", tB[:], xts[:], PA[0:M, :], ALU.add)
            fw.dma("sp", io.s3s[:, :], tB[:])
        fw.release(base_mark)

    if "3" in phases:
        ffn_phase(1, io.s3, io.y_p, io.s3s, io.y_s, True)

    fw.finish()
    fw.close()
    return nc, fw


def prep_common(inp):
    f = lambda k: np.ascontiguousarray(np.asarray(inp[k], np.float32))
    m = {}
    m["cst"] = host_consts()
    m["w_in0"] = f("w_in0")[0]
    m["w_out0"] = f("w_out0")[0]
    m["norm_mix"] = f("norm_mix")
    m["norm_ffn"] = f("norm_ffn")
    m["norm_final"] = f("norm_final")
    m["ssd_norm"] = f("ssd_norm")[0]
    small0 = np.zeros(64, np.float32)
    small0[0:16] = f("ssd_dt_bias")[0]
    small0[16:32] = f("ssd_a_log")[0]
    small0[32:48] = f("ssd_d")[0]
    small0[48:52] = f("ml_i_bias")[0]
    small0[52:56] = f("ml_f_bias")[0]
    m["small0"] = small0
    cw = f("conv_w")[0].reshape(4, 20, 128).transpose(2, 1, 0)
    cb = f("conv_b")[0].reshape(20, 128).T[:, :, None]
    m["convp"] = np.ascontiguousarray(np.concatenate([cw, cb], axis=2))
    m["mlcol"] = np.ascontiguousarray(np.stack([f("ml_norm")[0].reshape(8, 128).T, f("ml_skip")[0].reshape(8, 128).T], axis=2))
    m["bdq"] = blockdiag(f("ml_wq")[0])
    m["bdk"] = blockdiag(f("ml_wk")[0])
    m["bdv"] = blockdiag(f("ml_wv")[0])
    for nm in ("rw_wr", "rw_wk", "rw_wv", "rw_wo", "rw_w1", "rw_w2", "rw_a1", "rw_a2", "rw_g1", "rw_g2"):
        m[nm] = f(nm)[0]
    m["rw_rows"] = np.ascontiguousarray(np.stack([f(k)[0] for k in ("rw_w0", "rw_a0", "rw_k_k", "rw_k_a", "rw_r_k", "rw_ln_w", "rw_ln_b")]))
    m["rw_mu"] = np.ascontiguousarray(f("rw_mu")[0].reshape(6, 8, 128).transpose(2, 1, 0))
    m["w_gu"] = f("ffn_w_gate_up")
    m["w_dn"] = f("ffn_w_down")
    return m


def prep_core(inp, core):
    f = lambda k: np.asarray(inp[k], np.float32)
    b0 = core * NB
    m = {}
    m["xp"] = np.ascontiguousarray(f("x_prompt")[core])
    m["xs"] = np.ascontiguousarray(f("x_sample")[b0:b0 + NB, 0, :])
    m["conv_s_in"] = np.ascontiguousarray(f("state_conv")[0, b0:b0 + NB].reshape(NB, 3, 20, 128).transpose(3, 2, 1, 0))
    m["ssm_s_in"] = np.ascontiguousarray(f("state_ssm")[0, b0:b0 + NB].reshape(NB, D, 128))
    m["mc_s_in"] = np.ascontiguousarray(f("state_mlstm_c")[0, b0:b0 + NB])
    m["mn_s_in"] = np.ascontiguousarray(f("state_mlstm_n")[0, b0:b0 + NB].reshape(NB, 4, 2, 128).transpose(3, 1, 2, 0).reshape(128, 8, NB))
    m["mm_s_in"] = np.ascontiguousarray(f("state_mlstm_m")[0, b0:b0 + NB])
    m["shift_s_in"] = np.ascontiguousarray(f("state_shift")[0, b0:b0 + NB].reshape(NB, 8, 128).transpose(2, 1, 0))
    m["wkv_s_in"] = np.ascontiguousarray(f("state_wkv")[0, b0:b0 + NB].reshape(NB, 8, 2, 64, 64).transpose(0, 2, 4, 1, 3).reshape(NB, 128, 8, 64))
    return m


def prep_consts(inp):
    f = lambda k: np.asarray(inp[k], np.float32)
    m = prep_common(inp)
    m["c16"] = host_consts16()
    m["eye16"] = np.ascontiguousarray(np.broadcast_to(np.eye(16, dtype=np.float32), (128, 16, 16)))
    dtcol = np.zeros((16, 4), np.float32)
    dtcol[:, 0] = f("ssd_dt_bias")[0]
    dtcol[:, 1] = f("ssd_a_log")[0]
    dtcol[:, 2] = f("ssd_d")[0]
    m["dtcol"] = dtcol
    return m


_NC_CACHE = {}


def kernel(**inp):
    if "nc" not in _NC_CACHE:
        _NC_CACHE["nc"] = build({})
    nc = _NC_CACHE["nc"]
    cm = prep_consts(inp)
    in_maps = [dict(cm, **prep_core(inp, c)) for c in range(NCORE)]
    res = run_bass_kernel_spmd(nc, in_maps, core_ids=list(range(NCORE)))
    R = res.results
    BT = NCORE * NB
    y_p = np.zeros((NCORE, T, D), np.float32)
    y_s = np.zeros((BT, 1, D), np.float32)
    conv_p = np.zeros((1, NCORE, 3, 2560), np.float32)
    conv_s = np.zeros((1, BT, 3, 2560), np.float32)
    ssm_p = np.zeros((1, NCORE, 16, 64, 128), np.float32)
    ssm_s = np.zeros((1, BT, 16, 64, 128), np.float32)
    mc_p = np.zeros((1, NCORE, 4, 256, 256), np.float32)
    mc_s = np.zeros((1, BT, 4, 256, 256), np.float32)
    mn_p = np.zeros((1, NCORE, 4, 256), np.float32)
    mn_s = np.zeros((1, BT, 4, 256), np.float32)
    mm_p = np.zeros((1, NCORE, 4), np.float32)
    mm_s = np.zeros((1, BT, 4), np.float32)
    sh_p = np.zeros((1, NCORE, D), np.float32)
    sh_s = np.zeros((1, BT, D), np.float32)
    wkv_p = np.zeros((1, NCORE, 16, 64, 64), np.float32)
    wkv_s = np.zeros((1, BT, 16, 64, 64), np.float32)
    for c in range(NCORE):
        r = R[c]
        sl = slice(c * NB, (c + 1) * NB)
        y_p[c] = r["y_p"]
        y_s[sl, 0] = r["y_s"]
        conv_p[0, c] = r["conv_p"].transpose(2, 1, 0).reshape(3, 2560)
        conv_s[0, sl] = r["conv_s"].transpose(3, 2, 1, 0).reshape(NB, 3, 2560)
        ssm_p[0, c] = r["ssm_p"].reshape(128, 16, 64).transpose(1, 2, 0)
        ssm_s[0, sl] = r["ssm_s"].reshape(NB, 16, 64, 128)
        mc = r["mc_p"]
        mc_p[0, c] = mc[:, :, :, :256].transpose(2, 1, 0, 3).reshape(4, 256, 256)
        mn_p[0, c] = mc[:, :, :, 256].transpose(2, 1, 0).reshape(4, 256)
        mc_s[0, sl] = r["mc_s"]
        mn_s[0, sl] = r["mn_s"].reshape(128, 4, 2, NB).transpose(3, 1, 2, 0).reshape(NB, 4, 256)
        mm_p[0, c] = r["mm_p"][0]
        mm_s[0, sl] = r["mm_s"]
        sh_p[0, c] = r["shift_p"][0]
        sh_s[0, sl] = r["shift_s"]
        wkv_p[0, c] = r["wkv_p"].reshape(2, 64, 8, 64).transpose(2, 0, 3, 1).reshape(16, 64, 64)
        wkv_s[0, sl] = r["wkv_s"].reshape(NB, 2, 64, 8, 64).transpose(0, 3, 1, 4, 2).reshape(NB, 16, 64, 64)
    return (y_p, y_s, conv_p, conv_s, ssm_p, ssm_s, mc_p, mc_s, mn_p, mn_s, mm_p, mm_s, sh_p, sh_s, wkv_p, wkv_s)
```
